# Optimizing a Trainium2 kernel written in Bass

```python
import math
import jax, jax.numpy as jnp
from jax import lax
import numpy as np

D_MODEL = 1024
BATCH = 32
SEQ = 2048
DEPTH = 2

GRID_W = 64
CTX_LEN = 256
MIX_W = D_MODEL
D_FF = 4 * D_MODEL
N_EVEN = (DEPTH + 1) // 2
N_ODD = DEPTH // 2
DEEPNORM_ALPHA = (2.0 * DEPTH) ** 0.25
DEEPNORM_BETA = (8.0 * DEPTH) ** -0.25
LN_EPS = 1e-5
N_MOD = 6

HY_W = MIX_W // 2
HY_SHORT = 3
HY_BANDS = 16
HY_PE_DIM = 2 * HY_BANDS + 1
HY_FFN = 64
HY_DECAY_MIN = -math.log(1e-2) / 1.5
HY_DECAY_MAX = -math.log(1e-2) / 0.3

HEAD_DIM = 64
SWA_HEADS = (MIX_W - HY_W) // HEAD_DIM
SWA_KV_HEADS = 2
SWA_GROUP = SWA_HEADS // SWA_KV_HEADS
WINDOW = 128
SWA_BLOCK = 128
ROPE_BASE = 10000.0

E_HY_COLS = 3 * HY_W
E_Q_COLS = SWA_HEADS * HEAD_DIM
E_KV_COLS = SWA_KV_HEADS * HEAD_DIM
E_IN_COLS = E_HY_COLS + E_Q_COLS + 2 * E_KV_COLS

RW_HEAD = 64
RW_W = MIX_W // 2
RW_HEADS = RW_W // RW_HEAD
RW_DECAY_LORA = 64
RW_AAA_LORA = 64
RW_GATE_LORA = 128
RW_GN_EPS = 64e-5
RW_COLS = 3 * RW_W + 2 * RW_DECAY_LORA + 2 * RW_AAA_LORA + RW_GATE_LORA

DN_HEAD = 128
DN_W = MIX_W - RW_W
DN_HEADS = DN_W // DN_HEAD
DN_SHORT = 3
DN_CHUNK = 64
DN_COLS = 4 * DN_W + 4 * DN_HEADS
O_IN_COLS = RW_COLS + DN_COLS

kernel_name = "hybrid_hyena_swa_rwkv7_gdn_dit_block"

F32 = jnp.float32


def layer_norm(x, g, b):
    xf = x.astype(F32)
    mu = jnp.mean(xf, -1, keepdims=True)
    var = jnp.mean(jnp.square(xf - mu), -1, keepdims=True)
    return ((xf - mu) * lax.rsqrt(var + LN_EPS) * g + b).astype(x.dtype)


def l2_normalize(t, eps=1e-6):
    tf = t.astype(F32)
    return tf * lax.rsqrt(jnp.sum(tf * tf, -1, keepdims=True) + eps)


def centred_conv(u, w):
    width = w.shape[0]
    pad = width // 2
    L = u.shape[1]
    up = jnp.pad(u, ((0, 0), (pad, pad), (0, 0)))
    out = up[:, 0:L] * w[0]
    for j in range(1, width):
        out = out + up[:, j:j + L] * w[j]
    return out


def sq_relu_mlp(h, w1, w2):
    return jnp.square(jax.nn.relu(h @ w1)) @ w2


def hyena_filters(L, w1, b1, w2, b2, freq, w3, decay):
    t = jnp.arange(L, dtype=F32)
    t_norm = t / max(L - 1, 1)
    bands = jnp.linspace(1e-4, HY_BANDS - 1, HY_BANDS, dtype=F32)
    ang = 2.0 * math.pi * t[:, None] * bands[None, :] / L
    pe = jnp.concatenate([t_norm[:, None], jnp.cos(ang), -jnp.sin(ang)], axis=-1)
    h = jnp.sin(freq * (pe @ w1 + b1))
    h = jnp.sin(freq * (h @ w2 + b2))
    h = (h @ w3) * jnp.exp(-t_norm[:, None] * jnp.abs(decay))
    return h[:, :HY_W], h[:, HY_W:]


def bidir_long_conv(u, h_fwd, h_bwd, bias):
    L = u.shape[1]
    k = jnp.concatenate([h_fwd, jnp.zeros_like(h_fwd[:1]), h_bwd[:0:-1]], axis=0)
    uf = jnp.fft.rfft(u.astype(F32), n=2 * L, axis=1)
    kf = jnp.fft.rfft(k.astype(F32), n=2 * L, axis=0)
    y = jnp.fft.irfft(uf * kf[None], n=2 * L, axis=1)[:, :L]
    return (y + u.astype(F32) * bias).astype(u.dtype)


def hyena_mixer(p, conv_w, w1, b1, w2, b2, freq, w3, decay, bias):
    z = centred_conv(p, conv_w)
    x0, x1, v = jnp.split(z, 3, axis=-1)
    h_f, h_b = hyena_filters(p.shape[1], w1, b1, w2, b2, freq, w3, decay)
    return x0 * bidir_long_conv(x1 * v, h_f, h_b, bias)


def axial_rope(t):
    L = t.shape[1]
    pos = jnp.arange(L, dtype=jnp.int32)
    half = HEAD_DIM // 2
    quarter = half // 2
    inv_freq = ROPE_BASE ** (-jnp.arange(quarter, dtype=F32) / quarter)
    tf = t.astype(F32)

    def rot(u, p):
        ang = p.astype(F32)[:, None] * inv_freq[None, :]
        cos = jnp.cos(ang)[None, :, None, :]
        sin = jnp.sin(ang)[None, :, None, :]
        u1, u2 = u[..., :quarter], u[..., quarter:]
        return jnp.concatenate([u1 * cos - u2 * sin, u1 * sin + u2 * cos], axis=-1)

    out = jnp.concatenate([rot(tf[..., :half], pos // GRID_W), rot(tf[..., half:], pos % GRID_W)], axis=-1)
    return out.astype(t.dtype)


def sink_softmax(s, sink):
    sk = sink.astype(F32)[None, :, :, None, None]
    m = jnp.maximum(jnp.max(s, -1, keepdims=True), sk)
    p = jnp.exp(s - m)
    return p / (jnp.sum(p, -1, keepdims=True) + jnp.exp(sk - m))


def windowed_attention_latent(q, k, v, kc, vc, sink):
    bsz, L = q.shape[0], q.shape[1]
    n_blk = L // SWA_BLOCK
    span = SWA_BLOCK + 2 * WINDOW
    scale = HEAD_DIM ** -0.5
    kp = jnp.pad(k, ((0, 0), (WINDOW, WINDOW), (0, 0), (0, 0)))
    vp = jnp.pad(v, ((0, 0), (WINDOW, WINDOW), (0, 0), (0, 0)))

    def one_block(b):
        start = b * SWA_BLOCK
        qb = lax.dynamic_slice_in_dim(q, start, SWA_BLOCK, axis=1)
        kb = lax.dynamic_slice_in_dim(kp, start, span, axis=1)
        vb = lax.dynamic_slice_in_dim(vp, start, span, axis=1)
        qpos = start + jnp.arange(SWA_BLOCK)
        kpos = start - WINDOW + jnp.arange(span)
        ok = (jnp.abs(qpos[:, None] - kpos[None, :]) <= WINDOW) & (kpos[None, :] >= 0) & (kpos[None, :] < L)
        s_loc = jnp.einsum('bqhgd,bkhd->bhgqk', qb, kb).astype(F32) * scale
        s_loc = jnp.where(ok, s_loc, -jnp.inf)
        s_ctx = jnp.einsum('bqhgd,bchd->bhgqc', qb, kc).astype(F32) * scale
        p = sink_softmax(jnp.concatenate([s_loc, s_ctx], axis=-1), sink).astype(v.dtype)
        return (jnp.einsum('bhgqk,bkhd->bqhgd', p[..., :span], vb)
                + jnp.einsum('bhgqc,bchd->bqhgd', p[..., span:], vc))

    o = lax.map(one_block, jnp.arange(n_blk))
    return jnp.moveaxis(o, 0, 1).reshape(bsz, L, SWA_HEADS * HEAD_DIM)


def context_attention(qc, kc, vc, sink):
    bsz, Lc = qc.shape[0], qc.shape[1]
    s = jnp.einsum('bqhgd,bchd->bhgqc', qc, kc).astype(F32) * (HEAD_DIM ** -0.5)
    p = sink_softmax(s, sink).astype(vc.dtype)
    return jnp.einsum('bhgqc,bchd->bqhgd', p, vc).reshape(bsz, Lc, SWA_HEADS * HEAD_DIM)


def split_swa(p):
    bsz, L = p.shape[0], p.shape[1]
    o1 = E_HY_COLS
    o2 = o1 + E_Q_COLS
    o3 = o2 + E_KV_COLS
    q = p[..., o1:o2].reshape(bsz, L, SWA_HEADS, HEAD_DIM)
    k = p[..., o2:o3].reshape(bsz, L, SWA_KV_HEADS, HEAD_DIM)
    v = p[..., o3:].reshape(bsz, L, SWA_KV_HEADS, HEAD_DIM)
    return q, k, v


def even_mixer(h_lat, h_ctx, w_in, w_out, hy_conv, hy_w1, hy_b1, hy_w2, hy_b2, hy_freq, hy_w3,
               hy_decay, hy_bias, sink, ctx_out):
    bsz, L = h_lat.shape[0], h_lat.shape[1]
    Lc = h_ctx.shape[1]
    p_lat = h_lat @ w_in
    p_ctx = h_ctx @ w_in
    sink_g = sink.reshape(SWA_KV_HEADS, SWA_GROUP)
    q, k, v = split_swa(p_lat)
    qc, kc, vc = split_swa(p_ctx)
    q = axial_rope(q).reshape(bsz, L, SWA_KV_HEADS, SWA_GROUP, HEAD_DIM)
    k = axial_rope(k)
    y_a = hyena_mixer(p_lat[..., :E_HY_COLS], hy_conv, hy_w1, hy_b1, hy_w2, hy_b2, hy_freq, hy_w3,
                      hy_decay, hy_bias)
    y_b = windowed_attention_latent(q, k, v, kc, vc, sink_g)
    y_lat = jnp.concatenate([y_a, y_b], axis=-1) @ w_out
    if not ctx_out:
        return y_lat, None
    yc_a = hyena_mixer(p_ctx[..., :E_HY_COLS], hy_conv, hy_w1, hy_b1, hy_w2, hy_b2, hy_freq, hy_w3,
                       hy_decay, hy_bias)
    yc_b = context_attention(qc.reshape(bsz, Lc, SWA_KV_HEADS, SWA_GROUP, HEAD_DIM), kc, vc, sink_g)
    y_ctx = jnp.concatenate([yc_a, yc_b], axis=-1) @ w_out
    return y_lat, y_ctx


def _heads(t):
    return t.astype(F32).reshape(t.shape[0], t.shape[1], RW_HEADS, RW_HEAD)


def rwkv7_features(p, mu, w0, w2, a0, a2, g2, k_k, k_a):
    bsz, L = p.shape[0], p.shape[1]
    pp = jnp.pad(p, ((0, 0), (1, 1), (0, 0)))
    p = p + mu * (0.5 * (pp[:, :-2] + pp[:, 2:]) - p)
    o1, o2, o3 = RW_W, 2 * RW_W, 3 * RW_W
    o4 = o3 + 2 * RW_DECAY_LORA
    o5 = o4 + 2 * RW_AAA_LORA
    r, k, v = p[..., :o1], p[..., o1:o2], p[..., o2:o3]
    wd = p[..., o3:o4].reshape(bsz, L, 2, RW_DECAY_LORA)
    ad = p[..., o4:o5].reshape(bsz, L, 2, RW_AAA_LORA)
    gd = p[..., o5:]
    w_log = -jax.nn.softplus(-(w0 + jnp.einsum('bldr,drc->bldc', jnp.tanh(wd), w2))) - 0.5
    decay = jnp.exp(-jnp.exp(w_log.astype(F32)))
    a = jax.nn.sigmoid(a0 + jnp.einsum('bldr,drc->bldc', ad, a2))
    g = jax.nn.sigmoid(gd) @ g2
    kk = l2_normalize((k * k_k).reshape(bsz, L, RW_HEADS, RW_HEAD)).reshape(bsz, L, RW_W)
    k_dir = k[:, :, None] * (1.0 + (a - 1.0) * k_a)
    b_dir = kk[:, :, None] * a
    return r, v, g, kk, decay, k_dir, b_dir


def rwkv7_scan(r, decay, k, v, kk, b, s0, reverse):
    def step(S, inp):
        r_t, w_t, k_t, v_t, kk_t, b_t = inp
        sa = jnp.einsum('bhvk,bhk->bhv', S, kk_t)
        S = S * w_t[:, :, None, :] - sa[..., None] * b_t[:, :, None, :] + v_t[..., None] * k_t[:, :, None, :]
        return S, jnp.einsum('bhvk,bhk->bhv', S, r_t)

    xs = tuple(jnp.moveaxis(t, 1, 0) for t in (r, decay, k, v, kk, b))
    S, ys = lax.scan(step, s0, xs, reverse=reverse)
    return jnp.moveaxis(ys, 0, 1), S


def rwkv7_direction(f_lat, f_ctx, d, s0, r_k):
    reverse = d == 1

    def run(f, s_init):
        r, v, g, kk, decay, k_dir, b_dir = f
        rh, vh, kh = _heads(r), _heads(v), _heads(k_dir[:, :, d])
        y, s = rwkv7_scan(rh, _heads(decay[:, :, d]), kh, vh, _heads(kk), _heads(b_dir[:, :, d]), s_init, reverse)
        bonus = jnp.sum(rh * kh * r_k, -1, keepdims=True) * vh
        return y, bonus, s

    yc, bc, s_ctx = run(f_ctx, s0)
    yl, bl, _ = run(f_lat, s_ctx)
    return yl, bl, yc, bc


def rwkv7_output(y, bonus, g, gn_g, gn_b):
    bsz, L = y.shape[0], y.shape[1]
    mu = jnp.mean(y, -1, keepdims=True)
    var = jnp.mean(jnp.square(y - mu), -1, keepdims=True)
    yn = ((y - mu) * lax.rsqrt(var + RW_GN_EPS)).reshape(bsz, L, RW_W) * gn_g + gn_b
    return ((yn + bonus.reshape(bsz, L, RW_W)) * g).astype(g.dtype)


def rwkv7_mixer(p_lat, p_ctx, mu, w0, w2, a0, a2, g2, k_k, k_a, r_k, gn_g, gn_b, ctx_out):
    f_lat = rwkv7_features(p_lat, mu, w0, w2, a0, a2, g2, k_k, k_a)
    f_ctx = rwkv7_features(p_ctx, mu, w0, w2, a0, a2, g2, k_k, k_a)
    s_zero = jnp.zeros((p_lat.shape[0], RW_HEADS, RW_HEAD, RW_HEAD), F32)
    yl_f, bl_f, yc_f, bc_f = rwkv7_direction(f_lat, f_ctx, 0, s_zero, r_k)
    yl_b, bl_b, yc_b, bc_b = rwkv7_direction(f_lat, f_ctx, 1, s_zero, r_k)
    y_lat = rwkv7_output(yl_f + yl_b, bl_f + bl_b, f_lat[2], gn_g, gn_b)
    y_ctx = rwkv7_output(yc_f + yc_b, bc_f + bc_b, f_ctx[2], gn_g, gn_b) if ctx_out else None
    return y_lat, y_ctx


def gdn_features(p, conv_w, A_log, dt_bias):
    bsz, L = p.shape[0], p.shape[1]
    qkv = jax.nn.silu(centred_conv(p[..., :3 * DN_W], conv_w))
    q, k, v = jnp.split(qkv, 3, axis=-1)
    q = l2_normalize(q.reshape(bsz, L, DN_HEADS, DN_HEAD)) * (DN_HEAD ** -0.5)
    k = l2_normalize(k.reshape(bsz, L, DN_HEADS, DN_HEAD))
    v = v.reshape(bsz, L, DN_HEADS, DN_HEAD).astype(F32)
    z = p[..., 3 * DN_W:4 * DN_W]
    gates = p[..., 4 * DN_W:].astype(F32).reshape(bsz, L, 2, 2, DN_HEADS)
    g_log = -jnp.exp(A_log) * jax.nn.softplus(gates[:, :, 0] + dt_bias)
    beta = jax.nn.sigmoid(gates[:, :, 1])
    return q, k, v, z, g_log, beta


def gdn_chunked(q, k, v, g_log, beta, s0):
    bsz, L = q.shape[0], q.shape[1]
    C = DN_CHUNK
    n = L // C

    def chunks(t):
        return jnp.moveaxis(t.reshape(bsz, n, C, *t.shape[2:]), 2, 3)

    qc, kc, vc = chunks(q), chunks(k), chunks(v)
    G = jnp.cumsum(chunks(g_log), axis=-1)
    bc = chunks(beta)
    causal = jnp.tril(jnp.ones((C, C), bool))
    strict = jnp.tril(jnp.ones((C, C), bool), -1)
    diff = G[..., :, None] - G[..., None, :]
    decay_mat = jnp.where(causal, jnp.exp(jnp.where(causal, diff, 0.0)), 0.0)
    A = jnp.where(strict, bc[..., :, None] * jnp.einsum('bnhid,bnhjd->bnhij', kc, kc) * decay_mat, 0.0)
    T = A + jnp.eye(C, dtype=F32)
    u0 = lax.linalg.triangular_solve(T, bc[..., None] * vc, left_side=True, lower=True, unit_diagonal=True)
    w = lax.linalg.triangular_solve(T, (bc * jnp.exp(G))[..., None] * kc, left_side=True, lower=True,
                                    unit_diagonal=True)
    qk = jnp.einsum('bnhid,bnhjd->bnhij', qc, kc) * decay_mat
    q_dec = qc * jnp.exp(G)[..., None]
    k_dec = kc * jnp.exp(G[..., -1:] - G)[..., None]
    g_end = jnp.exp(G[..., -1])

    def step(S, inp):
        u0_c, w_c, qk_c, q_c, k_c, ge = inp
        u = u0_c - jnp.einsum('bhck,bhkv->bhcv', w_c, S)
        o = jnp.einsum('bhck,bhkv->bhcv', q_c, S) + jnp.einsum('bhij,bhjv->bhiv', qk_c, u)
        S = ge[..., None, None] * S + jnp.einsum('bhck,bhcv->bhkv', k_c, u)
        return S, o

    xs = tuple(jnp.moveaxis(t, 1, 0) for t in (u0, w, qk, q_dec, k_dec, g_end))
    S, o = lax.scan(step, s0, xs)
    o = jnp.moveaxis(jnp.moveaxis(o, 0, 1), 2, 3).reshape(bsz, L, DN_HEADS, DN_HEAD)
    return o, S


def gdn_run(f, d, s0):
    q, k, v, z, g_log, beta = f
    g_d, b_d = g_log[:, :, d], beta[:, :, d]
    if d == 0:
        return gdn_chunked(q, k, v, g_d, b_d, s0)
    flip = lambda t: jnp.flip(t, axis=1)
    o, S = gdn_chunked(flip(q), flip(k), flip(v), flip(g_d), flip(b_d), s0)
    return flip(o), S


def gdn_output(o, z, norm_g):
    bsz, L = o.shape[0], o.shape[1]
    on = o * lax.rsqrt(jnp.mean(o * o, -1, keepdims=True) + 1e-6) * norm_g
    return (on.reshape(bsz, L, DN_W) * jax.nn.silu(z.astype(F32))).astype(z.dtype)


def gdn_mixer(p_lat, p_ctx, conv_w, A_log, dt_bias, norm_g, ctx_out):
    f_lat = gdn_features(p_lat, conv_w, A_log, dt_bias)
    f_ctx = gdn_features(p_ctx, conv_w, A_log, dt_bias)
    s_zero = jnp.zeros((p_lat.shape[0], DN_HEADS, DN_HEAD, DN_HEAD), F32)
    oc_f, sc_f = gdn_run(f_ctx, 0, s_zero)
    ol_f, _ = gdn_run(f_lat, 0, sc_f)
    oc_b, sc_b = gdn_run(f_ctx, 1, s_zero)
    ol_b, _ = gdn_run(f_lat, 1, sc_b)
    y_lat = gdn_output(ol_f + ol_b, f_lat[3], norm_g)
    y_ctx = gdn_output(oc_f + oc_b, f_ctx[3], norm_g) if ctx_out else None
    return y_lat, y_ctx


def odd_mixer(h_lat, h_ctx, w_in, w_out, rw_mu, rw_w0, rw_w2, rw_a0, rw_a2, rw_g2, rw_kk, rw_ka, rw_rk,
              rw_lnx_g, rw_lnx_b, dn_conv, dn_A_log, dn_dt_bias, dn_norm_g, ctx_out):
    p_lat = h_lat @ w_in
    p_ctx = h_ctx @ w_in
    yc_l, yc_c = rwkv7_mixer(p_lat[..., :RW_COLS], p_ctx[..., :RW_COLS], rw_mu, rw_w0, rw_w2, rw_a0, rw_a2,
                             rw_g2, rw_kk, rw_ka, rw_rk, rw_lnx_g, rw_lnx_b, ctx_out)
    yd_l, yd_c = gdn_mixer(p_lat[..., RW_COLS:], p_ctx[..., RW_COLS:], dn_conv, dn_A_log, dn_dt_bias,
                           dn_norm_g, ctx_out)
    y_lat = jnp.concatenate([yc_l, yd_l], axis=-1) @ w_out
    y_ctx = jnp.concatenate([yc_c, yd_c], axis=-1) @ w_out if ctx_out else None
    return y_lat, y_ctx


def setup_inputs(seed: int = 0) -> dict:
    key = jax.random.key(seed)
    keys = iter(jax.random.split(key, 64))

    def normal(shape, std):
        return std * jax.random.normal(next(keys), shape, F32)

    def uniform(shape, lo, hi):
        return jax.random.uniform(next(keys), shape, F32, lo, hi)

    dt = jnp.exp(uniform((N_ODD, 2, DN_HEADS), math.log(1e-3), math.log(1e-1)))
    return {
        "x": normal((BATCH, SEQ, D_MODEL), 1.0),
        "c": normal((BATCH, D_MODEL), 1.0),
        "ctx": normal((BATCH, CTX_LEN, D_MODEL), 1.0),
        "c_ctx": normal((D_MODEL,), 1.0),
        "mod_w": normal((DEPTH, D_MODEL, N_MOD * D_MODEL), 0.5 * D_MODEL ** -0.5),
        "mod_b": normal((DEPTH, N_MOD * D_MODEL), 0.02),
        "ln_g": 1.0 + normal((DEPTH, 2, D_MODEL), 0.02),
        "ln_b": normal((DEPTH, 2, D_MODEL), 0.02),
        "mlp_w1": normal((DEPTH, D_MODEL, D_FF), D_MODEL ** -0.5),
        "mlp_w2": normal((DEPTH, D_FF, D_MODEL), DEEPNORM_BETA * D_FF ** -0.5),
        "e_w_in": normal((N_EVEN, D_MODEL, E_IN_COLS), D_MODEL ** -0.5),
        "e_w_out": normal((N_EVEN, MIX_W, D_MODEL), DEEPNORM_BETA * MIX_W ** -0.5),
        "hy_conv": normal((N_EVEN, HY_SHORT, E_HY_COLS), HY_SHORT ** -0.5),
        "hy_ffn_w1": normal((N_EVEN, HY_PE_DIM, HY_FFN), HY_PE_DIM ** -0.5),
        "hy_ffn_b1": normal((N_EVEN, HY_FFN), 0.1),
        "hy_ffn_w2": normal((N_EVEN, HY_FFN, HY_FFN), HY_FFN ** -0.5),
        "hy_ffn_b2": normal((N_EVEN, HY_FFN), 0.1),
        "hy_sin_freq": 1.0 + normal((N_EVEN, HY_FFN), 0.1),
        "hy_ffn_w3": normal((N_EVEN, HY_FFN, 2 * HY_W), HY_FFN ** -0.5),
        "hy_decay": jnp.linspace(HY_DECAY_MIN, HY_DECAY_MAX, 2 * HY_W, dtype=F32)
                    * (1.0 + normal((N_EVEN, 2 * HY_W), 0.05)),
        "hy_bias": normal((N_EVEN, HY_W), 1.0),
        "attn_sink": normal((N_EVEN, SWA_HEADS), 0.5),
        "o_w_in": normal((N_ODD, D_MODEL, O_IN_COLS), D_MODEL ** -0.5),
        "o_w_out": normal((N_ODD, MIX_W, D_MODEL), DEEPNORM_BETA * MIX_W ** -0.5),
        "rw_mu": 0.5 + normal((N_ODD, RW_COLS), 0.1),
        "rw_w0": jnp.linspace(-6.0, -1.0, RW_W, dtype=F32) + normal((N_ODD, 2, RW_W), 0.1),
        "rw_w2": normal((N_ODD, 2, RW_DECAY_LORA, RW_W), 0.5 * RW_DECAY_LORA ** -0.5),
        "rw_a0": normal((N_ODD, 2, RW_W), 0.1),
        "rw_a2": normal((N_ODD, 2, RW_AAA_LORA, RW_W), 0.5 * RW_AAA_LORA ** -0.5),
        "rw_g2": normal((N_ODD, RW_GATE_LORA, RW_W), RW_GATE_LORA ** -0.5),
        "rw_kk": 0.85 + normal((N_ODD, RW_W), 0.05),
        "rw_ka": 1.0 + normal((N_ODD, RW_W), 0.05),
        "rw_rk": normal((N_ODD, RW_HEADS, RW_HEAD), 0.1),
        "rw_lnx_g": 1.0 + normal((N_ODD, RW_W), 0.02),
        "rw_lnx_b": normal((N_ODD, RW_W), 0.02),
        "dn_conv": normal((N_ODD, DN_SHORT, 3 * DN_W), DN_SHORT ** -0.5),
        "dn_A_log": jnp.log(uniform((N_ODD, 2, DN_HEADS), 1.0, 16.0)),
        "dn_dt_bias": dt + jnp.log(-jnp.expm1(-dt)),
        "dn_norm_g": 1.0 + normal((N_ODD, DN_HEAD), 0.02),
    }


def reference(x, c, ctx, c_ctx, mod_w, mod_b, ln_g, ln_b, mlp_w1, mlp_w2, e_w_in, e_w_out, hy_conv,
              hy_ffn_w1, hy_ffn_b1, hy_ffn_w2, hy_ffn_b2, hy_sin_freq, hy_ffn_w3, hy_decay, hy_bias, attn_sink,
              o_w_in, o_w_out, rw_mu, rw_w0, rw_w2, rw_a0, rw_a2, rw_g2, rw_kk, rw_ka, rw_rk, rw_lnx_g,
              rw_lnx_b, dn_conv, dn_A_log, dn_dt_bias, dn_norm_g):
    alpha = DEEPNORM_ALPHA
    for i in range(DEPTH):
        ctx_out = i != DEPTH - 1
        m_lat = jax.nn.silu(c) @ mod_w[i] + mod_b[i]
        m_ctx = jax.nn.silu(c_ctx) @ mod_w[i] + mod_b[i]
        sh1, sc1, g1, sh2, sc2, g2 = jnp.split(m_lat[:, None, :], N_MOD, axis=-1)
        csh1, csc1, cg1, csh2, csc2, cg2 = jnp.split(m_ctx, N_MOD, axis=-1)
        h_lat = x * (1.0 + sc1) + sh1
        h_ctx = ctx * (1.0 + csc1) + csh1
        j = i // 2
        if i % 2 == 0:
            y_lat, y_ctx = even_mixer(h_lat, h_ctx, e_w_in[j], e_w_out[j], hy_conv[j], hy_ffn_w1[j], hy_ffn_b1[j],
                                      hy_ffn_w2[j], hy_ffn_b2[j], hy_sin_freq[j], hy_ffn_w3[j], hy_decay[j],
                                      hy_bias[j], attn_sink[j], ctx_out)
        else:
            y_lat, y_ctx = odd_mixer(h_lat, h_ctx, o_w_in[j], o_w_out[j], rw_mu[j], rw_w0[j], rw_w2[j], rw_a0[j],
                                     rw_a2[j], rw_g2[j], rw_kk[j], rw_ka[j], rw_rk[j], rw_lnx_g[j], rw_lnx_b[j],
                                     dn_conv[j], dn_A_log[j], dn_dt_bias[j], dn_norm_g[j], ctx_out)
        x = layer_norm(alpha * x + g1 * y_lat, ln_g[i, 0], ln_b[i, 0])
        x = layer_norm(alpha * x + g2 * sq_relu_mlp(x * (1.0 + sc2) + sh2, mlp_w1[i], mlp_w2[i]),
                       ln_g[i, 1], ln_b[i, 1])
        if ctx_out:
            ctx = layer_norm(alpha * ctx + cg1 * y_ctx, ln_g[i, 0], ln_b[i, 0])
            ctx = layer_norm(alpha * ctx + cg2 * sq_relu_mlp(ctx * (1.0 + csc2) + csh2, mlp_w1[i], mlp_w2[i]),
                             ln_g[i, 1], ln_b[i, 1])
    return x
```

```python
import numpy as np
from contextlib import ExitStack
import concourse.bass as bass
import concourse.mybir as mybir
from concourse.bass_utils import run_bass_kernel_spmd

F32 = mybir.dt.float32
BF16 = mybir.dt.bfloat16
AF = mybir.ActivationFunctionType
ALU = mybir.AluOpType
AX = mybir.AxisListType

SAME_ENGINE_SYNC = True
EPOCH = 30000


class Buf:
    __slots__ = ("ap", "name", "lw", "rd")

    def __init__(self, ap, name):
        self.ap = ap
        self.name = name
        self.lw = None
        self.rd = []


class Sched:
    def __init__(self, nc, ndma=10):
        self.nc = nc
        self.stack = ExitStack()
        self.engs = {"pe": nc.tensor, "act": nc.scalar, "dve": nc.vector, "pool": nc.gpsimd, "sp": nc.sync}
        self.csem = {}
        self.ccnt = {}
        self.nsem = 0
        for e in ("pe", "act", "dve", "pool"):
            self._new_csem(e)
        self.dq = {}
        for q in ("sp", "pool", "act"):
            self.dq[q] = [[self._sem(f"d_{q}{i}"), 0] for i in range(ndma)]
        self.dqi = {q: 0 for q in self.dq}
        self.waited = {}
        self.phase_stack = None
        self.ninst = 0

    def _sem(self, name):
        self.nsem += 1
        return self.stack.enter_context(self.nc.semaphore(name))

    def _new_csem(self, e):
        self.csem[e] = self._sem(f"c_{e}_{self.nsem}")
        self.ccnt[e] = 0

    def sb(self, name, shape, dtype, persist=False):
        st = self.stack if (persist or self.phase_stack is None) else self.phase_stack
        self.nsem += 0
        self.uid = getattr(self, "uid", 0) + 1
        t = st.enter_context(self.nc.sbuf_tensor(f"{name}_{self.uid}", list(shape), dtype))
        return Buf(t, name)

    def ps(self, name, shape, dtype, persist=False):
        st = self.stack if (persist or self.phase_stack is None) else self.phase_stack
        self.uid = getattr(self, "uid", 0) + 1
        t = st.enter_context(self.nc.psum_tensor(f"{name}_{self.uid}", list(shape), dtype))
        return Buf(t, name)

    def view(self, ap, name="v"):
        return Buf(ap, name)

    def begin_sub(self):
        if not hasattr(self, "sub_stk"):
            self.sub_stk = []
        self.sub_stk.append(self.phase_stack)
        self.phase_stack = ExitStack()

    def end_sub(self):
        self.barrier()
        self.phase_stack.close()
        self.phase_stack = self.sub_stk.pop()

    def begin_phase(self):
        assert self.phase_stack is None
        self.phase_stack = ExitStack()

    def end_phase(self):
        self.barrier()
        self.phase_stack.close()
        self.phase_stack = None

    def _wait(self, F, tok):
        sem, val, eng = tok
        if eng == F == "pe":
            return
        if eng == F and not SAME_ENGINE_SYNC:
            return
        key = (F, id(sem))
        if self.waited.get(key, 0) >= val:
            return
        self.engs[F].wait_ge(sem, val)
        self.waited[key] = val
        self.ninst += 1

    def _deps(self, F, reads, writes):
        for b in reads:
            if b.lw is not None:
                self._wait(F, b.lw)
        for b in writes:
            if b.lw is not None:
                self._wait(F, b.lw)
            for t in b.rd:
                self._wait(F, t)

    def _commit(self, tok, reads, writes):
        for b in reads:
            if tok[2] == "dma":
                b.rd.append(tok)
            else:
                b.rd = [t for t in b.rd if t[2] != tok[2]]
                b.rd.append(tok)
        for b in writes:
            b.lw = tok
            b.rd = []

    def op(self, F, fn, reads=(), writes=()):
        self._deps(F, reads, writes)
        if self.ccnt[F] >= EPOCH:
            self._new_csem(F)
        inst = fn(self.engs[F])
        self.ccnt[F] += 1
        inst.then_inc(self.csem[F], 1)
        tok = (self.csem[F], self.ccnt[F], F)
        self._commit(tok, reads, writes)
        self.ninst += 1
        return tok

    def dma(self, Q, out, in_, reads=(), writes=(), **kw):
        self._deps(Q, reads, writes)
        pool = self.dq[Q]
        i = self.dqi[Q]
        self.dqi[Q] = (i + 1) % len(pool)
        sem, val = pool[i]
        if val > 0:
            self._wait(Q, (sem, val, "dma"))
        inst = self.engs[Q].dma_start(out=out, in_=in_, **kw)
        inst.then_inc(sem, 16)
        pool[i][1] = val + 16
        tok = (sem, val + 16, "dma")
        self._commit(tok, reads, writes)
        self.ninst += 1
        return tok

    def barrier(self, engines=("pe", "act", "dve", "pool", "sp")):
        toks = []
        for e in ("pe", "act", "dve", "pool"):
            if self.ccnt[e] > 0:
                toks.append((self.csem[e], self.ccnt[e], "bar"))
        for q in self.dq:
            for sem, val in self.dq[q]:
                if val > 0:
                    toks.append((sem, val, "dma"))
        for F in engines:
            for t in toks:
                self._wait(F, t)

    def finish(self):
        self.barrier()
        self.stack.close()


D = 1024
LAT = 2048
CTX = 256
T = LAT + CTX
NTT = T // 128
DFF = 4096
ALPHA = (2.0 * 2) ** 0.25
LN_EPS = 1e-5


class Ring:
    def __init__(self, bufs):
        self.bufs = bufs
        self.i = 0

    def next(self):
        b = self.bufs[self.i]
        self.i = (self.i + 1) % len(self.bufs)
        return b


def mk_ring(S, name, shape, dtype, n=2, psum=False):
    f = S.ps if psum else S.sb
    return Ring([f(f"{name}{i}", shape, dtype) for i in range(n)])


class Ctx:
    pass


def prep_weight(S, dst, src, K, N, scale=None, tag="pw"):
    S.begin_phase()
    CB = min(N, 2048)
    rin = mk_ring(S, tag + "i", [128, CB], F32, 2)
    rout = mk_ring(S, tag + "o", [128, CB], BF16, 2)
    sc = S.sb(tag + "s", [128, CB], F32) if scale is not None else None
    for c0 in range(0, N, CB):
        cw = min(CB, N - c0)
        if scale is not None:
            S.dma("sp", sc.ap[:, :cw], scale[c0:c0 + cw].partition_broadcast(128), writes=[sc])
        for k0 in range(0, K, 128):
            kw = min(128, K - k0)
            a = rin.next()
            o = rout.next()
            S.dma("sp", a.ap[:kw, :cw], src[k0:k0 + kw, c0:c0 + cw], writes=[a])
            if scale is not None:
                S.op("dve", lambda e: e.tensor_tensor(out=o.ap[:kw, :cw], in0=a.ap[:kw, :cw], in1=sc.ap[:kw, :cw], op=ALU.mult),
                     reads=[a, sc], writes=[o])
            else:
                S.op("dve", lambda e: e.tensor_copy(out=o.ap[:kw, :cw], in_=a.ap[:kw, :cw]), reads=[a], writes=[o])
            S.dma("pool", dst[k0:k0 + kw, c0:c0 + cw], o.ap[:kw, :cw], reads=[o])
    S.end_phase()


def phase_mod(S, C):
    nc, R = C.nc, C.R
    S.begin_phase()
    cT = S.sb("cT", [128, 8, R], F32)
    S.dma("sp", cT.ap[:], C.cT[:, :, :], writes=[cT])
    sig = S.sb("sig", [128, 8, R], F32)
    scT = S.sb("scT", [128, 8, R], BF16)
    S.op("act", lambda e: e.activation(out=sig.ap[:], in_=cT.ap[:], func=AF.Sigmoid), reads=[cT], writes=[sig])
    S.op("dve", lambda e: e.tensor_tensor(out=scT.ap[:], in0=cT.ap[:], in1=sig.ap[:], op=ALU.mult), reads=[cT, sig], writes=[scT])
    scbc = S.sb("scbc", [128, 8, R, 128], BF16)
    for kc in range(8):
        for r in range(R):
            S.op("dve", lambda e: e.tensor_copy(out=scbc.ap[:, kc, r, :], in_=scT.ap[:, kc, r:r + 1].to_broadcast([128, 128])),
                 reads=[scT], writes=[scbc])
    mb = S.sb("mb", [128, 2, 48], F32)
    S.dma("sp", mb.ap[:], C.mod_bT[:, :, :], writes=[mb])
    wst = mk_ring(S, "mws", [128, 3072], F32, 2)
    wbf = S.sb("mwbf", [128, 8, 6144], BF16)
    pacc = mk_ring(S, "mps", [128, 512], F32, 2, psum=True)
    gt = mk_ring(S, "mgt", [128, 512], F32, 2)
    mbr = S.sb("mbr", [128, 2048], F32)
    for l in range(2):
        for kc in range(8):
            for hf in range(2):
                a = wst.next()
                S.dma("sp" if hf == 0 else "pool", a.ap[:], C.mod_w[l, kc * 128:(kc + 1) * 128, hf * 3072:(hf + 1) * 3072], writes=[a])
                S.op("dve" if hf == 0 else "act",
                     (lambda e: e.tensor_copy(out=wbf.ap[:, kc, hf * 3072:(hf + 1) * 3072], in_=a.ap[:])) if hf == 0 else
                     (lambda e: e.activation(out=wbf.ap[:, kc, hf * 3072:(hf + 1) * 3072], in_=a.ap[:], func=AF.Identity)),
                     reads=[a], writes=[wbf])
        for fc in range(48):
            p = pacc.next()
            for kc in range(8):
                S.op("pe", lambda e: e.matmul(p.ap[:, :R], wbf.ap[:, kc, fc * 128:(fc + 1) * 128], scT.ap[:, kc, :],
                                              start=(kc == 0), stop=(kc == 7)), reads=[wbf, scT], writes=[p])
            is_scale = (fc // 8) in (1, 4)
            S.op("dve", lambda e: e.tensor_scalar(out=C.modT.ap[:, l, fc, :], in0=p.ap[:, :R], scalar1=mb.ap[:, l, fc:fc + 1],
                                                  scalar2=(1.0 if is_scale else 0.0), op0=ALU.add, op1=ALU.add),
                 reads=[p, mb], writes=[C.modT])
        for gi, c0 in enumerate((2048, 5120)):
            S.dma("sp", mbr.ap[:, gi * 1024:(gi + 1) * 1024], C.mod_b[l, c0:c0 + 1024].partition_broadcast(128), writes=[mbr])
        for gi, c0 in enumerate((2048, 5120)):
            for r in range(R):
                for hf in range(2):
                    p = pacc.next()
                    for kc in range(8):
                        S.op("pe", lambda e: e.matmul(p.ap[:], scbc.ap[:, kc, r, :], wbf.ap[:, kc, c0 + hf * 512:c0 + (hf + 1) * 512],
                                                      start=(kc == 0), stop=(kc == 7)), reads=[scbc, wbf], writes=[p])
                    g = gt.next()
                    S.op("dve", lambda e: e.tensor_tensor(out=g.ap[:], in0=p.ap[:], in1=mbr.ap[:, gi * 1024 + hf * 512:gi * 1024 + (hf + 1) * 512], op=ALU.add),
                         reads=[p, mbr], writes=[g])
                    S.dma("pool", C.gbc[l, gi, r, :, hf * 512:(hf + 1) * 512], g.ap[:], reads=[g])
    S.end_phase()


def tile_src(C, stage, bi, i):
    if stage == 0:
        if i < 2:
            return C.ctx[bi, i * 128:(i + 1) * 128, :]
        return C.x[bi, (i - 2) * 128:(i - 1) * 128, :]
    return C.xs[stage - 1][bi, i * 128:(i + 1) * 128, :]


def rstd_op(S, mv, o, i, eps):
    S.op("dve", lambda e: e.tensor_scalar(out=mv.ap[:, o:o + 1], in0=mv.ap[:, i:i + 1], scalar1=eps, scalar2=None, op0=ALU.add), reads=[mv], writes=[mv])
    S.op("act", lambda e: e.activation(out=mv.ap[:, o:o + 1], in_=mv.ap[:, o:o + 1], func=AF.Sqrt), reads=[mv], writes=[mv])
    S.op("dve", lambda e: e.reciprocal(out=mv.ap[:, o:o + 1], in_=mv.ap[:, o:o + 1]), reads=[mv], writes=[mv])


class Epi:
    def __init__(self, S, C, tag):
        self.S, self.C = S, C
        self.gb = [S.sb(tag + "gb0", [128, 1024], F32), S.sb(tag + "gb1", [128, 1024], F32)]
        self.lg = S.sb(tag + "lg", [128, 1024], F32)
        self.lb = S.sb(tag + "lb", [128, 1024], F32)
        self.t1 = mk_ring(S, tag + "t1", [128, 1024], F32, 2)
        self.xo = mk_ring(S, tag + "xo", [128, 1024], F32, 2)
        self.st = mk_ring(S, tag + "st", [128, 2, 6], F32, 2)
        self.mv = mk_ring(S, tag + "mv", [128, 4], F32, 2)

    def load(self, l, sub, bi):
        S, C = self.S, self.C
        S.dma("sp", self.gb[0].ap[:], C.gbc[l, sub, bi, :, :], writes=[self.gb[0]])
        S.dma("sp", self.gb[1].ap[:], C.gbc[l, sub, C.R - 1, :, :], writes=[self.gb[1]])
        S.dma("sp", self.lg.ap[:], C.ln_g[l, sub, :].partition_broadcast(128), writes=[self.lg])
        S.dma("sp", self.lb.ap[:], C.ln_b[l, sub, :].partition_broadcast(128), writes=[self.lb])

    def run(self, y, xin, is_ctx, dst):
        S = self.S
        gb = self.gb[1 if is_ctx else 0]
        t1, xo, st, mv = self.t1.next(), self.xo.next(), self.st.next(), self.mv.next()
        S.op("dve", lambda e: e.tensor_tensor(out=t1.ap[:], in0=y.ap[:], in1=gb.ap[:], op=ALU.mult), reads=[y, gb], writes=[t1])
        S.op("dve", lambda e: e.scalar_tensor_tensor(out=t1.ap[:], in0=xin.ap[:], scalar=ALPHA, in1=t1.ap[:], op0=ALU.mult, op1=ALU.add),
             reads=[xin, t1], writes=[t1])
        for h in range(2):
            S.op("dve", lambda e: e.bn_stats(out=st.ap[:, h, :], in_=t1.ap[:, h * 512:(h + 1) * 512]), reads=[t1], writes=[st])
        S.op("dve", lambda e: e.bn_aggr(out=mv.ap[:, 0:2], in_=st.ap[:]), reads=[st], writes=[mv])
        rstd_op(S, mv, 2, 1, LN_EPS)
        S.op("dve", lambda e: e.tensor_scalar(out=mv.ap[:, 3:4], in0=mv.ap[:, 0:1], scalar1=mv.ap[:, 2:3], scalar2=-1.0, op0=ALU.mult, op1=ALU.mult),
             reads=[mv], writes=[mv])
        S.op("act", lambda e: e.activation(out=xo.ap[:], in_=t1.ap[:], func=AF.Identity, scale=mv.ap[:, 2:3], bias=mv.ap[:, 3:4]),
             reads=[t1, mv], writes=[xo])
        S.op("pool", lambda e: e.tensor_tensor(out=xo.ap[:], in0=xo.ap[:], in1=self.lg.ap[:], op=ALU.mult), reads=[xo, self.lg], writes=[xo])
        S.op("pool", lambda e: e.tensor_tensor(out=xo.ap[:], in0=xo.ap[:], in1=self.lb.ap[:], op=ALU.add), reads=[xo, self.lb], writes=[xo])
        S.dma("pool", dst, xo.ap[:], reads=[xo])


def transpose_mod(S, C, xin, hT, col0, l, fc0, r, ptr, ident):
    for g in range(2):
        p = ptr.next()
        for j in range(4):
            kc = g * 4 + j
            S.op("pe", lambda e: e.transpose(out=p.ap[:, j * 128:(j + 1) * 128], in_=xin.ap[:, kc * 128:(kc + 1) * 128], identity=ident.ap[:]),
                 reads=[xin, ident], writes=[p])
        for j in range(4):
            kc = g * 4 + j
            S.op("act", lambda e: e.activation(out=hT.ap[:, kc, col0:col0 + 128], in_=p.ap[:, j * 128:(j + 1) * 128], func=AF.Identity,
                                               scale=C.modT.ap[:, l, fc0 + 8 + kc, r:r + 1], bias=C.modT.ap[:, l, fc0 + kc, r:r + 1]),
                 reads=[p, C.modT], writes=[hT])


def phase_mlp(S, C, l, bi, src_stage, dst_stage, last):
    S.begin_phase()
    ident = C.ident
    w1 = S.sb("w1", [128, 8, DFF], BF16)
    w2 = S.sb("w2", [128, 32, D], BF16)
    for kc in range(8):
        S.dma("sp" if kc % 2 == 0 else "pool", w1.ap[:, kc, :], C.w1b[l][kc * 128:(kc + 1) * 128, :], writes=[w1])
    for ko in range(32):
        S.dma("sp" if ko % 2 == 0 else "pool", w2.ap[:, ko, :], C.w2b[l][ko * 128:(ko + 1) * 128, :], writes=[w2])
    epi = Epi(S, C, "m")
    epi.load(l, 1, bi)
    xin = mk_ring(S, "mxin", [128, 1024], F32, 4)
    hT = mk_ring(S, "mhT", [128, 8, 256], BF16, 2)
    aT = S.sb("maT", [128, 32, 256], BF16)
    rl = mk_ring(S, "mrl", [128, 256], BF16, 2)
    ptr = mk_ring(S, "mptr", [128, 512], F32, 2, psum=True)
    pup = mk_ring(S, "mpup", [128, 256], F32, 2, psum=True)
    pdn = mk_ring(S, "mpdn", [128, 1024], F32, 2, psum=True)
    t0 = 1 if last else 0
    for tt in range(t0, 9):
        is_ctx = tt == 0
        r = C.R - 1 if is_ctx else bi
        xs_ = []
        h = hT.next()
        for s in range(2):
            xi = xin.next()
            S.dma("sp", xi.ap[:], tile_src(C, src_stage, bi, tt * 2 + s), writes=[xi])
            transpose_mod(S, C, xi, h, s * 128, l, 24, r, ptr, ident)
            xs_.append(xi)
        for fo in range(32):
            p = pup.next()
            for kc in range(8):
                S.op("pe", lambda e: e.matmul(p.ap[:], w1.ap[:, kc, fo * 128:(fo + 1) * 128], h.ap[:, kc, :], start=(kc == 0), stop=(kc == 7)),
                     reads=[w1, h], writes=[p])
            rr = rl.next()
            S.op("act", lambda e: e.activation(out=rr.ap[:], in_=p.ap[:], func=AF.Relu), reads=[p], writes=[rr])
            S.op("pool", lambda e: e.tensor_tensor(out=aT.ap[:, fo, :], in0=rr.ap[:], in1=rr.ap[:], op=ALU.mult), reads=[rr], writes=[aT])
        for s in range(2):
            p = pdn.next()
            for hf in range(2):
                for ko in range(32):
                    S.op("pe", lambda e: e.matmul(p.ap[:, hf * 512:(hf + 1) * 512], aT.ap[:, ko, s * 128:(s + 1) * 128], w2.ap[:, ko, hf * 512:(hf + 1) * 512],
                                                  start=(ko == 0), stop=(ko == 31)), reads=[aT, w2], writes=[p])
            i = tt * 2 + s
            if last:
                dst = C.out[bi, (i - 2) * 128:(i - 1) * 128, :]
            else:
                dst = C.xs[dst_stage - 1][bi, i * 128:(i + 1) * 128, :]
            epi.run(p, xs_[s], is_ctx, dst)
    S.end_phase()


def dram(nc, name, shape, dtype, kind=None):
    if kind is None:
        return nc.dram_tensor(name, list(shape), dtype).ap()
    return nc.dram_tensor(name, list(shape), dtype, kind=kind).ap()


def declare_common(nc, NB, dbg=None):
    C = Ctx()
    C.nc, C.NB, C.R = nc, NB, NB + 1
    R = C.R
    I = lambda n, s: dram(nc, n, s, F32, "ExternalInput")
    C.x = I("x", [NB, LAT, D])
    C.ctx = I("ctx", [NB, CTX, D])
    C.cT = I("cT", [128, 8, R])
    C.mod_w = I("mod_w", [2, D, 6 * D])
    C.mod_b = I("mod_b", [2, 6 * D])
    C.mod_bT = I("mod_bT", [128, 2, 48])
    C.ln_g = I("ln_g", [2, 2, D])
    C.ln_b = I("ln_b", [2, 2, D])
    C.mlp_w1 = I("mlp_w1", [2, D, DFF])
    C.mlp_w2 = I("mlp_w2", [2, DFF, D])
    C.ident_d = I("ident", [128, 128])
    C.out = dram(nc, "out", [NB, LAT, D], F32, "ExternalOutput")
    C.gbc = dram(nc, "gbc", [2, 2, R, 128, D], F32)
    C.w1b = [dram(nc, f"w1b{l}", [D, DFF], BF16) for l in range(2)]
    C.w2b = [dram(nc, f"w2b{l}", [DFF, D], BF16) for l in range(2)]
    nst = 3
    C.xs = [dram(nc, f"xs{i}", [NB, T, D], F32, "ExternalOutput" if (dbg and f"xs{i}" in dbg) else None) for i in range(nst)]
    return C


def common_setup(S, C):
    C.modT = S.sb("modT", [128, 2, 48, C.R], F32, persist=True)
    C.ident = S.sb("identsb", [128, 128], F32, persist=True)
    S.dma("sp", C.ident.ap[:], C.ident_d[:, :], writes=[C.ident])


def host_common(inputs, core, NB):
    b0 = core * NB
    f = lambda a: np.ascontiguousarray(np.asarray(a, dtype=np.float32))
    cs = np.concatenate([np.asarray(inputs["c"])[b0:b0 + NB], np.asarray(inputs["c_ctx"])[None, :]], 0)
    m = {
        "x": f(np.asarray(inputs["x"])[b0:b0 + NB]),
        "ctx": f(np.asarray(inputs["ctx"])[b0:b0 + NB]),
        "cT": f(cs.reshape(NB + 1, 8, 128).transpose(2, 1, 0)),
        "mod_w": f(inputs["mod_w"]),
        "mod_b": f(inputs["mod_b"]),
        "mod_bT": f(np.asarray(inputs["mod_b"]).reshape(2, 48, 128).transpose(2, 0, 1)),
        "ln_g": f(inputs["ln_g"]),
        "ln_b": f(inputs["ln_b"]),
        "mlp_w1": f(inputs["mlp_w1"]),
        "mlp_w2": f(inputs["mlp_w2"]),
        "ident": np.eye(128, dtype=np.float32),
    }
    return m


HYW = 512
PI = float(np.pi)


def colof(i):
    return 1 + 128 * i if i < 2 else 259 + 128 * (i - 2)


def declare_l0(nc, C, dbg=None):
    NB = C.NB
    I = lambda n, s, dt=F32: dram(nc, n, s, dt, "ExternalInput")
    Sx = lambda n, s, dt=BF16: dram(nc, n, s, dt, "ExternalOutput" if (dbg and n in dbg) else None)
    C.e_w_hy = I("e_w_hy", [D, 1536])
    C.e_w_qkv = I("e_w_qkv", [D, 768 + 640])
    C.e_w_out = I("e_w_out", [D, D])
    C.hy_conv = I("hy_conv", [3, 1536])
    C.hy_w1 = I("hy_w1", [33, 64])
    C.hy_w2 = I("hy_w2", [64, 64])
    C.hy_w3 = I("hy_w3", [64, 1024])
    C.hy_vec = I("hy_vec", [64, 3])
    C.hy_decay = I("hy_decay", [1024])
    C.hy_bias = I("hy_bias", [512])
    C.attn_sink = I("attn_sink", [8])
    C.peT = [I("peT_l", [33, LAT]), I("peT_c", [33, CTX])]
    C.negtn = [I("negtn_l", [128, LAT // 128]), I("negtn_c", [128, CTX // 128])]
    C.fwd = [I("fwd_l", [16, 2, 128, 16, 128], BF16), I("fwd_c", [2, 2, 128, 2, 128], BF16)]
    C.inv = [I("inv_l", [4, 128, 16, 2, 512], BF16), I("inv_c", [1, 128, 2, 2, 256], BF16)]
    C.rope = I("rope", [64, 2, LAT])
    C.amask = I("amask", [128, 384])
    C.ident_b = I("ident_b", [128, 128], BF16)
    C.whb = [Sx(f"whb{j}", [D, 1536]) for j in range(3)]
    C.wqb = Sx("wqb", [D, 1408])
    C.wob0 = Sx("wob0", [D, D])
    C.kspec = [Sx("kspec_l", [LAT, 2, 512], F32), Sx("kspec_c", [CTX, 2, 512], F32)]
    C.filt = [Sx("filt_l", [LAT, 2, 512]), Sx("filt_c", [CTX, 2, 512])]
    C.u = Sx("u_s", [NB, T, 512])
    C.x0T = Sx("x0T_s", [NB, 512, T])
    C.qT = Sx("qT_s", [NB, 8, 64, T])
    C.kT = Sx("kT_s", [NB, 2, 64, T])
    C.v = Sx("v_s", [NB, T, 128])
    C.yaT = Sx("yaT_s", [NB, 512, T])
    C.ybT = Sx("ybT_s", [NB, 8, 64, T])


def host_l0(inputs, m):
    f = lambda a: np.ascontiguousarray(np.asarray(a, dtype=np.float32))
    import ml_dtypes
    bf = lambda a: np.ascontiguousarray(np.asarray(a, dtype=np.float32).astype(ml_dtypes.bfloat16))
    w = np.asarray(inputs["e_w_in"])[0]
    d = np.arange(64)
    partner = np.where((d % 32) < 16, d + 16, d - 16)
    qcols = 1536 + (np.arange(8)[:, None] * 64 + partner[None, :]).reshape(-1)
    kcols = 2048 + (np.arange(2)[:, None] * 64 + partner[None, :]).reshape(-1)
    m["e_w_hy"] = f(w[:, :1536])
    m["e_w_qkv"] = f(np.concatenate([w[:, 1536:2304], w[:, qcols], w[:, kcols]], 1))
    m["e_w_out"] = f(np.asarray(inputs["e_w_out"])[0])
    m["hy_conv"] = f(np.asarray(inputs["hy_conv"])[0])
    m["hy_w1"] = f(np.asarray(inputs["hy_ffn_w1"])[0])
    m["hy_w2"] = f(np.asarray(inputs["hy_ffn_w2"])[0])
    m["hy_w3"] = f(np.asarray(inputs["hy_ffn_w3"])[0])
    m["hy_vec"] = f(np.stack([np.asarray(inputs["hy_ffn_b1"])[0], np.asarray(inputs["hy_ffn_b2"])[0], np.asarray(inputs["hy_sin_freq"])[0]], 1))
    m["hy_decay"] = f(np.asarray(inputs["hy_decay"])[0])
    m["hy_bias"] = f(np.asarray(inputs["hy_bias"])[0])
    m["attn_sink"] = f(np.asarray(inputs["attn_sink"])[0])
    for tag, Lf in (("l", LAT), ("c", CTX)):
        t = np.arange(Lf, dtype=np.float32)
        t_norm = t / np.float32(max(Lf - 1, 1))
        bands = np.linspace(1e-4, 15, 16, dtype=np.float32)
        ang = (2.0 * np.pi * t[:, None] * bands[None, :] / Lf).astype(np.float32)
        pe = np.concatenate([t_norm[:, None], np.cos(ang), -np.sin(ang)], -1).astype(np.float32)
        m["peT_" + tag] = f(pe.T)
        m["negtn_" + tag] = f((-t_norm).reshape(Lf // 128, 128).T)
        N = 2 * Lf
        nt = Lf // 128
        tt = np.arange(Lf, dtype=np.float64)
        ff = np.arange(Lf, dtype=np.float64) + 0.5
        th = 2.0 * np.pi * np.outer(tt, ff) / N
        Cm, Sm = np.cos(th), np.sin(th)
        fw = np.stack([Cm, Sm], 0).reshape(2, nt, 128, nt, 128)
        m["fwd_" + tag] = bf(fw.transpose(3, 0, 2, 1, 4))
        tw = min(512, Lf)
        iv = np.stack([Cm.T, -Sm.T], 0).reshape(2, nt, 128, Lf // tw, tw)
        m["inv_" + tag] = bf(iv.transpose(3, 2, 1, 0, 4))
    pos = np.arange(LAT)
    inv_freq = (10000.0 ** (-np.arange(16, dtype=np.float32) / 16)).astype(np.float32)
    P = np.where(d[:, None] < 32, (pos // 64)[None, :], (pos % 64)[None, :]).astype(np.float32)
    ang = (P * inv_freq[d % 16][:, None]).astype(np.float32)
    sgn = np.where((d % 32) < 16, -1.0, 1.0)[:, None]
    m["rope"] = f(np.stack([np.cos(ang), sgn * np.sin(ang)], 1))
    qi = np.arange(128)[:, None]
    kj = np.arange(384)[None, :] - 128
    m["amask"] = f(np.where(np.abs(qi - kj) <= 128, 0.0, -30000.0))
    m["ident_b"] = bf(np.eye(128))
    return m


def l0_setup(S, C):
    for j in range(3):
        prep_weight(S, C.whb[j], C.e_w_hy, D, 1536, scale=C.hy_conv[j, :], tag=f"ph{j}")
    prep_weight(S, C.wqb, C.e_w_qkv, D, 1408, tag="pq")
    prep_weight(S, C.wob0, C.e_w_out, D, D, tag="po")
    for si, Lf in enumerate((LAT, CTX)):
        hyena_filter(S, C, si, Lf)
        hyena_fwd(S, C, si, Lf, C.filt[si], None, C.kspec[si], is_filter=True)


def hyena_filter(S, C, si, Lf):
    S.begin_phase()
    w1 = S.sb("hw1", [33, 64], F32)
    w2 = S.sb("hw2", [64, 64], F32)
    w3 = S.sb("hw3", [64, 1024], F32)
    vec = S.sb("hvec", [64, 3], F32)
    peT = S.sb("hpe", [33, Lf], F32)
    ntn = S.sb("hntn", [128, Lf // 128], F32)
    dec = S.sb("hdec", [128, 1024], F32)
    for dst, src in ((w1, C.hy_w1), (w2, C.hy_w2), (w3, C.hy_w3), (vec, C.hy_vec), (peT, C.peT[si]), (ntn, C.negtn[si])):
        S.dma("sp", dst.ap[:], src, writes=[dst])
    S.dma("sp", dec.ap[:], C.hy_decay[:].partition_broadcast(128), writes=[dec])
    S.op("dve", lambda e: e.scalar_tensor_tensor(out=dec.ap[:], in0=dec.ap[:], scalar=-1.0, in1=dec.ap[:], op0=ALU.mult, op1=ALU.max), reads=[dec], writes=[dec])
    h1 = S.sb("hh1", [64, Lf], F32)
    h2 = S.sb("hh2", [64, Lf], F32)
    tmp = S.sb("htmp", [64, 512], F32)
    S.sin_ki = S.sb("hki", [64, 512], mybir.dt.int32)
    S.sin_kf = S.sb("hkf", [64, 512], F32)
    pp = mk_ring(S, "hpp", [128, 512], F32, 2, psum=True)
    W = min(512, Lf)
    for c0 in range(0, Lf, W):
        p = pp.next()
        S.op("pe", lambda e: e.matmul(p.ap[:64, :W], w1.ap[:], peT.ap[:, c0:c0 + W], start=True, stop=True), reads=[w1, peT], writes=[p])
        S.op("dve", lambda e: e.tensor_scalar(out=tmp.ap[:, :W], in0=p.ap[:64, :W], scalar1=vec.ap[:, 0:1], scalar2=vec.ap[:, 2:3], op0=ALU.add, op1=ALU.mult),
             reads=[p, vec], writes=[tmp])
        sin_tail(S, h1, c0, W, tmp)
    for c0 in range(0, Lf, W):
        p = pp.next()
        S.op("pe", lambda e: e.matmul(p.ap[:64, :W], w2.ap[:], h1.ap[:, c0:c0 + W], start=True, stop=True), reads=[w2, h1], writes=[p])
        S.op("dve", lambda e: e.tensor_scalar(out=tmp.ap[:, :W], in0=p.ap[:64, :W], scalar1=vec.ap[:, 1:2], scalar2=vec.ap[:, 2:3], op0=ALU.add, op1=ALU.mult),
             reads=[p, vec], writes=[tmp])
        sin_tail(S, h2, c0, W, tmp)
    ex = mk_ring(S, "hex", [128, 1024], F32, 2)
    fo = mk_ring(S, "hfo", [128, 2, 512], BF16, 2)
    for tc in range(Lf // 128):
        e_ = ex.next()
        S.op("act", lambda e: e.activation(out=e_.ap[:], in_=dec.ap[:], func=AF.Exp, scale=ntn.ap[:, tc:tc + 1]), reads=[dec, ntn], writes=[e_])
        for hf in range(2):
            p = pp.next()
            S.op("pe", lambda e: e.matmul(p.ap[:], h2.ap[:, tc * 128:(tc + 1) * 128], w3.ap[:, hf * 512:(hf + 1) * 512], start=True, stop=True),
                 reads=[h2, w3], writes=[p])
            S.op("dve", lambda e: e.tensor_tensor(out=e_.ap[:, hf * 512:(hf + 1) * 512], in0=p.ap[:], in1=e_.ap[:, hf * 512:(hf + 1) * 512], op=ALU.mult),
                 reads=[p, e_], writes=[e_])
        if tc == 0:
            S.op("dve", lambda e: e.memset(e_.ap[0:1, 512:1024], 0.0), reads=[], writes=[e_])
        o = fo.next()
        S.op("dve", lambda e: e.tensor_tensor(out=o.ap[:, 0, :], in0=e_.ap[:, 512:1024], in1=e_.ap[:, 0:512], op=ALU.add), reads=[e_], writes=[o])
        S.op("pool", lambda e: e.tensor_tensor(out=o.ap[:, 1, :], in0=e_.ap[:, 512:1024], in1=e_.ap[:, 0:512], op=ALU.subtract), reads=[e_], writes=[o])
        S.dma("sp", C.filt[si][tc * 128:(tc + 1) * 128, :, :], o.ap[:], reads=[o])
    S.end_phase()


def sin_tail(S, dst, c0, W, tmp):
    ki, kf = S.sin_ki, S.sin_kf
    S.op("dve", lambda e: e.tensor_scalar(out=tmp.ap[:, :W], in0=tmp.ap[:, :W], scalar1=1.0 / (2.0 * PI), scalar2=16.5, op0=ALU.mult, op1=ALU.add),
         reads=[tmp], writes=[tmp])
    S.op("dve", lambda e: e.tensor_copy(out=ki.ap[:, :W], in_=tmp.ap[:, :W]), reads=[tmp], writes=[ki])
    S.op("dve", lambda e: e.tensor_copy(out=kf.ap[:, :W], in_=ki.ap[:, :W]), reads=[ki], writes=[kf])
    S.op("dve", lambda e: e.scalar_tensor_tensor(out=tmp.ap[:, :W], in0=tmp.ap[:, :W], scalar=-0.5, in1=kf.ap[:, :W], op0=ALU.add, op1=ALU.subtract),
         reads=[tmp, kf], writes=[tmp])
    S.op("dve", lambda e: e.scalar_tensor_tensor(out=tmp.ap[:, :W], in0=tmp.ap[:, :W], scalar=-0.5, in1=tmp.ap[:, :W], op0=ALU.is_lt, op1=ALU.add),
         reads=[tmp], writes=[tmp])
    S.op("act", lambda e: e.activation(out=dst.ap[:, c0:c0 + W], in_=tmp.ap[:, :W], func=AF.Sin, scale=2.0 * PI * 0.999999), reads=[tmp], writes=[dst])


def hyena_fwd(S, C, si, Lf, src, bi, dst, is_filter):
    nt = Lf // 128
    N = 2 * Lf
    if is_filter:
        S.begin_phase()
    a_in = S.sb("fa", [128, nt, 2 if is_filter else 1, 512], BF16)
    if is_filter:
        S.dma("sp", a_in.ap[:], src.rearrange("(tc p) s c -> p tc s c", p=128), writes=[a_in])
        bb = S.sb("fbias", [128, 512], F32)
        S.dma("sp", bb.ap[:], C.hy_bias[:].partition_broadcast(128), writes=[bb])
        S.op("dve", lambda e: e.tensor_scalar(out=bb.ap[:], in0=bb.ap[:], scalar1=2.0 / N, scalar2=None, op0=ALU.mult), reads=[bb], writes=[bb])
    else:
        S.dma("sp", a_in.ap[:, :, 0, :], src.rearrange("(tc p) c -> p tc c", p=128), writes=[a_in])
    fm = mk_ring(S, "ffm", [128, 2, nt, 128], BF16, 2)
    pr = mk_ring(S, "fpr", [128, 512], F32, 2, psum=True)
    pi_ = mk_ring(S, "fpi", [128, 512], F32, 2, psum=True)
    if is_filter:
        ko = mk_ring(S, "fko", [128, 2, 512], F32, 2)
    else:
        ks = mk_ring(S, "fks", [128, 2, 512], F32, 2)
        tt = mk_ring(S, "ftt", [128, 4, 512], F32, 2)
    for fcn in range(nt):
        m = fm.next()
        for cs in range(2):
            S.dma("sp" if cs == 0 else "pool", m.ap[:, cs, :, :], C.fwd[si][fcn, cs, :, :, :], writes=[m])
        a, b = pr.next(), pi_.next()
        for cs, p in ((0, a), (1, b)):
            for tc in range(nt):
                S.op("pe", lambda e: e.matmul(p.ap[:], m.ap[:, cs, tc, :], a_in.ap[:, tc, cs if is_filter else 0, :], start=(tc == 0), stop=(tc == nt - 1)),
                     reads=[m, a_in], writes=[p])
        if is_filter:
            o = ko.next()
            S.op("dve", lambda e: e.scalar_tensor_tensor(out=o.ap[:, 0, :], in0=a.ap[:], scalar=2.0 / N, in1=bb.ap[:], op0=ALU.mult, op1=ALU.add),
                 reads=[a, bb], writes=[o])
            S.op("act", lambda e: e.activation(out=o.ap[:, 1, :], in_=b.ap[:], func=AF.Identity, scale=2.0 / N), reads=[b], writes=[o])
            S.dma("pool", dst[fcn * 128:(fcn + 1) * 128, :, :], o.ap[:], reads=[o])
        else:
            k = ks.next()
            S.dma("sp", k.ap[:], C.kspec[si][fcn * 128:(fcn + 1) * 128, :, :], writes=[k])
            t = tt.next()
            S.op("dve", lambda e: e.tensor_tensor(out=t.ap[:, 0, :], in0=a.ap[:], in1=k.ap[:, 0, :], op=ALU.mult), reads=[a, k], writes=[t])
            S.op("dve", lambda e: e.tensor_tensor(out=t.ap[:, 1, :], in0=b.ap[:], in1=k.ap[:, 1, :], op=ALU.mult), reads=[b, k], writes=[t])
            S.op("dve", lambda e: e.tensor_tensor(out=t.ap[:, 2, :], in0=a.ap[:], in1=k.ap[:, 1, :], op=ALU.mult), reads=[a, k], writes=[t])
            S.op("dve", lambda e: e.tensor_tensor(out=t.ap[:, 3, :], in0=b.ap[:], in1=k.ap[:, 0, :], op=ALU.mult), reads=[b, k], writes=[t])
            S.op("pool", lambda e: e.tensor_tensor(out=dst.ap[:, fcn, 0, :], in0=t.ap[:, 0, :], in1=t.ap[:, 1, :], op=ALU.add), reads=[t], writes=[dst])
            S.op("pool", lambda e: e.tensor_tensor(out=dst.ap[:, fcn, 1, :], in0=t.ap[:, 2, :], in1=t.ap[:, 3, :], op=ALU.subtract), reads=[t], writes=[dst])
    if is_filter:
        S.end_phase()


def l0_inproj(S, C, bi, src_stage):
    S.begin_phase()
    ident = C.ident
    hT = S.sb("ihT", [128, 8, T + 4], BF16)
    S.op("pool", lambda e: e.memset(hT.ap[:], 0.0), writes=[hT])
    xin = mk_ring(S, "ixin", [128, 1024], F32, 2)
    ptr = mk_ring(S, "iptr", [128, 512], F32, 2, psum=True)
    for i in range(NTT):
        xi = xin.next()
        S.dma("sp", xi.ap[:], tile_src(C, src_stage, bi, i), writes=[xi])
        transpose_mod(S, C, xi, hT, colof(i), 0, 0, (C.R - 1 if i < 2 else bi), ptr, ident)
    S.begin_sub()
    wt = S.sb("iwt", [128, 3, 8, 1024], BF16)
    wv = S.sb("iwv", [128, 8, 128], BF16)
    for j in range(3):
        for kc in range(8):
            S.dma("sp" if kc % 2 else "pool", wt.ap[:, j, kc, :], C.whb[j][kc * 128:(kc + 1) * 128, 512:1536], writes=[wt])
    for kc in range(8):
        S.dma("sp", wv.ap[:, kc, :], C.wqb[kc * 128:(kc + 1) * 128, 640:768], writes=[wv])
    pa = mk_ring(S, "ipa", [128, 512], F32, 2, psum=True)
    pb = mk_ring(S, "ipb", [128, 512], F32, 2, psum=True)
    x1s = mk_ring(S, "ix1", [128, 512], F32, 2)
    ut = mk_ring(S, "iut", [128, 512], BF16, 2)
    vt = mk_ring(S, "ivt", [128, 128], BF16, 2)
    for i in range(NTT):
        c0 = colof(i)
        a, b = pa.next(), pb.next()
        for half, p in ((0, a), (1, b)):
            n = 0
            for j in range(3):
                for kc in range(8):
                    S.op("pe", lambda e: e.matmul(p.ap[:], hT.ap[:, kc, c0 + j - 1:c0 + j - 1 + 128], wt.ap[:, j, kc, half * 512:(half + 1) * 512],
                                                  start=(n == 0), stop=(n == 23)), reads=[hT, wt], writes=[p])
                    n += 1
        x1 = x1s.next()
        S.op("act", lambda e: e.activation(out=x1.ap[:], in_=a.ap[:], func=AF.Identity), reads=[a], writes=[x1])
        u = ut.next()
        S.op("dve", lambda e: e.tensor_tensor(out=u.ap[:], in0=b.ap[:], in1=x1.ap[:], op=ALU.mult), reads=[b, x1], writes=[u])
        S.dma("pool", C.u[bi, i * 128:(i + 1) * 128, :], u.ap[:], reads=[u])
        p = pa.next()
        for kc in range(8):
            S.op("pe", lambda e: e.matmul(p.ap[:, :128], hT.ap[:, kc, c0:c0 + 128], wv.ap[:, kc, :], start=(kc == 0), stop=(kc == 7)),
                 reads=[hT, wv], writes=[p])
        v = vt.next()
        S.op("act", lambda e: e.activation(out=v.ap[:], in_=p.ap[:, :128], func=AF.Identity), reads=[p], writes=[v])
        S.dma("pool", C.v[bi, i * 128:(i + 1) * 128, :], v.ap[:], reads=[v])
    S.end_sub()
    S.begin_sub()
    w0 = S.sb("iw0", [128, 3, 8, 512], BF16)
    wq = S.sb("iwq", [128, 8, 1280], BF16)
    rope = S.sb("irope", [64, 2, LAT], F32)
    S.dma("sp", rope.ap[:], C.rope[:, :, :], writes=[rope])
    for j in range(3):
        for kc in range(8):
            S.dma("sp" if kc % 2 else "pool", w0.ap[:, j, kc, :], C.whb[j][kc * 128:(kc + 1) * 128, 0:512], writes=[w0])
    for kc in range(8):
        S.dma("sp", wq.ap[:, kc, 0:640], C.wqb[kc * 128:(kc + 1) * 128, 0:640], writes=[wq])
        S.dma("pool", wq.ap[:, kc, 640:1280], C.wqb[kc * 128:(kc + 1) * 128, 768:1408], writes=[wq])
    pa = mk_ring(S, "jpa", [128, 512], F32, 3, psum=True)
    ot = mk_ring(S, "jot", [128, 512], BF16, 3)
    t1 = mk_ring(S, "jt1", [64, 512], F32, 2)
    t2 = mk_ring(S, "jt2", [64, 512], F32, 2)
    tiles = [(0, 256)] + [(256 + 512 * k, 512) for k in range(4)]
    for (tok0, w) in tiles:
        c0 = colof(tok0 // 128)
        for cc in range(4):
            p = pa.next()
            n = 0
            for j in range(3):
                for kc in range(8):
                    S.op("pe", lambda e: e.matmul(p.ap[:, :w], w0.ap[:, j, kc, cc * 128:(cc + 1) * 128], hT.ap[:, kc, c0 + j - 1:c0 + j - 1 + w],
                                                  start=(n == 0), stop=(n == 23)), reads=[w0, hT], writes=[p])
                    n += 1
            o = ot.next()
            S.op("act", lambda e: e.activation(out=o.ap[:, :w], in_=p.ap[:, :w], func=AF.Identity), reads=[p], writes=[o])
            S.dma("pool", C.x0T[bi, cc * 128:(cc + 1) * 128, tok0:tok0 + w], o.ap[:, :w], reads=[o])
        for hh in range(10):
            p = pa.next()
            for kc in range(8):
                S.op("pe", lambda e: e.matmul(p.ap[:64, :w], wq.ap[:, kc, hh * 64:(hh + 1) * 64], hT.ap[:, kc, c0:c0 + w], start=(kc == 0), stop=(kc == 7)),
                     reads=[wq, hT], writes=[p])
            o = ot.next()
            dst = C.qT[bi, hh, :, tok0:tok0 + w] if hh < 8 else C.kT[bi, hh - 8, :, tok0:tok0 + w]
            if tok0 == 0:
                S.op("act", lambda e: e.activation(out=o.ap[:64, :w], in_=p.ap[:64, :w], func=AF.Identity), reads=[p], writes=[o])
            else:
                p2 = pa.next()
                for kc in range(8):
                    S.op("pe", lambda e: e.matmul(p2.ap[:64, :w], wq.ap[:, kc, 640 + hh * 64:640 + (hh + 1) * 64], hT.ap[:, kc, c0:c0 + w],
                                                  start=(kc == 0), stop=(kc == 7)), reads=[wq, hT], writes=[p2])
                l0_ = tok0 - 256
                a, b = t1.next(), t2.next()
                S.op("dve", lambda e: e.tensor_tensor(out=a.ap[:, :w], in0=p.ap[:64, :w], in1=rope.ap[:, 0, l0_:l0_ + w], op=ALU.mult), reads=[p, rope], writes=[a])
                S.op("dve", lambda e: e.tensor_tensor(out=b.ap[:, :w], in0=p2.ap[:64, :w], in1=rope.ap[:, 1, l0_:l0_ + w], op=ALU.mult), reads=[p2, rope], writes=[b])
                S.op("pool", lambda e: e.tensor_tensor(out=o.ap[:64, :w], in0=a.ap[:, :w], in1=b.ap[:, :w], op=ALU.add), reads=[a, b], writes=[o])
            S.dma("sp", dst, o.ap[:64, :w], reads=[o])
    S.end_sub()
    S.end_phase()


def l0_hyena(S, C, bi):
    for si, (Lf, tok0) in enumerate(((LAT, 256), (CTX, 0))):
        S.begin_phase()
        nt = Lf // 128
        Y = S.sb("hyY", [128, nt, 2, 512], BF16)
        S.begin_sub()
        hyena_fwd(S, C, si, Lf, C.u[bi, tok0:tok0 + Lf, :], bi, Y, is_filter=False)
        S.end_sub()
        tw = min(512, Lf)
        iv = mk_ring(S, "hyiv", [128, nt, 2, tw], BF16, 2 if Lf == CTX else 1)
        x0 = mk_ring(S, "hyx0", [128, tw], BF16, 2)
        ya = mk_ring(S, "hyya", [128, tw], BF16, 2)
        pp = mk_ring(S, "hypp", [128, 512], F32, 2, psum=True)
        for tt in range(Lf // tw):
            m = iv.next()
            for fc in range(nt):
                S.dma("sp" if fc % 2 else "pool", m.ap[:, fc, :, :], C.inv[si][tt, :, fc, :, :], writes=[m])
            for cc in range(4):
                p = pp.next()
                n = 0
                for fc in range(nt):
                    for cs in range(2):
                        S.op("pe", lambda e: e.matmul(p.ap[:, :tw], Y.ap[:, fc, cs, cc * 128:(cc + 1) * 128], m.ap[:, fc, cs, :],
                                                      start=(n == 0), stop=(n == 2 * nt - 1)), reads=[Y, m], writes=[p])
                        n += 1
                xz = x0.next()
                t0_ = tok0 + tt * tw
                S.dma("sp", xz.ap[:], C.x0T[bi, cc * 128:(cc + 1) * 128, t0_:t0_ + tw], writes=[xz])
                o = ya.next()
                S.op("dve", lambda e: e.tensor_tensor(out=o.ap[:], in0=p.ap[:, :tw], in1=xz.ap[:], op=ALU.mult), reads=[p, xz], writes=[o])
                S.dma("pool", C.yaT[bi, cc * 128:(cc + 1) * 128, t0_:t0_ + tw], o.ap[:], reads=[o])
        S.end_phase()


def l0_attn(S, C, bi):
    S.begin_phase()
    qT = S.sb("aqT", [64, 8, T], BF16)
    kT = S.sb("akT", [64, 2, T], BF16)
    v = S.sb("av", [128, NTT, 128], BF16)
    yb = S.sb("ayb", [64, 8, T], BF16)
    mask = S.sb("amask", [128, 384], F32)
    sink = S.sb("asink", [128, 8], F32)
    idb = S.sb("aidb", [128, 128], BF16)
    for h in range(8):
        S.dma("sp" if h % 2 else "pool", qT.ap[:, h, :], C.qT[bi, h, :, :], writes=[qT])
    for h in range(2):
        S.dma("sp", kT.ap[:, h, :], C.kT[bi, h, :, :], writes=[kT])
    S.dma("sp", v.ap[:], C.v[bi].rearrange("(i p) c -> p i c", p=128), writes=[v])
    S.dma("sp", mask.ap[:], C.amask[:, :], writes=[mask])
    S.dma("sp", sink.ap[:], C.attn_sink[:].partition_broadcast(128), writes=[sink])
    S.dma("sp", idb.ap[:], C.ident_b[:, :], writes=[idb])
    psl = mk_ring(S, "apsl", [128, 512], F32, 2, psum=True)
    psc = mk_ring(S, "apsc", [128, 512], F32, 2, psum=True)
    ppt = mk_ring(S, "appt", [128, 5, 128], BF16, 2, psum=True)
    ppv = mk_ring(S, "appv", [128, 128], F32, 2, psum=True)
    sc = mk_ring(S, "asc", [128, 640], F32, 2)
    pe_ = mk_ring(S, "ape", [128, 640], F32, 2)
    pn = mk_ring(S, "apn", [128, 640], BF16, 2)
    pts = mk_ring(S, "apts", [128, 5, 128], BF16, 2)
    sm = mk_ring(S, "asm", [128, 8], F32, 4)
    for qb in range(NTT):
        is_ctx = qb < 2
        q0 = qb * 128
        if is_ctx:
            n = 0
            ktiles = []
        else:
            lq = q0 - 256
            lo, hi = max(0, lq - 128), min(LAT, lq + 256)
            n = hi - lo
            m0 = lo - (lq - 128)
            ktiles = [2 + lo // 128 + j for j in range(n // 128)]
        ktiles = ktiles + [0, 1]
        nk = n + 256
        for hh in range(8):
            h = hh // 4
            s_ = sc.next()
            if n:
                a = psl.next()
                S.op("pe", lambda e: e.matmul(a.ap[:, :n], qT.ap[:, hh, q0:q0 + 128], kT.ap[:, h, 256 + lo:256 + hi], start=True, stop=True),
                     reads=[qT, kT], writes=[a])
                S.op("dve", lambda e: e.tensor_tensor(out=s_.ap[:, :n], in0=a.ap[:, :n], in1=mask.ap[:, m0:m0 + n], op=ALU.add), reads=[a, mask], writes=[s_])
            b = psc.next()
            S.op("pe", lambda e: e.matmul(b.ap[:, :256], qT.ap[:, hh, q0:q0 + 128], kT.ap[:, h, 0:256], start=True, stop=True), reads=[qT, kT], writes=[b])
            S.op("act", lambda e: e.activation(out=s_.ap[:, n:nk], in_=b.ap[:, :256], func=AF.Identity), reads=[b], writes=[s_])
            w = sm.next()
            S.op("dve", lambda e: e.tensor_reduce(out=w.ap[:, 0:1], in_=s_.ap[:, :nk], axis=AX.X, op=ALU.max), reads=[s_], writes=[w])
            S.op("dve", lambda e: e.tensor_scalar(out=w.ap[:, 1:2], in0=w.ap[:, 0:1], scalar1=0.125, scalar2=sink.ap[:, hh:hh + 1], op0=ALU.mult, op1=ALU.max),
                 reads=[w, sink], writes=[w])
            S.op("dve", lambda e: e.tensor_scalar(out=w.ap[:, 2:3], in0=w.ap[:, 1:2], scalar1=-1.0, scalar2=None, op0=ALU.mult), reads=[w], writes=[w])
            p_ = pe_.next()
            S.op("act", lambda e: e.activation(out=p_.ap[:, :nk], in_=s_.ap[:, :nk], func=AF.Exp, scale=0.125, bias=w.ap[:, 2:3]),
                 reads=[s_, w], writes=[p_])
            S.op("dve", lambda e: e.tensor_reduce(out=w.ap[:, 3:4], in_=p_.ap[:, :nk], axis=AX.X, op=ALU.add), reads=[p_], writes=[w])
            S.op("act", lambda e: e.activation(out=w.ap[:, 4:5], in_=w.ap[:, 2:3], func=AF.Exp, bias=sink.ap[:, hh:hh + 1], scale=1.0), reads=[w, sink], writes=[w])
            S.op("dve", lambda e: e.tensor_tensor(out=w.ap[:, 5:6], in0=w.ap[:, 3:4], in1=w.ap[:, 4:5], op=ALU.add), reads=[w], writes=[w])
            S.op("dve", lambda e: e.reciprocal(out=w.ap[:, 6:7], in_=w.ap[:, 5:6]), reads=[w], writes=[w])
            pn_ = pn.next()
            S.op("dve", lambda e: e.tensor_scalar(out=pn_.ap[:, :nk], in0=p_.ap[:, :nk], scalar1=w.ap[:, 6:7], scalar2=None, op0=ALU.mult), reads=[p_, w], writes=[pn_])
            pt = ppt.next()
            nch = nk // 128
            for j in range(nch):
                S.op("pe", lambda e: e.transpose(out=pt.ap[:, j, :], in_=pn_.ap[:, j * 128:(j + 1) * 128], identity=idb.ap[:]), reads=[pn_, idb], writes=[pt])
            ps_ = pts.next()
            S.op("act", lambda e: e.activation(out=ps_.ap[:, :nch, :], in_=pt.ap[:, :nch, :], func=AF.Identity), reads=[pt], writes=[ps_])
            o = ppv.next()
            for j in range(nch):
                S.op("pe", lambda e: e.matmul(o.ap[:64, :], v.ap[:, ktiles[j], h * 64:(h + 1) * 64], ps_.ap[:, j, :], start=(j == 0), stop=(j == nch - 1)),
                     reads=[v, ps_], writes=[o])
            S.op("pool" if False else "dve", lambda e: e.tensor_copy(out=yb.ap[:, hh, q0:q0 + 128], in_=o.ap[:64, :]), reads=[o], writes=[yb])
    for h in range(8):
        S.dma("sp" if h % 2 else "pool", C.ybT[bi, h, :, :], yb.ap[:, h, :], reads=[yb])
    S.end_phase()


def l0_outproj(S, C, bi, src_stage, dst_stage):
    S.begin_phase()
    ya = S.sb("oya", [128, 4, T], BF16)
    yb = S.sb("oyb", [64, 8, T], BF16)
    wa = S.sb("owa", [128, 4, D], BF16)
    wb = S.sb("owb", [64, 8, D], BF16)
    for c in range(4):
        S.dma("sp", ya.ap[:, c, :], C.yaT[bi, c * 128:(c + 1) * 128, :], writes=[ya])
        S.dma("pool", wa.ap[:, c, :], C.wob0[c * 128:(c + 1) * 128, :], writes=[wa])
    for h in range(8):
        S.dma("sp", yb.ap[:, h, :], C.ybT[bi, h, :, :], writes=[yb])
        S.dma("pool", wb.ap[:, h, :], C.wob0[512 + h * 64:512 + (h + 1) * 64, :], writes=[wb])
    epi = Epi(S, C, "o")
    epi.load(0, 0, bi)
    xin = mk_ring(S, "oxin", [128, 1024], F32, 3)
    py = mk_ring(S, "opy", [128, 1024], F32, 2, psum=True)
    for i in range(NTT):
        xi = xin.next()
        S.dma("sp", xi.ap[:], tile_src(C, src_stage, bi, i), writes=[xi])
        p = py.next()
        for hf in range(2):
            for c in range(4):
                S.op("pe", lambda e: e.matmul(p.ap[:, hf * 512:(hf + 1) * 512], ya.ap[:, c, i * 128:(i + 1) * 128], wa.ap[:, c, hf * 512:(hf + 1) * 512],
                                              start=(c == 0), stop=False), reads=[ya, wa], writes=[p])
            for h in range(8):
                S.op("pe", lambda e: e.matmul(p.ap[:, hf * 512:(hf + 1) * 512], yb.ap[:, h, i * 128:(i + 1) * 128], wb.ap[:, h, hf * 512:(hf + 1) * 512],
                                              start=False, stop=(h == 7)), reads=[yb, wb], writes=[p])
        epi.run(p, xi, i < 2, C.xs[dst_stage - 1][bi, i * 128:(i + 1) * 128, :])
    S.end_phase()


def V(b, *idx):
    return (b, b.ap[idx] if idx else b.ap[:])


def TT(S, eng, o, a, b, op):
    return S.op(eng, lambda e: e.tensor_tensor(out=o[1], in0=a[1], in1=b[1], op=op), reads=[a[0], b[0]], writes=[o[0]])


def TS(S, eng, o, a, s1, s2, op0, op1=None, extra=()):
    if op1 is None:
        return S.op(eng, lambda e: e.tensor_scalar(out=o[1], in0=a[1], scalar1=s1, scalar2=None, op0=op0), reads=[a[0], *extra], writes=[o[0]])
    return S.op(eng, lambda e: e.tensor_scalar(out=o[1], in0=a[1], scalar1=s1, scalar2=s2, op0=op0, op1=op1), reads=[a[0], *extra], writes=[o[0]])


def STT(S, o, a, sc, b, op0, op1, extra=()):
    return S.op("dve", lambda e: e.scalar_tensor_tensor(out=o[1], in0=a[1], scalar=sc, in1=b[1], op0=op0, op1=op1), reads=[a[0], b[0], *extra], writes=[o[0]])


def ACT(S, o, a, func, scale=1.0, bias=None, extra=()):
    if bias is None:
        return S.op("act", lambda e: e.activation(out=o[1], in_=a[1], func=func, scale=scale), reads=[a[0], *extra], writes=[o[0]])
    return S.op("act", lambda e: e.activation(out=o[1], in_=a[1], func=func, scale=scale, bias=bias), reads=[a[0], *extra], writes=[o[0]])


def MM(S, o, l, r, start=True, stop=True):
    return S.op("pe", lambda e: e.matmul(o[1], l[1], r[1], start=start, stop=stop), reads=[l[0], r[0]], writes=[o[0]])


def TR(S, o, a, ident):
    return S.op("pe", lambda e: e.transpose(out=o[1], in_=a[1], identity=ident[1]), reads=[a[0], ident[0]], writes=[o[0]])


RW_E = float(np.exp(-0.5))


def declare_l1(nc, C, dbg=None):
    NB = C.NB
    I = lambda n, s, dt=F32: dram(nc, n, s, dt, "ExternalInput")
    Sx = lambda n, s, dt=BF16: dram(nc, n, s, dt, "ExternalOutput" if (dbg and n in dbg) else None)
    C.o_w_rw = I("o_w_rw", [D, 1920])
    C.o_w_dn = I("o_w_dn", [D, 1536])
    C.o_w_z = I("o_w_z", [D, 528])
    C.o_w_out = I("o_w_out", [D, D])
    C.rw_mu = I("rw_mu", [1920])
    C.dn_conv = I("dn_conv", [3, 1536])
    C.rw_rows = I("rw_rows", [10, 512])
    C.rw_w2 = I("rw_w2", [128, 512])
    C.rw_a2 = I("rw_a2", [128, 512])
    C.rw_g2 = I("rw_g2", [128, 512])
    C.dn_rows = I("dn_rows", [3, 8])
    C.dn_ng = I("dn_ng", [512])
    C.tri = I("tri", [2, 128, 128])
    C.tris = I("tris", [2, 128, 128])
    C.blkm = I("blkm", [4, 128, 128])
    C.tsw = Sx("tsw", [3, 1920], F32)
    C.wrb = [Sx(f"wrb{j}", [D, 1920]) for j in range(3)]
    C.wdb = [Sx(f"wdb{j}", [D, 1536]) for j in range(3)]
    C.wzb = Sx("wzb", [D, 528])
    C.wob1 = Sx("wob1", [D, D])
    C.rw_ops = Sx("rw_ops", [NB, 2, 6, T, 512])
    C.rw_v = Sx("rw_v", [NB, T, 512])
    C.rw_gc = Sx("rw_gc", [NB, 2, NTT, 64, 8], F32)
    C.rw_g = Sx("rw_g", [NB, T, 512], F32)
    C.rw_bonus = Sx("rw_bonus", [NB, T, 512], F32)
    C.y_rw = Sx("y_rw", [NB, 2, T, 512], F32)
    C.dn_qk = Sx("dn_qk", [NB, 2, T, 512])
    C.dn_ops = Sx("dn_ops", [NB, 2, 5, T, 512])
    C.dn_G = Sx("dn_G", [NB, 2, 3, T, 4], F32)
    C.dn_z = Sx("dn_z", [NB, T, 512], F32)
    C.y_dn = Sx("y_dn", [NB, 2, T, 512], F32)


def host_l1(inputs, m):
    f = lambda a: np.ascontiguousarray(np.asarray(a, dtype=np.float32))
    w = np.asarray(inputs["o_w_in"])[0]
    m["o_w_rw"] = f(w[:, :1920])
    m["o_w_dn"] = f(w[:, 1920:1920 + 1536])
    m["o_w_z"] = f(w[:, 1920 + 1536:])
    m["o_w_out"] = f(np.asarray(inputs["o_w_out"])[0])
    m["rw_mu"] = f(np.asarray(inputs["rw_mu"])[0])
    m["dn_conv"] = f(np.asarray(inputs["dn_conv"])[0])
    g = lambda k: np.asarray(inputs[k])[0]
    m["rw_rows"] = f(np.stack([g("rw_w0")[0], g("rw_w0")[1], g("rw_a0")[0], g("rw_a0")[1], g("rw_kk"), g("rw_ka"),
                               g("rw_rk").reshape(512), g("rw_lnx_g"), g("rw_lnx_b"), np.zeros(512, np.float32)], 0))
    m["rw_w2"] = f(g("rw_w2").reshape(128, 512))
    m["rw_a2"] = f(g("rw_a2").reshape(128, 512))
    m["rw_g2"] = f(g("rw_g2"))
    m["dn_rows"] = f(np.stack([g("dn_A_log").reshape(8), g("dn_dt_bias").reshape(8), np.zeros(8, np.float32)], 0))
    m["dn_ng"] = f(np.tile(g("dn_norm_g"), 4))
    j = np.arange(128)[:, None]
    t = np.arange(128)[None, :]
    m["tri"] = f(np.stack([(j <= t), (j >= t)], 0))
    m["tris"] = f(np.stack([(j < t), (j > t)], 0))
    bd = lambda n: (j // n == t // n)
    m["blkm"] = f(np.stack([bd(16), bd(32) & ~bd(16), bd(64) & ~bd(32), ~bd(64)], 0))
    return m


def l1_setup(S, C):
    S.begin_phase()
    mu = S.sb("smu", [1, 1920], F32)
    o = S.sb("smo", [1, 3, 1920], F32)
    S.dma("sp", mu.ap[:], C.rw_mu[:].partition_broadcast(1), writes=[mu])
    TS(S, "dve", V(o, slice(None), 0, slice(None)), V(mu), 0.5, None, ALU.mult)
    TS(S, "dve", V(o, slice(None), 1, slice(None)), V(mu), -1.0, 1.0, ALU.mult, ALU.add)
    TS(S, "dve", V(o, slice(None), 2, slice(None)), V(mu), 0.5, None, ALU.mult)
    S.dma("sp", C.tsw.rearrange("(o j) n -> o j n", o=1), o.ap[:], reads=[o])
    S.end_phase()
    for j in range(3):
        prep_weight(S, C.wrb[j], C.o_w_rw, D, 1920, scale=C.tsw[j, :], tag=f"qr{j}")
        prep_weight(S, C.wdb[j], C.o_w_dn, D, 1536, scale=C.dn_conv[j, :], tag=f"qd{j}")
    prep_weight(S, C.wzb, C.o_w_z, D, 528, tag="qz")
    prep_weight(S, C.wob1, C.o_w_out, D, D, tag="qo")


def build_hT(S, C, bi, src_stage, l):
    hT = S.sb("bhT", [128, 8, T + 4], BF16)
    S.op("pool", lambda e: e.memset(hT.ap[:], 0.0), writes=[hT])
    S.begin_sub()
    xin = mk_ring(S, "bxin", [128, 1024], F32, 2)
    ptr = mk_ring(S, "bptr", [128, 512], F32, 2, psum=True)
    for i in range(NTT):
        xi = xin.next()
        S.dma("sp", xi.ap[:], tile_src(C, src_stage, bi, i), writes=[xi])
        transpose_mod(S, C, xi, hT, colof(i), l, 0, (C.R - 1 if i < 2 else bi), ptr, C.ident)
    S.end_sub()
    return hT


def bc3(b, n, w):
    return (b, b.ap[:, 0:n].unsqueeze(2).to_broadcast([128, n, w]))


def r3(b, n, *pre):
    ap = b.ap[pre] if pre else b.ap[:]
    return (b, ap.rearrange("p (h d) -> p h d", h=n))


def proj3(S, C, p, hT, c0, w, wt, col0, ncol):
    n = 0
    for j in range(3):
        for kc in range(8):
            MM(S, (p, p.ap[:, :ncol]), (hT, hT.ap[:, kc, c0 + j - 1:c0 + j - 1 + 128]), (wt, wt.ap[:, j, kc, col0:col0 + ncol]), start=(n == 0), stop=(n == 23))
            n += 1


def l1_feat_rw(S, C, bi, hT):
    S.begin_sub()
    ones = S.sb("fones", [128, 128], F32)
    S.op("pool", lambda e: e.memset(ones.ap[:], 1.0), writes=[ones])
    tri = S.sb("ftri", [128, 2, 128], F32)
    for d in range(2):
        S.dma("sp", tri.ap[:, d, :], C.tri[d, :, :], writes=[tri])
    rows = S.sb("frows", [128, 9, 512], F32)
    for q in range(9):
        S.dma("sp", rows.ap[:, q, :], C.rw_rows[q, :].partition_broadcast(128), writes=[rows])
    lw = S.sb("flw", [128, 3, 512], F32)
    lwb = S.sb("flwb", [128, 3, 512], BF16)
    for q, src in enumerate((C.rw_w2, C.rw_a2, C.rw_g2)):
        S.dma("sp", lw.ap[:, q, :], src[:, :], writes=[lw])
    S.op("dve", lambda e: e.tensor_copy(out=lwb.ap[:], in_=lw.ap[:]), reads=[lw], writes=[lwb])
    loraT = S.sb("floraT", [128, 3, T], BF16)
    S.begin_sub()
    wl = S.sb("fwl", [128, 3, 8, 384], BF16)
    for j in range(3):
        for kc in range(8):
            S.dma("sp" if kc % 2 else "pool", wl.ap[:, j, kc, :], C.wrb[j][kc * 128:(kc + 1) * 128, 1536:1920], writes=[wl])
    pp = mk_ring(S, "fpp", [128, 512], F32, 2, psum=True)
    for (tok0, w) in [(0, 256)] + [(256 + 512 * k, 512) for k in range(4)]:
        c0 = colof(tok0 // 128)
        for q, fn in enumerate((AF.Tanh, AF.Identity, AF.Sigmoid)):
            p = pp.next()
            n = 0
            for j in range(3):
                for kc in range(8):
                    MM(S, (p, p.ap[:, :w]), (wl, wl.ap[:, j, kc, q * 128:(q + 1) * 128]), (hT, hT.ap[:, kc, c0 + j - 1:c0 + j - 1 + w]), start=(n == 0), stop=(n == 23))
                    n += 1
            ACT(S, (loraT, loraT.ap[:, q, tok0:tok0 + w]), (p, p.ap[:, :w]), fn)
    S.end_sub()
    wr = S.sb("fwr", [128, 3, 8, 1536], BF16)
    for j in range(3):
        for kc in range(8):
            S.dma("sp" if kc % 2 else "pool", wr.ap[:, j, kc, :], C.wrb[j][kc * 128:(kc + 1) * 128, 0:1536], writes=[wr])
    prkv = [S.ps(f"fp{n}", [128, 512], F32) for n in "rkv"]
    pq = mk_ring(S, "fpq", [128, 512], F32, 3, psum=True)
    pgc = S.ps("fpgc", [64, 8], F32)
    F = lambda n: S.sb("f_" + n, [128, 512], F32)
    rs, ks, kkr, sq, kk, zt, logw, a_, Gs, eG, enG, eE, t1, kd, bd, ksum = [F(n) for n in
        "rs ks kkr sq kk zt logw a Gs eG enG eE t1 kd bd ksum".split()]
    eP, tb, bon, gs = kkr, sq, zt, t1
    vb = mk_ring(S, "fvb", [128, 512], BF16, 2)
    ot = mk_ring(S, "fot", [128, 6, 512], BF16, 2)
    sm = mk_ring(S, "fsm", [128, 16], F32, 2)
    gcs = mk_ring(S, "fgcs", [64, 8], F32, 2)
    W0, A0, KK, KA, RK = [(rows, rows.ap[:, q, :]) for q in (0, 2, 4, 5, 6)]
    for i in range(NTT):
        c0 = colof(i)
        rws = slice(i * 128, (i + 1) * 128)
        for n in range(3):
            proj3(S, C, prkv[n], hT, c0, 128, wr, n * 512, 512)
        ACT(S, V(rs), V(prkv[0]), AF.Identity)
        ACT(S, V(ks), V(prkv[1]), AF.Identity)
        v_ = vb.next()
        ACT(S, V(v_), V(prkv[2]), AF.Identity)
        S.dma("pool", C.rw_v[bi, rws, :], v_.ap[:], reads=[v_])
        w = sm.next()
        TT(S, "dve", V(kkr), V(ks), KK, ALU.mult)
        TT(S, "pool", V(sq), V(kkr), V(kkr), ALU.mult)
        S.op("dve", lambda e: e.tensor_reduce(out=w.ap[:, 0:8], in_=r3(sq, 8)[1], axis=AX.X, op=ALU.add), reads=[sq], writes=[w])
        TS(S, "dve", V(w, slice(None), slice(0, 8)), V(w, slice(None), slice(0, 8)), 1e-6, None, ALU.add)
        ACT(S, V(w, slice(None), slice(0, 8)), V(w, slice(None), slice(0, 8)), AF.Sqrt)
        S.op("dve", lambda e: e.reciprocal(out=w.ap[:, 0:8], in_=w.ap[:, 0:8]), reads=[w], writes=[w])
        TT(S, "dve", r3(kk, 8), r3(kkr, 8), bc3(w, 8, 64), ALU.mult)
        for d in range(2):
            o = ot.next()
            pz, pa_ = pq.next(), pq.next()
            MM(S, V(pz), (loraT, loraT.ap[d * 64:(d + 1) * 64, 0, rws]), (lwb, lwb.ap[d * 64:(d + 1) * 64, 0, :]))
            MM(S, V(pa_), (loraT, loraT.ap[d * 64:(d + 1) * 64, 1, rws]), (lwb, lwb.ap[d * 64:(d + 1) * 64, 1, :]))
            TT(S, "dve", V(zt), V(pz), (rows, rows.ap[:, d, :]), ALU.add)
            ACT(S, V(zt), V(zt), AF.Sigmoid)
            TS(S, "pool", V(logw), V(zt), -RW_E, None, ALU.mult)
            TT(S, "dve", V(a_), V(pa_), (rows, rows.ap[:, 2 + d, :]), ALU.add)
            ACT(S, V(a_), V(a_), AF.Sigmoid)
            pG, pT = pq.next(), pq.next()
            MM(S, V(pG), (tri, tri.ap[:, d, :]), V(logw))
            MM(S, V(pT), V(ones), V(logw))
            for h in range(8):
                MM(S, (pgc, pgc.ap[:, h:h + 1]), (logw, logw.ap[:, h * 64:(h + 1) * 64]), (ones, ones.ap[:, 0:1]))
            gc = gcs.next()
            ACT(S, V(gc), V(pgc), AF.Exp)
            S.dma("pool", C.rw_gc[bi, d, i, :, :], gc.ap[:], reads=[gc])
            ACT(S, V(Gs), V(pG), AF.Identity)
            ACT(S, V(eG), V(Gs), AF.Exp)
            ACT(S, V(enG), V(Gs), AF.Exp, scale=-1.0)
            TT(S, "dve", V(eP), V(Gs), V(logw), ALU.subtract)
            ACT(S, V(eP), V(eP), AF.Exp)
            TT(S, "dve", V(eE), V(pT), V(Gs), ALU.subtract)
            ACT(S, V(eE), V(eE), AF.Exp)
            STT(S, V(t1), V(a_), -1.0, KA, ALU.add, ALU.mult)
            STT(S, V(kd), V(t1), 1.0, V(ks), ALU.add, ALU.mult)
            TT(S, "pool", V(bd), V(kk), V(a_), ALU.mult)
            O = lambda q: (o, o.ap[:, q, :])
            TT(S, "dve", O(0), V(rs), V(eG), ALU.mult)
            TT(S, "pool", O(1), V(kd), V(enG), ALU.mult)
            TT(S, "dve", O(2), V(bd), V(enG), ALU.mult)
            TT(S, "pool", O(3), V(kk), V(eP), ALU.mult)
            TT(S, "dve", O(4), V(kd), V(eE), ALU.mult)
            TT(S, "pool", O(5), V(bd), V(eE), ALU.mult)
            S.dma("sp", C.rw_ops[bi, d, :, rws, :].rearrange("q t c -> t q c"), o.ap[:], reads=[o])
            if d == 0:
                S.op("pool", lambda e: e.tensor_copy(out=ksum.ap[:], in_=kd.ap[:]), reads=[kd], writes=[ksum])
            else:
                TT(S, "pool", V(ksum), V(ksum), V(kd), ALU.add)
        TT(S, "dve", V(tb), V(rs), V(ksum), ALU.mult)
        TT(S, "pool", V(tb), V(tb), RK, ALU.mult)
        S.op("dve", lambda e: e.tensor_reduce(out=w.ap[:, 8:16], in_=r3(tb, 8)[1], axis=AX.X, op=ALU.add), reads=[tb], writes=[w])
        TT(S, "dve", r3(bon, 8), r3(v_, 8), (w, w.ap[:, 8:16].unsqueeze(2).to_broadcast([128, 8, 64])), ALU.mult)
        S.dma("pool", C.rw_bonus[bi, rws, :], bon.ap[:], reads=[bon])
        pg = pq.next()
        MM(S, V(pg), (loraT, loraT.ap[:, 2, rws]), (lwb, lwb.ap[:, 2, :]))
        ACT(S, V(gs), V(pg), AF.Identity)
        S.dma("pool", C.rw_g[bi, rws, :], gs.ap[:], reads=[gs])
    S.end_sub()


def l1_feat_dn(S, C, bi, hT):
    S.begin_sub()
    ones = S.sb("gones", [128, 128], F32)
    S.op("pool", lambda e: e.memset(ones.ap[:], 1.0), writes=[ones])
    tri = S.sb("gtri", [128, 2, 128], F32)
    for d in range(2):
        S.dma("sp", tri.ap[:, d, :], C.tri[d, :, :], writes=[tri])
    dr = S.sb("gdr", [128, 2, 8], F32)
    for q in range(2):
        S.dma("sp", dr.ap[:, q, :], C.dn_rows[q, :].partition_broadcast(128), writes=[dr])
    ACT(S, V(dr, slice(None), 0, slice(None)), V(dr, slice(None), 0, slice(None)), AF.Exp)
    TS(S, "dve", V(dr, slice(None), 0, slice(None)), V(dr, slice(None), 0, slice(None)), -1.0, None, ALU.mult)
    wd = S.sb("gwd", [128, 3, 8, 1536], BF16)
    wz = S.sb("gwz", [128, 8, 528], BF16)
    for j in range(3):
        for kc in range(8):
            S.dma("sp" if kc % 2 else "pool", wd.ap[:, j, kc, :], C.wdb[j][kc * 128:(kc + 1) * 128, :], writes=[wd])
    for kc in range(8):
        S.dma("sp", wz.ap[:, kc, :], C.wzb[kc * 128:(kc + 1) * 128, :], writes=[wz])
    pqkv = [S.ps(f"gp{n}", [128, 512], F32) for n in "qkv"]
    pz = S.ps("gpz", [128, 512], F32)
    pgt = S.ps("gpgt", [128, 16], F32)
    pG = mk_ring(S, "gpG", [128, 8], F32, 2, psum=True)
    F = lambda n: S.sb("g_" + n, [128, 512], F32)
    qs, ks, vs, sq, zs = [F(n) for n in "qs ks vs sq zs".split()]
    qk = mk_ring(S, "gqk", [128, 2, 512], BF16, 2)
    ot = mk_ring(S, "got", [128, 5, 512], BF16, 2)
    sm = mk_ring(S, "gsm", [128, 64], F32, 2)
    go = mk_ring(S, "ggo", [128, 3, 4], F32, 2)
    for i in range(NTT):
        c0 = colof(i)
        rws = slice(i * 128, (i + 1) * 128)
        for n in range(3):
            proj3(S, C, pqkv[n], hT, c0, 128, wd, n * 512, 512)
        for kc in range(8):
            MM(S, V(pz), (hT, hT.ap[:, kc, c0:c0 + 128]), (wz, wz.ap[:, kc, 0:512]), start=(kc == 0), stop=(kc == 7))
        for kc in range(8):
            MM(S, V(pgt), (hT, hT.ap[:, kc, c0:c0 + 128]), (wz, wz.ap[:, kc, 512:528]), start=(kc == 0), stop=(kc == 7))
        for src, dst in zip(pqkv + [pz], (qs, ks, vs, zs)):
            ACT(S, V(dst), V(src), AF.Silu)
        S.dma("pool", C.dn_z[bi, rws, :], zs.ap[:], reads=[zs])
        w = sm.next()
        Wc = lambda a, b: (w, w.ap[:, a:b])
        ACT(S, Wc(0, 16), V(pgt), AF.Identity)
        qk_ = qk.next()
        for n, (src, sc) in enumerate(((qs, 128.0 ** -0.5), (ks, 1.0))):
            TT(S, "pool", V(sq), V(src), V(src), ALU.mult)
            S.op("dve", lambda e: e.tensor_reduce(out=w.ap[:, 16 + 4 * n:20 + 4 * n], in_=r3(sq, 4)[1], axis=AX.X, op=ALU.add), reads=[sq], writes=[w])
            TS(S, "dve", Wc(16 + 4 * n, 20 + 4 * n), Wc(16 + 4 * n, 20 + 4 * n), 1e-6, None, ALU.add)
            ACT(S, Wc(16 + 4 * n, 20 + 4 * n), Wc(16 + 4 * n, 20 + 4 * n), AF.Sqrt)
            S.op("dve", lambda e: e.reciprocal(out=w.ap[:, 16 + 4 * n:20 + 4 * n], in_=w.ap[:, 16 + 4 * n:20 + 4 * n]), reads=[w], writes=[w])
            if sc != 1.0:
                TS(S, "dve", Wc(16, 20), Wc(16, 20), sc, None, ALU.mult)
            TT(S, "dve", r3(src, 4), r3(src, 4), (w, w.ap[:, 16 + 4 * n:20 + 4 * n].unsqueeze(2).to_broadcast([128, 4, 128])), ALU.mult)
            S.op("pool", lambda e: e.tensor_copy(out=qk_.ap[:, n, :], in_=src.ap[:]), reads=[src], writes=[qk_])
        S.dma("sp", C.dn_qk[bi, :, rws, :].rearrange("q t c -> t q c"), qk_.ap[:], reads=[qk_])
        for d in range(2):
            o = ot.next()
            g_ = go.next()
            TT(S, "dve", Wc(24, 28), Wc(d * 4, d * 4 + 4), (dr, dr.ap[:, 1, d * 4:d * 4 + 4]), ALU.add)
            ACT(S, Wc(24, 28), Wc(24, 28), AF.Exp)
            ACT(S, Wc(24, 28), Wc(24, 28), AF.Ln, bias=1.0)
            TT(S, "dve", (g_, g_.ap[:, 0, :]), Wc(24, 28), (dr, dr.ap[:, 0, d * 4:d * 4 + 4]), ALU.mult)
            ACT(S, Wc(28, 32), Wc(8 + d * 4, 12 + d * 4), AF.Sigmoid)
            p = pG.next()
            MM(S, (p, p.ap[:, 0:4]), (tri, tri.ap[:, d, :]), (g_, g_.ap[:, 0, :]))
            MM(S, (p, p.ap[:, 4:8]), V(ones), (g_, g_.ap[:, 0, :]))
            ACT(S, (g_, g_.ap[:, 1:3, :]), (p, p.ap[:, 0:8].rearrange("p (a b) -> p a b", a=2)), AF.Identity)
            S.dma("pool", C.dn_G[bi, d, :, rws, :].rearrange("q t c -> t q c"), g_.ap[:], reads=[g_])
            ACT(S, Wc(32, 36), (g_, g_.ap[:, 1, :]), AF.Exp)
            TT(S, "dve", Wc(36, 40), (g_, g_.ap[:, 2, :]), (g_, g_.ap[:, 1, :]), ALU.subtract)
            ACT(S, Wc(36, 40), Wc(36, 40), AF.Exp)
            TT(S, "dve", Wc(40, 44), Wc(28, 32), Wc(32, 36), ALU.mult)
            B4 = lambda a: (w, w.ap[:, a:a + 4].unsqueeze(2).to_broadcast([128, 4, 128]))
            O = lambda q: (o, o.ap[:, q, :].rearrange("p (h d) -> p h d", h=4))
            TT(S, "dve", O(0), r3(qs, 4), B4(32), ALU.mult)
            TT(S, "pool", O(1), r3(ks, 4), B4(28), ALU.mult)
            TT(S, "dve", O(2), r3(ks, 4), B4(40), ALU.mult)
            TT(S, "pool", O(3), r3(ks, 4), B4(36), ALU.mult)
            TT(S, "dve", O(4), r3(vs, 4), B4(28), ALU.mult)
            S.dma("sp", C.dn_ops[bi, d, :, rws, :].rearrange("q t c -> t q c"), o.ap[:], reads=[o])
    S.end_sub()


def inv_group(S, P, PT, K, ps, nh=4):
    mk = lambda: K.rW.next()
    PD, PDT, Z, ZT = mk(), mk(), K.rZ.next(), K.rZ.next()
    TT(S, "dve", V(PD), V(P), V(K.mb[0]), ALU.mult)
    TT(S, "pool", V(PDT), V(PT), V(K.mb[0]), ALU.mult)
    TT(S, "dve", V(Z), V(K.ident4), V(PD), ALU.subtract)
    TT(S, "pool", V(ZT), V(K.ident4), V(PDT), ALU.subtract)
    cur, curT = PD, PDT
    for lv in range(3):
        pA, pB, pC, pD = ps
        for h in range(nh):
            MM(S, (pA, pA.ap[:, h, :]), (curT, curT.ap[:, h, :]), (cur, cur.ap[:, h, :]))
        for h in range(nh):
            MM(S, (pB, pB.ap[:, h, :]), (cur, cur.ap[:, h, :]), (curT, curT.ap[:, h, :]))
        Pn, PTn = mk(), mk()
        ACT(S, V(Pn), V(pA), AF.Identity)
        S.op("dve", lambda e: e.tensor_copy(out=PTn.ap[:], in_=pB.ap[:]), reads=[pB], writes=[PTn])
        for h in range(nh):
            MM(S, (pC, pC.ap[:, h, :]), (PTn, PTn.ap[:, h, :]), (Z, Z.ap[:, h, :]))
        for h in range(nh):
            MM(S, (pD, pD.ap[:, h, :]), (Pn, Pn.ap[:, h, :]), (ZT, ZT.ap[:, h, :]))
        TT(S, "dve", V(Z), V(Z), V(pC), ALU.add)
        TT(S, "dve", V(ZT), V(ZT), V(pD), ALU.add)
        cur, curT = Pn, PTn
    for m in range(1, 4):
        last = m == 3
        pA, pB, pC, pD = ps
        O, OT, Y = mk(), mk(), mk()
        TT(S, "dve", V(O), V(P), V(K.mb[m]), ALU.mult)
        TT(S, "pool", V(OT), V(PT), V(K.mb[m]), ALU.mult)
        for h in range(nh):
            MM(S, (pA, pA.ap[:, h, :]), (OT, OT.ap[:, h, :]), (Z, Z.ap[:, h, :]))
        ACT(S, V(Y), V(pA), AF.Identity)
        if not last:
            YT = mk()
            for h in range(nh):
                MM(S, (pB, pB.ap[:, h, :]), (Z, Z.ap[:, h, :]), (OT, OT.ap[:, h, :]))
            S.op("dve", lambda e: e.tensor_copy(out=YT.ap[:], in_=pB.ap[:]), reads=[pB], writes=[YT])
        for h in range(nh):
            MM(S, (pC, pC.ap[:, h, :]), (ZT, ZT.ap[:, h, :]), (Y, Y.ap[:, h, :]))
        if not last:
            for h in range(nh):
                MM(S, (pD, pD.ap[:, h, :]), (Y, Y.ap[:, h, :]), (ZT, ZT.ap[:, h, :]))
        TT(S, "dve", V(Z), V(Z), V(pC), ALU.subtract)
        if not last:
            TT(S, "dve", V(ZT), V(ZT), V(pD), ALU.subtract)
    return Z


def scan_consts(S, C, tag):
    K = Ctx()
    K.idb = S.sb(tag + "idb", [128, 128], BF16)
    S.dma("sp", K.idb.ap[:], C.ident_b[:, :], writes=[K.idb])
    K.ident4 = S.sb(tag + "id4", [128, 4, 128], F32)
    K.mSI = [S.sb(tag + f"mSI{d}", [128, 4, 2, 128], F32) for d in range(2)]
    K.mS = [S.sb(tag + f"mS{d}", [128, 4, 128], F32) for d in range(2)]
    K.mI = [S.sb(tag + f"mI{d}", [128, 4, 128], F32) for d in range(2)]
    for h in range(4):
        S.dma("sp", K.ident4.ap[:, h, :], C.ident_d[:, :], writes=[K.ident4])
        for d in range(2):
            S.dma("sp", K.mSI[d].ap[:, h, 0, :], C.tris[d, :, :], writes=[K.mSI[d]])
            S.dma("pool", K.mSI[d].ap[:, h, 1, :], C.tri[d, :, :], writes=[K.mSI[d]])
            S.dma("sp", K.mS[d].ap[:, h, :], C.tris[d, :, :], writes=[K.mS[d]])
            S.dma("pool", K.mI[d].ap[:, h, :], C.tri[d, :, :], writes=[K.mI[d]])
    K.rP = mk_ring(S, tag + "rP", [128, 4, 128], F32, 2)
    K.rPT = mk_ring(S, tag + "rPT", [128, 4, 128], F32, 2)
    K.rW = mk_ring(S, tag + "rW", [128, 4, 128], F32, 8)
    K.rZ = mk_ring(S, tag + "rZ", [128, 4, 128], F32, 4)
    K.mb = [S.sb(tag + f"mb{m}", [128, 4, 128], F32) for m in range(4)]
    for m in range(4):
        for h in range(4):
            S.dma("sp" if h % 2 else "pool", K.mb[m].ap[:, h, :], C.blkm[m, :, :], writes=[K.mb[m]])
    return K


def tile_order(d):
    return list(range(NTT)) if d == 0 else [1, 0] + list(range(NTT - 1, 1, -1))


def l1_scan_rw(S, C, bi):
    S.begin_phase()
    K = scan_consts(S, C, "r")
    psA = mk_ring(S, "rpsA", [128, 4, 128], F32, 1, psum=True)
    psB = mk_ring(S, "rpsB", [128, 4, 128], F32, 1, psum=True)
    psC = mk_ring(S, "rpsC", [128, 4, 128], F32, 1, psum=True)
    psD = mk_ring(S, "rpsD", [128, 4, 128], F32, 1, psum=True)
    ps4 = (psA.bufs[0], psB.bufs[0], psC.bufs[0], psD.bufs[0])
    pBN = S.ps("rpBN", [128, 4, 2, 128], F32)
    pKN = pBN
    ptr = S.ps("rptr", [64, 8, 128], BF16)
    ot_r = mk_ring(S, "rot", [128, 6, 512], BF16, 2)
    vt_r = mk_ring(S, "rvt", [128, 512], BF16, 2)
    gc_r = mk_ring(S, "rgc", [64, 8], F32, 2)
    AR_r = mk_ring(S, "rAR", [64, 8, 2, 128], BF16, 2)
    KT_r = mk_ring(S, "rKT", [64, 8, 128], BF16, 2)
    BT_r = mk_ring(S, "rBT", [64, 8, 128], BF16, 2)
    KN_r = mk_ring(S, "rKN", [128, 8, 2, 128], BF16, 2)
    NB_r = mk_ring(S, "rNB", [128, 8, 128], BF16, 2)
    Zb_r = mk_ring(S, "rZb", [128, 4, 128], BF16, 2)
    WT_r = mk_ring(S, "rWT", [64, 8, 128], BF16, 2)
    Xs_r = mk_ring(S, "rXs", [128, 4, 64], BF16, 2)
    nU0_r = mk_ring(S, "rnU0", [128, 8, 64], F32, 2)
    nU_r = mk_ring(S, "rnU", [128, 8, 64], BF16, 2)
    ys_r = mk_ring(S, "rys", [128, 512], F32, 2)
    ST = S.sb("rST", [64, 8, 64], F32)
    STb = S.sb("rSTb", [64, 8, 64], BF16)
    for d in range(2):
        S.op("dve", lambda e: e.memset(ST.ap[:], 0.0), writes=[ST])
        S.op("dve", lambda e: e.memset(STb.ap[:], 0.0), writes=[STb])
        for i in tile_order(d):
            rws = slice(i * 128, (i + 1) * 128)
            ot, vt, gc = ot_r.next(), vt_r.next(), gc_r.next()
            S.dma("sp", ot.ap[:], C.rw_ops[bi, d, :, rws, :].rearrange("q t c -> t q c"), writes=[ot])
            S.dma("pool", vt.ap[:], C.rw_v[bi, rws, :], writes=[vt])
            S.dma("pool", gc.ap[:], C.rw_gc[bi, d, i, :, :], writes=[gc])
            AR, KT, BT = AR_r.next(), KT_r.next(), BT_r.next()
            for q, dst in ((3, (AR, AR.ap[:, :, 0, :])), (0, (AR, AR.ap[:, :, 1, :])), (1, V(KT)), (2, V(BT))):
                for h in range(8):
                    TR(S, (ptr, ptr.ap[:, h, :]), (ot, ot.ap[:, q, h * 64:(h + 1) * 64]), V(K.idb))
                if q in (3, 1):
                    ACT(S, dst, V(ptr), AF.Identity)
                else:
                    S.op("dve", lambda e: e.tensor_copy(out=dst[1], in_=ptr.ap[:]), reads=[ptr], writes=[dst[0]])
            KN, NBm, WT, nU0 = KN_r.next(), NB_r.next(), WT_r.next(), nU0_r.next()
            for g in range(2):
                for hl in range(4):
                    h = g * 4 + hl
                    MM(S, (pBN, pBN.ap[:, hl, :, :]), (BT, BT.ap[:, h, :]), (AR, AR.ap[:, h, :, :]))
                P, PT = K.rP.next(), K.rPT.next()
                TT(S, "dve", V(P), (pBN, pBN.ap[:, :, 0, :]), V(K.mS[d]), ALU.mult)
                TT(S, "dve", (NBm, NBm.ap[:, g * 4:(g + 1) * 4, :]), (pBN, pBN.ap[:, :, 1, :]), V(K.mI[d]), ALU.mult)
                for hl in range(4):
                    h = g * 4 + hl
                    MM(S, (pKN, pKN.ap[:, hl, :, :]), (KT, KT.ap[:, h, :]), (AR, AR.ap[:, h, :, :]))
                TT(S, "dve", (KN, KN.ap[:, g * 4:(g + 1) * 4, :, :]), V(pKN), V(K.mSI[d]), ALU.mult)
                pB = psB.next()
                for hl in range(4):
                    h = g * 4 + hl
                    MM(S, (pB, pB.ap[:, hl, :]), (AR, AR.ap[:, h, 0, :]), (BT, BT.ap[:, h, :]))
                TT(S, "dve", V(PT), V(pB), V(K.mS[1 - d]), ALU.mult)
                Z = inv_group(S, P, PT, K, ps4)
                Zb = Zb_r.next()
                ACT(S, V(Zb), V(Z), AF.Identity)
                pW = psA.next()
                for hl in range(4):
                    h = g * 4 + hl
                    MM(S, (pW, pW.ap[:64, hl, :]), (ot, ot.ap[:, 3, h * 64:(h + 1) * 64]), (Zb, Zb.ap[:, hl, :]))
                ACT(S, (WT, WT.ap[:, g * 4:(g + 1) * 4, :]), (pW, pW.ap[:64, :, :]), AF.Identity)
                pX = psB.next()
                for hl in range(4):
                    h = g * 4 + hl
                    MM(S, (pX, pX.ap[:, hl, 0:64]), (KN, KN.ap[:, h, 0, :]), (vt, vt.ap[:, h * 64:(h + 1) * 64]))
                Xs = Xs_r.next()
                ACT(S, V(Xs), (pX, pX.ap[:, :, 0:64]), AF.Identity)
                pU0 = psC.next()
                for hl in range(4):
                    MM(S, (pU0, pU0.ap[:, hl, 0:64]), (Zb, Zb.ap[:, hl, :]), (Xs, Xs.ap[:, hl, :]))
                TS(S, "dve", (nU0, nU0.ap[:, g * 4:(g + 1) * 4, :]), (pU0, pU0.ap[:, :, 0:64]), -1.0, None, ALU.mult)
            pU = psA.next()
            pUv = pU.ap[:].rearrange("p a b -> p (a b)").rearrange("p (h v) -> p h v", v=64)
            for h in range(8):
                MM(S, (pU, pUv[:, h, :]), (WT, WT.ap[:, h, :]), (STb, STb.ap[:, h, :]))
            nU = nU_r.next()
            TT(S, "dve", V(nU), V(nU0), (pU, pUv), ALU.subtract)
            pY = psB.next()
            pYv = pY.ap[:].rearrange("p a b -> p (a b)").rearrange("p (h v) -> p h v", v=64)
            for h in range(8):
                MM(S, (pY, pYv[:, h, :]), (AR, AR.ap[:, h, 1, :]), (STb, STb.ap[:, h, :]), start=True, stop=False)
                MM(S, (pY, pYv[:, h, :]), (KN, KN.ap[:, h, 1, :]), (vt, vt.ap[:, h * 64:(h + 1) * 64]), start=False, stop=False)
                MM(S, (pY, pYv[:, h, :]), (NBm, NBm.ap[:, h, :]), (nU, nU.ap[:, h, :]), start=False, stop=True)
            ys = ys_r.next()
            ACT(S, r3(ys, 8), (pY, pYv), AF.Identity)
            S.dma("sp", C.y_rw[bi, d, rws, :], ys.ap[:], reads=[ys])
            pS = psC.next()
            pSv = pS.ap[:].rearrange("p a b -> p (a b)").rearrange("p (h v) -> p h v", v=64)
            for h in range(8):
                MM(S, (pS, pSv[:64, h, :]), (ot, ot.ap[:, 4, h * 64:(h + 1) * 64]), (vt, vt.ap[:, h * 64:(h + 1) * 64]), start=True, stop=False)
                MM(S, (pS, pSv[:64, h, :]), (ot, ot.ap[:, 5, h * 64:(h + 1) * 64]), (nU, nU.ap[:, h, :]), start=False, stop=True)
            TT(S, "dve", V(ST), V(ST), (gc, gc.ap[:, 0:8].unsqueeze(2).to_broadcast([64, 8, 64])), ALU.mult)
            TT(S, "dve", V(ST), V(ST), (pS, pSv[:64, :, :]), ALU.add)
            ACT(S, V(STb), V(ST), AF.Identity)
    S.end_phase()


def l1_scan_dn(S, C, bi):
    S.begin_phase()
    K = scan_consts(S, C, "d")
    ones = S.sb("dones", [128, 128], F32)
    S.op("pool", lambda e: e.memset(ones.ap[:], 1.0), writes=[ones])
    tri = S.sb("dtri", [128, 2, 128], F32)
    for d in range(2):
        S.dma("sp", tri.ap[:, d, :], C.tri[d, :, :], writes=[tri])
    psA = mk_ring(S, "dpsA", [128, 4, 128], F32, 1, psum=True)
    psB = mk_ring(S, "dpsB", [128, 4, 128], F32, 1, psum=True)
    psC = mk_ring(S, "dpsC", [128, 4, 128], F32, 1, psum=True)
    psD = mk_ring(S, "dpsD", [128, 4, 128], F32, 1, psum=True)
    ps4 = (psA.bufs[0], psB.bufs[0], psC.bufs[0], psD.bufs[0])
    ptr = mk_ring(S, "dptr", [128, 4, 128], BF16, 2, psum=True)
    ot_r = mk_ring(S, "dot", [128, 5, 512], BF16, 2)
    qk_r = mk_ring(S, "dqk", [128, 2, 512], BF16, 2)
    G_r = mk_ring(S, "dG", [128, 3, 4], F32, 2)
    FT_r = mk_ring(S, "dFT", [128, 4, 4, 128], BF16, 2)
    gl_r = mk_ring(S, "dgl", [128, 4, 128], F32, 2)
    ET_r = mk_ring(S, "dET", [128, 4, 128], F32, 2)
    qkT_r = mk_ring(S, "dqkT", [128, 4, 128], BF16, 2)
    Zb_r = mk_ring(S, "dZb", [128, 4, 128], BF16, 2)
    wT_r = mk_ring(S, "dwT", [128, 4, 128], BF16, 2)
    u0_r = mk_ring(S, "du0", [128, 4, 128], F32, 2)
    u_r = mk_ring(S, "du", [128, 4, 128], BF16, 2)
    ys_r = mk_ring(S, "dys", [128, 512], F32, 2)
    sm_r = mk_ring(S, "dsm", [128, 8], F32, 2)
    ST = S.sb("dST", [128, 4, 128], F32)
    STb = S.sb("dSTb", [128, 4, 128], BF16)
    for d in range(2):
        S.op("dve", lambda e: e.memset(ST.ap[:], 0.0), writes=[ST])
        S.op("dve", lambda e: e.memset(STb.ap[:], 0.0), writes=[STb])
        for i in tile_order(d):
            rws = slice(i * 128, (i + 1) * 128)
            ot, qk, G = ot_r.next(), qk_r.next(), G_r.next()
            S.dma("sp", ot.ap[:], C.dn_ops[bi, d, :, rws, :].rearrange("q t c -> t q c"), writes=[ot])
            S.dma("pool", qk.ap[:], C.dn_qk[bi, :, rws, :].rearrange("q t c -> t q c"), writes=[qk])
            S.dma("pool", G.ap[:], C.dn_G[bi, d, :, rws, :].rearrange("q t c -> t q c"), writes=[G])
            FT = FT_r.next()
            for q, src in enumerate(((qk, 1), (qk, 0), (ot, 1), (ot, 0))):
                p = ptr.next()
                for h in range(4):
                    TR(S, (p, p.ap[:, h, :]), (src[0], src[0].ap[:, src[1], h * 128:(h + 1) * 128]), V(K.idb))
                if q % 2:
                    ACT(S, (FT, FT.ap[:, q, :, :]), V(p), AF.Identity)
                else:
                    S.op("dve", lambda e: e.tensor_copy(out=FT.ap[:, q, :, :], in_=p.ap[:]), reads=[p], writes=[FT])
            gl = gl_r.next()
            for h in range(4):
                TS(S, "dve", (gl, gl.ap[:, h, :]), (tri, tri.ap[:, d, :]), G.ap[:, 0, h:h + 1], None, ALU.mult, extra=[G])
            pG = psA.next()
            for h in range(4):
                MM(S, (pG, pG.ap[:, h, :]), V(ones), (gl, gl.ap[:, h, :]))
            ET = ET_r.next()
            for h in range(4):
                TS(S, "dve", (ET, ET.ap[:, h, :]), (pG, pG.ap[:, h, :]), G.ap[:, 1, h:h + 1], 0.0, ALU.subtract, ALU.min, extra=[G])
            ACT(S, V(ET), V(ET), AF.Exp)
            pA_, pQ = psB.next(), psC.next()
            for h in range(4):
                MM(S, (pA_, pA_.ap[:, h, :]), (FT, FT.ap[:, 0, h, :]), (FT, FT.ap[:, 2, h, :]))
                MM(S, (pQ, pQ.ap[:, h, :]), (FT, FT.ap[:, 0, h, :]), (FT, FT.ap[:, 1, h, :]))
            P, PT = K.rP.next(), K.rPT.next()
            TT(S, "dve", V(P), V(pA_), V(ET), ALU.mult)
            TT(S, "dve", V(P), V(P), V(K.mS[d]), ALU.mult)
            qkT = qkT_r.next()
            TT(S, "dve", V(ET), V(ET), V(K.mI[d]), ALU.mult)
            TT(S, "dve", V(qkT), V(pQ), V(ET), ALU.mult)
            pT_ = psA.next()
            for h in range(4):
                S.op("pe", lambda e: e.transpose(out=pT_.ap[:, h, :], in_=P.ap[:, h, :], identity=C.ident.ap[:]), reads=[P, C.ident], writes=[pT_])
            ACT(S, V(PT), V(pT_), AF.Identity)
            Z = inv_group(S, P, PT, K, ps4)
            Zb = Zb_r.next()
            ACT(S, V(Zb), V(Z), AF.Identity)
            pu0, pw = psB.next(), psC.next()
            for h in range(4):
                MM(S, (pu0, pu0.ap[:, h, :]), (Zb, Zb.ap[:, h, :]), (ot, ot.ap[:, 4, h * 128:(h + 1) * 128]))
                MM(S, (pw, pw.ap[:, h, :]), (ot, ot.ap[:, 2, h * 128:(h + 1) * 128]), (Zb, Zb.ap[:, h, :]))
            u0, wT = u0_r.next(), wT_r.next()
            ACT(S, V(u0), V(pu0), AF.Identity)
            S.op("dve", lambda e: e.tensor_copy(out=wT.ap[:], in_=pw.ap[:]), reads=[pw], writes=[wT])
            pU = psA.next()
            for h in range(4):
                MM(S, (pU, pU.ap[:, h, :]), (wT, wT.ap[:, h, :]), (STb, STb.ap[:, h, :]))
            u = u_r.next()
            TT(S, "dve", V(u), V(u0), V(pU), ALU.subtract)
            pY = psB.next()
            for h in range(4):
                MM(S, (pY, pY.ap[:, h, :]), (FT, FT.ap[:, 3, h, :]), (STb, STb.ap[:, h, :]), start=True, stop=False)
                MM(S, (pY, pY.ap[:, h, :]), (qkT, qkT.ap[:, h, :]), (u, u.ap[:, h, :]), start=False, stop=True)
            ys = ys_r.next()
            ACT(S, r3(ys, 4), V(pY), AF.Identity)
            S.dma("sp", C.y_dn[bi, d, rws, :], ys.ap[:], reads=[ys])
            pS = psC.next()
            for h in range(4):
                MM(S, (pS, pS.ap[:, h, :]), (ot, ot.ap[:, 3, h * 128:(h + 1) * 128]), (u, u.ap[:, h, :]))
            sm = sm_r.next()
            ACT(S, (sm, sm.ap[:, 0:4]), (G, G.ap[:, 2, :]), AF.Exp)
            TT(S, "dve", V(ST), V(ST), (sm, sm.ap[:, 0:4].unsqueeze(2).to_broadcast([128, 4, 128])), ALU.mult)
            TT(S, "dve", V(ST), V(ST), V(pS), ALU.add)
            ACT(S, V(STb), V(ST), AF.Identity)
    S.end_phase()


def l1_out(S, C, bi, src_stage, dst_stage, last):
    S.begin_phase()
    wo = S.sb("xwo", [128, 8, D], BF16)
    for c in range(8):
        S.dma("sp" if c % 2 else "pool", wo.ap[:, c, :], C.wob1[c * 128:(c + 1) * 128, :], writes=[wo])
    rows = S.sb("xrows", [128, 3, 512], F32)
    S.dma("sp", rows.ap[:, 0, :], C.rw_rows[7, :].partition_broadcast(128), writes=[rows])
    S.dma("sp", rows.ap[:, 1, :], C.rw_rows[8, :].partition_broadcast(128), writes=[rows])
    S.dma("sp", rows.ap[:, 2, :], C.dn_ng[:].partition_broadcast(128), writes=[rows])
    epi = Epi(S, C, "x")
    epi.load(1, 0, bi)
    xin = mk_ring(S, "xxin", [128, 1024], F32, 2)
    ya = mk_ring(S, "xya", [128, 2, 512], F32, 2)
    yb = mk_ring(S, "xyb", [128, 2, 512], F32, 2)
    ex = mk_ring(S, "xex", [128, 3, 512], F32, 2)
    sq = S.sb("xsq", [128, 512], F32)
    ycat = mk_ring(S, "xyc", [128, 1024], F32, 2)
    yT = mk_ring(S, "xyT", [128, 8, 128], BF16, 2)
    sm = mk_ring(S, "xsm", [128, 32], F32, 2)
    ptr = mk_ring(S, "xptr", [128, 4, 128], F32, 2, psum=True)
    py = mk_ring(S, "xpy", [128, 1024], F32, 2, psum=True)
    for i in range(2 if last else 0, NTT):
        rws = slice(i * 128, (i + 1) * 128)
        xi, a, b, e_, yc, w = xin.next(), ya.next(), yb.next(), ex.next(), ycat.next(), sm.next()
        S.dma("sp", xi.ap[:], tile_src(C, src_stage, bi, i), writes=[xi])
        S.dma("sp", a.ap[:], C.y_rw[bi, :, rws, :].rearrange("q t c -> t q c"), writes=[a])
        S.dma("pool", b.ap[:], C.y_dn[bi, :, rws, :].rearrange("q t c -> t q c"), writes=[b])
        S.dma("sp", e_.ap[:, 0, :], C.rw_g[bi, rws, :], writes=[e_])
        S.dma("pool", e_.ap[:, 1, :], C.rw_bonus[bi, rws, :], writes=[e_])
        S.dma("sp", e_.ap[:, 2, :], C.dn_z[bi, rws, :], writes=[e_])
        y = (a, a.ap[:, 0, :])
        TT(S, "dve", y, y, (a, a.ap[:, 1, :]), ALU.add)
        S.op("dve", lambda e: e.tensor_reduce(out=w.ap[:, 0:8], in_=r3(a, 8, slice(None), 0, slice(None))[1], axis=AX.X, op=ALU.add), reads=[a], writes=[w])
        TS(S, "dve", (w, w.ap[:, 0:8]), (w, w.ap[:, 0:8]), 1.0 / 64, None, ALU.mult)
        TT(S, "dve", r3(a, 8, slice(None), 0, slice(None)), r3(a, 8, slice(None), 0, slice(None)), bc3(w, 8, 64), ALU.subtract)
        TT(S, "pool", V(sq), y, y, ALU.mult)
        S.op("dve", lambda e: e.tensor_reduce(out=w.ap[:, 8:16], in_=r3(sq, 8)[1], axis=AX.X, op=ALU.add), reads=[sq], writes=[w])
        TS(S, "dve", (w, w.ap[:, 8:16]), (w, w.ap[:, 8:16]), 1.0 / 64, 64e-5, ALU.mult, ALU.add)
        ACT(S, (w, w.ap[:, 8:16]), (w, w.ap[:, 8:16]), AF.Sqrt)
        S.op("dve", lambda e: e.reciprocal(out=w.ap[:, 8:16], in_=w.ap[:, 8:16]), reads=[w], writes=[w])
        TT(S, "dve", r3(a, 8, slice(None), 0, slice(None)), r3(a, 8, slice(None), 0, slice(None)),
           (w, w.ap[:, 8:16].unsqueeze(2).to_broadcast([128, 8, 64])), ALU.mult)
        TT(S, "pool", y, y, (rows, rows.ap[:, 0, :]), ALU.mult)
        TT(S, "pool", y, y, (rows, rows.ap[:, 1, :]), ALU.add)
        TT(S, "dve", y, y, (e_, e_.ap[:, 1, :]), ALU.add)
        TT(S, "dve", (yc, yc.ap[:, 0:512]), y, (e_, e_.ap[:, 0, :]), ALU.mult)
        o = (b, b.ap[:, 0, :])
        TT(S, "dve", o, o, (b, b.ap[:, 1, :]), ALU.add)
        TT(S, "pool", V(sq), o, o, ALU.mult)
        S.op("dve", lambda e: e.tensor_reduce(out=w.ap[:, 16:20], in_=r3(sq, 4)[1], axis=AX.X, op=ALU.add), reads=[sq], writes=[w])
        TS(S, "dve", (w, w.ap[:, 16:20]), (w, w.ap[:, 16:20]), 1.0 / 128, 1e-6, ALU.mult, ALU.add)
        ACT(S, (w, w.ap[:, 16:20]), (w, w.ap[:, 16:20]), AF.Sqrt)
        S.op("dve", lambda e: e.reciprocal(out=w.ap[:, 16:20], in_=w.ap[:, 16:20]), reads=[w], writes=[w])
        TT(S, "dve", r3(b, 4, slice(None), 0, slice(None)), r3(b, 4, slice(None), 0, slice(None)),
           (w, w.ap[:, 16:20].unsqueeze(2).to_broadcast([128, 4, 128])), ALU.mult)
        TT(S, "pool", o, o, (rows, rows.ap[:, 2, :]), ALU.mult)
        TT(S, "dve", (yc, yc.ap[:, 512:1024]), o, (e_, e_.ap[:, 2, :]), ALU.mult)
        yt = yT.next()
        for g in range(2):
            p = ptr.next()
            for j in range(4):
                c = g * 4 + j
                S.op("pe", lambda e: e.transpose(out=p.ap[:, j, :], in_=yc.ap[:, c * 128:(c + 1) * 128], identity=C.ident.ap[:]), reads=[yc, C.ident], writes=[p])
            ACT(S, (yt, yt.ap[:, g * 4:(g + 1) * 4, :]), V(p), AF.Identity)
        p = py.next()
        for hf in range(2):
            for c in range(8):
                MM(S, (p, p.ap[:, hf * 512:(hf + 1) * 512]), (yt, yt.ap[:, c, :]), (wo, wo.ap[:, c, hf * 512:(hf + 1) * 512]), start=(c == 0), stop=(c == 7))
        if last:
            dst = C.xs[dst_stage - 1][bi, rws, :]
        else:
            dst = C.xs[dst_stage - 1][bi, rws, :]
        epi.run(p, xi, i < 2, dst)
    S.end_phase()


NB_FULL = 4
N_CORES = 8


def build_full(NB, dbg=None):
    nc = bass.Bass("TRN2", target_bir_lowering=False)
    C = declare_common(nc, NB, dbg=dbg)
    declare_l0(nc, C, dbg=dbg)
    declare_l1(nc, C, dbg=dbg)
    S = Sched(nc)
    common_setup(S, C)
    phase_mod(S, C)
    for l in range(2):
        prep_weight(S, C.w1b[l], C.mlp_w1[l], D, DFF, tag=f"pm1{l}")
        prep_weight(S, C.w2b[l], C.mlp_w2[l], DFF, D, tag=f"pm2{l}")
    l0_setup(S, C)
    l1_setup(S, C)
    for bi in range(NB):
        l0_inproj(S, C, bi, 0)
        l0_hyena(S, C, bi)
        l0_attn(S, C, bi)
        l0_outproj(S, C, bi, 0, 1)
        phase_mlp(S, C, 0, bi, 1, 2, False)
        S.begin_phase()
        hT = build_hT(S, C, bi, 2, 1)
        l1_feat_rw(S, C, bi, hT)
        l1_feat_dn(S, C, bi, hT)
        S.end_phase()
        l1_scan_rw(S, C, bi)
        l1_scan_dn(S, C, bi)
        l1_out(S, C, bi, 2, 3, True)
        phase_mlp(S, C, 1, bi, 3, None, True)
    S.finish()
    return nc


def kernel(**inputs):
    NB = NB_FULL
    nc = build_full(NB)
    in_maps = [host_l1(inputs, host_l0(inputs, host_common(inputs, c, NB))) for c in range(N_CORES)]
    res = run_bass_kernel_spmd(nc, in_maps, core_ids=list(range(N_CORES)))
    out = np.concatenate([np.asarray(r["out"], dtype=np.float32) for r in res.results], axis=0)
    return out
```

```python
import numpy as np
from contextlib import ExitStack
import concourse.bass as bass
import concourse.mybir as mybir
from concourse.bass_utils import run_bass_kernel_spmd

F32 = mybir.dt.float32
BF16 = mybir.dt.bfloat16
AF = mybir.ActivationFunctionType
ALU = mybir.AluOpType
AX = mybir.AxisListType

SAME_ENGINE_SYNC = True
NO_SWDGE = True
EPOCH = 30000


class Buf:
    __slots__ = ("ap", "name", "lw", "rd")

    def __init__(self, ap, name):
        self.ap = ap
        self.name = name
        self.lw = None
        self.rd = []


class Sched:
    def __init__(self, nc, ndma=24):
        self.nc = nc
        self.stack = ExitStack()
        self.engs = {"pe": nc.tensor, "act": nc.scalar, "dve": nc.vector, "pool": nc.gpsimd, "sp": nc.sync}
        self.csem = {}
        self.ccnt = {}
        self.nsem = 0
        for e in ("pe", "act", "dve", "pool"):
            self._new_csem(e)
        self.dq = {}
        for q in ("sp", "pool", "act"):
            self.dq[q] = [[self._sem(f"d_{q}{i}"), 0] for i in range(ndma)]
        self.dqi = {q: 0 for q in self.dq}
        self.waited = {}
        self.phase_stack = None
        self.ninst = 0

    def _sem(self, name):
        self.nsem += 1
        return self.stack.enter_context(self.nc.semaphore(name))

    def _new_csem(self, e):
        self.csem[e] = self._sem(f"c_{e}_{self.nsem}")
        self.ccnt[e] = 0

    def sb(self, name, shape, dtype, persist=False):
        st = self.stack if (persist or self.phase_stack is None) else self.phase_stack
        self.nsem += 0
        self.uid = getattr(self, "uid", 0) + 1
        t = st.enter_context(self.nc.sbuf_tensor(f"{name}_{self.uid}", list(shape), dtype))
        return Buf(t, name)

    def ps(self, name, shape, dtype, persist=False):
        st = self.stack if (persist or self.phase_stack is None) else self.phase_stack
        self.uid = getattr(self, "uid", 0) + 1
        t = st.enter_context(self.nc.psum_tensor(f"{name}_{self.uid}", list(shape), dtype))
        return Buf(t, name)

    def view(self, ap, name="v"):
        return Buf(ap, name)

    def begin_sub(self):
        if not hasattr(self, "sub_stk"):
            self.sub_stk = []
        self.sub_stk.append(self.phase_stack)
        self.phase_stack = ExitStack()

    def end_sub(self):
        self.barrier()
        self.phase_stack.close()
        self.phase_stack = self.sub_stk.pop()

    def begin_phase(self):
        assert self.phase_stack is None
        self.phase_stack = ExitStack()

    def end_phase(self):
        self.barrier()
        self.phase_stack.close()
        self.phase_stack = None

    def _wait(self, F, tok):
        sem, val, eng = tok
        if eng == F == "pe":
            return
        if eng == F and not SAME_ENGINE_SYNC:
            return
        key = (F, id(sem))
        if self.waited.get(key, 0) >= val:
            return
        self.engs[F].wait_ge(sem, val)
        self.waited[key] = val
        self.ninst += 1

    def _deps(self, F, reads, writes):
        for b in reads:
            if b.lw is not None:
                self._wait(F, b.lw)
        for b in writes:
            if b.lw is not None:
                self._wait(F, b.lw)
            for t in b.rd:
                self._wait(F, t)

    def _commit(self, tok, reads, writes):
        for b in reads:
            if tok[2] == "dma":
                b.rd.append(tok)
            else:
                b.rd = [t for t in b.rd if t[2] != tok[2]]
                b.rd.append(tok)
        for b in writes:
            b.lw = tok
            b.rd = []

    def op(self, F, fn, reads=(), writes=()):
        self._deps(F, reads, writes)
        if self.ccnt[F] >= EPOCH:
            self._new_csem(F)
        inst = fn(self.engs[F])
        self.ccnt[F] += 1
        inst.then_inc(self.csem[F], 1)
        tok = (self.csem[F], self.ccnt[F], F)
        self._commit(tok, reads, writes)
        self.ninst += 1
        return tok

    def dma(self, Q, out, in_, reads=(), writes=(), **kw):
        if NO_SWDGE:
            Q = "sp"
        self._deps(Q, reads, writes)
        pool = self.dq[Q]
        i = self.dqi[Q]
        self.dqi[Q] = (i + 1) % len(pool)
        sem, val = pool[i]
        if val > 0:
            self._wait(Q, (sem, val, "dma"))
        inst = self.engs[Q].dma_start(out=out, in_=in_, **kw)
        inst.then_inc(sem, 16)
        pool[i][1] = val + 16
        tok = (sem, val + 16, "dma")
        self._commit(tok, reads, writes)
        self.ninst += 1
        return tok

    def barrier(self, engines=("pe", "act", "dve", "pool", "sp")):
        toks = []
        for e in ("pe", "act", "dve", "pool"):
            if self.ccnt[e] > 0:
                toks.append((self.csem[e], self.ccnt[e], "bar"))
        for q in self.dq:
            for sem, val in self.dq[q]:
                if val > 0:
                    toks.append((sem, val, "dma"))
        for F in engines:
            for t in toks:
                self._wait(F, t)

    def finish(self):
        self.barrier()
        self.stack.close()


D = 1024
LAT = 2048
CTX = 256
T = LAT + CTX
NTT = T // 128
DFF = 4096
ALPHA = (2.0 * 2) ** 0.25
LN_EPS = 1e-5


class Ring:
    def __init__(self, bufs):
        self.bufs = bufs
        self.i = 0

    def next(self):
        b = self.bufs[self.i]
        self.i = (self.i + 1) % len(self.bufs)
        return b


def mk_ring(S, name, shape, dtype, n=2, psum=False):
    f = S.ps if psum else S.sb
    return Ring([f(f"{name}{i}", shape, dtype) for i in range(n)])


class Ctx:
    pass


def prep_weight(S, dst, src, K, N, scale=None, tag="pw"):
    S.begin_phase()
    CB = min(N, 2048)
    rin = mk_ring(S, tag + "i", [128, CB], F32, 2)
    rout = mk_ring(S, tag + "o", [128, CB], BF16, 2)
    sc = S.sb(tag + "s", [128, CB], F32) if scale is not None else None
    for c0 in range(0, N, CB):
        cw = min(CB, N - c0)
        if scale is not None:
            S.dma("sp", sc.ap[:, :cw], scale[c0:c0 + cw].partition_broadcast(128), writes=[sc])
        for k0 in range(0, K, 128):
            kw = min(128, K - k0)
            a = rin.next()
            o = rout.next()
            S.dma("sp", a.ap[:kw, :cw], src[k0:k0 + kw, c0:c0 + cw], writes=[a])
            if scale is not None:
                S.op("dve", lambda e: e.tensor_tensor(out=o.ap[:kw, :cw], in0=a.ap[:kw, :cw], in1=sc.ap[:kw, :cw], op=ALU.mult),
                     reads=[a, sc], writes=[o])
            else:
                S.op("dve", lambda e: e.tensor_copy(out=o.ap[:kw, :cw], in_=a.ap[:kw, :cw]), reads=[a], writes=[o])
            S.dma("pool", dst[k0:k0 + kw, c0:c0 + cw], o.ap[:kw, :cw], reads=[o])
    S.end_phase()


def phase_mod(S, C):
    nc, R = C.nc, C.R
    S.begin_phase()
    cT = S.sb("cT", [128, 8, R], F32)
    S.dma("sp", cT.ap[:], C.cT[:, :, :], writes=[cT])
    sig = S.sb("sig", [128, 8, R], F32)
    scT = S.sb("scT", [128, 8, R], BF16)
    S.op("act", lambda e: e.activation(out=sig.ap[:], in_=cT.ap[:], func=AF.Sigmoid), reads=[cT], writes=[sig])
    S.op("dve", lambda e: e.tensor_tensor(out=scT.ap[:], in0=cT.ap[:], in1=sig.ap[:], op=ALU.mult), reads=[cT, sig], writes=[scT])
    scbc = S.sb("scbc", [128, 8, R, 128], BF16)
    for kc in range(8):
        for r in range(R):
            S.op("dve", lambda e: e.tensor_copy(out=scbc.ap[:, kc, r, :], in_=scT.ap[:, kc, r:r + 1].to_broadcast([128, 128])),
                 reads=[scT], writes=[scbc])
    mb = S.sb("mb", [128, 2, 48], F32)
    S.dma("sp", mb.ap[:], C.mod_bT[:, :, :], writes=[mb])
    wst = mk_ring(S, "mws", [128, 3072], F32, 2)
    wbf = S.sb("mwbf", [128, 8, 6144], BF16)
    pacc = mk_ring(S, "mps", [128, 512], F32, 2, psum=True)
    gt = mk_ring(S, "mgt", [128, 512], F32, 2)
    mbr = S.sb("mbr", [128, 2048], F32)
    for l in range(2):
        for kc in range(8):
            for hf in range(2):
                a = wst.next()
                S.dma("sp" if hf == 0 else "pool", a.ap[:], C.mod_w[l, kc * 128:(kc + 1) * 128, hf * 3072:(hf + 1) * 3072], writes=[a])
                S.op("dve" if hf == 0 else "act",
                     (lambda e: e.tensor_copy(out=wbf.ap[:, kc, hf * 3072:(hf + 1) * 3072], in_=a.ap[:])) if hf == 0 else
                     (lambda e: e.activation(out=wbf.ap[:, kc, hf * 3072:(hf + 1) * 3072], in_=a.ap[:], func=AF.Identity)),
                     reads=[a], writes=[wbf])
        for fc in range(48):
            p = pacc.next()
            for kc in range(8):
                S.op("pe", lambda e: e.matmul(p.ap[:, :R], wbf.ap[:, kc, fc * 128:(fc + 1) * 128], scT.ap[:, kc, :],
                                              start=(kc == 0), stop=(kc == 7)), reads=[wbf, scT], writes=[p])
            is_scale = (fc // 8) in (1, 4)
            S.op("dve", lambda e: e.tensor_scalar(out=C.modT.ap[:, l, fc, :], in0=p.ap[:, :R], scalar1=mb.ap[:, l, fc:fc + 1],
                                                  scalar2=(1.0 if is_scale else 0.0), op0=ALU.add, op1=ALU.add),
                 reads=[p, mb], writes=[C.modT])
        for gi, c0 in enumerate((2048, 5120)):
            S.dma("sp", mbr.ap[:, gi * 1024:(gi + 1) * 1024], C.mod_b[l, c0:c0 + 1024].partition_broadcast(128), writes=[mbr])
        for gi, c0 in enumerate((2048, 5120)):
            for r in range(R):
                for hf in range(2):
                    p = pacc.next()
                    for kc in range(8):
                        S.op("pe", lambda e: e.matmul(p.ap[:], scbc.ap[:, kc, r, :], wbf.ap[:, kc, c0 + hf * 512:c0 + (hf + 1) * 512],
                                                      start=(kc == 0), stop=(kc == 7)), reads=[scbc, wbf], writes=[p])
                    g = gt.next()
                    S.op("dve", lambda e: e.tensor_tensor(out=g.ap[:], in0=p.ap[:], in1=mbr.ap[:, gi * 1024 + hf * 512:gi * 1024 + (hf + 1) * 512], op=ALU.add),
                         reads=[p, mbr], writes=[g])
                    S.dma("pool", C.gbc[l, gi, r, :, hf * 512:(hf + 1) * 512], g.ap[:], reads=[g])
    S.end_phase()


def tile_src(C, stage, bi, i):
    if stage == 0:
        if i < 2:
            return C.ctx[bi, i * 128:(i + 1) * 128, :]
        return C.x[bi, (i - 2) * 128:(i - 1) * 128, :]
    return C.xs[stage - 1][bi, i * 128:(i + 1) * 128, :]


def rstd_op(S, mv, o, i, eps):
    S.op("dve", lambda e: e.tensor_scalar(out=mv.ap[:, o:o + 1], in0=mv.ap[:, i:i + 1], scalar1=eps, scalar2=None, op0=ALU.add), reads=[mv], writes=[mv])
    S.op("act", lambda e: e.activation(out=mv.ap[:, o:o + 1], in_=mv.ap[:, o:o + 1], func=AF.Sqrt), reads=[mv], writes=[mv])
    S.op("dve", lambda e: e.reciprocal(out=mv.ap[:, o:o + 1], in_=mv.ap[:, o:o + 1]), reads=[mv], writes=[mv])


class Epi:
    def __init__(self, S, C, tag):
        self.S, self.C = S, C
        self.gb = [S.sb(tag + "gb0", [128, 1024], F32), S.sb(tag + "gb1", [128, 1024], F32)]
        self.lg = S.sb(tag + "lg", [128, 1024], F32)
        self.lb = S.sb(tag + "lb", [128, 1024], F32)
        self.t1 = mk_ring(S, tag + "t1", [128, 1024], F32, 2)
        self.xo = mk_ring(S, tag + "xo", [128, 1024], F32, 2)
        self.st = mk_ring(S, tag + "st", [128, 2, 6], F32, 2)
        self.mv = mk_ring(S, tag + "mv", [128, 4], F32, 2)

    def load(self, l, sub, bi):
        S, C = self.S, self.C
        S.dma("sp", self.gb[0].ap[:], C.gbc[l, sub, bi, :, :], writes=[self.gb[0]])
        S.dma("sp", self.gb[1].ap[:], C.gbc[l, sub, C.R - 1, :, :], writes=[self.gb[1]])
        S.dma("sp", self.lg.ap[:], C.ln_g[l, sub, :].partition_broadcast(128), writes=[self.lg])
        S.dma("sp", self.lb.ap[:], C.ln_b[l, sub, :].partition_broadcast(128), writes=[self.lb])

    def run(self, y, xin, is_ctx, dst):
        S = self.S
        gb = self.gb[1 if is_ctx else 0]
        t1, xo, st, mv = self.t1.next(), self.xo.next(), self.st.next(), self.mv.next()
        S.op("dve", lambda e: e.tensor_tensor(out=t1.ap[:], in0=y.ap[:], in1=gb.ap[:], op=ALU.mult), reads=[y, gb], writes=[t1])
        S.op("dve", lambda e: e.scalar_tensor_tensor(out=t1.ap[:], in0=xin.ap[:], scalar=ALPHA, in1=t1.ap[:], op0=ALU.mult, op1=ALU.add),
             reads=[xin, t1], writes=[t1])
        for h in range(2):
            S.op("dve", lambda e: e.bn_stats(out=st.ap[:, h, :], in_=t1.ap[:, h * 512:(h + 1) * 512]), reads=[t1], writes=[st])
        S.op("dve", lambda e: e.bn_aggr(out=mv.ap[:, 0:2], in_=st.ap[:]), reads=[st], writes=[mv])
        rstd_op(S, mv, 2, 1, LN_EPS)
        S.op("dve", lambda e: e.tensor_scalar(out=mv.ap[:, 3:4], in0=mv.ap[:, 0:1], scalar1=mv.ap[:, 2:3], scalar2=-1.0, op0=ALU.mult, op1=ALU.mult),
             reads=[mv], writes=[mv])
        S.op("act", lambda e: e.activation(out=xo.ap[:], in_=t1.ap[:], func=AF.Identity, scale=mv.ap[:, 2:3], bias=mv.ap[:, 3:4]),
             reads=[t1, mv], writes=[xo])
        S.op("pool", lambda e: e.tensor_tensor(out=xo.ap[:], in0=xo.ap[:], in1=self.lg.ap[:], op=ALU.mult), reads=[xo, self.lg], writes=[xo])
        S.op("pool", lambda e: e.tensor_tensor(out=xo.ap[:], in0=xo.ap[:], in1=self.lb.ap[:], op=ALU.add), reads=[xo, self.lb], writes=[xo])
        S.dma("pool", dst, xo.ap[:], reads=[xo])


def transpose_mod(S, C, xin, hT, col0, l, fc0, r, ptr, ident):
    for g in range(2):
        p = ptr.next()
        for j in range(4):
            kc = g * 4 + j
            S.op("pe", lambda e: e.transpose(out=p.ap[:, j * 128:(j + 1) * 128], in_=xin.ap[:, kc * 128:(kc + 1) * 128], identity=ident.ap[:]),
                 reads=[xin, ident], writes=[p])
        for j in range(4):
            kc = g * 4 + j
            S.op("act", lambda e: e.activation(out=hT.ap[:, kc, col0:col0 + 128], in_=p.ap[:, j * 128:(j + 1) * 128], func=AF.Identity,
                                               scale=C.modT.ap[:, l, fc0 + 8 + kc, r:r + 1], bias=C.modT.ap[:, l, fc0 + kc, r:r + 1]),
                 reads=[p, C.modT], writes=[hT])


def phase_mlp(S, C, l, bi, src_stage, dst_stage, last):
    S.begin_phase()
    ident = C.ident
    w1 = S.sb("w1", [128, 8, DFF], BF16)
    w2 = S.sb("w2", [128, 32, D], BF16)
    for kc in range(8):
        S.dma("sp" if kc % 2 == 0 else "pool", w1.ap[:, kc, :], C.w1b[l][kc * 128:(kc + 1) * 128, :], writes=[w1])
    for ko in range(32):
        S.dma("sp" if ko % 2 == 0 else "pool", w2.ap[:, ko, :], C.w2b[l][ko * 128:(ko + 1) * 128, :], writes=[w2])
    epi = Epi(S, C, "m")
    epi.load(l, 1, bi)
    xin = mk_ring(S, "mxin", [128, 1024], F32, 4)
    hT = mk_ring(S, "mhT", [128, 8, 256], BF16, 2)
    aT = S.sb("maT", [128, 32, 256], BF16)
    rl = mk_ring(S, "mrl", [128, 256], BF16, 2)
    ptr = mk_ring(S, "mptr", [128, 512], F32, 2, psum=True)
    pup = mk_ring(S, "mpup", [128, 256], F32, 2, psum=True)
    pdn = mk_ring(S, "mpdn", [128, 1024], F32, 2, psum=True)
    t0 = 1 if last else 0
    for tt in range(t0, 9):
        is_ctx = tt == 0
        r = C.R - 1 if is_ctx else bi
        xs_ = []
        h = hT.next()
        for s in range(2):
            xi = xin.next()
            S.dma("sp", xi.ap[:], tile_src(C, src_stage, bi, tt * 2 + s), writes=[xi])
            transpose_mod(S, C, xi, h, s * 128, l, 24, r, ptr, ident)
            xs_.append(xi)
        for fo in range(32):
            p = pup.next()
            for kc in range(8):
                S.op("pe", lambda e: e.matmul(p.ap[:], w1.ap[:, kc, fo * 128:(fo + 1) * 128], h.ap[:, kc, :], start=(kc == 0), stop=(kc == 7)),
                     reads=[w1, h], writes=[p])
            rr = rl.next()
            S.op("act", lambda e: e.activation(out=rr.ap[:], in_=p.ap[:], func=AF.Relu), reads=[p], writes=[rr])
            S.op("pool", lambda e: e.tensor_tensor(out=aT.ap[:, fo, :], in0=rr.ap[:], in1=rr.ap[:], op=ALU.mult), reads=[rr], writes=[aT])
        for s in range(2):
            p = pdn.next()
            for hf in range(2):
                for ko in range(32):
                    S.op("pe", lambda e: e.matmul(p.ap[:, hf * 512:(hf + 1) * 512], aT.ap[:, ko, s * 128:(s + 1) * 128], w2.ap[:, ko, hf * 512:(hf + 1) * 512],
                                                  start=(ko == 0), stop=(ko == 31)), reads=[aT, w2], writes=[p])
            i = tt * 2 + s
            if last:
                dst = C.out[bi, (i - 2) * 128:(i - 1) * 128, :]
            else:
                dst = C.xs[dst_stage - 1][bi, i * 128:(i + 1) * 128, :]
            epi.run(p, xs_[s], is_ctx, dst)
    S.end_phase()


def dram(nc, name, shape, dtype, kind=None):
    if kind is None:
        return nc.dram_tensor(name, list(shape), dtype).ap()
    return nc.dram_tensor(name, list(shape), dtype, kind=kind).ap()


def declare_common(nc, NB, dbg=None):
    C = Ctx()
    C.nc, C.NB, C.R = nc, NB, NB + 1
    R = C.R
    I = lambda n, s: dram(nc, n, s, F32, "ExternalInput")
    C.x = I("x", [NB, LAT, D])
    C.ctx = I("ctx", [NB, CTX, D])
    C.cT = I("cT", [128, 8, R])
    C.mod_w = I("mod_w", [2, D, 6 * D])
    C.mod_b = I("mod_b", [2, 6 * D])
    C.mod_bT = I("mod_bT", [128, 2, 48])
    C.ln_g = I("ln_g", [2, 2, D])
    C.ln_b = I("ln_b", [2, 2, D])
    C.mlp_w1 = I("mlp_w1", [2, D, DFF])
    C.mlp_w2 = I("mlp_w2", [2, DFF, D])
    C.ident_d = I("ident", [128, 128])
    C.out = dram(nc, "out", [NB, LAT, D], F32, "ExternalOutput")
    C.gbc = dram(nc, "gbc", [2, 2, R, 128, D], F32)
    C.w1b = [dram(nc, f"w1b{l}", [D, DFF], BF16) for l in range(2)]
    C.w2b = [dram(nc, f"w2b{l}", [DFF, D], BF16) for l in range(2)]
    nst = 3
    C.xs = [dram(nc, f"xs{i}", [NB, T, D], F32, "ExternalOutput" if (dbg and f"xs{i}" in dbg) else None) for i in range(nst)]
    return C


def common_setup(S, C):
    C.modT = S.sb("modT", [128, 2, 48, C.R], F32, persist=True)
    C.ident = S.sb("identsb", [128, 128], F32, persist=True)
    S.dma("sp", C.ident.ap[:], C.ident_d[:, :], writes=[C.ident])


def host_common(inputs, core, NB):
    b0 = core * NB
    f = lambda a: np.ascontiguousarray(np.asarray(a, dtype=np.float32))
    cs = np.concatenate([np.asarray(inputs["c"])[b0:b0 + NB], np.asarray(inputs["c_ctx"])[None, :]], 0)
    m = {
        "x": f(np.asarray(inputs["x"])[b0:b0 + NB]),
        "ctx": f(np.asarray(inputs["ctx"])[b0:b0 + NB]),
        "cT": f(cs.reshape(NB + 1, 8, 128).transpose(2, 1, 0)),
        "mod_w": f(inputs["mod_w"]),
        "mod_b": f(inputs["mod_b"]),
        "mod_bT": f(np.asarray(inputs["mod_b"]).reshape(2, 48, 128).transpose(2, 0, 1)),
        "ln_g": f(inputs["ln_g"]),
        "ln_b": f(inputs["ln_b"]),
        "mlp_w1": f(inputs["mlp_w1"]),
        "mlp_w2": f(inputs["mlp_w2"]),
        "ident": np.eye(128, dtype=np.float32),
    }
    return m


HYW = 512
PI = float(np.pi)


def colof(i):
    return 1 + 128 * i if i < 2 else 259 + 128 * (i - 2)


def declare_l0(nc, C, dbg=None):
    NB = C.NB
    I = lambda n, s, dt=F32: dram(nc, n, s, dt, "ExternalInput")
    Sx = lambda n, s, dt=BF16: dram(nc, n, s, dt, "ExternalOutput" if (dbg and n in dbg) else None)
    C.e_w_hy = I("e_w_hy", [D, 1536])
    C.e_w_qkv = I("e_w_qkv", [D, 768 + 640])
    C.e_w_out = I("e_w_out", [D, D])
    C.hy_conv = I("hy_conv", [3, 1536])
    C.hy_w1 = I("hy_w1", [33, 64])
    C.hy_w2 = I("hy_w2", [64, 64])
    C.hy_w3 = I("hy_w3", [64, 1024])
    C.hy_vec = I("hy_vec", [64, 3])
    C.hy_decay = I("hy_decay", [1024])
    C.hy_bias = I("hy_bias", [512])
    C.attn_sink = I("attn_sink", [8])
    C.peT = [I("peT_l", [33, LAT]), I("peT_c", [33, CTX])]
    C.negtn = [I("negtn_l", [128, LAT // 128]), I("negtn_c", [128, CTX // 128])]
    C.fwd = [I("fwd_l", [16, 2, 128, 16, 128], BF16), I("fwd_c", [2, 2, 128, 2, 128], BF16)]
    C.inv = [I("inv_l", [4, 128, 16, 2, 512], BF16), I("inv_c", [1, 128, 2, 2, 256], BF16)]
    C.rope = I("rope", [64, 2, LAT])
    C.amask = I("amask", [128, 384])
    C.ident_b = I("ident_b", [128, 128], BF16)
    C.whb = [Sx(f"whb{j}", [D, 1536]) for j in range(3)]
    C.wqb = Sx("wqb", [D, 1408])
    C.wob0 = Sx("wob0", [D, D])
    C.kspec = [Sx("kspec_l", [LAT, 2, 512], F32), Sx("kspec_c", [CTX, 2, 512], F32)]
    C.filt = [Sx("filt_l", [LAT, 2, 512]), Sx("filt_c", [CTX, 2, 512])]
    C.u = Sx("u_s", [NB, T, 512])
    C.x0T = Sx("x0T_s", [NB, 512, T])
    C.qT = Sx("qT_s", [NB, 8, 64, T])
    C.kT = Sx("kT_s", [NB, 2, 64, T])
    C.v = Sx("v_s", [NB, T, 128])
    C.yaT = Sx("yaT_s", [NB, 512, T])
    C.ybT = Sx("ybT_s", [NB, 8, 64, T])


def host_l0(inputs, m):
    f = lambda a: np.ascontiguousarray(np.asarray(a, dtype=np.float32))
    import ml_dtypes
    bf = lambda a: np.ascontiguousarray(np.asarray(a, dtype=np.float32).astype(ml_dtypes.bfloat16))
    w = np.asarray(inputs["e_w_in"])[0]
    d = np.arange(64)
    partner = np.where((d % 32) < 16, d + 16, d - 16)
    qcols = 1536 + (np.arange(8)[:, None] * 64 + partner[None, :]).reshape(-1)
    kcols = 2048 + (np.arange(2)[:, None] * 64 + partner[None, :]).reshape(-1)
    m["e_w_hy"] = f(w[:, :1536])
    m["e_w_qkv"] = f(np.concatenate([w[:, 1536:2304], w[:, qcols], w[:, kcols]], 1))
    m["e_w_out"] = f(np.asarray(inputs["e_w_out"])[0])
    m["hy_conv"] = f(np.asarray(inputs["hy_conv"])[0])
    m["hy_w1"] = f(np.asarray(inputs["hy_ffn_w1"])[0])
    m["hy_w2"] = f(np.asarray(inputs["hy_ffn_w2"])[0])
    m["hy_w3"] = f(np.asarray(inputs["hy_ffn_w3"])[0])
    m["hy_vec"] = f(np.stack([np.asarray(inputs["hy_ffn_b1"])[0], np.asarray(inputs["hy_ffn_b2"])[0], np.asarray(inputs["hy_sin_freq"])[0]], 1))
    m["hy_decay"] = f(np.asarray(inputs["hy_decay"])[0])
    m["hy_bias"] = f(np.asarray(inputs["hy_bias"])[0])
    m["attn_sink"] = f(np.asarray(inputs["attn_sink"])[0])
    for tag, Lf in (("l", LAT), ("c", CTX)):
        t = np.arange(Lf, dtype=np.float32)
        t_norm = t / np.float32(max(Lf - 1, 1))
        bands = np.linspace(1e-4, 15, 16, dtype=np.float32)
        ang = (2.0 * np.pi * t[:, None] * bands[None, :] / Lf).astype(np.float32)
        pe = np.concatenate([t_norm[:, None], np.cos(ang), -np.sin(ang)], -1).astype(np.float32)
        m["peT_" + tag] = f(pe.T)
        m["negtn_" + tag] = f((-t_norm).reshape(Lf // 128, 128).T)
        N = 2 * Lf
        nt = Lf // 128
        tt = np.arange(Lf, dtype=np.float64)
        ff = np.arange(Lf, dtype=np.float64) + 0.5
        th = 2.0 * np.pi * np.outer(tt, ff) / N
        Cm, Sm = np.cos(th), np.sin(th)
        fw = np.stack([Cm, Sm], 0).reshape(2, nt, 128, nt, 128)
        m["fwd_" + tag] = bf(fw.transpose(3, 0, 2, 1, 4))
        tw = min(512, Lf)
        iv = np.stack([Cm.T, -Sm.T], 0).reshape(2, nt, 128, Lf // tw, tw)
        m["inv_" + tag] = bf(iv.transpose(3, 2, 1, 0, 4))
    pos = np.arange(LAT)
    inv_freq = (10000.0 ** (-np.arange(16, dtype=np.float32) / 16)).astype(np.float32)
    P = np.where(d[:, None] < 32, (pos // 64)[None, :], (pos % 64)[None, :]).astype(np.float32)
    ang = (P * inv_freq[d % 16][:, None]).astype(np.float32)
    sgn = np.where((d % 32) < 16, -1.0, 1.0)[:, None]
    m["rope"] = f(np.stack([np.cos(ang), sgn * np.sin(ang)], 1))
    qi = np.arange(128)[:, None]
    kj = np.arange(384)[None, :] - 128
    m["amask"] = f(np.where(np.abs(qi - kj) <= 128, 0.0, -30000.0))
    m["ident_b"] = bf(np.eye(128))
    return m


def l0_setup(S, C):
    for j in range(3):
        prep_weight(S, C.whb[j], C.e_w_hy, D, 1536, scale=C.hy_conv[j, :], tag=f"ph{j}")
    prep_weight(S, C.wqb, C.e_w_qkv, D, 1408, tag="pq")
    prep_weight(S, C.wob0, C.e_w_out, D, D, tag="po")
    for si, Lf in enumerate((LAT, CTX)):
        hyena_filter(S, C, si, Lf)
        hyena_fwd(S, C, si, Lf, C.filt[si], None, C.kspec[si], is_filter=True)


def hyena_filter(S, C, si, Lf):
    S.begin_phase()
    w1 = S.sb("hw1", [33, 64], F32)
    w2 = S.sb("hw2", [64, 64], F32)
    w3 = S.sb("hw3", [64, 1024], F32)
    vec = S.sb("hvec", [64, 3], F32)
    peT = S.sb("hpe", [33, Lf], F32)
    ntn = S.sb("hntn", [128, Lf // 128], F32)
    dec = S.sb("hdec", [128, 1024], F32)
    for dst, src in ((w1, C.hy_w1), (w2, C.hy_w2), (w3, C.hy_w3), (vec, C.hy_vec), (peT, C.peT[si]), (ntn, C.negtn[si])):
        S.dma("sp", dst.ap[:], src, writes=[dst])
    S.dma("sp", dec.ap[:], C.hy_decay[:].partition_broadcast(128), writes=[dec])
    S.op("dve", lambda e: e.scalar_tensor_tensor(out=dec.ap[:], in0=dec.ap[:], scalar=-1.0, in1=dec.ap[:], op0=ALU.mult, op1=ALU.max), reads=[dec], writes=[dec])
    h1 = S.sb("hh1", [64, Lf], F32)
    h2 = S.sb("hh2", [64, Lf], F32)
    tmp = S.sb("htmp", [64, 512], F32)
    S.sin_ki = S.sb("hki", [64, 512], mybir.dt.int32)
    S.sin_kf = S.sb("hkf", [64, 512], F32)
    pp = mk_ring(S, "hpp", [128, 512], F32, 2, psum=True)
    W = min(512, Lf)
    for c0 in range(0, Lf, W):
        p = pp.next()
        S.op("pe", lambda e: e.matmul(p.ap[:64, :W], w1.ap[:], peT.ap[:, c0:c0 + W], start=True, stop=True), reads=[w1, peT], writes=[p])
        S.op("dve", lambda e: e.tensor_scalar(out=tmp.ap[:, :W], in0=p.ap[:64, :W], scalar1=vec.ap[:, 0:1], scalar2=vec.ap[:, 2:3], op0=ALU.add, op1=ALU.mult),
             reads=[p, vec], writes=[tmp])
        sin_tail(S, h1, c0, W, tmp)
    for c0 in range(0, Lf, W):
        p = pp.next()
        S.op("pe", lambda e: e.matmul(p.ap[:64, :W], w2.ap[:], h1.ap[:, c0:c0 + W], start=True, stop=True), reads=[w2, h1], writes=[p])
        S.op("dve", lambda e: e.tensor_scalar(out=tmp.ap[:, :W], in0=p.ap[:64, :W], scalar1=vec.ap[:, 1:2], scalar2=vec.ap[:, 2:3], op0=ALU.add, op1=ALU.mult),
             reads=[p, vec], writes=[tmp])
        sin_tail(S, h2, c0, W, tmp)
    ex = mk_ring(S, "hex", [128, 1024], F32, 2)
    fo = mk_ring(S, "hfo", [128, 2, 512], BF16, 2)
    for tc in range(Lf // 128):
        e_ = ex.next()
        S.op("act", lambda e: e.activation(out=e_.ap[:], in_=dec.ap[:], func=AF.Exp, scale=ntn.ap[:, tc:tc + 1]), reads=[dec, ntn], writes=[e_])
        for hf in range(2):
            p = pp.next()
            S.op("pe", lambda e: e.matmul(p.ap[:], h2.ap[:, tc * 128:(tc + 1) * 128], w3.ap[:, hf * 512:(hf + 1) * 512], start=True, stop=True),
                 reads=[h2, w3], writes=[p])
            S.op("dve", lambda e: e.tensor_tensor(out=e_.ap[:, hf * 512:(hf + 1) * 512], in0=p.ap[:], in1=e_.ap[:, hf * 512:(hf + 1) * 512], op=ALU.mult),
                 reads=[p, e_], writes=[e_])
        if tc == 0:
            S.op("dve", lambda e: e.memset(e_.ap[0:1, 512:1024], 0.0), reads=[], writes=[e_])
        o = fo.next()
        S.op("dve", lambda e: e.tensor_tensor(out=o.ap[:, 0, :], in0=e_.ap[:, 512:1024], in1=e_.ap[:, 0:512], op=ALU.add), reads=[e_], writes=[o])
        S.op("pool", lambda e: e.tensor_tensor(out=o.ap[:, 1, :], in0=e_.ap[:, 512:1024], in1=e_.ap[:, 0:512], op=ALU.subtract), reads=[e_], writes=[o])
        S.dma("sp", C.filt[si][tc * 128:(tc + 1) * 128, :, :], o.ap[:], reads=[o])
    S.end_phase()


def sin_tail(S, dst, c0, W, tmp):
    ki, kf = S.sin_ki, S.sin_kf
    S.op("dve", lambda e: e.tensor_scalar(out=tmp.ap[:, :W], in0=tmp.ap[:, :W], scalar1=1.0 / (2.0 * PI), scalar2=16.5, op0=ALU.mult, op1=ALU.add),
         reads=[tmp], writes=[tmp])
    S.op("dve", lambda e: e.tensor_copy(out=ki.ap[:, :W], in_=tmp.ap[:, :W]), reads=[tmp], writes=[ki])
    S.op("dve", lambda e: e.tensor_copy(out=kf.ap[:, :W], in_=ki.ap[:, :W]), reads=[ki], writes=[kf])
    S.op("dve", lambda e: e.scalar_tensor_tensor(out=tmp.ap[:, :W], in0=tmp.ap[:, :W], scalar=-0.5, in1=kf.ap[:, :W], op0=ALU.add, op1=ALU.subtract),
         reads=[tmp, kf], writes=[tmp])
    S.op("dve", lambda e: e.scalar_tensor_tensor(out=tmp.ap[:, :W], in0=tmp.ap[:, :W], scalar=-0.5, in1=tmp.ap[:, :W], op0=ALU.is_lt, op1=ALU.add),
         reads=[tmp], writes=[tmp])
    S.op("act", lambda e: e.activation(out=dst.ap[:, c0:c0 + W], in_=tmp.ap[:, :W], func=AF.Sin, scale=2.0 * PI * 0.999999), reads=[tmp], writes=[dst])


def hyena_fwd(S, C, si, Lf, src, bi, dst, is_filter):
    nt = Lf // 128
    N = 2 * Lf
    if is_filter:
        S.begin_phase()
    a_in = S.sb("fa", [128, nt, 2 if is_filter else 1, 512], BF16)
    if is_filter:
        S.dma("sp", a_in.ap[:], src.rearrange("(tc p) s c -> p tc s c", p=128), writes=[a_in])
        bb = S.sb("fbias", [128, 512], F32)
        S.dma("sp", bb.ap[:], C.hy_bias[:].partition_broadcast(128), writes=[bb])
        S.op("dve", lambda e: e.tensor_scalar(out=bb.ap[:], in0=bb.ap[:], scalar1=2.0 / N, scalar2=None, op0=ALU.mult), reads=[bb], writes=[bb])
    else:
        S.dma("sp", a_in.ap[:, :, 0, :], src.rearrange("(tc p) c -> p tc c", p=128), writes=[a_in])
    fm = mk_ring(S, "ffm", [128, 2, nt, 128], BF16, 2)
    pr = mk_ring(S, "fpr", [128, 512], F32, 2, psum=True)
    pi_ = mk_ring(S, "fpi", [128, 512], F32, 2, psum=True)
    if is_filter:
        ko = mk_ring(S, "fko", [128, 2, 512], F32, 2)
    else:
        ks = mk_ring(S, "fks", [128, 2, 512], F32, 2)
        tt = mk_ring(S, "ftt", [128, 4, 512], F32, 2)
    for fcn in range(nt):
        m = fm.next()
        for cs in range(2):
            S.dma("sp" if cs == 0 else "pool", m.ap[:, cs, :, :], C.fwd[si][fcn, cs, :, :, :], writes=[m])
        a, b = pr.next(), pi_.next()
        for cs, p in ((0, a), (1, b)):
            for tc in range(nt):
                S.op("pe", lambda e: e.matmul(p.ap[:], m.ap[:, cs, tc, :], a_in.ap[:, tc, cs if is_filter else 0, :], start=(tc == 0), stop=(tc == nt - 1)),
                     reads=[m, a_in], writes=[p])
        if is_filter:
            o = ko.next()
            S.op("dve", lambda e: e.scalar_tensor_tensor(out=o.ap[:, 0, :], in0=a.ap[:], scalar=2.0 / N, in1=bb.ap[:], op0=ALU.mult, op1=ALU.add),
                 reads=[a, bb], writes=[o])
            S.op("act", lambda e: e.activation(out=o.ap[:, 1, :], in_=b.ap[:], func=AF.Identity, scale=2.0 / N), reads=[b], writes=[o])
            S.dma("pool", dst[fcn * 128:(fcn + 1) * 128, :, :], o.ap[:], reads=[o])
        else:
            k = ks.next()
            S.dma("sp", k.ap[:], C.kspec[si][fcn * 128:(fcn + 1) * 128, :, :], writes=[k])
            t = tt.next()
            S.op("dve", lambda e: e.tensor_tensor(out=t.ap[:, 0, :], in0=a.ap[:], in1=k.ap[:, 0, :], op=ALU.mult), reads=[a, k], writes=[t])
            S.op("dve", lambda e: e.tensor_tensor(out=t.ap[:, 1, :], in0=b.ap[:], in1=k.ap[:, 1, :], op=ALU.mult), reads=[b, k], writes=[t])
            S.op("dve", lambda e: e.tensor_tensor(out=t.ap[:, 2, :], in0=a.ap[:], in1=k.ap[:, 1, :], op=ALU.mult), reads=[a, k], writes=[t])
            S.op("dve", lambda e: e.tensor_tensor(out=t.ap[:, 3, :], in0=b.ap[:], in1=k.ap[:, 0, :], op=ALU.mult), reads=[b, k], writes=[t])
            S.op("pool", lambda e: e.tensor_tensor(out=dst.ap[:, fcn, 0, :], in0=t.ap[:, 0, :], in1=t.ap[:, 1, :], op=ALU.add), reads=[t], writes=[dst])
            S.op("pool", lambda e: e.tensor_tensor(out=dst.ap[:, fcn, 1, :], in0=t.ap[:, 2, :], in1=t.ap[:, 3, :], op=ALU.subtract), reads=[t], writes=[dst])
    if is_filter:
        S.end_phase()


def l0_inproj(S, C, bi, src_stage):
    S.begin_phase()
    ident = C.ident
    hT = S.sb("ihT", [128, 8, T + 4], BF16)
    S.op("pool", lambda e: e.memset(hT.ap[:], 0.0), writes=[hT])
    xin = mk_ring(S, "ixin", [128, 1024], F32, 2)
    ptr = mk_ring(S, "iptr", [128, 512], F32, 2, psum=True)
    for i in range(NTT):
        xi = xin.next()
        S.dma("sp", xi.ap[:], tile_src(C, src_stage, bi, i), writes=[xi])
        transpose_mod(S, C, xi, hT, colof(i), 0, 0, (C.R - 1 if i < 2 else bi), ptr, ident)
    S.begin_sub()
    wt = S.sb("iwt", [128, 3, 8, 1024], BF16)
    wv = S.sb("iwv", [128, 8, 128], BF16)
    for j in range(3):
        for kc in range(8):
            S.dma("sp" if kc % 2 else "pool", wt.ap[:, j, kc, :], C.whb[j][kc * 128:(kc + 1) * 128, 512:1536], writes=[wt])
    for kc in range(8):
        S.dma("sp", wv.ap[:, kc, :], C.wqb[kc * 128:(kc + 1) * 128, 640:768], writes=[wv])
    pa = mk_ring(S, "ipa", [128, 512], F32, 2, psum=True)
    pb = mk_ring(S, "ipb", [128, 512], F32, 2, psum=True)
    x1s = mk_ring(S, "ix1", [128, 512], F32, 2)
    ut = mk_ring(S, "iut", [128, 512], BF16, 2)
    vt = mk_ring(S, "ivt", [128, 128], BF16, 2)
    for i in range(NTT):
        c0 = colof(i)
        a, b = pa.next(), pb.next()
        for half, p in ((0, a), (1, b)):
            n = 0
            for j in range(3):
                for kc in range(8):
                    S.op("pe", lambda e: e.matmul(p.ap[:], hT.ap[:, kc, c0 + j - 1:c0 + j - 1 + 128], wt.ap[:, j, kc, half * 512:(half + 1) * 512],
                                                  start=(n == 0), stop=(n == 23)), reads=[hT, wt], writes=[p])
                    n += 1
        x1 = x1s.next()
        S.op("act", lambda e: e.activation(out=x1.ap[:], in_=a.ap[:], func=AF.Identity), reads=[a], writes=[x1])
        u = ut.next()
        S.op("dve", lambda e: e.tensor_tensor(out=u.ap[:], in0=b.ap[:], in1=x1.ap[:], op=ALU.mult), reads=[b, x1], writes=[u])
        S.dma("pool", C.u[bi, i * 128:(i + 1) * 128, :], u.ap[:], reads=[u])
        p = pa.next()
        for kc in range(8):
            S.op("pe", lambda e: e.matmul(p.ap[:, :128], hT.ap[:, kc, c0:c0 + 128], wv.ap[:, kc, :], start=(kc == 0), stop=(kc == 7)),
                 reads=[hT, wv], writes=[p])
        v = vt.next()
        S.op("act", lambda e: e.activation(out=v.ap[:], in_=p.ap[:, :128], func=AF.Identity), reads=[p], writes=[v])
        S.dma("pool", C.v[bi, i * 128:(i + 1) * 128, :], v.ap[:], reads=[v])
    S.end_sub()
    S.begin_sub()
    w0 = S.sb("iw0", [128, 3, 8, 512], BF16)
    wq = S.sb("iwq", [128, 8, 1280], BF16)
    rope = S.sb("irope", [64, 2, LAT], F32)
    S.dma("sp", rope.ap[:], C.rope[:, :, :], writes=[rope])
    for j in range(3):
        for kc in range(8):
            S.dma("sp" if kc % 2 else "pool", w0.ap[:, j, kc, :], C.whb[j][kc * 128:(kc + 1) * 128, 0:512], writes=[w0])
    for kc in range(8):
        S.dma("sp", wq.ap[:, kc, 0:640], C.wqb[kc * 128:(kc + 1) * 128, 0:640], writes=[wq])
        S.dma("pool", wq.ap[:, kc, 640:1280], C.wqb[kc * 128:(kc + 1) * 128, 768:1408], writes=[wq])
    pa = mk_ring(S, "jpa", [128, 512], F32, 3, psum=True)
    ot = mk_ring(S, "jot", [128, 512], BF16, 3)
    t1 = mk_ring(S, "jt1", [64, 512], F32, 2)
    t2 = mk_ring(S, "jt2", [64, 512], F32, 2)
    tiles = [(0, 256)] + [(256 + 512 * k, 512) for k in range(4)]
    for (tok0, w) in tiles:
        c0 = colof(tok0 // 128)
        for cc in range(4):
            p = pa.next()
            n = 0
            for j in range(3):
                for kc in range(8):
                    S.op("pe", lambda e: e.matmul(p.ap[:, :w], w0.ap[:, j, kc, cc * 128:(cc + 1) * 128], hT.ap[:, kc, c0 + j - 1:c0 + j - 1 + w],
                                                  start=(n == 0), stop=(n == 23)), reads=[w0, hT], writes=[p])
                    n += 1
            o = ot.next()
            S.op("act", lambda e: e.activation(out=o.ap[:, :w], in_=p.ap[:, :w], func=AF.Identity), reads=[p], writes=[o])
            S.dma("pool", C.x0T[bi, cc * 128:(cc + 1) * 128, tok0:tok0 + w], o.ap[:, :w], reads=[o])
        for hh in range(10):
            p = pa.next()
            for kc in range(8):
                S.op("pe", lambda e: e.matmul(p.ap[:64, :w], wq.ap[:, kc, hh * 64:(hh + 1) * 64], hT.ap[:, kc, c0:c0 + w], start=(kc == 0), stop=(kc == 7)),
                     reads=[wq, hT], writes=[p])
            o = ot.next()
            dst = C.qT[bi, hh, :, tok0:tok0 + w] if hh < 8 else C.kT[bi, hh - 8, :, tok0:tok0 + w]
            if tok0 == 0:
                S.op("act", lambda e: e.activation(out=o.ap[:64, :w], in_=p.ap[:64, :w], func=AF.Identity), reads=[p], writes=[o])
            else:
                p2 = pa.next()
                for kc in range(8):
                    S.op("pe", lambda e: e.matmul(p2.ap[:64, :w], wq.ap[:, kc, 640 + hh * 64:640 + (hh + 1) * 64], hT.ap[:, kc, c0:c0 + w],
                                                  start=(kc == 0), stop=(kc == 7)), reads=[wq, hT], writes=[p2])
                l0_ = tok0 - 256
                a, b = t1.next(), t2.next()
                S.op("dve", lambda e: e.tensor_tensor(out=a.ap[:, :w], in0=p.ap[:64, :w], in1=rope.ap[:, 0, l0_:l0_ + w], op=ALU.mult), reads=[p, rope], writes=[a])
                S.op("dve", lambda e: e.tensor_tensor(out=b.ap[:, :w], in0=p2.ap[:64, :w], in1=rope.ap[:, 1, l0_:l0_ + w], op=ALU.mult), reads=[p2, rope], writes=[b])
                S.op("pool", lambda e: e.tensor_tensor(out=o.ap[:64, :w], in0=a.ap[:, :w], in1=b.ap[:, :w], op=ALU.add), reads=[a, b], writes=[o])
            S.dma("sp", dst, o.ap[:64, :w], reads=[o])
    S.end_sub()
    S.end_phase()


def l0_hyena(S, C, bi):
    for si, (Lf, tok0) in enumerate(((LAT, 256), (CTX, 0))):
        S.begin_phase()
        nt = Lf // 128
        Y = S.sb("hyY", [128, nt, 2, 512], BF16)
        S.begin_sub()
        hyena_fwd(S, C, si, Lf, C.u[bi, tok0:tok0 + Lf, :], bi, Y, is_filter=False)
        S.end_sub()
        tw = min(512, Lf)
        iv = mk_ring(S, "hyiv", [128, nt, 2, tw], BF16, 2 if Lf == CTX else 1)
        x0 = mk_ring(S, "hyx0", [128, tw], BF16, 2)
        ya = mk_ring(S, "hyya", [128, tw], BF16, 2)
        pp = mk_ring(S, "hypp", [128, 512], F32, 2, psum=True)
        for tt in range(Lf // tw):
            m = iv.next()
            for fc in range(nt):
                S.dma("sp" if fc % 2 else "pool", m.ap[:, fc, :, :], C.inv[si][tt, :, fc, :, :], writes=[m])
            for cc in range(4):
                p = pp.next()
                n = 0
                for fc in range(nt):
                    for cs in range(2):
                        S.op("pe", lambda e: e.matmul(p.ap[:, :tw], Y.ap[:, fc, cs, cc * 128:(cc + 1) * 128], m.ap[:, fc, cs, :],
                                                      start=(n == 0), stop=(n == 2 * nt - 1)), reads=[Y, m], writes=[p])
                        n += 1
                xz = x0.next()
                t0_ = tok0 + tt * tw
                S.dma("sp", xz.ap[:], C.x0T[bi, cc * 128:(cc + 1) * 128, t0_:t0_ + tw], writes=[xz])
                o = ya.next()
                S.op("dve", lambda e: e.tensor_tensor(out=o.ap[:], in0=p.ap[:, :tw], in1=xz.ap[:], op=ALU.mult), reads=[p, xz], writes=[o])
                S.dma("pool", C.yaT[bi, cc * 128:(cc + 1) * 128, t0_:t0_ + tw], o.ap[:], reads=[o])
        S.end_phase()


def l0_attn(S, C, bi):
    S.begin_phase()
    qT = S.sb("aqT", [64, 8, T], BF16)
    kT = S.sb("akT", [64, 2, T], BF16)
    v = S.sb("av", [128, NTT, 128], BF16)
    yb = S.sb("ayb", [64, 8, T], BF16)
    mask = S.sb("amask", [128, 384], F32)
    sink = S.sb("asink", [128, 8], F32)
    idb = S.sb("aidb", [128, 128], BF16)
    for h in range(8):
        S.dma("sp" if h % 2 else "pool", qT.ap[:, h, :], C.qT[bi, h, :, :], writes=[qT])
    for h in range(2):
        S.dma("sp", kT.ap[:, h, :], C.kT[bi, h, :, :], writes=[kT])
    S.dma("sp", v.ap[:], C.v[bi].rearrange("(i p) c -> p i c", p=128), writes=[v])
    S.dma("sp", mask.ap[:], C.amask[:, :], writes=[mask])
    S.dma("sp", sink.ap[:], C.attn_sink[:].partition_broadcast(128), writes=[sink])
    S.dma("sp", idb.ap[:], C.ident_b[:, :], writes=[idb])
    psl = mk_ring(S, "apsl", [128, 512], F32, 2, psum=True)
    psc = mk_ring(S, "apsc", [128, 512], F32, 2, psum=True)
    ppt = mk_ring(S, "appt", [128, 5, 128], BF16, 2, psum=True)
    ppv = mk_ring(S, "appv", [128, 128], F32, 2, psum=True)
    sc = mk_ring(S, "asc", [128, 640], F32, 2)
    pe_ = mk_ring(S, "ape", [128, 640], F32, 2)
    pn = mk_ring(S, "apn", [128, 640], BF16, 2)
    pts = mk_ring(S, "apts", [128, 5, 128], BF16, 2)
    sm = mk_ring(S, "asm", [128, 8], F32, 4)
    def head_chain(qb, hh, n, lo, hi, m0, ktiles, nk, q0):
            h = hh // 4
            s_ = sc.next()
            if n:
                a = psl.next()
                S.op("pe", lambda e: e.matmul(a.ap[:, :n], qT.ap[:, hh, q0:q0 + 128], kT.ap[:, h, 256 + lo:256 + hi], start=True, stop=True),
                     reads=[qT, kT], writes=[a])
                S.op("dve", lambda e: e.tensor_tensor(out=s_.ap[:, :n], in0=a.ap[:, :n], in1=mask.ap[:, m0:m0 + n], op=ALU.add), reads=[a, mask], writes=[s_])
            b = psc.next()
            S.op("pe", lambda e: e.matmul(b.ap[:, :256], qT.ap[:, hh, q0:q0 + 128], kT.ap[:, h, 0:256], start=True, stop=True), reads=[qT, kT], writes=[b])
            S.op("act", lambda e: e.activation(out=s_.ap[:, n:nk], in_=b.ap[:, :256], func=AF.Identity), reads=[b], writes=[s_])
            yield
            w = sm.next()
            S.op("dve", lambda e: e.tensor_reduce(out=w.ap[:, 0:1], in_=s_.ap[:, :nk], axis=AX.X, op=ALU.max), reads=[s_], writes=[w])
            S.op("dve", lambda e: e.tensor_scalar(out=w.ap[:, 1:2], in0=w.ap[:, 0:1], scalar1=0.125, scalar2=sink.ap[:, hh:hh + 1], op0=ALU.mult, op1=ALU.max),
                 reads=[w, sink], writes=[w])
            S.op("dve", lambda e: e.tensor_scalar(out=w.ap[:, 2:3], in0=w.ap[:, 1:2], scalar1=-1.0, scalar2=None, op0=ALU.mult), reads=[w], writes=[w])
            p_ = pe_.next()
            S.op("act", lambda e: e.activation(out=p_.ap[:, :nk], in_=s_.ap[:, :nk], func=AF.Exp, scale=0.125, bias=w.ap[:, 2:3]),
                 reads=[s_, w], writes=[p_])
            yield
            S.op("dve", lambda e: e.tensor_reduce(out=w.ap[:, 3:4], in_=p_.ap[:, :nk], axis=AX.X, op=ALU.add), reads=[p_], writes=[w])
            S.op("act", lambda e: e.activation(out=w.ap[:, 4:5], in_=w.ap[:, 2:3], func=AF.Exp, bias=sink.ap[:, hh:hh + 1], scale=1.0), reads=[w, sink], writes=[w])
            S.op("dve", lambda e: e.tensor_tensor(out=w.ap[:, 5:6], in0=w.ap[:, 3:4], in1=w.ap[:, 4:5], op=ALU.add), reads=[w], writes=[w])
            S.op("dve", lambda e: e.reciprocal(out=w.ap[:, 6:7], in_=w.ap[:, 5:6]), reads=[w], writes=[w])
            pn_ = pn.next()
            S.op("pool", lambda e: e.tensor_scalar(out=pn_.ap[:, :nk], in0=p_.ap[:, :nk], scalar1=w.ap[:, 6:7], scalar2=None, op0=ALU.mult), reads=[p_, w], writes=[pn_])
            yield
            pt = ppt.next()
            nch = nk // 128
            for j in range(nch):
                S.op("pe", lambda e: e.transpose(out=pt.ap[:, j, :], in_=pn_.ap[:, j * 128:(j + 1) * 128], identity=idb.ap[:]), reads=[pn_, idb], writes=[pt])
            ps_ = pts.next()
            S.op("act", lambda e: e.activation(out=ps_.ap[:, :nch, :], in_=pt.ap[:, :nch, :], func=AF.Identity), reads=[pt], writes=[ps_])
            yield
            o = ppv.next()
            for j in range(nch):
                S.op("pe", lambda e: e.matmul(o.ap[:64, :], v.ap[:, ktiles[j], h * 64:(h + 1) * 64], ps_.ap[:, j, :], start=(j == 0), stop=(j == nch - 1)),
                     reads=[v, ps_], writes=[o])
            S.op("dve", lambda e: e.tensor_copy(out=yb.ap[:, hh, q0:q0 + 128], in_=o.ap[:64, :]), reads=[o], writes=[yb])
            yield

    for qb in range(NTT):
        is_ctx = qb < 2
        q0 = qb * 128
        lo = hi = m0 = 0
        if is_ctx:
            n = 0
            ktiles = []
        else:
            lq = q0 - 256
            lo, hi = max(0, lq - 128), min(LAT, lq + 256)
            n = hi - lo
            m0 = lo - (lq - 128)
            ktiles = [2 + lo // 128 + j for j in range(n // 128)]
        ktiles = ktiles + [0, 1]
        nk = n + 256
        for h0 in range(0, 8, 2):
            interleave([head_chain(qb, hh, n, lo, hi, m0, ktiles, nk, q0) for hh in (h0, h0 + 1)])
    for h in range(8):
        S.dma("sp" if h % 2 else "pool", C.ybT[bi, h, :, :], yb.ap[:, h, :], reads=[yb])
    S.end_phase()


def l0_outproj(S, C, bi, src_stage, dst_stage):
    S.begin_phase()
    ya = S.sb("oya", [128, 4, T], BF16)
    yb = S.sb("oyb", [64, 8, T], BF16)
    wa = S.sb("owa", [128, 4, D], BF16)
    wb = S.sb("owb", [64, 8, D], BF16)
    for c in range(4):
        S.dma("sp", ya.ap[:, c, :], C.yaT[bi, c * 128:(c + 1) * 128, :], writes=[ya])
        S.dma("pool", wa.ap[:, c, :], C.wob0[c * 128:(c + 1) * 128, :], writes=[wa])
    for h in range(8):
        S.dma("sp", yb.ap[:, h, :], C.ybT[bi, h, :, :], writes=[yb])
        S.dma("pool", wb.ap[:, h, :], C.wob0[512 + h * 64:512 + (h + 1) * 64, :], writes=[wb])
    epi = Epi(S, C, "o")
    epi.load(0, 0, bi)
    xin = mk_ring(S, "oxin", [128, 1024], F32, 3)
    py = mk_ring(S, "opy", [128, 1024], F32, 2, psum=True)
    for i in range(NTT):
        xi = xin.next()
        S.dma("sp", xi.ap[:], tile_src(C, src_stage, bi, i), writes=[xi])
        p = py.next()
        for hf in range(2):
            for c in range(4):
                S.op("pe", lambda e: e.matmul(p.ap[:, hf * 512:(hf + 1) * 512], ya.ap[:, c, i * 128:(i + 1) * 128], wa.ap[:, c, hf * 512:(hf + 1) * 512],
                                              start=(c == 0), stop=False), reads=[ya, wa], writes=[p])
            for h in range(8):
                S.op("pe", lambda e: e.matmul(p.ap[:, hf * 512:(hf + 1) * 512], yb.ap[:, h, i * 128:(i + 1) * 128], wb.ap[:, h, hf * 512:(hf + 1) * 512],
                                              start=False, stop=(h == 7)), reads=[yb, wb], writes=[p])
        epi.run(p, xi, i < 2, C.xs[dst_stage - 1][bi, i * 128:(i + 1) * 128, :])
    S.end_phase()


def V(b, *idx):
    return (b, b.ap[idx] if idx else b.ap[:])


def TT(S, eng, o, a, b, op):
    return S.op(eng, lambda e: e.tensor_tensor(out=o[1], in0=a[1], in1=b[1], op=op), reads=[a[0], b[0]], writes=[o[0]])


def TS(S, eng, o, a, s1, s2, op0, op1=None, extra=()):
    if op1 is None:
        return S.op(eng, lambda e: e.tensor_scalar(out=o[1], in0=a[1], scalar1=s1, scalar2=None, op0=op0), reads=[a[0], *extra], writes=[o[0]])
    return S.op(eng, lambda e: e.tensor_scalar(out=o[1], in0=a[1], scalar1=s1, scalar2=s2, op0=op0, op1=op1), reads=[a[0], *extra], writes=[o[0]])


def STT(S, o, a, sc, b, op0, op1, extra=()):
    return S.op("dve", lambda e: e.scalar_tensor_tensor(out=o[1], in0=a[1], scalar=sc, in1=b[1], op0=op0, op1=op1), reads=[a[0], b[0], *extra], writes=[o[0]])


def ACT(S, o, a, func, scale=1.0, bias=None, extra=()):
    if bias is None:
        return S.op("act", lambda e: e.activation(out=o[1], in_=a[1], func=func, scale=scale), reads=[a[0], *extra], writes=[o[0]])
    return S.op("act", lambda e: e.activation(out=o[1], in_=a[1], func=func, scale=scale, bias=bias), reads=[a[0], *extra], writes=[o[0]])


def MM(S, o, l, r, start=True, stop=True):
    return S.op("pe", lambda e: e.matmul(o[1], l[1], r[1], start=start, stop=stop), reads=[l[0], r[0]], writes=[o[0]])


def TR(S, o, a, ident):
    return S.op("pe", lambda e: e.transpose(out=o[1], in_=a[1], identity=ident[1]), reads=[a[0], ident[0]], writes=[o[0]])


RW_E = float(np.exp(-0.5))
INV_DT = BF16


def declare_l1(nc, C, dbg=None):
    NB = C.NB
    I = lambda n, s, dt=F32: dram(nc, n, s, dt, "ExternalInput")
    Sx = lambda n, s, dt=BF16: dram(nc, n, s, dt, "ExternalOutput" if (dbg and n in dbg) else None)
    C.o_w_rw = I("o_w_rw", [D, 1920])
    C.o_w_dn = I("o_w_dn", [D, 1536])
    C.o_w_z = I("o_w_z", [D, 528])
    C.o_w_out = I("o_w_out", [D, D])
    C.rw_mu = I("rw_mu", [1920])
    C.dn_conv = I("dn_conv", [3, 1536])
    C.rw_rows = I("rw_rows", [10, 512])
    C.rw_w2 = I("rw_w2", [128, 512])
    C.rw_a2 = I("rw_a2", [128, 512])
    C.rw_g2 = I("rw_g2", [128, 512])
    C.dn_rows = I("dn_rows", [3, 8])
    C.dn_ng = I("dn_ng", [512])
    C.tri = I("tri", [2, 128, 128])
    C.tris = I("tris", [2, 128, 128])
    C.blkm = I("blkm", [4, 128, 128])
    C.tsw = Sx("tsw", [3, 1920], F32)
    C.wrb = [Sx(f"wrb{j}", [D, 1920]) for j in range(3)]
    C.wdb = [Sx(f"wdb{j}", [D, 1536]) for j in range(3)]
    C.wzb = Sx("wzb", [D, 528])
    C.wob1 = Sx("wob1", [D, D])
    C.rw_ops = Sx("rw_ops", [NB, 2, 6, T, 512])
    C.rw_v = Sx("rw_v", [NB, T, 512])
    C.rw_gc = Sx("rw_gc", [NB, 2, NTT, 64, 8], F32)
    C.rw_g = Sx("rw_g", [NB, T, 512], F32)
    C.rw_bonus = Sx("rw_bonus", [NB, T, 512], F32)
    C.y_rw = Sx("y_rw", [NB, 2, T, 512], F32)
    C.dn_qk = Sx("dn_qk", [NB, 2, T, 512])
    C.dn_ops = Sx("dn_ops", [NB, 2, 5, T, 512])
    C.dn_G = Sx("dn_G", [NB, 2, 3, T, 4], F32)
    C.dn_z = Sx("dn_z", [NB, T, 512], F32)
    C.y_dn = Sx("y_dn", [NB, 2, T, 512], F32)


def host_l1(inputs, m):
    f = lambda a: np.ascontiguousarray(np.asarray(a, dtype=np.float32))
    w = np.asarray(inputs["o_w_in"])[0]
    m["o_w_rw"] = f(w[:, :1920])
    m["o_w_dn"] = f(w[:, 1920:1920 + 1536])
    m["o_w_z"] = f(w[:, 1920 + 1536:])
    m["o_w_out"] = f(np.asarray(inputs["o_w_out"])[0])
    m["rw_mu"] = f(np.asarray(inputs["rw_mu"])[0])
    m["dn_conv"] = f(np.asarray(inputs["dn_conv"])[0])
    g = lambda k: np.asarray(inputs[k])[0]
    m["rw_rows"] = f(np.stack([g("rw_w0")[0], g("rw_w0")[1], g("rw_a0")[0], g("rw_a0")[1], g("rw_kk"), g("rw_ka"),
                               g("rw_rk").reshape(512), g("rw_lnx_g"), g("rw_lnx_b"), np.zeros(512, np.float32)], 0))
    m["rw_w2"] = f(g("rw_w2").reshape(128, 512))
    m["rw_a2"] = f(g("rw_a2").reshape(128, 512))
    m["rw_g2"] = f(g("rw_g2"))
    m["dn_rows"] = f(np.stack([g("dn_A_log").reshape(8), g("dn_dt_bias").reshape(8), np.zeros(8, np.float32)], 0))
    m["dn_ng"] = f(np.tile(g("dn_norm_g"), 4))
    j = np.arange(128)[:, None]
    t = np.arange(128)[None, :]
    m["tri"] = f(np.stack([(j <= t), (j >= t)], 0))
    m["tris"] = f(np.stack([(j < t), (j > t)], 0))
    bd = lambda n: (j // n == t // n)
    m["blkm"] = f(np.stack([bd(16), bd(32) & ~bd(16), bd(64) & ~bd(32), ~bd(64)], 0))
    return m


def l1_setup(S, C):
    S.begin_phase()
    mu = S.sb("smu", [1, 1920], F32)
    o = S.sb("smo", [1, 3, 1920], F32)
    S.dma("sp", mu.ap[:], C.rw_mu[:].partition_broadcast(1), writes=[mu])
    TS(S, "dve", V(o, slice(None), 0, slice(None)), V(mu), 0.5, None, ALU.mult)
    TS(S, "dve", V(o, slice(None), 1, slice(None)), V(mu), -1.0, 1.0, ALU.mult, ALU.add)
    TS(S, "dve", V(o, slice(None), 2, slice(None)), V(mu), 0.5, None, ALU.mult)
    S.dma("sp", C.tsw.rearrange("(o j) n -> o j n", o=1), o.ap[:], reads=[o])
    S.end_phase()
    for j in range(3):
        prep_weight(S, C.wrb[j], C.o_w_rw, D, 1920, scale=C.tsw[j, :], tag=f"qr{j}")
        prep_weight(S, C.wdb[j], C.o_w_dn, D, 1536, scale=C.dn_conv[j, :], tag=f"qd{j}")
    prep_weight(S, C.wzb, C.o_w_z, D, 528, tag="qz")
    prep_weight(S, C.wob1, C.o_w_out, D, D, tag="qo")


def build_hT(S, C, bi, src_stage, l):
    hT = S.sb("bhT", [128, 8, T + 4], BF16)
    S.op("pool", lambda e: e.memset(hT.ap[:], 0.0), writes=[hT])
    S.begin_sub()
    xin = mk_ring(S, "bxin", [128, 1024], F32, 2)
    ptr = mk_ring(S, "bptr", [128, 512], F32, 2, psum=True)
    for i in range(NTT):
        xi = xin.next()
        S.dma("sp", xi.ap[:], tile_src(C, src_stage, bi, i), writes=[xi])
        transpose_mod(S, C, xi, hT, colof(i), l, 0, (C.R - 1 if i < 2 else bi), ptr, C.ident)
    S.end_sub()
    return hT


def bc3(b, n, w):
    return (b, b.ap[:, 0:n].unsqueeze(2).to_broadcast([128, n, w]))


def r3(b, n, *pre):
    ap = b.ap[pre] if pre else b.ap[:]
    return (b, ap.rearrange("p (h d) -> p h d", h=n))


def proj3(S, C, p, hT, c0, w, wt, col0, ncol):
    n = 0
    for j in range(3):
        for kc in range(8):
            MM(S, (p, p.ap[:, :ncol]), (hT, hT.ap[:, kc, c0 + j - 1:c0 + j - 1 + 128]), (wt, wt.ap[:, j, kc, col0:col0 + ncol]), start=(n == 0), stop=(n == 23))
            n += 1


def l1_feat_rw(S, C, bi, hT):
    S.begin_sub()
    ones = S.sb("fones", [128, 128], F32)
    S.op("pool", lambda e: e.memset(ones.ap[:], 1.0), writes=[ones])
    tri = S.sb("ftri", [128, 2, 128], F32)
    for d in range(2):
        S.dma("sp", tri.ap[:, d, :], C.tri[d, :, :], writes=[tri])
    rows = S.sb("frows", [128, 9, 512], F32)
    for q in range(9):
        S.dma("sp", rows.ap[:, q, :], C.rw_rows[q, :].partition_broadcast(128), writes=[rows])
    lw = S.sb("flw", [128, 3, 512], F32)
    lwb = S.sb("flwb", [128, 3, 512], BF16)
    for q, src in enumerate((C.rw_w2, C.rw_a2, C.rw_g2)):
        S.dma("sp", lw.ap[:, q, :], src[:, :], writes=[lw])
    S.op("dve", lambda e: e.tensor_copy(out=lwb.ap[:], in_=lw.ap[:]), reads=[lw], writes=[lwb])
    loraT = S.sb("floraT", [128, 3, T], BF16)
    S.begin_sub()
    wl = S.sb("fwl", [128, 3, 8, 384], BF16)
    for j in range(3):
        for kc in range(8):
            S.dma("sp" if kc % 2 else "pool", wl.ap[:, j, kc, :], C.wrb[j][kc * 128:(kc + 1) * 128, 1536:1920], writes=[wl])
    pp = mk_ring(S, "fpp", [128, 512], F32, 2, psum=True)
    for (tok0, w) in [(0, 256)] + [(256 + 512 * k, 512) for k in range(4)]:
        c0 = colof(tok0 // 128)
        for q, fn in enumerate((AF.Tanh, AF.Identity, AF.Sigmoid)):
            p = pp.next()
            n = 0
            for j in range(3):
                for kc in range(8):
                    MM(S, (p, p.ap[:, :w]), (wl, wl.ap[:, j, kc, q * 128:(q + 1) * 128]), (hT, hT.ap[:, kc, c0 + j - 1:c0 + j - 1 + w]), start=(n == 0), stop=(n == 23))
                    n += 1
            ACT(S, (loraT, loraT.ap[:, q, tok0:tok0 + w]), (p, p.ap[:, :w]), fn)
    S.end_sub()
    wr = S.sb("fwr", [128, 3, 8, 1536], BF16)
    for j in range(3):
        for kc in range(8):
            S.dma("sp" if kc % 2 else "pool", wr.ap[:, j, kc, :], C.wrb[j][kc * 128:(kc + 1) * 128, 0:1536], writes=[wr])
    prkv = [S.ps(f"fp{n}", [128, 512], F32) for n in "rkv"]
    pq = mk_ring(S, "fpq", [128, 512], F32, 3, psum=True)
    pgc = S.ps("fpgc", [64, 8], F32)
    F = lambda n: S.sb("f_" + n, [128, 512], F32)
    rs, ks, kkr, sq, kk, zt, logw, a_, Gs, eG, enG, eE, t1, kd, bd, ksum = [F(n) for n in
        "rs ks kkr sq kk zt logw a Gs eG enG eE t1 kd bd ksum".split()]
    eP, tb, bon, gs = kkr, sq, zt, t1
    vb = mk_ring(S, "fvb", [128, 512], BF16, 2)
    ot = mk_ring(S, "fot", [128, 6, 512], BF16, 2)
    sm = mk_ring(S, "fsm", [128, 16], F32, 2)
    gcs = mk_ring(S, "fgcs", [64, 8], F32, 2)
    W0, A0, KK, KA, RK = [(rows, rows.ap[:, q, :]) for q in (0, 2, 4, 5, 6)]
    for i in range(NTT):
        c0 = colof(i)
        rws = slice(i * 128, (i + 1) * 128)
        for n in range(3):
            proj3(S, C, prkv[n], hT, c0, 128, wr, n * 512, 512)
        ACT(S, V(rs), V(prkv[0]), AF.Identity)
        ACT(S, V(ks), V(prkv[1]), AF.Identity)
        v_ = vb.next()
        ACT(S, V(v_), V(prkv[2]), AF.Identity)
        S.dma("pool", C.rw_v[bi, rws, :], v_.ap[:], reads=[v_])
        w = sm.next()
        TT(S, "dve", V(kkr), V(ks), KK, ALU.mult)
        TT(S, "pool", V(sq), V(kkr), V(kkr), ALU.mult)
        S.op("dve", lambda e: e.tensor_reduce(out=w.ap[:, 0:8], in_=r3(sq, 8)[1], axis=AX.X, op=ALU.add), reads=[sq], writes=[w])
        TS(S, "dve", V(w, slice(None), slice(0, 8)), V(w, slice(None), slice(0, 8)), 1e-6, None, ALU.add)
        ACT(S, V(w, slice(None), slice(0, 8)), V(w, slice(None), slice(0, 8)), AF.Sqrt)
        S.op("dve", lambda e: e.reciprocal(out=w.ap[:, 0:8], in_=w.ap[:, 0:8]), reads=[w], writes=[w])
        TT(S, "dve", r3(kk, 8), r3(kkr, 8), bc3(w, 8, 64), ALU.mult)
        for d in range(2):
            o = ot.next()
            pz, pa_ = pq.next(), pq.next()
            MM(S, V(pz), (loraT, loraT.ap[d * 64:(d + 1) * 64, 0, rws]), (lwb, lwb.ap[d * 64:(d + 1) * 64, 0, :]))
            MM(S, V(pa_), (loraT, loraT.ap[d * 64:(d + 1) * 64, 1, rws]), (lwb, lwb.ap[d * 64:(d + 1) * 64, 1, :]))
            TT(S, "dve", V(zt), V(pz), (rows, rows.ap[:, d, :]), ALU.add)
            ACT(S, V(zt), V(zt), AF.Sigmoid)
            TS(S, "pool", V(logw), V(zt), -RW_E, None, ALU.mult)
            TT(S, "dve", V(a_), V(pa_), (rows, rows.ap[:, 2 + d, :]), ALU.add)
            ACT(S, V(a_), V(a_), AF.Sigmoid)
            pG, pT = pq.next(), pq.next()
            MM(S, V(pG), (tri, tri.ap[:, d, :]), V(logw))
            MM(S, V(pT), V(ones), V(logw))
            for h in range(8):
                MM(S, (pgc, pgc.ap[:, h:h + 1]), (logw, logw.ap[:, h * 64:(h + 1) * 64]), (ones, ones.ap[:, 0:1]))
            gc = gcs.next()
            ACT(S, V(gc), V(pgc), AF.Exp)
            S.dma("pool", C.rw_gc[bi, d, i, :, :], gc.ap[:], reads=[gc])
            ACT(S, V(Gs), V(pG), AF.Identity)
            ACT(S, V(eG), V(Gs), AF.Exp)
            ACT(S, V(enG), V(Gs), AF.Exp, scale=-1.0)
            TT(S, "dve", V(eP), V(Gs), V(logw), ALU.subtract)
            ACT(S, V(eP), V(eP), AF.Exp)
            TT(S, "dve", V(eE), V(pT), V(Gs), ALU.subtract)
            ACT(S, V(eE), V(eE), AF.Exp)
            STT(S, V(t1), V(a_), -1.0, KA, ALU.add, ALU.mult)
            STT(S, V(kd), V(t1), 1.0, V(ks), ALU.add, ALU.mult)
            TT(S, "pool", V(bd), V(kk), V(a_), ALU.mult)
            O = lambda q: (o, o.ap[:, q, :])
            TT(S, "dve", O(0), V(rs), V(eG), ALU.mult)
            TT(S, "pool", O(1), V(kd), V(enG), ALU.mult)
            TT(S, "dve", O(2), V(bd), V(enG), ALU.mult)
            TT(S, "pool", O(3), V(kk), V(eP), ALU.mult)
            TT(S, "dve", O(4), V(kd), V(eE), ALU.mult)
            TT(S, "pool", O(5), V(bd), V(eE), ALU.mult)
            S.dma("sp", C.rw_ops[bi, d, :, rws, :].rearrange("q t c -> t q c"), o.ap[:], reads=[o])
            if d == 0:
                S.op("pool", lambda e: e.tensor_copy(out=ksum.ap[:], in_=kd.ap[:]), reads=[kd], writes=[ksum])
            else:
                TT(S, "pool", V(ksum), V(ksum), V(kd), ALU.add)
        TT(S, "dve", V(tb), V(rs), V(ksum), ALU.mult)
        TT(S, "pool", V(tb), V(tb), RK, ALU.mult)
        S.op("dve", lambda e: e.tensor_reduce(out=w.ap[:, 8:16], in_=r3(tb, 8)[1], axis=AX.X, op=ALU.add), reads=[tb], writes=[w])
        TT(S, "dve", r3(bon, 8), r3(v_, 8), (w, w.ap[:, 8:16].unsqueeze(2).to_broadcast([128, 8, 64])), ALU.mult)
        S.dma("pool", C.rw_bonus[bi, rws, :], bon.ap[:], reads=[bon])
        pg = pq.next()
        MM(S, V(pg), (loraT, loraT.ap[:, 2, rws]), (lwb, lwb.ap[:, 2, :]))
        ACT(S, V(gs), V(pg), AF.Identity)
        S.dma("pool", C.rw_g[bi, rws, :], gs.ap[:], reads=[gs])
    S.end_sub()


def l1_feat_dn(S, C, bi, hT):
    S.begin_sub()
    ones = S.sb("gones", [128, 128], F32)
    S.op("pool", lambda e: e.memset(ones.ap[:], 1.0), writes=[ones])
    tri = S.sb("gtri", [128, 2, 128], F32)
    for d in range(2):
        S.dma("sp", tri.ap[:, d, :], C.tri[d, :, :], writes=[tri])
    dr = S.sb("gdr", [128, 2, 8], F32)
    for q in range(2):
        S.dma("sp", dr.ap[:, q, :], C.dn_rows[q, :].partition_broadcast(128), writes=[dr])
    ACT(S, V(dr, slice(None), 0, slice(None)), V(dr, slice(None), 0, slice(None)), AF.Exp)
    TS(S, "dve", V(dr, slice(None), 0, slice(None)), V(dr, slice(None), 0, slice(None)), -1.0, None, ALU.mult)
    wd = S.sb("gwd", [128, 3, 8, 1536], BF16)
    wz = S.sb("gwz", [128, 8, 528], BF16)
    for j in range(3):
        for kc in range(8):
            S.dma("sp" if kc % 2 else "pool", wd.ap[:, j, kc, :], C.wdb[j][kc * 128:(kc + 1) * 128, :], writes=[wd])
    for kc in range(8):
        S.dma("sp", wz.ap[:, kc, :], C.wzb[kc * 128:(kc + 1) * 128, :], writes=[wz])
    pqkv = [S.ps(f"gp{n}", [128, 512], F32) for n in "qkv"]
    pz = S.ps("gpz", [128, 512], F32)
    pgt = S.ps("gpgt", [128, 16], F32)
    pG = mk_ring(S, "gpG", [128, 8], F32, 2, psum=True)
    F = lambda n: S.sb("g_" + n, [128, 512], F32)
    qs, ks, vs, sq, zs = [F(n) for n in "qs ks vs sq zs".split()]
    qk = mk_ring(S, "gqk", [128, 2, 512], BF16, 2)
    ot = mk_ring(S, "got", [128, 5, 512], BF16, 2)
    sm = mk_ring(S, "gsm", [128, 64], F32, 2)
    go = mk_ring(S, "ggo", [128, 3, 4], F32, 2)
    for i in range(NTT):
        c0 = colof(i)
        rws = slice(i * 128, (i + 1) * 128)
        for n in range(3):
            proj3(S, C, pqkv[n], hT, c0, 128, wd, n * 512, 512)
        for kc in range(8):
            MM(S, V(pz), (hT, hT.ap[:, kc, c0:c0 + 128]), (wz, wz.ap[:, kc, 0:512]), start=(kc == 0), stop=(kc == 7))
        for kc in range(8):
            MM(S, V(pgt), (hT, hT.ap[:, kc, c0:c0 + 128]), (wz, wz.ap[:, kc, 512:528]), start=(kc == 0), stop=(kc == 7))
        for src, dst in zip(pqkv + [pz], (qs, ks, vs, zs)):
            ACT(S, V(dst), V(src), AF.Silu)
        S.dma("pool", C.dn_z[bi, rws, :], zs.ap[:], reads=[zs])
        w = sm.next()
        Wc = lambda a, b: (w, w.ap[:, a:b])
        ACT(S, Wc(0, 16), V(pgt), AF.Identity)
        qk_ = qk.next()
        for n, (src, sc) in enumerate(((qs, 128.0 ** -0.5), (ks, 1.0))):
            TT(S, "pool", V(sq), V(src), V(src), ALU.mult)
            S.op("dve", lambda e: e.tensor_reduce(out=w.ap[:, 16 + 4 * n:20 + 4 * n], in_=r3(sq, 4)[1], axis=AX.X, op=ALU.add), reads=[sq], writes=[w])
            TS(S, "dve", Wc(16 + 4 * n, 20 + 4 * n), Wc(16 + 4 * n, 20 + 4 * n), 1e-6, None, ALU.add)
            ACT(S, Wc(16 + 4 * n, 20 + 4 * n), Wc(16 + 4 * n, 20 + 4 * n), AF.Sqrt)
            S.op("dve", lambda e: e.reciprocal(out=w.ap[:, 16 + 4 * n:20 + 4 * n], in_=w.ap[:, 16 + 4 * n:20 + 4 * n]), reads=[w], writes=[w])
            if sc != 1.0:
                TS(S, "dve", Wc(16, 20), Wc(16, 20), sc, None, ALU.mult)
            TT(S, "dve", r3(src, 4), r3(src, 4), (w, w.ap[:, 16 + 4 * n:20 + 4 * n].unsqueeze(2).to_broadcast([128, 4, 128])), ALU.mult)
            S.op("pool", lambda e: e.tensor_copy(out=qk_.ap[:, n, :], in_=src.ap[:]), reads=[src], writes=[qk_])
        S.dma("sp", C.dn_qk[bi, :, rws, :].rearrange("q t c -> t q c"), qk_.ap[:], reads=[qk_])
        for d in range(2):
            o = ot.next()
            g_ = go.next()
            TT(S, "dve", Wc(24, 28), Wc(d * 4, d * 4 + 4), (dr, dr.ap[:, 1, d * 4:d * 4 + 4]), ALU.add)
            ACT(S, Wc(24, 28), Wc(24, 28), AF.Exp)
            ACT(S, Wc(24, 28), Wc(24, 28), AF.Ln, bias=1.0)
            TT(S, "dve", (g_, g_.ap[:, 0, :]), Wc(24, 28), (dr, dr.ap[:, 0, d * 4:d * 4 + 4]), ALU.mult)
            ACT(S, Wc(28, 32), Wc(8 + d * 4, 12 + d * 4), AF.Sigmoid)
            p = pG.next()
            MM(S, (p, p.ap[:, 0:4]), (tri, tri.ap[:, d, :]), (g_, g_.ap[:, 0, :]))
            MM(S, (p, p.ap[:, 4:8]), V(ones), (g_, g_.ap[:, 0, :]))
            ACT(S, (g_, g_.ap[:, 1:3, :]), (p, p.ap[:, 0:8].rearrange("p (a b) -> p a b", a=2)), AF.Identity)
            S.dma("pool", C.dn_G[bi, d, :, rws, :].rearrange("q t c -> t q c"), g_.ap[:], reads=[g_])
            ACT(S, Wc(32, 36), (g_, g_.ap[:, 1, :]), AF.Exp)
            TT(S, "dve", Wc(36, 40), (g_, g_.ap[:, 2, :]), (g_, g_.ap[:, 1, :]), ALU.subtract)
            ACT(S, Wc(36, 40), Wc(36, 40), AF.Exp)
            TT(S, "dve", Wc(40, 44), Wc(28, 32), Wc(32, 36), ALU.mult)
            B4 = lambda a: (w, w.ap[:, a:a + 4].unsqueeze(2).to_broadcast([128, 4, 128]))
            O = lambda q: (o, o.ap[:, q, :].rearrange("p (h d) -> p h d", h=4))
            TT(S, "dve", O(0), r3(qs, 4), B4(32), ALU.mult)
            TT(S, "pool", O(1), r3(ks, 4), B4(28), ALU.mult)
            TT(S, "dve", O(2), r3(ks, 4), B4(40), ALU.mult)
            TT(S, "pool", O(3), r3(ks, 4), B4(36), ALU.mult)
            TT(S, "dve", O(4), r3(vs, 4), B4(28), ALU.mult)
            S.dma("sp", C.dn_ops[bi, d, :, rws, :].rearrange("q t c -> t q c"), o.ap[:], reads=[o])
    S.end_sub()


def inv_group(S, P, PT, K, ps, out, nh=4):
    mk = lambda: K.rW.next()
    pA, pB, pC = ps
    PD, PDT, Z, ZT = mk(), mk(), K.rZ.next(), K.rZ.next()
    TT(S, "dve", V(PD), V(P), V(K.mb[0]), ALU.mult)
    TT(S, "pool", V(PDT), V(PT), V(K.mb[0]), ALU.mult)
    TT(S, "dve", V(Z), V(K.ident4), V(PD), ALU.subtract)
    TT(S, "pool", V(ZT), V(K.ident4), V(PDT), ALU.subtract)
    yield
    cur, curT = PD, PDT
    for lv in range(3):
        for h in range(nh):
            MM(S, (pA, pA.ap[:, h, :]), (curT, curT.ap[:, h, :]), (cur, cur.ap[:, h, :]))
        for h in range(nh):
            MM(S, (pB, pB.ap[:, h, :]), (cur, cur.ap[:, h, :]), (curT, curT.ap[:, h, :]))
        Pn, PTn = mk(), mk()
        ACT(S, V(Pn), V(pA), AF.Identity)
        S.op("dve", lambda e: e.tensor_copy(out=PTn.ap[:], in_=pB.ap[:]), reads=[pB], writes=[PTn])
        yield
        for h in range(nh):
            MM(S, (pC, pC.ap[:, h, :]), (PTn, PTn.ap[:, h, :]), (Z, Z.ap[:, h, :]))
        for h in range(nh):
            MM(S, (pA, pA.ap[:, h, :]), (Pn, Pn.ap[:, h, :]), (ZT, ZT.ap[:, h, :]))
        TT(S, "dve", V(Z), V(Z), V(pC), ALU.add)
        TT(S, "dve", V(ZT), V(ZT), V(pA), ALU.add)
        yield
        cur, curT = Pn, PTn
    for m in range(1, 4):
        last = m == 3
        O, OT, Y = mk(), mk(), mk()
        TT(S, "dve", V(O), V(P), V(K.mb[m]), ALU.mult)
        TT(S, "pool", V(OT), V(PT), V(K.mb[m]), ALU.mult)
        for h in range(nh):
            MM(S, (pA, pA.ap[:, h, :]), (OT, OT.ap[:, h, :]), (Z, Z.ap[:, h, :]))
        ACT(S, V(Y), V(pA), AF.Identity)
        if not last:
            YT = mk()
            for h in range(nh):
                MM(S, (pB, pB.ap[:, h, :]), (Z, Z.ap[:, h, :]), (OT, OT.ap[:, h, :]))
            S.op("dve", lambda e: e.tensor_copy(out=YT.ap[:], in_=pB.ap[:]), reads=[pB], writes=[YT])
        yield
        for h in range(nh):
            MM(S, (pC, pC.ap[:, h, :]), (ZT, ZT.ap[:, h, :]), (Y, Y.ap[:, h, :]))
        if not last:
            for h in range(nh):
                MM(S, (pB, pB.ap[:, h, :]), (Y, Y.ap[:, h, :]), (ZT, ZT.ap[:, h, :]))
        TT(S, "dve", V(Z), V(Z), V(pC), ALU.subtract)
        if not last:
            TT(S, "dve", V(ZT), V(ZT), V(pB), ALU.subtract)
        yield
    out.append(Z)


def interleave(gens):
    gens = list(gens)
    while gens:
        for g in list(gens):
            try:
                next(g)
            except StopIteration:
                gens.remove(g)


def scan_consts(S, C, tag):
    K = Ctx()
    K.idb = S.sb(tag + "idb", [128, 128], BF16)
    S.dma("sp", K.idb.ap[:], C.ident_b[:, :], writes=[K.idb])
    K.ident4 = S.sb(tag + "id4", [128, 4, 128], F32)
    K.mSI = [S.sb(tag + f"mSI{d}", [128, 4, 2, 128], F32) for d in range(2)]
    K.mS = [S.sb(tag + f"mS{d}", [128, 4, 128], F32) for d in range(2)]
    K.mI = [S.sb(tag + f"mI{d}", [128, 4, 128], F32) for d in range(2)]
    for h in range(4):
        S.dma("sp", K.ident4.ap[:, h, :], C.ident_d[:, :], writes=[K.ident4])
        for d in range(2):
            S.dma("sp", K.mSI[d].ap[:, h, 0, :], C.tris[d, :, :], writes=[K.mSI[d]])
            S.dma("pool", K.mSI[d].ap[:, h, 1, :], C.tri[d, :, :], writes=[K.mSI[d]])
            S.dma("sp", K.mS[d].ap[:, h, :], C.tris[d, :, :], writes=[K.mS[d]])
            S.dma("pool", K.mI[d].ap[:, h, :], C.tri[d, :, :], writes=[K.mI[d]])
    K.mb = [S.sb(tag + f"mb{m}", [128, 4, 128], F32) for m in range(4)]
    for m in range(4):
        for h in range(4):
            S.dma("sp" if h % 2 else "pool", K.mb[m].ap[:, h, :], C.blkm[m, :, :], writes=[K.mb[m]])
    return K


def chain_res(S, K, tag):
    R = Ctx()
    R.__dict__.update(K.__dict__)
    R.rP = mk_ring(S, tag + "rP", [128, 4, 128], INV_DT, 1)
    R.rPT = mk_ring(S, tag + "rPT", [128, 4, 128], INV_DT, 1)
    R.rW = mk_ring(S, tag + "rW", [128, 4, 128], INV_DT, 8)
    R.rZ = mk_ring(S, tag + "rZ", [128, 4, 128], INV_DT, 4)
    return R


def tile_order(d):
    return list(range(NTT)) if d == 0 else [1, 0] + list(range(NTT - 1, 1, -1))


def l1_scan_rw(S, C, bi):
    S.begin_phase()
    K0 = scan_consts(S, C, "r")
    interleave([rw_chain(S, C, chain_res(S, K0, f"r{d}"), bi, d) for d in range(2)])
    S.end_phase()


def rw_chain(S, C, K, bi, d):
    t = f"r{d}"
    X0, X1, X2 = [S.ps(t + f"X{n}", [128, 4, 128], F32) for n in range(3)]
    ptr = S.ps(t + "ptr", [64, 8, 128], BF16)
    ot_r = mk_ring(S, t + "ot", [128, 6, 512], BF16, 2)
    vt_r = mk_ring(S, t + "vt", [128, 512], BF16, 2)
    gc_r = mk_ring(S, t + "gc", [64, 8], F32, 2)
    AR_r = mk_ring(S, t + "AR", [64, 8, 2, 128], BF16, 1)
    KT_r = mk_ring(S, t + "KT", [64, 8, 128], BF16, 1)
    BT_r = mk_ring(S, t + "BT", [64, 8, 128], BF16, 1)
    KN_r = mk_ring(S, t + "KN", [128, 8, 2, 128], BF16, 1)
    NB_r = mk_ring(S, t + "NB", [128, 8, 128], BF16, 1)
    Zb_r = mk_ring(S, t + "Zb", [128, 4, 128], BF16, 1)
    WT_r = mk_ring(S, t + "WT", [64, 8, 128], BF16, 1)
    Xs_r = mk_ring(S, t + "Xs", [128, 4, 64], BF16, 1)
    nU0_r = mk_ring(S, t + "nU0", [128, 8, 64], F32, 1)
    nU_r = mk_ring(S, t + "nU", [128, 8, 64], BF16, 1)
    ys_r = mk_ring(S, t + "ys", [128, 512], F32, 2)
    ST = S.sb(t + "ST", [64, 8, 64], F32)
    STb = S.sb(t + "STb", [64, 8, 64], BF16)
    S.op("dve", lambda e: e.memset(ST.ap[:], 0.0), writes=[ST])
    S.op("dve", lambda e: e.memset(STb.ap[:], 0.0), writes=[STb])
    v8 = lambda p: p.ap[:].rearrange("p a b -> p (a b)").rearrange("p (h v) -> p h v", v=64)
    for i in tile_order(d):
        rws = slice(i * 128, (i + 1) * 128)
        ot, vt, gc = ot_r.next(), vt_r.next(), gc_r.next()
        S.dma("sp", ot.ap[:], C.rw_ops[bi, d, :, rws, :].rearrange("q t c -> t q c"), writes=[ot])
        S.dma("pool", vt.ap[:], C.rw_v[bi, rws, :], writes=[vt])
        S.dma("pool", gc.ap[:], C.rw_gc[bi, d, i, :, :], writes=[gc])
        AR, KT, BT = AR_r.next(), KT_r.next(), BT_r.next()
        for q, dst in ((3, (AR, AR.ap[:, :, 0, :])), (0, (AR, AR.ap[:, :, 1, :])), (1, V(KT)), (2, V(BT))):
            for h in range(8):
                TR(S, (ptr, ptr.ap[:, h, :]), (ot, ot.ap[:, q, h * 64:(h + 1) * 64]), V(K.idb))
            if q in (3, 1):
                ACT(S, dst, V(ptr), AF.Identity)
            else:
                S.op("dve", lambda e: e.tensor_copy(out=dst[1], in_=ptr.ap[:]), reads=[ptr], writes=[dst[0]])
            yield
        KN, NBm, WT, nU0 = KN_r.next(), NB_r.next(), WT_r.next(), nU0_r.next()
        for g in range(2):
            hs = [(hl, g * 4 + hl) for hl in range(4)]
            P, PT = K.rP.next(), K.rPT.next()
            for hl, h in hs:
                MM(S, (X0, X0.ap[:, hl, :]), (BT, BT.ap[:, h, :]), (AR, AR.ap[:, h, 0, :]))
            for hl, h in hs:
                MM(S, (X1, X1.ap[:, hl, :]), (AR, AR.ap[:, h, 0, :]), (BT, BT.ap[:, h, :]))
            for hl, h in hs:
                MM(S, (X2, X2.ap[:, hl, :]), (BT, BT.ap[:, h, :]), (AR, AR.ap[:, h, 1, :]))
            TT(S, "dve", V(P), V(X0), V(K.mS[d]), ALU.mult)
            TT(S, "dve", V(PT), V(X1), V(K.mS[1 - d]), ALU.mult)
            TT(S, "dve", (NBm, NBm.ap[:, g * 4:(g + 1) * 4, :]), V(X2), V(K.mI[d]), ALU.mult)
            yield
            for hl, h in hs:
                MM(S, (X0, X0.ap[:, hl, :]), (KT, KT.ap[:, h, :]), (AR, AR.ap[:, h, 0, :]))
            for hl, h in hs:
                MM(S, (X1, X1.ap[:, hl, :]), (KT, KT.ap[:, h, :]), (AR, AR.ap[:, h, 1, :]))
            TT(S, "dve", (KN, KN.ap[:, g * 4:(g + 1) * 4, 0, :]), V(X0), V(K.mS[d]), ALU.mult)
            TT(S, "dve", (KN, KN.ap[:, g * 4:(g + 1) * 4, 1, :]), V(X1), V(K.mI[d]), ALU.mult)
            yield
            zo = []
            yield from inv_group(S, P, PT, K, (X0, X1, X2), zo)
            Zb = zo[0]
            for hl, h in hs:
                MM(S, (X0, X0.ap[:, hl, 0:64]), (KN, KN.ap[:, h, 0, :]), (vt, vt.ap[:, h * 64:(h + 1) * 64]))
            Xs = Xs_r.next()
            ACT(S, V(Xs), (X0, X0.ap[:, :, 0:64]), AF.Identity)
            yield
            for hl, h in hs:
                MM(S, (X1, X1.ap[:64, hl, :]), (ot, ot.ap[:, 3, h * 64:(h + 1) * 64]), (Zb, Zb.ap[:, hl, :]))
            ACT(S, (WT, WT.ap[:, g * 4:(g + 1) * 4, :]), (X1, X1.ap[:64, :, :]), AF.Identity)
            for hl, h in hs:
                MM(S, (X2, X2.ap[:, hl, 0:64]), (Zb, Zb.ap[:, hl, :]), (Xs, Xs.ap[:, hl, :]))
            TS(S, "dve", (nU0, nU0.ap[:, g * 4:(g + 1) * 4, :]), (X2, X2.ap[:, :, 0:64]), -1.0, None, ALU.mult)
            yield
        for h in range(8):
            MM(S, (X0, v8(X0)[:, h, :]), (WT, WT.ap[:, h, :]), (STb, STb.ap[:, h, :]))
        nU = nU_r.next()
        TT(S, "dve", V(nU), V(nU0), (X0, v8(X0)), ALU.subtract)
        yield
        for h in range(8):
            MM(S, (X1, v8(X1)[:, h, :]), (AR, AR.ap[:, h, 1, :]), (STb, STb.ap[:, h, :]), start=True, stop=False)
            MM(S, (X1, v8(X1)[:, h, :]), (KN, KN.ap[:, h, 1, :]), (vt, vt.ap[:, h * 64:(h + 1) * 64]), start=False, stop=False)
            MM(S, (X1, v8(X1)[:, h, :]), (NBm, NBm.ap[:, h, :]), (nU, nU.ap[:, h, :]), start=False, stop=True)
        ys = ys_r.next()
        ACT(S, r3(ys, 8), (X1, v8(X1)), AF.Identity)
        S.dma("sp", C.y_rw[bi, d, rws, :], ys.ap[:], reads=[ys])
        for h in range(8):
            MM(S, (X2, v8(X2)[:64, h, :]), (ot, ot.ap[:, 4, h * 64:(h + 1) * 64]), (vt, vt.ap[:, h * 64:(h + 1) * 64]), start=True, stop=False)
            MM(S, (X2, v8(X2)[:64, h, :]), (ot, ot.ap[:, 5, h * 64:(h + 1) * 64]), (nU, nU.ap[:, h, :]), start=False, stop=True)
        TT(S, "dve", V(ST), V(ST), (gc, gc.ap[:, 0:8].unsqueeze(2).to_broadcast([64, 8, 64])), ALU.mult)
        TT(S, "dve", V(ST), V(ST), (X2, v8(X2)[:64, :, :]), ALU.add)
        ACT(S, V(STb), V(ST), AF.Identity)
        yield


def l1_scan_dn(S, C, bi):
    S.begin_phase()
    K0 = scan_consts(S, C, "d")
    K0.ones = S.sb("dones", [128, 128], F32)
    S.op("pool", lambda e: e.memset(K0.ones.ap[:], 1.0), writes=[K0.ones])
    K0.tri = S.sb("dtri", [128, 2, 128], F32)
    for d in range(2):
        S.dma("sp", K0.tri.ap[:, d, :], C.tri[d, :, :], writes=[K0.tri])
    interleave([dn_chain(S, C, chain_res(S, K0, f"d{d}"), bi, d) for d in range(2)])
    S.end_phase()


def dn_chain(S, C, K, bi, d):
    t = f"d{d}"
    ones, tri = K.ones, K.tri
    X0, X1, X2 = [S.ps(t + f"X{n}", [128, 4, 128], F32) for n in range(3)]
    ptr = S.ps(t + "ptr", [128, 4, 128], BF16)
    ot_r = mk_ring(S, t + "ot", [128, 5, 512], BF16, 2)
    qk_r = mk_ring(S, t + "qk", [128, 2, 512], BF16, 2)
    G_r = mk_ring(S, t + "G", [128, 3, 4], F32, 2)
    FT_r = mk_ring(S, t + "FT", [128, 4, 4, 128], BF16, 2)
    gl_r = mk_ring(S, t + "gl", [128, 4, 128], F32, 1)
    ET_r = mk_ring(S, t + "ET", [128, 4, 128], F32, 1)
    qkT_r = mk_ring(S, t + "qkT", [128, 4, 128], BF16, 2)
    Zb_r = mk_ring(S, t + "Zb", [128, 4, 128], BF16, 2)
    wT_r = mk_ring(S, t + "wT", [128, 4, 128], BF16, 2)
    u0_r = mk_ring(S, t + "u0", [128, 4, 128], F32, 2)
    u_r = mk_ring(S, t + "u", [128, 4, 128], BF16, 2)
    ys_r = mk_ring(S, t + "ys", [128, 512], F32, 2)
    sm_r = mk_ring(S, t + "sm", [128, 8], F32, 2)
    ST = S.sb(t + "ST", [128, 4, 128], F32)
    STb = S.sb(t + "STb", [128, 4, 128], BF16)
    S.op("dve", lambda e: e.memset(ST.ap[:], 0.0), writes=[ST])
    S.op("dve", lambda e: e.memset(STb.ap[:], 0.0), writes=[STb])
    for i in tile_order(d):
        rws = slice(i * 128, (i + 1) * 128)
        ot, qk, G = ot_r.next(), qk_r.next(), G_r.next()
        S.dma("sp", ot.ap[:], C.dn_ops[bi, d, :, rws, :].rearrange("q t c -> t q c"), writes=[ot])
        S.dma("pool", qk.ap[:], C.dn_qk[bi, :, rws, :].rearrange("q t c -> t q c"), writes=[qk])
        S.dma("pool", G.ap[:], C.dn_G[bi, d, :, rws, :].rearrange("q t c -> t q c"), writes=[G])
        FT = FT_r.next()
        for q, src in enumerate(((qk, 1), (qk, 0), (ot, 1), (ot, 0))):
            for h in range(4):
                TR(S, (ptr, ptr.ap[:, h, :]), (src[0], src[0].ap[:, src[1], h * 128:(h + 1) * 128]), V(K.idb))
            if q % 2:
                ACT(S, (FT, FT.ap[:, q, :, :]), V(ptr), AF.Identity)
            else:
                S.op("dve", lambda e: e.tensor_copy(out=FT.ap[:, q, :, :], in_=ptr.ap[:]), reads=[ptr], writes=[FT])
            yield
        gl = gl_r.next()
        for h in range(4):
            TS(S, "pool", (gl, gl.ap[:, h, :]), (tri, tri.ap[:, d, :]), G.ap[:, 0, h:h + 1], None, ALU.mult, extra=[G])
        for h in range(4):
            MM(S, (X0, X0.ap[:, h, :]), V(ones), (gl, gl.ap[:, h, :]))
        ET = ET_r.next()
        for h in range(4):
            TS(S, "dve", (ET, ET.ap[:, h, :]), (X0, X0.ap[:, h, :]), G.ap[:, 1, h:h + 1], 0.0, ALU.subtract, ALU.min, extra=[G])
        ACT(S, V(ET), V(ET), AF.Exp)
        yield
        for h in range(4):
            MM(S, (X1, X1.ap[:, h, :]), (FT, FT.ap[:, 0, h, :]), (FT, FT.ap[:, 2, h, :]))
            MM(S, (X2, X2.ap[:, h, :]), (FT, FT.ap[:, 0, h, :]), (FT, FT.ap[:, 1, h, :]))
        P, PT = K.rP.next(), K.rPT.next()
        TT(S, "dve", V(P), V(X1), V(ET), ALU.mult)
        TT(S, "dve", V(P), V(P), V(K.mS[d]), ALU.mult)
        qkT = qkT_r.next()
        TT(S, "pool", V(ET), V(ET), V(K.mI[d]), ALU.mult)
        TT(S, "dve", V(qkT), V(X2), V(ET), ALU.mult)
        yield
        for h in range(4):
            TR(S, (ptr, ptr.ap[:, h, :]), (P, P.ap[:, h, :]), V(K.idb))
        ACT(S, V(PT), V(ptr), AF.Identity)
        yield
        zo = []
        yield from inv_group(S, P, PT, K, (X0, X1, X2), zo)
        Zb = zo[0]
        for h in range(4):
            MM(S, (X0, X0.ap[:, h, :]), (Zb, Zb.ap[:, h, :]), (ot, ot.ap[:, 4, h * 128:(h + 1) * 128]))
            MM(S, (X1, X1.ap[:, h, :]), (ot, ot.ap[:, 2, h * 128:(h + 1) * 128]), (Zb, Zb.ap[:, h, :]))
        u0, wT = u0_r.next(), wT_r.next()
        ACT(S, V(u0), V(X0), AF.Identity)
        S.op("dve", lambda e: e.tensor_copy(out=wT.ap[:], in_=X1.ap[:]), reads=[X1], writes=[wT])
        yield
        for h in range(4):
            MM(S, (X2, X2.ap[:, h, :]), (wT, wT.ap[:, h, :]), (STb, STb.ap[:, h, :]))
        u = u_r.next()
        TT(S, "dve", V(u), V(u0), V(X2), ALU.subtract)
        yield
        for h in range(4):
            MM(S, (X0, X0.ap[:, h, :]), (FT, FT.ap[:, 3, h, :]), (STb, STb.ap[:, h, :]), start=True, stop=False)
            MM(S, (X0, X0.ap[:, h, :]), (qkT, qkT.ap[:, h, :]), (u, u.ap[:, h, :]), start=False, stop=True)
        ys = ys_r.next()
        ACT(S, r3(ys, 4), V(X0), AF.Identity)
        S.dma("sp", C.y_dn[bi, d, rws, :], ys.ap[:], reads=[ys])
        for h in range(4):
            MM(S, (X1, X1.ap[:, h, :]), (ot, ot.ap[:, 3, h * 128:(h + 1) * 128]), (u, u.ap[:, h, :]))
        sm = sm_r.next()
        ACT(S, (sm, sm.ap[:, 0:4]), (G, G.ap[:, 2, :]), AF.Exp)
        TT(S, "dve", V(ST), V(ST), (sm, sm.ap[:, 0:4].unsqueeze(2).to_broadcast([128, 4, 128])), ALU.mult)
        TT(S, "dve", V(ST), V(ST), V(X1), ALU.add)
        ACT(S, V(STb), V(ST), AF.Identity)
        yield


def l1_out(S, C, bi, src_stage, dst_stage, last):
    S.begin_phase()
    wo = S.sb("xwo", [128, 8, D], BF16)
    for c in range(8):
        S.dma("sp" if c % 2 else "pool", wo.ap[:, c, :], C.wob1[c * 128:(c + 1) * 128, :], writes=[wo])
    rows = S.sb("xrows", [128, 3, 512], F32)
    S.dma("sp", rows.ap[:, 0, :], C.rw_rows[7, :].partition_broadcast(128), writes=[rows])
    S.dma("sp", rows.ap[:, 1, :], C.rw_rows[8, :].partition_broadcast(128), writes=[rows])
    S.dma("sp", rows.ap[:, 2, :], C.dn_ng[:].partition_broadcast(128), writes=[rows])
    epi = Epi(S, C, "x")
    epi.load(1, 0, bi)
    xin = mk_ring(S, "xxin", [128, 1024], F32, 2)
    ya = mk_ring(S, "xya", [128, 2, 512], F32, 2)
    yb = mk_ring(S, "xyb", [128, 2, 512], F32, 2)
    ex = mk_ring(S, "xex", [128, 3, 512], F32, 2)
    sq = S.sb("xsq", [128, 512], F32)
    ycat = mk_ring(S, "xyc", [128, 1024], F32, 2)
    yT = mk_ring(S, "xyT", [128, 8, 128], BF16, 2)
    sm = mk_ring(S, "xsm", [128, 32], F32, 2)
    ptr = mk_ring(S, "xptr", [128, 4, 128], F32, 2, psum=True)
    py = mk_ring(S, "xpy", [128, 1024], F32, 2, psum=True)
    for i in range(2 if last else 0, NTT):
        rws = slice(i * 128, (i + 1) * 128)
        xi, a, b, e_, yc, w = xin.next(), ya.next(), yb.next(), ex.next(), ycat.next(), sm.next()
        S.dma("sp", xi.ap[:], tile_src(C, src_stage, bi, i), writes=[xi])
        S.dma("sp", a.ap[:], C.y_rw[bi, :, rws, :].rearrange("q t c -> t q c"), writes=[a])
        S.dma("pool", b.ap[:], C.y_dn[bi, :, rws, :].rearrange("q t c -> t q c"), writes=[b])
        S.dma("sp", e_.ap[:, 0, :], C.rw_g[bi, rws, :], writes=[e_])
        S.dma("pool", e_.ap[:, 1, :], C.rw_bonus[bi, rws, :], writes=[e_])
        S.dma("sp", e_.ap[:, 2, :], C.dn_z[bi, rws, :], writes=[e_])
        y = (a, a.ap[:, 0, :])
        TT(S, "dve", y, y, (a, a.ap[:, 1, :]), ALU.add)
        S.op("dve", lambda e: e.tensor_reduce(out=w.ap[:, 0:8], in_=r3(a, 8, slice(None), 0, slice(None))[1], axis=AX.X, op=ALU.add), reads=[a], writes=[w])
        TS(S, "dve", (w, w.ap[:, 0:8]), (w, w.ap[:, 0:8]), 1.0 / 64, None, ALU.mult)
        TT(S, "dve", r3(a, 8, slice(None), 0, slice(None)), r3(a, 8, slice(None), 0, slice(None)), bc3(w, 8, 64), ALU.subtract)
        TT(S, "pool", V(sq), y, y, ALU.mult)
        S.op("dve", lambda e: e.tensor_reduce(out=w.ap[:, 8:16], in_=r3(sq, 8)[1], axis=AX.X, op=ALU.add), reads=[sq], writes=[w])
        TS(S, "dve", (w, w.ap[:, 8:16]), (w, w.ap[:, 8:16]), 1.0 / 64, 64e-5, ALU.mult, ALU.add)
        ACT(S, (w, w.ap[:, 8:16]), (w, w.ap[:, 8:16]), AF.Sqrt)
        S.op("dve", lambda e: e.reciprocal(out=w.ap[:, 8:16], in_=w.ap[:, 8:16]), reads=[w], writes=[w])
        TT(S, "dve", r3(a, 8, slice(None), 0, slice(None)), r3(a, 8, slice(None), 0, slice(None)),
           (w, w.ap[:, 8:16].unsqueeze(2).to_broadcast([128, 8, 64])), ALU.mult)
        TT(S, "pool", y, y, (rows, rows.ap[:, 0, :]), ALU.mult)
        TT(S, "pool", y, y, (rows, rows.ap[:, 1, :]), ALU.add)
        TT(S, "dve", y, y, (e_, e_.ap[:, 1, :]), ALU.add)
        TT(S, "dve", (yc, yc.ap[:, 0:512]), y, (e_, e_.ap[:, 0, :]), ALU.mult)
        o = (b, b.ap[:, 0, :])
        TT(S, "dve", o, o, (b, b.ap[:, 1, :]), ALU.add)
        TT(S, "pool", V(sq), o, o, ALU.mult)
        S.op("dve", lambda e: e.tensor_reduce(out=w.ap[:, 16:20], in_=r3(sq, 4)[1], axis=AX.X, op=ALU.add), reads=[sq], writes=[w])
        TS(S, "dve", (w, w.ap[:, 16:20]), (w, w.ap[:, 16:20]), 1.0 / 128, 1e-6, ALU.mult, ALU.add)
        ACT(S, (w, w.ap[:, 16:20]), (w, w.ap[:, 16:20]), AF.Sqrt)
        S.op("dve", lambda e: e.reciprocal(out=w.ap[:, 16:20], in_=w.ap[:, 16:20]), reads=[w], writes=[w])
        TT(S, "dve", r3(b, 4, slice(None), 0, slice(None)), r3(b, 4, slice(None), 0, slice(None)),
           (w, w.ap[:, 16:20].unsqueeze(2).to_broadcast([128, 4, 128])), ALU.mult)
        TT(S, "pool", o, o, (rows, rows.ap[:, 2, :]), ALU.mult)
        TT(S, "dve", (yc, yc.ap[:, 512:1024]), o, (e_, e_.ap[:, 2, :]), ALU.mult)
        yt = yT.next()
        for g in range(2):
            p = ptr.next()
            for j in range(4):
                c = g * 4 + j
                S.op("pe", lambda e: e.transpose(out=p.ap[:, j, :], in_=yc.ap[:, c * 128:(c + 1) * 128], identity=C.ident.ap[:]), reads=[yc, C.ident], writes=[p])
            ACT(S, (yt, yt.ap[:, g * 4:(g + 1) * 4, :]), V(p), AF.Identity)
        p = py.next()
        for hf in range(2):
            for c in range(8):
                MM(S, (p, p.ap[:, hf * 512:(hf + 1) * 512]), (yt, yt.ap[:, c, :]), (wo, wo.ap[:, c, hf * 512:(hf + 1) * 512]), start=(c == 0), stop=(c == 7))
        if last:
            dst = C.xs[dst_stage - 1][bi, rws, :]
        else:
            dst = C.xs[dst_stage - 1][bi, rws, :]
        epi.run(p, xi, i < 2, dst)
    S.end_phase()


NB_FULL = 4
N_CORES = 8


def build_full(NB, dbg=None):
    nc = bass.Bass("TRN2", target_bir_lowering=False)
    C = declare_common(nc, NB, dbg=dbg)
    declare_l0(nc, C, dbg=dbg)
    declare_l1(nc, C, dbg=dbg)
    S = Sched(nc)
    common_setup(S, C)
    phase_mod(S, C)
    for l in range(2):
        prep_weight(S, C.w1b[l], C.mlp_w1[l], D, DFF, tag=f"pm1{l}")
        prep_weight(S, C.w2b[l], C.mlp_w2[l], DFF, D, tag=f"pm2{l}")
    l0_setup(S, C)
    l1_setup(S, C)
    for bi in range(NB):
        l0_inproj(S, C, bi, 0)
        l0_hyena(S, C, bi)
        l0_attn(S, C, bi)
        l0_outproj(S, C, bi, 0, 1)
        phase_mlp(S, C, 0, bi, 1, 2, False)
        S.begin_phase()
        hT = build_hT(S, C, bi, 2, 1)
        l1_feat_rw(S, C, bi, hT)
        l1_feat_dn(S, C, bi, hT)
        S.end_phase()
        l1_scan_rw(S, C, bi)
        l1_scan_dn(S, C, bi)
        l1_out(S, C, bi, 2, 3, True)
        phase_mlp(S, C, 1, bi, 3, None, True)
    S.finish()
    return nc


def kernel(**inputs):
    NB = NB_FULL
    nc = build_full(NB)
    in_maps = [host_l1(inputs, host_l0(inputs, host_common(inputs, c, NB))) for c in range(N_CORES)]
    res = run_bass_kernel_spmd(nc, in_maps, core_ids=list(range(N_CORES)))
    out = np.concatenate([np.asarray(r["out"], dtype=np.float32) for r in res.results], axis=0)
    return out
```

```python
import numpy as np
from contextlib import ExitStack
import concourse.bass as bass
import concourse.mybir as mybir
from concourse.bass_utils import run_bass_kernel_spmd

F32 = mybir.dt.float32
BF16 = mybir.dt.bfloat16
AF = mybir.ActivationFunctionType
ALU = mybir.AluOpType
AX = mybir.AxisListType

SAME_ENGINE_SYNC = True
NO_SWDGE = True
EPOCH = 30000


class Buf:
    __slots__ = ("ap", "name", "lw", "rd")

    def __init__(self, ap, name):
        self.ap = ap
        self.name = name
        self.lw = None
        self.rd = []


class Sched:
    def __init__(self, nc, ndma=24):
        self.nc = nc
        self.stack = ExitStack()
        self.engs = {"pe": nc.tensor, "act": nc.scalar, "dve": nc.vector, "pool": nc.gpsimd, "sp": nc.sync}
        self.csem = {}
        self.ccnt = {}
        self.nsem = 0
        for e in ("pe", "act", "dve", "pool"):
            self._new_csem(e)
        self.dq = {}
        for q in ("sp", "pool", "act"):
            self.dq[q] = [[self._sem(f"d_{q}{i}"), 0] for i in range(ndma)]
        self.dqi = {q: 0 for q in self.dq}
        self.waited = {}
        self.phase_stack = None
        self.ninst = 0

    def _sem(self, name):
        self.nsem += 1
        return self.stack.enter_context(self.nc.semaphore(name))

    def _new_csem(self, e):
        self.csem[e] = self._sem(f"c_{e}_{self.nsem}")
        self.ccnt[e] = 0

    def sb(self, name, shape, dtype, persist=False):
        st = self.stack if (persist or self.phase_stack is None) else self.phase_stack
        self.nsem += 0
        self.uid = getattr(self, "uid", 0) + 1
        t = st.enter_context(self.nc.sbuf_tensor(f"{name}_{self.uid}", list(shape), dtype))
        return Buf(t, name)

    def ps(self, name, shape, dtype, persist=False):
        st = self.stack if (persist or self.phase_stack is None) else self.phase_stack
        self.uid = getattr(self, "uid", 0) + 1
        t = st.enter_context(self.nc.psum_tensor(f"{name}_{self.uid}", list(shape), dtype))
        return Buf(t, name)

    def view(self, ap, name="v"):
        return Buf(ap, name)

    def begin_sub(self):
        if not hasattr(self, "sub_stk"):
            self.sub_stk = []
        self.sub_stk.append(self.phase_stack)
        self.phase_stack = ExitStack()

    def end_sub(self):
        self.barrier()
        self.phase_stack.close()
        self.phase_stack = self.sub_stk.pop()

    def begin_phase(self):
        assert self.phase_stack is None
        self.phase_stack = ExitStack()

    def end_phase(self):
        self.barrier()
        self.phase_stack.close()
        self.phase_stack = None

    def _wait(self, F, tok):
        sem, val, eng = tok
        if eng == F == "pe":
            return
        if eng == F and not SAME_ENGINE_SYNC:
            return
        key = (F, id(sem))
        if self.waited.get(key, 0) >= val:
            return
        self.engs[F].wait_ge(sem, val)
        self.waited[key] = val
        self.ninst += 1

    def _deps(self, F, reads, writes):
        for b in reads:
            if b.lw is not None:
                self._wait(F, b.lw)
        for b in writes:
            if b.lw is not None:
                self._wait(F, b.lw)
            for t in b.rd:
                self._wait(F, t)

    def _commit(self, tok, reads, writes):
        for b in reads:
            if tok[2] == "dma":
                b.rd.append(tok)
            else:
                b.rd = [t for t in b.rd if t[2] != tok[2]]
                b.rd.append(tok)
        for b in writes:
            b.lw = tok
            b.rd = []

    def op(self, F, fn, reads=(), writes=()):
        self._deps(F, reads, writes)
        if self.ccnt[F] >= EPOCH:
            self._new_csem(F)
        inst = fn(self.engs[F])
        self.ccnt[F] += 1
        inst.then_inc(self.csem[F], 1)
        tok = (self.csem[F], self.ccnt[F], F)
        self._commit(tok, reads, writes)
        self.ninst += 1
        return tok

    def dma(self, Q, out, in_, reads=(), writes=(), **kw):
        if NO_SWDGE:
            Q = "sp"
        self._deps(Q, reads, writes)
        pool = self.dq[Q]
        i = self.dqi[Q]
        self.dqi[Q] = (i + 1) % len(pool)
        sem, val = pool[i]
        if val > 0:
            self._wait(Q, (sem, val, "dma"))
        inst = self.engs[Q].dma_start(out=out, in_=in_, **kw)
        inst.then_inc(sem, 16)
        pool[i][1] = val + 16
        tok = (sem, val + 16, "dma")
        self._commit(tok, reads, writes)
        self.ninst += 1
        return tok

    def barrier(self, engines=("pe", "act", "dve", "pool", "sp")):
        toks = []
        for e in ("pe", "act", "dve", "pool"):
            if self.ccnt[e] > 0:
                toks.append((self.csem[e], self.ccnt[e], "bar"))
        for q in self.dq:
            for sem, val in self.dq[q]:
                if val > 0:
                    toks.append((sem, val, "dma"))
        for F in engines:
            for t in toks:
                self._wait(F, t)

    def finish(self):
        self.barrier()
        self.stack.close()


D = 1024
LAT = 2048
CTX = 256
T = LAT + CTX
NTT = T // 128
DFF = 4096
ALPHA = (2.0 * 2) ** 0.25
LN_EPS = 1e-5


class Ring:
    def __init__(self, bufs):
        self.bufs = bufs
        self.i = 0

    def next(self):
        b = self.bufs[self.i]
        self.i = (self.i + 1) % len(self.bufs)
        return b


def mk_ring(S, name, shape, dtype, n=2, psum=False):
    f = S.ps if psum else S.sb
    return Ring([f(f"{name}{i}", shape, dtype) for i in range(n)])


class Ctx:
    pass


def prep_weight(S, dst, src, K, N, scale=None, tag="pw"):
    S.begin_phase()
    CB = min(N, 2048)
    rin = mk_ring(S, tag + "i", [128, CB], F32, 2)
    rout = mk_ring(S, tag + "o", [128, CB], BF16, 2)
    sc = S.sb(tag + "s", [128, CB], F32) if scale is not None else None
    for c0 in range(0, N, CB):
        cw = min(CB, N - c0)
        if scale is not None:
            S.dma("sp", sc.ap[:, :cw], scale[c0:c0 + cw].partition_broadcast(128), writes=[sc])
        for k0 in range(0, K, 128):
            kw = min(128, K - k0)
            a = rin.next()
            o = rout.next()
            S.dma("sp", a.ap[:kw, :cw], src[k0:k0 + kw, c0:c0 + cw], writes=[a])
            if scale is not None:
                S.op("dve", lambda e: e.tensor_tensor(out=o.ap[:kw, :cw], in0=a.ap[:kw, :cw], in1=sc.ap[:kw, :cw], op=ALU.mult),
                     reads=[a, sc], writes=[o])
            else:
                S.op("dve", lambda e: e.tensor_copy(out=o.ap[:kw, :cw], in_=a.ap[:kw, :cw]), reads=[a], writes=[o])
            S.dma("pool", dst[k0:k0 + kw, c0:c0 + cw], o.ap[:kw, :cw], reads=[o])
    S.end_phase()


def phase_mod(S, C):
    nc, R = C.nc, C.R
    S.begin_phase()
    cT = S.sb("cT", [128, 8, R], F32)
    S.dma("sp", cT.ap[:], C.cT[:, :, :], writes=[cT])
    sig = S.sb("sig", [128, 8, R], F32)
    scT = S.sb("scT", [128, 8, R], BF16)
    S.op("act", lambda e: e.activation(out=sig.ap[:], in_=cT.ap[:], func=AF.Sigmoid), reads=[cT], writes=[sig])
    S.op("dve", lambda e: e.tensor_tensor(out=scT.ap[:], in0=cT.ap[:], in1=sig.ap[:], op=ALU.mult), reads=[cT, sig], writes=[scT])
    scbc = S.sb("scbc", [128, 8, R, 128], BF16)
    for kc in range(8):
        for r in range(R):
            S.op("dve", lambda e: e.tensor_copy(out=scbc.ap[:, kc, r, :], in_=scT.ap[:, kc, r:r + 1].to_broadcast([128, 128])),
                 reads=[scT], writes=[scbc])
    mb = S.sb("mb", [128, 2, 48], F32)
    S.dma("sp", mb.ap[:], C.mod_bT[:, :, :], writes=[mb])
    wst = mk_ring(S, "mws", [128, 3072], F32, 2)
    wbf = S.sb("mwbf", [128, 8, 6144], BF16)
    pacc = mk_ring(S, "mps", [128, 512], F32, 2, psum=True)
    gt = mk_ring(S, "mgt", [128, 512], F32, 2)
    mbr = S.sb("mbr", [128, 2048], F32)
    for l in range(2):
        for kc in range(8):
            for hf in range(2):
                a = wst.next()
                S.dma("sp" if hf == 0 else "pool", a.ap[:], C.mod_w[l, kc * 128:(kc + 1) * 128, hf * 3072:(hf + 1) * 3072], writes=[a])
                S.op("dve" if hf == 0 else "act",
                     (lambda e: e.tensor_copy(out=wbf.ap[:, kc, hf * 3072:(hf + 1) * 3072], in_=a.ap[:])) if hf == 0 else
                     (lambda e: e.activation(out=wbf.ap[:, kc, hf * 3072:(hf + 1) * 3072], in_=a.ap[:], func=AF.Identity)),
                     reads=[a], writes=[wbf])
        for fc in range(48):
            p = pacc.next()
            for kc in range(8):
                S.op("pe", lambda e: e.matmul(p.ap[:, :R], wbf.ap[:, kc, fc * 128:(fc + 1) * 128], scT.ap[:, kc, :],
                                              start=(kc == 0), stop=(kc == 7)), reads=[wbf, scT], writes=[p])
            is_scale = (fc // 8) in (1, 4)
            S.op("dve", lambda e: e.tensor_scalar(out=C.modT.ap[:, l, fc, :], in0=p.ap[:, :R], scalar1=mb.ap[:, l, fc:fc + 1],
                                                  scalar2=(1.0 if is_scale else 0.0), op0=ALU.add, op1=ALU.add),
                 reads=[p, mb], writes=[C.modT])
        for gi, c0 in enumerate((2048, 5120)):
            S.dma("sp", mbr.ap[:, gi * 1024:(gi + 1) * 1024], C.mod_b[l, c0:c0 + 1024].partition_broadcast(128), writes=[mbr])
        for gi, c0 in enumerate((2048, 5120)):
            for r in range(R):
                for hf in range(2):
                    p = pacc.next()
                    for kc in range(8):
                        S.op("pe", lambda e: e.matmul(p.ap[:], scbc.ap[:, kc, r, :], wbf.ap[:, kc, c0 + hf * 512:c0 + (hf + 1) * 512],
                                                      start=(kc == 0), stop=(kc == 7)), reads=[scbc, wbf], writes=[p])
                    g = gt.next()
                    S.op("dve", lambda e: e.tensor_tensor(out=g.ap[:], in0=p.ap[:], in1=mbr.ap[:, gi * 1024 + hf * 512:gi * 1024 + (hf + 1) * 512], op=ALU.add),
                         reads=[p, mbr], writes=[g])
                    S.dma("pool", C.gbc[l, gi, r, :, hf * 512:(hf + 1) * 512], g.ap[:], reads=[g])
    S.end_phase()


def tile_src(C, stage, bi, i):
    if stage == 0:
        if i < 2:
            return C.ctx[bi, i * 128:(i + 1) * 128, :]
        return C.x[bi, (i - 2) * 128:(i - 1) * 128, :]
    return C.xs[stage - 1][bi, i * 128:(i + 1) * 128, :]


def rstd_op(S, mv, o, i, eps):
    S.op("dve", lambda e: e.tensor_scalar(out=mv.ap[:, o:o + 1], in0=mv.ap[:, i:i + 1], scalar1=eps, scalar2=None, op0=ALU.add), reads=[mv], writes=[mv])
    S.op("act", lambda e: e.activation(out=mv.ap[:, o:o + 1], in_=mv.ap[:, o:o + 1], func=AF.Sqrt), reads=[mv], writes=[mv])
    S.op("dve", lambda e: e.reciprocal(out=mv.ap[:, o:o + 1], in_=mv.ap[:, o:o + 1]), reads=[mv], writes=[mv])


class Epi:
    def __init__(self, S, C, tag):
        self.S, self.C = S, C
        self.gb = [S.sb(tag + "gb0", [128, 1024], F32), S.sb(tag + "gb1", [128, 1024], F32)]
        self.lg = S.sb(tag + "lg", [128, 1024], F32)
        self.lb = S.sb(tag + "lb", [128, 1024], F32)
        self.t1 = mk_ring(S, tag + "t1", [128, 1024], F32, 2)
        self.xo = mk_ring(S, tag + "xo", [128, 1024], F32, 2)
        self.st = mk_ring(S, tag + "st", [128, 2, 6], F32, 2)
        self.mv = mk_ring(S, tag + "mv", [128, 4], F32, 2)

    def load(self, l, sub, bi):
        S, C = self.S, self.C
        S.dma("sp", self.gb[0].ap[:], C.gbc[l, sub, bi, :, :], writes=[self.gb[0]])
        S.dma("sp", self.gb[1].ap[:], C.gbc[l, sub, C.R - 1, :, :], writes=[self.gb[1]])
        S.dma("sp", self.lg.ap[:], C.ln_g[l, sub, :].partition_broadcast(128), writes=[self.lg])
        S.dma("sp", self.lb.ap[:], C.ln_b[l, sub, :].partition_broadcast(128), writes=[self.lb])

    def run(self, y, xin, is_ctx, dst):
        S = self.S
        gb = self.gb[1 if is_ctx else 0]
        t1, xo, st, mv = self.t1.next(), self.xo.next(), self.st.next(), self.mv.next()
        S.op("dve", lambda e: e.tensor_tensor(out=t1.ap[:], in0=y.ap[:], in1=gb.ap[:], op=ALU.mult), reads=[y, gb], writes=[t1])
        S.op("dve", lambda e: e.scalar_tensor_tensor(out=t1.ap[:], in0=xin.ap[:], scalar=ALPHA, in1=t1.ap[:], op0=ALU.mult, op1=ALU.add),
             reads=[xin, t1], writes=[t1])
        for h in range(2):
            S.op("dve", lambda e: e.bn_stats(out=st.ap[:, h, :], in_=t1.ap[:, h * 512:(h + 1) * 512]), reads=[t1], writes=[st])
        S.op("dve", lambda e: e.bn_aggr(out=mv.ap[:, 0:2], in_=st.ap[:]), reads=[st], writes=[mv])
        rstd_op(S, mv, 2, 1, LN_EPS)
        S.op("dve", lambda e: e.tensor_scalar(out=mv.ap[:, 3:4], in0=mv.ap[:, 0:1], scalar1=mv.ap[:, 2:3], scalar2=-1.0, op0=ALU.mult, op1=ALU.mult),
             reads=[mv], writes=[mv])
        S.op("act", lambda e: e.activation(out=xo.ap[:], in_=t1.ap[:], func=AF.Identity, scale=mv.ap[:, 2:3], bias=mv.ap[:, 3:4]),
             reads=[t1, mv], writes=[xo])
        S.op("pool", lambda e: e.tensor_tensor(out=xo.ap[:], in0=xo.ap[:], in1=self.lg.ap[:], op=ALU.mult), reads=[xo, self.lg], writes=[xo])
        S.op("pool", lambda e: e.tensor_tensor(out=xo.ap[:], in0=xo.ap[:], in1=self.lb.ap[:], op=ALU.add), reads=[xo, self.lb], writes=[xo])
        S.dma("pool", dst, xo.ap[:], reads=[xo])


def transpose_mod(S, C, xin, hT, col0, l, fc0, r, ptr, ident):
    for g in range(2):
        p = ptr.next()
        for j in range(4):
            kc = g * 4 + j
            S.op("pe", lambda e: e.transpose(out=p.ap[:, j * 128:(j + 1) * 128], in_=xin.ap[:, kc * 128:(kc + 1) * 128], identity=ident.ap[:]),
                 reads=[xin, ident], writes=[p])
        for j in range(4):
            kc = g * 4 + j
            S.op("act", lambda e: e.activation(out=hT.ap[:, kc, col0:col0 + 128], in_=p.ap[:, j * 128:(j + 1) * 128], func=AF.Identity,
                                               scale=C.modT.ap[:, l, fc0 + 8 + kc, r:r + 1], bias=C.modT.ap[:, l, fc0 + kc, r:r + 1]),
                 reads=[p, C.modT], writes=[hT])


def phase_mlp(S, C, l, bi, src_stage, dst_stage, last):
    S.begin_phase()
    ident = C.ident
    w1 = S.sb("w1", [128, 8, DFF], BF16)
    w2 = S.sb("w2", [128, 32, D], BF16)
    for kc in range(8):
        S.dma("sp" if kc % 2 == 0 else "pool", w1.ap[:, kc, :], C.w1b[l][kc * 128:(kc + 1) * 128, :], writes=[w1])
    for ko in range(32):
        S.dma("sp" if ko % 2 == 0 else "pool", w2.ap[:, ko, :], C.w2b[l][ko * 128:(ko + 1) * 128, :], writes=[w2])
    epi = Epi(S, C, "m")
    epi.load(l, 1, bi)
    xin = mk_ring(S, "mxin", [128, 1024], F32, 4)
    hT = mk_ring(S, "mhT", [128, 8, 256], BF16, 2)
    aT = S.sb("maT", [128, 32, 256], BF16)
    rl = mk_ring(S, "mrl", [128, 256], BF16, 2)
    ptr = mk_ring(S, "mptr", [128, 512], F32, 2, psum=True)
    pup = mk_ring(S, "mpup", [128, 256], F32, 2, psum=True)
    pdn = mk_ring(S, "mpdn", [128, 1024], F32, 2, psum=True)
    t0 = 1 if last else 0
    for tt in range(t0, 9):
        is_ctx = tt == 0
        r = C.R - 1 if is_ctx else bi
        xs_ = []
        h = hT.next()
        for s in range(2):
            xi = xin.next()
            S.dma("sp", xi.ap[:], tile_src(C, src_stage, bi, tt * 2 + s), writes=[xi])
            transpose_mod(S, C, xi, h, s * 128, l, 24, r, ptr, ident)
            xs_.append(xi)
        for fo in range(32):
            p = pup.next()
            for kc in range(8):
                S.op("pe", lambda e: e.matmul(p.ap[:], w1.ap[:, kc, fo * 128:(fo + 1) * 128], h.ap[:, kc, :], start=(kc == 0), stop=(kc == 7)),
                     reads=[w1, h], writes=[p])
            rr = rl.next()
            S.op("act", lambda e: e.activation(out=rr.ap[:], in_=p.ap[:], func=AF.Relu), reads=[p], writes=[rr])
            S.op("pool", lambda e: e.tensor_tensor(out=aT.ap[:, fo, :], in0=rr.ap[:], in1=rr.ap[:], op=ALU.mult), reads=[rr], writes=[aT])
        for s in range(2):
            p = pdn.next()
            for hf in range(2):
                for ko in range(32):
                    S.op("pe", lambda e: e.matmul(p.ap[:, hf * 512:(hf + 1) * 512], aT.ap[:, ko, s * 128:(s + 1) * 128], w2.ap[:, ko, hf * 512:(hf + 1) * 512],
                                                  start=(ko == 0), stop=(ko == 31)), reads=[aT, w2], writes=[p])
            i = tt * 2 + s
            if last:
                dst = C.out[bi, (i - 2) * 128:(i - 1) * 128, :]
            else:
                dst = C.xs[dst_stage - 1][bi, i * 128:(i + 1) * 128, :]
            epi.run(p, xs_[s], is_ctx, dst)
    S.end_phase()


def dram(nc, name, shape, dtype, kind=None):
    if kind is None:
        return nc.dram_tensor(name, list(shape), dtype).ap()
    return nc.dram_tensor(name, list(shape), dtype, kind=kind).ap()


def declare_common(nc, NB, dbg=None):
    C = Ctx()
    C.nc, C.NB, C.R = nc, NB, NB + 1
    R = C.R
    I = lambda n, s: dram(nc, n, s, F32, "ExternalInput")
    C.x = I("x", [NB, LAT, D])
    C.ctx = I("ctx", [NB, CTX, D])
    C.cT = I("cT", [128, 8, R])
    C.mod_w = I("mod_w", [2, D, 6 * D])
    C.mod_b = I("mod_b", [2, 6 * D])
    C.mod_bT = I("mod_bT", [128, 2, 48])
    C.ln_g = I("ln_g", [2, 2, D])
    C.ln_b = I("ln_b", [2, 2, D])
    C.mlp_w1 = I("mlp_w1", [2, D, DFF])
    C.mlp_w2 = I("mlp_w2", [2, DFF, D])
    C.ident_d = I("ident", [128, 128])
    C.out = dram(nc, "out", [NB, LAT, D], F32, "ExternalOutput")
    C.gbc = dram(nc, "gbc", [2, 2, R, 128, D], F32)
    C.w1b = [dram(nc, f"w1b{l}", [D, DFF], BF16) for l in range(2)]
    C.w2b = [dram(nc, f"w2b{l}", [DFF, D], BF16) for l in range(2)]
    nst = 3
    C.xs = [dram(nc, f"xs{i}", [NB, T, D], F32, "ExternalOutput" if (dbg and f"xs{i}" in dbg) else None) for i in range(nst)]
    return C


def common_setup(S, C):
    C.modT = S.sb("modT", [128, 2, 48, C.R], F32, persist=True)
    C.ident = S.sb("identsb", [128, 128], F32, persist=True)
    S.dma("sp", C.ident.ap[:], C.ident_d[:, :], writes=[C.ident])


def host_common(inputs, core, NB):
    b0 = core * NB
    f = lambda a: np.ascontiguousarray(np.asarray(a, dtype=np.float32))
    cs = np.concatenate([np.asarray(inputs["c"])[b0:b0 + NB], np.asarray(inputs["c_ctx"])[None, :]], 0)
    m = {
        "x": f(np.asarray(inputs["x"])[b0:b0 + NB]),
        "ctx": f(np.asarray(inputs["ctx"])[b0:b0 + NB]),
        "cT": f(cs.reshape(NB + 1, 8, 128).transpose(2, 1, 0)),
        "mod_w": f(inputs["mod_w"]),
        "mod_b": f(inputs["mod_b"]),
        "mod_bT": f(np.asarray(inputs["mod_b"]).reshape(2, 48, 128).transpose(2, 0, 1)),
        "ln_g": f(inputs["ln_g"]),
        "ln_b": f(inputs["ln_b"]),
        "mlp_w1": f(inputs["mlp_w1"]),
        "mlp_w2": f(inputs["mlp_w2"]),
        "ident": np.eye(128, dtype=np.float32),
    }
    return m


HYW = 512
PI = float(np.pi)


def colof(i):
    return 1 + 128 * i if i < 2 else 259 + 128 * (i - 2)


def declare_l0(nc, C, dbg=None):
    NB = C.NB
    I = lambda n, s, dt=F32: dram(nc, n, s, dt, "ExternalInput")
    Sx = lambda n, s, dt=BF16: dram(nc, n, s, dt, "ExternalOutput" if (dbg and n in dbg) else None)
    C.e_w_hy = I("e_w_hy", [D, 1536])
    C.e_w_qkv = I("e_w_qkv", [D, 768 + 640])
    C.e_w_out = I("e_w_out", [D, D])
    C.hy_conv = I("hy_conv", [3, 1536])
    C.hy_w1 = I("hy_w1", [33, 64])
    C.hy_w2 = I("hy_w2", [64, 64])
    C.hy_w3 = I("hy_w3", [64, 1024])
    C.hy_vec = I("hy_vec", [64, 3])
    C.hy_decay = I("hy_decay", [1024])
    C.hy_bias = I("hy_bias", [512])
    C.attn_sink = I("attn_sink", [8])
    C.peT = [I("peT_l", [33, LAT]), I("peT_c", [33, CTX])]
    C.negtn = [I("negtn_l", [128, LAT // 128]), I("negtn_c", [128, CTX // 128])]
    C.fwd = [I("fwd_l", [16, 2, 128, 16, 128], BF16), I("fwd_c", [2, 2, 128, 2, 128], BF16)]
    C.inv = [I("inv_l", [4, 128, 16, 2, 512], BF16), I("inv_c", [1, 128, 2, 2, 256], BF16)]
    C.rope = I("rope", [64, 2, LAT])
    C.amask = I("amask", [128, 384])
    C.ident_b = I("ident_b", [128, 128], BF16)
    C.whb = [Sx(f"whb{j}", [D, 1536]) for j in range(3)]
    C.wqb = Sx("wqb", [D, 1408])
    C.wob0 = Sx("wob0", [D, D])
    C.kspec = [Sx("kspec_l", [LAT, 2, 512], F32), Sx("kspec_c", [CTX, 2, 512], F32)]
    C.filt = [Sx("filt_l", [LAT, 2, 512]), Sx("filt_c", [CTX, 2, 512])]
    C.u = Sx("u_s", [NB, T, 512])
    C.x0T = Sx("x0T_s", [NB, 512, T])
    C.qT = Sx("qT_s", [NB, 8, 64, T])
    C.kT = Sx("kT_s", [NB, 2, 64, T])
    C.v = Sx("v_s", [NB, T, 128])
    C.yaT = Sx("yaT_s", [NB, 512, T])
    C.ybT = Sx("ybT_s", [NB, 8, 64, T])


def host_l0(inputs, m):
    f = lambda a: np.ascontiguousarray(np.asarray(a, dtype=np.float32))
    import ml_dtypes
    bf = lambda a: np.ascontiguousarray(np.asarray(a, dtype=np.float32).astype(ml_dtypes.bfloat16))
    w = np.asarray(inputs["e_w_in"])[0]
    d = np.arange(64)
    partner = np.where((d % 32) < 16, d + 16, d - 16)
    qcols = 1536 + (np.arange(8)[:, None] * 64 + partner[None, :]).reshape(-1)
    kcols = 2048 + (np.arange(2)[:, None] * 64 + partner[None, :]).reshape(-1)
    m["e_w_hy"] = f(w[:, :1536])
    m["e_w_qkv"] = f(np.concatenate([w[:, 1536:2304], w[:, qcols], w[:, kcols]], 1))
    m["e_w_out"] = f(np.asarray(inputs["e_w_out"])[0])
    m["hy_conv"] = f(np.asarray(inputs["hy_conv"])[0])
    m["hy_w1"] = f(np.asarray(inputs["hy_ffn_w1"])[0])
    m["hy_w2"] = f(np.asarray(inputs["hy_ffn_w2"])[0])
    m["hy_w3"] = f(np.asarray(inputs["hy_ffn_w3"])[0])
    m["hy_vec"] = f(np.stack([np.asarray(inputs["hy_ffn_b1"])[0], np.asarray(inputs["hy_ffn_b2"])[0], np.asarray(inputs["hy_sin_freq"])[0]], 1))
    m["hy_decay"] = f(np.asarray(inputs["hy_decay"])[0])
    m["hy_bias"] = f(np.asarray(inputs["hy_bias"])[0])
    m["attn_sink"] = f(np.asarray(inputs["attn_sink"])[0])
    for tag, Lf in (("l", LAT), ("c", CTX)):
        t = np.arange(Lf, dtype=np.float32)
        t_norm = t / np.float32(max(Lf - 1, 1))
        bands = np.linspace(1e-4, 15, 16, dtype=np.float32)
        ang = (2.0 * np.pi * t[:, None] * bands[None, :] / Lf).astype(np.float32)
        pe = np.concatenate([t_norm[:, None], np.cos(ang), -np.sin(ang)], -1).astype(np.float32)
        m["peT_" + tag] = f(pe.T)
        m["negtn_" + tag] = f((-t_norm).reshape(Lf // 128, 128).T)
        N = 2 * Lf
        nt = Lf // 128
        tt = np.arange(Lf, dtype=np.float64)
        ff = np.arange(Lf, dtype=np.float64) + 0.5
        th = 2.0 * np.pi * np.outer(tt, ff) / N
        Cm, Sm = np.cos(th), np.sin(th)
        fw = np.stack([Cm, Sm], 0).reshape(2, nt, 128, nt, 128)
        m["fwd_" + tag] = bf(fw.transpose(3, 0, 2, 1, 4))
        tw = min(512, Lf)
        iv = np.stack([Cm.T, -Sm.T], 0).reshape(2, nt, 128, Lf // tw, tw)
        m["inv_" + tag] = bf(iv.transpose(3, 2, 1, 0, 4))
    pos = np.arange(LAT)
    inv_freq = (10000.0 ** (-np.arange(16, dtype=np.float32) / 16)).astype(np.float32)
    P = np.where(d[:, None] < 32, (pos // 64)[None, :], (pos % 64)[None, :]).astype(np.float32)
    ang = (P * inv_freq[d % 16][:, None]).astype(np.float32)
    sgn = np.where((d % 32) < 16, -1.0, 1.0)[:, None]
    m["rope"] = f(np.stack([np.cos(ang), sgn * np.sin(ang)], 1))
    qi = np.arange(128)[:, None]
    kj = np.arange(384)[None, :] - 128
    m["amask"] = f(np.where(np.abs(qi - kj) <= 128, 0.0, -30000.0))
    m["ident_b"] = bf(np.eye(128))
    return m


def l0_setup(S, C):
    for j in range(3):
        prep_weight(S, C.whb[j], C.e_w_hy, D, 1536, scale=C.hy_conv[j, :], tag=f"ph{j}")
    prep_weight(S, C.wqb, C.e_w_qkv, D, 1408, tag="pq")
    prep_weight(S, C.wob0, C.e_w_out, D, D, tag="po")
    for si, Lf in enumerate((LAT, CTX)):
        hyena_filter(S, C, si, Lf)
        hyena_fwd(S, C, si, Lf, C.filt[si], None, C.kspec[si], is_filter=True)


def hyena_filter(S, C, si, Lf):
    S.begin_phase()
    w1 = S.sb("hw1", [33, 64], F32)
    w2 = S.sb("hw2", [64, 64], F32)
    w3 = S.sb("hw3", [64, 1024], F32)
    vec = S.sb("hvec", [64, 3], F32)
    peT = S.sb("hpe", [33, Lf], F32)
    ntn = S.sb("hntn", [128, Lf // 128], F32)
    dec = S.sb("hdec", [128, 1024], F32)
    for dst, src in ((w1, C.hy_w1), (w2, C.hy_w2), (w3, C.hy_w3), (vec, C.hy_vec), (peT, C.peT[si]), (ntn, C.negtn[si])):
        S.dma("sp", dst.ap[:], src, writes=[dst])
    S.dma("sp", dec.ap[:], C.hy_decay[:].partition_broadcast(128), writes=[dec])
    S.op("dve", lambda e: e.scalar_tensor_tensor(out=dec.ap[:], in0=dec.ap[:], scalar=-1.0, in1=dec.ap[:], op0=ALU.mult, op1=ALU.max), reads=[dec], writes=[dec])
    h1 = S.sb("hh1", [64, Lf], F32)
    h2 = S.sb("hh2", [64, Lf], F32)
    tmp = S.sb("htmp", [64, 512], F32)
    S.sin_ki = S.sb("hki", [64, 512], mybir.dt.int32)
    S.sin_kf = S.sb("hkf", [64, 512], F32)
    pp = mk_ring(S, "hpp", [128, 512], F32, 2, psum=True)
    W = min(512, Lf)
    for c0 in range(0, Lf, W):
        p = pp.next()
        S.op("pe", lambda e: e.matmul(p.ap[:64, :W], w1.ap[:], peT.ap[:, c0:c0 + W], start=True, stop=True), reads=[w1, peT], writes=[p])
        S.op("dve", lambda e: e.tensor_scalar(out=tmp.ap[:, :W], in0=p.ap[:64, :W], scalar1=vec.ap[:, 0:1], scalar2=vec.ap[:, 2:3], op0=ALU.add, op1=ALU.mult),
             reads=[p, vec], writes=[tmp])
        sin_tail(S, h1, c0, W, tmp)
    for c0 in range(0, Lf, W):
        p = pp.next()
        S.op("pe", lambda e: e.matmul(p.ap[:64, :W], w2.ap[:], h1.ap[:, c0:c0 + W], start=True, stop=True), reads=[w2, h1], writes=[p])
        S.op("dve", lambda e: e.tensor_scalar(out=tmp.ap[:, :W], in0=p.ap[:64, :W], scalar1=vec.ap[:, 1:2], scalar2=vec.ap[:, 2:3], op0=ALU.add, op1=ALU.mult),
             reads=[p, vec], writes=[tmp])
        sin_tail(S, h2, c0, W, tmp)
    ex = mk_ring(S, "hex", [128, 1024], F32, 2)
    fo = mk_ring(S, "hfo", [128, 2, 512], BF16, 2)
    for tc in range(Lf // 128):
        e_ = ex.next()
        S.op("act", lambda e: e.activation(out=e_.ap[:], in_=dec.ap[:], func=AF.Exp, scale=ntn.ap[:, tc:tc + 1]), reads=[dec, ntn], writes=[e_])
        for hf in range(2):
            p = pp.next()
            S.op("pe", lambda e: e.matmul(p.ap[:], h2.ap[:, tc * 128:(tc + 1) * 128], w3.ap[:, hf * 512:(hf + 1) * 512], start=True, stop=True),
                 reads=[h2, w3], writes=[p])
            S.op("dve", lambda e: e.tensor_tensor(out=e_.ap[:, hf * 512:(hf + 1) * 512], in0=p.ap[:], in1=e_.ap[:, hf * 512:(hf + 1) * 512], op=ALU.mult),
                 reads=[p, e_], writes=[e_])
        if tc == 0:
            S.op("dve", lambda e: e.memset(e_.ap[0:1, 512:1024], 0.0), reads=[], writes=[e_])
        o = fo.next()
        S.op("dve", lambda e: e.tensor_tensor(out=o.ap[:, 0, :], in0=e_.ap[:, 512:1024], in1=e_.ap[:, 0:512], op=ALU.add), reads=[e_], writes=[o])
        S.op("pool", lambda e: e.tensor_tensor(out=o.ap[:, 1, :], in0=e_.ap[:, 512:1024], in1=e_.ap[:, 0:512], op=ALU.subtract), reads=[e_], writes=[o])
        S.dma("sp", C.filt[si][tc * 128:(tc + 1) * 128, :, :], o.ap[:], reads=[o])
    S.end_phase()


def sin_tail(S, dst, c0, W, tmp):
    ki, kf = S.sin_ki, S.sin_kf
    S.op("dve", lambda e: e.tensor_scalar(out=tmp.ap[:, :W], in0=tmp.ap[:, :W], scalar1=1.0 / (2.0 * PI), scalar2=16.5, op0=ALU.mult, op1=ALU.add),
         reads=[tmp], writes=[tmp])
    S.op("dve", lambda e: e.tensor_copy(out=ki.ap[:, :W], in_=tmp.ap[:, :W]), reads=[tmp], writes=[ki])
    S.op("dve", lambda e: e.tensor_copy(out=kf.ap[:, :W], in_=ki.ap[:, :W]), reads=[ki], writes=[kf])
    S.op("dve", lambda e: e.scalar_tensor_tensor(out=tmp.ap[:, :W], in0=tmp.ap[:, :W], scalar=-0.5, in1=kf.ap[:, :W], op0=ALU.add, op1=ALU.subtract),
         reads=[tmp, kf], writes=[tmp])
    S.op("dve", lambda e: e.scalar_tensor_tensor(out=tmp.ap[:, :W], in0=tmp.ap[:, :W], scalar=-0.5, in1=tmp.ap[:, :W], op0=ALU.is_lt, op1=ALU.add),
         reads=[tmp], writes=[tmp])
    S.op("act", lambda e: e.activation(out=dst.ap[:, c0:c0 + W], in_=tmp.ap[:, :W], func=AF.Sin, scale=2.0 * PI * 0.999999), reads=[tmp], writes=[dst])


def hyena_fwd(S, C, si, Lf, src, bi, dst, is_filter):
    nt = Lf // 128
    N = 2 * Lf
    if is_filter:
        S.begin_phase()
    a_in = S.sb("fa", [128, nt, 2 if is_filter else 1, 512], BF16)
    if is_filter:
        S.dma("sp", a_in.ap[:], src.rearrange("(tc p) s c -> p tc s c", p=128), writes=[a_in])
        bb = S.sb("fbias", [128, 512], F32)
        S.dma("sp", bb.ap[:], C.hy_bias[:].partition_broadcast(128), writes=[bb])
        S.op("dve", lambda e: e.tensor_scalar(out=bb.ap[:], in0=bb.ap[:], scalar1=2.0 / N, scalar2=None, op0=ALU.mult), reads=[bb], writes=[bb])
    else:
        S.dma("sp", a_in.ap[:, :, 0, :], src.rearrange("(tc p) c -> p tc c", p=128), writes=[a_in])
    fm = mk_ring(S, "ffm", [128, 2, nt, 128], BF16, 2)
    pr = mk_ring(S, "fpr", [128, 512], F32, 2, psum=True)
    pi_ = mk_ring(S, "fpi", [128, 512], F32, 2, psum=True)
    if is_filter:
        ko = mk_ring(S, "fko", [128, 2, 512], F32, 2)
    else:
        ks = mk_ring(S, "fks", [128, 2, 512], F32, 2)
        tt = mk_ring(S, "ftt", [128, 4, 512], F32, 2)
    for fcn in range(nt):
        m = fm.next()
        for cs in range(2):
            S.dma("sp" if cs == 0 else "pool", m.ap[:, cs, :, :], C.fwd[si][fcn, cs, :, :, :], writes=[m])
        a, b = pr.next(), pi_.next()
        for cs, p in ((0, a), (1, b)):
            for tc in range(nt):
                S.op("pe", lambda e: e.matmul(p.ap[:], m.ap[:, cs, tc, :], a_in.ap[:, tc, cs if is_filter else 0, :], start=(tc == 0), stop=(tc == nt - 1)),
                     reads=[m, a_in], writes=[p])
        if is_filter:
            o = ko.next()
            S.op("dve", lambda e: e.scalar_tensor_tensor(out=o.ap[:, 0, :], in0=a.ap[:], scalar=2.0 / N, in1=bb.ap[:], op0=ALU.mult, op1=ALU.add),
                 reads=[a, bb], writes=[o])
            S.op("act", lambda e: e.activation(out=o.ap[:, 1, :], in_=b.ap[:], func=AF.Identity, scale=2.0 / N), reads=[b], writes=[o])
            S.dma("pool", dst[fcn * 128:(fcn + 1) * 128, :, :], o.ap[:], reads=[o])
        else:
            k = ks.next()
            S.dma("sp", k.ap[:], C.kspec[si][fcn * 128:(fcn + 1) * 128, :, :], writes=[k])
            t = tt.next()
            S.op("dve", lambda e: e.tensor_tensor(out=t.ap[:, 0, :], in0=a.ap[:], in1=k.ap[:, 0, :], op=ALU.mult), reads=[a, k], writes=[t])
            S.op("dve", lambda e: e.tensor_tensor(out=t.ap[:, 1, :], in0=b.ap[:], in1=k.ap[:, 1, :], op=ALU.mult), reads=[b, k], writes=[t])
            S.op("dve", lambda e: e.tensor_tensor(out=t.ap[:, 2, :], in0=a.ap[:], in1=k.ap[:, 1, :], op=ALU.mult), reads=[a, k], writes=[t])
            S.op("dve", lambda e: e.tensor_tensor(out=t.ap[:, 3, :], in0=b.ap[:], in1=k.ap[:, 0, :], op=ALU.mult), reads=[b, k], writes=[t])
            S.op("pool", lambda e: e.tensor_tensor(out=dst.ap[:, fcn, 0, :], in0=t.ap[:, 0, :], in1=t.ap[:, 1, :], op=ALU.add), reads=[t], writes=[dst])
            S.op("pool", lambda e: e.tensor_tensor(out=dst.ap[:, fcn, 1, :], in0=t.ap[:, 2, :], in1=t.ap[:, 3, :], op=ALU.subtract), reads=[t], writes=[dst])
    if is_filter:
        S.end_phase()


def l0_inproj(S, C, bi, src_stage):
    S.begin_phase()
    ident = C.ident
    hT = S.sb("ihT", [128, 8, T + 4], BF16)
    S.op("pool", lambda e: e.memset(hT.ap[:], 0.0), writes=[hT])
    xin = mk_ring(S, "ixin", [128, 1024], F32, 2)
    ptr = mk_ring(S, "iptr", [128, 512], F32, 2, psum=True)
    for i in range(NTT):
        xi = xin.next()
        S.dma("sp", xi.ap[:], tile_src(C, src_stage, bi, i), writes=[xi])
        transpose_mod(S, C, xi, hT, colof(i), 0, 0, (C.R - 1 if i < 2 else bi), ptr, ident)
    S.begin_sub()
    wt = S.sb("iwt", [128, 3, 8, 1024], BF16)
    wv = S.sb("iwv", [128, 8, 128], BF16)
    for j in range(3):
        for kc in range(8):
            S.dma("sp" if kc % 2 else "pool", wt.ap[:, j, kc, :], C.whb[j][kc * 128:(kc + 1) * 128, 512:1536], writes=[wt])
    for kc in range(8):
        S.dma("sp", wv.ap[:, kc, :], C.wqb[kc * 128:(kc + 1) * 128, 640:768], writes=[wv])
    pa = mk_ring(S, "ipa", [128, 512], F32, 2, psum=True)
    pb = mk_ring(S, "ipb", [128, 512], F32, 2, psum=True)
    x1s = mk_ring(S, "ix1", [128, 512], F32, 2)
    ut = mk_ring(S, "iut", [128, 512], BF16, 2)
    vt = mk_ring(S, "ivt", [128, 128], BF16, 2)
    for i in range(NTT):
        c0 = colof(i)
        a, b = pa.next(), pb.next()
        for half, p in ((0, a), (1, b)):
            n = 0
            for j in range(3):
                for kc in range(8):
                    S.op("pe", lambda e: e.matmul(p.ap[:], hT.ap[:, kc, c0 + j - 1:c0 + j - 1 + 128], wt.ap[:, j, kc, half * 512:(half + 1) * 512],
                                                  start=(n == 0), stop=(n == 23)), reads=[hT, wt], writes=[p])
                    n += 1
        x1 = x1s.next()
        S.op("act", lambda e: e.activation(out=x1.ap[:], in_=a.ap[:], func=AF.Identity), reads=[a], writes=[x1])
        u = ut.next()
        S.op("dve", lambda e: e.tensor_tensor(out=u.ap[:], in0=b.ap[:], in1=x1.ap[:], op=ALU.mult), reads=[b, x1], writes=[u])
        S.dma("pool", C.u[bi, i * 128:(i + 1) * 128, :], u.ap[:], reads=[u])
        p = pa.next()
        for kc in range(8):
            S.op("pe", lambda e: e.matmul(p.ap[:, :128], hT.ap[:, kc, c0:c0 + 128], wv.ap[:, kc, :], start=(kc == 0), stop=(kc == 7)),
                 reads=[hT, wv], writes=[p])
        v = vt.next()
        S.op("act", lambda e: e.activation(out=v.ap[:], in_=p.ap[:, :128], func=AF.Identity), reads=[p], writes=[v])
        S.dma("pool", C.v[bi, i * 128:(i + 1) * 128, :], v.ap[:], reads=[v])
    S.end_sub()
    S.begin_sub()
    w0 = S.sb("iw0", [128, 3, 8, 512], BF16)
    wq = S.sb("iwq", [128, 8, 1280], BF16)
    rope = S.sb("irope", [64, 2, LAT], F32)
    S.dma("sp", rope.ap[:], C.rope[:, :, :], writes=[rope])
    for j in range(3):
        for kc in range(8):
            S.dma("sp" if kc % 2 else "pool", w0.ap[:, j, kc, :], C.whb[j][kc * 128:(kc + 1) * 128, 0:512], writes=[w0])
    for kc in range(8):
        S.dma("sp", wq.ap[:, kc, 0:640], C.wqb[kc * 128:(kc + 1) * 128, 0:640], writes=[wq])
        S.dma("pool", wq.ap[:, kc, 640:1280], C.wqb[kc * 128:(kc + 1) * 128, 768:1408], writes=[wq])
    pa = mk_ring(S, "jpa", [128, 512], F32, 3, psum=True)
    ot = mk_ring(S, "jot", [128, 512], BF16, 3)
    t1 = mk_ring(S, "jt1", [64, 512], F32, 2)
    t2 = mk_ring(S, "jt2", [64, 512], F32, 2)
    tiles = [(0, 256)] + [(256 + 512 * k, 512) for k in range(4)]
    for (tok0, w) in tiles:
        c0 = colof(tok0 // 128)
        for cc in range(4):
            p = pa.next()
            n = 0
            for j in range(3):
                for kc in range(8):
                    S.op("pe", lambda e: e.matmul(p.ap[:, :w], w0.ap[:, j, kc, cc * 128:(cc + 1) * 128], hT.ap[:, kc, c0 + j - 1:c0 + j - 1 + w],
                                                  start=(n == 0), stop=(n == 23)), reads=[w0, hT], writes=[p])
                    n += 1
            o = ot.next()
            S.op("act", lambda e: e.activation(out=o.ap[:, :w], in_=p.ap[:, :w], func=AF.Identity), reads=[p], writes=[o])
            S.dma("pool", C.x0T[bi, cc * 128:(cc + 1) * 128, tok0:tok0 + w], o.ap[:, :w], reads=[o])
        for hh in range(10):
            p = pa.next()
            for kc in range(8):
                S.op("pe", lambda e: e.matmul(p.ap[:64, :w], wq.ap[:, kc, hh * 64:(hh + 1) * 64], hT.ap[:, kc, c0:c0 + w], start=(kc == 0), stop=(kc == 7)),
                     reads=[wq, hT], writes=[p])
            o = ot.next()
            dst = C.qT[bi, hh, :, tok0:tok0 + w] if hh < 8 else C.kT[bi, hh - 8, :, tok0:tok0 + w]
            if tok0 == 0:
                S.op("act", lambda e: e.activation(out=o.ap[:64, :w], in_=p.ap[:64, :w], func=AF.Identity), reads=[p], writes=[o])
            else:
                p2 = pa.next()
                for kc in range(8):
                    S.op("pe", lambda e: e.matmul(p2.ap[:64, :w], wq.ap[:, kc, 640 + hh * 64:640 + (hh + 1) * 64], hT.ap[:, kc, c0:c0 + w],
                                                  start=(kc == 0), stop=(kc == 7)), reads=[wq, hT], writes=[p2])
                l0_ = tok0 - 256
                a, b = t1.next(), t2.next()
                S.op("dve", lambda e: e.tensor_tensor(out=a.ap[:, :w], in0=p.ap[:64, :w], in1=rope.ap[:, 0, l0_:l0_ + w], op=ALU.mult), reads=[p, rope], writes=[a])
                S.op("dve", lambda e: e.tensor_tensor(out=b.ap[:, :w], in0=p2.ap[:64, :w], in1=rope.ap[:, 1, l0_:l0_ + w], op=ALU.mult), reads=[p2, rope], writes=[b])
                S.op("pool", lambda e: e.tensor_tensor(out=o.ap[:64, :w], in0=a.ap[:, :w], in1=b.ap[:, :w], op=ALU.add), reads=[a, b], writes=[o])
            S.dma("sp", dst, o.ap[:64, :w], reads=[o])
    S.end_sub()
    S.end_phase()


def l0_hyena(S, C, bi):
    for si, (Lf, tok0) in enumerate(((LAT, 256), (CTX, 0))):
        S.begin_phase()
        nt = Lf // 128
        Y = S.sb("hyY", [128, nt, 2, 512], BF16)
        S.begin_sub()
        hyena_fwd(S, C, si, Lf, C.u[bi, tok0:tok0 + Lf, :], bi, Y, is_filter=False)
        S.end_sub()
        tw = min(512, Lf)
        iv = mk_ring(S, "hyiv", [128, nt, 2, tw], BF16, 2 if Lf == CTX else 1)
        x0 = mk_ring(S, "hyx0", [128, tw], BF16, 2)
        ya = mk_ring(S, "hyya", [128, tw], BF16, 2)
        pp = mk_ring(S, "hypp", [128, 512], F32, 2, psum=True)
        for tt in range(Lf // tw):
            m = iv.next()
            for fc in range(nt):
                S.dma("sp" if fc % 2 else "pool", m.ap[:, fc, :, :], C.inv[si][tt, :, fc, :, :], writes=[m])
            for cc in range(4):
                p = pp.next()
                n = 0
                for fc in range(nt):
                    for cs in range(2):
                        S.op("pe", lambda e: e.matmul(p.ap[:, :tw], Y.ap[:, fc, cs, cc * 128:(cc + 1) * 128], m.ap[:, fc, cs, :],
                                                      start=(n == 0), stop=(n == 2 * nt - 1)), reads=[Y, m], writes=[p])
                        n += 1
                xz = x0.next()
                t0_ = tok0 + tt * tw
                S.dma("sp", xz.ap[:], C.x0T[bi, cc * 128:(cc + 1) * 128, t0_:t0_ + tw], writes=[xz])
                o = ya.next()
                S.op("dve", lambda e: e.tensor_tensor(out=o.ap[:], in0=p.ap[:, :tw], in1=xz.ap[:], op=ALU.mult), reads=[p, xz], writes=[o])
                S.dma("pool", C.yaT[bi, cc * 128:(cc + 1) * 128, t0_:t0_ + tw], o.ap[:], reads=[o])
        S.end_phase()


def l0_attn(S, C, bi):
    S.begin_phase()
    qT = S.sb("aqT", [64, 8, T], BF16)
    kT = S.sb("akT", [64, 2, T], BF16)
    v = S.sb("av", [128, NTT, 128], BF16)
    yb = S.sb("ayb", [64, 8, T], BF16)
    mask = S.sb("amask", [128, 384], F32)
    sink = S.sb("asink", [128, 8], F32)
    idb = S.sb("aidb", [128, 128], BF16)
    for h in range(8):
        S.dma("sp" if h % 2 else "pool", qT.ap[:, h, :], C.qT[bi, h, :, :], writes=[qT])
    for h in range(2):
        S.dma("sp", kT.ap[:, h, :], C.kT[bi, h, :, :], writes=[kT])
    S.dma("sp", v.ap[:], C.v[bi].rearrange("(i p) c -> p i c", p=128), writes=[v])
    S.dma("sp", mask.ap[:], C.amask[:, :], writes=[mask])
    S.dma("sp", sink.ap[:], C.attn_sink[:].partition_broadcast(128), writes=[sink])
    S.dma("sp", idb.ap[:], C.ident_b[:, :], writes=[idb])
    psl = mk_ring(S, "apsl", [128, 512], F32, 2, psum=True)
    psc = mk_ring(S, "apsc", [128, 512], F32, 2, psum=True)
    ppt = mk_ring(S, "appt", [128, 5, 128], BF16, 2, psum=True)
    ppv = mk_ring(S, "appv", [128, 128], F32, 2, psum=True)
    sc = mk_ring(S, "asc", [128, 640], F32, 2)
    pe_ = mk_ring(S, "ape", [128, 640], F32, 2)
    pn = mk_ring(S, "apn", [128, 640], BF16, 2)
    pts = mk_ring(S, "apts", [128, 5, 128], BF16, 2)
    sm = mk_ring(S, "asm", [128, 8], F32, 4)
    def head_chain(qb, hh, n, lo, hi, m0, ktiles, nk, q0):
            h = hh // 4
            s_ = sc.next()
            if n:
                a = psl.next()
                S.op("pe", lambda e: e.matmul(a.ap[:, :n], qT.ap[:, hh, q0:q0 + 128], kT.ap[:, h, 256 + lo:256 + hi], start=True, stop=True),
                     reads=[qT, kT], writes=[a])
                S.op("dve", lambda e: e.tensor_tensor(out=s_.ap[:, :n], in0=a.ap[:, :n], in1=mask.ap[:, m0:m0 + n], op=ALU.add), reads=[a, mask], writes=[s_])
            b = psc.next()
            S.op("pe", lambda e: e.matmul(b.ap[:, :256], qT.ap[:, hh, q0:q0 + 128], kT.ap[:, h, 0:256], start=True, stop=True), reads=[qT, kT], writes=[b])
            S.op("act", lambda e: e.activation(out=s_.ap[:, n:nk], in_=b.ap[:, :256], func=AF.Identity), reads=[b], writes=[s_])
            yield
            w = sm.next()
            S.op("dve", lambda e: e.tensor_reduce(out=w.ap[:, 0:1], in_=s_.ap[:, :nk], axis=AX.X, op=ALU.max), reads=[s_], writes=[w])
            S.op("dve", lambda e: e.tensor_scalar(out=w.ap[:, 1:2], in0=w.ap[:, 0:1], scalar1=0.125, scalar2=sink.ap[:, hh:hh + 1], op0=ALU.mult, op1=ALU.max),
                 reads=[w, sink], writes=[w])
            S.op("dve", lambda e: e.tensor_scalar(out=w.ap[:, 2:3], in0=w.ap[:, 1:2], scalar1=-1.0, scalar2=None, op0=ALU.mult), reads=[w], writes=[w])
            p_ = pe_.next()
            S.op("act", lambda e: e.activation(out=p_.ap[:, :nk], in_=s_.ap[:, :nk], func=AF.Exp, scale=0.125, bias=w.ap[:, 2:3]),
                 reads=[s_, w], writes=[p_])
            yield
            S.op("dve", lambda e: e.tensor_reduce(out=w.ap[:, 3:4], in_=p_.ap[:, :nk], axis=AX.X, op=ALU.add), reads=[p_], writes=[w])
            S.op("act", lambda e: e.activation(out=w.ap[:, 4:5], in_=w.ap[:, 2:3], func=AF.Exp, bias=sink.ap[:, hh:hh + 1], scale=1.0), reads=[w, sink], writes=[w])
            S.op("dve", lambda e: e.tensor_tensor(out=w.ap[:, 5:6], in0=w.ap[:, 3:4], in1=w.ap[:, 4:5], op=ALU.add), reads=[w], writes=[w])
            S.op("dve", lambda e: e.reciprocal(out=w.ap[:, 6:7], in_=w.ap[:, 5:6]), reads=[w], writes=[w])
            pn_ = pn.next()
            S.op("pool", lambda e: e.tensor_scalar(out=pn_.ap[:, :nk], in0=p_.ap[:, :nk], scalar1=w.ap[:, 6:7], scalar2=None, op0=ALU.mult), reads=[p_, w], writes=[pn_])
            yield
            pt = ppt.next()
            nch = nk // 128
            for j in range(nch):
                S.op("pe", lambda e: e.transpose(out=pt.ap[:, j, :], in_=pn_.ap[:, j * 128:(j + 1) * 128], identity=idb.ap[:]), reads=[pn_, idb], writes=[pt])
            ps_ = pts.next()
            S.op("act", lambda e: e.activation(out=ps_.ap[:, :nch, :], in_=pt.ap[:, :nch, :], func=AF.Identity), reads=[pt], writes=[ps_])
            yield
            o = ppv.next()
            for j in range(nch):
                S.op("pe", lambda e: e.matmul(o.ap[:64, :], v.ap[:, ktiles[j], h * 64:(h + 1) * 64], ps_.ap[:, j, :], start=(j == 0), stop=(j == nch - 1)),
                     reads=[v, ps_], writes=[o])
            S.op("dve", lambda e: e.tensor_copy(out=yb.ap[:, hh, q0:q0 + 128], in_=o.ap[:64, :]), reads=[o], writes=[yb])
            yield

    for qb in range(NTT):
        is_ctx = qb < 2
        q0 = qb * 128
        lo = hi = m0 = 0
        if is_ctx:
            n = 0
            ktiles = []
        else:
            lq = q0 - 256
            lo, hi = max(0, lq - 128), min(LAT, lq + 256)
            n = hi - lo
            m0 = lo - (lq - 128)
            ktiles = [2 + lo // 128 + j for j in range(n // 128)]
        ktiles = ktiles + [0, 1]
        nk = n + 256
        for h0 in range(0, 8, 2):
            interleave([head_chain(qb, hh, n, lo, hi, m0, ktiles, nk, q0) for hh in (h0, h0 + 1)])
    for h in range(8):
        S.dma("sp" if h % 2 else "pool", C.ybT[bi, h, :, :], yb.ap[:, h, :], reads=[yb])
    S.end_phase()


def l0_outproj(S, C, bi, src_stage, dst_stage):
    S.begin_phase()
    ya = S.sb("oya", [128, 4, T], BF16)
    yb = S.sb("oyb", [64, 8, T], BF16)
    wa = S.sb("owa", [128, 4, D], BF16)
    wb = S.sb("owb", [64, 8, D], BF16)
    for c in range(4):
        S.dma("sp", ya.ap[:, c, :], C.yaT[bi, c * 128:(c + 1) * 128, :], writes=[ya])
        S.dma("pool", wa.ap[:, c, :], C.wob0[c * 128:(c + 1) * 128, :], writes=[wa])
    for h in range(8):
        S.dma("sp", yb.ap[:, h, :], C.ybT[bi, h, :, :], writes=[yb])
        S.dma("pool", wb.ap[:, h, :], C.wob0[512 + h * 64:512 + (h + 1) * 64, :], writes=[wb])
    epi = Epi(S, C, "o")
    epi.load(0, 0, bi)
    xin = mk_ring(S, "oxin", [128, 1024], F32, 3)
    py = mk_ring(S, "opy", [128, 1024], F32, 2, psum=True)
    for i in range(NTT):
        xi = xin.next()
        S.dma("sp", xi.ap[:], tile_src(C, src_stage, bi, i), writes=[xi])
        p = py.next()
        for hf in range(2):
            for c in range(4):
                S.op("pe", lambda e: e.matmul(p.ap[:, hf * 512:(hf + 1) * 512], ya.ap[:, c, i * 128:(i + 1) * 128], wa.ap[:, c, hf * 512:(hf + 1) * 512],
                                              start=(c == 0), stop=False), reads=[ya, wa], writes=[p])
            for h in range(8):
                S.op("pe", lambda e: e.matmul(p.ap[:, hf * 512:(hf + 1) * 512], yb.ap[:, h, i * 128:(i + 1) * 128], wb.ap[:, h, hf * 512:(hf + 1) * 512],
                                              start=False, stop=(h == 7)), reads=[yb, wb], writes=[p])
        epi.run(p, xi, i < 2, C.xs[dst_stage - 1][bi, i * 128:(i + 1) * 128, :])
    S.end_phase()


def V(b, *idx):
    return (b, b.ap[idx] if idx else b.ap[:])


def TT(S, eng, o, a, b, op):
    return S.op(eng, lambda e: e.tensor_tensor(out=o[1], in0=a[1], in1=b[1], op=op), reads=[a[0], b[0]], writes=[o[0]])


def TS(S, eng, o, a, s1, s2, op0, op1=None, extra=()):
    if op1 is None:
        return S.op(eng, lambda e: e.tensor_scalar(out=o[1], in0=a[1], scalar1=s1, scalar2=None, op0=op0), reads=[a[0], *extra], writes=[o[0]])
    return S.op(eng, lambda e: e.tensor_scalar(out=o[1], in0=a[1], scalar1=s1, scalar2=s2, op0=op0, op1=op1), reads=[a[0], *extra], writes=[o[0]])


def STT(S, o, a, sc, b, op0, op1, extra=()):
    return S.op("dve", lambda e: e.scalar_tensor_tensor(out=o[1], in0=a[1], scalar=sc, in1=b[1], op0=op0, op1=op1), reads=[a[0], b[0], *extra], writes=[o[0]])


def ACT(S, o, a, func, scale=1.0, bias=None, extra=()):
    if bias is None:
        return S.op("act", lambda e: e.activation(out=o[1], in_=a[1], func=func, scale=scale), reads=[a[0], *extra], writes=[o[0]])
    return S.op("act", lambda e: e.activation(out=o[1], in_=a[1], func=func, scale=scale, bias=bias), reads=[a[0], *extra], writes=[o[0]])


def MM(S, o, l, r, start=True, stop=True):
    return S.op("pe", lambda e: e.matmul(o[1], l[1], r[1], start=start, stop=stop), reads=[l[0], r[0]], writes=[o[0]])


def TR(S, o, a, ident):
    return S.op("pe", lambda e: e.transpose(out=o[1], in_=a[1], identity=ident[1]), reads=[a[0], ident[0]], writes=[o[0]])


RW_E = float(np.exp(-0.5))
INV_DT = BF16


def declare_l1(nc, C, dbg=None):
    NB = C.NB
    I = lambda n, s, dt=F32: dram(nc, n, s, dt, "ExternalInput")
    Sx = lambda n, s, dt=BF16: dram(nc, n, s, dt, "ExternalOutput" if (dbg and n in dbg) else None)
    C.o_w_rw = I("o_w_rw", [D, 1920])
    C.o_w_dn = I("o_w_dn", [D, 1536])
    C.o_w_z = I("o_w_z", [D, 528])
    C.o_w_out = I("o_w_out", [D, D])
    C.rw_mu = I("rw_mu", [1920])
    C.dn_conv = I("dn_conv", [3, 1536])
    C.rw_rows = I("rw_rows", [10, 512])
    C.rw_w2 = I("rw_w2", [128, 512])
    C.rw_a2 = I("rw_a2", [128, 512])
    C.rw_g2 = I("rw_g2", [128, 512])
    C.dn_rows = I("dn_rows", [3, 8])
    C.dn_ng = I("dn_ng", [512])
    C.tri = I("tri", [2, 128, 128])
    C.tris = I("tris", [2, 128, 128])
    C.blkm = I("blkm", [4, 128, 128])
    C.tsw = Sx("tsw", [3, 1920], F32)
    C.wrb = [Sx(f"wrb{j}", [D, 1920]) for j in range(3)]
    C.wdb = [Sx(f"wdb{j}", [D, 1536]) for j in range(3)]
    C.wzb = Sx("wzb", [D, 528])
    C.wob1 = Sx("wob1", [D, D])
    C.rw_ops = Sx("rw_ops", [NB, 2, 6, T, 512])
    C.rw_v = Sx("rw_v", [NB, T, 512])
    C.rw_gc = Sx("rw_gc", [NB, 2, NTT, 64, 8], F32)
    C.rw_g = Sx("rw_g", [NB, T, 512], F32)
    C.rw_bonus = Sx("rw_bonus", [NB, T, 512], F32)
    C.y_rw = Sx("y_rw", [NB, 2, T, 512], F32)
    C.dn_qk = Sx("dn_qk", [NB, 2, T, 512])
    C.dn_ops = Sx("dn_ops", [NB, 2, 5, T, 512])
    C.dn_G = Sx("dn_G", [NB, 2, 3, T, 4], F32)
    C.dn_z = Sx("dn_z", [NB, T, 512], F32)
    C.y_dn = Sx("y_dn", [NB, 2, T, 512], F32)


def host_l1(inputs, m):
    f = lambda a: np.ascontiguousarray(np.asarray(a, dtype=np.float32))
    w = np.asarray(inputs["o_w_in"])[0]
    m["o_w_rw"] = f(w[:, :1920])
    m["o_w_dn"] = f(w[:, 1920:1920 + 1536])
    m["o_w_z"] = f(w[:, 1920 + 1536:])
    m["o_w_out"] = f(np.asarray(inputs["o_w_out"])[0])
    m["rw_mu"] = f(np.asarray(inputs["rw_mu"])[0])
    m["dn_conv"] = f(np.asarray(inputs["dn_conv"])[0])
    g = lambda k: np.asarray(inputs[k])[0]
    m["rw_rows"] = f(np.stack([g("rw_w0")[0], g("rw_w0")[1], g("rw_a0")[0], g("rw_a0")[1], g("rw_kk"), g("rw_ka"),
                               g("rw_rk").reshape(512), g("rw_lnx_g"), g("rw_lnx_b"), np.zeros(512, np.float32)], 0))
    m["rw_w2"] = f(g("rw_w2").reshape(128, 512))
    m["rw_a2"] = f(g("rw_a2").reshape(128, 512))
    m["rw_g2"] = f(g("rw_g2"))
    m["dn_rows"] = f(np.stack([g("dn_A_log").reshape(8), g("dn_dt_bias").reshape(8), np.zeros(8, np.float32)], 0))
    m["dn_ng"] = f(np.tile(g("dn_norm_g"), 4))
    j = np.arange(128)[:, None]
    t = np.arange(128)[None, :]
    m["tri"] = f(np.stack([(j <= t), (j >= t)], 0))
    m["tris"] = f(np.stack([(j < t), (j > t)], 0))
    bd = lambda n: (j // n == t // n)
    m["blkm"] = f(np.stack([bd(16), bd(32) & ~bd(16), bd(64) & ~bd(32), ~bd(64)], 0))
    return m


def l1_setup(S, C):
    S.begin_phase()
    mu = S.sb("smu", [1, 1920], F32)
    o = S.sb("smo", [1, 3, 1920], F32)
    S.dma("sp", mu.ap[:], C.rw_mu[:].partition_broadcast(1), writes=[mu])
    TS(S, "dve", V(o, slice(None), 0, slice(None)), V(mu), 0.5, None, ALU.mult)
    TS(S, "dve", V(o, slice(None), 1, slice(None)), V(mu), -1.0, 1.0, ALU.mult, ALU.add)
    TS(S, "dve", V(o, slice(None), 2, slice(None)), V(mu), 0.5, None, ALU.mult)
    S.dma("sp", C.tsw.rearrange("(o j) n -> o j n", o=1), o.ap[:], reads=[o])
    S.end_phase()
    for j in range(3):
        prep_weight(S, C.wrb[j], C.o_w_rw, D, 1920, scale=C.tsw[j, :], tag=f"qr{j}")
        prep_weight(S, C.wdb[j], C.o_w_dn, D, 1536, scale=C.dn_conv[j, :], tag=f"qd{j}")
    prep_weight(S, C.wzb, C.o_w_z, D, 528, tag="qz")
    prep_weight(S, C.wob1, C.o_w_out, D, D, tag="qo")


def build_hT(S, C, bi, src_stage, l):
    hT = S.sb("bhT", [128, 8, T + 4], BF16)
    S.op("pool", lambda e: e.memset(hT.ap[:], 0.0), writes=[hT])
    S.begin_sub()
    xin = mk_ring(S, "bxin", [128, 1024], F32, 2)
    ptr = mk_ring(S, "bptr", [128, 512], F32, 2, psum=True)
    for i in range(NTT):
        xi = xin.next()
        S.dma("sp", xi.ap[:], tile_src(C, src_stage, bi, i), writes=[xi])
        transpose_mod(S, C, xi, hT, colof(i), l, 0, (C.R - 1 if i < 2 else bi), ptr, C.ident)
    S.end_sub()
    return hT


def bc3(b, n, w):
    return (b, b.ap[:, 0:n].unsqueeze(2).to_broadcast([128, n, w]))


def r3(b, n, *pre):
    ap = b.ap[pre] if pre else b.ap[:]
    return (b, ap.rearrange("p (h d) -> p h d", h=n))


def proj3(S, C, p, hT, c0, w, wt, col0, ncol):
    n = 0
    for j in range(3):
        for kc in range(8):
            MM(S, (p, p.ap[:, :ncol]), (hT, hT.ap[:, kc, c0 + j - 1:c0 + j - 1 + 128]), (wt, wt.ap[:, j, kc, col0:col0 + ncol]), start=(n == 0), stop=(n == 23))
            n += 1


def l1_feat_rw(S, C, bi, hT):
    S.begin_sub()
    ones = S.sb("fones", [128, 128], F32)
    S.op("pool", lambda e: e.memset(ones.ap[:], 1.0), writes=[ones])
    tri = S.sb("ftri", [128, 2, 128], F32)
    for d in range(2):
        S.dma("sp", tri.ap[:, d, :], C.tri[d, :, :], writes=[tri])
    rows = S.sb("frows", [128, 7, 512], F32)
    for q in range(7):
        S.dma("sp", rows.ap[:, q, :], C.rw_rows[q, :].partition_broadcast(128), writes=[rows])
    lwb = S.sb("flwb", [128, 3, 512], BF16)
    loraT = S.sb("floraT", [128, 3, T], BF16)
    S.begin_sub()
    lw = S.sb("flw", [128, 3, 512], F32)
    for q, src in enumerate((C.rw_w2, C.rw_a2, C.rw_g2)):
        S.dma("sp", lw.ap[:, q, :], src[:, :], writes=[lw])
    S.op("dve", lambda e: e.tensor_copy(out=lwb.ap[:], in_=lw.ap[:]), reads=[lw], writes=[lwb])
    wl = S.sb("fwl", [128, 3, 8, 384], BF16)
    for j in range(3):
        for kc in range(8):
            S.dma("sp" if kc % 2 else "pool", wl.ap[:, j, kc, :], C.wrb[j][kc * 128:(kc + 1) * 128, 1536:1920], writes=[wl])
    pp = mk_ring(S, "fpp", [128, 512], F32, 2, psum=True)
    for (tok0, w) in [(0, 256)] + [(256 + 512 * k, 512) for k in range(4)]:
        c0 = colof(tok0 // 128)
        for q, fn in enumerate((AF.Tanh, AF.Identity, AF.Sigmoid)):
            p = pp.next()
            n = 0
            for j in range(3):
                for kc in range(8):
                    MM(S, (p, p.ap[:, :w]), (wl, wl.ap[:, j, kc, q * 128:(q + 1) * 128]), (hT, hT.ap[:, kc, c0 + j - 1:c0 + j - 1 + w]), start=(n == 0), stop=(n == 23))
                    n += 1
            ACT(S, (loraT, loraT.ap[:, q, tok0:tok0 + w]), (p, p.ap[:, :w]), fn)
    S.end_sub()
    wr = S.sb("fwr", [128, 3, 8, 1536], BF16)
    for j in range(3):
        for kc in range(8):
            S.dma("sp" if kc % 2 else "pool", wr.ap[:, j, kc, :], C.wrb[j][kc * 128:(kc + 1) * 128, 0:1536], writes=[wr])
    prkv = [S.ps(f"fp{n}", [128, 512], F32) for n in "rkv"]
    pqd = [mk_ring(S, f"fpq{d}", [128, 512], F32, 2, psum=True) for d in range(2)]
    F = lambda n: S.sb("f_" + n, [128, 512], F32)
    rs, ks, kkr, sq, kk, ksum = [F(n) for n in "rs ks kkr sq kk ksum".split()]
    Fd = []
    for d in range(2):
        X = Ctx()
        X.zt, X.logw, X.a_, X.Gs, X.eG, X.enG, X.eE, X.kd, X.bd = [F(f"{n}{d}") for n in "zt logw a Gs eG enG eE kd bd".split()]
        X.eP, X.t1 = X.zt, X.Gs
        Fd.append(X)
    tb, bon, gs = sq, Fd[0].zt, Fd[0].Gs
    vb = mk_ring(S, "fvb", [128, 512], BF16, 2)
    ot = mk_ring(S, "fot", [128, 6, 512], BF16, 2)
    sm = mk_ring(S, "fsm", [128, 16], F32, 2)
    gcs = mk_ring(S, "fgcs", [64, 8], F32, 2)
    KK, KA, RK = [(rows, rows.ap[:, q, :]) for q in (4, 5, 6)]

    def dir_chain(d, i, rws):
        X = Fd[d]
        zt, logw, a_, Gs, eG, enG, eP, eE, t1, kd, bd = X.zt, X.logw, X.a_, X.Gs, X.eG, X.enG, X.eP, X.eE, X.t1, X.kd, X.bd
        o = ot.next()
        pq = pqd[d]
        pgc = prkv[d]
        pz, pa_ = pq.next(), pq.next()
        MM(S, V(pz), (loraT, loraT.ap[d * 64:(d + 1) * 64, 0, rws]), (lwb, lwb.ap[d * 64:(d + 1) * 64, 0, :]))
        MM(S, V(pa_), (loraT, loraT.ap[d * 64:(d + 1) * 64, 1, rws]), (lwb, lwb.ap[d * 64:(d + 1) * 64, 1, :]))
        TT(S, "dve", V(zt), V(pz), (rows, rows.ap[:, d, :]), ALU.add)
        ACT(S, V(zt), V(zt), AF.Sigmoid)
        TS(S, "pool", V(logw), V(zt), -RW_E, None, ALU.mult)
        TT(S, "dve", V(a_), V(pa_), (rows, rows.ap[:, 2 + d, :]), ALU.add)
        ACT(S, V(a_), V(a_), AF.Sigmoid)
        yield
        pG, pT = pq.next(), pq.next()
        MM(S, V(pG), (tri, tri.ap[:, d, :]), V(logw))
        MM(S, V(pT), V(ones), V(logw))
        for h in range(8):
            MM(S, (pgc, pgc.ap[:64, h:h + 1]), (logw, logw.ap[:, h * 64:(h + 1) * 64]), (ones, ones.ap[:, 0:1]))
        gc = gcs.next()
        ACT(S, V(gc), (pgc, pgc.ap[:64, 0:8]), AF.Exp)
        S.dma("pool", C.rw_gc[bi, d, i, :, :], gc.ap[:], reads=[gc])
        ACT(S, V(Gs), V(pG), AF.Identity)
        yield
        ACT(S, V(eG), V(Gs), AF.Exp)
        ACT(S, V(enG), V(Gs), AF.Exp, scale=-1.0)
        TT(S, "dve", V(eP), V(Gs), V(logw), ALU.subtract)
        ACT(S, V(eP), V(eP), AF.Exp)
        TT(S, "dve", V(eE), V(pT), V(Gs), ALU.subtract)
        ACT(S, V(eE), V(eE), AF.Exp)
        yield
        STT(S, V(t1), V(a_), -1.0, KA, ALU.add, ALU.mult)
        STT(S, V(kd), V(t1), 1.0, V(ks), ALU.add, ALU.mult)
        TT(S, "pool", V(bd), V(kk), V(a_), ALU.mult)
        yield
        O = lambda q: (o, o.ap[:, q, :])
        TT(S, "dve", O(0), V(rs), V(eG), ALU.mult)
        TT(S, "pool", O(1), V(kd), V(enG), ALU.mult)
        TT(S, "dve", O(2), V(bd), V(enG), ALU.mult)
        TT(S, "pool", O(3), V(kk), V(eP), ALU.mult)
        TT(S, "dve", O(4), V(kd), V(eE), ALU.mult)
        TT(S, "pool", O(5), V(bd), V(eE), ALU.mult)
        S.dma("sp", C.rw_ops[bi, d, :, rws, :].rearrange("q t c -> t q c"), o.ap[:], reads=[o])
        yield

    for i in range(NTT):
        c0 = colof(i)
        rws = slice(i * 128, (i + 1) * 128)
        for n in range(3):
            proj3(S, C, prkv[n], hT, c0, 128, wr, n * 512, 512)
        ACT(S, V(rs), V(prkv[0]), AF.Identity)
        ACT(S, V(ks), V(prkv[1]), AF.Identity)
        v_ = vb.next()
        ACT(S, V(v_), V(prkv[2]), AF.Identity)
        S.dma("pool", C.rw_v[bi, rws, :], v_.ap[:], reads=[v_])
        w = sm.next()
        TT(S, "dve", V(kkr), V(ks), KK, ALU.mult)
        TT(S, "pool", V(sq), V(kkr), V(kkr), ALU.mult)
        S.op("dve", lambda e: e.tensor_reduce(out=w.ap[:, 0:8], in_=r3(sq, 8)[1], axis=AX.X, op=ALU.add), reads=[sq], writes=[w])
        TS(S, "dve", V(w, slice(None), slice(0, 8)), V(w, slice(None), slice(0, 8)), 1e-6, None, ALU.add)
        ACT(S, V(w, slice(None), slice(0, 8)), V(w, slice(None), slice(0, 8)), AF.Sqrt)
        S.op("dve", lambda e: e.reciprocal(out=w.ap[:, 0:8], in_=w.ap[:, 0:8]), reads=[w], writes=[w])
        TT(S, "dve", r3(kk, 8), r3(kkr, 8), bc3(w, 8, 64), ALU.mult)
        interleave([dir_chain(d, i, rws) for d in range(2)])
        TT(S, "pool", V(ksum), V(Fd[0].kd), V(Fd[1].kd), ALU.add)
        TT(S, "dve", V(tb), V(rs), V(ksum), ALU.mult)
        TT(S, "pool", V(tb), V(tb), RK, ALU.mult)
        S.op("dve", lambda e: e.tensor_reduce(out=w.ap[:, 8:16], in_=r3(tb, 8)[1], axis=AX.X, op=ALU.add), reads=[tb], writes=[w])
        TT(S, "dve", r3(bon, 8), r3(v_, 8), (w, w.ap[:, 8:16].unsqueeze(2).to_broadcast([128, 8, 64])), ALU.mult)
        S.dma("pool", C.rw_bonus[bi, rws, :], bon.ap[:], reads=[bon])
        pg = pqd[0].next()
        MM(S, V(pg), (loraT, loraT.ap[:, 2, rws]), (lwb, lwb.ap[:, 2, :]))
        ACT(S, V(gs), V(pg), AF.Identity)
        S.dma("pool", C.rw_g[bi, rws, :], gs.ap[:], reads=[gs])
    S.end_sub()


def l1_feat_dn(S, C, bi, hT):
    S.begin_sub()
    ones = S.sb("gones", [128, 128], F32)
    S.op("pool", lambda e: e.memset(ones.ap[:], 1.0), writes=[ones])
    tri = S.sb("gtri", [128, 2, 128], F32)
    for d in range(2):
        S.dma("sp", tri.ap[:, d, :], C.tri[d, :, :], writes=[tri])
    dr = S.sb("gdr", [128, 2, 8], F32)
    for q in range(2):
        S.dma("sp", dr.ap[:, q, :], C.dn_rows[q, :].partition_broadcast(128), writes=[dr])
    ACT(S, V(dr, slice(None), 0, slice(None)), V(dr, slice(None), 0, slice(None)), AF.Exp)
    TS(S, "dve", V(dr, slice(None), 0, slice(None)), V(dr, slice(None), 0, slice(None)), -1.0, None, ALU.mult)
    wd = S.sb("gwd", [128, 3, 8, 1536], BF16)
    wz = S.sb("gwz", [128, 8, 528], BF16)
    for j in range(3):
        for kc in range(8):
            S.dma("sp" if kc % 2 else "pool", wd.ap[:, j, kc, :], C.wdb[j][kc * 128:(kc + 1) * 128, :], writes=[wd])
    for kc in range(8):
        S.dma("sp", wz.ap[:, kc, :], C.wzb[kc * 128:(kc + 1) * 128, :], writes=[wz])
    pqkv = [S.ps(f"gp{n}", [128, 512], F32) for n in "qkv"]
    pz = S.ps("gpz", [128, 512], F32)
    pgt = S.ps("gpgt", [128, 16], F32)
    pG = mk_ring(S, "gpG", [128, 8], F32, 2, psum=True)
    F = lambda n: S.sb("g_" + n, [128, 512], F32)
    qs, ks, vs, sq, zs = [F(n) for n in "qs ks vs sq zs".split()]
    qk = mk_ring(S, "gqk", [128, 2, 512], BF16, 2)
    ot = mk_ring(S, "got", [128, 5, 512], BF16, 2)
    sm = mk_ring(S, "gsm", [128, 64], F32, 2)
    go = mk_ring(S, "ggo", [128, 3, 4], F32, 2)
    for i in range(NTT):
        c0 = colof(i)
        rws = slice(i * 128, (i + 1) * 128)
        for n in range(3):
            proj3(S, C, pqkv[n], hT, c0, 128, wd, n * 512, 512)
        for kc in range(8):
            MM(S, V(pz), (hT, hT.ap[:, kc, c0:c0 + 128]), (wz, wz.ap[:, kc, 0:512]), start=(kc == 0), stop=(kc == 7))
        for kc in range(8):
            MM(S, V(pgt), (hT, hT.ap[:, kc, c0:c0 + 128]), (wz, wz.ap[:, kc, 512:528]), start=(kc == 0), stop=(kc == 7))
        for src, dst in zip(pqkv + [pz], (qs, ks, vs, zs)):
            ACT(S, V(dst), V(src), AF.Silu)
        S.dma("pool", C.dn_z[bi, rws, :], zs.ap[:], reads=[zs])
        w = sm.next()
        Wc = lambda a, b: (w, w.ap[:, a:b])
        ACT(S, Wc(0, 16), V(pgt), AF.Identity)
        qk_ = qk.next()
        for n, (src, sc) in enumerate(((qs, 128.0 ** -0.5), (ks, 1.0))):
            TT(S, "pool", V(sq), V(src), V(src), ALU.mult)
            S.op("dve", lambda e: e.tensor_reduce(out=w.ap[:, 16 + 4 * n:20 + 4 * n], in_=r3(sq, 4)[1], axis=AX.X, op=ALU.add), reads=[sq], writes=[w])
            TS(S, "dve", Wc(16 + 4 * n, 20 + 4 * n), Wc(16 + 4 * n, 20 + 4 * n), 1e-6, None, ALU.add)
            ACT(S, Wc(16 + 4 * n, 20 + 4 * n), Wc(16 + 4 * n, 20 + 4 * n), AF.Sqrt)
            S.op("dve", lambda e: e.reciprocal(out=w.ap[:, 16 + 4 * n:20 + 4 * n], in_=w.ap[:, 16 + 4 * n:20 + 4 * n]), reads=[w], writes=[w])
            if sc != 1.0:
                TS(S, "dve", Wc(16, 20), Wc(16, 20), sc, None, ALU.mult)
            TT(S, "dve", r3(src, 4), r3(src, 4), (w, w.ap[:, 16 + 4 * n:20 + 4 * n].unsqueeze(2).to_broadcast([128, 4, 128])), ALU.mult)
            S.op("pool", lambda e: e.tensor_copy(out=qk_.ap[:, n, :], in_=src.ap[:]), reads=[src], writes=[qk_])
        S.dma("sp", C.dn_qk[bi, :, rws, :].rearrange("q t c -> t q c"), qk_.ap[:], reads=[qk_])
        for d in range(2):
            o = ot.next()
            g_ = go.next()
            TT(S, "dve", Wc(24, 28), Wc(d * 4, d * 4 + 4), (dr, dr.ap[:, 1, d * 4:d * 4 + 4]), ALU.add)
            ACT(S, Wc(24, 28), Wc(24, 28), AF.Exp)
            ACT(S, Wc(24, 28), Wc(24, 28), AF.Ln, bias=1.0)
            TT(S, "dve", (g_, g_.ap[:, 0, :]), Wc(24, 28), (dr, dr.ap[:, 0, d * 4:d * 4 + 4]), ALU.mult)
            ACT(S, Wc(28, 32), Wc(8 + d * 4, 12 + d * 4), AF.Sigmoid)
            p = pG.next()
            MM(S, (p, p.ap[:, 0:4]), (tri, tri.ap[:, d, :]), (g_, g_.ap[:, 0, :]))
            MM(S, (p, p.ap[:, 4:8]), V(ones), (g_, g_.ap[:, 0, :]))
            ACT(S, (g_, g_.ap[:, 1:3, :]), (p, p.ap[:, 0:8].rearrange("p (a b) -> p a b", a=2)), AF.Identity)
            S.dma("pool", C.dn_G[bi, d, :, rws, :].rearrange("q t c -> t q c"), g_.ap[:], reads=[g_])
            ACT(S, Wc(32, 36), (g_, g_.ap[:, 1, :]), AF.Exp)
            TT(S, "dve", Wc(36, 40), (g_, g_.ap[:, 2, :]), (g_, g_.ap[:, 1, :]), ALU.subtract)
            ACT(S, Wc(36, 40), Wc(36, 40), AF.Exp)
            TT(S, "dve", Wc(40, 44), Wc(28, 32), Wc(32, 36), ALU.mult)
            B4 = lambda a: (w, w.ap[:, a:a + 4].unsqueeze(2).to_broadcast([128, 4, 128]))
            O = lambda q: (o, o.ap[:, q, :].rearrange("p (h d) -> p h d", h=4))
            TT(S, "dve", O(0), r3(qs, 4), B4(32), ALU.mult)
            TT(S, "pool", O(1), r3(ks, 4), B4(28), ALU.mult)
            TT(S, "dve", O(2), r3(ks, 4), B4(40), ALU.mult)
            TT(S, "pool", O(3), r3(ks, 4), B4(36), ALU.mult)
            TT(S, "dve", O(4), r3(vs, 4), B4(28), ALU.mult)
            S.dma("sp", C.dn_ops[bi, d, :, rws, :].rearrange("q t c -> t q c"), o.ap[:], reads=[o])
    S.end_sub()


def inv_group(S, P, PT, K, ps, out, nh=4):
    mk = lambda: K.rW.next()
    pA, pB, pC = ps
    PD, PDT, Z, ZT = mk(), mk(), K.rZ.next(), K.rZ.next()
    TT(S, "dve", V(PD), V(P), V(K.mb[0]), ALU.mult)
    TT(S, "pool", V(PDT), V(PT), V(K.mb[0]), ALU.mult)
    TT(S, "dve", V(Z), V(K.ident4), V(PD), ALU.subtract)
    TT(S, "pool", V(ZT), V(K.ident4), V(PDT), ALU.subtract)
    yield
    cur, curT = PD, PDT
    for lv in range(3):
        for h in range(nh):
            MM(S, (pA, pA.ap[:, h, :]), (curT, curT.ap[:, h, :]), (cur, cur.ap[:, h, :]))
        for h in range(nh):
            MM(S, (pB, pB.ap[:, h, :]), (cur, cur.ap[:, h, :]), (curT, curT.ap[:, h, :]))
        Pn, PTn = mk(), mk()
        ACT(S, V(Pn), V(pA), AF.Identity)
        S.op("dve", lambda e: e.tensor_copy(out=PTn.ap[:], in_=pB.ap[:]), reads=[pB], writes=[PTn])
        yield
        for h in range(nh):
            MM(S, (pC, pC.ap[:, h, :]), (PTn, PTn.ap[:, h, :]), (Z, Z.ap[:, h, :]))
        for h in range(nh):
            MM(S, (pA, pA.ap[:, h, :]), (Pn, Pn.ap[:, h, :]), (ZT, ZT.ap[:, h, :]))
        TT(S, "dve", V(Z), V(Z), V(pC), ALU.add)
        TT(S, "dve", V(ZT), V(ZT), V(pA), ALU.add)
        yield
        cur, curT = Pn, PTn
    for m in range(1, 4):
        last = m == 3
        O, OT, Y = mk(), mk(), mk()
        TT(S, "dve", V(O), V(P), V(K.mb[m]), ALU.mult)
        TT(S, "pool", V(OT), V(PT), V(K.mb[m]), ALU.mult)
        for h in range(nh):
            MM(S, (pA, pA.ap[:, h, :]), (OT, OT.ap[:, h, :]), (Z, Z.ap[:, h, :]))
        ACT(S, V(Y), V(pA), AF.Identity)
        if not last:
            YT = mk()
            for h in range(nh):
                MM(S, (pB, pB.ap[:, h, :]), (Z, Z.ap[:, h, :]), (OT, OT.ap[:, h, :]))
            S.op("dve", lambda e: e.tensor_copy(out=YT.ap[:], in_=pB.ap[:]), reads=[pB], writes=[YT])
        yield
        for h in range(nh):
            MM(S, (pC, pC.ap[:, h, :]), (ZT, ZT.ap[:, h, :]), (Y, Y.ap[:, h, :]))
        if not last:
            for h in range(nh):
                MM(S, (pB, pB.ap[:, h, :]), (Y, Y.ap[:, h, :]), (ZT, ZT.ap[:, h, :]))
        TT(S, "dve", V(Z), V(Z), V(pC), ALU.subtract)
        if not last:
            TT(S, "dve", V(ZT), V(ZT), V(pB), ALU.subtract)
        yield
    out.append(Z)


def interleave(gens):
    gens = list(gens)
    while gens:
        for g in list(gens):
            try:
                next(g)
            except StopIteration:
                gens.remove(g)


def scan_consts(S, C, tag):
    K = Ctx()
    K.idb = S.sb(tag + "idb", [128, 128], BF16)
    S.dma("sp", K.idb.ap[:], C.ident_b[:, :], writes=[K.idb])
    K.ident4 = S.sb(tag + "id4", [128, 4, 128], F32)
    K.mSI = [S.sb(tag + f"mSI{d}", [128, 4, 2, 128], F32) for d in range(2)]
    K.mS = [S.sb(tag + f"mS{d}", [128, 4, 128], F32) for d in range(2)]
    K.mI = [S.sb(tag + f"mI{d}", [128, 4, 128], F32) for d in range(2)]
    for h in range(4):
        S.dma("sp", K.ident4.ap[:, h, :], C.ident_d[:, :], writes=[K.ident4])
        for d in range(2):
            S.dma("sp", K.mSI[d].ap[:, h, 0, :], C.tris[d, :, :], writes=[K.mSI[d]])
            S.dma("pool", K.mSI[d].ap[:, h, 1, :], C.tri[d, :, :], writes=[K.mSI[d]])
            S.dma("sp", K.mS[d].ap[:, h, :], C.tris[d, :, :], writes=[K.mS[d]])
            S.dma("pool", K.mI[d].ap[:, h, :], C.tri[d, :, :], writes=[K.mI[d]])
    K.mb = [S.sb(tag + f"mb{m}", [128, 4, 128], F32) for m in range(4)]
    for m in range(4):
        for h in range(4):
            S.dma("sp" if h % 2 else "pool", K.mb[m].ap[:, h, :], C.blkm[m, :, :], writes=[K.mb[m]])
    return K


def chain_res(S, K, tag):
    R = Ctx()
    R.__dict__.update(K.__dict__)
    R.rP = mk_ring(S, tag + "rP", [128, 4, 128], INV_DT, 1)
    R.rPT = mk_ring(S, tag + "rPT", [128, 4, 128], INV_DT, 1)
    R.rW = mk_ring(S, tag + "rW", [128, 4, 128], INV_DT, 8)
    R.rZ = mk_ring(S, tag + "rZ", [128, 4, 128], INV_DT, 4)
    return R


def tile_order(d):
    return list(range(NTT)) if d == 0 else [1, 0] + list(range(NTT - 1, 1, -1))


def l1_scan_rw(S, C, bi):
    S.begin_phase()
    K0 = scan_consts(S, C, "r")
    interleave([rw_chain(S, C, chain_res(S, K0, f"r{d}"), bi, d) for d in range(2)])
    S.end_phase()


def rw_chain(S, C, K, bi, d):
    t = f"r{d}"
    X0, X1, X2 = [S.ps(t + f"X{n}", [128, 4, 128], F32) for n in range(3)]
    ptr = S.ps(t + "ptr", [64, 8, 128], BF16)
    ot_r = mk_ring(S, t + "ot", [128, 6, 512], BF16, 2)
    vt_r = mk_ring(S, t + "vt", [128, 512], BF16, 2)
    gc_r = mk_ring(S, t + "gc", [64, 8], F32, 2)
    AR_r = mk_ring(S, t + "AR", [64, 8, 2, 128], BF16, 1)
    KT_r = mk_ring(S, t + "KT", [64, 8, 128], BF16, 1)
    BT_r = mk_ring(S, t + "BT", [64, 8, 128], BF16, 1)
    KN_r = mk_ring(S, t + "KN", [128, 8, 2, 128], BF16, 1)
    NB_r = mk_ring(S, t + "NB", [128, 8, 128], BF16, 1)
    Zb_r = mk_ring(S, t + "Zb", [128, 4, 128], BF16, 1)
    WT_r = mk_ring(S, t + "WT", [64, 8, 128], BF16, 1)
    Xs_r = mk_ring(S, t + "Xs", [128, 4, 64], BF16, 1)
    nU0_r = mk_ring(S, t + "nU0", [128, 8, 64], F32, 1)
    nU_r = mk_ring(S, t + "nU", [128, 8, 64], BF16, 1)
    ys_r = mk_ring(S, t + "ys", [128, 512], F32, 2)
    ST = S.sb(t + "ST", [64, 8, 64], F32)
    STb = S.sb(t + "STb", [64, 8, 64], BF16)
    S.op("dve", lambda e: e.memset(ST.ap[:], 0.0), writes=[ST])
    S.op("dve", lambda e: e.memset(STb.ap[:], 0.0), writes=[STb])
    v8 = lambda p: p.ap[:].rearrange("p a b -> p (a b)").rearrange("p (h v) -> p h v", v=64)
    for i in tile_order(d):
        rws = slice(i * 128, (i + 1) * 128)
        ot, vt, gc = ot_r.next(), vt_r.next(), gc_r.next()
        S.dma("sp", ot.ap[:], C.rw_ops[bi, d, :, rws, :].rearrange("q t c -> t q c"), writes=[ot])
        S.dma("pool", vt.ap[:], C.rw_v[bi, rws, :], writes=[vt])
        S.dma("pool", gc.ap[:], C.rw_gc[bi, d, i, :, :], writes=[gc])
        AR, KT, BT = AR_r.next(), KT_r.next(), BT_r.next()
        for q, dst in ((3, (AR, AR.ap[:, :, 0, :])), (0, (AR, AR.ap[:, :, 1, :])), (1, V(KT)), (2, V(BT))):
            for h in range(8):
                TR(S, (ptr, ptr.ap[:, h, :]), (ot, ot.ap[:, q, h * 64:(h + 1) * 64]), V(K.idb))
            if q in (3, 1):
                ACT(S, dst, V(ptr), AF.Identity)
            else:
                S.op("dve", lambda e: e.tensor_copy(out=dst[1], in_=ptr.ap[:]), reads=[ptr], writes=[dst[0]])
            yield
        KN, NBm, WT, nU0 = KN_r.next(), NB_r.next(), WT_r.next(), nU0_r.next()
        for g in range(2):
            hs = [(hl, g * 4 + hl) for hl in range(4)]
            P, PT = K.rP.next(), K.rPT.next()
            for hl, h in hs:
                MM(S, (X0, X0.ap[:, hl, :]), (BT, BT.ap[:, h, :]), (AR, AR.ap[:, h, 0, :]))
            for hl, h in hs:
                MM(S, (X1, X1.ap[:, hl, :]), (AR, AR.ap[:, h, 0, :]), (BT, BT.ap[:, h, :]))
            for hl, h in hs:
                MM(S, (X2, X2.ap[:, hl, :]), (BT, BT.ap[:, h, :]), (AR, AR.ap[:, h, 1, :]))
            TT(S, "dve", V(P), V(X0), V(K.mS[d]), ALU.mult)
            TT(S, "dve", V(PT), V(X1), V(K.mS[1 - d]), ALU.mult)
            TT(S, "dve", (NBm, NBm.ap[:, g * 4:(g + 1) * 4, :]), V(X2), V(K.mI[d]), ALU.mult)
            yield
            for hl, h in hs:
                MM(S, (X0, X0.ap[:, hl, :]), (KT, KT.ap[:, h, :]), (AR, AR.ap[:, h, 0, :]))
            for hl, h in hs:
                MM(S, (X1, X1.ap[:, hl, :]), (KT, KT.ap[:, h, :]), (AR, AR.ap[:, h, 1, :]))
            TT(S, "dve", (KN, KN.ap[:, g * 4:(g + 1) * 4, 0, :]), V(X0), V(K.mS[d]), ALU.mult)
            TT(S, "dve", (KN, KN.ap[:, g * 4:(g + 1) * 4, 1, :]), V(X1), V(K.mI[d]), ALU.mult)
            yield
            zo = []
            yield from inv_group(S, P, PT, K, (X0, X1, X2), zo)
            Zb = zo[0]
            for hl, h in hs:
                MM(S, (X0, X0.ap[:, hl, 0:64]), (KN, KN.ap[:, h, 0, :]), (vt, vt.ap[:, h * 64:(h + 1) * 64]))
            Xs = Xs_r.next()
            ACT(S, V(Xs), (X0, X0.ap[:, :, 0:64]), AF.Identity)
            yield
            for hl, h in hs:
                MM(S, (X1, X1.ap[:64, hl, :]), (ot, ot.ap[:, 3, h * 64:(h + 1) * 64]), (Zb, Zb.ap[:, hl, :]))
            ACT(S, (WT, WT.ap[:, g * 4:(g + 1) * 4, :]), (X1, X1.ap[:64, :, :]), AF.Identity)
            for hl, h in hs:
                MM(S, (X2, X2.ap[:, hl, 0:64]), (Zb, Zb.ap[:, hl, :]), (Xs, Xs.ap[:, hl, :]))
            TS(S, "dve", (nU0, nU0.ap[:, g * 4:(g + 1) * 4, :]), (X2, X2.ap[:, :, 0:64]), -1.0, None, ALU.mult)
            yield
        for h in range(8):
            MM(S, (X0, v8(X0)[:, h, :]), (WT, WT.ap[:, h, :]), (STb, STb.ap[:, h, :]))
        nU = nU_r.next()
        TT(S, "dve", V(nU), V(nU0), (X0, v8(X0)), ALU.subtract)
        yield
        for h in range(8):
            MM(S, (X1, v8(X1)[:, h, :]), (AR, AR.ap[:, h, 1, :]), (STb, STb.ap[:, h, :]), start=True, stop=False)
            MM(S, (X1, v8(X1)[:, h, :]), (KN, KN.ap[:, h, 1, :]), (vt, vt.ap[:, h * 64:(h + 1) * 64]), start=False, stop=False)
            MM(S, (X1, v8(X1)[:, h, :]), (NBm, NBm.ap[:, h, :]), (nU, nU.ap[:, h, :]), start=False, stop=True)
        ys = ys_r.next()
        ACT(S, r3(ys, 8), (X1, v8(X1)), AF.Identity)
        S.dma("sp", C.y_rw[bi, d, rws, :], ys.ap[:], reads=[ys])
        for h in range(8):
            MM(S, (X2, v8(X2)[:64, h, :]), (ot, ot.ap[:, 4, h * 64:(h + 1) * 64]), (vt, vt.ap[:, h * 64:(h + 1) * 64]), start=True, stop=False)
            MM(S, (X2, v8(X2)[:64, h, :]), (ot, ot.ap[:, 5, h * 64:(h + 1) * 64]), (nU, nU.ap[:, h, :]), start=False, stop=True)
        TT(S, "dve", V(ST), V(ST), (gc, gc.ap[:, 0:8].unsqueeze(2).to_broadcast([64, 8, 64])), ALU.mult)
        TT(S, "dve", V(ST), V(ST), (X2, v8(X2)[:64, :, :]), ALU.add)
        ACT(S, V(STb), V(ST), AF.Identity)
        yield


def l1_scan_dn(S, C, bi):
    S.begin_phase()
    K0 = scan_consts(S, C, "d")
    K0.ones = S.sb("dones", [128, 128], F32)
    S.op("pool", lambda e: e.memset(K0.ones.ap[:], 1.0), writes=[K0.ones])
    K0.tri = S.sb("dtri", [128, 2, 128], F32)
    for d in range(2):
        S.dma("sp", K0.tri.ap[:, d, :], C.tri[d, :, :], writes=[K0.tri])
    interleave([dn_chain(S, C, chain_res(S, K0, f"d{d}"), bi, d) for d in range(2)])
    S.end_phase()


def dn_chain(S, C, K, bi, d):
    t = f"d{d}"
    ones, tri = K.ones, K.tri
    X0, X1, X2 = [S.ps(t + f"X{n}", [128, 4, 128], F32) for n in range(3)]
    ptr = S.ps(t + "ptr", [128, 4, 128], BF16)
    ot_r = mk_ring(S, t + "ot", [128, 5, 512], BF16, 2)
    qk_r = mk_ring(S, t + "qk", [128, 2, 512], BF16, 2)
    G_r = mk_ring(S, t + "G", [128, 3, 4], F32, 2)
    FT_r = mk_ring(S, t + "FT", [128, 4, 4, 128], BF16, 2)
    gl_r = mk_ring(S, t + "gl", [128, 4, 128], F32, 1)
    ET_r = mk_ring(S, t + "ET", [128, 4, 128], F32, 1)
    qkT_r = mk_ring(S, t + "qkT", [128, 4, 128], BF16, 2)
    Zb_r = mk_ring(S, t + "Zb", [128, 4, 128], BF16, 2)
    wT_r = mk_ring(S, t + "wT", [128, 4, 128], BF16, 2)
    u0_r = mk_ring(S, t + "u0", [128, 4, 128], F32, 2)
    u_r = mk_ring(S, t + "u", [128, 4, 128], BF16, 2)
    ys_r = mk_ring(S, t + "ys", [128, 512], F32, 2)
    sm_r = mk_ring(S, t + "sm", [128, 8], F32, 2)
    ST = S.sb(t + "ST", [128, 4, 128], F32)
    STb = S.sb(t + "STb", [128, 4, 128], BF16)
    S.op("dve", lambda e: e.memset(ST.ap[:], 0.0), writes=[ST])
    S.op("dve", lambda e: e.memset(STb.ap[:], 0.0), writes=[STb])
    for i in tile_order(d):
        rws = slice(i * 128, (i + 1) * 128)
        ot, qk, G = ot_r.next(), qk_r.next(), G_r.next()
        S.dma("sp", ot.ap[:], C.dn_ops[bi, d, :, rws, :].rearrange("q t c -> t q c"), writes=[ot])
        S.dma("pool", qk.ap[:], C.dn_qk[bi, :, rws, :].rearrange("q t c -> t q c"), writes=[qk])
        S.dma("pool", G.ap[:], C.dn_G[bi, d, :, rws, :].rearrange("q t c -> t q c"), writes=[G])
        FT = FT_r.next()
        for q, src in enumerate(((qk, 1), (qk, 0), (ot, 1), (ot, 0))):
            for h in range(4):
                TR(S, (ptr, ptr.ap[:, h, :]), (src[0], src[0].ap[:, src[1], h * 128:(h + 1) * 128]), V(K.idb))
            if q % 2:
                ACT(S, (FT, FT.ap[:, q, :, :]), V(ptr), AF.Identity)
            else:
                S.op("dve", lambda e: e.tensor_copy(out=FT.ap[:, q, :, :], in_=ptr.ap[:]), reads=[ptr], writes=[FT])
            yield
        gl = gl_r.next()
        for h in range(4):
            TS(S, "pool", (gl, gl.ap[:, h, :]), (tri, tri.ap[:, d, :]), G.ap[:, 0, h:h + 1], None, ALU.mult, extra=[G])
        for h in range(4):
            MM(S, (X0, X0.ap[:, h, :]), V(ones), (gl, gl.ap[:, h, :]))
        ET = ET_r.next()
        for h in range(4):
            TS(S, "dve", (ET, ET.ap[:, h, :]), (X0, X0.ap[:, h, :]), G.ap[:, 1, h:h + 1], 0.0, ALU.subtract, ALU.min, extra=[G])
        ACT(S, V(ET), V(ET), AF.Exp)
        yield
        for h in range(4):
            MM(S, (X1, X1.ap[:, h, :]), (FT, FT.ap[:, 0, h, :]), (FT, FT.ap[:, 2, h, :]))
            MM(S, (X2, X2.ap[:, h, :]), (FT, FT.ap[:, 0, h, :]), (FT, FT.ap[:, 1, h, :]))
        P, PT = K.rP.next(), K.rPT.next()
        TT(S, "dve", V(P), V(X1), V(ET), ALU.mult)
        TT(S, "dve", V(P), V(P), V(K.mS[d]), ALU.mult)
        qkT = qkT_r.next()
        TT(S, "pool", V(ET), V(ET), V(K.mI[d]), ALU.mult)
        TT(S, "dve", V(qkT), V(X2), V(ET), ALU.mult)
        yield
        for h in range(4):
            TR(S, (ptr, ptr.ap[:, h, :]), (P, P.ap[:, h, :]), V(K.idb))
        ACT(S, V(PT), V(ptr), AF.Identity)
        yield
        zo = []
        yield from inv_group(S, P, PT, K, (X0, X1, X2), zo)
        Zb = zo[0]
        for h in range(4):
            MM(S, (X0, X0.ap[:, h, :]), (Zb, Zb.ap[:, h, :]), (ot, ot.ap[:, 4, h * 128:(h + 1) * 128]))
            MM(S, (X1, X1.ap[:, h, :]), (ot, ot.ap[:, 2, h * 128:(h + 1) * 128]), (Zb, Zb.ap[:, h, :]))
        u0, wT = u0_r.next(), wT_r.next()
        ACT(S, V(u0), V(X0), AF.Identity)
        S.op("dve", lambda e: e.tensor_copy(out=wT.ap[:], in_=X1.ap[:]), reads=[X1], writes=[wT])
        yield
        for h in range(4):
            MM(S, (X2, X2.ap[:, h, :]), (wT, wT.ap[:, h, :]), (STb, STb.ap[:, h, :]))
        u = u_r.next()
        TT(S, "dve", V(u), V(u0), V(X2), ALU.subtract)
        yield
        for h in range(4):
            MM(S, (X0, X0.ap[:, h, :]), (FT, FT.ap[:, 3, h, :]), (STb, STb.ap[:, h, :]), start=True, stop=False)
            MM(S, (X0, X0.ap[:, h, :]), (qkT, qkT.ap[:, h, :]), (u, u.ap[:, h, :]), start=False, stop=True)
        ys = ys_r.next()
        ACT(S, r3(ys, 4), V(X0), AF.Identity)
        S.dma("sp", C.y_dn[bi, d, rws, :], ys.ap[:], reads=[ys])
        for h in range(4):
            MM(S, (X1, X1.ap[:, h, :]), (ot, ot.ap[:, 3, h * 128:(h + 1) * 128]), (u, u.ap[:, h, :]))
        sm = sm_r.next()
        ACT(S, (sm, sm.ap[:, 0:4]), (G, G.ap[:, 2, :]), AF.Exp)
        TT(S, "dve", V(ST), V(ST), (sm, sm.ap[:, 0:4].unsqueeze(2).to_broadcast([128, 4, 128])), ALU.mult)
        TT(S, "dve", V(ST), V(ST), V(X1), ALU.add)
        ACT(S, V(STb), V(ST), AF.Identity)
        yield


def l1_out(S, C, bi, src_stage, dst_stage, last):
    S.begin_phase()
    wo = S.sb("xwo", [128, 8, D], BF16)
    for c in range(8):
        S.dma("sp" if c % 2 else "pool", wo.ap[:, c, :], C.wob1[c * 128:(c + 1) * 128, :], writes=[wo])
    rows = S.sb("xrows", [128, 3, 512], F32)
    S.dma("sp", rows.ap[:, 0, :], C.rw_rows[7, :].partition_broadcast(128), writes=[rows])
    S.dma("sp", rows.ap[:, 1, :], C.rw_rows[8, :].partition_broadcast(128), writes=[rows])
    S.dma("sp", rows.ap[:, 2, :], C.dn_ng[:].partition_broadcast(128), writes=[rows])
    epi = Epi(S, C, "x")
    epi.load(1, 0, bi)
    xin = mk_ring(S, "xxin", [128, 1024], F32, 2)
    ya = mk_ring(S, "xya", [128, 2, 512], F32, 2)
    yb = mk_ring(S, "xyb", [128, 2, 512], F32, 2)
    ex = mk_ring(S, "xex", [128, 3, 512], F32, 2)
    sq_r = mk_ring(S, "xsq", [128, 512], F32, 2)
    ycat = mk_ring(S, "xyc", [128, 1024], F32, 2)
    yT = mk_ring(S, "xyT", [128, 8, 128], BF16, 2)
    sm = mk_ring(S, "xsm", [128, 32], F32, 2)
    ptr = mk_ring(S, "xptr", [128, 4, 128], F32, 2, psum=True)
    py = mk_ring(S, "xpy", [128, 1024], F32, 2, psum=True)
    def out_chain(i):
        rws = slice(i * 128, (i + 1) * 128)
        xi, a, b, e_, yc, w, sq = xin.next(), ya.next(), yb.next(), ex.next(), ycat.next(), sm.next(), sq_r.next()
        S.dma("sp", xi.ap[:], tile_src(C, src_stage, bi, i), writes=[xi])
        S.dma("sp", a.ap[:], C.y_rw[bi, :, rws, :].rearrange("q t c -> t q c"), writes=[a])
        S.dma("pool", b.ap[:], C.y_dn[bi, :, rws, :].rearrange("q t c -> t q c"), writes=[b])
        S.dma("sp", e_.ap[:, 0, :], C.rw_g[bi, rws, :], writes=[e_])
        S.dma("pool", e_.ap[:, 1, :], C.rw_bonus[bi, rws, :], writes=[e_])
        S.dma("sp", e_.ap[:, 2, :], C.dn_z[bi, rws, :], writes=[e_])
        y = (a, a.ap[:, 0, :])
        TT(S, "dve", y, y, (a, a.ap[:, 1, :]), ALU.add)
        S.op("dve", lambda e: e.tensor_reduce(out=w.ap[:, 0:8], in_=r3(a, 8, slice(None), 0, slice(None))[1], axis=AX.X, op=ALU.add), reads=[a], writes=[w])
        TS(S, "dve", (w, w.ap[:, 0:8]), (w, w.ap[:, 0:8]), 1.0 / 64, None, ALU.mult)
        TT(S, "dve", r3(a, 8, slice(None), 0, slice(None)), r3(a, 8, slice(None), 0, slice(None)), bc3(w, 8, 64), ALU.subtract)
        TT(S, "pool", V(sq), y, y, ALU.mult)
        S.op("dve", lambda e: e.tensor_reduce(out=w.ap[:, 8:16], in_=r3(sq, 8)[1], axis=AX.X, op=ALU.add), reads=[sq], writes=[w])
        TS(S, "dve", (w, w.ap[:, 8:16]), (w, w.ap[:, 8:16]), 1.0 / 64, 64e-5, ALU.mult, ALU.add)
        ACT(S, (w, w.ap[:, 8:16]), (w, w.ap[:, 8:16]), AF.Sqrt)
        S.op("dve", lambda e: e.reciprocal(out=w.ap[:, 8:16], in_=w.ap[:, 8:16]), reads=[w], writes=[w])
        TT(S, "dve", r3(a, 8, slice(None), 0, slice(None)), r3(a, 8, slice(None), 0, slice(None)),
           (w, w.ap[:, 8:16].unsqueeze(2).to_broadcast([128, 8, 64])), ALU.mult)
        TT(S, "pool", y, y, (rows, rows.ap[:, 0, :]), ALU.mult)
        TT(S, "pool", y, y, (rows, rows.ap[:, 1, :]), ALU.add)
        TT(S, "dve", y, y, (e_, e_.ap[:, 1, :]), ALU.add)
        TT(S, "dve", (yc, yc.ap[:, 0:512]), y, (e_, e_.ap[:, 0, :]), ALU.mult)
        yield
        o = (b, b.ap[:, 0, :])
        TT(S, "dve", o, o, (b, b.ap[:, 1, :]), ALU.add)
        TT(S, "pool", V(sq), o, o, ALU.mult)
        S.op("dve", lambda e: e.tensor_reduce(out=w.ap[:, 16:20], in_=r3(sq, 4)[1], axis=AX.X, op=ALU.add), reads=[sq], writes=[w])
        TS(S, "dve", (w, w.ap[:, 16:20]), (w, w.ap[:, 16:20]), 1.0 / 128, 1e-6, ALU.mult, ALU.add)
        ACT(S, (w, w.ap[:, 16:20]), (w, w.ap[:, 16:20]), AF.Sqrt)
        S.op("dve", lambda e: e.reciprocal(out=w.ap[:, 16:20], in_=w.ap[:, 16:20]), reads=[w], writes=[w])
        TT(S, "dve", r3(b, 4, slice(None), 0, slice(None)), r3(b, 4, slice(None), 0, slice(None)),
           (w, w.ap[:, 16:20].unsqueeze(2).to_broadcast([128, 4, 128])), ALU.mult)
        TT(S, "pool", o, o, (rows, rows.ap[:, 2, :]), ALU.mult)
        TT(S, "dve", (yc, yc.ap[:, 512:1024]), o, (e_, e_.ap[:, 2, :]), ALU.mult)
        yield
        yt = yT.next()
        for g in range(2):
            p = ptr.next()
            for j in range(4):
                c = g * 4 + j
                S.op("pe", lambda e: e.transpose(out=p.ap[:, j, :], in_=yc.ap[:, c * 128:(c + 1) * 128], identity=C.ident.ap[:]), reads=[yc, C.ident], writes=[p])
            ACT(S, (yt, yt.ap[:, g * 4:(g + 1) * 4, :]), V(p), AF.Identity)
        yield
        p = py.next()
        for hf in range(2):
            for c in range(8):
                MM(S, (p, p.ap[:, hf * 512:(hf + 1) * 512]), (yt, yt.ap[:, c, :]), (wo, wo.ap[:, c, hf * 512:(hf + 1) * 512]), start=(c == 0), stop=(c == 7))
        if last:
            dst = C.xs[dst_stage - 1][bi, rws, :]
        else:
            dst = C.xs[dst_stage - 1][bi, rws, :]
        epi.run(p, xi, i < 2, dst)
        yield

    tl = list(range(2 if last else 0, NTT))
    for k in range(0, len(tl), 2):
        interleave([out_chain(i) for i in tl[k:k + 2]])
    S.end_phase()


NB_FULL = 4
N_CORES = 8


def build_full(NB, dbg=None):
    nc = bass.Bass("TRN2", target_bir_lowering=False)
    C = declare_common(nc, NB, dbg=dbg)
    declare_l0(nc, C, dbg=dbg)
    declare_l1(nc, C, dbg=dbg)
    S = Sched(nc)
    common_setup(S, C)
    phase_mod(S, C)
    for l in range(2):
        prep_weight(S, C.w1b[l], C.mlp_w1[l], D, DFF, tag=f"pm1{l}")
        prep_weight(S, C.w2b[l], C.mlp_w2[l], DFF, D, tag=f"pm2{l}")
    l0_setup(S, C)
    l1_setup(S, C)
    for bi in range(NB):
        l0_inproj(S, C, bi, 0)
        l0_hyena(S, C, bi)
        l0_attn(S, C, bi)
        l0_outproj(S, C, bi, 0, 1)
        phase_mlp(S, C, 0, bi, 1, 2, False)
        S.begin_phase()
        hT = build_hT(S, C, bi, 2, 1)
        l1_feat_rw(S, C, bi, hT)
        l1_feat_dn(S, C, bi, hT)
        S.end_phase()
        l1_scan_rw(S, C, bi)
        l1_scan_dn(S, C, bi)
        l1_out(S, C, bi, 2, 3, True)
        phase_mlp(S, C, 1, bi, 3, None, True)
    S.finish()
    return nc


def kernel(**inputs):
    NB = NB_FULL
    nc = build_full(NB)
    in_maps = [host_l1(inputs, host_l0(inputs, host_common(inputs, c, NB))) for c in range(N_CORES)]
    res = run_bass_kernel_spmd(nc, in_maps, core_ids=list(range(N_CORES)))
    out = np.concatenate([np.asarray(r["out"], dtype=np.float32) for r in res.results], axis=0)
    return out
```

```python
import numpy as np
from contextlib import ExitStack
import concourse.bass as bass
import concourse.mybir as mybir
from concourse.bass_utils import run_bass_kernel_spmd

F32 = mybir.dt.float32
BF16 = mybir.dt.bfloat16
AF = mybir.ActivationFunctionType
ALU = mybir.AluOpType
AX = mybir.AxisListType

SAME_ENGINE_SYNC = True
NO_SWDGE = True
POOL_TO_DVE = True
EPOCH = 30000


class Buf:
    __slots__ = ("ap", "name", "lw", "rd")

    def __init__(self, ap, name):
        self.ap = ap
        self.name = name
        self.lw = None
        self.rd = []


class Sched:
    def __init__(self, nc, ndma=24):
        self.nc = nc
        self.stack = ExitStack()
        self.engs = {"pe": nc.tensor, "act": nc.scalar, "dve": nc.vector, "pool": nc.gpsimd, "sp": nc.sync}
        self.csem = {}
        self.ccnt = {}
        self.nsem = 0
        for e in ("pe", "act", "dve", "pool"):
            self._new_csem(e)
        self.dq = {}
        for q in ("sp", "pool", "act"):
            self.dq[q] = [[self._sem(f"d_{q}{i}"), 0] for i in range(ndma)]
        self.dqi = {q: 0 for q in self.dq}
        self.waited = {}
        self.phase_stack = None
        self.ninst = 0

    def _sem(self, name):
        self.nsem += 1
        return self.stack.enter_context(self.nc.semaphore(name))

    def _new_csem(self, e):
        self.csem[e] = self._sem(f"c_{e}_{self.nsem}")
        self.ccnt[e] = 0

    def sb(self, name, shape, dtype, persist=False):
        st = self.stack if (persist or self.phase_stack is None) else self.phase_stack
        self.nsem += 0
        self.uid = getattr(self, "uid", 0) + 1
        t = st.enter_context(self.nc.sbuf_tensor(f"{name}_{self.uid}", list(shape), dtype))
        return Buf(t, name)

    def ps(self, name, shape, dtype, persist=False):
        st = self.stack if (persist or self.phase_stack is None) else self.phase_stack
        self.uid = getattr(self, "uid", 0) + 1
        t = st.enter_context(self.nc.psum_tensor(f"{name}_{self.uid}", list(shape), dtype))
        return Buf(t, name)

    def view(self, ap, name="v"):
        return Buf(ap, name)

    def begin_sub(self):
        if not hasattr(self, "sub_stk"):
            self.sub_stk = []
        self.sub_stk.append(self.phase_stack)
        self.phase_stack = ExitStack()

    def end_sub(self):
        self.barrier()
        self.phase_stack.close()
        self.phase_stack = self.sub_stk.pop()

    def begin_phase(self):
        assert self.phase_stack is None
        self.phase_stack = ExitStack()

    def end_phase(self):
        self.barrier()
        self.phase_stack.close()
        self.phase_stack = None

    def _wait(self, F, tok):
        sem, val, eng = tok
        if eng == F == "pe":
            return
        if eng == F and not SAME_ENGINE_SYNC:
            return
        key = (F, id(sem))
        if self.waited.get(key, 0) >= val:
            return
        self.engs[F].wait_ge(sem, val)
        self.waited[key] = val
        self.ninst += 1

    def _deps(self, F, reads, writes):
        for b in reads:
            if b.lw is not None:
                self._wait(F, b.lw)
        for b in writes:
            if b.lw is not None:
                self._wait(F, b.lw)
            for t in b.rd:
                self._wait(F, t)

    def _commit(self, tok, reads, writes):
        for b in reads:
            if tok[2] == "dma":
                b.rd.append(tok)
            else:
                b.rd = [t for t in b.rd if t[2] != tok[2]]
                b.rd.append(tok)
        for b in writes:
            b.lw = tok
            b.rd = []

    def op(self, F, fn, reads=(), writes=()):
        if F == "pool" and POOL_TO_DVE and not getattr(self, "keep_pool", False):
            F = "dve"
        self._deps(F, reads, writes)
        if self.ccnt[F] >= EPOCH:
            self._new_csem(F)
        inst = fn(self.engs[F])
        self.ccnt[F] += 1
        inst.then_inc(self.csem[F], 1)
        tok = (self.csem[F], self.ccnt[F], F)
        self._commit(tok, reads, writes)
        self.ninst += 1
        return tok

    def dma(self, Q, out, in_, reads=(), writes=(), **kw):
        if NO_SWDGE:
            Q = "sp"
        self._deps(Q, reads, writes)
        pool = self.dq[Q]
        i = self.dqi[Q]
        self.dqi[Q] = (i + 1) % len(pool)
        sem, val = pool[i]
        if val > 0:
            self._wait(Q, (sem, val, "dma"))
        inst = self.engs[Q].dma_start(out=out, in_=in_, **kw)
        inst.then_inc(sem, 16)
        pool[i][1] = val + 16
        tok = (sem, val + 16, "dma")
        self._commit(tok, reads, writes)
        self.ninst += 1
        return tok

    def barrier(self, engines=("pe", "act", "dve", "pool", "sp")):
        toks = []
        for e in ("pe", "act", "dve", "pool"):
            if self.ccnt[e] > 0:
                toks.append((self.csem[e], self.ccnt[e], "bar"))
        for q in self.dq:
            for sem, val in self.dq[q]:
                if val > 0:
                    toks.append((sem, val, "dma"))
        for F in engines:
            for t in toks:
                self._wait(F, t)

    def finish(self):
        self.barrier()
        self.stack.close()


D = 1024
LAT = 2048
CTX = 256
T = LAT + CTX
NTT = T // 128
DFF = 4096
ALPHA = (2.0 * 2) ** 0.25
LN_EPS = 1e-5


class Ring:
    def __init__(self, bufs):
        self.bufs = bufs
        self.i = 0

    def next(self):
        b = self.bufs[self.i]
        self.i = (self.i + 1) % len(self.bufs)
        return b


def mk_ring(S, name, shape, dtype, n=2, psum=False):
    f = S.ps if psum else S.sb
    return Ring([f(f"{name}{i}", shape, dtype) for i in range(n)])


class Ctx:
    pass


def prep_weight(S, dst, src, K, N, scale=None, tag="pw"):
    S.begin_phase()
    CB = min(N, 2048)
    rin = mk_ring(S, tag + "i", [128, CB], F32, 2)
    rout = mk_ring(S, tag + "o", [128, CB], BF16, 2)
    sc = S.sb(tag + "s", [128, CB], F32) if scale is not None else None
    for c0 in range(0, N, CB):
        cw = min(CB, N - c0)
        if scale is not None:
            S.dma("sp", sc.ap[:, :cw], scale[c0:c0 + cw].partition_broadcast(128), writes=[sc])
        for k0 in range(0, K, 128):
            kw = min(128, K - k0)
            a = rin.next()
            o = rout.next()
            S.dma("sp", a.ap[:kw, :cw], src[k0:k0 + kw, c0:c0 + cw], writes=[a])
            if scale is not None:
                S.op("dve", lambda e: e.tensor_tensor(out=o.ap[:kw, :cw], in0=a.ap[:kw, :cw], in1=sc.ap[:kw, :cw], op=ALU.mult),
                     reads=[a, sc], writes=[o])
            else:
                S.op("dve", lambda e: e.tensor_copy(out=o.ap[:kw, :cw], in_=a.ap[:kw, :cw]), reads=[a], writes=[o])
            S.dma("pool", dst[k0:k0 + kw, c0:c0 + cw], o.ap[:kw, :cw], reads=[o])
    S.end_phase()


def phase_mod(S, C):
    nc, R = C.nc, C.R
    S.begin_phase()
    cT = S.sb("cT", [128, 8, R], F32)
    S.dma("sp", cT.ap[:], C.cT[:, :, :], writes=[cT])
    sig = S.sb("sig", [128, 8, R], F32)
    scT = S.sb("scT", [128, 8, R], BF16)
    S.op("act", lambda e: e.activation(out=sig.ap[:], in_=cT.ap[:], func=AF.Sigmoid), reads=[cT], writes=[sig])
    S.op("dve", lambda e: e.tensor_tensor(out=scT.ap[:], in0=cT.ap[:], in1=sig.ap[:], op=ALU.mult), reads=[cT, sig], writes=[scT])
    scbc = S.sb("scbc", [128, 8, R, 128], BF16)
    for kc in range(8):
        for r in range(R):
            S.op("dve", lambda e: e.tensor_copy(out=scbc.ap[:, kc, r, :], in_=scT.ap[:, kc, r:r + 1].to_broadcast([128, 128])),
                 reads=[scT], writes=[scbc])
    mb = S.sb("mb", [128, 2, 48], F32)
    S.dma("sp", mb.ap[:], C.mod_bT[:, :, :], writes=[mb])
    wst = mk_ring(S, "mws", [128, 3072], F32, 2)
    wbf = S.sb("mwbf", [128, 8, 6144], BF16)
    pacc = mk_ring(S, "mps", [128, 512], F32, 2, psum=True)
    gt = mk_ring(S, "mgt", [128, 512], F32, 2)
    mbr = S.sb("mbr", [128, 2048], F32)
    for l in range(2):
        for kc in range(8):
            for hf in range(2):
                a = wst.next()
                S.dma("sp" if hf == 0 else "pool", a.ap[:], C.mod_w[l, kc * 128:(kc + 1) * 128, hf * 3072:(hf + 1) * 3072], writes=[a])
                S.op("dve" if hf == 0 else "act",
                     (lambda e: e.tensor_copy(out=wbf.ap[:, kc, hf * 3072:(hf + 1) * 3072], in_=a.ap[:])) if hf == 0 else
                     (lambda e: e.activation(out=wbf.ap[:, kc, hf * 3072:(hf + 1) * 3072], in_=a.ap[:], func=AF.Identity)),
                     reads=[a], writes=[wbf])
        for fc in range(48):
            p = pacc.next()
            for kc in range(8):
                S.op("pe", lambda e: e.matmul(p.ap[:, :R], wbf.ap[:, kc, fc * 128:(fc + 1) * 128], scT.ap[:, kc, :],
                                              start=(kc == 0), stop=(kc == 7)), reads=[wbf, scT], writes=[p])
            is_scale = (fc // 8) in (1, 4)
            S.op("dve", lambda e: e.tensor_scalar(out=C.modT.ap[:, l, fc, :], in0=p.ap[:, :R], scalar1=mb.ap[:, l, fc:fc + 1],
                                                  scalar2=(1.0 if is_scale else 0.0), op0=ALU.add, op1=ALU.add),
                 reads=[p, mb], writes=[C.modT])
        for gi, c0 in enumerate((2048, 5120)):
            S.dma("sp", mbr.ap[:, gi * 1024:(gi + 1) * 1024], C.mod_b[l, c0:c0 + 1024].partition_broadcast(128), writes=[mbr])
        for gi, c0 in enumerate((2048, 5120)):
            for r in range(R):
                for hf in range(2):
                    p = pacc.next()
                    for kc in range(8):
                        S.op("pe", lambda e: e.matmul(p.ap[:], scbc.ap[:, kc, r, :], wbf.ap[:, kc, c0 + hf * 512:c0 + (hf + 1) * 512],
                                                      start=(kc == 0), stop=(kc == 7)), reads=[scbc, wbf], writes=[p])
                    g = gt.next()
                    S.op("dve", lambda e: e.tensor_tensor(out=g.ap[:], in0=p.ap[:], in1=mbr.ap[:, gi * 1024 + hf * 512:gi * 1024 + (hf + 1) * 512], op=ALU.add),
                         reads=[p, mbr], writes=[g])
                    S.dma("pool", C.gbc[l, gi, r, :, hf * 512:(hf + 1) * 512], g.ap[:], reads=[g])
    S.end_phase()


def tile_src(C, stage, bi, i):
    if stage == 0:
        if i < 2:
            return C.ctx[bi, i * 128:(i + 1) * 128, :]
        return C.x[bi, (i - 2) * 128:(i - 1) * 128, :]
    return C.xs[stage - 1][bi, i * 128:(i + 1) * 128, :]


def rstd_op(S, mv, o, i, eps):
    S.op("dve", lambda e: e.tensor_scalar(out=mv.ap[:, o:o + 1], in0=mv.ap[:, i:i + 1], scalar1=eps, scalar2=None, op0=ALU.add), reads=[mv], writes=[mv])
    S.op("act", lambda e: e.activation(out=mv.ap[:, o:o + 1], in_=mv.ap[:, o:o + 1], func=AF.Sqrt), reads=[mv], writes=[mv])
    S.op("dve", lambda e: e.reciprocal(out=mv.ap[:, o:o + 1], in_=mv.ap[:, o:o + 1]), reads=[mv], writes=[mv])


class Epi:
    def __init__(self, S, C, tag):
        self.S, self.C = S, C
        self.gb = [S.sb(tag + "gb0", [128, 1024], F32), S.sb(tag + "gb1", [128, 1024], F32)]
        self.lg = S.sb(tag + "lg", [128, 1024], F32)
        self.lb = S.sb(tag + "lb", [128, 1024], F32)
        self.t1 = mk_ring(S, tag + "t1", [128, 1024], F32, 2)
        self.xo = mk_ring(S, tag + "xo", [128, 1024], F32, 2)
        self.st = mk_ring(S, tag + "st", [128, 2, 6], F32, 2)
        self.mv = mk_ring(S, tag + "mv", [128, 4], F32, 2)

    def load(self, l, sub, bi):
        S, C = self.S, self.C
        S.dma("sp", self.gb[0].ap[:], C.gbc[l, sub, bi, :, :], writes=[self.gb[0]])
        S.dma("sp", self.gb[1].ap[:], C.gbc[l, sub, C.R - 1, :, :], writes=[self.gb[1]])
        S.dma("sp", self.lg.ap[:], C.ln_g[l, sub, :].partition_broadcast(128), writes=[self.lg])
        S.dma("sp", self.lb.ap[:], C.ln_b[l, sub, :].partition_broadcast(128), writes=[self.lb])

    def run(self, y, xin, is_ctx, dst):
        S = self.S
        gb = self.gb[1 if is_ctx else 0]
        t1, xo, st, mv = self.t1.next(), self.xo.next(), self.st.next(), self.mv.next()
        S.op("dve", lambda e: e.tensor_tensor(out=t1.ap[:], in0=y.ap[:], in1=gb.ap[:], op=ALU.mult), reads=[y, gb], writes=[t1])
        S.op("dve", lambda e: e.scalar_tensor_tensor(out=t1.ap[:], in0=xin.ap[:], scalar=ALPHA, in1=t1.ap[:], op0=ALU.mult, op1=ALU.add),
             reads=[xin, t1], writes=[t1])
        for h in range(2):
            S.op("dve", lambda e: e.bn_stats(out=st.ap[:, h, :], in_=t1.ap[:, h * 512:(h + 1) * 512]), reads=[t1], writes=[st])
        S.op("dve", lambda e: e.bn_aggr(out=mv.ap[:, 0:2], in_=st.ap[:]), reads=[st], writes=[mv])
        rstd_op(S, mv, 2, 1, LN_EPS)
        S.op("dve", lambda e: e.tensor_scalar(out=mv.ap[:, 3:4], in0=mv.ap[:, 0:1], scalar1=mv.ap[:, 2:3], scalar2=-1.0, op0=ALU.mult, op1=ALU.mult),
             reads=[mv], writes=[mv])
        S.op("act", lambda e: e.activation(out=xo.ap[:], in_=t1.ap[:], func=AF.Identity, scale=mv.ap[:, 2:3], bias=mv.ap[:, 3:4]),
             reads=[t1, mv], writes=[xo])
        S.op("pool", lambda e: e.tensor_tensor(out=xo.ap[:], in0=xo.ap[:], in1=self.lg.ap[:], op=ALU.mult), reads=[xo, self.lg], writes=[xo])
        S.op("pool", lambda e: e.tensor_tensor(out=xo.ap[:], in0=xo.ap[:], in1=self.lb.ap[:], op=ALU.add), reads=[xo, self.lb], writes=[xo])
        S.dma("pool", dst, xo.ap[:], reads=[xo])


def transpose_mod(S, C, xin, hT, col0, l, fc0, r, ptr, ident):
    for g in range(2):
        p = ptr.next()
        for j in range(4):
            kc = g * 4 + j
            S.op("pe", lambda e: e.transpose(out=p.ap[:, j * 128:(j + 1) * 128], in_=xin.ap[:, kc * 128:(kc + 1) * 128], identity=ident.ap[:]),
                 reads=[xin, ident], writes=[p])
        for j in range(4):
            kc = g * 4 + j
            S.op("act", lambda e: e.activation(out=hT.ap[:, kc, col0:col0 + 128], in_=p.ap[:, j * 128:(j + 1) * 128], func=AF.Identity,
                                               scale=C.modT.ap[:, l, fc0 + 8 + kc, r:r + 1], bias=C.modT.ap[:, l, fc0 + kc, r:r + 1]),
                 reads=[p, C.modT], writes=[hT])


def phase_mlp(S, C, l, bi, src_stage, dst_stage, last):
    S.begin_phase()
    ident = C.ident
    w1 = S.sb("w1", [128, 8, DFF], BF16)
    w2 = S.sb("w2", [128, 32, D], BF16)
    for kc in range(8):
        S.dma("sp" if kc % 2 == 0 else "pool", w1.ap[:, kc, :], C.w1b[l][kc * 128:(kc + 1) * 128, :], writes=[w1])
    for ko in range(32):
        S.dma("sp" if ko % 2 == 0 else "pool", w2.ap[:, ko, :], C.w2b[l][ko * 128:(ko + 1) * 128, :], writes=[w2])
    epi = Epi(S, C, "m")
    epi.load(l, 1, bi)
    xin = mk_ring(S, "mxin", [128, 1024], F32, 4)
    hT = mk_ring(S, "mhT", [128, 8, 256], BF16, 2)
    aT = S.sb("maT", [128, 32, 256], BF16)
    rl = mk_ring(S, "mrl", [128, 256], BF16, 2)
    ptr = mk_ring(S, "mptr", [128, 512], F32, 2, psum=True)
    pup = mk_ring(S, "mpup", [128, 256], F32, 2, psum=True)
    pdn = mk_ring(S, "mpdn", [128, 1024], F32, 2, psum=True)
    t0 = 1 if last else 0
    for tt in range(t0, 9):
        is_ctx = tt == 0
        r = C.R - 1 if is_ctx else bi
        xs_ = []
        h = hT.next()
        for s in range(2):
            xi = xin.next()
            S.dma("sp", xi.ap[:], tile_src(C, src_stage, bi, tt * 2 + s), writes=[xi])
            transpose_mod(S, C, xi, h, s * 128, l, 24, r, ptr, ident)
            xs_.append(xi)
        for fo in range(32):
            p = pup.next()
            for kc in range(8):
                S.op("pe", lambda e: e.matmul(p.ap[:], w1.ap[:, kc, fo * 128:(fo + 1) * 128], h.ap[:, kc, :], start=(kc == 0), stop=(kc == 7)),
                     reads=[w1, h], writes=[p])
            rr = rl.next()
            S.op("act", lambda e: e.activation(out=rr.ap[:], in_=p.ap[:], func=AF.Relu), reads=[p], writes=[rr])
            S.op("pool", lambda e: e.tensor_tensor(out=aT.ap[:, fo, :], in0=rr.ap[:], in1=rr.ap[:], op=ALU.mult), reads=[rr], writes=[aT])
        for s in range(2):
            p = pdn.next()
            for hf in range(2):
                for ko in range(32):
                    S.op("pe", lambda e: e.matmul(p.ap[:, hf * 512:(hf + 1) * 512], aT.ap[:, ko, s * 128:(s + 1) * 128], w2.ap[:, ko, hf * 512:(hf + 1) * 512],
                                                  start=(ko == 0), stop=(ko == 31)), reads=[aT, w2], writes=[p])
            i = tt * 2 + s
            if last:
                dst = C.out[bi, (i - 2) * 128:(i - 1) * 128, :]
            else:
                dst = C.xs[dst_stage - 1][bi, i * 128:(i + 1) * 128, :]
            epi.run(p, xs_[s], is_ctx, dst)
    S.end_phase()


def dram(nc, name, shape, dtype, kind=None):
    if kind is None:
        return nc.dram_tensor(name, list(shape), dtype).ap()
    return nc.dram_tensor(name, list(shape), dtype, kind=kind).ap()


def declare_common(nc, NB, dbg=None):
    C = Ctx()
    C.nc, C.NB, C.R = nc, NB, NB + 1
    R = C.R
    I = lambda n, s: dram(nc, n, s, F32, "ExternalInput")
    C.x = I("x", [NB, LAT, D])
    C.ctx = I("ctx", [NB, CTX, D])
    C.cT = I("cT", [128, 8, R])
    C.mod_w = I("mod_w", [2, D, 6 * D])
    C.mod_b = I("mod_b", [2, 6 * D])
    C.mod_bT = I("mod_bT", [128, 2, 48])
    C.ln_g = I("ln_g", [2, 2, D])
    C.ln_b = I("ln_b", [2, 2, D])
    C.mlp_w1 = I("mlp_w1", [2, D, DFF])
    C.mlp_w2 = I("mlp_w2", [2, DFF, D])
    C.ident_d = I("ident", [128, 128])
    C.out = dram(nc, "out", [NB, LAT, D], F32, "ExternalOutput")
    C.gbc = dram(nc, "gbc", [2, 2, R, 128, D], F32)
    C.w1b = [dram(nc, f"w1b{l}", [D, DFF], BF16) for l in range(2)]
    C.w2b = [dram(nc, f"w2b{l}", [DFF, D], BF16) for l in range(2)]
    nst = 3
    C.xs = [dram(nc, f"xs{i}", [NB, T, D], F32, "ExternalOutput" if (dbg and f"xs{i}" in dbg) else None) for i in range(nst)]
    return C


def common_setup(S, C):
    C.modT = S.sb("modT", [128, 2, 48, C.R], F32, persist=True)
    C.ident = S.sb("identsb", [128, 128], F32, persist=True)
    S.dma("sp", C.ident.ap[:], C.ident_d[:, :], writes=[C.ident])


def host_common(inputs, core, NB):
    b0 = core * NB
    f = lambda a: np.ascontiguousarray(np.asarray(a, dtype=np.float32))
    cs = np.concatenate([np.asarray(inputs["c"])[b0:b0 + NB], np.asarray(inputs["c_ctx"])[None, :]], 0)
    m = {
        "x": f(np.asarray(inputs["x"])[b0:b0 + NB]),
        "ctx": f(np.asarray(inputs["ctx"])[b0:b0 + NB]),
        "cT": f(cs.reshape(NB + 1, 8, 128).transpose(2, 1, 0)),
        "mod_w": f(inputs["mod_w"]),
        "mod_b": f(inputs["mod_b"]),
        "mod_bT": f(np.asarray(inputs["mod_b"]).reshape(2, 48, 128).transpose(2, 0, 1)),
        "ln_g": f(inputs["ln_g"]),
        "ln_b": f(inputs["ln_b"]),
        "mlp_w1": f(inputs["mlp_w1"]),
        "mlp_w2": f(inputs["mlp_w2"]),
        "ident": np.eye(128, dtype=np.float32),
    }
    return m


HYW = 512
PI = float(np.pi)


def colof(i):
    return 1 + 128 * i if i < 2 else 259 + 128 * (i - 2)


def declare_l0(nc, C, dbg=None):
    NB = C.NB
    I = lambda n, s, dt=F32: dram(nc, n, s, dt, "ExternalInput")
    Sx = lambda n, s, dt=BF16: dram(nc, n, s, dt, "ExternalOutput" if (dbg and n in dbg) else None)
    C.e_w_hy = I("e_w_hy", [D, 1536])
    C.e_w_qkv = I("e_w_qkv", [D, 768 + 640])
    C.e_w_out = I("e_w_out", [D, D])
    C.hy_conv = I("hy_conv", [3, 1536])
    C.hy_w1 = I("hy_w1", [33, 64])
    C.hy_w2 = I("hy_w2", [64, 64])
    C.hy_w3 = I("hy_w3", [64, 1024])
    C.hy_vec = I("hy_vec", [64, 3])
    C.hy_decay = I("hy_decay", [1024])
    C.hy_bias = I("hy_bias", [512])
    C.attn_sink = I("attn_sink", [8])
    C.peT = [I("peT_l", [33, LAT]), I("peT_c", [33, CTX])]
    C.negtn = [I("negtn_l", [128, LAT // 128]), I("negtn_c", [128, CTX // 128])]
    C.fwd = [I("fwd_l", [16, 2, 128, 16, 128], BF16), I("fwd_c", [2, 2, 128, 2, 128], BF16)]
    C.inv = [I("inv_l", [4, 128, 16, 2, 512], BF16), I("inv_c", [1, 128, 2, 2, 256], BF16)]
    C.rope = I("rope", [64, 2, LAT])
    C.amask = I("amask", [128, 384])
    C.ident_b = I("ident_b", [128, 128], BF16)
    C.whb = [Sx(f"whb{j}", [D, 1536]) for j in range(3)]
    C.wqb = Sx("wqb", [D, 1408])
    C.wob0 = Sx("wob0", [D, D])
    C.kspec = [Sx("kspec_l", [LAT, 2, 512], F32), Sx("kspec_c", [CTX, 2, 512], F32)]
    C.filt = [Sx("filt_l", [LAT, 2, 512]), Sx("filt_c", [CTX, 2, 512])]
    C.u = Sx("u_s", [NB, T, 512])
    C.x0T = Sx("x0T_s", [NB, 512, T])
    C.qT = Sx("qT_s", [NB, 8, 64, T])
    C.kT = Sx("kT_s", [NB, 2, 64, T])
    C.v = Sx("v_s", [NB, T, 128])
    C.yaT = Sx("yaT_s", [NB, 512, T])
    C.ybT = Sx("ybT_s", [NB, 8, 64, T])


def host_l0(inputs, m):
    f = lambda a: np.ascontiguousarray(np.asarray(a, dtype=np.float32))
    import ml_dtypes
    bf = lambda a: np.ascontiguousarray(np.asarray(a, dtype=np.float32).astype(ml_dtypes.bfloat16))
    w = np.asarray(inputs["e_w_in"])[0]
    d = np.arange(64)
    partner = np.where((d % 32) < 16, d + 16, d - 16)
    qcols = 1536 + (np.arange(8)[:, None] * 64 + partner[None, :]).reshape(-1)
    kcols = 2048 + (np.arange(2)[:, None] * 64 + partner[None, :]).reshape(-1)
    m["e_w_hy"] = f(w[:, :1536])
    m["e_w_qkv"] = f(np.concatenate([w[:, 1536:2304], w[:, qcols], w[:, kcols]], 1))
    m["e_w_out"] = f(np.asarray(inputs["e_w_out"])[0])
    m["hy_conv"] = f(np.asarray(inputs["hy_conv"])[0])
    m["hy_w1"] = f(np.asarray(inputs["hy_ffn_w1"])[0])
    m["hy_w2"] = f(np.asarray(inputs["hy_ffn_w2"])[0])
    m["hy_w3"] = f(np.asarray(inputs["hy_ffn_w3"])[0])
    m["hy_vec"] = f(np.stack([np.asarray(inputs["hy_ffn_b1"])[0], np.asarray(inputs["hy_ffn_b2"])[0], np.asarray(inputs["hy_sin_freq"])[0]], 1))
    m["hy_decay"] = f(np.asarray(inputs["hy_decay"])[0])
    m["hy_bias"] = f(np.asarray(inputs["hy_bias"])[0])
    m["attn_sink"] = f(np.asarray(inputs["attn_sink"])[0])
    for tag, Lf in (("l", LAT), ("c", CTX)):
        t = np.arange(Lf, dtype=np.float32)
        t_norm = t / np.float32(max(Lf - 1, 1))
        bands = np.linspace(1e-4, 15, 16, dtype=np.float32)
        ang = (2.0 * np.pi * t[:, None] * bands[None, :] / Lf).astype(np.float32)
        pe = np.concatenate([t_norm[:, None], np.cos(ang), -np.sin(ang)], -1).astype(np.float32)
        m["peT_" + tag] = f(pe.T)
        m["negtn_" + tag] = f((-t_norm).reshape(Lf // 128, 128).T)
        N = 2 * Lf
        nt = Lf // 128
        tt = np.arange(Lf, dtype=np.float64)
        ff = np.arange(Lf, dtype=np.float64) + 0.5
        th = 2.0 * np.pi * np.outer(tt, ff) / N
        Cm, Sm = np.cos(th), np.sin(th)
        fw = np.stack([Cm, Sm], 0).reshape(2, nt, 128, nt, 128)
        m["fwd_" + tag] = bf(fw.transpose(3, 0, 2, 1, 4))
        tw = min(512, Lf)
        iv = np.stack([Cm.T, -Sm.T], 0).reshape(2, nt, 128, Lf // tw, tw)
        m["inv_" + tag] = bf(iv.transpose(3, 2, 1, 0, 4))
    pos = np.arange(LAT)
    inv_freq = (10000.0 ** (-np.arange(16, dtype=np.float32) / 16)).astype(np.float32)
    P = np.where(d[:, None] < 32, (pos // 64)[None, :], (pos % 64)[None, :]).astype(np.float32)
    ang = (P * inv_freq[d % 16][:, None]).astype(np.float32)
    sgn = np.where((d % 32) < 16, -1.0, 1.0)[:, None]
    m["rope"] = f(np.stack([np.cos(ang), sgn * np.sin(ang)], 1))
    qi = np.arange(128)[:, None]
    kj = np.arange(384)[None, :] - 128
    m["amask"] = f(np.where(np.abs(qi - kj) <= 128, 0.0, -30000.0))
    m["ident_b"] = bf(np.eye(128))
    return m


def l0_setup(S, C):
    for j in range(3):
        prep_weight(S, C.whb[j], C.e_w_hy, D, 1536, scale=C.hy_conv[j, :], tag=f"ph{j}")
    prep_weight(S, C.wqb, C.e_w_qkv, D, 1408, tag="pq")
    prep_weight(S, C.wob0, C.e_w_out, D, D, tag="po")
    for si, Lf in enumerate((LAT, CTX)):
        hyena_filter(S, C, si, Lf)
        hyena_fwd(S, C, si, Lf, C.filt[si], None, C.kspec[si], is_filter=True)


def hyena_filter(S, C, si, Lf):
    S.begin_phase()
    w1 = S.sb("hw1", [33, 64], F32)
    w2 = S.sb("hw2", [64, 64], F32)
    w3 = S.sb("hw3", [64, 1024], F32)
    vec = S.sb("hvec", [64, 3], F32)
    peT = S.sb("hpe", [33, Lf], F32)
    ntn = S.sb("hntn", [128, Lf // 128], F32)
    dec = S.sb("hdec", [128, 1024], F32)
    for dst, src in ((w1, C.hy_w1), (w2, C.hy_w2), (w3, C.hy_w3), (vec, C.hy_vec), (peT, C.peT[si]), (ntn, C.negtn[si])):
        S.dma("sp", dst.ap[:], src, writes=[dst])
    S.dma("sp", dec.ap[:], C.hy_decay[:].partition_broadcast(128), writes=[dec])
    S.op("dve", lambda e: e.scalar_tensor_tensor(out=dec.ap[:], in0=dec.ap[:], scalar=-1.0, in1=dec.ap[:], op0=ALU.mult, op1=ALU.max), reads=[dec], writes=[dec])
    h1 = S.sb("hh1", [64, Lf], F32)
    h2 = S.sb("hh2", [64, Lf], F32)
    tmp = S.sb("htmp", [64, 512], F32)
    S.sin_ki = S.sb("hki", [64, 512], mybir.dt.int32)
    S.sin_kf = S.sb("hkf", [64, 512], F32)
    pp = mk_ring(S, "hpp", [128, 512], F32, 2, psum=True)
    W = min(512, Lf)
    for c0 in range(0, Lf, W):
        p = pp.next()
        S.op("pe", lambda e: e.matmul(p.ap[:64, :W], w1.ap[:], peT.ap[:, c0:c0 + W], start=True, stop=True), reads=[w1, peT], writes=[p])
        S.op("dve", lambda e: e.tensor_scalar(out=tmp.ap[:, :W], in0=p.ap[:64, :W], scalar1=vec.ap[:, 0:1], scalar2=vec.ap[:, 2:3], op0=ALU.add, op1=ALU.mult),
             reads=[p, vec], writes=[tmp])
        sin_tail(S, h1, c0, W, tmp)
    for c0 in range(0, Lf, W):
        p = pp.next()
        S.op("pe", lambda e: e.matmul(p.ap[:64, :W], w2.ap[:], h1.ap[:, c0:c0 + W], start=True, stop=True), reads=[w2, h1], writes=[p])
        S.op("dve", lambda e: e.tensor_scalar(out=tmp.ap[:, :W], in0=p.ap[:64, :W], scalar1=vec.ap[:, 1:2], scalar2=vec.ap[:, 2:3], op0=ALU.add, op1=ALU.mult),
             reads=[p, vec], writes=[tmp])
        sin_tail(S, h2, c0, W, tmp)
    ex = mk_ring(S, "hex", [128, 1024], F32, 2)
    fo = mk_ring(S, "hfo", [128, 2, 512], BF16, 2)
    for tc in range(Lf // 128):
        e_ = ex.next()
        S.op("act", lambda e: e.activation(out=e_.ap[:], in_=dec.ap[:], func=AF.Exp, scale=ntn.ap[:, tc:tc + 1]), reads=[dec, ntn], writes=[e_])
        for hf in range(2):
            p = pp.next()
            S.op("pe", lambda e: e.matmul(p.ap[:], h2.ap[:, tc * 128:(tc + 1) * 128], w3.ap[:, hf * 512:(hf + 1) * 512], start=True, stop=True),
                 reads=[h2, w3], writes=[p])
            S.op("dve", lambda e: e.tensor_tensor(out=e_.ap[:, hf * 512:(hf + 1) * 512], in0=p.ap[:], in1=e_.ap[:, hf * 512:(hf + 1) * 512], op=ALU.mult),
                 reads=[p, e_], writes=[e_])
        if tc == 0:
            S.op("dve", lambda e: e.memset(e_.ap[0:1, 512:1024], 0.0), reads=[], writes=[e_])
        o = fo.next()
        S.op("dve", lambda e: e.tensor_tensor(out=o.ap[:, 0, :], in0=e_.ap[:, 512:1024], in1=e_.ap[:, 0:512], op=ALU.add), reads=[e_], writes=[o])
        S.op("pool", lambda e: e.tensor_tensor(out=o.ap[:, 1, :], in0=e_.ap[:, 512:1024], in1=e_.ap[:, 0:512], op=ALU.subtract), reads=[e_], writes=[o])
        S.dma("sp", C.filt[si][tc * 128:(tc + 1) * 128, :, :], o.ap[:], reads=[o])
    S.end_phase()


def sin_tail(S, dst, c0, W, tmp):
    ki, kf = S.sin_ki, S.sin_kf
    S.op("dve", lambda e: e.tensor_scalar(out=tmp.ap[:, :W], in0=tmp.ap[:, :W], scalar1=1.0 / (2.0 * PI), scalar2=16.5, op0=ALU.mult, op1=ALU.add),
         reads=[tmp], writes=[tmp])
    S.op("dve", lambda e: e.tensor_copy(out=ki.ap[:, :W], in_=tmp.ap[:, :W]), reads=[tmp], writes=[ki])
    S.op("dve", lambda e: e.tensor_copy(out=kf.ap[:, :W], in_=ki.ap[:, :W]), reads=[ki], writes=[kf])
    S.op("dve", lambda e: e.scalar_tensor_tensor(out=tmp.ap[:, :W], in0=tmp.ap[:, :W], scalar=-0.5, in1=kf.ap[:, :W], op0=ALU.add, op1=ALU.subtract),
         reads=[tmp, kf], writes=[tmp])
    S.op("dve", lambda e: e.scalar_tensor_tensor(out=tmp.ap[:, :W], in0=tmp.ap[:, :W], scalar=-0.5, in1=tmp.ap[:, :W], op0=ALU.is_lt, op1=ALU.add),
         reads=[tmp], writes=[tmp])
    S.op("act", lambda e: e.activation(out=dst.ap[:, c0:c0 + W], in_=tmp.ap[:, :W], func=AF.Sin, scale=2.0 * PI * 0.999999), reads=[tmp], writes=[dst])


def hyena_fwd(S, C, si, Lf, src, bi, dst, is_filter):
    nt = Lf // 128
    N = 2 * Lf
    if is_filter:
        S.begin_phase()
    a_in = S.sb("fa", [128, nt, 2 if is_filter else 1, 512], BF16)
    if is_filter:
        S.dma("sp", a_in.ap[:], src.rearrange("(tc p) s c -> p tc s c", p=128), writes=[a_in])
        bb = S.sb("fbias", [128, 512], F32)
        S.dma("sp", bb.ap[:], C.hy_bias[:].partition_broadcast(128), writes=[bb])
        S.op("dve", lambda e: e.tensor_scalar(out=bb.ap[:], in0=bb.ap[:], scalar1=2.0 / N, scalar2=None, op0=ALU.mult), reads=[bb], writes=[bb])
    else:
        S.dma("sp", a_in.ap[:, :, 0, :], src.rearrange("(tc p) c -> p tc c", p=128), writes=[a_in])
    fm = mk_ring(S, "ffm", [128, 2, nt, 128], BF16, 2)
    pr = mk_ring(S, "fpr", [128, 512], F32, 2, psum=True)
    pi_ = mk_ring(S, "fpi", [128, 512], F32, 2, psum=True)
    if is_filter:
        ko = mk_ring(S, "fko", [128, 2, 512], F32, 2)
    else:
        ks = mk_ring(S, "fks", [128, 2, 512], F32, 2)
        tt = mk_ring(S, "ftt", [128, 4, 512], F32, 2)
    for fcn in range(nt):
        m = fm.next()
        for cs in range(2):
            S.dma("sp" if cs == 0 else "pool", m.ap[:, cs, :, :], C.fwd[si][fcn, cs, :, :, :], writes=[m])
        a, b = pr.next(), pi_.next()
        for cs, p in ((0, a), (1, b)):
            for tc in range(nt):
                S.op("pe", lambda e: e.matmul(p.ap[:], m.ap[:, cs, tc, :], a_in.ap[:, tc, cs if is_filter else 0, :], start=(tc == 0), stop=(tc == nt - 1)),
                     reads=[m, a_in], writes=[p])
        if is_filter:
            o = ko.next()
            S.op("dve", lambda e: e.scalar_tensor_tensor(out=o.ap[:, 0, :], in0=a.ap[:], scalar=2.0 / N, in1=bb.ap[:], op0=ALU.mult, op1=ALU.add),
                 reads=[a, bb], writes=[o])
            S.op("act", lambda e: e.activation(out=o.ap[:, 1, :], in_=b.ap[:], func=AF.Identity, scale=2.0 / N), reads=[b], writes=[o])
            S.dma("pool", dst[fcn * 128:(fcn + 1) * 128, :, :], o.ap[:], reads=[o])
        else:
            k = ks.next()
            S.dma("sp", k.ap[:], C.kspec[si][fcn * 128:(fcn + 1) * 128, :, :], writes=[k])
            t = tt.next()
            S.op("dve", lambda e: e.tensor_tensor(out=t.ap[:, 0, :], in0=a.ap[:], in1=k.ap[:, 0, :], op=ALU.mult), reads=[a, k], writes=[t])
            S.op("dve", lambda e: e.tensor_tensor(out=t.ap[:, 1, :], in0=b.ap[:], in1=k.ap[:, 1, :], op=ALU.mult), reads=[b, k], writes=[t])
            S.op("dve", lambda e: e.tensor_tensor(out=t.ap[:, 2, :], in0=a.ap[:], in1=k.ap[:, 1, :], op=ALU.mult), reads=[a, k], writes=[t])
            S.op("dve", lambda e: e.tensor_tensor(out=t.ap[:, 3, :], in0=b.ap[:], in1=k.ap[:, 0, :], op=ALU.mult), reads=[b, k], writes=[t])
            S.op("pool", lambda e: e.tensor_tensor(out=dst.ap[:, fcn, 0, :], in0=t.ap[:, 0, :], in1=t.ap[:, 1, :], op=ALU.add), reads=[t], writes=[dst])
            S.op("pool", lambda e: e.tensor_tensor(out=dst.ap[:, fcn, 1, :], in0=t.ap[:, 2, :], in1=t.ap[:, 3, :], op=ALU.subtract), reads=[t], writes=[dst])
    if is_filter:
        S.end_phase()


def l0_inproj(S, C, bi, src_stage):
    S.begin_phase()
    ident = C.ident
    hT = S.sb("ihT", [128, 8, T + 4], BF16)
    S.op("pool", lambda e: e.memset(hT.ap[:], 0.0), writes=[hT])
    xin = mk_ring(S, "ixin", [128, 1024], F32, 2)
    ptr = mk_ring(S, "iptr", [128, 512], F32, 2, psum=True)
    for i in range(NTT):
        xi = xin.next()
        S.dma("sp", xi.ap[:], tile_src(C, src_stage, bi, i), writes=[xi])
        transpose_mod(S, C, xi, hT, colof(i), 0, 0, (C.R - 1 if i < 2 else bi), ptr, ident)
    S.begin_sub()
    wt = S.sb("iwt", [128, 3, 8, 1024], BF16)
    wv = S.sb("iwv", [128, 8, 128], BF16)
    for j in range(3):
        for kc in range(8):
            S.dma("sp" if kc % 2 else "pool", wt.ap[:, j, kc, :], C.whb[j][kc * 128:(kc + 1) * 128, 512:1536], writes=[wt])
    for kc in range(8):
        S.dma("sp", wv.ap[:, kc, :], C.wqb[kc * 128:(kc + 1) * 128, 640:768], writes=[wv])
    pa = mk_ring(S, "ipa", [128, 512], F32, 2, psum=True)
    pb = mk_ring(S, "ipb", [128, 512], F32, 2, psum=True)
    x1s = mk_ring(S, "ix1", [128, 512], F32, 2)
    ut = mk_ring(S, "iut", [128, 512], BF16, 2)
    vt = mk_ring(S, "ivt", [128, 128], BF16, 2)
    for i in range(NTT):
        c0 = colof(i)
        a, b = pa.next(), pb.next()
        for half, p in ((0, a), (1, b)):
            n = 0
            for j in range(3):
                for kc in range(8):
                    S.op("pe", lambda e: e.matmul(p.ap[:], hT.ap[:, kc, c0 + j - 1:c0 + j - 1 + 128], wt.ap[:, j, kc, half * 512:(half + 1) * 512],
                                                  start=(n == 0), stop=(n == 23)), reads=[hT, wt], writes=[p])
                    n += 1
        x1 = x1s.next()
        S.op("act", lambda e: e.activation(out=x1.ap[:], in_=a.ap[:], func=AF.Identity), reads=[a], writes=[x1])
        u = ut.next()
        S.op("dve", lambda e: e.tensor_tensor(out=u.ap[:], in0=b.ap[:], in1=x1.ap[:], op=ALU.mult), reads=[b, x1], writes=[u])
        S.dma("pool", C.u[bi, i * 128:(i + 1) * 128, :], u.ap[:], reads=[u])
        p = pa.next()
        for kc in range(8):
            S.op("pe", lambda e: e.matmul(p.ap[:, :128], hT.ap[:, kc, c0:c0 + 128], wv.ap[:, kc, :], start=(kc == 0), stop=(kc == 7)),
                 reads=[hT, wv], writes=[p])
        v = vt.next()
        S.op("act", lambda e: e.activation(out=v.ap[:], in_=p.ap[:, :128], func=AF.Identity), reads=[p], writes=[v])
        S.dma("pool", C.v[bi, i * 128:(i + 1) * 128, :], v.ap[:], reads=[v])
    S.end_sub()
    S.begin_sub()
    w0 = S.sb("iw0", [128, 3, 8, 512], BF16)
    wq = S.sb("iwq", [128, 8, 1280], BF16)
    rope = S.sb("irope", [64, 2, LAT], F32)
    S.dma("sp", rope.ap[:], C.rope[:, :, :], writes=[rope])
    for j in range(3):
        for kc in range(8):
            S.dma("sp" if kc % 2 else "pool", w0.ap[:, j, kc, :], C.whb[j][kc * 128:(kc + 1) * 128, 0:512], writes=[w0])
    for kc in range(8):
        S.dma("sp", wq.ap[:, kc, 0:640], C.wqb[kc * 128:(kc + 1) * 128, 0:640], writes=[wq])
        S.dma("pool", wq.ap[:, kc, 640:1280], C.wqb[kc * 128:(kc + 1) * 128, 768:1408], writes=[wq])
    pa = mk_ring(S, "jpa", [128, 512], F32, 3, psum=True)
    ot = mk_ring(S, "jot", [128, 512], BF16, 3)
    t1 = mk_ring(S, "jt1", [64, 512], F32, 2)
    t2 = mk_ring(S, "jt2", [64, 512], F32, 2)
    tiles = [(0, 256)] + [(256 + 512 * k, 512) for k in range(4)]
    for (tok0, w) in tiles:
        c0 = colof(tok0 // 128)
        for cc in range(4):
            p = pa.next()
            n = 0
            for j in range(3):
                for kc in range(8):
                    S.op("pe", lambda e: e.matmul(p.ap[:, :w], w0.ap[:, j, kc, cc * 128:(cc + 1) * 128], hT.ap[:, kc, c0 + j - 1:c0 + j - 1 + w],
                                                  start=(n == 0), stop=(n == 23)), reads=[w0, hT], writes=[p])
                    n += 1
            o = ot.next()
            S.op("act", lambda e: e.activation(out=o.ap[:, :w], in_=p.ap[:, :w], func=AF.Identity), reads=[p], writes=[o])
            S.dma("pool", C.x0T[bi, cc * 128:(cc + 1) * 128, tok0:tok0 + w], o.ap[:, :w], reads=[o])
        for hh in range(10):
            p = pa.next()
            for kc in range(8):
                S.op("pe", lambda e: e.matmul(p.ap[:64, :w], wq.ap[:, kc, hh * 64:(hh + 1) * 64], hT.ap[:, kc, c0:c0 + w], start=(kc == 0), stop=(kc == 7)),
                     reads=[wq, hT], writes=[p])
            o = ot.next()
            dst = C.qT[bi, hh, :, tok0:tok0 + w] if hh < 8 else C.kT[bi, hh - 8, :, tok0:tok0 + w]
            if tok0 == 0:
                S.op("act", lambda e: e.activation(out=o.ap[:64, :w], in_=p.ap[:64, :w], func=AF.Identity), reads=[p], writes=[o])
            else:
                p2 = pa.next()
                for kc in range(8):
                    S.op("pe", lambda e: e.matmul(p2.ap[:64, :w], wq.ap[:, kc, 640 + hh * 64:640 + (hh + 1) * 64], hT.ap[:, kc, c0:c0 + w],
                                                  start=(kc == 0), stop=(kc == 7)), reads=[wq, hT], writes=[p2])
                l0_ = tok0 - 256
                a, b = t1.next(), t2.next()
                S.op("dve", lambda e: e.tensor_tensor(out=a.ap[:, :w], in0=p.ap[:64, :w], in1=rope.ap[:, 0, l0_:l0_ + w], op=ALU.mult), reads=[p, rope], writes=[a])
                S.op("dve", lambda e: e.tensor_tensor(out=b.ap[:, :w], in0=p2.ap[:64, :w], in1=rope.ap[:, 1, l0_:l0_ + w], op=ALU.mult), reads=[p2, rope], writes=[b])
                S.op("pool", lambda e: e.tensor_tensor(out=o.ap[:64, :w], in0=a.ap[:, :w], in1=b.ap[:, :w], op=ALU.add), reads=[a, b], writes=[o])
            S.dma("sp", dst, o.ap[:64, :w], reads=[o])
    S.end_sub()
    S.end_phase()


def l0_hyena(S, C, bi):
    for si, (Lf, tok0) in enumerate(((LAT, 256), (CTX, 0))):
        S.begin_phase()
        nt = Lf // 128
        Y = S.sb("hyY", [128, nt, 2, 512], BF16)
        S.begin_sub()
        hyena_fwd(S, C, si, Lf, C.u[bi, tok0:tok0 + Lf, :], bi, Y, is_filter=False)
        S.end_sub()
        tw = min(512, Lf)
        iv = mk_ring(S, "hyiv", [128, nt, 2, tw], BF16, 2 if Lf == CTX else 1)
        x0 = mk_ring(S, "hyx0", [128, tw], BF16, 2)
        ya = mk_ring(S, "hyya", [128, tw], BF16, 2)
        pp = mk_ring(S, "hypp", [128, 512], F32, 2, psum=True)
        for tt in range(Lf // tw):
            m = iv.next()
            for fc in range(nt):
                S.dma("sp" if fc % 2 else "pool", m.ap[:, fc, :, :], C.inv[si][tt, :, fc, :, :], writes=[m])
            for cc in range(4):
                p = pp.next()
                n = 0
                for fc in range(nt):
                    for cs in range(2):
                        S.op("pe", lambda e: e.matmul(p.ap[:, :tw], Y.ap[:, fc, cs, cc * 128:(cc + 1) * 128], m.ap[:, fc, cs, :],
                                                      start=(n == 0), stop=(n == 2 * nt - 1)), reads=[Y, m], writes=[p])
                        n += 1
                xz = x0.next()
                t0_ = tok0 + tt * tw
                S.dma("sp", xz.ap[:], C.x0T[bi, cc * 128:(cc + 1) * 128, t0_:t0_ + tw], writes=[xz])
                o = ya.next()
                S.op("dve", lambda e: e.tensor_tensor(out=o.ap[:], in0=p.ap[:, :tw], in1=xz.ap[:], op=ALU.mult), reads=[p, xz], writes=[o])
                S.dma("pool", C.yaT[bi, cc * 128:(cc + 1) * 128, t0_:t0_ + tw], o.ap[:], reads=[o])
        S.end_phase()


def l0_attn(S, C, bi):
    S.begin_phase()
    qT = S.sb("aqT", [64, 8, T], BF16)
    kT = S.sb("akT", [64, 2, T], BF16)
    v = S.sb("av", [128, NTT, 128], BF16)
    yb = S.sb("ayb", [64, 8, T], BF16)
    mask = S.sb("amask", [128, 384], F32)
    sink = S.sb("asink", [128, 8], F32)
    idb = S.sb("aidb", [128, 128], BF16)
    for h in range(8):
        S.dma("sp" if h % 2 else "pool", qT.ap[:, h, :], C.qT[bi, h, :, :], writes=[qT])
    for h in range(2):
        S.dma("sp", kT.ap[:, h, :], C.kT[bi, h, :, :], writes=[kT])
    S.dma("sp", v.ap[:], C.v[bi].rearrange("(i p) c -> p i c", p=128), writes=[v])
    S.dma("sp", mask.ap[:], C.amask[:, :], writes=[mask])
    S.dma("sp", sink.ap[:], C.attn_sink[:].partition_broadcast(128), writes=[sink])
    S.dma("sp", idb.ap[:], C.ident_b[:, :], writes=[idb])
    psl = mk_ring(S, "apsl", [128, 512], F32, 2, psum=True)
    psc = mk_ring(S, "apsc", [128, 512], F32, 2, psum=True)
    ppt = mk_ring(S, "appt", [128, 5, 128], BF16, 2, psum=True)
    ppv = mk_ring(S, "appv", [128, 128], F32, 2, psum=True)
    sc = mk_ring(S, "asc", [128, 640], F32, 2)
    pe_ = mk_ring(S, "ape", [128, 640], F32, 2)
    pn = mk_ring(S, "apn", [128, 640], BF16, 2)
    pts = mk_ring(S, "apts", [128, 5, 128], BF16, 2)
    sm = mk_ring(S, "asm", [128, 8], F32, 4)
    def head_chain(qb, hh, n, lo, hi, m0, ktiles, nk, q0):
            h = hh // 4
            s_ = sc.next()
            if n:
                a = psl.next()
                S.op("pe", lambda e: e.matmul(a.ap[:, :n], qT.ap[:, hh, q0:q0 + 128], kT.ap[:, h, 256 + lo:256 + hi], start=True, stop=True),
                     reads=[qT, kT], writes=[a])
                S.op("dve", lambda e: e.tensor_tensor(out=s_.ap[:, :n], in0=a.ap[:, :n], in1=mask.ap[:, m0:m0 + n], op=ALU.add), reads=[a, mask], writes=[s_])
            b = psc.next()
            S.op("pe", lambda e: e.matmul(b.ap[:, :256], qT.ap[:, hh, q0:q0 + 128], kT.ap[:, h, 0:256], start=True, stop=True), reads=[qT, kT], writes=[b])
            S.op("act", lambda e: e.activation(out=s_.ap[:, n:nk], in_=b.ap[:, :256], func=AF.Identity), reads=[b], writes=[s_])
            yield
            w = sm.next()
            S.op("dve", lambda e: e.tensor_reduce(out=w.ap[:, 0:1], in_=s_.ap[:, :nk], axis=AX.X, op=ALU.max), reads=[s_], writes=[w])
            S.op("dve", lambda e: e.tensor_scalar(out=w.ap[:, 1:2], in0=w.ap[:, 0:1], scalar1=0.125, scalar2=sink.ap[:, hh:hh + 1], op0=ALU.mult, op1=ALU.max),
                 reads=[w, sink], writes=[w])
            S.op("dve", lambda e: e.tensor_scalar(out=w.ap[:, 2:3], in0=w.ap[:, 1:2], scalar1=-1.0, scalar2=None, op0=ALU.mult), reads=[w], writes=[w])
            p_ = pe_.next()
            S.op("act", lambda e: e.activation(out=p_.ap[:, :nk], in_=s_.ap[:, :nk], func=AF.Exp, scale=0.125, bias=w.ap[:, 2:3]),
                 reads=[s_, w], writes=[p_])
            yield
            S.op("dve", lambda e: e.tensor_reduce(out=w.ap[:, 3:4], in_=p_.ap[:, :nk], axis=AX.X, op=ALU.add), reads=[p_], writes=[w])
            S.op("act", lambda e: e.activation(out=w.ap[:, 4:5], in_=w.ap[:, 2:3], func=AF.Exp, bias=sink.ap[:, hh:hh + 1], scale=1.0), reads=[w, sink], writes=[w])
            S.op("dve", lambda e: e.tensor_tensor(out=w.ap[:, 5:6], in0=w.ap[:, 3:4], in1=w.ap[:, 4:5], op=ALU.add), reads=[w], writes=[w])
            S.op("dve", lambda e: e.reciprocal(out=w.ap[:, 6:7], in_=w.ap[:, 5:6]), reads=[w], writes=[w])
            pn_ = pn.next()
            S.op("act", lambda e: e.activation(out=pn_.ap[:, :nk], in_=p_.ap[:, :nk], func=AF.Identity, scale=w.ap[:, 6:7]), reads=[p_, w], writes=[pn_])
            yield
            pt = ppt.next()
            nch = nk // 128
            for j in range(nch):
                S.op("pe", lambda e: e.transpose(out=pt.ap[:, j, :], in_=pn_.ap[:, j * 128:(j + 1) * 128], identity=idb.ap[:]), reads=[pn_, idb], writes=[pt])
            ps_ = pts.next()
            S.op("act", lambda e: e.activation(out=ps_.ap[:, :nch, :], in_=pt.ap[:, :nch, :], func=AF.Identity), reads=[pt], writes=[ps_])
            yield
            o = ppv.next()
            for j in range(nch):
                S.op("pe", lambda e: e.matmul(o.ap[:64, :], v.ap[:, ktiles[j], h * 64:(h + 1) * 64], ps_.ap[:, j, :], start=(j == 0), stop=(j == nch - 1)),
                     reads=[v, ps_], writes=[o])
            S.op("dve", lambda e: e.tensor_copy(out=yb.ap[:, hh, q0:q0 + 128], in_=o.ap[:64, :]), reads=[o], writes=[yb])
            yield

    for qb in range(NTT):
        is_ctx = qb < 2
        q0 = qb * 128
        lo = hi = m0 = 0
        if is_ctx:
            n = 0
            ktiles = []
        else:
            lq = q0 - 256
            lo, hi = max(0, lq - 128), min(LAT, lq + 256)
            n = hi - lo
            m0 = lo - (lq - 128)
            ktiles = [2 + lo // 128 + j for j in range(n // 128)]
        ktiles = ktiles + [0, 1]
        nk = n + 256
        for h0 in range(0, 8, 2):
            interleave([head_chain(qb, hh, n, lo, hi, m0, ktiles, nk, q0) for hh in (h0, h0 + 1)])
    for h in range(8):
        S.dma("sp" if h % 2 else "pool", C.ybT[bi, h, :, :], yb.ap[:, h, :], reads=[yb])
    S.end_phase()


def l0_outproj(S, C, bi, src_stage, dst_stage):
    S.begin_phase()
    ya = S.sb("oya", [128, 4, T], BF16)
    yb = S.sb("oyb", [64, 8, T], BF16)
    wa = S.sb("owa", [128, 4, D], BF16)
    wb = S.sb("owb", [64, 8, D], BF16)
    for c in range(4):
        S.dma("sp", ya.ap[:, c, :], C.yaT[bi, c * 128:(c + 1) * 128, :], writes=[ya])
        S.dma("pool", wa.ap[:, c, :], C.wob0[c * 128:(c + 1) * 128, :], writes=[wa])
    for h in range(8):
        S.dma("sp", yb.ap[:, h, :], C.ybT[bi, h, :, :], writes=[yb])
        S.dma("pool", wb.ap[:, h, :], C.wob0[512 + h * 64:512 + (h + 1) * 64, :], writes=[wb])
    epi = Epi(S, C, "o")
    epi.load(0, 0, bi)
    xin = mk_ring(S, "oxin", [128, 1024], F32, 3)
    py = mk_ring(S, "opy", [128, 1024], F32, 2, psum=True)
    for i in range(NTT):
        xi = xin.next()
        S.dma("sp", xi.ap[:], tile_src(C, src_stage, bi, i), writes=[xi])
        p = py.next()
        for hf in range(2):
            for c in range(4):
                S.op("pe", lambda e: e.matmul(p.ap[:, hf * 512:(hf + 1) * 512], ya.ap[:, c, i * 128:(i + 1) * 128], wa.ap[:, c, hf * 512:(hf + 1) * 512],
                                              start=(c == 0), stop=False), reads=[ya, wa], writes=[p])
            for h in range(8):
                S.op("pe", lambda e: e.matmul(p.ap[:, hf * 512:(hf + 1) * 512], yb.ap[:, h, i * 128:(i + 1) * 128], wb.ap[:, h, hf * 512:(hf + 1) * 512],
                                              start=False, stop=(h == 7)), reads=[yb, wb], writes=[p])
        epi.run(p, xi, i < 2, C.xs[dst_stage - 1][bi, i * 128:(i + 1) * 128, :])
    S.end_phase()


def V(b, *idx):
    return (b, b.ap[idx] if idx else b.ap[:])


def TT(S, eng, o, a, b, op):
    return S.op(eng, lambda e: e.tensor_tensor(out=o[1], in0=a[1], in1=b[1], op=op), reads=[a[0], b[0]], writes=[o[0]])


def TS(S, eng, o, a, s1, s2, op0, op1=None, extra=()):
    if op1 is None:
        return S.op(eng, lambda e: e.tensor_scalar(out=o[1], in0=a[1], scalar1=s1, scalar2=None, op0=op0), reads=[a[0], *extra], writes=[o[0]])
    return S.op(eng, lambda e: e.tensor_scalar(out=o[1], in0=a[1], scalar1=s1, scalar2=s2, op0=op0, op1=op1), reads=[a[0], *extra], writes=[o[0]])


def STT(S, o, a, sc, b, op0, op1, extra=()):
    return S.op("dve", lambda e: e.scalar_tensor_tensor(out=o[1], in0=a[1], scalar=sc, in1=b[1], op0=op0, op1=op1), reads=[a[0], b[0], *extra], writes=[o[0]])


def ACT(S, o, a, func, scale=1.0, bias=None, extra=()):
    if bias is None:
        return S.op("act", lambda e: e.activation(out=o[1], in_=a[1], func=func, scale=scale), reads=[a[0], *extra], writes=[o[0]])
    return S.op("act", lambda e: e.activation(out=o[1], in_=a[1], func=func, scale=scale, bias=bias), reads=[a[0], *extra], writes=[o[0]])


def MM(S, o, l, r, start=True, stop=True):
    return S.op("pe", lambda e: e.matmul(o[1], l[1], r[1], start=start, stop=stop), reads=[l[0], r[0]], writes=[o[0]])


def TR(S, o, a, ident):
    return S.op("pe", lambda e: e.transpose(out=o[1], in_=a[1], identity=ident[1]), reads=[a[0], ident[0]], writes=[o[0]])


RW_E = float(np.exp(-0.5))
INV_DT = BF16


def declare_l1(nc, C, dbg=None):
    NB = C.NB
    I = lambda n, s, dt=F32: dram(nc, n, s, dt, "ExternalInput")
    Sx = lambda n, s, dt=BF16: dram(nc, n, s, dt, "ExternalOutput" if (dbg and n in dbg) else None)
    C.o_w_rw = I("o_w_rw", [D, 1920])
    C.o_w_dn = I("o_w_dn", [D, 1536])
    C.o_w_z = I("o_w_z", [D, 528])
    C.o_w_out = I("o_w_out", [D, D])
    C.rw_mu = I("rw_mu", [1920])
    C.dn_conv = I("dn_conv", [3, 1536])
    C.rw_rows = I("rw_rows", [10, 512])
    C.rw_w2 = I("rw_w2", [128, 512])
    C.rw_a2 = I("rw_a2", [128, 512])
    C.rw_g2 = I("rw_g2", [128, 512])
    C.dn_rows = I("dn_rows", [3, 8])
    C.dn_ng = I("dn_ng", [512])
    C.tri = I("tri", [2, 128, 128])
    C.tris = I("tris", [2, 128, 128])
    C.blkm = I("blkm", [4, 128, 128])
    C.tsw = Sx("tsw", [3, 1920], F32)
    C.wrb = [Sx(f"wrb{j}", [D, 1920]) for j in range(3)]
    C.wdb = [Sx(f"wdb{j}", [D, 1536]) for j in range(3)]
    C.wzb = Sx("wzb", [D, 528])
    C.wob1 = Sx("wob1", [D, D])
    C.rw_ops = Sx("rw_ops", [NB, 2, 6, T, 512])
    C.rw_v = Sx("rw_v", [NB, T, 512])
    C.rw_gc = Sx("rw_gc", [NB, 2, NTT, 64, 8], F32)
    C.rw_g = Sx("rw_g", [NB, T, 512], F32)
    C.rw_bonus = Sx("rw_bonus", [NB, T, 512], F32)
    C.y_rw = Sx("y_rw", [NB, 2, T, 512], F32)
    C.dn_qk = Sx("dn_qk", [NB, 2, T, 512])
    C.dn_ops = Sx("dn_ops", [NB, 2, 5, T, 512])
    C.dn_G = Sx("dn_G", [NB, 2, 3, T, 4], F32)
    C.dn_z = Sx("dn_z", [NB, T, 512], F32)
    C.y_dn = Sx("y_dn", [NB, 2, T, 512], F32)


def host_l1(inputs, m):
    f = lambda a: np.ascontiguousarray(np.asarray(a, dtype=np.float32))
    w = np.asarray(inputs["o_w_in"])[0]
    m["o_w_rw"] = f(w[:, :1920])
    m["o_w_dn"] = f(w[:, 1920:1920 + 1536])
    m["o_w_z"] = f(w[:, 1920 + 1536:])
    m["o_w_out"] = f(np.asarray(inputs["o_w_out"])[0])
    m["rw_mu"] = f(np.asarray(inputs["rw_mu"])[0])
    m["dn_conv"] = f(np.asarray(inputs["dn_conv"])[0])
    g = lambda k: np.asarray(inputs[k])[0]
    m["rw_rows"] = f(np.stack([g("rw_w0")[0], g("rw_w0")[1], g("rw_a0")[0], g("rw_a0")[1], g("rw_kk"), g("rw_ka"),
                               g("rw_rk").reshape(512), g("rw_lnx_g"), g("rw_lnx_b"), np.zeros(512, np.float32)], 0))
    m["rw_w2"] = f(g("rw_w2").reshape(128, 512))
    m["rw_a2"] = f(g("rw_a2").reshape(128, 512))
    m["rw_g2"] = f(g("rw_g2"))
    m["dn_rows"] = f(np.stack([g("dn_A_log").reshape(8), g("dn_dt_bias").reshape(8), np.zeros(8, np.float32)], 0))
    m["dn_ng"] = f(np.tile(g("dn_norm_g"), 4))
    j = np.arange(128)[:, None]
    t = np.arange(128)[None, :]
    m["tri"] = f(np.stack([(j <= t), (j >= t)], 0))
    m["tris"] = f(np.stack([(j < t), (j > t)], 0))
    bd = lambda n: (j // n == t // n)
    m["blkm"] = f(np.stack([bd(16), bd(32) & ~bd(16), bd(64) & ~bd(32), ~bd(64)], 0))
    return m


def l1_setup(S, C):
    S.begin_phase()
    mu = S.sb("smu", [1, 1920], F32)
    o = S.sb("smo", [1, 3, 1920], F32)
    S.dma("sp", mu.ap[:], C.rw_mu[:].partition_broadcast(1), writes=[mu])
    TS(S, "dve", V(o, slice(None), 0, slice(None)), V(mu), 0.5, None, ALU.mult)
    TS(S, "dve", V(o, slice(None), 1, slice(None)), V(mu), -1.0, 1.0, ALU.mult, ALU.add)
    TS(S, "dve", V(o, slice(None), 2, slice(None)), V(mu), 0.5, None, ALU.mult)
    S.dma("sp", C.tsw.rearrange("(o j) n -> o j n", o=1), o.ap[:], reads=[o])
    S.end_phase()
    for j in range(3):
        prep_weight(S, C.wrb[j], C.o_w_rw, D, 1920, scale=C.tsw[j, :], tag=f"qr{j}")
        prep_weight(S, C.wdb[j], C.o_w_dn, D, 1536, scale=C.dn_conv[j, :], tag=f"qd{j}")
    prep_weight(S, C.wzb, C.o_w_z, D, 528, tag="qz")
    prep_weight(S, C.wob1, C.o_w_out, D, D, tag="qo")


def build_hT(S, C, bi, src_stage, l):
    hT = S.sb("bhT", [128, 8, T + 4], BF16)
    S.op("pool", lambda e: e.memset(hT.ap[:], 0.0), writes=[hT])
    S.begin_sub()
    xin = mk_ring(S, "bxin", [128, 1024], F32, 2)
    ptr = mk_ring(S, "bptr", [128, 512], F32, 2, psum=True)
    for i in range(NTT):
        xi = xin.next()
        S.dma("sp", xi.ap[:], tile_src(C, src_stage, bi, i), writes=[xi])
        transpose_mod(S, C, xi, hT, colof(i), l, 0, (C.R - 1 if i < 2 else bi), ptr, C.ident)
    S.end_sub()
    return hT


def bc3(b, n, w):
    return (b, b.ap[:, 0:n].unsqueeze(2).to_broadcast([128, n, w]))


def r3(b, n, *pre):
    ap = b.ap[pre] if pre else b.ap[:]
    return (b, ap.rearrange("p (h d) -> p h d", h=n))


def proj3(S, C, p, hT, c0, w, wt, col0, ncol):
    n = 0
    for j in range(3):
        for kc in range(8):
            MM(S, (p, p.ap[:, :ncol]), (hT, hT.ap[:, kc, c0 + j - 1:c0 + j - 1 + 128]), (wt, wt.ap[:, j, kc, col0:col0 + ncol]), start=(n == 0), stop=(n == 23))
            n += 1


def l1_feat_rw(S, C, bi, hT):
    S.begin_sub()
    ones = S.sb("fones", [128, 128], F32)
    S.op("pool", lambda e: e.memset(ones.ap[:], 1.0), writes=[ones])
    tri = S.sb("ftri", [128, 2, 128], F32)
    for d in range(2):
        S.dma("sp", tri.ap[:, d, :], C.tri[d, :, :], writes=[tri])
    rows = S.sb("frows", [128, 7, 512], F32)
    for q in range(7):
        S.dma("sp", rows.ap[:, q, :], C.rw_rows[q, :].partition_broadcast(128), writes=[rows])
    lwb = S.sb("flwb", [128, 3, 512], BF16)
    loraT = S.sb("floraT", [128, 3, T], BF16)
    S.begin_sub()
    lw = S.sb("flw", [128, 3, 512], F32)
    for q, src in enumerate((C.rw_w2, C.rw_a2, C.rw_g2)):
        S.dma("sp", lw.ap[:, q, :], src[:, :], writes=[lw])
    S.op("dve", lambda e: e.tensor_copy(out=lwb.ap[:], in_=lw.ap[:]), reads=[lw], writes=[lwb])
    wl = S.sb("fwl", [128, 3, 8, 384], BF16)
    for j in range(3):
        for kc in range(8):
            S.dma("sp" if kc % 2 else "pool", wl.ap[:, j, kc, :], C.wrb[j][kc * 128:(kc + 1) * 128, 1536:1920], writes=[wl])
    pp = mk_ring(S, "fpp", [128, 512], F32, 2, psum=True)
    for (tok0, w) in [(0, 256)] + [(256 + 512 * k, 512) for k in range(4)]:
        c0 = colof(tok0 // 128)
        for q, fn in enumerate((AF.Tanh, AF.Identity, AF.Sigmoid)):
            p = pp.next()
            n = 0
            for j in range(3):
                for kc in range(8):
                    MM(S, (p, p.ap[:, :w]), (wl, wl.ap[:, j, kc, q * 128:(q + 1) * 128]), (hT, hT.ap[:, kc, c0 + j - 1:c0 + j - 1 + w]), start=(n == 0), stop=(n == 23))
                    n += 1
            ACT(S, (loraT, loraT.ap[:, q, tok0:tok0 + w]), (p, p.ap[:, :w]), fn)
    S.end_sub()
    wr = S.sb("fwr", [128, 3, 8, 1536], BF16)
    for j in range(3):
        for kc in range(8):
            S.dma("sp" if kc % 2 else "pool", wr.ap[:, j, kc, :], C.wrb[j][kc * 128:(kc + 1) * 128, 0:1536], writes=[wr])
    prkv = [S.ps(f"fp{n}", [128, 512], F32) for n in "rkv"]
    pqd = [mk_ring(S, f"fpq{d}", [128, 512], F32, 2, psum=True) for d in range(2)]
    F = lambda n: S.sb("f_" + n, [128, 512], F32)
    rs, ks, kkr, sq, kk, ksum = [F(n) for n in "rs ks kkr sq kk ksum".split()]
    Fd = []
    for d in range(2):
        X = Ctx()
        X.zt, X.logw, X.a_, X.Gs, X.eG, X.enG, X.eE, X.kd, X.bd = [F(f"{n}{d}") for n in "zt logw a Gs eG enG eE kd bd".split()]
        X.eP, X.t1 = X.zt, X.Gs
        Fd.append(X)
    tb, bon, gs = sq, Fd[0].zt, Fd[0].Gs
    vb = mk_ring(S, "fvb", [128, 512], BF16, 2)
    ot = mk_ring(S, "fot", [128, 6, 512], BF16, 2)
    sm = mk_ring(S, "fsm", [128, 16], F32, 2)
    gcs = mk_ring(S, "fgcs", [64, 8], F32, 2)
    KK, KA, RK = [(rows, rows.ap[:, q, :]) for q in (4, 5, 6)]

    def dir_chain(d, i, rws):
        X = Fd[d]
        zt, logw, a_, Gs, eG, enG, eP, eE, t1, kd, bd = X.zt, X.logw, X.a_, X.Gs, X.eG, X.enG, X.eP, X.eE, X.t1, X.kd, X.bd
        o = ot.next()
        pq = pqd[d]
        pgc = prkv[d]
        pz, pa_ = pq.next(), pq.next()
        MM(S, V(pz), (loraT, loraT.ap[d * 64:(d + 1) * 64, 0, rws]), (lwb, lwb.ap[d * 64:(d + 1) * 64, 0, :]))
        MM(S, V(pa_), (loraT, loraT.ap[d * 64:(d + 1) * 64, 1, rws]), (lwb, lwb.ap[d * 64:(d + 1) * 64, 1, :]))
        TT(S, "dve", V(zt), V(pz), (rows, rows.ap[:, d, :]), ALU.add)
        ACT(S, V(zt), V(zt), AF.Sigmoid)
        TS(S, "pool", V(logw), V(zt), -RW_E, None, ALU.mult)
        TT(S, "dve", V(a_), V(pa_), (rows, rows.ap[:, 2 + d, :]), ALU.add)
        ACT(S, V(a_), V(a_), AF.Sigmoid)
        yield
        pG, pT = pq.next(), pq.next()
        MM(S, V(pG), (tri, tri.ap[:, d, :]), V(logw))
        MM(S, V(pT), V(ones), V(logw))
        for h in range(8):
            MM(S, (pgc, pgc.ap[:64, h:h + 1]), (logw, logw.ap[:, h * 64:(h + 1) * 64]), (ones, ones.ap[:, 0:1]))
        gc = gcs.next()
        ACT(S, V(gc), (pgc, pgc.ap[:64, 0:8]), AF.Exp)
        S.dma("pool", C.rw_gc[bi, d, i, :, :], gc.ap[:], reads=[gc])
        ACT(S, V(Gs), V(pG), AF.Identity)
        yield
        ACT(S, V(eG), V(Gs), AF.Exp)
        ACT(S, V(enG), V(Gs), AF.Exp, scale=-1.0)
        TT(S, "dve", V(eP), V(Gs), V(logw), ALU.subtract)
        ACT(S, V(eP), V(eP), AF.Exp)
        TT(S, "dve", V(eE), V(pT), V(Gs), ALU.subtract)
        ACT(S, V(eE), V(eE), AF.Exp)
        yield
        STT(S, V(t1), V(a_), -1.0, KA, ALU.add, ALU.mult)
        STT(S, V(kd), V(t1), 1.0, V(ks), ALU.add, ALU.mult)
        TT(S, "pool", V(bd), V(kk), V(a_), ALU.mult)
        yield
        O = lambda q: (o, o.ap[:, q, :])
        TT(S, "dve", O(0), V(rs), V(eG), ALU.mult)
        TT(S, "pool", O(1), V(kd), V(enG), ALU.mult)
        TT(S, "dve", O(2), V(bd), V(enG), ALU.mult)
        TT(S, "pool", O(3), V(kk), V(eP), ALU.mult)
        TT(S, "dve", O(4), V(kd), V(eE), ALU.mult)
        TT(S, "pool", O(5), V(bd), V(eE), ALU.mult)
        S.dma("sp", C.rw_ops[bi, d, :, rws, :].rearrange("q t c -> t q c"), o.ap[:], reads=[o])
        yield

    for i in range(NTT):
        c0 = colof(i)
        rws = slice(i * 128, (i + 1) * 128)
        for n in range(3):
            proj3(S, C, prkv[n], hT, c0, 128, wr, n * 512, 512)
        ACT(S, V(rs), V(prkv[0]), AF.Identity)
        ACT(S, V(ks), V(prkv[1]), AF.Identity)
        v_ = vb.next()
        ACT(S, V(v_), V(prkv[2]), AF.Identity)
        S.dma("pool", C.rw_v[bi, rws, :], v_.ap[:], reads=[v_])
        w = sm.next()
        TT(S, "dve", V(kkr), V(ks), KK, ALU.mult)
        TT(S, "pool", V(sq), V(kkr), V(kkr), ALU.mult)
        S.op("dve", lambda e: e.tensor_reduce(out=w.ap[:, 0:8], in_=r3(sq, 8)[1], axis=AX.X, op=ALU.add), reads=[sq], writes=[w])
        TS(S, "dve", V(w, slice(None), slice(0, 8)), V(w, slice(None), slice(0, 8)), 1e-6, None, ALU.add)
        ACT(S, V(w, slice(None), slice(0, 8)), V(w, slice(None), slice(0, 8)), AF.Sqrt)
        S.op("dve", lambda e: e.reciprocal(out=w.ap[:, 0:8], in_=w.ap[:, 0:8]), reads=[w], writes=[w])
        TT(S, "dve", r3(kk, 8), r3(kkr, 8), bc3(w, 8, 64), ALU.mult)
        interleave([dir_chain(d, i, rws) for d in range(2)])
        TT(S, "pool", V(ksum), V(Fd[0].kd), V(Fd[1].kd), ALU.add)
        TT(S, "dve", V(tb), V(rs), V(ksum), ALU.mult)
        TT(S, "pool", V(tb), V(tb), RK, ALU.mult)
        S.op("dve", lambda e: e.tensor_reduce(out=w.ap[:, 8:16], in_=r3(tb, 8)[1], axis=AX.X, op=ALU.add), reads=[tb], writes=[w])
        TT(S, "dve", r3(bon, 8), r3(v_, 8), (w, w.ap[:, 8:16].unsqueeze(2).to_broadcast([128, 8, 64])), ALU.mult)
        S.dma("pool", C.rw_bonus[bi, rws, :], bon.ap[:], reads=[bon])
        pg = pqd[0].next()
        MM(S, V(pg), (loraT, loraT.ap[:, 2, rws]), (lwb, lwb.ap[:, 2, :]))
        ACT(S, V(gs), V(pg), AF.Identity)
        S.dma("pool", C.rw_g[bi, rws, :], gs.ap[:], reads=[gs])
    S.end_sub()


def l1_feat_dn(S, C, bi, hT):
    S.begin_sub()
    ones = S.sb("gones", [128, 128], F32)
    S.op("pool", lambda e: e.memset(ones.ap[:], 1.0), writes=[ones])
    tri = S.sb("gtri", [128, 2, 128], F32)
    for d in range(2):
        S.dma("sp", tri.ap[:, d, :], C.tri[d, :, :], writes=[tri])
    dr = S.sb("gdr", [128, 2, 8], F32)
    for q in range(2):
        S.dma("sp", dr.ap[:, q, :], C.dn_rows[q, :].partition_broadcast(128), writes=[dr])
    ACT(S, V(dr, slice(None), 0, slice(None)), V(dr, slice(None), 0, slice(None)), AF.Exp)
    TS(S, "dve", V(dr, slice(None), 0, slice(None)), V(dr, slice(None), 0, slice(None)), -1.0, None, ALU.mult)
    wd = S.sb("gwd", [128, 3, 8, 1536], BF16)
    wz = S.sb("gwz", [128, 8, 528], BF16)
    for j in range(3):
        for kc in range(8):
            S.dma("sp" if kc % 2 else "pool", wd.ap[:, j, kc, :], C.wdb[j][kc * 128:(kc + 1) * 128, :], writes=[wd])
    for kc in range(8):
        S.dma("sp", wz.ap[:, kc, :], C.wzb[kc * 128:(kc + 1) * 128, :], writes=[wz])
    pqkv = [S.ps(f"gp{n}", [128, 512], F32) for n in "qkv"]
    pz = S.ps("gpz", [128, 512], F32)
    pgt = S.ps("gpgt", [128, 16], F32)
    pG = mk_ring(S, "gpG", [128, 8], F32, 2, psum=True)
    F = lambda n: S.sb("g_" + n, [128, 512], F32)
    qs, ks, vs, sq, zs = [F(n) for n in "qs ks vs sq zs".split()]
    qk = mk_ring(S, "gqk", [128, 2, 512], BF16, 2)
    ot = mk_ring(S, "got", [128, 5, 512], BF16, 2)
    sm = mk_ring(S, "gsm", [128, 64], F32, 2)
    go = mk_ring(S, "ggo", [128, 3, 4], F32, 2)
    for i in range(NTT):
        c0 = colof(i)
        rws = slice(i * 128, (i + 1) * 128)
        for n in range(3):
            proj3(S, C, pqkv[n], hT, c0, 128, wd, n * 512, 512)
        for kc in range(8):
            MM(S, V(pz), (hT, hT.ap[:, kc, c0:c0 + 128]), (wz, wz.ap[:, kc, 0:512]), start=(kc == 0), stop=(kc == 7))
        for kc in range(8):
            MM(S, V(pgt), (hT, hT.ap[:, kc, c0:c0 + 128]), (wz, wz.ap[:, kc, 512:528]), start=(kc == 0), stop=(kc == 7))
        for src, dst in zip(pqkv + [pz], (qs, ks, vs, zs)):
            ACT(S, V(dst), V(src), AF.Silu)
        S.dma("pool", C.dn_z[bi, rws, :], zs.ap[:], reads=[zs])
        w = sm.next()
        Wc = lambda a, b: (w, w.ap[:, a:b])
        ACT(S, Wc(0, 16), V(pgt), AF.Identity)
        qk_ = qk.next()
        for n, (src, sc) in enumerate(((qs, 128.0 ** -0.5), (ks, 1.0))):
            TT(S, "pool", V(sq), V(src), V(src), ALU.mult)
            S.op("dve", lambda e: e.tensor_reduce(out=w.ap[:, 16 + 4 * n:20 + 4 * n], in_=r3(sq, 4)[1], axis=AX.X, op=ALU.add), reads=[sq], writes=[w])
            TS(S, "dve", Wc(16 + 4 * n, 20 + 4 * n), Wc(16 + 4 * n, 20 + 4 * n), 1e-6, None, ALU.add)
            ACT(S, Wc(16 + 4 * n, 20 + 4 * n), Wc(16 + 4 * n, 20 + 4 * n), AF.Sqrt)
            S.op("dve", lambda e: e.reciprocal(out=w.ap[:, 16 + 4 * n:20 + 4 * n], in_=w.ap[:, 16 + 4 * n:20 + 4 * n]), reads=[w], writes=[w])
            if sc != 1.0:
                TS(S, "dve", Wc(16, 20), Wc(16, 20), sc, None, ALU.mult)
            TT(S, "dve", r3(src, 4), r3(src, 4), (w, w.ap[:, 16 + 4 * n:20 + 4 * n].unsqueeze(2).to_broadcast([128, 4, 128])), ALU.mult)
            S.op("pool", lambda e: e.tensor_copy(out=qk_.ap[:, n, :], in_=src.ap[:]), reads=[src], writes=[qk_])
        S.dma("sp", C.dn_qk[bi, :, rws, :].rearrange("q t c -> t q c"), qk_.ap[:], reads=[qk_])
        for d in range(2):
            o = ot.next()
            g_ = go.next()
            TT(S, "dve", Wc(24, 28), Wc(d * 4, d * 4 + 4), (dr, dr.ap[:, 1, d * 4:d * 4 + 4]), ALU.add)
            ACT(S, Wc(24, 28), Wc(24, 28), AF.Exp)
            ACT(S, Wc(24, 28), Wc(24, 28), AF.Ln, bias=1.0)
            TT(S, "dve", (g_, g_.ap[:, 0, :]), Wc(24, 28), (dr, dr.ap[:, 0, d * 4:d * 4 + 4]), ALU.mult)
            ACT(S, Wc(28, 32), Wc(8 + d * 4, 12 + d * 4), AF.Sigmoid)
            p = pG.next()
            MM(S, (p, p.ap[:, 0:4]), (tri, tri.ap[:, d, :]), (g_, g_.ap[:, 0, :]))
            MM(S, (p, p.ap[:, 4:8]), V(ones), (g_, g_.ap[:, 0, :]))
            ACT(S, (g_, g_.ap[:, 1:3, :]), (p, p.ap[:, 0:8].rearrange("p (a b) -> p a b", a=2)), AF.Identity)
            S.dma("pool", C.dn_G[bi, d, :, rws, :].rearrange("q t c -> t q c"), g_.ap[:], reads=[g_])
            ACT(S, Wc(32, 36), (g_, g_.ap[:, 1, :]), AF.Exp)
            TT(S, "dve", Wc(36, 40), (g_, g_.ap[:, 2, :]), (g_, g_.ap[:, 1, :]), ALU.subtract)
            ACT(S, Wc(36, 40), Wc(36, 40), AF.Exp)
            TT(S, "dve", Wc(40, 44), Wc(28, 32), Wc(32, 36), ALU.mult)
            B4 = lambda a: (w, w.ap[:, a:a + 4].unsqueeze(2).to_broadcast([128, 4, 128]))
            O = lambda q: (o, o.ap[:, q, :].rearrange("p (h d) -> p h d", h=4))
            TT(S, "dve", O(0), r3(qs, 4), B4(32), ALU.mult)
            TT(S, "pool", O(1), r3(ks, 4), B4(28), ALU.mult)
            TT(S, "dve", O(2), r3(ks, 4), B4(40), ALU.mult)
            TT(S, "pool", O(3), r3(ks, 4), B4(36), ALU.mult)
            TT(S, "dve", O(4), r3(vs, 4), B4(28), ALU.mult)
            S.dma("sp", C.dn_ops[bi, d, :, rws, :].rearrange("q t c -> t q c"), o.ap[:], reads=[o])
    S.end_sub()


def inv_group(S, P, PT, K, ps, out, nh=4):
    mk = lambda: K.rW.next()
    pA, pB, pC = ps
    PD, PDT, Z, ZT = mk(), mk(), K.rZ.next(), K.rZ.next()
    TT(S, "dve", V(PD), V(P), V(K.mb[0]), ALU.mult)
    TT(S, "pool", V(PDT), V(PT), V(K.mb[0]), ALU.mult)
    TT(S, "dve", V(Z), V(K.ident4), V(PD), ALU.subtract)
    TT(S, "pool", V(ZT), V(K.ident4), V(PDT), ALU.subtract)
    yield
    cur, curT = PD, PDT
    for lv in range(3):
        for h in range(nh):
            MM(S, (pA, pA.ap[:, h, :]), (curT, curT.ap[:, h, :]), (cur, cur.ap[:, h, :]))
        for h in range(nh):
            MM(S, (pB, pB.ap[:, h, :]), (cur, cur.ap[:, h, :]), (curT, curT.ap[:, h, :]))
        Pn, PTn = mk(), mk()
        ACT(S, V(Pn), V(pA), AF.Identity)
        S.op("dve", lambda e: e.tensor_copy(out=PTn.ap[:], in_=pB.ap[:]), reads=[pB], writes=[PTn])
        yield
        for h in range(nh):
            MM(S, (pC, pC.ap[:, h, :]), (PTn, PTn.ap[:, h, :]), (Z, Z.ap[:, h, :]))
        for h in range(nh):
            MM(S, (pA, pA.ap[:, h, :]), (Pn, Pn.ap[:, h, :]), (ZT, ZT.ap[:, h, :]))
        TT(S, "dve", V(Z), V(Z), V(pC), ALU.add)
        TT(S, "dve", V(ZT), V(ZT), V(pA), ALU.add)
        yield
        cur, curT = Pn, PTn
    for m in range(1, 4):
        last = m == 3
        O, OT, Y = mk(), mk(), mk()
        TT(S, "dve", V(O), V(P), V(K.mb[m]), ALU.mult)
        TT(S, "pool", V(OT), V(PT), V(K.mb[m]), ALU.mult)
        for h in range(nh):
            MM(S, (pA, pA.ap[:, h, :]), (OT, OT.ap[:, h, :]), (Z, Z.ap[:, h, :]))
        ACT(S, V(Y), V(pA), AF.Identity)
        if not last:
            YT = mk()
            for h in range(nh):
                MM(S, (pB, pB.ap[:, h, :]), (Z, Z.ap[:, h, :]), (OT, OT.ap[:, h, :]))
            S.op("dve", lambda e: e.tensor_copy(out=YT.ap[:], in_=pB.ap[:]), reads=[pB], writes=[YT])
        yield
        for h in range(nh):
            MM(S, (pC, pC.ap[:, h, :]), (ZT, ZT.ap[:, h, :]), (Y, Y.ap[:, h, :]))
        if not last:
            for h in range(nh):
                MM(S, (pB, pB.ap[:, h, :]), (Y, Y.ap[:, h, :]), (ZT, ZT.ap[:, h, :]))
        TT(S, "dve", V(Z), V(Z), V(pC), ALU.subtract)
        if not last:
            TT(S, "dve", V(ZT), V(ZT), V(pB), ALU.subtract)
        yield
    out.append(Z)


def interleave(gens):
    gens = list(gens)
    while gens:
        for g in list(gens):
            try:
                next(g)
            except StopIteration:
                gens.remove(g)


def scan_consts(S, C, tag):
    K = Ctx()
    K.idb = S.sb(tag + "idb", [128, 128], BF16)
    S.dma("sp", K.idb.ap[:], C.ident_b[:, :], writes=[K.idb])
    K.ident4 = S.sb(tag + "id4", [128, 4, 128], F32)
    K.mSI = [S.sb(tag + f"mSI{d}", [128, 4, 2, 128], F32) for d in range(2)]
    K.mS = [S.sb(tag + f"mS{d}", [128, 4, 128], F32) for d in range(2)]
    K.mI = [S.sb(tag + f"mI{d}", [128, 4, 128], F32) for d in range(2)]
    for h in range(4):
        S.dma("sp", K.ident4.ap[:, h, :], C.ident_d[:, :], writes=[K.ident4])
        for d in range(2):
            S.dma("sp", K.mSI[d].ap[:, h, 0, :], C.tris[d, :, :], writes=[K.mSI[d]])
            S.dma("pool", K.mSI[d].ap[:, h, 1, :], C.tri[d, :, :], writes=[K.mSI[d]])
            S.dma("sp", K.mS[d].ap[:, h, :], C.tris[d, :, :], writes=[K.mS[d]])
            S.dma("pool", K.mI[d].ap[:, h, :], C.tri[d, :, :], writes=[K.mI[d]])
    K.mb = [S.sb(tag + f"mb{m}", [128, 4, 128], F32) for m in range(4)]
    for m in range(4):
        for h in range(4):
            S.dma("sp" if h % 2 else "pool", K.mb[m].ap[:, h, :], C.blkm[m, :, :], writes=[K.mb[m]])
    return K


def chain_res(S, K, tag):
    R = Ctx()
    R.__dict__.update(K.__dict__)
    R.rP = mk_ring(S, tag + "rP", [128, 4, 128], INV_DT, 1)
    R.rPT = mk_ring(S, tag + "rPT", [128, 4, 128], INV_DT, 1)
    R.rW = mk_ring(S, tag + "rW", [128, 4, 128], INV_DT, 8)
    R.rZ = mk_ring(S, tag + "rZ", [128, 4, 128], INV_DT, 4)
    return R


def tile_order(d):
    return list(range(NTT)) if d == 0 else [1, 0] + list(range(NTT - 1, 1, -1))


def l1_scan_rw(S, C, bi):
    S.begin_phase()
    S.keep_pool = True
    K0 = scan_consts(S, C, "r")
    interleave([rw_chain(S, C, chain_res(S, K0, f"r{d}"), bi, d) for d in range(2)])
    S.keep_pool = False
    S.end_phase()


def rw_chain(S, C, K, bi, d):
    t = f"r{d}"
    X0, X1, X2 = [S.ps(t + f"X{n}", [128, 4, 128], F32) for n in range(3)]
    ptr = S.ps(t + "ptr", [64, 8, 128], BF16)
    ot_r = mk_ring(S, t + "ot", [128, 6, 512], BF16, 2)
    vt_r = mk_ring(S, t + "vt", [128, 512], BF16, 2)
    gc_r = mk_ring(S, t + "gc", [64, 8], F32, 2)
    AR_r = mk_ring(S, t + "AR", [64, 8, 2, 128], BF16, 1)
    KT_r = mk_ring(S, t + "KT", [64, 8, 128], BF16, 1)
    BT_r = mk_ring(S, t + "BT", [64, 8, 128], BF16, 1)
    KN_r = mk_ring(S, t + "KN", [128, 8, 2, 128], BF16, 1)
    NB_r = mk_ring(S, t + "NB", [128, 8, 128], BF16, 1)
    Zb_r = mk_ring(S, t + "Zb", [128, 4, 128], BF16, 1)
    WT_r = mk_ring(S, t + "WT", [64, 8, 128], BF16, 1)
    Xs_r = mk_ring(S, t + "Xs", [128, 4, 64], BF16, 1)
    nU0_r = mk_ring(S, t + "nU0", [128, 8, 64], F32, 1)
    nU_r = mk_ring(S, t + "nU", [128, 8, 64], BF16, 1)
    ys_r = mk_ring(S, t + "ys", [128, 512], F32, 2)
    ST = S.sb(t + "ST", [64, 8, 64], F32)
    STb = S.sb(t + "STb", [64, 8, 64], BF16)
    S.op("dve", lambda e: e.memset(ST.ap[:], 0.0), writes=[ST])
    S.op("dve", lambda e: e.memset(STb.ap[:], 0.0), writes=[STb])
    v8 = lambda p: p.ap[:].rearrange("p a b -> p (a b)").rearrange("p (h v) -> p h v", v=64)
    for i in tile_order(d):
        rws = slice(i * 128, (i + 1) * 128)
        ot, vt, gc = ot_r.next(), vt_r.next(), gc_r.next()
        S.dma("sp", ot.ap[:], C.rw_ops[bi, d, :, rws, :].rearrange("q t c -> t q c"), writes=[ot])
        S.dma("pool", vt.ap[:], C.rw_v[bi, rws, :], writes=[vt])
        S.dma("pool", gc.ap[:], C.rw_gc[bi, d, i, :, :], writes=[gc])
        AR, KT, BT = AR_r.next(), KT_r.next(), BT_r.next()
        for q, dst in ((3, (AR, AR.ap[:, :, 0, :])), (0, (AR, AR.ap[:, :, 1, :])), (1, V(KT)), (2, V(BT))):
            for h in range(8):
                TR(S, (ptr, ptr.ap[:, h, :]), (ot, ot.ap[:, q, h * 64:(h + 1) * 64]), V(K.idb))
            if q in (3, 1):
                ACT(S, dst, V(ptr), AF.Identity)
            else:
                S.op("dve", lambda e: e.tensor_copy(out=dst[1], in_=ptr.ap[:]), reads=[ptr], writes=[dst[0]])
            yield
        KN, NBm, WT, nU0 = KN_r.next(), NB_r.next(), WT_r.next(), nU0_r.next()
        for g in range(2):
            hs = [(hl, g * 4 + hl) for hl in range(4)]
            P, PT = K.rP.next(), K.rPT.next()
            for hl, h in hs:
                MM(S, (X0, X0.ap[:, hl, :]), (BT, BT.ap[:, h, :]), (AR, AR.ap[:, h, 0, :]))
            for hl, h in hs:
                MM(S, (X1, X1.ap[:, hl, :]), (AR, AR.ap[:, h, 0, :]), (BT, BT.ap[:, h, :]))
            for hl, h in hs:
                MM(S, (X2, X2.ap[:, hl, :]), (BT, BT.ap[:, h, :]), (AR, AR.ap[:, h, 1, :]))
            TT(S, "dve", V(P), V(X0), V(K.mS[d]), ALU.mult)
            TT(S, "dve", V(PT), V(X1), V(K.mS[1 - d]), ALU.mult)
            TT(S, "dve", (NBm, NBm.ap[:, g * 4:(g + 1) * 4, :]), V(X2), V(K.mI[d]), ALU.mult)
            yield
            for hl, h in hs:
                MM(S, (X0, X0.ap[:, hl, :]), (KT, KT.ap[:, h, :]), (AR, AR.ap[:, h, 0, :]))
            for hl, h in hs:
                MM(S, (X1, X1.ap[:, hl, :]), (KT, KT.ap[:, h, :]), (AR, AR.ap[:, h, 1, :]))
            TT(S, "dve", (KN, KN.ap[:, g * 4:(g + 1) * 4, 0, :]), V(X0), V(K.mS[d]), ALU.mult)
            TT(S, "dve", (KN, KN.ap[:, g * 4:(g + 1) * 4, 1, :]), V(X1), V(K.mI[d]), ALU.mult)
            yield
            zo = []
            yield from inv_group(S, P, PT, K, (X0, X1, X2), zo)
            Zb = zo[0]
            for hl, h in hs:
                MM(S, (X0, X0.ap[:, hl, 0:64]), (KN, KN.ap[:, h, 0, :]), (vt, vt.ap[:, h * 64:(h + 1) * 64]))
            Xs = Xs_r.next()
            ACT(S, V(Xs), (X0, X0.ap[:, :, 0:64]), AF.Identity)
            yield
            for hl, h in hs:
                MM(S, (X1, X1.ap[:64, hl, :]), (ot, ot.ap[:, 3, h * 64:(h + 1) * 64]), (Zb, Zb.ap[:, hl, :]))
            ACT(S, (WT, WT.ap[:, g * 4:(g + 1) * 4, :]), (X1, X1.ap[:64, :, :]), AF.Identity)
            for hl, h in hs:
                MM(S, (X2, X2.ap[:, hl, 0:64]), (Zb, Zb.ap[:, hl, :]), (Xs, Xs.ap[:, hl, :]))
            TS(S, "dve", (nU0, nU0.ap[:, g * 4:(g + 1) * 4, :]), (X2, X2.ap[:, :, 0:64]), -1.0, None, ALU.mult)
            yield
        for h in range(8):
            MM(S, (X0, v8(X0)[:, h, :]), (WT, WT.ap[:, h, :]), (STb, STb.ap[:, h, :]))
        nU = nU_r.next()
        TT(S, "dve", V(nU), V(nU0), (X0, v8(X0)), ALU.subtract)
        yield
        for h in range(8):
            MM(S, (X1, v8(X1)[:, h, :]), (AR, AR.ap[:, h, 1, :]), (STb, STb.ap[:, h, :]), start=True, stop=False)
            MM(S, (X1, v8(X1)[:, h, :]), (KN, KN.ap[:, h, 1, :]), (vt, vt.ap[:, h * 64:(h + 1) * 64]), start=False, stop=False)
            MM(S, (X1, v8(X1)[:, h, :]), (NBm, NBm.ap[:, h, :]), (nU, nU.ap[:, h, :]), start=False, stop=True)
        ys = ys_r.next()
        ACT(S, r3(ys, 8), (X1, v8(X1)), AF.Identity)
        S.dma("sp", C.y_rw[bi, d, rws, :], ys.ap[:], reads=[ys])
        for h in range(8):
            MM(S, (X2, v8(X2)[:64, h, :]), (ot, ot.ap[:, 4, h * 64:(h + 1) * 64]), (vt, vt.ap[:, h * 64:(h + 1) * 64]), start=True, stop=False)
            MM(S, (X2, v8(X2)[:64, h, :]), (ot, ot.ap[:, 5, h * 64:(h + 1) * 64]), (nU, nU.ap[:, h, :]), start=False, stop=True)
        TT(S, "dve", V(ST), V(ST), (gc, gc.ap[:, 0:8].unsqueeze(2).to_broadcast([64, 8, 64])), ALU.mult)
        TT(S, "dve", V(ST), V(ST), (X2, v8(X2)[:64, :, :]), ALU.add)
        ACT(S, V(STb), V(ST), AF.Identity)
        yield


def l1_scan_dn(S, C, bi):
    S.begin_phase()
    S.keep_pool = True
    K0 = scan_consts(S, C, "d")
    K0.ones = S.sb("dones", [128, 128], F32)
    S.op("pool", lambda e: e.memset(K0.ones.ap[:], 1.0), writes=[K0.ones])
    K0.tri = S.sb("dtri", [128, 2, 128], F32)
    for d in range(2):
        S.dma("sp", K0.tri.ap[:, d, :], C.tri[d, :, :], writes=[K0.tri])
    interleave([dn_chain(S, C, chain_res(S, K0, f"d{d}"), bi, d) for d in range(2)])
    S.keep_pool = False
    S.end_phase()


def dn_chain(S, C, K, bi, d):
    t = f"d{d}"
    ones, tri = K.ones, K.tri
    X0, X1, X2 = [S.ps(t + f"X{n}", [128, 4, 128], F32) for n in range(3)]
    ptr = S.ps(t + "ptr", [128, 4, 128], BF16)
    ot_r = mk_ring(S, t + "ot", [128, 5, 512], BF16, 2)
    qk_r = mk_ring(S, t + "qk", [128, 2, 512], BF16, 2)
    G_r = mk_ring(S, t + "G", [128, 3, 4], F32, 2)
    FT_r = mk_ring(S, t + "FT", [128, 4, 4, 128], BF16, 2)
    gl_r = mk_ring(S, t + "gl", [128, 4, 128], F32, 1)
    ET_r = mk_ring(S, t + "ET", [128, 4, 128], F32, 1)
    qkT_r = mk_ring(S, t + "qkT", [128, 4, 128], BF16, 2)
    Zb_r = mk_ring(S, t + "Zb", [128, 4, 128], BF16, 2)
    wT_r = mk_ring(S, t + "wT", [128, 4, 128], BF16, 2)
    u0_r = mk_ring(S, t + "u0", [128, 4, 128], F32, 2)
    u_r = mk_ring(S, t + "u", [128, 4, 128], BF16, 2)
    ys_r = mk_ring(S, t + "ys", [128, 512], F32, 2)
    sm_r = mk_ring(S, t + "sm", [128, 8], F32, 2)
    ST = S.sb(t + "ST", [128, 4, 128], F32)
    STb = S.sb(t + "STb", [128, 4, 128], BF16)
    S.op("dve", lambda e: e.memset(ST.ap[:], 0.0), writes=[ST])
    S.op("dve", lambda e: e.memset(STb.ap[:], 0.0), writes=[STb])
    for i in tile_order(d):
        rws = slice(i * 128, (i + 1) * 128)
        ot, qk, G = ot_r.next(), qk_r.next(), G_r.next()
        S.dma("sp", ot.ap[:], C.dn_ops[bi, d, :, rws, :].rearrange("q t c -> t q c"), writes=[ot])
        S.dma("pool", qk.ap[:], C.dn_qk[bi, :, rws, :].rearrange("q t c -> t q c"), writes=[qk])
        S.dma("pool", G.ap[:], C.dn_G[bi, d, :, rws, :].rearrange("q t c -> t q c"), writes=[G])
        FT = FT_r.next()
        for q, src in enumerate(((qk, 1), (qk, 0), (ot, 1), (ot, 0))):
            for h in range(4):
                TR(S, (ptr, ptr.ap[:, h, :]), (src[0], src[0].ap[:, src[1], h * 128:(h + 1) * 128]), V(K.idb))
            if q % 2:
                ACT(S, (FT, FT.ap[:, q, :, :]), V(ptr), AF.Identity)
            else:
                S.op("dve", lambda e: e.tensor_copy(out=FT.ap[:, q, :, :], in_=ptr.ap[:]), reads=[ptr], writes=[FT])
            yield
        gl = gl_r.next()
        for h in range(4):
            TS(S, "pool", (gl, gl.ap[:, h, :]), (tri, tri.ap[:, d, :]), G.ap[:, 0, h:h + 1], None, ALU.mult, extra=[G])
        for h in range(4):
            MM(S, (X0, X0.ap[:, h, :]), V(ones), (gl, gl.ap[:, h, :]))
        ET = ET_r.next()
        for h in range(4):
            TS(S, "dve", (ET, ET.ap[:, h, :]), (X0, X0.ap[:, h, :]), G.ap[:, 1, h:h + 1], 0.0, ALU.subtract, ALU.min, extra=[G])
        ACT(S, V(ET), V(ET), AF.Exp)
        yield
        for h in range(4):
            MM(S, (X1, X1.ap[:, h, :]), (FT, FT.ap[:, 0, h, :]), (FT, FT.ap[:, 2, h, :]))
            MM(S, (X2, X2.ap[:, h, :]), (FT, FT.ap[:, 0, h, :]), (FT, FT.ap[:, 1, h, :]))
        P, PT = K.rP.next(), K.rPT.next()
        TT(S, "dve", V(P), V(X1), V(ET), ALU.mult)
        TT(S, "dve", V(P), V(P), V(K.mS[d]), ALU.mult)
        qkT = qkT_r.next()
        TT(S, "pool", V(ET), V(ET), V(K.mI[d]), ALU.mult)
        TT(S, "dve", V(qkT), V(X2), V(ET), ALU.mult)
        yield
        for h in range(4):
            TR(S, (ptr, ptr.ap[:, h, :]), (P, P.ap[:, h, :]), V(K.idb))
        ACT(S, V(PT), V(ptr), AF.Identity)
        yield
        zo = []
        yield from inv_group(S, P, PT, K, (X0, X1, X2), zo)
        Zb = zo[0]
        for h in range(4):
            MM(S, (X0, X0.ap[:, h, :]), (Zb, Zb.ap[:, h, :]), (ot, ot.ap[:, 4, h * 128:(h + 1) * 128]))
            MM(S, (X1, X1.ap[:, h, :]), (ot, ot.ap[:, 2, h * 128:(h + 1) * 128]), (Zb, Zb.ap[:, h, :]))
        u0, wT = u0_r.next(), wT_r.next()
        ACT(S, V(u0), V(X0), AF.Identity)
        S.op("dve", lambda e: e.tensor_copy(out=wT.ap[:], in_=X1.ap[:]), reads=[X1], writes=[wT])
        yield
        for h in range(4):
            MM(S, (X2, X2.ap[:, h, :]), (wT, wT.ap[:, h, :]), (STb, STb.ap[:, h, :]))
        u = u_r.next()
        TT(S, "dve", V(u), V(u0), V(X2), ALU.subtract)
        yield
        for h in range(4):
            MM(S, (X0, X0.ap[:, h, :]), (FT, FT.ap[:, 3, h, :]), (STb, STb.ap[:, h, :]), start=True, stop=False)
            MM(S, (X0, X0.ap[:, h, :]), (qkT, qkT.ap[:, h, :]), (u, u.ap[:, h, :]), start=False, stop=True)
        ys = ys_r.next()
        ACT(S, r3(ys, 4), V(X0), AF.Identity)
        S.dma("sp", C.y_dn[bi, d, rws, :], ys.ap[:], reads=[ys])
        for h in range(4):
            MM(S, (X1, X1.ap[:, h, :]), (ot, ot.ap[:, 3, h * 128:(h + 1) * 128]), (u, u.ap[:, h, :]))
        sm = sm_r.next()
        ACT(S, (sm, sm.ap[:, 0:4]), (G, G.ap[:, 2, :]), AF.Exp)
        TT(S, "dve", V(ST), V(ST), (sm, sm.ap[:, 0:4].unsqueeze(2).to_broadcast([128, 4, 128])), ALU.mult)
        TT(S, "dve", V(ST), V(ST), V(X1), ALU.add)
        ACT(S, V(STb), V(ST), AF.Identity)
        yield


def l1_out(S, C, bi, src_stage, dst_stage, last):
    S.begin_phase()
    wo = S.sb("xwo", [128, 8, D], BF16)
    for c in range(8):
        S.dma("sp" if c % 2 else "pool", wo.ap[:, c, :], C.wob1[c * 128:(c + 1) * 128, :], writes=[wo])
    rows = S.sb("xrows", [128, 3, 512], F32)
    S.dma("sp", rows.ap[:, 0, :], C.rw_rows[7, :].partition_broadcast(128), writes=[rows])
    S.dma("sp", rows.ap[:, 1, :], C.rw_rows[8, :].partition_broadcast(128), writes=[rows])
    S.dma("sp", rows.ap[:, 2, :], C.dn_ng[:].partition_broadcast(128), writes=[rows])
    epi = Epi(S, C, "x")
    epi.load(1, 0, bi)
    xin = mk_ring(S, "xxin", [128, 1024], F32, 2)
    ya = mk_ring(S, "xya", [128, 2, 512], F32, 2)
    yb = mk_ring(S, "xyb", [128, 2, 512], F32, 2)
    ex = mk_ring(S, "xex", [128, 3, 512], F32, 2)
    sq_r = mk_ring(S, "xsq", [128, 512], F32, 2)
    ycat = mk_ring(S, "xyc", [128, 1024], F32, 2)
    yT = mk_ring(S, "xyT", [128, 8, 128], BF16, 2)
    sm = mk_ring(S, "xsm", [128, 32], F32, 2)
    ptr = mk_ring(S, "xptr", [128, 4, 128], F32, 2, psum=True)
    py = mk_ring(S, "xpy", [128, 1024], F32, 2, psum=True)
    def out_chain(i):
        rws = slice(i * 128, (i + 1) * 128)
        xi, a, b, e_, yc, w, sq = xin.next(), ya.next(), yb.next(), ex.next(), ycat.next(), sm.next(), sq_r.next()
        S.dma("sp", xi.ap[:], tile_src(C, src_stage, bi, i), writes=[xi])
        S.dma("sp", a.ap[:], C.y_rw[bi, :, rws, :].rearrange("q t c -> t q c"), writes=[a])
        S.dma("pool", b.ap[:], C.y_dn[bi, :, rws, :].rearrange("q t c -> t q c"), writes=[b])
        S.dma("sp", e_.ap[:, 0, :], C.rw_g[bi, rws, :], writes=[e_])
        S.dma("pool", e_.ap[:, 1, :], C.rw_bonus[bi, rws, :], writes=[e_])
        S.dma("sp", e_.ap[:, 2, :], C.dn_z[bi, rws, :], writes=[e_])
        y = (a, a.ap[:, 0, :])
        TT(S, "dve", y, y, (a, a.ap[:, 1, :]), ALU.add)
        S.op("dve", lambda e: e.tensor_reduce(out=w.ap[:, 0:8], in_=r3(a, 8, slice(None), 0, slice(None))[1], axis=AX.X, op=ALU.add), reads=[a], writes=[w])
        TS(S, "dve", (w, w.ap[:, 0:8]), (w, w.ap[:, 0:8]), 1.0 / 64, None, ALU.mult)
        TT(S, "dve", r3(a, 8, slice(None), 0, slice(None)), r3(a, 8, slice(None), 0, slice(None)), bc3(w, 8, 64), ALU.subtract)
        TT(S, "pool", V(sq), y, y, ALU.mult)
        S.op("dve", lambda e: e.tensor_reduce(out=w.ap[:, 8:16], in_=r3(sq, 8)[1], axis=AX.X, op=ALU.add), reads=[sq], writes=[w])
        TS(S, "dve", (w, w.ap[:, 8:16]), (w, w.ap[:, 8:16]), 1.0 / 64, 64e-5, ALU.mult, ALU.add)
        ACT(S, (w, w.ap[:, 8:16]), (w, w.ap[:, 8:16]), AF.Sqrt)
        S.op("dve", lambda e: e.reciprocal(out=w.ap[:, 8:16], in_=w.ap[:, 8:16]), reads=[w], writes=[w])
        TT(S, "dve", r3(a, 8, slice(None), 0, slice(None)), r3(a, 8, slice(None), 0, slice(None)),
           (w, w.ap[:, 8:16].unsqueeze(2).to_broadcast([128, 8, 64])), ALU.mult)
        TT(S, "pool", y, y, (rows, rows.ap[:, 0, :]), ALU.mult)
        TT(S, "pool", y, y, (rows, rows.ap[:, 1, :]), ALU.add)
        TT(S, "dve", y, y, (e_, e_.ap[:, 1, :]), ALU.add)
        TT(S, "dve", (yc, yc.ap[:, 0:512]), y, (e_, e_.ap[:, 0, :]), ALU.mult)
        yield
        o = (b, b.ap[:, 0, :])
        TT(S, "dve", o, o, (b, b.ap[:, 1, :]), ALU.add)
        TT(S, "pool", V(sq), o, o, ALU.mult)
        S.op("dve", lambda e: e.tensor_reduce(out=w.ap[:, 16:20], in_=r3(sq, 4)[1], axis=AX.X, op=ALU.add), reads=[sq], writes=[w])
        TS(S, "dve", (w, w.ap[:, 16:20]), (w, w.ap[:, 16:20]), 1.0 / 128, 1e-6, ALU.mult, ALU.add)
        ACT(S, (w, w.ap[:, 16:20]), (w, w.ap[:, 16:20]), AF.Sqrt)
        S.op("dve", lambda e: e.reciprocal(out=w.ap[:, 16:20], in_=w.ap[:, 16:20]), reads=[w], writes=[w])
        TT(S, "dve", r3(b, 4, slice(None), 0, slice(None)), r3(b, 4, slice(None), 0, slice(None)),
           (w, w.ap[:, 16:20].unsqueeze(2).to_broadcast([128, 4, 128])), ALU.mult)
        TT(S, "pool", o, o, (rows, rows.ap[:, 2, :]), ALU.mult)
        TT(S, "dve", (yc, yc.ap[:, 512:1024]), o, (e_, e_.ap[:, 2, :]), ALU.mult)
        yield
        yt = yT.next()
        for g in range(2):
            p = ptr.next()
            for j in range(4):
                c = g * 4 + j
                S.op("pe", lambda e: e.transpose(out=p.ap[:, j, :], in_=yc.ap[:, c * 128:(c + 1) * 128], identity=C.ident.ap[:]), reads=[yc, C.ident], writes=[p])
            ACT(S, (yt, yt.ap[:, g * 4:(g + 1) * 4, :]), V(p), AF.Identity)
        yield
        p = py.next()
        for hf in range(2):
            for c in range(8):
                MM(S, (p, p.ap[:, hf * 512:(hf + 1) * 512]), (yt, yt.ap[:, c, :]), (wo, wo.ap[:, c, hf * 512:(hf + 1) * 512]), start=(c == 0), stop=(c == 7))
        if last:
            dst = C.xs[dst_stage - 1][bi, rws, :]
        else:
            dst = C.xs[dst_stage - 1][bi, rws, :]
        epi.run(p, xi, i < 2, dst)
        yield

    tl = list(range(2 if last else 0, NTT))
    for k in range(0, len(tl), 2):
        interleave([out_chain(i) for i in tl[k:k + 2]])
    S.end_phase()


NB_FULL = 4
N_CORES = 8


def build_full(NB, dbg=None):
    nc = bass.Bass("TRN2", target_bir_lowering=False)
    C = declare_common(nc, NB, dbg=dbg)
    declare_l0(nc, C, dbg=dbg)
    declare_l1(nc, C, dbg=dbg)
    S = Sched(nc)
    common_setup(S, C)
    phase_mod(S, C)
    for l in range(2):
        prep_weight(S, C.w1b[l], C.mlp_w1[l], D, DFF, tag=f"pm1{l}")
        prep_weight(S, C.w2b[l], C.mlp_w2[l], DFF, D, tag=f"pm2{l}")
    l0_setup(S, C)
    l1_setup(S, C)
    for bi in range(NB):
        l0_inproj(S, C, bi, 0)
        l0_hyena(S, C, bi)
        l0_attn(S, C, bi)
        l0_outproj(S, C, bi, 0, 1)
        phase_mlp(S, C, 0, bi, 1, 2, False)
        S.begin_phase()
        hT = build_hT(S, C, bi, 2, 1)
        l1_feat_rw(S, C, bi, hT)
        l1_feat_dn(S, C, bi, hT)
        S.end_phase()
        l1_scan_rw(S, C, bi)
        l1_scan_dn(S, C, bi)
        l1_out(S, C, bi, 2, 3, True)
        phase_mlp(S, C, 1, bi, 3, None, True)
    S.finish()
    return nc


def kernel(**inputs):
    NB = NB_FULL
    nc = build_full(NB)
    in_maps = [host_l1(inputs, host_l0(inputs, host_common(inputs, c, NB))) for c in range(N_CORES)]
    res = run_bass_kernel_spmd(nc, in_maps, core_ids=list(range(N_CORES)))
    out = np.concatenate([np.asarray(r["out"], dtype=np.float32) for r in res.results], axis=0)
    return out
```

```python
import numpy as np
from contextlib import ExitStack
import concourse.bass as bass
import concourse.mybir as mybir
from concourse.bass_utils import run_bass_kernel_spmd

F32 = mybir.dt.float32
BF16 = mybir.dt.bfloat16
AF = mybir.ActivationFunctionType
ALU = mybir.AluOpType
AX = mybir.AxisListType

SAME_ENGINE_SYNC = True
NO_SWDGE = True
POOL_TO_DVE = True
EPOCH = 30000


class Buf:
    __slots__ = ("ap", "name", "lw", "rd")

    def __init__(self, ap, name):
        self.ap = ap
        self.name = name
        self.lw = None
        self.rd = []


class Sched:
    def __init__(self, nc, ndma=24):
        self.nc = nc
        self.stack = ExitStack()
        self.engs = {"pe": nc.tensor, "act": nc.scalar, "dve": nc.vector, "pool": nc.gpsimd, "sp": nc.sync}
        self.csem = {}
        self.ccnt = {}
        self.nsem = 0
        for e in ("pe", "act", "dve", "pool"):
            self._new_csem(e)
        self.dq = {}
        for q in ("sp", "pool", "act"):
            self.dq[q] = [[self._sem(f"d_{q}{i}"), 0] for i in range(ndma)]
        self.dqi = {q: 0 for q in self.dq}
        self.waited = {}
        self.phase_stack = None
        self.ninst = 0

    def _sem(self, name):
        self.nsem += 1
        return self.stack.enter_context(self.nc.semaphore(name))

    def _new_csem(self, e):
        self.csem[e] = self._sem(f"c_{e}_{self.nsem}")
        self.ccnt[e] = 0

    def sb(self, name, shape, dtype, persist=False):
        st = self.stack if (persist or self.phase_stack is None) else self.phase_stack
        self.nsem += 0
        self.uid = getattr(self, "uid", 0) + 1
        t = st.enter_context(self.nc.sbuf_tensor(f"{name}_{self.uid}", list(shape), dtype))
        return Buf(t, name)

    def ps(self, name, shape, dtype, persist=False):
        st = self.stack if (persist or self.phase_stack is None) else self.phase_stack
        self.uid = getattr(self, "uid", 0) + 1
        t = st.enter_context(self.nc.psum_tensor(f"{name}_{self.uid}", list(shape), dtype))
        return Buf(t, name)

    def view(self, ap, name="v"):
        return Buf(ap, name)

    def begin_sub(self):
        if not hasattr(self, "sub_stk"):
            self.sub_stk = []
        self.sub_stk.append(self.phase_stack)
        self.phase_stack = ExitStack()

    def end_sub(self):
        self.barrier()
        self.phase_stack.close()
        self.phase_stack = self.sub_stk.pop()

    def begin_phase(self):
        assert self.phase_stack is None
        self.phase_stack = ExitStack()

    def end_phase(self):
        self.barrier()
        self.phase_stack.close()
        self.phase_stack = None

    def _wait(self, F, tok):
        sem, val, eng = tok
        if eng == F == "pe":
            return
        if eng == F and not SAME_ENGINE_SYNC:
            return
        key = (F, id(sem))
        if self.waited.get(key, 0) >= val:
            return
        self.engs[F].wait_ge(sem, val)
        self.waited[key] = val
        self.ninst += 1

    def _deps(self, F, reads, writes):
        for b in reads:
            if b.lw is not None:
                self._wait(F, b.lw)
        for b in writes:
            if b.lw is not None:
                self._wait(F, b.lw)
            for t in b.rd:
                self._wait(F, t)

    def _commit(self, tok, reads, writes):
        for b in reads:
            if tok[2] == "dma":
                b.rd.append(tok)
            else:
                b.rd = [t for t in b.rd if t[2] != tok[2]]
                b.rd.append(tok)
        for b in writes:
            b.lw = tok
            b.rd = []

    def op(self, F, fn, reads=(), writes=()):
        if F == "pool" and POOL_TO_DVE and not getattr(self, "keep_pool", False):
            F = "dve"
        self._deps(F, reads, writes)
        if self.ccnt[F] >= EPOCH:
            self._new_csem(F)
        inst = fn(self.engs[F])
        self.ccnt[F] += 1
        inst.then_inc(self.csem[F], 1)
        tok = (self.csem[F], self.ccnt[F], F)
        self._commit(tok, reads, writes)
        self.ninst += 1
        return tok

    def dma(self, Q, out, in_, reads=(), writes=(), **kw):
        if NO_SWDGE:
            Q = "sp"
        self._deps(Q, reads, writes)
        pool = self.dq[Q]
        i = self.dqi[Q]
        self.dqi[Q] = (i + 1) % len(pool)
        sem, val = pool[i]
        if val > 0:
            self._wait(Q, (sem, val, "dma"))
        inst = self.engs[Q].dma_start(out=out, in_=in_, **kw)
        inst.then_inc(sem, 16)
        pool[i][1] = val + 16
        tok = (sem, val + 16, "dma")
        self._commit(tok, reads, writes)
        self.ninst += 1
        return tok

    def barrier(self, engines=("pe", "act", "dve", "pool", "sp")):
        toks = []
        for e in ("pe", "act", "dve", "pool"):
            if self.ccnt[e] > 0:
                toks.append((self.csem[e], self.ccnt[e], "bar"))
        for q in self.dq:
            for sem, val in self.dq[q]:
                if val > 0:
                    toks.append((sem, val, "dma"))
        for F in engines:
            for t in toks:
                self._wait(F, t)

    def finish(self):
        self.barrier()
        self.stack.close()


D = 1024
LAT = 2048
CTX = 256
T = LAT + CTX
NTT = T // 128
DFF = 4096
ALPHA = (2.0 * 2) ** 0.25
LN_EPS = 1e-5


class Ring:
    def __init__(self, bufs):
        self.bufs = bufs
        self.i = 0

    def next(self):
        b = self.bufs[self.i]
        self.i = (self.i + 1) % len(self.bufs)
        return b


def mk_ring(S, name, shape, dtype, n=2, psum=False):
    f = S.ps if psum else S.sb
    return Ring([f(f"{name}{i}", shape, dtype) for i in range(n)])


class Ctx:
    pass


def prep_weight(S, dst, src, K, N, scale=None, tag="pw"):
    S.begin_phase()
    CB = min(N, 2048)
    rin = mk_ring(S, tag + "i", [128, CB], F32, 2)
    rout = mk_ring(S, tag + "o", [128, CB], BF16, 2)
    sc = S.sb(tag + "s", [128, CB], F32) if scale is not None else None
    for c0 in range(0, N, CB):
        cw = min(CB, N - c0)
        if scale is not None:
            S.dma("sp", sc.ap[:, :cw], scale[c0:c0 + cw].partition_broadcast(128), writes=[sc])
        for k0 in range(0, K, 128):
            kw = min(128, K - k0)
            a = rin.next()
            o = rout.next()
            S.dma("sp", a.ap[:kw, :cw], src[k0:k0 + kw, c0:c0 + cw], writes=[a])
            if scale is not None:
                S.op("dve", lambda e: e.tensor_tensor(out=o.ap[:kw, :cw], in0=a.ap[:kw, :cw], in1=sc.ap[:kw, :cw], op=ALU.mult),
                     reads=[a, sc], writes=[o])
            else:
                S.op("dve", lambda e: e.tensor_copy(out=o.ap[:kw, :cw], in_=a.ap[:kw, :cw]), reads=[a], writes=[o])
            S.dma("pool", dst[k0:k0 + kw, c0:c0 + cw], o.ap[:kw, :cw], reads=[o])
    S.end_phase()


def phase_mod(S, C):
    nc, R = C.nc, C.R
    S.begin_phase()
    cT = S.sb("cT", [128, 8, R], F32)
    S.dma("sp", cT.ap[:], C.cT[:, :, :], writes=[cT])
    sig = S.sb("sig", [128, 8, R], F32)
    scT = S.sb("scT", [128, 8, R], BF16)
    S.op("act", lambda e: e.activation(out=sig.ap[:], in_=cT.ap[:], func=AF.Sigmoid), reads=[cT], writes=[sig])
    S.op("dve", lambda e: e.tensor_tensor(out=scT.ap[:], in0=cT.ap[:], in1=sig.ap[:], op=ALU.mult), reads=[cT, sig], writes=[scT])
    scbc = S.sb("scbc", [128, 8, R, 128], BF16)
    for kc in range(8):
        for r in range(R):
            S.op("dve", lambda e: e.tensor_copy(out=scbc.ap[:, kc, r, :], in_=scT.ap[:, kc, r:r + 1].to_broadcast([128, 128])),
                 reads=[scT], writes=[scbc])
    mb = S.sb("mb", [128, 2, 48], F32)
    S.dma("sp", mb.ap[:], C.mod_bT[:, :, :], writes=[mb])
    wst = mk_ring(S, "mws", [128, 3072], F32, 2)
    wbf = S.sb("mwbf", [128, 8, 6144], BF16)
    pacc = mk_ring(S, "mps", [128, 512], F32, 2, psum=True)
    gt = mk_ring(S, "mgt", [128, 512], F32, 2)
    mbr = S.sb("mbr", [128, 2048], F32)
    for l in range(2):
        for kc in range(8):
            for hf in range(2):
                a = wst.next()
                S.dma("sp" if hf == 0 else "pool", a.ap[:], C.mod_w[l, kc * 128:(kc + 1) * 128, hf * 3072:(hf + 1) * 3072], writes=[a])
                S.op("dve" if hf == 0 else "act",
                     (lambda e: e.tensor_copy(out=wbf.ap[:, kc, hf * 3072:(hf + 1) * 3072], in_=a.ap[:])) if hf == 0 else
                     (lambda e: e.activation(out=wbf.ap[:, kc, hf * 3072:(hf + 1) * 3072], in_=a.ap[:], func=AF.Identity)),
                     reads=[a], writes=[wbf])
        for fc in range(48):
            p = pacc.next()
            for kc in range(8):
                S.op("pe", lambda e: e.matmul(p.ap[:, :R], wbf.ap[:, kc, fc * 128:(fc + 1) * 128], scT.ap[:, kc, :],
                                              start=(kc == 0), stop=(kc == 7)), reads=[wbf, scT], writes=[p])
            is_scale = (fc // 8) in (1, 4)
            S.op("dve", lambda e: e.tensor_scalar(out=C.modT.ap[:, l, fc, :], in0=p.ap[:, :R], scalar1=mb.ap[:, l, fc:fc + 1],
                                                  scalar2=(1.0 if is_scale else 0.0), op0=ALU.add, op1=ALU.add),
                 reads=[p, mb], writes=[C.modT])
        for gi, c0 in enumerate((2048, 5120)):
            S.dma("sp", mbr.ap[:, gi * 1024:(gi + 1) * 1024], C.mod_b[l, c0:c0 + 1024].partition_broadcast(128), writes=[mbr])
        for gi, c0 in enumerate((2048, 5120)):
            for r in range(R):
                for hf in range(2):
                    p = pacc.next()
                    for kc in range(8):
                        S.op("pe", lambda e: e.matmul(p.ap[:], scbc.ap[:, kc, r, :], wbf.ap[:, kc, c0 + hf * 512:c0 + (hf + 1) * 512],
                                                      start=(kc == 0), stop=(kc == 7)), reads=[scbc, wbf], writes=[p])
                    g = gt.next()
                    S.op("dve", lambda e: e.tensor_tensor(out=g.ap[:], in0=p.ap[:], in1=mbr.ap[:, gi * 1024 + hf * 512:gi * 1024 + (hf + 1) * 512], op=ALU.add),
                         reads=[p, mbr], writes=[g])
                    S.dma("pool", C.gbc[l, gi, r, :, hf * 512:(hf + 1) * 512], g.ap[:], reads=[g])
    S.end_phase()


def tile_src(C, stage, bi, i):
    if stage == 0:
        if i < 2:
            return C.ctx[bi, i * 128:(i + 1) * 128, :]
        return C.x[bi, (i - 2) * 128:(i - 1) * 128, :]
    return C.xs[stage - 1][bi, i * 128:(i + 1) * 128, :]


def rstd_op(S, mv, o, i, eps):
    S.op("dve", lambda e: e.tensor_scalar(out=mv.ap[:, o:o + 1], in0=mv.ap[:, i:i + 1], scalar1=eps, scalar2=None, op0=ALU.add), reads=[mv], writes=[mv])
    S.op("act", lambda e: e.activation(out=mv.ap[:, o:o + 1], in_=mv.ap[:, o:o + 1], func=AF.Sqrt), reads=[mv], writes=[mv])
    S.op("dve", lambda e: e.reciprocal(out=mv.ap[:, o:o + 1], in_=mv.ap[:, o:o + 1]), reads=[mv], writes=[mv])


class Epi:
    def __init__(self, S, C, tag):
        self.S, self.C = S, C
        self.gb = [S.sb(tag + "gb0", [128, 1024], F32), S.sb(tag + "gb1", [128, 1024], F32)]
        self.lg = S.sb(tag + "lg", [128, 1024], F32)
        self.lb = S.sb(tag + "lb", [128, 1024], F32)
        self.t1 = mk_ring(S, tag + "t1", [128, 1024], F32, 2)
        self.xo = mk_ring(S, tag + "xo", [128, 1024], F32, 2)
        self.st = mk_ring(S, tag + "st", [128, 2, 6], F32, 2)
        self.mv = mk_ring(S, tag + "mv", [128, 4], F32, 2)

    def load(self, l, sub, bi):
        S, C = self.S, self.C
        S.dma("sp", self.gb[0].ap[:], C.gbc[l, sub, bi, :, :], writes=[self.gb[0]])
        S.dma("sp", self.gb[1].ap[:], C.gbc[l, sub, C.R - 1, :, :], writes=[self.gb[1]])
        S.dma("sp", self.lg.ap[:], C.ln_g[l, sub, :].partition_broadcast(128), writes=[self.lg])
        S.dma("sp", self.lb.ap[:], C.ln_b[l, sub, :].partition_broadcast(128), writes=[self.lb])

    def run(self, y, xin, is_ctx, dst):
        S = self.S
        gb = self.gb[1 if is_ctx else 0]
        t1, xo, st, mv = self.t1.next(), self.xo.next(), self.st.next(), self.mv.next()
        S.op("dve", lambda e: e.tensor_tensor(out=t1.ap[:], in0=y.ap[:], in1=gb.ap[:], op=ALU.mult), reads=[y, gb], writes=[t1])
        S.op("dve", lambda e: e.scalar_tensor_tensor(out=t1.ap[:], in0=xin.ap[:], scalar=ALPHA, in1=t1.ap[:], op0=ALU.mult, op1=ALU.add),
             reads=[xin, t1], writes=[t1])
        for h in range(2):
            S.op("dve", lambda e: e.bn_stats(out=st.ap[:, h, :], in_=t1.ap[:, h * 512:(h + 1) * 512]), reads=[t1], writes=[st])
        S.op("dve", lambda e: e.bn_aggr(out=mv.ap[:, 0:2], in_=st.ap[:]), reads=[st], writes=[mv])
        rstd_op(S, mv, 2, 1, LN_EPS)
        S.op("dve", lambda e: e.tensor_scalar(out=mv.ap[:, 3:4], in0=mv.ap[:, 0:1], scalar1=mv.ap[:, 2:3], scalar2=-1.0, op0=ALU.mult, op1=ALU.mult),
             reads=[mv], writes=[mv])
        S.op("act", lambda e: e.activation(out=xo.ap[:], in_=t1.ap[:], func=AF.Identity, scale=mv.ap[:, 2:3], bias=mv.ap[:, 3:4]),
             reads=[t1, mv], writes=[xo])
        S.op("pool", lambda e: e.tensor_tensor(out=xo.ap[:], in0=xo.ap[:], in1=self.lg.ap[:], op=ALU.mult), reads=[xo, self.lg], writes=[xo])
        S.op("pool", lambda e: e.tensor_tensor(out=xo.ap[:], in0=xo.ap[:], in1=self.lb.ap[:], op=ALU.add), reads=[xo, self.lb], writes=[xo])
        S.dma("pool", dst, xo.ap[:], reads=[xo])


def transpose_mod(S, C, xin, hT, col0, l, fc0, r, ptr, ident):
    for g in range(2):
        p = ptr.next()
        for j in range(4):
            kc = g * 4 + j
            S.op("pe", lambda e: e.transpose(out=p.ap[:, j * 128:(j + 1) * 128], in_=xin.ap[:, kc * 128:(kc + 1) * 128], identity=ident.ap[:]),
                 reads=[xin, ident], writes=[p])
        for j in range(4):
            kc = g * 4 + j
            S.op("act", lambda e: e.activation(out=hT.ap[:, kc, col0:col0 + 128], in_=p.ap[:, j * 128:(j + 1) * 128], func=AF.Identity,
                                               scale=C.modT.ap[:, l, fc0 + 8 + kc, r:r + 1], bias=C.modT.ap[:, l, fc0 + kc, r:r + 1]),
                 reads=[p, C.modT], writes=[hT])


def phase_mlp(S, C, l, bi, src_stage, dst_stage, last):
    S.begin_phase()
    ident = C.ident
    w1 = S.sb("w1", [128, 8, DFF], BF16)
    w2 = S.sb("w2", [128, 32, D], BF16)
    for kc in range(8):
        S.dma("sp" if kc % 2 == 0 else "pool", w1.ap[:, kc, :], C.w1b[l][kc * 128:(kc + 1) * 128, :], writes=[w1])
    for ko in range(32):
        S.dma("sp" if ko % 2 == 0 else "pool", w2.ap[:, ko, :], C.w2b[l][ko * 128:(ko + 1) * 128, :], writes=[w2])
    epi = Epi(S, C, "m")
    epi.load(l, 1, bi)
    xin = mk_ring(S, "mxin", [128, 1024], F32, 4)
    hT = mk_ring(S, "mhT", [128, 8, 256], BF16, 2)
    aT = S.sb("maT", [128, 32, 256], BF16)
    rl = mk_ring(S, "mrl", [128, 256], BF16, 2)
    ptr = mk_ring(S, "mptr", [128, 512], F32, 2, psum=True)
    pup = mk_ring(S, "mpup", [128, 256], F32, 2, psum=True)
    pdn = mk_ring(S, "mpdn", [128, 1024], F32, 2, psum=True)
    t0 = 1 if last else 0
    for tt in range(t0, 9):
        is_ctx = tt == 0
        r = C.R - 1 if is_ctx else bi
        xs_ = []
        h = hT.next()
        for s in range(2):
            xi = xin.next()
            S.dma("sp", xi.ap[:], tile_src(C, src_stage, bi, tt * 2 + s), writes=[xi])
            transpose_mod(S, C, xi, h, s * 128, l, 24, r, ptr, ident)
            xs_.append(xi)
        for fo in range(32):
            p = pup.next()
            for kc in range(8):
                S.op("pe", lambda e: e.matmul(p.ap[:], w1.ap[:, kc, fo * 128:(fo + 1) * 128], h.ap[:, kc, :], start=(kc == 0), stop=(kc == 7)),
                     reads=[w1, h], writes=[p])
            rr = rl.next()
            S.op("act", lambda e: e.activation(out=rr.ap[:], in_=p.ap[:], func=AF.Relu), reads=[p], writes=[rr])
            S.op("pool", lambda e: e.tensor_tensor(out=aT.ap[:, fo, :], in0=rr.ap[:], in1=rr.ap[:], op=ALU.mult), reads=[rr], writes=[aT])
        for s in range(2):
            p = pdn.next()
            for hf in range(2):
                for ko in range(32):
                    S.op("pe", lambda e: e.matmul(p.ap[:, hf * 512:(hf + 1) * 512], aT.ap[:, ko, s * 128:(s + 1) * 128], w2.ap[:, ko, hf * 512:(hf + 1) * 512],
                                                  start=(ko == 0), stop=(ko == 31)), reads=[aT, w2], writes=[p])
            i = tt * 2 + s
            if last:
                dst = C.out[bi, (i - 2) * 128:(i - 1) * 128, :]
            else:
                dst = C.xs[dst_stage - 1][bi, i * 128:(i + 1) * 128, :]
            epi.run(p, xs_[s], is_ctx, dst)
    S.end_phase()


def dram(nc, name, shape, dtype, kind=None):
    if kind is None:
        return nc.dram_tensor(name, list(shape), dtype).ap()
    return nc.dram_tensor(name, list(shape), dtype, kind=kind).ap()


def declare_common(nc, NB, dbg=None):
    C = Ctx()
    C.nc, C.NB, C.R = nc, NB, NB + 1
    R = C.R
    I = lambda n, s: dram(nc, n, s, F32, "ExternalInput")
    C.x = I("x", [NB, LAT, D])
    C.ctx = I("ctx", [NB, CTX, D])
    C.cT = I("cT", [128, 8, R])
    C.mod_w = I("mod_w", [2, D, 6 * D])
    C.mod_b = I("mod_b", [2, 6 * D])
    C.mod_bT = I("mod_bT", [128, 2, 48])
    C.ln_g = I("ln_g", [2, 2, D])
    C.ln_b = I("ln_b", [2, 2, D])
    C.mlp_w1 = I("mlp_w1", [2, D, DFF])
    C.mlp_w2 = I("mlp_w2", [2, DFF, D])
    C.ident_d = I("ident", [128, 128])
    C.out = dram(nc, "out", [NB, LAT, D], F32, "ExternalOutput")
    C.gbc = dram(nc, "gbc", [2, 2, R, 128, D], F32)
    C.w1b = [dram(nc, f"w1b{l}", [D, DFF], BF16) for l in range(2)]
    C.w2b = [dram(nc, f"w2b{l}", [DFF, D], BF16) for l in range(2)]
    nst = 3
    C.xs = [dram(nc, f"xs{i}", [NB, T, D], F32, "ExternalOutput" if (dbg and f"xs{i}" in dbg) else None) for i in range(nst)]
    return C


def common_setup(S, C):
    C.modT = S.sb("modT", [128, 2, 48, C.R], F32, persist=True)
    C.ident = S.sb("identsb", [128, 128], F32, persist=True)
    S.dma("sp", C.ident.ap[:], C.ident_d[:, :], writes=[C.ident])


def host_common(inputs, core, NB):
    b0 = core * NB
    f = lambda a: np.ascontiguousarray(np.asarray(a, dtype=np.float32))
    cs = np.concatenate([np.asarray(inputs["c"])[b0:b0 + NB], np.asarray(inputs["c_ctx"])[None, :]], 0)
    m = {
        "x": f(np.asarray(inputs["x"])[b0:b0 + NB]),
        "ctx": f(np.asarray(inputs["ctx"])[b0:b0 + NB]),
        "cT": f(cs.reshape(NB + 1, 8, 128).transpose(2, 1, 0)),
        "mod_w": f(inputs["mod_w"]),
        "mod_b": f(inputs["mod_b"]),
        "mod_bT": f(np.asarray(inputs["mod_b"]).reshape(2, 48, 128).transpose(2, 0, 1)),
        "ln_g": f(inputs["ln_g"]),
        "ln_b": f(inputs["ln_b"]),
        "mlp_w1": f(inputs["mlp_w1"]),
        "mlp_w2": f(inputs["mlp_w2"]),
        "ident": np.eye(128, dtype=np.float32),
    }
    return m


HYW = 512
PI = float(np.pi)


def colof(i):
    return 1 + 128 * i if i < 2 else 259 + 128 * (i - 2)


def declare_l0(nc, C, dbg=None):
    NB = C.NB
    I = lambda n, s, dt=F32: dram(nc, n, s, dt, "ExternalInput")
    Sx = lambda n, s, dt=BF16: dram(nc, n, s, dt, "ExternalOutput" if (dbg and n in dbg) else None)
    C.e_w_hy = I("e_w_hy", [D, 1536])
    C.e_w_qkv = I("e_w_qkv", [D, 768 + 640])
    C.e_w_out = I("e_w_out", [D, D])
    C.hy_conv = I("hy_conv", [3, 1536])
    C.hy_w1 = I("hy_w1", [33, 64])
    C.hy_w2 = I("hy_w2", [64, 64])
    C.hy_w3 = I("hy_w3", [64, 1024])
    C.hy_vec = I("hy_vec", [64, 3])
    C.hy_decay = I("hy_decay", [1024])
    C.hy_bias = I("hy_bias", [512])
    C.attn_sink = I("attn_sink", [8])
    C.peT = [I("peT_l", [33, LAT]), I("peT_c", [33, CTX])]
    C.negtn = [I("negtn_l", [128, LAT // 128]), I("negtn_c", [128, CTX // 128])]
    C.fwd = [I("fwd_l", [16, 2, 128, 16, 128], BF16), I("fwd_c", [2, 2, 128, 2, 128], BF16)]
    C.inv = [I("inv_l", [4, 128, 16, 2, 512], BF16), I("inv_c", [1, 128, 2, 2, 256], BF16)]
    C.rope = I("rope", [64, 2, LAT])
    C.amask = I("amask", [128, 384])
    C.ident_b = I("ident_b", [128, 128], BF16)
    C.whb = [Sx(f"whb{j}", [D, 1536]) for j in range(3)]
    C.wqb = Sx("wqb", [D, 1408])
    C.wob0 = Sx("wob0", [D, D])
    C.kspec = [Sx("kspec_l", [LAT, 2, 512], F32), Sx("kspec_c", [CTX, 2, 512], F32)]
    C.filt = [Sx("filt_l", [LAT, 2, 512]), Sx("filt_c", [CTX, 2, 512])]
    C.u = Sx("u_s", [NB, T, 512])
    C.x0T = Sx("x0T_s", [NB, 512, T])
    C.qT = Sx("qT_s", [NB, 8, 64, T])
    C.kT = Sx("kT_s", [NB, 2, 64, T])
    C.v = Sx("v_s", [NB, T, 128])
    C.yaT = Sx("yaT_s", [NB, 512, T])
    C.ybT = Sx("ybT_s", [NB, 8, 64, T])


def host_l0(inputs, m):
    f = lambda a: np.ascontiguousarray(np.asarray(a, dtype=np.float32))
    import ml_dtypes
    bf = lambda a: np.ascontiguousarray(np.asarray(a, dtype=np.float32).astype(ml_dtypes.bfloat16))
    w = np.asarray(inputs["e_w_in"])[0]
    d = np.arange(64)
    partner = np.where((d % 32) < 16, d + 16, d - 16)
    qcols = 1536 + (np.arange(8)[:, None] * 64 + partner[None, :]).reshape(-1)
    kcols = 2048 + (np.arange(2)[:, None] * 64 + partner[None, :]).reshape(-1)
    m["e_w_hy"] = f(w[:, :1536])
    m["e_w_qkv"] = f(np.concatenate([w[:, 1536:2304], w[:, qcols], w[:, kcols]], 1))
    m["e_w_out"] = f(np.asarray(inputs["e_w_out"])[0])
    m["hy_conv"] = f(np.asarray(inputs["hy_conv"])[0])
    m["hy_w1"] = f(np.asarray(inputs["hy_ffn_w1"])[0])
    m["hy_w2"] = f(np.asarray(inputs["hy_ffn_w2"])[0])
    m["hy_w3"] = f(np.asarray(inputs["hy_ffn_w3"])[0])
    m["hy_vec"] = f(np.stack([np.asarray(inputs["hy_ffn_b1"])[0], np.asarray(inputs["hy_ffn_b2"])[0], np.asarray(inputs["hy_sin_freq"])[0]], 1))
    m["hy_decay"] = f(np.asarray(inputs["hy_decay"])[0])
    m["hy_bias"] = f(np.asarray(inputs["hy_bias"])[0])
    m["attn_sink"] = f(np.asarray(inputs["attn_sink"])[0])
    for tag, Lf in (("l", LAT), ("c", CTX)):
        t = np.arange(Lf, dtype=np.float32)
        t_norm = t / np.float32(max(Lf - 1, 1))
        bands = np.linspace(1e-4, 15, 16, dtype=np.float32)
        ang = (2.0 * np.pi * t[:, None] * bands[None, :] / Lf).astype(np.float32)
        pe = np.concatenate([t_norm[:, None], np.cos(ang), -np.sin(ang)], -1).astype(np.float32)
        m["peT_" + tag] = f(pe.T)
        m["negtn_" + tag] = f((-t_norm).reshape(Lf // 128, 128).T)
        N = 2 * Lf
        nt = Lf // 128
        tt = np.arange(Lf, dtype=np.float64)
        ff = np.arange(Lf, dtype=np.float64) + 0.5
        th = 2.0 * np.pi * np.outer(tt, ff) / N
        Cm, Sm = np.cos(th), np.sin(th)
        fw = np.stack([Cm, Sm], 0).reshape(2, nt, 128, nt, 128)
        m["fwd_" + tag] = bf(fw.transpose(3, 0, 2, 1, 4))
        tw = min(512, Lf)
        iv = np.stack([Cm.T, -Sm.T], 0).reshape(2, nt, 128, Lf // tw, tw)
        m["inv_" + tag] = bf(iv.transpose(3, 2, 1, 0, 4))
    pos = np.arange(LAT)
    inv_freq = (10000.0 ** (-np.arange(16, dtype=np.float32) / 16)).astype(np.float32)
    P = np.where(d[:, None] < 32, (pos // 64)[None, :], (pos % 64)[None, :]).astype(np.float32)
    ang = (P * inv_freq[d % 16][:, None]).astype(np.float32)
    sgn = np.where((d % 32) < 16, -1.0, 1.0)[:, None]
    m["rope"] = f(np.stack([np.cos(ang), sgn * np.sin(ang)], 1))
    qi = np.arange(128)[:, None]
    kj = np.arange(384)[None, :] - 128
    m["amask"] = f(np.where(np.abs(qi - kj) <= 128, 0.0, -30000.0))
    m["ident_b"] = bf(np.eye(128))
    return m


def l0_setup(S, C):
    for j in range(3):
        prep_weight(S, C.whb[j], C.e_w_hy, D, 1536, scale=C.hy_conv[j, :], tag=f"ph{j}")
    prep_weight(S, C.wqb, C.e_w_qkv, D, 1408, tag="pq")
    prep_weight(S, C.wob0, C.e_w_out, D, D, tag="po")
    for si, Lf in enumerate((LAT, CTX)):
        hyena_filter(S, C, si, Lf)
        hyena_fwd(S, C, si, Lf, C.filt[si], None, C.kspec[si], is_filter=True)


def hyena_filter(S, C, si, Lf):
    S.begin_phase()
    w1 = S.sb("hw1", [33, 64], F32)
    w2 = S.sb("hw2", [64, 64], F32)
    w3 = S.sb("hw3", [64, 1024], F32)
    vec = S.sb("hvec", [64, 3], F32)
    peT = S.sb("hpe", [33, Lf], F32)
    ntn = S.sb("hntn", [128, Lf // 128], F32)
    dec = S.sb("hdec", [128, 1024], F32)
    for dst, src in ((w1, C.hy_w1), (w2, C.hy_w2), (w3, C.hy_w3), (vec, C.hy_vec), (peT, C.peT[si]), (ntn, C.negtn[si])):
        S.dma("sp", dst.ap[:], src, writes=[dst])
    S.dma("sp", dec.ap[:], C.hy_decay[:].partition_broadcast(128), writes=[dec])
    S.op("dve", lambda e: e.scalar_tensor_tensor(out=dec.ap[:], in0=dec.ap[:], scalar=-1.0, in1=dec.ap[:], op0=ALU.mult, op1=ALU.max), reads=[dec], writes=[dec])
    h1 = S.sb("hh1", [64, Lf], F32)
    h2 = S.sb("hh2", [64, Lf], F32)
    tmp = S.sb("htmp", [64, 512], F32)
    S.sin_ki = S.sb("hki", [64, 512], mybir.dt.int32)
    S.sin_kf = S.sb("hkf", [64, 512], F32)
    pp = mk_ring(S, "hpp", [128, 512], F32, 2, psum=True)
    W = min(512, Lf)
    for c0 in range(0, Lf, W):
        p = pp.next()
        S.op("pe", lambda e: e.matmul(p.ap[:64, :W], w1.ap[:], peT.ap[:, c0:c0 + W], start=True, stop=True), reads=[w1, peT], writes=[p])
        S.op("dve", lambda e: e.tensor_scalar(out=tmp.ap[:, :W], in0=p.ap[:64, :W], scalar1=vec.ap[:, 0:1], scalar2=vec.ap[:, 2:3], op0=ALU.add, op1=ALU.mult),
             reads=[p, vec], writes=[tmp])
        sin_tail(S, h1, c0, W, tmp)
    for c0 in range(0, Lf, W):
        p = pp.next()
        S.op("pe", lambda e: e.matmul(p.ap[:64, :W], w2.ap[:], h1.ap[:, c0:c0 + W], start=True, stop=True), reads=[w2, h1], writes=[p])
        S.op("dve", lambda e: e.tensor_scalar(out=tmp.ap[:, :W], in0=p.ap[:64, :W], scalar1=vec.ap[:, 1:2], scalar2=vec.ap[:, 2:3], op0=ALU.add, op1=ALU.mult),
             reads=[p, vec], writes=[tmp])
        sin_tail(S, h2, c0, W, tmp)
    ex = mk_ring(S, "hex", [128, 1024], F32, 2)
    fo = mk_ring(S, "hfo", [128, 2, 512], BF16, 2)
    for tc in range(Lf // 128):
        e_ = ex.next()
        S.op("act", lambda e: e.activation(out=e_.ap[:], in_=dec.ap[:], func=AF.Exp, scale=ntn.ap[:, tc:tc + 1]), reads=[dec, ntn], writes=[e_])
        for hf in range(2):
            p = pp.next()
            S.op("pe", lambda e: e.matmul(p.ap[:], h2.ap[:, tc * 128:(tc + 1) * 128], w3.ap[:, hf * 512:(hf + 1) * 512], start=True, stop=True),
                 reads=[h2, w3], writes=[p])
            S.op("dve", lambda e: e.tensor_tensor(out=e_.ap[:, hf * 512:(hf + 1) * 512], in0=p.ap[:], in1=e_.ap[:, hf * 512:(hf + 1) * 512], op=ALU.mult),
                 reads=[p, e_], writes=[e_])
        if tc == 0:
            S.op("dve", lambda e: e.memset(e_.ap[0:1, 512:1024], 0.0), reads=[], writes=[e_])
        o = fo.next()
        S.op("dve", lambda e: e.tensor_tensor(out=o.ap[:, 0, :], in0=e_.ap[:, 512:1024], in1=e_.ap[:, 0:512], op=ALU.add), reads=[e_], writes=[o])
        S.op("pool", lambda e: e.tensor_tensor(out=o.ap[:, 1, :], in0=e_.ap[:, 512:1024], in1=e_.ap[:, 0:512], op=ALU.subtract), reads=[e_], writes=[o])
        S.dma("sp", C.filt[si][tc * 128:(tc + 1) * 128, :, :], o.ap[:], reads=[o])
    S.end_phase()


def sin_tail(S, dst, c0, W, tmp):
    ki, kf = S.sin_ki, S.sin_kf
    S.op("dve", lambda e: e.tensor_scalar(out=tmp.ap[:, :W], in0=tmp.ap[:, :W], scalar1=1.0 / (2.0 * PI), scalar2=16.5, op0=ALU.mult, op1=ALU.add),
         reads=[tmp], writes=[tmp])
    S.op("dve", lambda e: e.tensor_copy(out=ki.ap[:, :W], in_=tmp.ap[:, :W]), reads=[tmp], writes=[ki])
    S.op("dve", lambda e: e.tensor_copy(out=kf.ap[:, :W], in_=ki.ap[:, :W]), reads=[ki], writes=[kf])
    S.op("dve", lambda e: e.scalar_tensor_tensor(out=tmp.ap[:, :W], in0=tmp.ap[:, :W], scalar=-0.5, in1=kf.ap[:, :W], op0=ALU.add, op1=ALU.subtract),
         reads=[tmp, kf], writes=[tmp])
    S.op("dve", lambda e: e.scalar_tensor_tensor(out=tmp.ap[:, :W], in0=tmp.ap[:, :W], scalar=-0.5, in1=tmp.ap[:, :W], op0=ALU.is_lt, op1=ALU.add),
         reads=[tmp], writes=[tmp])
    S.op("act", lambda e: e.activation(out=dst.ap[:, c0:c0 + W], in_=tmp.ap[:, :W], func=AF.Sin, scale=2.0 * PI * 0.999999), reads=[tmp], writes=[dst])


def hyena_fwd(S, C, si, Lf, src, bi, dst, is_filter):
    nt = Lf // 128
    N = 2 * Lf
    if is_filter:
        S.begin_phase()
    a_in = S.sb("fa", [128, nt, 2 if is_filter else 1, 512], BF16)
    if is_filter:
        S.dma("sp", a_in.ap[:], src.rearrange("(tc p) s c -> p tc s c", p=128), writes=[a_in])
        bb = S.sb("fbias", [128, 512], F32)
        S.dma("sp", bb.ap[:], C.hy_bias[:].partition_broadcast(128), writes=[bb])
        S.op("dve", lambda e: e.tensor_scalar(out=bb.ap[:], in0=bb.ap[:], scalar1=2.0 / N, scalar2=None, op0=ALU.mult), reads=[bb], writes=[bb])
    else:
        S.dma("sp", a_in.ap[:, :, 0, :], src.rearrange("(tc p) c -> p tc c", p=128), writes=[a_in])
    fm = mk_ring(S, "ffm", [128, 2, nt, 128], BF16, 2)
    pr = mk_ring(S, "fpr", [128, 512], F32, 2, psum=True)
    pi_ = mk_ring(S, "fpi", [128, 512], F32, 2, psum=True)
    if is_filter:
        ko = mk_ring(S, "fko", [128, 2, 512], F32, 2)
    else:
        ks = mk_ring(S, "fks", [128, 2, 512], F32, 2)
        tt = mk_ring(S, "ftt", [128, 4, 512], F32, 2)
    for fcn in range(nt):
        m = fm.next()
        for cs in range(2):
            S.dma("sp" if cs == 0 else "pool", m.ap[:, cs, :, :], C.fwd[si][fcn, cs, :, :, :], writes=[m])
        a, b = pr.next(), pi_.next()
        for cs, p in ((0, a), (1, b)):
            for tc in range(nt):
                S.op("pe", lambda e: e.matmul(p.ap[:], m.ap[:, cs, tc, :], a_in.ap[:, tc, cs if is_filter else 0, :], start=(tc == 0), stop=(tc == nt - 1)),
                     reads=[m, a_in], writes=[p])
        if is_filter:
            o = ko.next()
            S.op("dve", lambda e: e.scalar_tensor_tensor(out=o.ap[:, 0, :], in0=a.ap[:], scalar=2.0 / N, in1=bb.ap[:], op0=ALU.mult, op1=ALU.add),
                 reads=[a, bb], writes=[o])
            S.op("act", lambda e: e.activation(out=o.ap[:, 1, :], in_=b.ap[:], func=AF.Identity, scale=2.0 / N), reads=[b], writes=[o])
            S.dma("pool", dst[fcn * 128:(fcn + 1) * 128, :, :], o.ap[:], reads=[o])
        else:
            k = ks.next()
            S.dma("sp", k.ap[:], C.kspec[si][fcn * 128:(fcn + 1) * 128, :, :], writes=[k])
            t = tt.next()
            S.op("dve", lambda e: e.tensor_tensor(out=t.ap[:, 0, :], in0=a.ap[:], in1=k.ap[:, 0, :], op=ALU.mult), reads=[a, k], writes=[t])
            S.op("dve", lambda e: e.tensor_tensor(out=t.ap[:, 1, :], in0=b.ap[:], in1=k.ap[:, 1, :], op=ALU.mult), reads=[b, k], writes=[t])
            S.op("dve", lambda e: e.tensor_tensor(out=t.ap[:, 2, :], in0=a.ap[:], in1=k.ap[:, 1, :], op=ALU.mult), reads=[a, k], writes=[t])
            S.op("dve", lambda e: e.tensor_tensor(out=t.ap[:, 3, :], in0=b.ap[:], in1=k.ap[:, 0, :], op=ALU.mult), reads=[b, k], writes=[t])
            S.op("pool", lambda e: e.tensor_tensor(out=dst.ap[:, fcn, 0, :], in0=t.ap[:, 0, :], in1=t.ap[:, 1, :], op=ALU.add), reads=[t], writes=[dst])
            S.op("pool", lambda e: e.tensor_tensor(out=dst.ap[:, fcn, 1, :], in0=t.ap[:, 2, :], in1=t.ap[:, 3, :], op=ALU.subtract), reads=[t], writes=[dst])
    if is_filter:
        S.end_phase()


def l0_inproj(S, C, bi, src_stage):
    S.begin_phase()
    ident = C.ident
    hT = S.sb("ihT", [128, 8, T + 4], BF16)
    S.op("pool", lambda e: e.memset(hT.ap[:], 0.0), writes=[hT])
    xin = mk_ring(S, "ixin", [128, 1024], F32, 2)
    ptr = mk_ring(S, "iptr", [128, 512], F32, 2, psum=True)
    for i in range(NTT):
        xi = xin.next()
        S.dma("sp", xi.ap[:], tile_src(C, src_stage, bi, i), writes=[xi])
        transpose_mod(S, C, xi, hT, colof(i), 0, 0, (C.R - 1 if i < 2 else bi), ptr, ident)
    S.begin_sub()
    wt = S.sb("iwt", [128, 3, 8, 1024], BF16)
    wv = S.sb("iwv", [128, 8, 128], BF16)
    for j in range(3):
        for kc in range(8):
            S.dma("sp" if kc % 2 else "pool", wt.ap[:, j, kc, :], C.whb[j][kc * 128:(kc + 1) * 128, 512:1536], writes=[wt])
    for kc in range(8):
        S.dma("sp", wv.ap[:, kc, :], C.wqb[kc * 128:(kc + 1) * 128, 640:768], writes=[wv])
    pa = mk_ring(S, "ipa", [128, 512], F32, 2, psum=True)
    pb = mk_ring(S, "ipb", [128, 512], F32, 2, psum=True)
    x1s = mk_ring(S, "ix1", [128, 512], F32, 2)
    ut = mk_ring(S, "iut", [128, 512], BF16, 2)
    vt = mk_ring(S, "ivt", [128, 128], BF16, 2)
    for i in range(NTT):
        c0 = colof(i)
        a, b = pa.next(), pb.next()
        for half, p in ((0, a), (1, b)):
            n = 0
            for j in range(3):
                for kc in range(8):
                    S.op("pe", lambda e: e.matmul(p.ap[:], hT.ap[:, kc, c0 + j - 1:c0 + j - 1 + 128], wt.ap[:, j, kc, half * 512:(half + 1) * 512],
                                                  start=(n == 0), stop=(n == 23)), reads=[hT, wt], writes=[p])
                    n += 1
        x1 = x1s.next()
        S.op("act", lambda e: e.activation(out=x1.ap[:], in_=a.ap[:], func=AF.Identity), reads=[a], writes=[x1])
        u = ut.next()
        S.op("dve", lambda e: e.tensor_tensor(out=u.ap[:], in0=b.ap[:], in1=x1.ap[:], op=ALU.mult), reads=[b, x1], writes=[u])
        S.dma("pool", C.u[bi, i * 128:(i + 1) * 128, :], u.ap[:], reads=[u])
        p = pa.next()
        for kc in range(8):
            S.op("pe", lambda e: e.matmul(p.ap[:, :128], hT.ap[:, kc, c0:c0 + 128], wv.ap[:, kc, :], start=(kc == 0), stop=(kc == 7)),
                 reads=[hT, wv], writes=[p])
        v = vt.next()
        S.op("act", lambda e: e.activation(out=v.ap[:], in_=p.ap[:, :128], func=AF.Identity), reads=[p], writes=[v])
        S.dma("pool", C.v[bi, i * 128:(i + 1) * 128, :], v.ap[:], reads=[v])
    S.end_sub()
    S.begin_sub()
    w0 = S.sb("iw0", [128, 3, 8, 512], BF16)
    wq = S.sb("iwq", [128, 8, 1280], BF16)
    rope = S.sb("irope", [64, 2, LAT], F32)
    S.dma("sp", rope.ap[:], C.rope[:, :, :], writes=[rope])
    for j in range(3):
        for kc in range(8):
            S.dma("sp" if kc % 2 else "pool", w0.ap[:, j, kc, :], C.whb[j][kc * 128:(kc + 1) * 128, 0:512], writes=[w0])
    for kc in range(8):
        S.dma("sp", wq.ap[:, kc, 0:640], C.wqb[kc * 128:(kc + 1) * 128, 0:640], writes=[wq])
        S.dma("pool", wq.ap[:, kc, 640:1280], C.wqb[kc * 128:(kc + 1) * 128, 768:1408], writes=[wq])
    pa = mk_ring(S, "jpa", [128, 512], F32, 3, psum=True)
    ot = mk_ring(S, "jot", [128, 512], BF16, 3)
    t1 = mk_ring(S, "jt1", [64, 512], F32, 2)
    t2 = mk_ring(S, "jt2", [64, 512], F32, 2)
    tiles = [(0, 256)] + [(256 + 512 * k, 512) for k in range(4)]
    for (tok0, w) in tiles:
        c0 = colof(tok0 // 128)
        for cc in range(4):
            p = pa.next()
            n = 0
            for j in range(3):
                for kc in range(8):
                    S.op("pe", lambda e: e.matmul(p.ap[:, :w], w0.ap[:, j, kc, cc * 128:(cc + 1) * 128], hT.ap[:, kc, c0 + j - 1:c0 + j - 1 + w],
                                                  start=(n == 0), stop=(n == 23)), reads=[w0, hT], writes=[p])
                    n += 1
            o = ot.next()
            S.op("act", lambda e: e.activation(out=o.ap[:, :w], in_=p.ap[:, :w], func=AF.Identity), reads=[p], writes=[o])
            S.dma("pool", C.x0T[bi, cc * 128:(cc + 1) * 128, tok0:tok0 + w], o.ap[:, :w], reads=[o])
        for hh in range(10):
            p = pa.next()
            for kc in range(8):
                S.op("pe", lambda e: e.matmul(p.ap[:64, :w], wq.ap[:, kc, hh * 64:(hh + 1) * 64], hT.ap[:, kc, c0:c0 + w], start=(kc == 0), stop=(kc == 7)),
                     reads=[wq, hT], writes=[p])
            o = ot.next()
            dst = C.qT[bi, hh, :, tok0:tok0 + w] if hh < 8 else C.kT[bi, hh - 8, :, tok0:tok0 + w]
            if tok0 == 0:
                S.op("act", lambda e: e.activation(out=o.ap[:64, :w], in_=p.ap[:64, :w], func=AF.Identity), reads=[p], writes=[o])
            else:
                p2 = pa.next()
                for kc in range(8):
                    S.op("pe", lambda e: e.matmul(p2.ap[:64, :w], wq.ap[:, kc, 640 + hh * 64:640 + (hh + 1) * 64], hT.ap[:, kc, c0:c0 + w],
                                                  start=(kc == 0), stop=(kc == 7)), reads=[wq, hT], writes=[p2])
                l0_ = tok0 - 256
                a, b = t1.next(), t2.next()
                S.op("dve", lambda e: e.tensor_tensor(out=a.ap[:, :w], in0=p.ap[:64, :w], in1=rope.ap[:, 0, l0_:l0_ + w], op=ALU.mult), reads=[p, rope], writes=[a])
                S.op("dve", lambda e: e.tensor_tensor(out=b.ap[:, :w], in0=p2.ap[:64, :w], in1=rope.ap[:, 1, l0_:l0_ + w], op=ALU.mult), reads=[p2, rope], writes=[b])
                S.op("pool", lambda e: e.tensor_tensor(out=o.ap[:64, :w], in0=a.ap[:, :w], in1=b.ap[:, :w], op=ALU.add), reads=[a, b], writes=[o])
            S.dma("sp", dst, o.ap[:64, :w], reads=[o])
    S.end_sub()
    S.end_phase()


def l0_hyena(S, C, bi):
    for si, (Lf, tok0) in enumerate(((LAT, 256), (CTX, 0))):
        S.begin_phase()
        nt = Lf // 128
        Y = S.sb("hyY", [128, nt, 2, 512], BF16)
        S.begin_sub()
        hyena_fwd(S, C, si, Lf, C.u[bi, tok0:tok0 + Lf, :], bi, Y, is_filter=False)
        S.end_sub()
        tw = min(512, Lf)
        iv = mk_ring(S, "hyiv", [128, nt, 2, tw], BF16, 2 if Lf == CTX else 1)
        x0 = mk_ring(S, "hyx0", [128, tw], BF16, 2)
        ya = mk_ring(S, "hyya", [128, tw], BF16, 2)
        pp = mk_ring(S, "hypp", [128, 512], F32, 2, psum=True)
        for tt in range(Lf // tw):
            m = iv.next()
            for fc in range(nt):
                S.dma("sp" if fc % 2 else "pool", m.ap[:, fc, :, :], C.inv[si][tt, :, fc, :, :], writes=[m])
            for cc in range(4):
                p = pp.next()
                n = 0
                for fc in range(nt):
                    for cs in range(2):
                        S.op("pe", lambda e: e.matmul(p.ap[:, :tw], Y.ap[:, fc, cs, cc * 128:(cc + 1) * 128], m.ap[:, fc, cs, :],
                                                      start=(n == 0), stop=(n == 2 * nt - 1)), reads=[Y, m], writes=[p])
                        n += 1
                xz = x0.next()
                t0_ = tok0 + tt * tw
                S.dma("sp", xz.ap[:], C.x0T[bi, cc * 128:(cc + 1) * 128, t0_:t0_ + tw], writes=[xz])
                o = ya.next()
                S.op("dve", lambda e: e.tensor_tensor(out=o.ap[:], in0=p.ap[:, :tw], in1=xz.ap[:], op=ALU.mult), reads=[p, xz], writes=[o])
                S.dma("pool", C.yaT[bi, cc * 128:(cc + 1) * 128, t0_:t0_ + tw], o.ap[:], reads=[o])
        S.end_phase()


def l0_attn(S, C, bi):
    S.begin_phase()
    qT = S.sb("aqT", [64, 8, T], BF16)
    kT = S.sb("akT", [64, 2, T], BF16)
    v = S.sb("av", [128, NTT, 128], BF16)
    yb = S.sb("ayb", [64, 8, T], BF16)
    mask = S.sb("amask", [128, 384], F32)
    sink = S.sb("asink", [128, 8], F32)
    idb = S.sb("aidb", [128, 128], BF16)
    for h in range(8):
        S.dma("sp" if h % 2 else "pool", qT.ap[:, h, :], C.qT[bi, h, :, :], writes=[qT])
    for h in range(2):
        S.dma("sp", kT.ap[:, h, :], C.kT[bi, h, :, :], writes=[kT])
    S.dma("sp", v.ap[:], C.v[bi].rearrange("(i p) c -> p i c", p=128), writes=[v])
    S.dma("sp", mask.ap[:], C.amask[:, :], writes=[mask])
    S.dma("sp", sink.ap[:], C.attn_sink[:].partition_broadcast(128), writes=[sink])
    S.dma("sp", idb.ap[:], C.ident_b[:, :], writes=[idb])
    psl = mk_ring(S, "apsl", [128, 512], F32, 2, psum=True)
    psc = mk_ring(S, "apsc", [128, 512], F32, 2, psum=True)
    ppt = mk_ring(S, "appt", [128, 5, 128], BF16, 2, psum=True)
    ppv = mk_ring(S, "appv", [128, 128], F32, 2, psum=True)
    sc = mk_ring(S, "asc", [128, 640], F32, 2)
    pe_ = mk_ring(S, "ape", [128, 640], F32, 2)
    pn = mk_ring(S, "apn", [128, 640], BF16, 2)
    pts = mk_ring(S, "apts", [128, 5, 128], BF16, 2)
    sm = mk_ring(S, "asm", [128, 8], F32, 4)
    def head_chain(qb, hh, n, lo, hi, m0, ktiles, nk, q0):
            h = hh // 4
            s_ = sc.next()
            if n:
                a = psl.next()
                S.op("pe", lambda e: e.matmul(a.ap[:, :n], qT.ap[:, hh, q0:q0 + 128], kT.ap[:, h, 256 + lo:256 + hi], start=True, stop=True),
                     reads=[qT, kT], writes=[a])
                S.op("dve", lambda e: e.tensor_tensor(out=s_.ap[:, :n], in0=a.ap[:, :n], in1=mask.ap[:, m0:m0 + n], op=ALU.add), reads=[a, mask], writes=[s_])
            b = psc.next()
            S.op("pe", lambda e: e.matmul(b.ap[:, :256], qT.ap[:, hh, q0:q0 + 128], kT.ap[:, h, 0:256], start=True, stop=True), reads=[qT, kT], writes=[b])
            S.op("act", lambda e: e.activation(out=s_.ap[:, n:nk], in_=b.ap[:, :256], func=AF.Identity), reads=[b], writes=[s_])
            yield
            w = sm.next()
            S.op("dve", lambda e: e.tensor_reduce(out=w.ap[:, 0:1], in_=s_.ap[:, :nk], axis=AX.X, op=ALU.max), reads=[s_], writes=[w])
            S.op("dve", lambda e: e.tensor_scalar(out=w.ap[:, 1:2], in0=w.ap[:, 0:1], scalar1=0.125, scalar2=sink.ap[:, hh:hh + 1], op0=ALU.mult, op1=ALU.max),
                 reads=[w, sink], writes=[w])
            S.op("dve", lambda e: e.tensor_scalar(out=w.ap[:, 2:3], in0=w.ap[:, 1:2], scalar1=-1.0, scalar2=None, op0=ALU.mult), reads=[w], writes=[w])
            p_ = pe_.next()
            S.op("act", lambda e: e.activation(out=p_.ap[:, :nk], in_=s_.ap[:, :nk], func=AF.Exp, scale=0.125, bias=w.ap[:, 2:3]),
                 reads=[s_, w], writes=[p_])
            yield
            S.op("dve", lambda e: e.tensor_reduce(out=w.ap[:, 3:4], in_=p_.ap[:, :nk], axis=AX.X, op=ALU.add), reads=[p_], writes=[w])
            S.op("act", lambda e: e.activation(out=w.ap[:, 4:5], in_=w.ap[:, 2:3], func=AF.Exp, bias=sink.ap[:, hh:hh + 1], scale=1.0), reads=[w, sink], writes=[w])
            S.op("dve", lambda e: e.tensor_tensor(out=w.ap[:, 5:6], in0=w.ap[:, 3:4], in1=w.ap[:, 4:5], op=ALU.add), reads=[w], writes=[w])
            S.op("dve", lambda e: e.reciprocal(out=w.ap[:, 6:7], in_=w.ap[:, 5:6]), reads=[w], writes=[w])
            pn_ = pn.next()
            S.op("act", lambda e: e.activation(out=pn_.ap[:, :nk], in_=p_.ap[:, :nk], func=AF.Identity, scale=w.ap[:, 6:7]), reads=[p_, w], writes=[pn_])
            yield
            pt = ppt.next()
            nch = nk // 128
            for j in range(nch):
                S.op("pe", lambda e: e.transpose(out=pt.ap[:, j, :], in_=pn_.ap[:, j * 128:(j + 1) * 128], identity=idb.ap[:]), reads=[pn_, idb], writes=[pt])
            ps_ = pts.next()
            S.op("act", lambda e: e.activation(out=ps_.ap[:, :nch, :], in_=pt.ap[:, :nch, :], func=AF.Identity), reads=[pt], writes=[ps_])
            yield
            o = ppv.next()
            for j in range(nch):
                S.op("pe", lambda e: e.matmul(o.ap[:64, :], v.ap[:, ktiles[j], h * 64:(h + 1) * 64], ps_.ap[:, j, :], start=(j == 0), stop=(j == nch - 1)),
                     reads=[v, ps_], writes=[o])
            S.op("dve", lambda e: e.tensor_copy(out=yb.ap[:, hh, q0:q0 + 128], in_=o.ap[:64, :]), reads=[o], writes=[yb])
            yield

    for qb in range(NTT):
        is_ctx = qb < 2
        q0 = qb * 128
        lo = hi = m0 = 0
        if is_ctx:
            n = 0
            ktiles = []
        else:
            lq = q0 - 256
            lo, hi = max(0, lq - 128), min(LAT, lq + 256)
            n = hi - lo
            m0 = lo - (lq - 128)
            ktiles = [2 + lo // 128 + j for j in range(n // 128)]
        ktiles = ktiles + [0, 1]
        nk = n + 256
        for h0 in range(0, 8, 2):
            interleave([head_chain(qb, hh, n, lo, hi, m0, ktiles, nk, q0) for hh in (h0, h0 + 1)])
    for h in range(8):
        S.dma("sp" if h % 2 else "pool", C.ybT[bi, h, :, :], yb.ap[:, h, :], reads=[yb])
    S.end_phase()


def l0_outproj(S, C, bi, src_stage, dst_stage):
    S.begin_phase()
    ya = S.sb("oya", [128, 4, T], BF16)
    yb = S.sb("oyb", [64, 8, T], BF16)
    wa = S.sb("owa", [128, 4, D], BF16)
    wb = S.sb("owb", [64, 8, D], BF16)
    for c in range(4):
        S.dma("sp", ya.ap[:, c, :], C.yaT[bi, c * 128:(c + 1) * 128, :], writes=[ya])
        S.dma("pool", wa.ap[:, c, :], C.wob0[c * 128:(c + 1) * 128, :], writes=[wa])
    for h in range(8):
        S.dma("sp", yb.ap[:, h, :], C.ybT[bi, h, :, :], writes=[yb])
        S.dma("pool", wb.ap[:, h, :], C.wob0[512 + h * 64:512 + (h + 1) * 64, :], writes=[wb])
    epi = Epi(S, C, "o")
    epi.load(0, 0, bi)
    xin = mk_ring(S, "oxin", [128, 1024], F32, 3)
    py = mk_ring(S, "opy", [128, 1024], F32, 2, psum=True)
    for i in range(NTT):
        xi = xin.next()
        S.dma("sp", xi.ap[:], tile_src(C, src_stage, bi, i), writes=[xi])
        p = py.next()
        for hf in range(2):
            for c in range(4):
                S.op("pe", lambda e: e.matmul(p.ap[:, hf * 512:(hf + 1) * 512], ya.ap[:, c, i * 128:(i + 1) * 128], wa.ap[:, c, hf * 512:(hf + 1) * 512],
                                              start=(c == 0), stop=False), reads=[ya, wa], writes=[p])
            for h in range(8):
                S.op("pe", lambda e: e.matmul(p.ap[:, hf * 512:(hf + 1) * 512], yb.ap[:, h, i * 128:(i + 1) * 128], wb.ap[:, h, hf * 512:(hf + 1) * 512],
                                              start=False, stop=(h == 7)), reads=[yb, wb], writes=[p])
        epi.run(p, xi, i < 2, C.xs[dst_stage - 1][bi, i * 128:(i + 1) * 128, :])
    S.end_phase()


def V(b, *idx):
    return (b, b.ap[idx] if idx else b.ap[:])


def TT(S, eng, o, a, b, op):
    return S.op(eng, lambda e: e.tensor_tensor(out=o[1], in0=a[1], in1=b[1], op=op), reads=[a[0], b[0]], writes=[o[0]])


def TS(S, eng, o, a, s1, s2, op0, op1=None, extra=()):
    if op1 is None:
        return S.op(eng, lambda e: e.tensor_scalar(out=o[1], in0=a[1], scalar1=s1, scalar2=None, op0=op0), reads=[a[0], *extra], writes=[o[0]])
    return S.op(eng, lambda e: e.tensor_scalar(out=o[1], in0=a[1], scalar1=s1, scalar2=s2, op0=op0, op1=op1), reads=[a[0], *extra], writes=[o[0]])


def STT(S, o, a, sc, b, op0, op1, extra=()):
    return S.op("dve", lambda e: e.scalar_tensor_tensor(out=o[1], in0=a[1], scalar=sc, in1=b[1], op0=op0, op1=op1), reads=[a[0], b[0], *extra], writes=[o[0]])


def ACT(S, o, a, func, scale=1.0, bias=None, extra=()):
    if bias is None:
        return S.op("act", lambda e: e.activation(out=o[1], in_=a[1], func=func, scale=scale), reads=[a[0], *extra], writes=[o[0]])
    return S.op("act", lambda e: e.activation(out=o[1], in_=a[1], func=func, scale=scale, bias=bias), reads=[a[0], *extra], writes=[o[0]])


def MM(S, o, l, r, start=True, stop=True):
    return S.op("pe", lambda e: e.matmul(o[1], l[1], r[1], start=start, stop=stop), reads=[l[0], r[0]], writes=[o[0]])


def TR(S, o, a, ident):
    return S.op("pe", lambda e: e.transpose(out=o[1], in_=a[1], identity=ident[1]), reads=[a[0], ident[0]], writes=[o[0]])


RW_E = float(np.exp(-0.5))
INV_DT = BF16


def declare_l1(nc, C, dbg=None):
    NB = C.NB
    I = lambda n, s, dt=F32: dram(nc, n, s, dt, "ExternalInput")
    Sx = lambda n, s, dt=BF16: dram(nc, n, s, dt, "ExternalOutput" if (dbg and n in dbg) else None)
    C.o_w_rw = I("o_w_rw", [D, 1920])
    C.o_w_dn = I("o_w_dn", [D, 1536])
    C.o_w_z = I("o_w_z", [D, 528])
    C.o_w_out = I("o_w_out", [D, D])
    C.rw_mu = I("rw_mu", [1920])
    C.dn_conv = I("dn_conv", [3, 1536])
    C.rw_rows = I("rw_rows", [10, 512])
    C.rw_w2 = I("rw_w2", [128, 512])
    C.rw_a2 = I("rw_a2", [128, 512])
    C.rw_g2 = I("rw_g2", [128, 512])
    C.dn_rows = I("dn_rows", [3, 8])
    C.dn_ng = I("dn_ng", [512])
    C.tri = I("tri", [2, 128, 128])
    C.tris = I("tris", [2, 128, 128])
    C.blkm = I("blkm", [4, 128, 128])
    C.tsw = Sx("tsw", [3, 1920], F32)
    C.wrb = [Sx(f"wrb{j}", [D, 1920]) for j in range(3)]
    C.wdb = [Sx(f"wdb{j}", [D, 1536]) for j in range(3)]
    C.wzb = Sx("wzb", [D, 528])
    C.wob1 = Sx("wob1", [D, D])
    C.rw_ops = Sx("rw_ops", [NB, 2, 6, T, 512])
    C.rw_v = Sx("rw_v", [NB, T, 512])
    C.rw_gc = Sx("rw_gc", [NB, 2, NTT, 64, 8], F32)
    C.rw_g = Sx("rw_g", [NB, T, 512], F32)
    C.rw_bonus = Sx("rw_bonus", [NB, T, 512], F32)
    C.y_rw = Sx("y_rw", [NB, 2, T, 512], F32)
    C.dn_qk = Sx("dn_qk", [NB, 2, T, 512])
    C.dn_ops = Sx("dn_ops", [NB, 2, 5, T, 512])
    C.dn_G = Sx("dn_G", [NB, 2, 3, T, 4], F32)
    C.dn_z = Sx("dn_z", [NB, T, 512], F32)
    C.y_dn = Sx("y_dn", [NB, 2, T, 512], F32)


def host_l1(inputs, m):
    f = lambda a: np.ascontiguousarray(np.asarray(a, dtype=np.float32))
    w = np.asarray(inputs["o_w_in"])[0]
    m["o_w_rw"] = f(w[:, :1920])
    m["o_w_dn"] = f(w[:, 1920:1920 + 1536])
    m["o_w_z"] = f(w[:, 1920 + 1536:])
    m["o_w_out"] = f(np.asarray(inputs["o_w_out"])[0])
    m["rw_mu"] = f(np.asarray(inputs["rw_mu"])[0])
    m["dn_conv"] = f(np.asarray(inputs["dn_conv"])[0])
    g = lambda k: np.asarray(inputs[k])[0]
    m["rw_rows"] = f(np.stack([g("rw_w0")[0], g("rw_w0")[1], g("rw_a0")[0], g("rw_a0")[1], g("rw_kk"), g("rw_ka"),
                               g("rw_rk").reshape(512), g("rw_lnx_g"), g("rw_lnx_b"), np.zeros(512, np.float32)], 0))
    m["rw_w2"] = f(g("rw_w2").reshape(128, 512))
    m["rw_a2"] = f(g("rw_a2").reshape(128, 512))
    m["rw_g2"] = f(g("rw_g2"))
    m["dn_rows"] = f(np.stack([g("dn_A_log").reshape(8), g("dn_dt_bias").reshape(8), np.zeros(8, np.float32)], 0))
    m["dn_ng"] = f(np.tile(g("dn_norm_g"), 4))
    j = np.arange(128)[:, None]
    t = np.arange(128)[None, :]
    m["tri"] = f(np.stack([(j <= t), (j >= t)], 0))
    m["tris"] = f(np.stack([(j < t), (j > t)], 0))
    bd = lambda n: (j // n == t // n)
    m["blkm"] = f(np.stack([bd(16), bd(32) & ~bd(16), bd(64) & ~bd(32), ~bd(64)], 0))
    return m


def l1_setup(S, C):
    S.begin_phase()
    mu = S.sb("smu", [1, 1920], F32)
    o = S.sb("smo", [1, 3, 1920], F32)
    S.dma("sp", mu.ap[:], C.rw_mu[:].partition_broadcast(1), writes=[mu])
    TS(S, "dve", V(o, slice(None), 0, slice(None)), V(mu), 0.5, None, ALU.mult)
    TS(S, "dve", V(o, slice(None), 1, slice(None)), V(mu), -1.0, 1.0, ALU.mult, ALU.add)
    TS(S, "dve", V(o, slice(None), 2, slice(None)), V(mu), 0.5, None, ALU.mult)
    S.dma("sp", C.tsw.rearrange("(o j) n -> o j n", o=1), o.ap[:], reads=[o])
    S.end_phase()
    for j in range(3):
        prep_weight(S, C.wrb[j], C.o_w_rw, D, 1920, scale=C.tsw[j, :], tag=f"qr{j}")
        prep_weight(S, C.wdb[j], C.o_w_dn, D, 1536, scale=C.dn_conv[j, :], tag=f"qd{j}")
    prep_weight(S, C.wzb, C.o_w_z, D, 528, tag="qz")
    prep_weight(S, C.wob1, C.o_w_out, D, D, tag="qo")


def build_hT(S, C, bi, src_stage, l):
    hT = S.sb("bhT", [128, 8, T + 4], BF16)
    S.op("pool", lambda e: e.memset(hT.ap[:], 0.0), writes=[hT])
    S.begin_sub()
    xin = mk_ring(S, "bxin", [128, 1024], F32, 2)
    ptr = mk_ring(S, "bptr", [128, 512], F32, 2, psum=True)
    for i in range(NTT):
        xi = xin.next()
        S.dma("sp", xi.ap[:], tile_src(C, src_stage, bi, i), writes=[xi])
        transpose_mod(S, C, xi, hT, colof(i), l, 0, (C.R - 1 if i < 2 else bi), ptr, C.ident)
    S.end_sub()
    return hT


def bc3(b, n, w):
    return (b, b.ap[:, 0:n].unsqueeze(2).to_broadcast([128, n, w]))


def r3(b, n, *pre):
    ap = b.ap[pre] if pre else b.ap[:]
    return (b, ap.rearrange("p (h d) -> p h d", h=n))


def proj3(S, C, p, hT, c0, w, wt, col0, ncol):
    n = 0
    for j in range(3):
        for kc in range(8):
            MM(S, (p, p.ap[:, :ncol]), (hT, hT.ap[:, kc, c0 + j - 1:c0 + j - 1 + 128]), (wt, wt.ap[:, j, kc, col0:col0 + ncol]), start=(n == 0), stop=(n == 23))
            n += 1


def l1_feat_rw(S, C, bi, hT):
    S.begin_sub()
    ones = S.sb("fones", [128, 128], F32)
    S.op("pool", lambda e: e.memset(ones.ap[:], 1.0), writes=[ones])
    tri = S.sb("ftri", [128, 2, 128], F32)
    for d in range(2):
        S.dma("sp", tri.ap[:, d, :], C.tri[d, :, :], writes=[tri])
    rows = S.sb("frows", [128, 7, 512], F32)
    for q in range(7):
        S.dma("sp", rows.ap[:, q, :], C.rw_rows[q, :].partition_broadcast(128), writes=[rows])
    lwb = S.sb("flwb", [128, 3, 512], BF16)
    loraT = S.sb("floraT", [128, 3, T], BF16)
    S.begin_sub()
    lw = S.sb("flw", [128, 3, 512], F32)
    for q, src in enumerate((C.rw_w2, C.rw_a2, C.rw_g2)):
        S.dma("sp", lw.ap[:, q, :], src[:, :], writes=[lw])
    S.op("dve", lambda e: e.tensor_copy(out=lwb.ap[:], in_=lw.ap[:]), reads=[lw], writes=[lwb])
    wl = S.sb("fwl", [128, 3, 8, 384], BF16)
    for j in range(3):
        for kc in range(8):
            S.dma("sp" if kc % 2 else "pool", wl.ap[:, j, kc, :], C.wrb[j][kc * 128:(kc + 1) * 128, 1536:1920], writes=[wl])
    pp = mk_ring(S, "fpp", [128, 512], F32, 2, psum=True)
    for (tok0, w) in [(0, 256)] + [(256 + 512 * k, 512) for k in range(4)]:
        c0 = colof(tok0 // 128)
        for q, fn in enumerate((AF.Tanh, AF.Identity, AF.Sigmoid)):
            p = pp.next()
            n = 0
            for j in range(3):
                for kc in range(8):
                    MM(S, (p, p.ap[:, :w]), (wl, wl.ap[:, j, kc, q * 128:(q + 1) * 128]), (hT, hT.ap[:, kc, c0 + j - 1:c0 + j - 1 + w]), start=(n == 0), stop=(n == 23))
                    n += 1
            ACT(S, (loraT, loraT.ap[:, q, tok0:tok0 + w]), (p, p.ap[:, :w]), fn)
    S.end_sub()
    wr = S.sb("fwr", [128, 3, 8, 1536], BF16)
    for j in range(3):
        for kc in range(8):
            S.dma("sp" if kc % 2 else "pool", wr.ap[:, j, kc, :], C.wrb[j][kc * 128:(kc + 1) * 128, 0:1536], writes=[wr])
    prkv = [S.ps(f"fp{n}", [128, 512], F32) for n in "rkv"]
    pqd = [mk_ring(S, f"fpq{d}", [128, 512], F32, 2, psum=True) for d in range(2)]
    F = lambda n: S.sb("f_" + n, [128, 512], F32)
    rs, ks, kkr, sq, kk, ksum = [F(n) for n in "rs ks kkr sq kk ksum".split()]
    Fd = []
    for d in range(2):
        X = Ctx()
        X.zt, X.logw, X.a_, X.Gs, X.eG, X.enG, X.eE, X.kd, X.bd = [F(f"{n}{d}") for n in "zt logw a Gs eG enG eE kd bd".split()]
        X.eP, X.t1 = X.zt, X.Gs
        Fd.append(X)
    tb, bon, gs = sq, Fd[0].zt, Fd[0].Gs
    vb = mk_ring(S, "fvb", [128, 512], BF16, 2)
    ot = mk_ring(S, "fot", [128, 6, 512], BF16, 2)
    sm = mk_ring(S, "fsm", [128, 16], F32, 2)
    gcs = mk_ring(S, "fgcs", [64, 8], F32, 2)
    KK, KA, RK = [(rows, rows.ap[:, q, :]) for q in (4, 5, 6)]

    def dir_chain(d, i, rws):
        X = Fd[d]
        zt, logw, a_, Gs, eG, enG, eP, eE, t1, kd, bd = X.zt, X.logw, X.a_, X.Gs, X.eG, X.enG, X.eP, X.eE, X.t1, X.kd, X.bd
        o = ot.next()
        pq = pqd[d]
        pgc = prkv[d]
        pz, pa_ = pq.next(), pq.next()
        MM(S, V(pz), (loraT, loraT.ap[d * 64:(d + 1) * 64, 0, rws]), (lwb, lwb.ap[d * 64:(d + 1) * 64, 0, :]))
        MM(S, V(pa_), (loraT, loraT.ap[d * 64:(d + 1) * 64, 1, rws]), (lwb, lwb.ap[d * 64:(d + 1) * 64, 1, :]))
        TT(S, "dve", V(zt), V(pz), (rows, rows.ap[:, d, :]), ALU.add)
        ACT(S, V(zt), V(zt), AF.Sigmoid)
        TS(S, "pool", V(logw), V(zt), -RW_E, None, ALU.mult)
        TT(S, "dve", V(a_), V(pa_), (rows, rows.ap[:, 2 + d, :]), ALU.add)
        ACT(S, V(a_), V(a_), AF.Sigmoid)
        yield
        pG, pT = pq.next(), pq.next()
        MM(S, V(pG), (tri, tri.ap[:, d, :]), V(logw))
        MM(S, V(pT), V(ones), V(logw))
        for h in range(8):
            MM(S, (pgc, pgc.ap[:64, h:h + 1]), (logw, logw.ap[:, h * 64:(h + 1) * 64]), (ones, ones.ap[:, 0:1]))
        gc = gcs.next()
        ACT(S, V(gc), (pgc, pgc.ap[:64, 0:8]), AF.Exp)
        S.dma("pool", C.rw_gc[bi, d, i, :, :], gc.ap[:], reads=[gc])
        ACT(S, V(Gs), V(pG), AF.Identity)
        yield
        ACT(S, V(eG), V(Gs), AF.Exp)
        ACT(S, V(enG), V(Gs), AF.Exp, scale=-1.0)
        TT(S, "dve", V(eP), V(Gs), V(logw), ALU.subtract)
        ACT(S, V(eP), V(eP), AF.Exp)
        TT(S, "dve", V(eE), V(pT), V(Gs), ALU.subtract)
        ACT(S, V(eE), V(eE), AF.Exp)
        yield
        STT(S, V(t1), V(a_), -1.0, KA, ALU.add, ALU.mult)
        STT(S, V(kd), V(t1), 1.0, V(ks), ALU.add, ALU.mult)
        TT(S, "pool", V(bd), V(kk), V(a_), ALU.mult)
        yield
        O = lambda q: (o, o.ap[:, q, :])
        TT(S, "dve", O(0), V(rs), V(eG), ALU.mult)
        TT(S, "pool", O(1), V(kd), V(enG), ALU.mult)
        TT(S, "dve", O(2), V(bd), V(enG), ALU.mult)
        TT(S, "pool", O(3), V(kk), V(eP), ALU.mult)
        TT(S, "dve", O(4), V(kd), V(eE), ALU.mult)
        TT(S, "pool", O(5), V(bd), V(eE), ALU.mult)
        S.dma("sp", C.rw_ops[bi, d, :, rws, :].rearrange("q t c -> t q c"), o.ap[:], reads=[o])
        yield

    for i in range(NTT):
        c0 = colof(i)
        rws = slice(i * 128, (i + 1) * 128)
        for n in range(3):
            proj3(S, C, prkv[n], hT, c0, 128, wr, n * 512, 512)
        ACT(S, V(rs), V(prkv[0]), AF.Identity)
        ACT(S, V(ks), V(prkv[1]), AF.Identity)
        v_ = vb.next()
        ACT(S, V(v_), V(prkv[2]), AF.Identity)
        S.dma("pool", C.rw_v[bi, rws, :], v_.ap[:], reads=[v_])
        w = sm.next()
        TT(S, "dve", V(kkr), V(ks), KK, ALU.mult)
        TT(S, "pool", V(sq), V(kkr), V(kkr), ALU.mult)
        S.op("dve", lambda e: e.tensor_reduce(out=w.ap[:, 0:8], in_=r3(sq, 8)[1], axis=AX.X, op=ALU.add), reads=[sq], writes=[w])
        TS(S, "dve", V(w, slice(None), slice(0, 8)), V(w, slice(None), slice(0, 8)), 1e-6, None, ALU.add)
        ACT(S, V(w, slice(None), slice(0, 8)), V(w, slice(None), slice(0, 8)), AF.Sqrt)
        S.op("dve", lambda e: e.reciprocal(out=w.ap[:, 0:8], in_=w.ap[:, 0:8]), reads=[w], writes=[w])
        TT(S, "dve", r3(kk, 8), r3(kkr, 8), bc3(w, 8, 64), ALU.mult)
        interleave([dir_chain(d, i, rws) for d in range(2)])
        TT(S, "pool", V(ksum), V(Fd[0].kd), V(Fd[1].kd), ALU.add)
        TT(S, "dve", V(tb), V(rs), V(ksum), ALU.mult)
        TT(S, "pool", V(tb), V(tb), RK, ALU.mult)
        S.op("dve", lambda e: e.tensor_reduce(out=w.ap[:, 8:16], in_=r3(tb, 8)[1], axis=AX.X, op=ALU.add), reads=[tb], writes=[w])
        TT(S, "dve", r3(bon, 8), r3(v_, 8), (w, w.ap[:, 8:16].unsqueeze(2).to_broadcast([128, 8, 64])), ALU.mult)
        S.dma("pool", C.rw_bonus[bi, rws, :], bon.ap[:], reads=[bon])
        pg = pqd[0].next()
        MM(S, V(pg), (loraT, loraT.ap[:, 2, rws]), (lwb, lwb.ap[:, 2, :]))
        ACT(S, V(gs), V(pg), AF.Identity)
        S.dma("pool", C.rw_g[bi, rws, :], gs.ap[:], reads=[gs])
    S.end_sub()


def l1_feat_dn(S, C, bi, hT):
    S.begin_sub()
    ones = S.sb("gones", [128, 128], F32)
    S.op("pool", lambda e: e.memset(ones.ap[:], 1.0), writes=[ones])
    tri = S.sb("gtri", [128, 2, 128], F32)
    for d in range(2):
        S.dma("sp", tri.ap[:, d, :], C.tri[d, :, :], writes=[tri])
    dr = S.sb("gdr", [128, 2, 8], F32)
    for q in range(2):
        S.dma("sp", dr.ap[:, q, :], C.dn_rows[q, :].partition_broadcast(128), writes=[dr])
    ACT(S, V(dr, slice(None), 0, slice(None)), V(dr, slice(None), 0, slice(None)), AF.Exp)
    TS(S, "dve", V(dr, slice(None), 0, slice(None)), V(dr, slice(None), 0, slice(None)), -1.0, None, ALU.mult)
    wd = S.sb("gwd", [128, 3, 8, 1536], BF16)
    wz = S.sb("gwz", [128, 8, 528], BF16)
    for j in range(3):
        for kc in range(8):
            S.dma("sp" if kc % 2 else "pool", wd.ap[:, j, kc, :], C.wdb[j][kc * 128:(kc + 1) * 128, :], writes=[wd])
    for kc in range(8):
        S.dma("sp", wz.ap[:, kc, :], C.wzb[kc * 128:(kc + 1) * 128, :], writes=[wz])
    pqkv = [S.ps(f"gp{n}", [128, 512], F32) for n in "qkv"]
    pz = S.ps("gpz", [128, 512], F32)
    pgt = S.ps("gpgt", [128, 16], F32)
    pG = mk_ring(S, "gpG", [128, 8], F32, 2, psum=True)
    F = lambda n: S.sb("g_" + n, [128, 512], F32)
    qs, ks, vs, sq, zs = [F(n) for n in "qs ks vs sq zs".split()]
    qk = mk_ring(S, "gqk", [128, 2, 512], BF16, 2)
    ot = mk_ring(S, "got", [128, 5, 512], BF16, 2)
    sm = mk_ring(S, "gsm", [128, 64], F32, 2)
    go = mk_ring(S, "ggo", [128, 3, 4], F32, 2)
    for i in range(NTT):
        c0 = colof(i)
        rws = slice(i * 128, (i + 1) * 128)
        for n in range(3):
            proj3(S, C, pqkv[n], hT, c0, 128, wd, n * 512, 512)
        for kc in range(8):
            MM(S, V(pz), (hT, hT.ap[:, kc, c0:c0 + 128]), (wz, wz.ap[:, kc, 0:512]), start=(kc == 0), stop=(kc == 7))
        for kc in range(8):
            MM(S, V(pgt), (hT, hT.ap[:, kc, c0:c0 + 128]), (wz, wz.ap[:, kc, 512:528]), start=(kc == 0), stop=(kc == 7))
        for src, dst in zip(pqkv + [pz], (qs, ks, vs, zs)):
            ACT(S, V(dst), V(src), AF.Silu)
        S.dma("pool", C.dn_z[bi, rws, :], zs.ap[:], reads=[zs])
        w = sm.next()
        Wc = lambda a, b: (w, w.ap[:, a:b])
        ACT(S, Wc(0, 16), V(pgt), AF.Identity)
        qk_ = qk.next()
        for n, (src, sc) in enumerate(((qs, 128.0 ** -0.5), (ks, 1.0))):
            TT(S, "pool", V(sq), V(src), V(src), ALU.mult)
            S.op("dve", lambda e: e.tensor_reduce(out=w.ap[:, 16 + 4 * n:20 + 4 * n], in_=r3(sq, 4)[1], axis=AX.X, op=ALU.add), reads=[sq], writes=[w])
            TS(S, "dve", Wc(16 + 4 * n, 20 + 4 * n), Wc(16 + 4 * n, 20 + 4 * n), 1e-6, None, ALU.add)
            ACT(S, Wc(16 + 4 * n, 20 + 4 * n), Wc(16 + 4 * n, 20 + 4 * n), AF.Sqrt)
            S.op("dve", lambda e: e.reciprocal(out=w.ap[:, 16 + 4 * n:20 + 4 * n], in_=w.ap[:, 16 + 4 * n:20 + 4 * n]), reads=[w], writes=[w])
            if sc != 1.0:
                TS(S, "dve", Wc(16, 20), Wc(16, 20), sc, None, ALU.mult)
            TT(S, "dve", r3(src, 4), r3(src, 4), (w, w.ap[:, 16 + 4 * n:20 + 4 * n].unsqueeze(2).to_broadcast([128, 4, 128])), ALU.mult)
            S.op("pool", lambda e: e.tensor_copy(out=qk_.ap[:, n, :], in_=src.ap[:]), reads=[src], writes=[qk_])
        S.dma("sp", C.dn_qk[bi, :, rws, :].rearrange("q t c -> t q c"), qk_.ap[:], reads=[qk_])
        for d in range(2):
            o = ot.next()
            g_ = go.next()
            TT(S, "dve", Wc(24, 28), Wc(d * 4, d * 4 + 4), (dr, dr.ap[:, 1, d * 4:d * 4 + 4]), ALU.add)
            ACT(S, Wc(24, 28), Wc(24, 28), AF.Exp)
            ACT(S, Wc(24, 28), Wc(24, 28), AF.Ln, bias=1.0)
            TT(S, "dve", (g_, g_.ap[:, 0, :]), Wc(24, 28), (dr, dr.ap[:, 0, d * 4:d * 4 + 4]), ALU.mult)
            ACT(S, Wc(28, 32), Wc(8 + d * 4, 12 + d * 4), AF.Sigmoid)
            p = pG.next()
            MM(S, (p, p.ap[:, 0:4]), (tri, tri.ap[:, d, :]), (g_, g_.ap[:, 0, :]))
            MM(S, (p, p.ap[:, 4:8]), V(ones), (g_, g_.ap[:, 0, :]))
            ACT(S, (g_, g_.ap[:, 1:3, :]), (p, p.ap[:, 0:8].rearrange("p (a b) -> p a b", a=2)), AF.Identity)
            S.dma("pool", C.dn_G[bi, d, :, rws, :].rearrange("q t c -> t q c"), g_.ap[:], reads=[g_])
            ACT(S, Wc(32, 36), (g_, g_.ap[:, 1, :]), AF.Exp)
            TT(S, "dve", Wc(36, 40), (g_, g_.ap[:, 2, :]), (g_, g_.ap[:, 1, :]), ALU.subtract)
            ACT(S, Wc(36, 40), Wc(36, 40), AF.Exp)
            TT(S, "dve", Wc(40, 44), Wc(28, 32), Wc(32, 36), ALU.mult)
            B4 = lambda a: (w, w.ap[:, a:a + 4].unsqueeze(2).to_broadcast([128, 4, 128]))
            O = lambda q: (o, o.ap[:, q, :].rearrange("p (h d) -> p h d", h=4))
            TT(S, "dve", O(0), r3(qs, 4), B4(32), ALU.mult)
            TT(S, "pool", O(1), r3(ks, 4), B4(28), ALU.mult)
            TT(S, "dve", O(2), r3(ks, 4), B4(40), ALU.mult)
            TT(S, "pool", O(3), r3(ks, 4), B4(36), ALU.mult)
            TT(S, "dve", O(4), r3(vs, 4), B4(28), ALU.mult)
            S.dma("sp", C.dn_ops[bi, d, :, rws, :].rearrange("q t c -> t q c"), o.ap[:], reads=[o])
    S.end_sub()


def inv_group(S, P, PT, K, ps, out, nh=4):
    mk = lambda: K.rW.next()
    pA, pB, pC = ps
    PD, PDT, Z, ZT = mk(), mk(), K.rZ.next(), K.rZ.next()
    TT(S, "dve", V(PD), V(P), V(K.mb[0]), ALU.mult)
    TT(S, "pool", V(PDT), V(PT), V(K.mb[0]), ALU.mult)
    TT(S, "dve", V(Z), V(K.ident4), V(PD), ALU.subtract)
    TT(S, "pool", V(ZT), V(K.ident4), V(PDT), ALU.subtract)
    yield
    cur, curT = PD, PDT
    for lv in range(3):
        for h in range(nh):
            MM(S, (pA, pA.ap[:, h, :]), (curT, curT.ap[:, h, :]), (cur, cur.ap[:, h, :]))
        for h in range(nh):
            MM(S, (pB, pB.ap[:, h, :]), (cur, cur.ap[:, h, :]), (curT, curT.ap[:, h, :]))
        Pn, PTn = mk(), mk()
        ACT(S, V(Pn), V(pA), AF.Identity)
        S.op("dve", lambda e: e.tensor_copy(out=PTn.ap[:], in_=pB.ap[:]), reads=[pB], writes=[PTn])
        yield
        for h in range(nh):
            MM(S, (pC, pC.ap[:, h, :]), (PTn, PTn.ap[:, h, :]), (Z, Z.ap[:, h, :]))
        for h in range(nh):
            MM(S, (pA, pA.ap[:, h, :]), (Pn, Pn.ap[:, h, :]), (ZT, ZT.ap[:, h, :]))
        TT(S, "dve", V(Z), V(Z), V(pC), ALU.add)
        TT(S, "dve", V(ZT), V(ZT), V(pA), ALU.add)
        yield
        cur, curT = Pn, PTn
    for m in range(1, 4):
        last = m == 3
        O, OT, Y = mk(), mk(), mk()
        TT(S, "dve", V(O), V(P), V(K.mb[m]), ALU.mult)
        TT(S, "pool", V(OT), V(PT), V(K.mb[m]), ALU.mult)
        for h in range(nh):
            MM(S, (pA, pA.ap[:, h, :]), (OT, OT.ap[:, h, :]), (Z, Z.ap[:, h, :]))
        ACT(S, V(Y), V(pA), AF.Identity)
        if not last:
            YT = mk()
            for h in range(nh):
                MM(S, (pB, pB.ap[:, h, :]), (Z, Z.ap[:, h, :]), (OT, OT.ap[:, h, :]))
            S.op("dve", lambda e: e.tensor_copy(out=YT.ap[:], in_=pB.ap[:]), reads=[pB], writes=[YT])
        yield
        for h in range(nh):
            MM(S, (pC, pC.ap[:, h, :]), (ZT, ZT.ap[:, h, :]), (Y, Y.ap[:, h, :]))
        if not last:
            for h in range(nh):
                MM(S, (pB, pB.ap[:, h, :]), (Y, Y.ap[:, h, :]), (ZT, ZT.ap[:, h, :]))
        TT(S, "dve", V(Z), V(Z), V(pC), ALU.subtract)
        if not last:
            TT(S, "dve", V(ZT), V(ZT), V(pB), ALU.subtract)
        yield
    out.append(Z)


def interleave(gens):
    gens = list(gens)
    while gens:
        for g in list(gens):
            try:
                next(g)
            except StopIteration:
                gens.remove(g)


def scan_consts(S, C, tag):
    K = Ctx()
    K.idb = S.sb(tag + "idb", [128, 128], BF16)
    S.dma("sp", K.idb.ap[:], C.ident_b[:, :], writes=[K.idb])
    K.ident4 = S.sb(tag + "id4", [128, 4, 128], F32)
    K.mSI = [S.sb(tag + f"mSI{d}", [128, 4, 2, 128], F32) for d in range(2)]
    K.mS = [S.sb(tag + f"mS{d}", [128, 4, 128], F32) for d in range(2)]
    K.mI = [S.sb(tag + f"mI{d}", [128, 4, 128], F32) for d in range(2)]
    for h in range(4):
        S.dma("sp", K.ident4.ap[:, h, :], C.ident_d[:, :], writes=[K.ident4])
        for d in range(2):
            S.dma("sp", K.mSI[d].ap[:, h, 0, :], C.tris[d, :, :], writes=[K.mSI[d]])
            S.dma("pool", K.mSI[d].ap[:, h, 1, :], C.tri[d, :, :], writes=[K.mSI[d]])
            S.dma("sp", K.mS[d].ap[:, h, :], C.tris[d, :, :], writes=[K.mS[d]])
            S.dma("pool", K.mI[d].ap[:, h, :], C.tri[d, :, :], writes=[K.mI[d]])
    K.mb = [S.sb(tag + f"mb{m}", [128, 4, 128], F32) for m in range(4)]
    for m in range(4):
        for h in range(4):
            S.dma("sp" if h % 2 else "pool", K.mb[m].ap[:, h, :], C.blkm[m, :, :], writes=[K.mb[m]])
    return K


def chain_res(S, K, tag):
    R = Ctx()
    R.__dict__.update(K.__dict__)
    R.rP = mk_ring(S, tag + "rP", [128, 4, 128], INV_DT, 1)
    R.rPT = mk_ring(S, tag + "rPT", [128, 4, 128], INV_DT, 1)
    R.rW = mk_ring(S, tag + "rW", [128, 4, 128], INV_DT, 8)
    R.rZ = mk_ring(S, tag + "rZ", [128, 4, 128], INV_DT, 4)
    return R


def tile_order(d):
    return list(range(NTT)) if d == 0 else [1, 0] + list(range(NTT - 1, 1, -1))


def l1_scan_rw(S, C, bi):
    S.begin_phase()
    S.keep_pool = True
    K0 = scan_consts(S, C, "r")
    interleave([rw_chain(S, C, chain_res(S, K0, f"r{d}"), bi, d) for d in range(2)])
    S.keep_pool = False
    S.end_phase()


def rw_chain(S, C, K, bi, d):
    t = f"r{d}"
    X0, X1, X2 = [S.ps(t + f"X{n}", [128, 4, 128], F32) for n in range(3)]
    ptr = S.ps(t + "ptr", [64, 8, 128], BF16)
    ot_r = mk_ring(S, t + "ot", [128, 6, 512], BF16, 2)
    vt_r = mk_ring(S, t + "vt", [128, 512], BF16, 2)
    gc_r = mk_ring(S, t + "gc", [64, 8], F32, 2)
    AR_r = mk_ring(S, t + "AR", [64, 8, 2, 128], BF16, 1)
    KT_r = mk_ring(S, t + "KT", [64, 8, 128], BF16, 1)
    BT_r = mk_ring(S, t + "BT", [64, 8, 128], BF16, 1)
    KN_r = mk_ring(S, t + "KN", [128, 8, 2, 128], BF16, 1)
    NB_r = mk_ring(S, t + "NB", [128, 8, 128], BF16, 1)
    Zb_r = mk_ring(S, t + "Zb", [128, 4, 128], BF16, 1)
    WT_r = mk_ring(S, t + "WT", [64, 8, 128], BF16, 1)
    Xs_r = mk_ring(S, t + "Xs", [128, 4, 64], BF16, 1)
    nU0_r = mk_ring(S, t + "nU0", [128, 8, 64], F32, 1)
    nU_r = mk_ring(S, t + "nU", [128, 8, 64], BF16, 1)
    ys_r = mk_ring(S, t + "ys", [128, 512], F32, 2)
    ST = S.sb(t + "ST", [64, 8, 64], F32)
    STb = S.sb(t + "STb", [64, 8, 64], BF16)
    S.op("dve", lambda e: e.memset(ST.ap[:], 0.0), writes=[ST])
    S.op("dve", lambda e: e.memset(STb.ap[:], 0.0), writes=[STb])
    v8 = lambda p: p.ap[:].rearrange("p a b -> p (a b)").rearrange("p (h v) -> p h v", v=64)
    for i in tile_order(d):
        rws = slice(i * 128, (i + 1) * 128)
        ot, vt, gc = ot_r.next(), vt_r.next(), gc_r.next()
        S.dma("sp", ot.ap[:], C.rw_ops[bi, d, :, rws, :].rearrange("q t c -> t q c"), writes=[ot])
        S.dma("pool", vt.ap[:], C.rw_v[bi, rws, :], writes=[vt])
        S.dma("pool", gc.ap[:], C.rw_gc[bi, d, i, :, :], writes=[gc])
        AR, KT, BT = AR_r.next(), KT_r.next(), BT_r.next()
        bview = lambda X: X.ap[:].bitcast(BF16).rearrange("p a (b c) -> p (a b) c", b=2)
        tgt = [(ptr, ptr.ap[:]), (X0, bview(X0)[:64]), (X1, bview(X1)[:64]), (X2, bview(X2)[:64])]
        for n_, (q, dst) in enumerate(((3, (AR, AR.ap[:, :, 0, :])), (0, (AR, AR.ap[:, :, 1, :])), (1, V(KT)), (2, V(BT)))):
            tb_, tv = tgt[n_]
            for h in range(8):
                TR(S, (tb_, tv[:, h, :]), (ot, ot.ap[:, q, h * 64:(h + 1) * 64]), V(K.idb))
        for n_, (q, dst) in enumerate(((3, (AR, AR.ap[:, :, 0, :])), (0, (AR, AR.ap[:, :, 1, :])), (1, V(KT)), (2, V(BT)))):
            tb_, tv = tgt[n_]
            if n_ % 2 == 0:
                ACT(S, dst, (tb_, tv), AF.Identity)
            else:
                S.op("dve", lambda e: e.tensor_copy(out=dst[1], in_=tv), reads=[tb_], writes=[dst[0]])
        yield
        KN, NBm, WT, nU0 = KN_r.next(), NB_r.next(), WT_r.next(), nU0_r.next()
        for g in range(2):
            hs = [(hl, g * 4 + hl) for hl in range(4)]
            P, PT = K.rP.next(), K.rPT.next()
            for hl, h in hs:
                MM(S, (X0, X0.ap[:, hl, :]), (BT, BT.ap[:, h, :]), (AR, AR.ap[:, h, 0, :]))
            for hl, h in hs:
                MM(S, (X1, X1.ap[:, hl, :]), (AR, AR.ap[:, h, 0, :]), (BT, BT.ap[:, h, :]))
            for hl, h in hs:
                MM(S, (X2, X2.ap[:, hl, :]), (BT, BT.ap[:, h, :]), (AR, AR.ap[:, h, 1, :]))
            TT(S, "dve", V(P), V(X0), V(K.mS[d]), ALU.mult)
            TT(S, "dve", V(PT), V(X1), V(K.mS[1 - d]), ALU.mult)
            TT(S, "dve", (NBm, NBm.ap[:, g * 4:(g + 1) * 4, :]), V(X2), V(K.mI[d]), ALU.mult)
            yield
            for hl, h in hs:
                MM(S, (X0, X0.ap[:, hl, :]), (KT, KT.ap[:, h, :]), (AR, AR.ap[:, h, 0, :]))
            for hl, h in hs:
                MM(S, (X1, X1.ap[:, hl, :]), (KT, KT.ap[:, h, :]), (AR, AR.ap[:, h, 1, :]))
            TT(S, "dve", (KN, KN.ap[:, g * 4:(g + 1) * 4, 0, :]), V(X0), V(K.mS[d]), ALU.mult)
            TT(S, "dve", (KN, KN.ap[:, g * 4:(g + 1) * 4, 1, :]), V(X1), V(K.mI[d]), ALU.mult)
            yield
            zo = []
            yield from inv_group(S, P, PT, K, (X0, X1, X2), zo)
            Zb = zo[0]
            for hl, h in hs:
                MM(S, (X0, X0.ap[:, hl, 0:64]), (KN, KN.ap[:, h, 0, :]), (vt, vt.ap[:, h * 64:(h + 1) * 64]))
            Xs = Xs_r.next()
            ACT(S, V(Xs), (X0, X0.ap[:, :, 0:64]), AF.Identity)
            yield
            for hl, h in hs:
                MM(S, (X1, X1.ap[:64, hl, :]), (ot, ot.ap[:, 3, h * 64:(h + 1) * 64]), (Zb, Zb.ap[:, hl, :]))
            ACT(S, (WT, WT.ap[:, g * 4:(g + 1) * 4, :]), (X1, X1.ap[:64, :, :]), AF.Identity)
            for hl, h in hs:
                MM(S, (X2, X2.ap[:, hl, 0:64]), (Zb, Zb.ap[:, hl, :]), (Xs, Xs.ap[:, hl, :]))
            TS(S, "dve", (nU0, nU0.ap[:, g * 4:(g + 1) * 4, :]), (X2, X2.ap[:, :, 0:64]), -1.0, None, ALU.mult)
            yield
        for h in range(8):
            MM(S, (X0, v8(X0)[:, h, :]), (WT, WT.ap[:, h, :]), (STb, STb.ap[:, h, :]))
        nU = nU_r.next()
        TT(S, "dve", V(nU), V(nU0), (X0, v8(X0)), ALU.subtract)
        yield
        for h in range(8):
            MM(S, (X1, v8(X1)[:, h, :]), (AR, AR.ap[:, h, 1, :]), (STb, STb.ap[:, h, :]), start=True, stop=False)
            MM(S, (X1, v8(X1)[:, h, :]), (KN, KN.ap[:, h, 1, :]), (vt, vt.ap[:, h * 64:(h + 1) * 64]), start=False, stop=False)
            MM(S, (X1, v8(X1)[:, h, :]), (NBm, NBm.ap[:, h, :]), (nU, nU.ap[:, h, :]), start=False, stop=True)
        ys = ys_r.next()
        ACT(S, r3(ys, 8), (X1, v8(X1)), AF.Identity)
        S.dma("sp", C.y_rw[bi, d, rws, :], ys.ap[:], reads=[ys])
        for h in range(8):
            MM(S, (X2, v8(X2)[:64, h, :]), (ot, ot.ap[:, 4, h * 64:(h + 1) * 64]), (vt, vt.ap[:, h * 64:(h + 1) * 64]), start=True, stop=False)
            MM(S, (X2, v8(X2)[:64, h, :]), (ot, ot.ap[:, 5, h * 64:(h + 1) * 64]), (nU, nU.ap[:, h, :]), start=False, stop=True)
        TT(S, "dve", V(ST), V(ST), (gc, gc.ap[:, 0:8].unsqueeze(2).to_broadcast([64, 8, 64])), ALU.mult)
        TT(S, "dve", V(ST), V(ST), (X2, v8(X2)[:64, :, :]), ALU.add)
        ACT(S, V(STb), V(ST), AF.Identity)
        yield


def l1_scan_dn(S, C, bi):
    S.begin_phase()
    S.keep_pool = True
    K0 = scan_consts(S, C, "d")
    K0.ones = S.sb("dones", [128, 128], F32)
    S.op("pool", lambda e: e.memset(K0.ones.ap[:], 1.0), writes=[K0.ones])
    K0.tri = S.sb("dtri", [128, 2, 128], F32)
    for d in range(2):
        S.dma("sp", K0.tri.ap[:, d, :], C.tri[d, :, :], writes=[K0.tri])
    interleave([dn_chain(S, C, chain_res(S, K0, f"d{d}"), bi, d) for d in range(2)])
    S.keep_pool = False
    S.end_phase()


def dn_chain(S, C, K, bi, d):
    t = f"d{d}"
    ones, tri = K.ones, K.tri
    X0, X1, X2 = [S.ps(t + f"X{n}", [128, 4, 128], F32) for n in range(3)]
    ptr = S.ps(t + "ptr", [128, 4, 128], BF16)
    ot_r = mk_ring(S, t + "ot", [128, 5, 512], BF16, 2)
    qk_r = mk_ring(S, t + "qk", [128, 2, 512], BF16, 2)
    G_r = mk_ring(S, t + "G", [128, 3, 4], F32, 2)
    FT_r = mk_ring(S, t + "FT", [128, 4, 4, 128], BF16, 2)
    gl_r = mk_ring(S, t + "gl", [128, 4, 128], F32, 1)
    ET_r = mk_ring(S, t + "ET", [128, 4, 128], F32, 1)
    qkT_r = mk_ring(S, t + "qkT", [128, 4, 128], BF16, 2)
    Zb_r = mk_ring(S, t + "Zb", [128, 4, 128], BF16, 2)
    wT_r = mk_ring(S, t + "wT", [128, 4, 128], BF16, 2)
    u0_r = mk_ring(S, t + "u0", [128, 4, 128], F32, 2)
    u_r = mk_ring(S, t + "u", [128, 4, 128], BF16, 2)
    ys_r = mk_ring(S, t + "ys", [128, 512], F32, 2)
    sm_r = mk_ring(S, t + "sm", [128, 8], F32, 2)
    ST = S.sb(t + "ST", [128, 4, 128], F32)
    STb = S.sb(t + "STb", [128, 4, 128], BF16)
    S.op("dve", lambda e: e.memset(ST.ap[:], 0.0), writes=[ST])
    S.op("dve", lambda e: e.memset(STb.ap[:], 0.0), writes=[STb])
    for i in tile_order(d):
        rws = slice(i * 128, (i + 1) * 128)
        ot, qk, G = ot_r.next(), qk_r.next(), G_r.next()
        S.dma("sp", ot.ap[:], C.dn_ops[bi, d, :, rws, :].rearrange("q t c -> t q c"), writes=[ot])
        S.dma("pool", qk.ap[:], C.dn_qk[bi, :, rws, :].rearrange("q t c -> t q c"), writes=[qk])
        S.dma("pool", G.ap[:], C.dn_G[bi, d, :, rws, :].rearrange("q t c -> t q c"), writes=[G])
        FT = FT_r.next()
        bview = lambda X: X.ap[:].bitcast(BF16).rearrange("p a (b c) -> p (a b) c", b=2)[:, 0:4, :]
        tgt = [(ptr, ptr.ap[:]), (X0, bview(X0)), (X1, bview(X1)), (X2, bview(X2))]
        for q, src in enumerate(((qk, 1), (qk, 0), (ot, 1), (ot, 0))):
            tb_, tv = tgt[q]
            for h in range(4):
                TR(S, (tb_, tv[:, h, :]), (src[0], src[0].ap[:, src[1], h * 128:(h + 1) * 128]), V(K.idb))
        for q in range(4):
            tb_, tv = tgt[q]
            if q % 2:
                ACT(S, (FT, FT.ap[:, q, :, :]), (tb_, tv), AF.Identity)
            else:
                S.op("dve", lambda e: e.tensor_copy(out=FT.ap[:, q, :, :], in_=tv), reads=[tb_], writes=[FT])
        yield
        gl = gl_r.next()
        for h in range(4):
            TS(S, "pool", (gl, gl.ap[:, h, :]), (tri, tri.ap[:, d, :]), G.ap[:, 0, h:h + 1], None, ALU.mult, extra=[G])
        for h in range(4):
            MM(S, (X0, X0.ap[:, h, :]), V(ones), (gl, gl.ap[:, h, :]))
        ET = ET_r.next()
        for h in range(4):
            TS(S, "dve", (ET, ET.ap[:, h, :]), (X0, X0.ap[:, h, :]), G.ap[:, 1, h:h + 1], 0.0, ALU.subtract, ALU.min, extra=[G])
        ACT(S, V(ET), V(ET), AF.Exp)
        yield
        for h in range(4):
            MM(S, (X1, X1.ap[:, h, :]), (FT, FT.ap[:, 0, h, :]), (FT, FT.ap[:, 2, h, :]))
            MM(S, (X2, X2.ap[:, h, :]), (FT, FT.ap[:, 0, h, :]), (FT, FT.ap[:, 1, h, :]))
        P, PT = K.rP.next(), K.rPT.next()
        TT(S, "dve", V(P), V(X1), V(ET), ALU.mult)
        TT(S, "dve", V(P), V(P), V(K.mS[d]), ALU.mult)
        qkT = qkT_r.next()
        TT(S, "pool", V(ET), V(ET), V(K.mI[d]), ALU.mult)
        TT(S, "dve", V(qkT), V(X2), V(ET), ALU.mult)
        yield
        for h in range(4):
            TR(S, (ptr, ptr.ap[:, h, :]), (P, P.ap[:, h, :]), V(K.idb))
        ACT(S, V(PT), V(ptr), AF.Identity)
        yield
        zo = []
        yield from inv_group(S, P, PT, K, (X0, X1, X2), zo)
        Zb = zo[0]
        for h in range(4):
            MM(S, (X0, X0.ap[:, h, :]), (Zb, Zb.ap[:, h, :]), (ot, ot.ap[:, 4, h * 128:(h + 1) * 128]))
            MM(S, (X1, X1.ap[:, h, :]), (ot, ot.ap[:, 2, h * 128:(h + 1) * 128]), (Zb, Zb.ap[:, h, :]))
        u0, wT = u0_r.next(), wT_r.next()
        ACT(S, V(u0), V(X0), AF.Identity)
        S.op("dve", lambda e: e.tensor_copy(out=wT.ap[:], in_=X1.ap[:]), reads=[X1], writes=[wT])
        yield
        for h in range(4):
            MM(S, (X2, X2.ap[:, h, :]), (wT, wT.ap[:, h, :]), (STb, STb.ap[:, h, :]))
        u = u_r.next()
        TT(S, "dve", V(u), V(u0), V(X2), ALU.subtract)
        yield
        for h in range(4):
            MM(S, (X0, X0.ap[:, h, :]), (FT, FT.ap[:, 3, h, :]), (STb, STb.ap[:, h, :]), start=True, stop=False)
            MM(S, (X0, X0.ap[:, h, :]), (qkT, qkT.ap[:, h, :]), (u, u.ap[:, h, :]), start=False, stop=True)
        ys = ys_r.next()
        ACT(S, r3(ys, 4), V(X0), AF.Identity)
        S.dma("sp", C.y_dn[bi, d, rws, :], ys.ap[:], reads=[ys])
        for h in range(4):
            MM(S, (X1, X1.ap[:, h, :]), (ot, ot.ap[:, 3, h * 128:(h + 1) * 128]), (u, u.ap[:, h, :]))
        sm = sm_r.next()
        ACT(S, (sm, sm.ap[:, 0:4]), (G, G.ap[:, 2, :]), AF.Exp)
        TT(S, "dve", V(ST), V(ST), (sm, sm.ap[:, 0:4].unsqueeze(2).to_broadcast([128, 4, 128])), ALU.mult)
        TT(S, "dve", V(ST), V(ST), V(X1), ALU.add)
        ACT(S, V(STb), V(ST), AF.Identity)
        yield


def l1_out(S, C, bi, src_stage, dst_stage, last):
    S.begin_phase()
    wo = S.sb("xwo", [128, 8, D], BF16)
    for c in range(8):
        S.dma("sp" if c % 2 else "pool", wo.ap[:, c, :], C.wob1[c * 128:(c + 1) * 128, :], writes=[wo])
    rows = S.sb("xrows", [128, 3, 512], F32)
    S.dma("sp", rows.ap[:, 0, :], C.rw_rows[7, :].partition_broadcast(128), writes=[rows])
    S.dma("sp", rows.ap[:, 1, :], C.rw_rows[8, :].partition_broadcast(128), writes=[rows])
    S.dma("sp", rows.ap[:, 2, :], C.dn_ng[:].partition_broadcast(128), writes=[rows])
    epi = Epi(S, C, "x")
    epi.load(1, 0, bi)
    xin = mk_ring(S, "xxin", [128, 1024], F32, 2)
    ya = mk_ring(S, "xya", [128, 2, 512], F32, 2)
    yb = mk_ring(S, "xyb", [128, 2, 512], F32, 2)
    ex = mk_ring(S, "xex", [128, 3, 512], F32, 2)
    sq_r = mk_ring(S, "xsq", [128, 512], F32, 2)
    ycat = mk_ring(S, "xyc", [128, 1024], F32, 2)
    yT = mk_ring(S, "xyT", [128, 8, 128], BF16, 2)
    sm = mk_ring(S, "xsm", [128, 32], F32, 2)
    ptr = mk_ring(S, "xptr", [128, 4, 128], F32, 2, psum=True)
    py = mk_ring(S, "xpy", [128, 1024], F32, 2, psum=True)
    def out_chain(i):
        rws = slice(i * 128, (i + 1) * 128)
        xi, a, b, e_, yc, w, sq = xin.next(), ya.next(), yb.next(), ex.next(), ycat.next(), sm.next(), sq_r.next()
        S.dma("sp", xi.ap[:], tile_src(C, src_stage, bi, i), writes=[xi])
        S.dma("sp", a.ap[:], C.y_rw[bi, :, rws, :].rearrange("q t c -> t q c"), writes=[a])
        S.dma("pool", b.ap[:], C.y_dn[bi, :, rws, :].rearrange("q t c -> t q c"), writes=[b])
        S.dma("sp", e_.ap[:, 0, :], C.rw_g[bi, rws, :], writes=[e_])
        S.dma("pool", e_.ap[:, 1, :], C.rw_bonus[bi, rws, :], writes=[e_])
        S.dma("sp", e_.ap[:, 2, :], C.dn_z[bi, rws, :], writes=[e_])
        y = (a, a.ap[:, 0, :])
        TT(S, "dve", y, y, (a, a.ap[:, 1, :]), ALU.add)
        S.op("dve", lambda e: e.tensor_reduce(out=w.ap[:, 0:8], in_=r3(a, 8, slice(None), 0, slice(None))[1], axis=AX.X, op=ALU.add), reads=[a], writes=[w])
        TS(S, "dve", (w, w.ap[:, 0:8]), (w, w.ap[:, 0:8]), 1.0 / 64, None, ALU.mult)
        TT(S, "dve", r3(a, 8, slice(None), 0, slice(None)), r3(a, 8, slice(None), 0, slice(None)), bc3(w, 8, 64), ALU.subtract)
        TT(S, "pool", V(sq), y, y, ALU.mult)
        S.op("dve", lambda e: e.tensor_reduce(out=w.ap[:, 8:16], in_=r3(sq, 8)[1], axis=AX.X, op=ALU.add), reads=[sq], writes=[w])
        TS(S, "dve", (w, w.ap[:, 8:16]), (w, w.ap[:, 8:16]), 1.0 / 64, 64e-5, ALU.mult, ALU.add)
        ACT(S, (w, w.ap[:, 8:16]), (w, w.ap[:, 8:16]), AF.Sqrt)
        S.op("dve", lambda e: e.reciprocal(out=w.ap[:, 8:16], in_=w.ap[:, 8:16]), reads=[w], writes=[w])
        TT(S, "dve", r3(a, 8, slice(None), 0, slice(None)), r3(a, 8, slice(None), 0, slice(None)),
           (w, w.ap[:, 8:16].unsqueeze(2).to_broadcast([128, 8, 64])), ALU.mult)
        TT(S, "pool", y, y, (rows, rows.ap[:, 0, :]), ALU.mult)
        TT(S, "pool", y, y, (rows, rows.ap[:, 1, :]), ALU.add)
        TT(S, "dve", y, y, (e_, e_.ap[:, 1, :]), ALU.add)
        TT(S, "dve", (yc, yc.ap[:, 0:512]), y, (e_, e_.ap[:, 0, :]), ALU.mult)
        yield
        o = (b, b.ap[:, 0, :])
        TT(S, "dve", o, o, (b, b.ap[:, 1, :]), ALU.add)
        TT(S, "pool", V(sq), o, o, ALU.mult)
        S.op("dve", lambda e: e.tensor_reduce(out=w.ap[:, 16:20], in_=r3(sq, 4)[1], axis=AX.X, op=ALU.add), reads=[sq], writes=[w])
        TS(S, "dve", (w, w.ap[:, 16:20]), (w, w.ap[:, 16:20]), 1.0 / 128, 1e-6, ALU.mult, ALU.add)
        ACT(S, (w, w.ap[:, 16:20]), (w, w.ap[:, 16:20]), AF.Sqrt)
        S.op("dve", lambda e: e.reciprocal(out=w.ap[:, 16:20], in_=w.ap[:, 16:20]), reads=[w], writes=[w])
        TT(S, "dve", r3(b, 4, slice(None), 0, slice(None)), r3(b, 4, slice(None), 0, slice(None)),
           (w, w.ap[:, 16:20].unsqueeze(2).to_broadcast([128, 4, 128])), ALU.mult)
        TT(S, "pool", o, o, (rows, rows.ap[:, 2, :]), ALU.mult)
        TT(S, "dve", (yc, yc.ap[:, 512:1024]), o, (e_, e_.ap[:, 2, :]), ALU.mult)
        yield
        yt = yT.next()
        for g in range(2):
            p = ptr.next()
            for j in range(4):
                c = g * 4 + j
                S.op("pe", lambda e: e.transpose(out=p.ap[:, j, :], in_=yc.ap[:, c * 128:(c + 1) * 128], identity=C.ident.ap[:]), reads=[yc, C.ident], writes=[p])
            ACT(S, (yt, yt.ap[:, g * 4:(g + 1) * 4, :]), V(p), AF.Identity)
        yield
        p = py.next()
        for hf in range(2):
            for c in range(8):
                MM(S, (p, p.ap[:, hf * 512:(hf + 1) * 512]), (yt, yt.ap[:, c, :]), (wo, wo.ap[:, c, hf * 512:(hf + 1) * 512]), start=(c == 0), stop=(c == 7))
        if last:
            dst = C.xs[dst_stage - 1][bi, rws, :]
        else:
            dst = C.xs[dst_stage - 1][bi, rws, :]
        epi.run(p, xi, i < 2, dst)
        yield

    tl = list(range(2 if last else 0, NTT))
    for k in range(0, len(tl), 2):
        interleave([out_chain(i) for i in tl[k:k + 2]])
    S.end_phase()


NB_FULL = 4
N_CORES = 8


def build_full(NB, dbg=None):
    nc = bass.Bass("TRN2", target_bir_lowering=False)
    C = declare_common(nc, NB, dbg=dbg)
    declare_l0(nc, C, dbg=dbg)
    declare_l1(nc, C, dbg=dbg)
    S = Sched(nc)
    common_setup(S, C)
    phase_mod(S, C)
    for l in range(2):
        prep_weight(S, C.w1b[l], C.mlp_w1[l], D, DFF, tag=f"pm1{l}")
        prep_weight(S, C.w2b[l], C.mlp_w2[l], DFF, D, tag=f"pm2{l}")
    l0_setup(S, C)
    l1_setup(S, C)
    for bi in range(NB):
        l0_inproj(S, C, bi, 0)
        l0_hyena(S, C, bi)
        l0_attn(S, C, bi)
        l0_outproj(S, C, bi, 0, 1)
        phase_mlp(S, C, 0, bi, 1, 2, False)
        S.begin_phase()
        hT = build_hT(S, C, bi, 2, 1)
        l1_feat_rw(S, C, bi, hT)
        l1_feat_dn(S, C, bi, hT)
        S.end_phase()
        l1_scan_rw(S, C, bi)
        l1_scan_dn(S, C, bi)
        l1_out(S, C, bi, 2, 3, True)
        phase_mlp(S, C, 1, bi, 3, None, True)
    S.finish()
    return nc


def kernel(**inputs):
    NB = NB_FULL
    nc = build_full(NB)
    in_maps = [host_l1(inputs, host_l0(inputs, host_common(inputs, c, NB))) for c in range(N_CORES)]
    res = run_bass_kernel_spmd(nc, in_maps, core_ids=list(range(N_CORES)))
    out = np.concatenate([np.asarray(r["out"], dtype=np.float32) for r in res.results], axis=0)
    return out
```

```python
import numpy as np
from contextlib import ExitStack
import concourse.bass as bass
import concourse.mybir as mybir
from concourse.bass_utils import run_bass_kernel_spmd

F32 = mybir.dt.float32
BF16 = mybir.dt.bfloat16
AF = mybir.ActivationFunctionType
ALU = mybir.AluOpType
AX = mybir.AxisListType

SAME_ENGINE_SYNC = True
NO_SWDGE = True
POOL_TO_DVE = True
EPOCH = 30000


class Buf:
    __slots__ = ("ap", "name", "lw", "rd")

    def __init__(self, ap, name):
        self.ap = ap
        self.name = name
        self.lw = None
        self.rd = []


class Sched:
    def __init__(self, nc, ndma=24):
        self.nc = nc
        self.stack = ExitStack()
        self.engs = {"pe": nc.tensor, "act": nc.scalar, "dve": nc.vector, "pool": nc.gpsimd, "sp": nc.sync}
        self.csem = {}
        self.ccnt = {}
        self.nsem = 0
        for e in ("pe", "act", "dve", "pool"):
            self._new_csem(e)
        self.dq = {}
        for q in ("sp", "pool", "act"):
            self.dq[q] = [[self._sem(f"d_{q}{i}"), 0] for i in range(ndma)]
        self.dqi = {q: 0 for q in self.dq}
        self.waited = {}
        self.phase_stack = None
        self.ninst = 0

    def _sem(self, name):
        self.nsem += 1
        return self.stack.enter_context(self.nc.semaphore(name))

    def _new_csem(self, e):
        self.csem[e] = self._sem(f"c_{e}_{self.nsem}")
        self.ccnt[e] = 0

    def sb(self, name, shape, dtype, persist=False):
        st = self.stack if (persist or self.phase_stack is None) else self.phase_stack
        self.nsem += 0
        self.uid = getattr(self, "uid", 0) + 1
        t = st.enter_context(self.nc.sbuf_tensor(f"{name}_{self.uid}", list(shape), dtype))
        return Buf(t, name)

    def ps(self, name, shape, dtype, persist=False):
        st = self.stack if (persist or self.phase_stack is None) else self.phase_stack
        self.uid = getattr(self, "uid", 0) + 1
        t = st.enter_context(self.nc.psum_tensor(f"{name}_{self.uid}", list(shape), dtype))
        return Buf(t, name)

    def view(self, ap, name="v"):
        return Buf(ap, name)

    def begin_sub(self):
        if not hasattr(self, "sub_stk"):
            self.sub_stk = []
        self.sub_stk.append(self.phase_stack)
        self.phase_stack = ExitStack()

    def end_sub(self):
        self.barrier()
        self.phase_stack.close()
        self.phase_stack = self.sub_stk.pop()

    def begin_phase(self):
        assert self.phase_stack is None
        self.phase_stack = ExitStack()

    def end_phase(self):
        self.barrier()
        self.phase_stack.close()
        self.phase_stack = None

    def _wait(self, F, tok):
        sem, val, eng = tok
        if eng == F == "pe":
            return
        if eng == F and not SAME_ENGINE_SYNC:
            return
        key = (F, id(sem))
        if self.waited.get(key, 0) >= val:
            return
        self.engs[F].wait_ge(sem, val)
        self.waited[key] = val
        self.ninst += 1

    def _deps(self, F, reads, writes):
        for b in reads:
            if b.lw is not None:
                self._wait(F, b.lw)
        for b in writes:
            if b.lw is not None:
                self._wait(F, b.lw)
            for t in b.rd:
                self._wait(F, t)

    def _commit(self, tok, reads, writes):
        for b in reads:
            if tok[2] == "dma":
                b.rd.append(tok)
            else:
                b.rd = [t for t in b.rd if t[2] != tok[2]]
                b.rd.append(tok)
        for b in writes:
            b.lw = tok
            b.rd = []

    def op(self, F, fn, reads=(), writes=()):
        if F == "pool" and POOL_TO_DVE and not getattr(self, "keep_pool", False):
            F = "dve"
        self._deps(F, reads, writes)
        if self.ccnt[F] >= EPOCH:
            self._new_csem(F)
        inst = fn(self.engs[F])
        self.ccnt[F] += 1
        inst.then_inc(self.csem[F], 1)
        tok = (self.csem[F], self.ccnt[F], F)
        self._commit(tok, reads, writes)
        self.ninst += 1
        return tok

    def dma(self, Q, out, in_, reads=(), writes=(), **kw):
        if NO_SWDGE:
            Q = "sp"
        self._deps(Q, reads, writes)
        pool = self.dq[Q]
        i = self.dqi[Q]
        self.dqi[Q] = (i + 1) % len(pool)
        sem, val = pool[i]
        if val > 0:
            self._wait(Q, (sem, val, "dma"))
        inst = self.engs[Q].dma_start(out=out, in_=in_, **kw)
        inst.then_inc(sem, 16)
        pool[i][1] = val + 16
        tok = (sem, val + 16, "dma")
        self._commit(tok, reads, writes)
        self.ninst += 1
        return tok

    def barrier(self, engines=("pe", "act", "dve", "pool", "sp")):
        toks = []
        for e in ("pe", "act", "dve", "pool"):
            if self.ccnt[e] > 0:
                toks.append((self.csem[e], self.ccnt[e], "bar"))
        for q in self.dq:
            for sem, val in self.dq[q]:
                if val > 0:
                    toks.append((sem, val, "dma"))
        for F in engines:
            for t in toks:
                self._wait(F, t)

    def finish(self):
        self.barrier()
        self.stack.close()


D = 1024
LAT = 2048
CTX = 256
T = LAT + CTX
NTT = T // 128
DFF = 4096
ALPHA = (2.0 * 2) ** 0.25
LN_EPS = 1e-5


class Ring:
    def __init__(self, bufs):
        self.bufs = bufs
        self.i = 0

    def next(self):
        b = self.bufs[self.i]
        self.i = (self.i + 1) % len(self.bufs)
        return b


def mk_ring(S, name, shape, dtype, n=2, psum=False):
    f = S.ps if psum else S.sb
    return Ring([f(f"{name}{i}", shape, dtype) for i in range(n)])


class Ctx:
    pass


def prep_weight(S, dst, src, K, N, scale=None, tag="pw"):
    S.begin_phase()
    CB = min(N, 2048)
    rin = mk_ring(S, tag + "i", [128, CB], F32, 3)
    rout = mk_ring(S, tag + "o", [128, CB], BF16, 2)
    sc = S.sb(tag + "s", [128, CB], F32) if scale is not None else None
    for c0 in range(0, N, CB):
        cw = min(CB, N - c0)
        if scale is not None:
            S.dma("sp", sc.ap[:, :cw], scale[c0:c0 + cw].partition_broadcast(128), writes=[sc])
        k0s = list(range(0, K, 128))
        pend = {}

        def load(k0):
            kw = min(128, K - k0)
            a = rin.next()
            S.dma("sp", a.ap[:kw, :cw], src[k0:k0 + kw, c0:c0 + cw], writes=[a])
            pend[k0] = a

        load(k0s[0])
        for n_, k0 in enumerate(k0s):
            kw = min(128, K - k0)
            if n_ + 1 < len(k0s):
                load(k0s[n_ + 1])
            a = pend.pop(k0)
            o = rout.next()
            if scale is not None:
                S.op("dve", lambda e: e.tensor_tensor(out=o.ap[:kw, :cw], in0=a.ap[:kw, :cw], in1=sc.ap[:kw, :cw], op=ALU.mult),
                     reads=[a, sc], writes=[o])
            else:
                S.op("dve", lambda e: e.tensor_copy(out=o.ap[:kw, :cw], in_=a.ap[:kw, :cw]), reads=[a], writes=[o])
            S.dma("pool", dst[k0:k0 + kw, c0:c0 + cw], o.ap[:kw, :cw], reads=[o])
    S.end_phase()


def phase_mod(S, C):
    nc, R = C.nc, C.R
    S.begin_phase()
    cT = S.sb("cT", [128, 8, R], F32)
    S.dma("sp", cT.ap[:], C.cT[:, :, :], writes=[cT])
    sig = S.sb("sig", [128, 8, R], F32)
    scT = S.sb("scT", [128, 8, R], BF16)
    S.op("act", lambda e: e.activation(out=sig.ap[:], in_=cT.ap[:], func=AF.Sigmoid), reads=[cT], writes=[sig])
    S.op("dve", lambda e: e.tensor_tensor(out=scT.ap[:], in0=cT.ap[:], in1=sig.ap[:], op=ALU.mult), reads=[cT, sig], writes=[scT])
    scbc = S.sb("scbc", [128, 8, R, 128], BF16)
    for kc in range(8):
        for r in range(R):
            S.op("dve", lambda e: e.tensor_copy(out=scbc.ap[:, kc, r, :], in_=scT.ap[:, kc, r:r + 1].to_broadcast([128, 128])),
                 reads=[scT], writes=[scbc])
    mb = S.sb("mb", [128, 2, 48], F32)
    S.dma("sp", mb.ap[:], C.mod_bT[:, :, :], writes=[mb])
    wst = mk_ring(S, "mws", [128, 3072], F32, 2)
    wbf = S.sb("mwbf", [128, 8, 6144], BF16)
    pacc = mk_ring(S, "mps", [128, 512], F32, 2, psum=True)
    gt = mk_ring(S, "mgt", [128, 512], F32, 2)
    mbr = S.sb("mbr", [128, 2048], F32)
    for l in range(2):
        for kc in range(8):
            for hf in range(2):
                a = wst.next()
                S.dma("sp" if hf == 0 else "pool", a.ap[:], C.mod_w[l, kc * 128:(kc + 1) * 128, hf * 3072:(hf + 1) * 3072], writes=[a])
                S.op("dve" if hf == 0 else "act",
                     (lambda e: e.tensor_copy(out=wbf.ap[:, kc, hf * 3072:(hf + 1) * 3072], in_=a.ap[:])) if hf == 0 else
                     (lambda e: e.activation(out=wbf.ap[:, kc, hf * 3072:(hf + 1) * 3072], in_=a.ap[:], func=AF.Identity)),
                     reads=[a], writes=[wbf])
        for fc in range(48):
            p = pacc.next()
            for kc in range(8):
                S.op("pe", lambda e: e.matmul(p.ap[:, :R], wbf.ap[:, kc, fc * 128:(fc + 1) * 128], scT.ap[:, kc, :],
                                              start=(kc == 0), stop=(kc == 7)), reads=[wbf, scT], writes=[p])
            is_scale = (fc // 8) in (1, 4)
            S.op("dve", lambda e: e.tensor_scalar(out=C.modT.ap[:, l, fc, :], in0=p.ap[:, :R], scalar1=mb.ap[:, l, fc:fc + 1],
                                                  scalar2=(1.0 if is_scale else 0.0), op0=ALU.add, op1=ALU.add),
                 reads=[p, mb], writes=[C.modT])
        for gi, c0 in enumerate((2048, 5120)):
            S.dma("sp", mbr.ap[:, gi * 1024:(gi + 1) * 1024], C.mod_b[l, c0:c0 + 1024].partition_broadcast(128), writes=[mbr])
        for gi, c0 in enumerate((2048, 5120)):
            for r in range(R):
                for hf in range(2):
                    p = pacc.next()
                    for kc in range(8):
                        S.op("pe", lambda e: e.matmul(p.ap[:], scbc.ap[:, kc, r, :], wbf.ap[:, kc, c0 + hf * 512:c0 + (hf + 1) * 512],
                                                      start=(kc == 0), stop=(kc == 7)), reads=[scbc, wbf], writes=[p])
                    g = gt.next()
                    S.op("dve", lambda e: e.tensor_tensor(out=g.ap[:], in0=p.ap[:], in1=mbr.ap[:, gi * 1024 + hf * 512:gi * 1024 + (hf + 1) * 512], op=ALU.add),
                         reads=[p, mbr], writes=[g])
                    S.dma("pool", C.gbc[l, gi, r, :, hf * 512:(hf + 1) * 512], g.ap[:], reads=[g])
    S.end_phase()


def tile_src(C, stage, bi, i):
    if stage == 0:
        if i < 2:
            return C.ctx[bi, i * 128:(i + 1) * 128, :]
        return C.x[bi, (i - 2) * 128:(i - 1) * 128, :]
    return C.xs[stage - 1][bi, i * 128:(i + 1) * 128, :]


def rstd_op(S, mv, o, i, eps):
    S.op("dve", lambda e: e.tensor_scalar(out=mv.ap[:, o:o + 1], in0=mv.ap[:, i:i + 1], scalar1=eps, scalar2=None, op0=ALU.add), reads=[mv], writes=[mv])
    S.op("act", lambda e: e.activation(out=mv.ap[:, o:o + 1], in_=mv.ap[:, o:o + 1], func=AF.Sqrt), reads=[mv], writes=[mv])
    S.op("dve", lambda e: e.reciprocal(out=mv.ap[:, o:o + 1], in_=mv.ap[:, o:o + 1]), reads=[mv], writes=[mv])


class Epi:
    def __init__(self, S, C, tag):
        self.S, self.C = S, C
        self.gb = [S.sb(tag + "gb0", [128, 1024], F32), S.sb(tag + "gb1", [128, 1024], F32)]
        self.lg = S.sb(tag + "lg", [128, 1024], F32)
        self.lb = S.sb(tag + "lb", [128, 1024], F32)
        self.t1 = mk_ring(S, tag + "t1", [128, 1024], F32, 2)
        self.xo = mk_ring(S, tag + "xo", [128, 1024], F32, 2)
        self.st = mk_ring(S, tag + "st", [128, 2, 6], F32, 2)
        self.mv = mk_ring(S, tag + "mv", [128, 4], F32, 2)

    def load(self, l, sub, bi):
        S, C = self.S, self.C
        S.dma("sp", self.gb[0].ap[:], C.gbc[l, sub, bi, :, :], writes=[self.gb[0]])
        S.dma("sp", self.gb[1].ap[:], C.gbc[l, sub, C.R - 1, :, :], writes=[self.gb[1]])
        S.dma("sp", self.lg.ap[:], C.ln_g[l, sub, :].partition_broadcast(128), writes=[self.lg])
        S.dma("sp", self.lb.ap[:], C.ln_b[l, sub, :].partition_broadcast(128), writes=[self.lb])

    def run(self, y, xin, is_ctx, dst):
        S = self.S
        gb = self.gb[1 if is_ctx else 0]
        t1, xo, st, mv = self.t1.next(), self.xo.next(), self.st.next(), self.mv.next()
        S.op("dve", lambda e: e.tensor_tensor(out=t1.ap[:], in0=y.ap[:], in1=gb.ap[:], op=ALU.mult), reads=[y, gb], writes=[t1])
        S.op("dve", lambda e: e.scalar_tensor_tensor(out=t1.ap[:], in0=xin.ap[:], scalar=ALPHA, in1=t1.ap[:], op0=ALU.mult, op1=ALU.add),
             reads=[xin, t1], writes=[t1])
        for h in range(2):
            S.op("dve", lambda e: e.bn_stats(out=st.ap[:, h, :], in_=t1.ap[:, h * 512:(h + 1) * 512]), reads=[t1], writes=[st])
        S.op("dve", lambda e: e.bn_aggr(out=mv.ap[:, 0:2], in_=st.ap[:]), reads=[st], writes=[mv])
        rstd_op(S, mv, 2, 1, LN_EPS)
        S.op("dve", lambda e: e.tensor_scalar(out=mv.ap[:, 3:4], in0=mv.ap[:, 0:1], scalar1=mv.ap[:, 2:3], scalar2=-1.0, op0=ALU.mult, op1=ALU.mult),
             reads=[mv], writes=[mv])
        S.op("act", lambda e: e.activation(out=xo.ap[:], in_=t1.ap[:], func=AF.Identity, scale=mv.ap[:, 2:3], bias=mv.ap[:, 3:4]),
             reads=[t1, mv], writes=[xo])
        S.op("pool", lambda e: e.tensor_tensor(out=xo.ap[:], in0=xo.ap[:], in1=self.lg.ap[:], op=ALU.mult), reads=[xo, self.lg], writes=[xo])
        S.op("pool", lambda e: e.tensor_tensor(out=xo.ap[:], in0=xo.ap[:], in1=self.lb.ap[:], op=ALU.add), reads=[xo, self.lb], writes=[xo])
        S.dma("pool", dst, xo.ap[:], reads=[xo])


def transpose_mod(S, C, xin, hT, col0, l, fc0, r, ptr, ident):
    for g in range(2):
        p = ptr.next()
        for j in range(4):
            kc = g * 4 + j
            S.op("pe", lambda e: e.transpose(out=p.ap[:, j * 128:(j + 1) * 128], in_=xin.ap[:, kc * 128:(kc + 1) * 128], identity=ident.ap[:]),
                 reads=[xin, ident], writes=[p])
        for j in range(4):
            kc = g * 4 + j
            S.op("act", lambda e: e.activation(out=hT.ap[:, kc, col0:col0 + 128], in_=p.ap[:, j * 128:(j + 1) * 128], func=AF.Identity,
                                               scale=C.modT.ap[:, l, fc0 + 8 + kc, r:r + 1], bias=C.modT.ap[:, l, fc0 + kc, r:r + 1]),
                 reads=[p, C.modT], writes=[hT])


def phase_mlp(S, C, l, bi, src_stage, dst_stage, last):
    S.begin_phase()
    ident = C.ident
    w1 = S.sb("w1", [128, 8, DFF], BF16)
    w2 = S.sb("w2", [128, 32, D], BF16)
    for kc in range(8):
        S.dma("sp" if kc % 2 == 0 else "pool", w1.ap[:, kc, :], C.w1b[l][kc * 128:(kc + 1) * 128, :], writes=[w1])
    for ko in range(32):
        S.dma("sp" if ko % 2 == 0 else "pool", w2.ap[:, ko, :], C.w2b[l][ko * 128:(ko + 1) * 128, :], writes=[w2])
    epi = Epi(S, C, "m")
    epi.load(l, 1, bi)
    xin = mk_ring(S, "mxin", [128, 1024], F32, 4)
    hT = mk_ring(S, "mhT", [128, 8, 256], BF16, 2)
    aT = S.sb("maT", [128, 32, 256], BF16)
    rl = mk_ring(S, "mrl", [128, 256], BF16, 2)
    ptr = mk_ring(S, "mptr", [128, 512], F32, 2, psum=True)
    pup = mk_ring(S, "mpup", [128, 256], F32, 2, psum=True)
    pdn = mk_ring(S, "mpdn", [128, 1024], F32, 2, psum=True)
    t0 = 1 if last else 0
    for tt in range(t0, 9):
        is_ctx = tt == 0
        r = C.R - 1 if is_ctx else bi
        xs_ = []
        h = hT.next()
        for s in range(2):
            xi = xin.next()
            S.dma("sp", xi.ap[:], tile_src(C, src_stage, bi, tt * 2 + s), writes=[xi])
            transpose_mod(S, C, xi, h, s * 128, l, 24, r, ptr, ident)
            xs_.append(xi)
        for fo in range(32):
            p = pup.next()
            for kc in range(8):
                S.op("pe", lambda e: e.matmul(p.ap[:], w1.ap[:, kc, fo * 128:(fo + 1) * 128], h.ap[:, kc, :], start=(kc == 0), stop=(kc == 7)),
                     reads=[w1, h], writes=[p])
            rr = rl.next()
            S.op("act", lambda e: e.activation(out=rr.ap[:], in_=p.ap[:], func=AF.Relu), reads=[p], writes=[rr])
            S.op("pool", lambda e: e.tensor_tensor(out=aT.ap[:, fo, :], in0=rr.ap[:], in1=rr.ap[:], op=ALU.mult), reads=[rr], writes=[aT])
        for s in range(2):
            p = pdn.next()
            for hf in range(2):
                for ko in range(32):
                    S.op("pe", lambda e: e.matmul(p.ap[:, hf * 512:(hf + 1) * 512], aT.ap[:, ko, s * 128:(s + 1) * 128], w2.ap[:, ko, hf * 512:(hf + 1) * 512],
                                                  start=(ko == 0), stop=(ko == 31)), reads=[aT, w2], writes=[p])
            i = tt * 2 + s
            if last:
                dst = C.out[bi, (i - 2) * 128:(i - 1) * 128, :]
            else:
                dst = C.xs[dst_stage - 1][bi, i * 128:(i + 1) * 128, :]
            epi.run(p, xs_[s], is_ctx, dst)
    S.end_phase()


def dram(nc, name, shape, dtype, kind=None):
    if kind is None:
        return nc.dram_tensor(name, list(shape), dtype).ap()
    return nc.dram_tensor(name, list(shape), dtype, kind=kind).ap()


def declare_common(nc, NB, dbg=None):
    C = Ctx()
    C.nc, C.NB, C.R = nc, NB, NB + 1
    R = C.R
    I = lambda n, s: dram(nc, n, s, F32, "ExternalInput")
    C.x = I("x", [NB, LAT, D])
    C.ctx = I("ctx", [NB, CTX, D])
    C.cT = I("cT", [128, 8, R])
    C.mod_w = I("mod_w", [2, D, 6 * D])
    C.mod_b = I("mod_b", [2, 6 * D])
    C.mod_bT = I("mod_bT", [128, 2, 48])
    C.ln_g = I("ln_g", [2, 2, D])
    C.ln_b = I("ln_b", [2, 2, D])
    C.mlp_w1 = I("mlp_w1", [2, D, DFF])
    C.mlp_w2 = I("mlp_w2", [2, DFF, D])
    C.ident_d = I("ident", [128, 128])
    C.out = dram(nc, "out", [NB, LAT, D], F32, "ExternalOutput")
    C.gbc = dram(nc, "gbc", [2, 2, R, 128, D], F32)
    C.w1b = [dram(nc, f"w1b{l}", [D, DFF], BF16) for l in range(2)]
    C.w2b = [dram(nc, f"w2b{l}", [DFF, D], BF16) for l in range(2)]
    nst = 3
    C.xs = [dram(nc, f"xs{i}", [NB, T, D], F32, "ExternalOutput" if (dbg and f"xs{i}" in dbg) else None) for i in range(nst)]
    return C


def common_setup(S, C):
    C.modT = S.sb("modT", [128, 2, 48, C.R], F32, persist=True)
    C.ident = S.sb("identsb", [128, 128], F32, persist=True)
    S.dma("sp", C.ident.ap[:], C.ident_d[:, :], writes=[C.ident])


def host_common(inputs, core, NB):
    b0 = core * NB
    f = lambda a: np.ascontiguousarray(np.asarray(a, dtype=np.float32))
    cs = np.concatenate([np.asarray(inputs["c"])[b0:b0 + NB], np.asarray(inputs["c_ctx"])[None, :]], 0)
    m = {
        "x": f(np.asarray(inputs["x"])[b0:b0 + NB]),
        "ctx": f(np.asarray(inputs["ctx"])[b0:b0 + NB]),
        "cT": f(cs.reshape(NB + 1, 8, 128).transpose(2, 1, 0)),
        "mod_w": f(inputs["mod_w"]),
        "mod_b": f(inputs["mod_b"]),
        "mod_bT": f(np.asarray(inputs["mod_b"]).reshape(2, 48, 128).transpose(2, 0, 1)),
        "ln_g": f(inputs["ln_g"]),
        "ln_b": f(inputs["ln_b"]),
        "mlp_w1": f(inputs["mlp_w1"]),
        "mlp_w2": f(inputs["mlp_w2"]),
        "ident": np.eye(128, dtype=np.float32),
    }
    return m


HYW = 512
PI = float(np.pi)


def colof(i):
    return 1 + 128 * i if i < 2 else 259 + 128 * (i - 2)


def declare_l0(nc, C, dbg=None):
    NB = C.NB
    I = lambda n, s, dt=F32: dram(nc, n, s, dt, "ExternalInput")
    Sx = lambda n, s, dt=BF16: dram(nc, n, s, dt, "ExternalOutput" if (dbg and n in dbg) else None)
    C.e_w_hy = I("e_w_hy", [D, 1536])
    C.e_w_qkv = I("e_w_qkv", [D, 768 + 640])
    C.e_w_out = I("e_w_out", [D, D])
    C.hy_conv = I("hy_conv", [3, 1536])
    C.hy_w1 = I("hy_w1", [33, 64])
    C.hy_w2 = I("hy_w2", [64, 64])
    C.hy_w3 = I("hy_w3", [64, 1024])
    C.hy_vec = I("hy_vec", [64, 3])
    C.hy_decay = I("hy_decay", [1024])
    C.hy_bias = I("hy_bias", [512])
    C.attn_sink = I("attn_sink", [8])
    C.peT = [I("peT_l", [33, LAT]), I("peT_c", [33, CTX])]
    C.negtn = [I("negtn_l", [128, LAT // 128]), I("negtn_c", [128, CTX // 128])]
    C.fwd = [I("fwd_l", [16, 2, 128, 16, 128], BF16), I("fwd_c", [2, 2, 128, 2, 128], BF16)]
    C.inv = [I("inv_l", [4, 128, 16, 2, 512], BF16), I("inv_c", [1, 128, 2, 2, 256], BF16)]
    C.rope = I("rope", [64, 2, LAT])
    C.amask = I("amask", [128, 384])
    C.ident_b = I("ident_b", [128, 128], BF16)
    C.whb = [Sx(f"whb{j}", [D, 1536]) for j in range(3)]
    C.wqb = Sx("wqb", [D, 1408])
    C.wob0 = Sx("wob0", [D, D])
    C.kspec = [Sx("kspec_l", [LAT, 2, 512], F32), Sx("kspec_c", [CTX, 2, 512], F32)]
    C.filt = [Sx("filt_l", [LAT, 2, 512]), Sx("filt_c", [CTX, 2, 512])]
    C.u = Sx("u_s", [NB, T, 512])
    C.x0T = Sx("x0T_s", [NB, 512, T])
    C.qT = Sx("qT_s", [NB, 8, 64, T])
    C.kT = Sx("kT_s", [NB, 2, 64, T])
    C.v = Sx("v_s", [NB, T, 128])
    C.yaT = Sx("yaT_s", [NB, 512, T])
    C.ybT = Sx("ybT_s", [NB, 8, 64, T])


def host_l0(inputs, m):
    f = lambda a: np.ascontiguousarray(np.asarray(a, dtype=np.float32))
    import ml_dtypes
    bf = lambda a: np.ascontiguousarray(np.asarray(a, dtype=np.float32).astype(ml_dtypes.bfloat16))
    w = np.asarray(inputs["e_w_in"])[0]
    d = np.arange(64)
    partner = np.where((d % 32) < 16, d + 16, d - 16)
    qcols = 1536 + (np.arange(8)[:, None] * 64 + partner[None, :]).reshape(-1)
    kcols = 2048 + (np.arange(2)[:, None] * 64 + partner[None, :]).reshape(-1)
    m["e_w_hy"] = f(w[:, :1536])
    m["e_w_qkv"] = f(np.concatenate([w[:, 1536:2304], w[:, qcols], w[:, kcols]], 1))
    m["e_w_out"] = f(np.asarray(inputs["e_w_out"])[0])
    m["hy_conv"] = f(np.asarray(inputs["hy_conv"])[0])
    m["hy_w1"] = f(np.asarray(inputs["hy_ffn_w1"])[0])
    m["hy_w2"] = f(np.asarray(inputs["hy_ffn_w2"])[0])
    m["hy_w3"] = f(np.asarray(inputs["hy_ffn_w3"])[0])
    m["hy_vec"] = f(np.stack([np.asarray(inputs["hy_ffn_b1"])[0], np.asarray(inputs["hy_ffn_b2"])[0], np.asarray(inputs["hy_sin_freq"])[0]], 1))
    m["hy_decay"] = f(np.asarray(inputs["hy_decay"])[0])
    m["hy_bias"] = f(np.asarray(inputs["hy_bias"])[0])
    m["attn_sink"] = f(np.asarray(inputs["attn_sink"])[0])
    for tag, Lf in (("l", LAT), ("c", CTX)):
        t = np.arange(Lf, dtype=np.float32)
        t_norm = t / np.float32(max(Lf - 1, 1))
        bands = np.linspace(1e-4, 15, 16, dtype=np.float32)
        ang = (2.0 * np.pi * t[:, None] * bands[None, :] / Lf).astype(np.float32)
        pe = np.concatenate([t_norm[:, None], np.cos(ang), -np.sin(ang)], -1).astype(np.float32)
        m["peT_" + tag] = f(pe.T)
        m["negtn_" + tag] = f((-t_norm).reshape(Lf // 128, 128).T)
        N = 2 * Lf
        nt = Lf // 128
        tt = np.arange(Lf, dtype=np.float64)
        ff = np.arange(Lf, dtype=np.float64) + 0.5
        th = 2.0 * np.pi * np.outer(tt, ff) / N
        Cm, Sm = np.cos(th), np.sin(th)
        fw = np.stack([Cm, Sm], 0).reshape(2, nt, 128, nt, 128)
        m["fwd_" + tag] = bf(fw.transpose(3, 0, 2, 1, 4))
        tw = min(512, Lf)
        iv = np.stack([Cm.T, -Sm.T], 0).reshape(2, nt, 128, Lf // tw, tw)
        m["inv_" + tag] = bf(iv.transpose(3, 2, 1, 0, 4))
    pos = np.arange(LAT)
    inv_freq = (10000.0 ** (-np.arange(16, dtype=np.float32) / 16)).astype(np.float32)
    P = np.where(d[:, None] < 32, (pos // 64)[None, :], (pos % 64)[None, :]).astype(np.float32)
    ang = (P * inv_freq[d % 16][:, None]).astype(np.float32)
    sgn = np.where((d % 32) < 16, -1.0, 1.0)[:, None]
    m["rope"] = f(np.stack([np.cos(ang), sgn * np.sin(ang)], 1))
    qi = np.arange(128)[:, None]
    kj = np.arange(384)[None, :] - 128
    m["amask"] = f(np.where(np.abs(qi - kj) <= 128, 0.0, -30000.0))
    m["ident_b"] = bf(np.eye(128))
    return m


def l0_setup(S, C):
    for j in range(3):
        prep_weight(S, C.whb[j], C.e_w_hy, D, 1536, scale=C.hy_conv[j, :], tag=f"ph{j}")
    prep_weight(S, C.wqb, C.e_w_qkv, D, 1408, tag="pq")
    prep_weight(S, C.wob0, C.e_w_out, D, D, tag="po")
    for si, Lf in enumerate((LAT, CTX)):
        hyena_filter(S, C, si, Lf)
        hyena_fwd(S, C, si, Lf, C.filt[si], None, C.kspec[si], is_filter=True)


def hyena_filter(S, C, si, Lf):
    S.begin_phase()
    w1 = S.sb("hw1", [33, 64], F32)
    w2 = S.sb("hw2", [64, 64], F32)
    w3 = S.sb("hw3", [64, 1024], F32)
    vec = S.sb("hvec", [64, 3], F32)
    peT = S.sb("hpe", [33, Lf], F32)
    ntn = S.sb("hntn", [128, Lf // 128], F32)
    dec = S.sb("hdec", [128, 1024], F32)
    for dst, src in ((w1, C.hy_w1), (w2, C.hy_w2), (w3, C.hy_w3), (vec, C.hy_vec), (peT, C.peT[si]), (ntn, C.negtn[si])):
        S.dma("sp", dst.ap[:], src, writes=[dst])
    S.dma("sp", dec.ap[:], C.hy_decay[:].partition_broadcast(128), writes=[dec])
    S.op("dve", lambda e: e.scalar_tensor_tensor(out=dec.ap[:], in0=dec.ap[:], scalar=-1.0, in1=dec.ap[:], op0=ALU.mult, op1=ALU.max), reads=[dec], writes=[dec])
    h1 = S.sb("hh1", [64, Lf], F32)
    h2 = S.sb("hh2", [64, Lf], F32)
    tmp = S.sb("htmp", [64, 512], F32)
    S.sin_ki = S.sb("hki", [64, 512], mybir.dt.int32)
    S.sin_kf = S.sb("hkf", [64, 512], F32)
    pp = mk_ring(S, "hpp", [128, 512], F32, 2, psum=True)
    W = min(512, Lf)
    for c0 in range(0, Lf, W):
        p = pp.next()
        S.op("pe", lambda e: e.matmul(p.ap[:64, :W], w1.ap[:], peT.ap[:, c0:c0 + W], start=True, stop=True), reads=[w1, peT], writes=[p])
        S.op("dve", lambda e: e.tensor_scalar(out=tmp.ap[:, :W], in0=p.ap[:64, :W], scalar1=vec.ap[:, 0:1], scalar2=vec.ap[:, 2:3], op0=ALU.add, op1=ALU.mult),
             reads=[p, vec], writes=[tmp])
        sin_tail(S, h1, c0, W, tmp)
    for c0 in range(0, Lf, W):
        p = pp.next()
        S.op("pe", lambda e: e.matmul(p.ap[:64, :W], w2.ap[:], h1.ap[:, c0:c0 + W], start=True, stop=True), reads=[w2, h1], writes=[p])
        S.op("dve", lambda e: e.tensor_scalar(out=tmp.ap[:, :W], in0=p.ap[:64, :W], scalar1=vec.ap[:, 1:2], scalar2=vec.ap[:, 2:3], op0=ALU.add, op1=ALU.mult),
             reads=[p, vec], writes=[tmp])
        sin_tail(S, h2, c0, W, tmp)
    ex = mk_ring(S, "hex", [128, 1024], F32, 2)
    fo = mk_ring(S, "hfo", [128, 2, 512], BF16, 2)
    for tc in range(Lf // 128):
        e_ = ex.next()
        S.op("act", lambda e: e.activation(out=e_.ap[:], in_=dec.ap[:], func=AF.Exp, scale=ntn.ap[:, tc:tc + 1]), reads=[dec, ntn], writes=[e_])
        for hf in range(2):
            p = pp.next()
            S.op("pe", lambda e: e.matmul(p.ap[:], h2.ap[:, tc * 128:(tc + 1) * 128], w3.ap[:, hf * 512:(hf + 1) * 512], start=True, stop=True),
                 reads=[h2, w3], writes=[p])
            S.op("dve", lambda e: e.tensor_tensor(out=e_.ap[:, hf * 512:(hf + 1) * 512], in0=p.ap[:], in1=e_.ap[:, hf * 512:(hf + 1) * 512], op=ALU.mult),
                 reads=[p, e_], writes=[e_])
        if tc == 0:
            S.op("dve", lambda e: e.memset(e_.ap[0:1, 512:1024], 0.0), reads=[], writes=[e_])
        o = fo.next()
        S.op("dve", lambda e: e.tensor_tensor(out=o.ap[:, 0, :], in0=e_.ap[:, 512:1024], in1=e_.ap[:, 0:512], op=ALU.add), reads=[e_], writes=[o])
        S.op("pool", lambda e: e.tensor_tensor(out=o.ap[:, 1, :], in0=e_.ap[:, 512:1024], in1=e_.ap[:, 0:512], op=ALU.subtract), reads=[e_], writes=[o])
        S.dma("sp", C.filt[si][tc * 128:(tc + 1) * 128, :, :], o.ap[:], reads=[o])
    S.end_phase()


def sin_tail(S, dst, c0, W, tmp):
    ki, kf = S.sin_ki, S.sin_kf
    S.op("dve", lambda e: e.tensor_scalar(out=tmp.ap[:, :W], in0=tmp.ap[:, :W], scalar1=1.0 / (2.0 * PI), scalar2=16.5, op0=ALU.mult, op1=ALU.add),
         reads=[tmp], writes=[tmp])
    S.op("dve", lambda e: e.tensor_copy(out=ki.ap[:, :W], in_=tmp.ap[:, :W]), reads=[tmp], writes=[ki])
    S.op("dve", lambda e: e.tensor_copy(out=kf.ap[:, :W], in_=ki.ap[:, :W]), reads=[ki], writes=[kf])
    S.op("dve", lambda e: e.scalar_tensor_tensor(out=tmp.ap[:, :W], in0=tmp.ap[:, :W], scalar=-0.5, in1=kf.ap[:, :W], op0=ALU.add, op1=ALU.subtract),
         reads=[tmp, kf], writes=[tmp])
    S.op("dve", lambda e: e.scalar_tensor_tensor(out=tmp.ap[:, :W], in0=tmp.ap[:, :W], scalar=-0.5, in1=tmp.ap[:, :W], op0=ALU.is_lt, op1=ALU.add),
         reads=[tmp], writes=[tmp])
    S.op("act", lambda e: e.activation(out=dst.ap[:, c0:c0 + W], in_=tmp.ap[:, :W], func=AF.Sin, scale=2.0 * PI * 0.999999), reads=[tmp], writes=[dst])


def hyena_fwd(S, C, si, Lf, src, bi, dst, is_filter):
    nt = Lf // 128
    N = 2 * Lf
    if is_filter:
        S.begin_phase()
    a_in = S.sb("fa", [128, nt, 2 if is_filter else 1, 512], BF16)
    if is_filter:
        S.dma("sp", a_in.ap[:], src.rearrange("(tc p) s c -> p tc s c", p=128), writes=[a_in])
        bb = S.sb("fbias", [128, 512], F32)
        S.dma("sp", bb.ap[:], C.hy_bias[:].partition_broadcast(128), writes=[bb])
        S.op("dve", lambda e: e.tensor_scalar(out=bb.ap[:], in0=bb.ap[:], scalar1=2.0 / N, scalar2=None, op0=ALU.mult), reads=[bb], writes=[bb])
    else:
        S.dma("sp", a_in.ap[:, :, 0, :], src.rearrange("(tc p) c -> p tc c", p=128), writes=[a_in])
    fm = mk_ring(S, "ffm", [128, 2, nt, 128], BF16, 2)
    pr = mk_ring(S, "fpr", [128, 512], F32, 2, psum=True)
    pi_ = mk_ring(S, "fpi", [128, 512], F32, 2, psum=True)
    if is_filter:
        ko = mk_ring(S, "fko", [128, 2, 512], F32, 2)
    else:
        ks = mk_ring(S, "fks", [128, 2, 512], F32, 2)
        tt = mk_ring(S, "ftt", [128, 4, 512], F32, 2)
    for fcn in range(nt):
        m = fm.next()
        for cs in range(2):
            S.dma("sp" if cs == 0 else "pool", m.ap[:, cs, :, :], C.fwd[si][fcn, cs, :, :, :], writes=[m])
        a, b = pr.next(), pi_.next()
        for cs, p in ((0, a), (1, b)):
            for tc in range(nt):
                S.op("pe", lambda e: e.matmul(p.ap[:], m.ap[:, cs, tc, :], a_in.ap[:, tc, cs if is_filter else 0, :], start=(tc == 0), stop=(tc == nt - 1)),
                     reads=[m, a_in], writes=[p])
        if is_filter:
            o = ko.next()
            S.op("dve", lambda e: e.scalar_tensor_tensor(out=o.ap[:, 0, :], in0=a.ap[:], scalar=2.0 / N, in1=bb.ap[:], op0=ALU.mult, op1=ALU.add),
                 reads=[a, bb], writes=[o])
            S.op("act", lambda e: e.activation(out=o.ap[:, 1, :], in_=b.ap[:], func=AF.Identity, scale=2.0 / N), reads=[b], writes=[o])
            S.dma("pool", dst[fcn * 128:(fcn + 1) * 128, :, :], o.ap[:], reads=[o])
        else:
            k = ks.next()
            S.dma("sp", k.ap[:], C.kspec[si][fcn * 128:(fcn + 1) * 128, :, :], writes=[k])
            t = tt.next()
            S.op("dve", lambda e: e.tensor_tensor(out=t.ap[:, 0, :], in0=a.ap[:], in1=k.ap[:, 0, :], op=ALU.mult), reads=[a, k], writes=[t])
            S.op("dve", lambda e: e.tensor_tensor(out=t.ap[:, 1, :], in0=b.ap[:], in1=k.ap[:, 1, :], op=ALU.mult), reads=[b, k], writes=[t])
            S.op("dve", lambda e: e.tensor_tensor(out=t.ap[:, 2, :], in0=a.ap[:], in1=k.ap[:, 1, :], op=ALU.mult), reads=[a, k], writes=[t])
            S.op("dve", lambda e: e.tensor_tensor(out=t.ap[:, 3, :], in0=b.ap[:], in1=k.ap[:, 0, :], op=ALU.mult), reads=[b, k], writes=[t])
            S.op("pool", lambda e: e.tensor_tensor(out=dst.ap[:, fcn, 0, :], in0=t.ap[:, 0, :], in1=t.ap[:, 1, :], op=ALU.add), reads=[t], writes=[dst])
            S.op("pool", lambda e: e.tensor_tensor(out=dst.ap[:, fcn, 1, :], in0=t.ap[:, 2, :], in1=t.ap[:, 3, :], op=ALU.subtract), reads=[t], writes=[dst])
    if is_filter:
        S.end_phase()


def l0_inproj(S, C, bi, src_stage):
    S.begin_phase()
    ident = C.ident
    hT = S.sb("ihT", [128, 8, T + 4], BF16)
    S.op("pool", lambda e: e.memset(hT.ap[:], 0.0), writes=[hT])
    xin = mk_ring(S, "ixin", [128, 1024], F32, 2)
    ptr = mk_ring(S, "iptr", [128, 512], F32, 2, psum=True)
    for i in range(NTT):
        xi = xin.next()
        S.dma("sp", xi.ap[:], tile_src(C, src_stage, bi, i), writes=[xi])
        transpose_mod(S, C, xi, hT, colof(i), 0, 0, (C.R - 1 if i < 2 else bi), ptr, ident)
    S.begin_sub()
    wt = S.sb("iwt", [128, 3, 8, 1024], BF16)
    wv = S.sb("iwv", [128, 8, 128], BF16)
    for j in range(3):
        for kc in range(8):
            S.dma("sp" if kc % 2 else "pool", wt.ap[:, j, kc, :], C.whb[j][kc * 128:(kc + 1) * 128, 512:1536], writes=[wt])
    for kc in range(8):
        S.dma("sp", wv.ap[:, kc, :], C.wqb[kc * 128:(kc + 1) * 128, 640:768], writes=[wv])
    pa = mk_ring(S, "ipa", [128, 512], F32, 2, psum=True)
    pb = mk_ring(S, "ipb", [128, 512], F32, 2, psum=True)
    x1s = mk_ring(S, "ix1", [128, 512], F32, 2)
    ut = mk_ring(S, "iut", [128, 512], BF16, 2)
    vt = mk_ring(S, "ivt", [128, 128], BF16, 2)
    for i in range(NTT):
        c0 = colof(i)
        a, b = pa.next(), pb.next()
        for half, p in ((0, a), (1, b)):
            n = 0
            for j in range(3):
                for kc in range(8):
                    S.op("pe", lambda e: e.matmul(p.ap[:], hT.ap[:, kc, c0 + j - 1:c0 + j - 1 + 128], wt.ap[:, j, kc, half * 512:(half + 1) * 512],
                                                  start=(n == 0), stop=(n == 23)), reads=[hT, wt], writes=[p])
                    n += 1
        x1 = x1s.next()
        S.op("act", lambda e: e.activation(out=x1.ap[:], in_=a.ap[:], func=AF.Identity), reads=[a], writes=[x1])
        u = ut.next()
        S.op("dve", lambda e: e.tensor_tensor(out=u.ap[:], in0=b.ap[:], in1=x1.ap[:], op=ALU.mult), reads=[b, x1], writes=[u])
        S.dma("pool", C.u[bi, i * 128:(i + 1) * 128, :], u.ap[:], reads=[u])
        p = pa.next()
        for kc in range(8):
            S.op("pe", lambda e: e.matmul(p.ap[:, :128], hT.ap[:, kc, c0:c0 + 128], wv.ap[:, kc, :], start=(kc == 0), stop=(kc == 7)),
                 reads=[hT, wv], writes=[p])
        v = vt.next()
        S.op("act", lambda e: e.activation(out=v.ap[:], in_=p.ap[:, :128], func=AF.Identity), reads=[p], writes=[v])
        S.dma("pool", C.v[bi, i * 128:(i + 1) * 128, :], v.ap[:], reads=[v])
    S.end_sub()
    S.begin_sub()
    w0 = S.sb("iw0", [128, 3, 8, 512], BF16)
    wq = S.sb("iwq", [128, 8, 1280], BF16)
    rope = S.sb("irope", [64, 2, LAT], F32)
    S.dma("sp", rope.ap[:], C.rope[:, :, :], writes=[rope])
    for j in range(3):
        for kc in range(8):
            S.dma("sp" if kc % 2 else "pool", w0.ap[:, j, kc, :], C.whb[j][kc * 128:(kc + 1) * 128, 0:512], writes=[w0])
    for kc in range(8):
        S.dma("sp", wq.ap[:, kc, 0:640], C.wqb[kc * 128:(kc + 1) * 128, 0:640], writes=[wq])
        S.dma("pool", wq.ap[:, kc, 640:1280], C.wqb[kc * 128:(kc + 1) * 128, 768:1408], writes=[wq])
    pa = mk_ring(S, "jpa", [128, 512], F32, 3, psum=True)
    ot = mk_ring(S, "jot", [128, 512], BF16, 3)
    t1 = mk_ring(S, "jt1", [64, 512], F32, 2)
    t2 = mk_ring(S, "jt2", [64, 512], F32, 2)
    tiles = [(0, 256)] + [(256 + 512 * k, 512) for k in range(4)]
    for (tok0, w) in tiles:
        c0 = colof(tok0 // 128)
        for cc in range(4):
            p = pa.next()
            n = 0
            for j in range(3):
                for kc in range(8):
                    S.op("pe", lambda e: e.matmul(p.ap[:, :w], w0.ap[:, j, kc, cc * 128:(cc + 1) * 128], hT.ap[:, kc, c0 + j - 1:c0 + j - 1 + w],
                                                  start=(n == 0), stop=(n == 23)), reads=[w0, hT], writes=[p])
                    n += 1
            o = ot.next()
            S.op("act", lambda e: e.activation(out=o.ap[:, :w], in_=p.ap[:, :w], func=AF.Identity), reads=[p], writes=[o])
            S.dma("pool", C.x0T[bi, cc * 128:(cc + 1) * 128, tok0:tok0 + w], o.ap[:, :w], reads=[o])
        for hh in range(10):
            p = pa.next()
            for kc in range(8):
                S.op("pe", lambda e: e.matmul(p.ap[:64, :w], wq.ap[:, kc, hh * 64:(hh + 1) * 64], hT.ap[:, kc, c0:c0 + w], start=(kc == 0), stop=(kc == 7)),
                     reads=[wq, hT], writes=[p])
            o = ot.next()
            dst = C.qT[bi, hh, :, tok0:tok0 + w] if hh < 8 else C.kT[bi, hh - 8, :, tok0:tok0 + w]
            if tok0 == 0:
                S.op("act", lambda e: e.activation(out=o.ap[:64, :w], in_=p.ap[:64, :w], func=AF.Identity), reads=[p], writes=[o])
            else:
                p2 = pa.next()
                for kc in range(8):
                    S.op("pe", lambda e: e.matmul(p2.ap[:64, :w], wq.ap[:, kc, 640 + hh * 64:640 + (hh + 1) * 64], hT.ap[:, kc, c0:c0 + w],
                                                  start=(kc == 0), stop=(kc == 7)), reads=[wq, hT], writes=[p2])
                l0_ = tok0 - 256
                a, b = t1.next(), t2.next()
                S.op("dve", lambda e: e.tensor_tensor(out=a.ap[:, :w], in0=p.ap[:64, :w], in1=rope.ap[:, 0, l0_:l0_ + w], op=ALU.mult), reads=[p, rope], writes=[a])
                S.op("dve", lambda e: e.tensor_tensor(out=b.ap[:, :w], in0=p2.ap[:64, :w], in1=rope.ap[:, 1, l0_:l0_ + w], op=ALU.mult), reads=[p2, rope], writes=[b])
                S.op("pool", lambda e: e.tensor_tensor(out=o.ap[:64, :w], in0=a.ap[:, :w], in1=b.ap[:, :w], op=ALU.add), reads=[a, b], writes=[o])
            S.dma("sp", dst, o.ap[:64, :w], reads=[o])
    S.end_sub()
    S.end_phase()


def l0_hyena(S, C, bi):
    for si, (Lf, tok0) in enumerate(((LAT, 256), (CTX, 0))):
        S.begin_phase()
        nt = Lf // 128
        Y = S.sb("hyY", [128, nt, 2, 512], BF16)
        S.begin_sub()
        hyena_fwd(S, C, si, Lf, C.u[bi, tok0:tok0 + Lf, :], bi, Y, is_filter=False)
        S.end_sub()
        tw = min(512, Lf)
        iv = mk_ring(S, "hyiv", [128, nt, 2, tw], BF16, 2 if Lf == CTX else 1)
        x0 = mk_ring(S, "hyx0", [128, tw], BF16, 2)
        ya = mk_ring(S, "hyya", [128, tw], BF16, 2)
        pp = mk_ring(S, "hypp", [128, 512], F32, 2, psum=True)
        for tt in range(Lf // tw):
            m = iv.next()
            for fc in range(nt):
                S.dma("sp" if fc % 2 else "pool", m.ap[:, fc, :, :], C.inv[si][tt, :, fc, :, :], writes=[m])
            for cc in range(4):
                p = pp.next()
                n = 0
                for fc in range(nt):
                    for cs in range(2):
                        S.op("pe", lambda e: e.matmul(p.ap[:, :tw], Y.ap[:, fc, cs, cc * 128:(cc + 1) * 128], m.ap[:, fc, cs, :],
                                                      start=(n == 0), stop=(n == 2 * nt - 1)), reads=[Y, m], writes=[p])
                        n += 1
                xz = x0.next()
                t0_ = tok0 + tt * tw
                S.dma("sp", xz.ap[:], C.x0T[bi, cc * 128:(cc + 1) * 128, t0_:t0_ + tw], writes=[xz])
                o = ya.next()
                S.op("dve", lambda e: e.tensor_tensor(out=o.ap[:], in0=p.ap[:, :tw], in1=xz.ap[:], op=ALU.mult), reads=[p, xz], writes=[o])
                S.dma("pool", C.yaT[bi, cc * 128:(cc + 1) * 128, t0_:t0_ + tw], o.ap[:], reads=[o])
        S.end_phase()


def l0_attn(S, C, bi):
    S.begin_phase()
    qT = S.sb("aqT", [64, 8, T], BF16)
    kT = S.sb("akT", [64, 2, T], BF16)
    v = S.sb("av", [128, NTT, 128], BF16)
    yb = S.sb("ayb", [64, 8, T], BF16)
    mask = S.sb("amask", [128, 384], F32)
    sink = S.sb("asink", [128, 8], F32)
    idb = S.sb("aidb", [128, 128], BF16)
    for h in range(8):
        S.dma("sp" if h % 2 else "pool", qT.ap[:, h, :], C.qT[bi, h, :, :], writes=[qT])
    for h in range(2):
        S.dma("sp", kT.ap[:, h, :], C.kT[bi, h, :, :], writes=[kT])
    S.dma("sp", v.ap[:], C.v[bi].rearrange("(i p) c -> p i c", p=128), writes=[v])
    S.dma("sp", mask.ap[:], C.amask[:, :], writes=[mask])
    S.dma("sp", sink.ap[:], C.attn_sink[:].partition_broadcast(128), writes=[sink])
    S.dma("sp", idb.ap[:], C.ident_b[:, :], writes=[idb])
    psl = mk_ring(S, "apsl", [128, 512], F32, 2, psum=True)
    psc = mk_ring(S, "apsc", [128, 512], F32, 2, psum=True)
    ppt = mk_ring(S, "appt", [128, 5, 128], BF16, 2, psum=True)
    ppv = mk_ring(S, "appv", [128, 128], F32, 2, psum=True)
    sc = mk_ring(S, "asc", [128, 640], F32, 2)
    pe_ = mk_ring(S, "ape", [128, 640], F32, 2)
    pn = mk_ring(S, "apn", [128, 640], BF16, 2)
    pts = mk_ring(S, "apts", [128, 5, 128], BF16, 2)
    sm = mk_ring(S, "asm", [128, 8], F32, 4)
    def head_chain(qb, hh, n, lo, hi, m0, ktiles, nk, q0):
            h = hh // 4
            s_ = sc.next()
            if n:
                a = psl.next()
                S.op("pe", lambda e: e.matmul(a.ap[:, :n], qT.ap[:, hh, q0:q0 + 128], kT.ap[:, h, 256 + lo:256 + hi], start=True, stop=True),
                     reads=[qT, kT], writes=[a])
                S.op("dve", lambda e: e.tensor_tensor(out=s_.ap[:, :n], in0=a.ap[:, :n], in1=mask.ap[:, m0:m0 + n], op=ALU.add), reads=[a, mask], writes=[s_])
            b = psc.next()
            S.op("pe", lambda e: e.matmul(b.ap[:, :256], qT.ap[:, hh, q0:q0 + 128], kT.ap[:, h, 0:256], start=True, stop=True), reads=[qT, kT], writes=[b])
            S.op("act", lambda e: e.activation(out=s_.ap[:, n:nk], in_=b.ap[:, :256], func=AF.Identity), reads=[b], writes=[s_])
            yield
            w = sm.next()
            S.op("dve", lambda e: e.tensor_reduce(out=w.ap[:, 0:1], in_=s_.ap[:, :nk], axis=AX.X, op=ALU.max), reads=[s_], writes=[w])
            S.op("dve", lambda e: e.tensor_scalar(out=w.ap[:, 1:2], in0=w.ap[:, 0:1], scalar1=0.125, scalar2=sink.ap[:, hh:hh + 1], op0=ALU.mult, op1=ALU.max),
                 reads=[w, sink], writes=[w])
            S.op("dve", lambda e: e.tensor_scalar(out=w.ap[:, 2:3], in0=w.ap[:, 1:2], scalar1=-1.0, scalar2=None, op0=ALU.mult), reads=[w], writes=[w])
            p_ = pe_.next()
            S.op("act", lambda e: e.activation(out=p_.ap[:, :nk], in_=s_.ap[:, :nk], func=AF.Exp, scale=0.125, bias=w.ap[:, 2:3]),
                 reads=[s_, w], writes=[p_])
            yield
            S.op("dve", lambda e: e.tensor_reduce(out=w.ap[:, 3:4], in_=p_.ap[:, :nk], axis=AX.X, op=ALU.add), reads=[p_], writes=[w])
            S.op("act", lambda e: e.activation(out=w.ap[:, 4:5], in_=w.ap[:, 2:3], func=AF.Exp, bias=sink.ap[:, hh:hh + 1], scale=1.0), reads=[w, sink], writes=[w])
            S.op("dve", lambda e: e.tensor_tensor(out=w.ap[:, 5:6], in0=w.ap[:, 3:4], in1=w.ap[:, 4:5], op=ALU.add), reads=[w], writes=[w])
            S.op("dve", lambda e: e.reciprocal(out=w.ap[:, 6:7], in_=w.ap[:, 5:6]), reads=[w], writes=[w])
            pn_ = pn.next()
            S.op("act", lambda e: e.activation(out=pn_.ap[:, :nk], in_=p_.ap[:, :nk], func=AF.Identity, scale=w.ap[:, 6:7]), reads=[p_, w], writes=[pn_])
            yield
            pt = ppt.next()
            nch = nk // 128
            for j in range(nch):
                S.op("pe", lambda e: e.transpose(out=pt.ap[:, j, :], in_=pn_.ap[:, j * 128:(j + 1) * 128], identity=idb.ap[:]), reads=[pn_, idb], writes=[pt])
            ps_ = pts.next()
            S.op("act", lambda e: e.activation(out=ps_.ap[:, :nch, :], in_=pt.ap[:, :nch, :], func=AF.Identity), reads=[pt], writes=[ps_])
            yield
            o = ppv.next()
            for j in range(nch):
                S.op("pe", lambda e: e.matmul(o.ap[:64, :], v.ap[:, ktiles[j], h * 64:(h + 1) * 64], ps_.ap[:, j, :], start=(j == 0), stop=(j == nch - 1)),
                     reads=[v, ps_], writes=[o])
            S.op("dve", lambda e: e.tensor_copy(out=yb.ap[:, hh, q0:q0 + 128], in_=o.ap[:64, :]), reads=[o], writes=[yb])
            yield

    for qb in range(NTT):
        is_ctx = qb < 2
        q0 = qb * 128
        lo = hi = m0 = 0
        if is_ctx:
            n = 0
            ktiles = []
        else:
            lq = q0 - 256
            lo, hi = max(0, lq - 128), min(LAT, lq + 256)
            n = hi - lo
            m0 = lo - (lq - 128)
            ktiles = [2 + lo // 128 + j for j in range(n // 128)]
        ktiles = ktiles + [0, 1]
        nk = n + 256
        for h0 in range(0, 8, 2):
            interleave([head_chain(qb, hh, n, lo, hi, m0, ktiles, nk, q0) for hh in (h0, h0 + 1)])
    for h in range(8):
        S.dma("sp" if h % 2 else "pool", C.ybT[bi, h, :, :], yb.ap[:, h, :], reads=[yb])
    S.end_phase()


def l0_outproj(S, C, bi, src_stage, dst_stage):
    S.begin_phase()
    ya = S.sb("oya", [128, 4, T], BF16)
    yb = S.sb("oyb", [64, 8, T], BF16)
    wa = S.sb("owa", [128, 4, D], BF16)
    wb = S.sb("owb", [64, 8, D], BF16)
    for c in range(4):
        S.dma("sp", ya.ap[:, c, :], C.yaT[bi, c * 128:(c + 1) * 128, :], writes=[ya])
        S.dma("pool", wa.ap[:, c, :], C.wob0[c * 128:(c + 1) * 128, :], writes=[wa])
    for h in range(8):
        S.dma("sp", yb.ap[:, h, :], C.ybT[bi, h, :, :], writes=[yb])
        S.dma("pool", wb.ap[:, h, :], C.wob0[512 + h * 64:512 + (h + 1) * 64, :], writes=[wb])
    epi = Epi(S, C, "o")
    epi.load(0, 0, bi)
    xin = mk_ring(S, "oxin", [128, 1024], F32, 3)
    py = mk_ring(S, "opy", [128, 1024], F32, 2, psum=True)
    for i in range(NTT):
        xi = xin.next()
        S.dma("sp", xi.ap[:], tile_src(C, src_stage, bi, i), writes=[xi])
        p = py.next()
        for hf in range(2):
            for c in range(4):
                S.op("pe", lambda e: e.matmul(p.ap[:, hf * 512:(hf + 1) * 512], ya.ap[:, c, i * 128:(i + 1) * 128], wa.ap[:, c, hf * 512:(hf + 1) * 512],
                                              start=(c == 0), stop=False), reads=[ya, wa], writes=[p])
            for h in range(8):
                S.op("pe", lambda e: e.matmul(p.ap[:, hf * 512:(hf + 1) * 512], yb.ap[:, h, i * 128:(i + 1) * 128], wb.ap[:, h, hf * 512:(hf + 1) * 512],
                                              start=False, stop=(h == 7)), reads=[yb, wb], writes=[p])
        epi.run(p, xi, i < 2, C.xs[dst_stage - 1][bi, i * 128:(i + 1) * 128, :])
    S.end_phase()


def V(b, *idx):
    return (b, b.ap[idx] if idx else b.ap[:])


def TT(S, eng, o, a, b, op):
    return S.op(eng, lambda e: e.tensor_tensor(out=o[1], in0=a[1], in1=b[1], op=op), reads=[a[0], b[0]], writes=[o[0]])


def TS(S, eng, o, a, s1, s2, op0, op1=None, extra=()):
    if op1 is None:
        return S.op(eng, lambda e: e.tensor_scalar(out=o[1], in0=a[1], scalar1=s1, scalar2=None, op0=op0), reads=[a[0], *extra], writes=[o[0]])
    return S.op(eng, lambda e: e.tensor_scalar(out=o[1], in0=a[1], scalar1=s1, scalar2=s2, op0=op0, op1=op1), reads=[a[0], *extra], writes=[o[0]])


def STT(S, o, a, sc, b, op0, op1, extra=()):
    return S.op("dve", lambda e: e.scalar_tensor_tensor(out=o[1], in0=a[1], scalar=sc, in1=b[1], op0=op0, op1=op1), reads=[a[0], b[0], *extra], writes=[o[0]])


def ACT(S, o, a, func, scale=1.0, bias=None, extra=()):
    if bias is None:
        return S.op("act", lambda e: e.activation(out=o[1], in_=a[1], func=func, scale=scale), reads=[a[0], *extra], writes=[o[0]])
    return S.op("act", lambda e: e.activation(out=o[1], in_=a[1], func=func, scale=scale, bias=bias), reads=[a[0], *extra], writes=[o[0]])


def MM(S, o, l, r, start=True, stop=True):
    return S.op("pe", lambda e: e.matmul(o[1], l[1], r[1], start=start, stop=stop), reads=[l[0], r[0]], writes=[o[0]])


def TR(S, o, a, ident):
    return S.op("pe", lambda e: e.transpose(out=o[1], in_=a[1], identity=ident[1]), reads=[a[0], ident[0]], writes=[o[0]])


RW_E = float(np.exp(-0.5))
INV_DT = BF16


def declare_l1(nc, C, dbg=None):
    NB = C.NB
    I = lambda n, s, dt=F32: dram(nc, n, s, dt, "ExternalInput")
    Sx = lambda n, s, dt=BF16: dram(nc, n, s, dt, "ExternalOutput" if (dbg and n in dbg) else None)
    C.o_w_rw = I("o_w_rw", [D, 1920])
    C.o_w_dn = I("o_w_dn", [D, 1536])
    C.o_w_z = I("o_w_z", [D, 528])
    C.o_w_out = I("o_w_out", [D, D])
    C.rw_mu = I("rw_mu", [1920])
    C.dn_conv = I("dn_conv", [3, 1536])
    C.rw_rows = I("rw_rows", [10, 512])
    C.rw_w2 = I("rw_w2", [128, 512])
    C.rw_a2 = I("rw_a2", [128, 512])
    C.rw_g2 = I("rw_g2", [128, 512])
    C.dn_rows = I("dn_rows", [3, 8])
    C.dn_ng = I("dn_ng", [512])
    C.tri = I("tri", [2, 128, 128])
    C.tris = I("tris", [2, 128, 128])
    C.blkm = I("blkm", [4, 128, 128])
    C.tsw = Sx("tsw", [3, 1920], F32)
    C.wrb = [Sx(f"wrb{j}", [D, 1920]) for j in range(3)]
    C.wdb = [Sx(f"wdb{j}", [D, 1536]) for j in range(3)]
    C.wzb = Sx("wzb", [D, 528])
    C.wob1 = Sx("wob1", [D, D])
    C.rw_ops = Sx("rw_ops", [NB, 2, 6, T, 512])
    C.rw_v = Sx("rw_v", [NB, T, 512])
    C.rw_gc = Sx("rw_gc", [NB, 2, NTT, 64, 8], F32)
    C.rw_g = Sx("rw_g", [NB, T, 512], F32)
    C.rw_bonus = Sx("rw_bonus", [NB, T, 512], F32)
    C.y_rw = Sx("y_rw", [NB, 2, T, 512], F32)
    C.dn_qk = Sx("dn_qk", [NB, 2, T, 512])
    C.dn_ops = Sx("dn_ops", [NB, 2, 5, T, 512])
    C.dn_G = Sx("dn_G", [NB, 2, 3, T, 4], F32)
    C.dn_z = Sx("dn_z", [NB, T, 512], F32)
    C.y_dn = Sx("y_dn", [NB, 2, T, 512], F32)


def host_l1(inputs, m):
    f = lambda a: np.ascontiguousarray(np.asarray(a, dtype=np.float32))
    w = np.asarray(inputs["o_w_in"])[0]
    m["o_w_rw"] = f(w[:, :1920])
    m["o_w_dn"] = f(w[:, 1920:1920 + 1536])
    m["o_w_z"] = f(w[:, 1920 + 1536:])
    m["o_w_out"] = f(np.asarray(inputs["o_w_out"])[0])
    m["rw_mu"] = f(np.asarray(inputs["rw_mu"])[0])
    m["dn_conv"] = f(np.asarray(inputs["dn_conv"])[0])
    g = lambda k: np.asarray(inputs[k])[0]
    m["rw_rows"] = f(np.stack([g("rw_w0")[0], g("rw_w0")[1], g("rw_a0")[0], g("rw_a0")[1], g("rw_kk"), g("rw_ka"),
                               g("rw_rk").reshape(512), g("rw_lnx_g"), g("rw_lnx_b"), np.zeros(512, np.float32)], 0))
    m["rw_w2"] = f(g("rw_w2").reshape(128, 512))
    m["rw_a2"] = f(g("rw_a2").reshape(128, 512))
    m["rw_g2"] = f(g("rw_g2"))
    m["dn_rows"] = f(np.stack([g("dn_A_log").reshape(8), g("dn_dt_bias").reshape(8), np.zeros(8, np.float32)], 0))
    m["dn_ng"] = f(np.tile(g("dn_norm_g"), 4))
    j = np.arange(128)[:, None]
    t = np.arange(128)[None, :]
    m["tri"] = f(np.stack([(j <= t), (j >= t)], 0))
    m["tris"] = f(np.stack([(j < t), (j > t)], 0))
    bd = lambda n: (j // n == t // n)
    m["blkm"] = f(np.stack([bd(16), bd(32) & ~bd(16), bd(64) & ~bd(32), ~bd(64)], 0))
    return m


def l1_setup(S, C):
    S.begin_phase()
    mu = S.sb("smu", [1, 1920], F32)
    o = S.sb("smo", [1, 3, 1920], F32)
    S.dma("sp", mu.ap[:], C.rw_mu[:].partition_broadcast(1), writes=[mu])
    TS(S, "dve", V(o, slice(None), 0, slice(None)), V(mu), 0.5, None, ALU.mult)
    TS(S, "dve", V(o, slice(None), 1, slice(None)), V(mu), -1.0, 1.0, ALU.mult, ALU.add)
    TS(S, "dve", V(o, slice(None), 2, slice(None)), V(mu), 0.5, None, ALU.mult)
    S.dma("sp", C.tsw.rearrange("(o j) n -> o j n", o=1), o.ap[:], reads=[o])
    S.end_phase()
    for j in range(3):
        prep_weight(S, C.wrb[j], C.o_w_rw, D, 1920, scale=C.tsw[j, :], tag=f"qr{j}")
        prep_weight(S, C.wdb[j], C.o_w_dn, D, 1536, scale=C.dn_conv[j, :], tag=f"qd{j}")
    prep_weight(S, C.wzb, C.o_w_z, D, 528, tag="qz")
    prep_weight(S, C.wob1, C.o_w_out, D, D, tag="qo")


def build_hT(S, C, bi, src_stage, l):
    hT = S.sb("bhT", [128, 8, T + 4], BF16)
    S.op("pool", lambda e: e.memset(hT.ap[:], 0.0), writes=[hT])
    S.begin_sub()
    xin = mk_ring(S, "bxin", [128, 1024], F32, 2)
    ptr = mk_ring(S, "bptr", [128, 512], F32, 2, psum=True)
    for i in range(NTT):
        xi = xin.next()
        S.dma("sp", xi.ap[:], tile_src(C, src_stage, bi, i), writes=[xi])
        transpose_mod(S, C, xi, hT, colof(i), l, 0, (C.R - 1 if i < 2 else bi), ptr, C.ident)
    S.end_sub()
    return hT


def bc3(b, n, w):
    return (b, b.ap[:, 0:n].unsqueeze(2).to_broadcast([128, n, w]))


def r3(b, n, *pre):
    ap = b.ap[pre] if pre else b.ap[:]
    return (b, ap.rearrange("p (h d) -> p h d", h=n))


def proj3(S, C, p, hT, c0, w, wt, col0, ncol):
    n = 0
    for j in range(3):
        for kc in range(8):
            MM(S, (p, p.ap[:, :ncol]), (hT, hT.ap[:, kc, c0 + j - 1:c0 + j - 1 + 128]), (wt, wt.ap[:, j, kc, col0:col0 + ncol]), start=(n == 0), stop=(n == 23))
            n += 1


def l1_feat_rw(S, C, bi, hT):
    S.begin_sub()
    ones = S.sb("fones", [128, 128], F32)
    S.op("pool", lambda e: e.memset(ones.ap[:], 1.0), writes=[ones])
    tri = S.sb("ftri", [128, 2, 128], F32)
    for d in range(2):
        S.dma("sp", tri.ap[:, d, :], C.tri[d, :, :], writes=[tri])
    rows = S.sb("frows", [128, 7, 512], F32)
    for q in range(7):
        S.dma("sp", rows.ap[:, q, :], C.rw_rows[q, :].partition_broadcast(128), writes=[rows])
    lwb = S.sb("flwb", [128, 3, 512], BF16)
    loraT = S.sb("floraT", [128, 3, T], BF16)
    S.begin_sub()
    lw = S.sb("flw", [128, 3, 512], F32)
    for q, src in enumerate((C.rw_w2, C.rw_a2, C.rw_g2)):
        S.dma("sp", lw.ap[:, q, :], src[:, :], writes=[lw])
    S.op("dve", lambda e: e.tensor_copy(out=lwb.ap[:], in_=lw.ap[:]), reads=[lw], writes=[lwb])
    wl = S.sb("fwl", [128, 3, 8, 384], BF16)
    for j in range(3):
        for kc in range(8):
            S.dma("sp" if kc % 2 else "pool", wl.ap[:, j, kc, :], C.wrb[j][kc * 128:(kc + 1) * 128, 1536:1920], writes=[wl])
    pp = mk_ring(S, "fpp", [128, 512], F32, 2, psum=True)
    for (tok0, w) in [(0, 256)] + [(256 + 512 * k, 512) for k in range(4)]:
        c0 = colof(tok0 // 128)
        for q, fn in enumerate((AF.Tanh, AF.Identity, AF.Sigmoid)):
            p = pp.next()
            n = 0
            for j in range(3):
                for kc in range(8):
                    MM(S, (p, p.ap[:, :w]), (wl, wl.ap[:, j, kc, q * 128:(q + 1) * 128]), (hT, hT.ap[:, kc, c0 + j - 1:c0 + j - 1 + w]), start=(n == 0), stop=(n == 23))
                    n += 1
            ACT(S, (loraT, loraT.ap[:, q, tok0:tok0 + w]), (p, p.ap[:, :w]), fn)
    S.end_sub()
    wr = S.sb("fwr", [128, 3, 8, 1536], BF16)
    for j in range(3):
        for kc in range(8):
            S.dma("sp" if kc % 2 else "pool", wr.ap[:, j, kc, :], C.wrb[j][kc * 128:(kc + 1) * 128, 0:1536], writes=[wr])
    prkv = [S.ps(f"fp{n}", [128, 512], F32) for n in "rkv"]
    pqd = [mk_ring(S, f"fpq{d}", [128, 512], F32, 2, psum=True) for d in range(2)]
    F = lambda n: S.sb("f_" + n, [128, 512], F32)
    rs, ks, kkr, sq, kk, ksum = [F(n) for n in "rs ks kkr sq kk ksum".split()]
    Fd = []
    for d in range(2):
        X = Ctx()
        X.zt, X.logw, X.a_, X.Gs, X.eG, X.enG, X.eE, X.kd, X.bd = [F(f"{n}{d}") for n in "zt logw a Gs eG enG eE kd bd".split()]
        X.eP, X.t1 = X.zt, X.Gs
        Fd.append(X)
    tb, bon, gs = sq, Fd[0].zt, Fd[0].Gs
    vb = mk_ring(S, "fvb", [128, 512], BF16, 2)
    ot = mk_ring(S, "fot", [128, 6, 512], BF16, 2)
    sm = mk_ring(S, "fsm", [128, 16], F32, 2)
    gcs = mk_ring(S, "fgcs", [64, 8], F32, 2)
    KK, KA, RK = [(rows, rows.ap[:, q, :]) for q in (4, 5, 6)]

    def dir_chain(d, i, rws):
        X = Fd[d]
        zt, logw, a_, Gs, eG, enG, eP, eE, t1, kd, bd = X.zt, X.logw, X.a_, X.Gs, X.eG, X.enG, X.eP, X.eE, X.t1, X.kd, X.bd
        o = ot.next()
        pq = pqd[d]
        pgc = prkv[d]
        pz, pa_ = pq.next(), pq.next()
        MM(S, V(pz), (loraT, loraT.ap[d * 64:(d + 1) * 64, 0, rws]), (lwb, lwb.ap[d * 64:(d + 1) * 64, 0, :]))
        MM(S, V(pa_), (loraT, loraT.ap[d * 64:(d + 1) * 64, 1, rws]), (lwb, lwb.ap[d * 64:(d + 1) * 64, 1, :]))
        TT(S, "dve", V(zt), V(pz), (rows, rows.ap[:, d, :]), ALU.add)
        ACT(S, V(zt), V(zt), AF.Sigmoid)
        TS(S, "pool", V(logw), V(zt), -RW_E, None, ALU.mult)
        TT(S, "dve", V(a_), V(pa_), (rows, rows.ap[:, 2 + d, :]), ALU.add)
        ACT(S, V(a_), V(a_), AF.Sigmoid)
        yield
        pG, pT = pq.next(), pq.next()
        MM(S, V(pG), (tri, tri.ap[:, d, :]), V(logw))
        MM(S, V(pT), V(ones), V(logw))
        for h in range(8):
            MM(S, (pgc, pgc.ap[:64, h:h + 1]), (logw, logw.ap[:, h * 64:(h + 1) * 64]), (ones, ones.ap[:, 0:1]))
        gc = gcs.next()
        ACT(S, V(gc), (pgc, pgc.ap[:64, 0:8]), AF.Exp)
        S.dma("pool", C.rw_gc[bi, d, i, :, :], gc.ap[:], reads=[gc])
        ACT(S, V(Gs), V(pG), AF.Identity)
        yield
        ACT(S, V(eG), V(Gs), AF.Exp)
        ACT(S, V(enG), V(Gs), AF.Exp, scale=-1.0)
        TT(S, "dve", V(eP), V(Gs), V(logw), ALU.subtract)
        ACT(S, V(eP), V(eP), AF.Exp)
        TT(S, "dve", V(eE), V(pT), V(Gs), ALU.subtract)
        ACT(S, V(eE), V(eE), AF.Exp)
        yield
        STT(S, V(t1), V(a_), -1.0, KA, ALU.add, ALU.mult)
        STT(S, V(kd), V(t1), 1.0, V(ks), ALU.add, ALU.mult)
        TT(S, "pool", V(bd), V(kk), V(a_), ALU.mult)
        yield
        O = lambda q: (o, o.ap[:, q, :])
        TT(S, "dve", O(0), V(rs), V(eG), ALU.mult)
        TT(S, "pool", O(1), V(kd), V(enG), ALU.mult)
        TT(S, "dve", O(2), V(bd), V(enG), ALU.mult)
        TT(S, "pool", O(3), V(kk), V(eP), ALU.mult)
        TT(S, "dve", O(4), V(kd), V(eE), ALU.mult)
        TT(S, "pool", O(5), V(bd), V(eE), ALU.mult)
        S.dma("sp", C.rw_ops[bi, d, :, rws, :].rearrange("q t c -> t q c"), o.ap[:], reads=[o])
        yield

    for i in range(NTT):
        c0 = colof(i)
        rws = slice(i * 128, (i + 1) * 128)
        for n in range(3):
            proj3(S, C, prkv[n], hT, c0, 128, wr, n * 512, 512)
        ACT(S, V(rs), V(prkv[0]), AF.Identity)
        ACT(S, V(ks), V(prkv[1]), AF.Identity)
        v_ = vb.next()
        ACT(S, V(v_), V(prkv[2]), AF.Identity)
        S.dma("pool", C.rw_v[bi, rws, :], v_.ap[:], reads=[v_])
        w = sm.next()
        TT(S, "dve", V(kkr), V(ks), KK, ALU.mult)
        TT(S, "pool", V(sq), V(kkr), V(kkr), ALU.mult)
        S.op("dve", lambda e: e.tensor_reduce(out=w.ap[:, 0:8], in_=r3(sq, 8)[1], axis=AX.X, op=ALU.add), reads=[sq], writes=[w])
        TS(S, "dve", V(w, slice(None), slice(0, 8)), V(w, slice(None), slice(0, 8)), 1e-6, None, ALU.add)
        ACT(S, V(w, slice(None), slice(0, 8)), V(w, slice(None), slice(0, 8)), AF.Sqrt)
        S.op("dve", lambda e: e.reciprocal(out=w.ap[:, 0:8], in_=w.ap[:, 0:8]), reads=[w], writes=[w])
        TT(S, "dve", r3(kk, 8), r3(kkr, 8), bc3(w, 8, 64), ALU.mult)
        interleave([dir_chain(d, i, rws) for d in range(2)])
        TT(S, "pool", V(ksum), V(Fd[0].kd), V(Fd[1].kd), ALU.add)
        TT(S, "dve", V(tb), V(rs), V(ksum), ALU.mult)
        TT(S, "pool", V(tb), V(tb), RK, ALU.mult)
        S.op("dve", lambda e: e.tensor_reduce(out=w.ap[:, 8:16], in_=r3(tb, 8)[1], axis=AX.X, op=ALU.add), reads=[tb], writes=[w])
        TT(S, "dve", r3(bon, 8), r3(v_, 8), (w, w.ap[:, 8:16].unsqueeze(2).to_broadcast([128, 8, 64])), ALU.mult)
        S.dma("pool", C.rw_bonus[bi, rws, :], bon.ap[:], reads=[bon])
        pg = pqd[0].next()
        MM(S, V(pg), (loraT, loraT.ap[:, 2, rws]), (lwb, lwb.ap[:, 2, :]))
        ACT(S, V(gs), V(pg), AF.Identity)
        S.dma("pool", C.rw_g[bi, rws, :], gs.ap[:], reads=[gs])
    S.end_sub()


def l1_feat_dn(S, C, bi, hT):
    S.begin_sub()
    ones = S.sb("gones", [128, 128], F32)
    S.op("pool", lambda e: e.memset(ones.ap[:], 1.0), writes=[ones])
    tri = S.sb("gtri", [128, 2, 128], F32)
    for d in range(2):
        S.dma("sp", tri.ap[:, d, :], C.tri[d, :, :], writes=[tri])
    dr = S.sb("gdr", [128, 2, 8], F32)
    for q in range(2):
        S.dma("sp", dr.ap[:, q, :], C.dn_rows[q, :].partition_broadcast(128), writes=[dr])
    ACT(S, V(dr, slice(None), 0, slice(None)), V(dr, slice(None), 0, slice(None)), AF.Exp)
    TS(S, "dve", V(dr, slice(None), 0, slice(None)), V(dr, slice(None), 0, slice(None)), -1.0, None, ALU.mult)
    wd = S.sb("gwd", [128, 3, 8, 1536], BF16)
    wz = S.sb("gwz", [128, 8, 528], BF16)
    for j in range(3):
        for kc in range(8):
            S.dma("sp" if kc % 2 else "pool", wd.ap[:, j, kc, :], C.wdb[j][kc * 128:(kc + 1) * 128, :], writes=[wd])
    for kc in range(8):
        S.dma("sp", wz.ap[:, kc, :], C.wzb[kc * 128:(kc + 1) * 128, :], writes=[wz])
    pqkv = [S.ps(f"gp{n}", [128, 512], F32) for n in "qkv"]
    pz = S.ps("gpz", [128, 512], F32)
    pgt = S.ps("gpgt", [128, 16], F32)
    pG = mk_ring(S, "gpG", [128, 8], F32, 2, psum=True)
    F = lambda n: S.sb("g_" + n, [128, 512], F32)
    qs, ks, vs, sq, zs = [F(n) for n in "qs ks vs sq zs".split()]
    qk = mk_ring(S, "gqk", [128, 2, 512], BF16, 2)
    ot = mk_ring(S, "got", [128, 5, 512], BF16, 2)
    sm = mk_ring(S, "gsm", [128, 64], F32, 2)
    go = mk_ring(S, "ggo", [128, 3, 4], F32, 2)
    for i in range(NTT):
        c0 = colof(i)
        rws = slice(i * 128, (i + 1) * 128)
        for n in range(3):
            proj3(S, C, pqkv[n], hT, c0, 128, wd, n * 512, 512)
        for kc in range(8):
            MM(S, V(pz), (hT, hT.ap[:, kc, c0:c0 + 128]), (wz, wz.ap[:, kc, 0:512]), start=(kc == 0), stop=(kc == 7))
        for kc in range(8):
            MM(S, V(pgt), (hT, hT.ap[:, kc, c0:c0 + 128]), (wz, wz.ap[:, kc, 512:528]), start=(kc == 0), stop=(kc == 7))
        for src, dst in zip(pqkv + [pz], (qs, ks, vs, zs)):
            ACT(S, V(dst), V(src), AF.Silu)
        S.dma("pool", C.dn_z[bi, rws, :], zs.ap[:], reads=[zs])
        w = sm.next()
        Wc = lambda a, b: (w, w.ap[:, a:b])
        ACT(S, Wc(0, 16), V(pgt), AF.Identity)
        qk_ = qk.next()
        for n, (src, sc) in enumerate(((qs, 128.0 ** -0.5), (ks, 1.0))):
            TT(S, "pool", V(sq), V(src), V(src), ALU.mult)
            S.op("dve", lambda e: e.tensor_reduce(out=w.ap[:, 16 + 4 * n:20 + 4 * n], in_=r3(sq, 4)[1], axis=AX.X, op=ALU.add), reads=[sq], writes=[w])
            TS(S, "dve", Wc(16 + 4 * n, 20 + 4 * n), Wc(16 + 4 * n, 20 + 4 * n), 1e-6, None, ALU.add)
            ACT(S, Wc(16 + 4 * n, 20 + 4 * n), Wc(16 + 4 * n, 20 + 4 * n), AF.Sqrt)
            S.op("dve", lambda e: e.reciprocal(out=w.ap[:, 16 + 4 * n:20 + 4 * n], in_=w.ap[:, 16 + 4 * n:20 + 4 * n]), reads=[w], writes=[w])
            if sc != 1.0:
                TS(S, "dve", Wc(16, 20), Wc(16, 20), sc, None, ALU.mult)
            TT(S, "dve", r3(src, 4), r3(src, 4), (w, w.ap[:, 16 + 4 * n:20 + 4 * n].unsqueeze(2).to_broadcast([128, 4, 128])), ALU.mult)
            S.op("pool", lambda e: e.tensor_copy(out=qk_.ap[:, n, :], in_=src.ap[:]), reads=[src], writes=[qk_])
        S.dma("sp", C.dn_qk[bi, :, rws, :].rearrange("q t c -> t q c"), qk_.ap[:], reads=[qk_])
        for d in range(2):
            o = ot.next()
            g_ = go.next()
            TT(S, "dve", Wc(24, 28), Wc(d * 4, d * 4 + 4), (dr, dr.ap[:, 1, d * 4:d * 4 + 4]), ALU.add)
            ACT(S, Wc(24, 28), Wc(24, 28), AF.Exp)
            ACT(S, Wc(24, 28), Wc(24, 28), AF.Ln, bias=1.0)
            TT(S, "dve", (g_, g_.ap[:, 0, :]), Wc(24, 28), (dr, dr.ap[:, 0, d * 4:d * 4 + 4]), ALU.mult)
            ACT(S, Wc(28, 32), Wc(8 + d * 4, 12 + d * 4), AF.Sigmoid)
            p = pG.next()
            MM(S, (p, p.ap[:, 0:4]), (tri, tri.ap[:, d, :]), (g_, g_.ap[:, 0, :]))
            MM(S, (p, p.ap[:, 4:8]), V(ones), (g_, g_.ap[:, 0, :]))
            ACT(S, (g_, g_.ap[:, 1:3, :]), (p, p.ap[:, 0:8].rearrange("p (a b) -> p a b", a=2)), AF.Identity)
            S.dma("pool", C.dn_G[bi, d, :, rws, :].rearrange("q t c -> t q c"), g_.ap[:], reads=[g_])
            ACT(S, Wc(32, 36), (g_, g_.ap[:, 1, :]), AF.Exp)
            TT(S, "dve", Wc(36, 40), (g_, g_.ap[:, 2, :]), (g_, g_.ap[:, 1, :]), ALU.subtract)
            ACT(S, Wc(36, 40), Wc(36, 40), AF.Exp)
            TT(S, "dve", Wc(40, 44), Wc(28, 32), Wc(32, 36), ALU.mult)
            B4 = lambda a: (w, w.ap[:, a:a + 4].unsqueeze(2).to_broadcast([128, 4, 128]))
            O = lambda q: (o, o.ap[:, q, :].rearrange("p (h d) -> p h d", h=4))
            TT(S, "dve", O(0), r3(qs, 4), B4(32), ALU.mult)
            TT(S, "pool", O(1), r3(ks, 4), B4(28), ALU.mult)
            TT(S, "dve", O(2), r3(ks, 4), B4(40), ALU.mult)
            TT(S, "pool", O(3), r3(ks, 4), B4(36), ALU.mult)
            TT(S, "dve", O(4), r3(vs, 4), B4(28), ALU.mult)
            S.dma("sp", C.dn_ops[bi, d, :, rws, :].rearrange("q t c -> t q c"), o.ap[:], reads=[o])
    S.end_sub()


def inv_group(S, P, PT, K, ps, out, nh=4):
    mk = lambda: K.rW.next()
    pA, pB, pC = ps
    PD, PDT, Z, ZT = mk(), mk(), K.rZ.next(), K.rZ.next()
    TT(S, "dve", V(PD), V(P), V(K.mb[0]), ALU.mult)
    TT(S, "pool", V(PDT), V(PT), V(K.mb[0]), ALU.mult)
    TT(S, "dve", V(Z), V(K.ident4), V(PD), ALU.subtract)
    TT(S, "pool", V(ZT), V(K.ident4), V(PDT), ALU.subtract)
    yield
    cur, curT = PD, PDT
    for lv in range(3):
        for h in range(nh):
            MM(S, (pA, pA.ap[:, h, :]), (curT, curT.ap[:, h, :]), (cur, cur.ap[:, h, :]))
        for h in range(nh):
            MM(S, (pB, pB.ap[:, h, :]), (cur, cur.ap[:, h, :]), (curT, curT.ap[:, h, :]))
        Pn, PTn = mk(), mk()
        ACT(S, V(Pn), V(pA), AF.Identity)
        S.op("dve", lambda e: e.tensor_copy(out=PTn.ap[:], in_=pB.ap[:]), reads=[pB], writes=[PTn])
        yield
        for h in range(nh):
            MM(S, (pC, pC.ap[:, h, :]), (PTn, PTn.ap[:, h, :]), (Z, Z.ap[:, h, :]))
        for h in range(nh):
            MM(S, (pA, pA.ap[:, h, :]), (Pn, Pn.ap[:, h, :]), (ZT, ZT.ap[:, h, :]))
        TT(S, "dve", V(Z), V(Z), V(pC), ALU.add)
        TT(S, "dve", V(ZT), V(ZT), V(pA), ALU.add)
        yield
        cur, curT = Pn, PTn
    for m in range(1, 4):
        last = m == 3
        O, OT, Y = mk(), mk(), mk()
        TT(S, "dve", V(O), V(P), V(K.mb[m]), ALU.mult)
        TT(S, "pool", V(OT), V(PT), V(K.mb[m]), ALU.mult)
        for h in range(nh):
            MM(S, (pA, pA.ap[:, h, :]), (OT, OT.ap[:, h, :]), (Z, Z.ap[:, h, :]))
        ACT(S, V(Y), V(pA), AF.Identity)
        if not last:
            YT = mk()
            for h in range(nh):
                MM(S, (pB, pB.ap[:, h, :]), (Z, Z.ap[:, h, :]), (OT, OT.ap[:, h, :]))
            S.op("dve", lambda e: e.tensor_copy(out=YT.ap[:], in_=pB.ap[:]), reads=[pB], writes=[YT])
        yield
        for h in range(nh):
            MM(S, (pC, pC.ap[:, h, :]), (ZT, ZT.ap[:, h, :]), (Y, Y.ap[:, h, :]))
        if not last:
            for h in range(nh):
                MM(S, (pB, pB.ap[:, h, :]), (Y, Y.ap[:, h, :]), (ZT, ZT.ap[:, h, :]))
        TT(S, "dve", V(Z), V(Z), V(pC), ALU.subtract)
        if not last:
            TT(S, "dve", V(ZT), V(ZT), V(pB), ALU.subtract)
        yield
    out.append(Z)


def interleave(gens):
    gens = list(gens)
    while gens:
        for g in list(gens):
            try:
                next(g)
            except StopIteration:
                gens.remove(g)


def scan_consts(S, C, tag):
    K = Ctx()
    K.idb = S.sb(tag + "idb", [128, 128], BF16)
    S.dma("sp", K.idb.ap[:], C.ident_b[:, :], writes=[K.idb])
    K.ident4 = S.sb(tag + "id4", [128, 4, 128], F32)
    K.mSI = [S.sb(tag + f"mSI{d}", [128, 4, 2, 128], F32) for d in range(2)]
    K.mS = [S.sb(tag + f"mS{d}", [128, 4, 128], F32) for d in range(2)]
    K.mI = [S.sb(tag + f"mI{d}", [128, 4, 128], F32) for d in range(2)]
    for h in range(4):
        S.dma("sp", K.ident4.ap[:, h, :], C.ident_d[:, :], writes=[K.ident4])
        for d in range(2):
            S.dma("sp", K.mSI[d].ap[:, h, 0, :], C.tris[d, :, :], writes=[K.mSI[d]])
            S.dma("pool", K.mSI[d].ap[:, h, 1, :], C.tri[d, :, :], writes=[K.mSI[d]])
            S.dma("sp", K.mS[d].ap[:, h, :], C.tris[d, :, :], writes=[K.mS[d]])
            S.dma("pool", K.mI[d].ap[:, h, :], C.tri[d, :, :], writes=[K.mI[d]])
    K.mb = [S.sb(tag + f"mb{m}", [128, 4, 128], F32) for m in range(4)]
    for m in range(4):
        for h in range(4):
            S.dma("sp" if h % 2 else "pool", K.mb[m].ap[:, h, :], C.blkm[m, :, :], writes=[K.mb[m]])
    return K


def chain_res(S, K, tag):
    R = Ctx()
    R.__dict__.update(K.__dict__)
    R.rP = mk_ring(S, tag + "rP", [128, 4, 128], INV_DT, 1)
    R.rPT = mk_ring(S, tag + "rPT", [128, 4, 128], INV_DT, 1)
    R.rW = mk_ring(S, tag + "rW", [128, 4, 128], INV_DT, 8)
    R.rZ = mk_ring(S, tag + "rZ", [128, 4, 128], INV_DT, 4)
    return R


def tile_order(d):
    return list(range(NTT)) if d == 0 else [1, 0] + list(range(NTT - 1, 1, -1))


def l1_scan_rw(S, C, bi):
    S.begin_phase()
    S.keep_pool = True
    K0 = scan_consts(S, C, "r")
    interleave([rw_chain(S, C, chain_res(S, K0, f"r{d}"), bi, d) for d in range(2)])
    S.keep_pool = False
    S.end_phase()


def rw_chain(S, C, K, bi, d):
    t = f"r{d}"
    X0, X1, X2 = [S.ps(t + f"X{n}", [128, 4, 128], F32) for n in range(3)]
    ptr = S.ps(t + "ptr", [64, 8, 128], BF16)
    ot_r = mk_ring(S, t + "ot", [128, 6, 512], BF16, 2)
    vt_r = mk_ring(S, t + "vt", [128, 512], BF16, 2)
    gc_r = mk_ring(S, t + "gc", [64, 8], F32, 2)
    AR_r = mk_ring(S, t + "AR", [64, 8, 2, 128], BF16, 1)
    KT_r = mk_ring(S, t + "KT", [64, 8, 128], BF16, 1)
    BT_r = mk_ring(S, t + "BT", [64, 8, 128], BF16, 1)
    KN_r = mk_ring(S, t + "KN", [128, 8, 2, 128], BF16, 1)
    NB_r = mk_ring(S, t + "NB", [128, 8, 128], BF16, 1)
    Zb_r = mk_ring(S, t + "Zb", [128, 4, 128], BF16, 1)
    WT_r = mk_ring(S, t + "WT", [64, 8, 128], BF16, 1)
    Xs_r = mk_ring(S, t + "Xs", [128, 4, 64], BF16, 1)
    nU0_r = mk_ring(S, t + "nU0", [128, 8, 64], F32, 1)
    nU_r = mk_ring(S, t + "nU", [128, 8, 64], BF16, 1)
    ys_r = mk_ring(S, t + "ys", [128, 512], F32, 2)
    ST = S.sb(t + "ST", [64, 8, 64], F32)
    STb = S.sb(t + "STb", [64, 8, 64], BF16)
    S.op("dve", lambda e: e.memset(ST.ap[:], 0.0), writes=[ST])
    S.op("dve", lambda e: e.memset(STb.ap[:], 0.0), writes=[STb])
    v8 = lambda p: p.ap[:].rearrange("p a b -> p (a b)").rearrange("p (h v) -> p h v", v=64)
    order = tile_order(d)
    pend = {}

    def load(i):
        rws_ = slice(i * 128, (i + 1) * 128)
        ot, vt, gc = ot_r.next(), vt_r.next(), gc_r.next()
        S.dma("sp", ot.ap[:], C.rw_ops[bi, d, :, rws_, :].rearrange("q t c -> t q c"), writes=[ot])
        S.dma("pool", vt.ap[:], C.rw_v[bi, rws_, :], writes=[vt])
        S.dma("pool", gc.ap[:], C.rw_gc[bi, d, i, :, :], writes=[gc])
        pend[i] = (ot, vt, gc)

    load(order[0])
    for n_, i in enumerate(order):
        rws = slice(i * 128, (i + 1) * 128)
        if n_ + 1 < len(order):
            load(order[n_ + 1])
        ot, vt, gc = pend.pop(i)
        AR, KT, BT = AR_r.next(), KT_r.next(), BT_r.next()
        bview = lambda X: X.ap[:].bitcast(BF16).rearrange("p a (b c) -> p (a b) c", b=2)
        tgt = [(ptr, ptr.ap[:]), (X0, bview(X0)[:64]), (X1, bview(X1)[:64]), (X2, bview(X2)[:64])]
        for n_, (q, dst) in enumerate(((3, (AR, AR.ap[:, :, 0, :])), (0, (AR, AR.ap[:, :, 1, :])), (1, V(KT)), (2, V(BT)))):
            tb_, tv = tgt[n_]
            for h in range(8):
                TR(S, (tb_, tv[:, h, :]), (ot, ot.ap[:, q, h * 64:(h + 1) * 64]), V(K.idb))
        for n_, (q, dst) in enumerate(((3, (AR, AR.ap[:, :, 0, :])), (0, (AR, AR.ap[:, :, 1, :])), (1, V(KT)), (2, V(BT)))):
            tb_, tv = tgt[n_]
            if n_ % 2 == 0:
                ACT(S, dst, (tb_, tv), AF.Identity)
            else:
                S.op("dve", lambda e: e.tensor_copy(out=dst[1], in_=tv), reads=[tb_], writes=[dst[0]])
        yield
        KN, NBm, WT, nU0 = KN_r.next(), NB_r.next(), WT_r.next(), nU0_r.next()
        for g in range(2):
            hs = [(hl, g * 4 + hl) for hl in range(4)]
            P, PT = K.rP.next(), K.rPT.next()
            for hl, h in hs:
                MM(S, (X0, X0.ap[:, hl, :]), (BT, BT.ap[:, h, :]), (AR, AR.ap[:, h, 0, :]))
            for hl, h in hs:
                MM(S, (X1, X1.ap[:, hl, :]), (AR, AR.ap[:, h, 0, :]), (BT, BT.ap[:, h, :]))
            for hl, h in hs:
                MM(S, (X2, X2.ap[:, hl, :]), (BT, BT.ap[:, h, :]), (AR, AR.ap[:, h, 1, :]))
            TT(S, "dve", V(P), V(X0), V(K.mS[d]), ALU.mult)
            TT(S, "dve", V(PT), V(X1), V(K.mS[1 - d]), ALU.mult)
            TT(S, "dve", (NBm, NBm.ap[:, g * 4:(g + 1) * 4, :]), V(X2), V(K.mI[d]), ALU.mult)
            yield
            for hl, h in hs:
                MM(S, (X0, X0.ap[:, hl, :]), (KT, KT.ap[:, h, :]), (AR, AR.ap[:, h, 0, :]))
            for hl, h in hs:
                MM(S, (X1, X1.ap[:, hl, :]), (KT, KT.ap[:, h, :]), (AR, AR.ap[:, h, 1, :]))
            TT(S, "dve", (KN, KN.ap[:, g * 4:(g + 1) * 4, 0, :]), V(X0), V(K.mS[d]), ALU.mult)
            TT(S, "dve", (KN, KN.ap[:, g * 4:(g + 1) * 4, 1, :]), V(X1), V(K.mI[d]), ALU.mult)
            yield
            zo = []
            yield from inv_group(S, P, PT, K, (X0, X1, X2), zo)
            Zb = zo[0]
            for hl, h in hs:
                MM(S, (X0, X0.ap[:, hl, 0:64]), (KN, KN.ap[:, h, 0, :]), (vt, vt.ap[:, h * 64:(h + 1) * 64]))
            Xs = Xs_r.next()
            ACT(S, V(Xs), (X0, X0.ap[:, :, 0:64]), AF.Identity)
            yield
            for hl, h in hs:
                MM(S, (X1, X1.ap[:64, hl, :]), (ot, ot.ap[:, 3, h * 64:(h + 1) * 64]), (Zb, Zb.ap[:, hl, :]))
            ACT(S, (WT, WT.ap[:, g * 4:(g + 1) * 4, :]), (X1, X1.ap[:64, :, :]), AF.Identity)
            for hl, h in hs:
                MM(S, (X2, X2.ap[:, hl, 0:64]), (Zb, Zb.ap[:, hl, :]), (Xs, Xs.ap[:, hl, :]))
            TS(S, "dve", (nU0, nU0.ap[:, g * 4:(g + 1) * 4, :]), (X2, X2.ap[:, :, 0:64]), -1.0, None, ALU.mult)
            yield
        for h in range(8):
            MM(S, (X0, v8(X0)[:, h, :]), (WT, WT.ap[:, h, :]), (STb, STb.ap[:, h, :]))
        nU = nU_r.next()
        TT(S, "dve", V(nU), V(nU0), (X0, v8(X0)), ALU.subtract)
        yield
        for h in range(8):
            MM(S, (X1, v8(X1)[:, h, :]), (AR, AR.ap[:, h, 1, :]), (STb, STb.ap[:, h, :]), start=True, stop=False)
            MM(S, (X1, v8(X1)[:, h, :]), (KN, KN.ap[:, h, 1, :]), (vt, vt.ap[:, h * 64:(h + 1) * 64]), start=False, stop=False)
            MM(S, (X1, v8(X1)[:, h, :]), (NBm, NBm.ap[:, h, :]), (nU, nU.ap[:, h, :]), start=False, stop=True)
        ys = ys_r.next()
        ACT(S, r3(ys, 8), (X1, v8(X1)), AF.Identity)
        S.dma("sp", C.y_rw[bi, d, rws, :], ys.ap[:], reads=[ys])
        for h in range(8):
            MM(S, (X2, v8(X2)[:64, h, :]), (ot, ot.ap[:, 4, h * 64:(h + 1) * 64]), (vt, vt.ap[:, h * 64:(h + 1) * 64]), start=True, stop=False)
            MM(S, (X2, v8(X2)[:64, h, :]), (ot, ot.ap[:, 5, h * 64:(h + 1) * 64]), (nU, nU.ap[:, h, :]), start=False, stop=True)
        TT(S, "dve", V(ST), V(ST), (gc, gc.ap[:, 0:8].unsqueeze(2).to_broadcast([64, 8, 64])), ALU.mult)
        TT(S, "dve", V(ST), V(ST), (X2, v8(X2)[:64, :, :]), ALU.add)
        ACT(S, V(STb), V(ST), AF.Identity)
        yield


def l1_scan_dn(S, C, bi):
    S.begin_phase()
    S.keep_pool = True
    K0 = scan_consts(S, C, "d")
    K0.ones = S.sb("dones", [128, 128], F32)
    S.op("pool", lambda e: e.memset(K0.ones.ap[:], 1.0), writes=[K0.ones])
    K0.tri = S.sb("dtri", [128, 2, 128], F32)
    for d in range(2):
        S.dma("sp", K0.tri.ap[:, d, :], C.tri[d, :, :], writes=[K0.tri])
    interleave([dn_chain(S, C, chain_res(S, K0, f"d{d}"), bi, d) for d in range(2)])
    S.keep_pool = False
    S.end_phase()


def dn_chain(S, C, K, bi, d):
    t = f"d{d}"
    ones, tri = K.ones, K.tri
    X0, X1, X2 = [S.ps(t + f"X{n}", [128, 4, 128], F32) for n in range(3)]
    ptr = S.ps(t + "ptr", [128, 4, 128], BF16)
    ot_r = mk_ring(S, t + "ot", [128, 5, 512], BF16, 2)
    qk_r = mk_ring(S, t + "qk", [128, 2, 512], BF16, 2)
    G_r = mk_ring(S, t + "G", [128, 3, 4], F32, 2)
    FT_r = mk_ring(S, t + "FT", [128, 4, 4, 128], BF16, 2)
    gl_r = mk_ring(S, t + "gl", [128, 4, 128], F32, 1)
    ET_r = mk_ring(S, t + "ET", [128, 4, 128], F32, 1)
    qkT_r = mk_ring(S, t + "qkT", [128, 4, 128], BF16, 2)
    Zb_r = mk_ring(S, t + "Zb", [128, 4, 128], BF16, 2)
    wT_r = mk_ring(S, t + "wT", [128, 4, 128], BF16, 2)
    u0_r = mk_ring(S, t + "u0", [128, 4, 128], F32, 2)
    u_r = mk_ring(S, t + "u", [128, 4, 128], BF16, 2)
    ys_r = mk_ring(S, t + "ys", [128, 512], F32, 2)
    sm_r = mk_ring(S, t + "sm", [128, 8], F32, 2)
    ST = S.sb(t + "ST", [128, 4, 128], F32)
    STb = S.sb(t + "STb", [128, 4, 128], BF16)
    S.op("dve", lambda e: e.memset(ST.ap[:], 0.0), writes=[ST])
    S.op("dve", lambda e: e.memset(STb.ap[:], 0.0), writes=[STb])
    order = tile_order(d)
    pend = {}

    def load(i):
        rws_ = slice(i * 128, (i + 1) * 128)
        ot, qk, G = ot_r.next(), qk_r.next(), G_r.next()
        S.dma("sp", ot.ap[:], C.dn_ops[bi, d, :, rws_, :].rearrange("q t c -> t q c"), writes=[ot])
        S.dma("pool", qk.ap[:], C.dn_qk[bi, :, rws_, :].rearrange("q t c -> t q c"), writes=[qk])
        S.dma("pool", G.ap[:], C.dn_G[bi, d, :, rws_, :].rearrange("q t c -> t q c"), writes=[G])
        pend[i] = (ot, qk, G)

    load(order[0])
    for n_, i in enumerate(order):
        rws = slice(i * 128, (i + 1) * 128)
        if n_ + 1 < len(order):
            load(order[n_ + 1])
        ot, qk, G = pend.pop(i)
        FT = FT_r.next()
        bview = lambda X: X.ap[:].bitcast(BF16).rearrange("p a (b c) -> p (a b) c", b=2)[:, 0:4, :]
        tgt = [(ptr, ptr.ap[:]), (X0, bview(X0)), (X1, bview(X1)), (X2, bview(X2))]
        for q, src in enumerate(((qk, 1), (qk, 0), (ot, 1), (ot, 0))):
            tb_, tv = tgt[q]
            for h in range(4):
                TR(S, (tb_, tv[:, h, :]), (src[0], src[0].ap[:, src[1], h * 128:(h + 1) * 128]), V(K.idb))
        for q in range(4):
            tb_, tv = tgt[q]
            if q % 2:
                ACT(S, (FT, FT.ap[:, q, :, :]), (tb_, tv), AF.Identity)
            else:
                S.op("dve", lambda e: e.tensor_copy(out=FT.ap[:, q, :, :], in_=tv), reads=[tb_], writes=[FT])
        yield
        gl = gl_r.next()
        for h in range(4):
            TS(S, "pool", (gl, gl.ap[:, h, :]), (tri, tri.ap[:, d, :]), G.ap[:, 0, h:h + 1], None, ALU.mult, extra=[G])
        for h in range(4):
            MM(S, (X0, X0.ap[:, h, :]), V(ones), (gl, gl.ap[:, h, :]))
        ET = ET_r.next()
        for h in range(4):
            TS(S, "dve", (ET, ET.ap[:, h, :]), (X0, X0.ap[:, h, :]), G.ap[:, 1, h:h + 1], 0.0, ALU.subtract, ALU.min, extra=[G])
        ACT(S, V(ET), V(ET), AF.Exp)
        yield
        for h in range(4):
            MM(S, (X1, X1.ap[:, h, :]), (FT, FT.ap[:, 0, h, :]), (FT, FT.ap[:, 2, h, :]))
            MM(S, (X2, X2.ap[:, h, :]), (FT, FT.ap[:, 0, h, :]), (FT, FT.ap[:, 1, h, :]))
        P, PT = K.rP.next(), K.rPT.next()
        TT(S, "dve", V(P), V(X1), V(ET), ALU.mult)
        TT(S, "dve", V(P), V(P), V(K.mS[d]), ALU.mult)
        qkT = qkT_r.next()
        TT(S, "pool", V(ET), V(ET), V(K.mI[d]), ALU.mult)
        TT(S, "dve", V(qkT), V(X2), V(ET), ALU.mult)
        yield
        for h in range(4):
            TR(S, (ptr, ptr.ap[:, h, :]), (P, P.ap[:, h, :]), V(K.idb))
        ACT(S, V(PT), V(ptr), AF.Identity)
        yield
        zo = []
        yield from inv_group(S, P, PT, K, (X0, X1, X2), zo)
        Zb = zo[0]
        for h in range(4):
            MM(S, (X0, X0.ap[:, h, :]), (Zb, Zb.ap[:, h, :]), (ot, ot.ap[:, 4, h * 128:(h + 1) * 128]))
            MM(S, (X1, X1.ap[:, h, :]), (ot, ot.ap[:, 2, h * 128:(h + 1) * 128]), (Zb, Zb.ap[:, h, :]))
        u0, wT = u0_r.next(), wT_r.next()
        ACT(S, V(u0), V(X0), AF.Identity)
        S.op("dve", lambda e: e.tensor_copy(out=wT.ap[:], in_=X1.ap[:]), reads=[X1], writes=[wT])
        yield
        for h in range(4):
            MM(S, (X2, X2.ap[:, h, :]), (wT, wT.ap[:, h, :]), (STb, STb.ap[:, h, :]))
        u = u_r.next()
        TT(S, "dve", V(u), V(u0), V(X2), ALU.subtract)
        yield
        for h in range(4):
            MM(S, (X0, X0.ap[:, h, :]), (FT, FT.ap[:, 3, h, :]), (STb, STb.ap[:, h, :]), start=True, stop=False)
            MM(S, (X0, X0.ap[:, h, :]), (qkT, qkT.ap[:, h, :]), (u, u.ap[:, h, :]), start=False, stop=True)
        ys = ys_r.next()
        ACT(S, r3(ys, 4), V(X0), AF.Identity)
        S.dma("sp", C.y_dn[bi, d, rws, :], ys.ap[:], reads=[ys])
        for h in range(4):
            MM(S, (X1, X1.ap[:, h, :]), (ot, ot.ap[:, 3, h * 128:(h + 1) * 128]), (u, u.ap[:, h, :]))
        sm = sm_r.next()
        ACT(S, (sm, sm.ap[:, 0:4]), (G, G.ap[:, 2, :]), AF.Exp)
        TT(S, "dve", V(ST), V(ST), (sm, sm.ap[:, 0:4].unsqueeze(2).to_broadcast([128, 4, 128])), ALU.mult)
        TT(S, "dve", V(ST), V(ST), V(X1), ALU.add)
        ACT(S, V(STb), V(ST), AF.Identity)
        yield


def l1_out(S, C, bi, src_stage, dst_stage, last):
    S.begin_phase()
    wo = S.sb("xwo", [128, 8, D], BF16)
    for c in range(8):
        S.dma("sp" if c % 2 else "pool", wo.ap[:, c, :], C.wob1[c * 128:(c + 1) * 128, :], writes=[wo])
    rows = S.sb("xrows", [128, 3, 512], F32)
    S.dma("sp", rows.ap[:, 0, :], C.rw_rows[7, :].partition_broadcast(128), writes=[rows])
    S.dma("sp", rows.ap[:, 1, :], C.rw_rows[8, :].partition_broadcast(128), writes=[rows])
    S.dma("sp", rows.ap[:, 2, :], C.dn_ng[:].partition_broadcast(128), writes=[rows])
    epi = Epi(S, C, "x")
    epi.load(1, 0, bi)
    xin = mk_ring(S, "xxin", [128, 1024], F32, 2)
    ya = mk_ring(S, "xya", [128, 2, 512], F32, 2)
    yb = mk_ring(S, "xyb", [128, 2, 512], F32, 2)
    ex = mk_ring(S, "xex", [128, 3, 512], F32, 2)
    sq_r = mk_ring(S, "xsq", [128, 512], F32, 2)
    ycat = mk_ring(S, "xyc", [128, 1024], F32, 2)
    yT = mk_ring(S, "xyT", [128, 8, 128], BF16, 2)
    sm = mk_ring(S, "xsm", [128, 32], F32, 2)
    ptr = mk_ring(S, "xptr", [128, 4, 128], F32, 2, psum=True)
    py = mk_ring(S, "xpy", [128, 1024], F32, 2, psum=True)
    def out_chain(i):
        rws = slice(i * 128, (i + 1) * 128)
        xi, a, b, e_, yc, w, sq = xin.next(), ya.next(), yb.next(), ex.next(), ycat.next(), sm.next(), sq_r.next()
        S.dma("sp", xi.ap[:], tile_src(C, src_stage, bi, i), writes=[xi])
        S.dma("sp", a.ap[:], C.y_rw[bi, :, rws, :].rearrange("q t c -> t q c"), writes=[a])
        S.dma("pool", b.ap[:], C.y_dn[bi, :, rws, :].rearrange("q t c -> t q c"), writes=[b])
        S.dma("sp", e_.ap[:, 0, :], C.rw_g[bi, rws, :], writes=[e_])
        S.dma("pool", e_.ap[:, 1, :], C.rw_bonus[bi, rws, :], writes=[e_])
        S.dma("sp", e_.ap[:, 2, :], C.dn_z[bi, rws, :], writes=[e_])
        y = (a, a.ap[:, 0, :])
        TT(S, "dve", y, y, (a, a.ap[:, 1, :]), ALU.add)
        S.op("dve", lambda e: e.tensor_reduce(out=w.ap[:, 0:8], in_=r3(a, 8, slice(None), 0, slice(None))[1], axis=AX.X, op=ALU.add), reads=[a], writes=[w])
        TS(S, "dve", (w, w.ap[:, 0:8]), (w, w.ap[:, 0:8]), 1.0 / 64, None, ALU.mult)
        TT(S, "dve", r3(a, 8, slice(None), 0, slice(None)), r3(a, 8, slice(None), 0, slice(None)), bc3(w, 8, 64), ALU.subtract)
        TT(S, "pool", V(sq), y, y, ALU.mult)
        S.op("dve", lambda e: e.tensor_reduce(out=w.ap[:, 8:16], in_=r3(sq, 8)[1], axis=AX.X, op=ALU.add), reads=[sq], writes=[w])
        TS(S, "dve", (w, w.ap[:, 8:16]), (w, w.ap[:, 8:16]), 1.0 / 64, 64e-5, ALU.mult, ALU.add)
        ACT(S, (w, w.ap[:, 8:16]), (w, w.ap[:, 8:16]), AF.Sqrt)
        S.op("dve", lambda e: e.reciprocal(out=w.ap[:, 8:16], in_=w.ap[:, 8:16]), reads=[w], writes=[w])
        TT(S, "dve", r3(a, 8, slice(None), 0, slice(None)), r3(a, 8, slice(None), 0, slice(None)),
           (w, w.ap[:, 8:16].unsqueeze(2).to_broadcast([128, 8, 64])), ALU.mult)
        TT(S, "pool", y, y, (rows, rows.ap[:, 0, :]), ALU.mult)
        TT(S, "pool", y, y, (rows, rows.ap[:, 1, :]), ALU.add)
        TT(S, "dve", y, y, (e_, e_.ap[:, 1, :]), ALU.add)
        TT(S, "dve", (yc, yc.ap[:, 0:512]), y, (e_, e_.ap[:, 0, :]), ALU.mult)
        yield
        o = (b, b.ap[:, 0, :])
        TT(S, "dve", o, o, (b, b.ap[:, 1, :]), ALU.add)
        TT(S, "pool", V(sq), o, o, ALU.mult)
        S.op("dve", lambda e: e.tensor_reduce(out=w.ap[:, 16:20], in_=r3(sq, 4)[1], axis=AX.X, op=ALU.add), reads=[sq], writes=[w])
        TS(S, "dve", (w, w.ap[:, 16:20]), (w, w.ap[:, 16:20]), 1.0 / 128, 1e-6, ALU.mult, ALU.add)
        ACT(S, (w, w.ap[:, 16:20]), (w, w.ap[:, 16:20]), AF.Sqrt)
        S.op("dve", lambda e: e.reciprocal(out=w.ap[:, 16:20], in_=w.ap[:, 16:20]), reads=[w], writes=[w])
        TT(S, "dve", r3(b, 4, slice(None), 0, slice(None)), r3(b, 4, slice(None), 0, slice(None)),
           (w, w.ap[:, 16:20].unsqueeze(2).to_broadcast([128, 4, 128])), ALU.mult)
        TT(S, "pool", o, o, (rows, rows.ap[:, 2, :]), ALU.mult)
        TT(S, "dve", (yc, yc.ap[:, 512:1024]), o, (e_, e_.ap[:, 2, :]), ALU.mult)
        yield
        yt = yT.next()
        for g in range(2):
            p = ptr.next()
            for j in range(4):
                c = g * 4 + j
                S.op("pe", lambda e: e.transpose(out=p.ap[:, j, :], in_=yc.ap[:, c * 128:(c + 1) * 128], identity=C.ident.ap[:]), reads=[yc, C.ident], writes=[p])
            ACT(S, (yt, yt.ap[:, g * 4:(g + 1) * 4, :]), V(p), AF.Identity)
        yield
        p = py.next()
        for hf in range(2):
            for c in range(8):
                MM(S, (p, p.ap[:, hf * 512:(hf + 1) * 512]), (yt, yt.ap[:, c, :]), (wo, wo.ap[:, c, hf * 512:(hf + 1) * 512]), start=(c == 0), stop=(c == 7))
        if last:
            dst = C.xs[dst_stage - 1][bi, rws, :]
        else:
            dst = C.xs[dst_stage - 1][bi, rws, :]
        epi.run(p, xi, i < 2, dst)
        yield

    tl = list(range(2 if last else 0, NTT))
    for k in range(0, len(tl), 2):
        interleave([out_chain(i) for i in tl[k:k + 2]])
    S.end_phase()


NB_FULL = 4
N_CORES = 8


def build_full(NB, dbg=None):
    nc = bass.Bass("TRN2", target_bir_lowering=False)
    C = declare_common(nc, NB, dbg=dbg)
    declare_l0(nc, C, dbg=dbg)
    declare_l1(nc, C, dbg=dbg)
    S = Sched(nc)
    common_setup(S, C)
    phase_mod(S, C)
    for l in range(2):
        prep_weight(S, C.w1b[l], C.mlp_w1[l], D, DFF, tag=f"pm1{l}")
        prep_weight(S, C.w2b[l], C.mlp_w2[l], DFF, D, tag=f"pm2{l}")
    l0_setup(S, C)
    l1_setup(S, C)
    for bi in range(NB):
        l0_inproj(S, C, bi, 0)
        l0_hyena(S, C, bi)
        l0_attn(S, C, bi)
        l0_outproj(S, C, bi, 0, 1)
        phase_mlp(S, C, 0, bi, 1, 2, False)
        S.begin_phase()
        hT = build_hT(S, C, bi, 2, 1)
        l1_feat_rw(S, C, bi, hT)
        l1_feat_dn(S, C, bi, hT)
        S.end_phase()
        l1_scan_rw(S, C, bi)
        l1_scan_dn(S, C, bi)
        l1_out(S, C, bi, 2, 3, True)
        phase_mlp(S, C, 1, bi, 3, None, True)
    S.finish()
    return nc


def kernel(**inputs):
    NB = NB_FULL
    nc = build_full(NB)
    in_maps = [host_l1(inputs, host_l0(inputs, host_common(inputs, c, NB))) for c in range(N_CORES)]
    res = run_bass_kernel_spmd(nc, in_maps, core_ids=list(range(N_CORES)))
    out = np.concatenate([np.asarray(r["out"], dtype=np.float32) for r in res.results], axis=0)
    return out
```

```python
import numpy as np
from contextlib import ExitStack
import concourse.bass as bass
import concourse.mybir as mybir
from concourse.bass_utils import run_bass_kernel_spmd

F32 = mybir.dt.float32
BF16 = mybir.dt.bfloat16
AF = mybir.ActivationFunctionType
ALU = mybir.AluOpType
AX = mybir.AxisListType

SAME_ENGINE_SYNC = True
NO_SWDGE = True
POOL_TO_DVE = True
EPOCH = 30000


class Buf:
    __slots__ = ("ap", "name", "lw", "rd")

    def __init__(self, ap, name):
        self.ap = ap
        self.name = name
        self.lw = None
        self.rd = []


class Sched:
    def __init__(self, nc, ndma=24):
        self.nc = nc
        self.stack = ExitStack()
        self.engs = {"pe": nc.tensor, "act": nc.scalar, "dve": nc.vector, "pool": nc.gpsimd, "sp": nc.sync}
        self.csem = {}
        self.ccnt = {}
        self.nsem = 0
        for e in ("pe", "act", "dve", "pool"):
            self._new_csem(e)
        self.dq = {}
        for q in ("sp", "pool", "act"):
            self.dq[q] = [[self._sem(f"d_{q}{i}"), 0] for i in range(ndma)]
        self.dqi = {q: 0 for q in self.dq}
        self.waited = {}
        self.phase_stack = None
        self.ninst = 0

    def _sem(self, name):
        self.nsem += 1
        return self.stack.enter_context(self.nc.semaphore(name))

    def _new_csem(self, e):
        self.csem[e] = self._sem(f"c_{e}_{self.nsem}")
        self.ccnt[e] = 0

    def sb(self, name, shape, dtype, persist=False):
        st = self.stack if (persist or self.phase_stack is None) else self.phase_stack
        self.nsem += 0
        self.uid = getattr(self, "uid", 0) + 1
        t = st.enter_context(self.nc.sbuf_tensor(f"{name}_{self.uid}", list(shape), dtype))
        return Buf(t, name)

    def ps(self, name, shape, dtype, persist=False):
        st = self.stack if (persist or self.phase_stack is None) else self.phase_stack
        self.uid = getattr(self, "uid", 0) + 1
        t = st.enter_context(self.nc.psum_tensor(f"{name}_{self.uid}", list(shape), dtype))
        return Buf(t, name)

    def view(self, ap, name="v"):
        return Buf(ap, name)

    def begin_sub(self):
        if not hasattr(self, "sub_stk"):
            self.sub_stk = []
        self.sub_stk.append(self.phase_stack)
        self.phase_stack = ExitStack()

    def end_sub(self):
        self.barrier()
        self.phase_stack.close()
        self.phase_stack = self.sub_stk.pop()

    def begin_phase(self):
        assert self.phase_stack is None
        self.phase_stack = ExitStack()

    def end_phase(self):
        self.barrier()
        self.phase_stack.close()
        self.phase_stack = None

    def _wait(self, F, tok):
        sem, val, eng = tok
        if eng == F == "pe":
            return
        if eng == F and not SAME_ENGINE_SYNC:
            return
        key = (F, id(sem))
        if self.waited.get(key, 0) >= val:
            return
        self.engs[F].wait_ge(sem, val)
        self.waited[key] = val
        self.ninst += 1

    def _deps(self, F, reads, writes):
        for b in reads:
            if b.lw is not None:
                self._wait(F, b.lw)
        for b in writes:
            if b.lw is not None:
                self._wait(F, b.lw)
            for t in b.rd:
                self._wait(F, t)

    def _commit(self, tok, reads, writes):
        for b in reads:
            if tok[2] == "dma":
                b.rd.append(tok)
            else:
                b.rd = [t for t in b.rd if t[2] != tok[2]]
                b.rd.append(tok)
        for b in writes:
            b.lw = tok
            b.rd = []

    def op(self, F, fn, reads=(), writes=()):
        if F == "pool" and POOL_TO_DVE and not getattr(self, "keep_pool", False):
            F = "dve"
        self._deps(F, reads, writes)
        if self.ccnt[F] >= EPOCH:
            self._new_csem(F)
        inst = fn(self.engs[F])
        self.ccnt[F] += 1
        inst.then_inc(self.csem[F], 1)
        tok = (self.csem[F], self.ccnt[F], F)
        self._commit(tok, reads, writes)
        self.ninst += 1
        return tok

    def dma(self, Q, out, in_, reads=(), writes=(), **kw):
        if NO_SWDGE:
            Q = "sp"
        self._deps(Q, reads, writes)
        pool = self.dq[Q]
        i = self.dqi[Q]
        self.dqi[Q] = (i + 1) % len(pool)
        sem, val = pool[i]
        if val > 0:
            self._wait(Q, (sem, val, "dma"))
        inst = self.engs[Q].dma_start(out=out, in_=in_, **kw)
        inst.then_inc(sem, 16)
        pool[i][1] = val + 16
        tok = (sem, val + 16, "dma")
        self._commit(tok, reads, writes)
        self.ninst += 1
        return tok

    def barrier(self, engines=("pe", "act", "dve", "pool", "sp")):
        toks = []
        for e in ("pe", "act", "dve", "pool"):
            if self.ccnt[e] > 0:
                toks.append((self.csem[e], self.ccnt[e], "bar"))
        for q in self.dq:
            for sem, val in self.dq[q]:
                if val > 0:
                    toks.append((sem, val, "dma"))
        for F in engines:
            for t in toks:
                self._wait(F, t)

    def finish(self):
        self.barrier()
        self.stack.close()


D = 1024
LAT = 2048
CTX = 256
T = LAT + CTX
NTT = T // 128
DFF = 4096
ALPHA = (2.0 * 2) ** 0.25
LN_EPS = 1e-5


class Ring:
    def __init__(self, bufs):
        self.bufs = bufs
        self.i = 0

    def next(self):
        b = self.bufs[self.i]
        self.i = (self.i + 1) % len(self.bufs)
        return b


def mk_ring(S, name, shape, dtype, n=2, psum=False):
    f = S.ps if psum else S.sb
    return Ring([f(f"{name}{i}", shape, dtype) for i in range(n)])


class Ctx:
    pass


def prep_weight(S, dst, src, K, N, scale=None, tag="pw"):
    S.begin_phase()
    CB = min(N, 2048)
    rin = mk_ring(S, tag + "i", [128, CB], F32, 3)
    rout = mk_ring(S, tag + "o", [128, CB], BF16, 2)
    sc = S.sb(tag + "s", [128, CB], F32) if scale is not None else None
    for c0 in range(0, N, CB):
        cw = min(CB, N - c0)
        if scale is not None:
            S.dma("sp", sc.ap[:, :cw], scale[c0:c0 + cw].partition_broadcast(128), writes=[sc])
        k0s = list(range(0, K, 128))
        pend = {}

        def load(k0):
            kw = min(128, K - k0)
            a = rin.next()
            S.dma("sp", a.ap[:kw, :cw], src[k0:k0 + kw, c0:c0 + cw], writes=[a])
            pend[k0] = a

        load(k0s[0])
        for n_, k0 in enumerate(k0s):
            kw = min(128, K - k0)
            if n_ + 1 < len(k0s):
                load(k0s[n_ + 1])
            a = pend.pop(k0)
            o = rout.next()
            if scale is not None:
                S.op("dve", lambda e: e.tensor_tensor(out=o.ap[:kw, :cw], in0=a.ap[:kw, :cw], in1=sc.ap[:kw, :cw], op=ALU.mult),
                     reads=[a, sc], writes=[o])
            else:
                S.op("dve", lambda e: e.tensor_copy(out=o.ap[:kw, :cw], in_=a.ap[:kw, :cw]), reads=[a], writes=[o])
            S.dma("pool", dst[k0:k0 + kw, c0:c0 + cw], o.ap[:kw, :cw], reads=[o])
    S.end_phase()


def phase_mod(S, C):
    nc, R = C.nc, C.R
    S.begin_phase()
    cT = S.sb("cT", [128, 8, R], F32)
    S.dma("sp", cT.ap[:], C.cT[:, :, :], writes=[cT])
    sig = S.sb("sig", [128, 8, R], F32)
    scT = S.sb("scT", [128, 8, R], BF16)
    S.op("act", lambda e: e.activation(out=sig.ap[:], in_=cT.ap[:], func=AF.Sigmoid), reads=[cT], writes=[sig])
    S.op("dve", lambda e: e.tensor_tensor(out=scT.ap[:], in0=cT.ap[:], in1=sig.ap[:], op=ALU.mult), reads=[cT, sig], writes=[scT])
    scbc = S.sb("scbc", [128, 8, R, 128], BF16)
    for kc in range(8):
        for r in range(R):
            S.op("dve", lambda e: e.tensor_copy(out=scbc.ap[:, kc, r, :], in_=scT.ap[:, kc, r:r + 1].to_broadcast([128, 128])),
                 reads=[scT], writes=[scbc])
    mb = S.sb("mb", [128, 2, 48], F32)
    S.dma("sp", mb.ap[:], C.mod_bT[:, :, :], writes=[mb])
    wst = mk_ring(S, "mws", [128, 3072], F32, 2)
    wbf = S.sb("mwbf", [128, 8, 6144], BF16)
    pacc = mk_ring(S, "mps", [128, 512], F32, 2, psum=True)
    gt = mk_ring(S, "mgt", [128, 512], F32, 2)
    mbr = S.sb("mbr", [128, 2048], F32)
    for l in range(2):
        for kc in range(8):
            for hf in range(2):
                a = wst.next()
                S.dma("sp" if hf == 0 else "pool", a.ap[:], C.mod_w[l, kc * 128:(kc + 1) * 128, hf * 3072:(hf + 1) * 3072], writes=[a])
                S.op("dve" if hf == 0 else "act",
                     (lambda e: e.tensor_copy(out=wbf.ap[:, kc, hf * 3072:(hf + 1) * 3072], in_=a.ap[:])) if hf == 0 else
                     (lambda e: e.activation(out=wbf.ap[:, kc, hf * 3072:(hf + 1) * 3072], in_=a.ap[:], func=AF.Identity)),
                     reads=[a], writes=[wbf])
        for fc in range(48):
            p = pacc.next()
            for kc in range(8):
                S.op("pe", lambda e: e.matmul(p.ap[:, :R], wbf.ap[:, kc, fc * 128:(fc + 1) * 128], scT.ap[:, kc, :],
                                              start=(kc == 0), stop=(kc == 7)), reads=[wbf, scT], writes=[p])
            is_scale = (fc // 8) in (1, 4)
            S.op("dve", lambda e: e.tensor_scalar(out=C.modT.ap[:, l, fc, :], in0=p.ap[:, :R], scalar1=mb.ap[:, l, fc:fc + 1],
                                                  scalar2=(1.0 if is_scale else 0.0), op0=ALU.add, op1=ALU.add),
                 reads=[p, mb], writes=[C.modT])
        for gi, c0 in enumerate((2048, 5120)):
            S.dma("sp", mbr.ap[:, gi * 1024:(gi + 1) * 1024], C.mod_b[l, c0:c0 + 1024].partition_broadcast(128), writes=[mbr])
        for gi, c0 in enumerate((2048, 5120)):
            for r in range(R):
                for hf in range(2):
                    p = pacc.next()
                    for kc in range(8):
                        S.op("pe", lambda e: e.matmul(p.ap[:], scbc.ap[:, kc, r, :], wbf.ap[:, kc, c0 + hf * 512:c0 + (hf + 1) * 512],
                                                      start=(kc == 0), stop=(kc == 7)), reads=[scbc, wbf], writes=[p])
                    g = gt.next()
                    S.op("dve", lambda e: e.tensor_tensor(out=g.ap[:], in0=p.ap[:], in1=mbr.ap[:, gi * 1024 + hf * 512:gi * 1024 + (hf + 1) * 512], op=ALU.add),
                         reads=[p, mbr], writes=[g])
                    S.dma("pool", C.gbc[l, gi, r, :, hf * 512:(hf + 1) * 512], g.ap[:], reads=[g])
    S.end_phase()


def tile_src(C, stage, bi, i):
    if stage == 0:
        if i < 2:
            return C.ctx[bi, i * 128:(i + 1) * 128, :]
        return C.x[bi, (i - 2) * 128:(i - 1) * 128, :]
    return C.xs[stage - 1][bi, i * 128:(i + 1) * 128, :]


def rstd_op(S, mv, o, i, eps):
    S.op("dve", lambda e: e.tensor_scalar(out=mv.ap[:, o:o + 1], in0=mv.ap[:, i:i + 1], scalar1=eps, scalar2=None, op0=ALU.add), reads=[mv], writes=[mv])
    S.op("act", lambda e: e.activation(out=mv.ap[:, o:o + 1], in_=mv.ap[:, o:o + 1], func=AF.Sqrt), reads=[mv], writes=[mv])
    S.op("dve", lambda e: e.reciprocal(out=mv.ap[:, o:o + 1], in_=mv.ap[:, o:o + 1]), reads=[mv], writes=[mv])


class Epi:
    def __init__(self, S, C, tag):
        self.S, self.C = S, C
        self.gb = [S.sb(tag + "gb0", [128, 1024], F32), S.sb(tag + "gb1", [128, 1024], F32)]
        self.lg = S.sb(tag + "lg", [128, 1024], F32)
        self.lb = S.sb(tag + "lb", [128, 1024], F32)
        self.t1 = mk_ring(S, tag + "t1", [128, 1024], F32, 2)
        self.xo = mk_ring(S, tag + "xo", [128, 1024], F32, 2)
        self.st = mk_ring(S, tag + "st", [128, 2, 6], F32, 2)
        self.mv = mk_ring(S, tag + "mv", [128, 4], F32, 2)

    def load(self, l, sub, bi):
        S, C = self.S, self.C
        S.dma("sp", self.gb[0].ap[:], C.gbc[l, sub, bi, :, :], writes=[self.gb[0]])
        S.dma("sp", self.gb[1].ap[:], C.gbc[l, sub, C.R - 1, :, :], writes=[self.gb[1]])
        S.dma("sp", self.lg.ap[:], C.ln_g[l, sub, :].partition_broadcast(128), writes=[self.lg])
        S.dma("sp", self.lb.ap[:], C.ln_b[l, sub, :].partition_broadcast(128), writes=[self.lb])

    def run(self, y, xin, is_ctx, dst):
        S = self.S
        gb = self.gb[1 if is_ctx else 0]
        t1, xo, st, mv = self.t1.next(), self.xo.next(), self.st.next(), self.mv.next()
        S.op("dve", lambda e: e.tensor_tensor(out=t1.ap[:], in0=y.ap[:], in1=gb.ap[:], op=ALU.mult), reads=[y, gb], writes=[t1])
        S.op("dve", lambda e: e.scalar_tensor_tensor(out=t1.ap[:], in0=xin.ap[:], scalar=ALPHA, in1=t1.ap[:], op0=ALU.mult, op1=ALU.add),
             reads=[xin, t1], writes=[t1])
        for h in range(2):
            S.op("dve", lambda e: e.bn_stats(out=st.ap[:, h, :], in_=t1.ap[:, h * 512:(h + 1) * 512]), reads=[t1], writes=[st])
        S.op("dve", lambda e: e.bn_aggr(out=mv.ap[:, 0:2], in_=st.ap[:]), reads=[st], writes=[mv])
        rstd_op(S, mv, 2, 1, LN_EPS)
        S.op("dve", lambda e: e.tensor_scalar(out=mv.ap[:, 3:4], in0=mv.ap[:, 0:1], scalar1=mv.ap[:, 2:3], scalar2=-1.0, op0=ALU.mult, op1=ALU.mult),
             reads=[mv], writes=[mv])
        S.op("act", lambda e: e.activation(out=xo.ap[:], in_=t1.ap[:], func=AF.Identity, scale=mv.ap[:, 2:3], bias=mv.ap[:, 3:4]),
             reads=[t1, mv], writes=[xo])
        S.op("pool", lambda e: e.tensor_tensor(out=xo.ap[:], in0=xo.ap[:], in1=self.lg.ap[:], op=ALU.mult), reads=[xo, self.lg], writes=[xo])
        S.op("pool", lambda e: e.tensor_tensor(out=xo.ap[:], in0=xo.ap[:], in1=self.lb.ap[:], op=ALU.add), reads=[xo, self.lb], writes=[xo])
        S.dma("pool", dst, xo.ap[:], reads=[xo])


def transpose_mod(S, C, xin, hT, col0, l, fc0, r, ptr, ident):
    for g in range(2):
        p = ptr.next()
        for j in range(4):
            kc = g * 4 + j
            S.op("pe", lambda e: e.transpose(out=p.ap[:, j * 128:(j + 1) * 128], in_=xin.ap[:, kc * 128:(kc + 1) * 128], identity=ident.ap[:]),
                 reads=[xin, ident], writes=[p])
        for j in range(4):
            kc = g * 4 + j
            S.op("act", lambda e: e.activation(out=hT.ap[:, kc, col0:col0 + 128], in_=p.ap[:, j * 128:(j + 1) * 128], func=AF.Identity,
                                               scale=C.modT.ap[:, l, fc0 + 8 + kc, r:r + 1], bias=C.modT.ap[:, l, fc0 + kc, r:r + 1]),
                 reads=[p, C.modT], writes=[hT])


def phase_mlp(S, C, l, bi, src_stage, dst_stage, last):
    S.begin_phase()
    ident = C.ident
    w1 = S.sb("w1", [128, 8, DFF], BF16)
    w2 = S.sb("w2", [128, 32, D], BF16)
    for kc in range(8):
        S.dma("sp" if kc % 2 == 0 else "pool", w1.ap[:, kc, :], C.w1b[l][kc * 128:(kc + 1) * 128, :], writes=[w1])
    for ko in range(32):
        S.dma("sp" if ko % 2 == 0 else "pool", w2.ap[:, ko, :], C.w2b[l][ko * 128:(ko + 1) * 128, :], writes=[w2])
    epi = Epi(S, C, "m")
    epi.load(l, 1, bi)
    xin = mk_ring(S, "mxin", [128, 1024], F32, 4)
    hT = mk_ring(S, "mhT", [128, 8, 256], BF16, 2)
    aT = S.sb("maT", [128, 32, 256], BF16)
    rl = mk_ring(S, "mrl", [128, 256], BF16, 2)
    ptr = mk_ring(S, "mptr", [128, 512], F32, 2, psum=True)
    pup = mk_ring(S, "mpup", [128, 256], F32, 2, psum=True)
    pdn = mk_ring(S, "mpdn", [128, 1024], F32, 2, psum=True)
    t0 = 1 if last else 0
    mlp_pend = {}

    def mlp_load(tt_):
        mlp_pend[tt_] = []
        for s_ in range(2):
            xi_ = xin.next()
            S.dma("sp", xi_.ap[:], tile_src(C, src_stage, bi, tt_ * 2 + s_), writes=[xi_])
            mlp_pend[tt_].append(xi_)

    for tt in range(t0, 9):
        is_ctx = tt == 0
        r = C.R - 1 if is_ctx else bi
        if tt == t0:
            mlp_load(tt)
        if tt + 1 < 9:
            mlp_load(tt + 1)
        xs_ = mlp_pend.pop(tt)
        h = hT.next()
        for s in range(2):
            transpose_mod(S, C, xs_[s], h, s * 128, l, 24, r, ptr, ident)
        for fo in range(32):
            p = pup.next()
            for kc in range(8):
                S.op("pe", lambda e: e.matmul(p.ap[:], w1.ap[:, kc, fo * 128:(fo + 1) * 128], h.ap[:, kc, :], start=(kc == 0), stop=(kc == 7)),
                     reads=[w1, h], writes=[p])
            rr = rl.next()
            S.op("act", lambda e: e.activation(out=rr.ap[:], in_=p.ap[:], func=AF.Relu), reads=[p], writes=[rr])
            S.op("pool", lambda e: e.tensor_tensor(out=aT.ap[:, fo, :], in0=rr.ap[:], in1=rr.ap[:], op=ALU.mult), reads=[rr], writes=[aT])
        for s in range(2):
            p = pdn.next()
            for hf in range(2):
                for ko in range(32):
                    S.op("pe", lambda e: e.matmul(p.ap[:, hf * 512:(hf + 1) * 512], aT.ap[:, ko, s * 128:(s + 1) * 128], w2.ap[:, ko, hf * 512:(hf + 1) * 512],
                                                  start=(ko == 0), stop=(ko == 31)), reads=[aT, w2], writes=[p])
            i = tt * 2 + s
            if last:
                dst = C.out[bi, (i - 2) * 128:(i - 1) * 128, :]
            else:
                dst = C.xs[dst_stage - 1][bi, i * 128:(i + 1) * 128, :]
            epi.run(p, xs_[s], is_ctx, dst)
    S.end_phase()


def dram(nc, name, shape, dtype, kind=None):
    if kind is None:
        return nc.dram_tensor(name, list(shape), dtype).ap()
    return nc.dram_tensor(name, list(shape), dtype, kind=kind).ap()


def declare_common(nc, NB, dbg=None):
    C = Ctx()
    C.nc, C.NB, C.R = nc, NB, NB + 1
    R = C.R
    I = lambda n, s: dram(nc, n, s, F32, "ExternalInput")
    C.x = I("x", [NB, LAT, D])
    C.ctx = I("ctx", [NB, CTX, D])
    C.cT = I("cT", [128, 8, R])
    C.mod_w = I("mod_w", [2, D, 6 * D])
    C.mod_b = I("mod_b", [2, 6 * D])
    C.mod_bT = I("mod_bT", [128, 2, 48])
    C.ln_g = I("ln_g", [2, 2, D])
    C.ln_b = I("ln_b", [2, 2, D])
    C.mlp_w1 = I("mlp_w1", [2, D, DFF])
    C.mlp_w2 = I("mlp_w2", [2, DFF, D])
    C.ident_d = I("ident", [128, 128])
    C.out = dram(nc, "out", [NB, LAT, D], F32, "ExternalOutput")
    C.gbc = dram(nc, "gbc", [2, 2, R, 128, D], F32)
    C.w1b = [dram(nc, f"w1b{l}", [D, DFF], BF16) for l in range(2)]
    C.w2b = [dram(nc, f"w2b{l}", [DFF, D], BF16) for l in range(2)]
    nst = 3
    C.xs = [dram(nc, f"xs{i}", [NB, T, D], F32, "ExternalOutput" if (dbg and f"xs{i}" in dbg) else None) for i in range(nst)]
    return C


def common_setup(S, C):
    C.modT = S.sb("modT", [128, 2, 48, C.R], F32, persist=True)
    C.ident = S.sb("identsb", [128, 128], F32, persist=True)
    S.dma("sp", C.ident.ap[:], C.ident_d[:, :], writes=[C.ident])


def host_common(inputs, core, NB):
    b0 = core * NB
    f = lambda a: np.ascontiguousarray(np.asarray(a, dtype=np.float32))
    cs = np.concatenate([np.asarray(inputs["c"])[b0:b0 + NB], np.asarray(inputs["c_ctx"])[None, :]], 0)
    m = {
        "x": f(np.asarray(inputs["x"])[b0:b0 + NB]),
        "ctx": f(np.asarray(inputs["ctx"])[b0:b0 + NB]),
        "cT": f(cs.reshape(NB + 1, 8, 128).transpose(2, 1, 0)),
        "mod_w": f(inputs["mod_w"]),
        "mod_b": f(inputs["mod_b"]),
        "mod_bT": f(np.asarray(inputs["mod_b"]).reshape(2, 48, 128).transpose(2, 0, 1)),
        "ln_g": f(inputs["ln_g"]),
        "ln_b": f(inputs["ln_b"]),
        "mlp_w1": f(inputs["mlp_w1"]),
        "mlp_w2": f(inputs["mlp_w2"]),
        "ident": np.eye(128, dtype=np.float32),
    }
    return m


HYW = 512
PI = float(np.pi)


def colof(i):
    return 1 + 128 * i if i < 2 else 259 + 128 * (i - 2)


def declare_l0(nc, C, dbg=None):
    NB = C.NB
    I = lambda n, s, dt=F32: dram(nc, n, s, dt, "ExternalInput")
    Sx = lambda n, s, dt=BF16: dram(nc, n, s, dt, "ExternalOutput" if (dbg and n in dbg) else None)
    C.e_w_hy = I("e_w_hy", [D, 1536])
    C.e_w_qkv = I("e_w_qkv", [D, 768 + 640])
    C.e_w_out = I("e_w_out", [D, D])
    C.hy_conv = I("hy_conv", [3, 1536])
    C.hy_w1 = I("hy_w1", [33, 64])
    C.hy_w2 = I("hy_w2", [64, 64])
    C.hy_w3 = I("hy_w3", [64, 1024])
    C.hy_vec = I("hy_vec", [64, 3])
    C.hy_decay = I("hy_decay", [1024])
    C.hy_bias = I("hy_bias", [512])
    C.attn_sink = I("attn_sink", [8])
    C.peT = [I("peT_l", [33, LAT]), I("peT_c", [33, CTX])]
    C.negtn = [I("negtn_l", [128, LAT // 128]), I("negtn_c", [128, CTX // 128])]
    C.fwd = [I("fwd_l", [16, 2, 128, 16, 128], BF16), I("fwd_c", [2, 2, 128, 2, 128], BF16)]
    C.inv = [I("inv_l", [4, 128, 16, 2, 512], BF16), I("inv_c", [1, 128, 2, 2, 256], BF16)]
    C.rope = I("rope", [64, 2, LAT])
    C.amask = I("amask", [128, 384])
    C.ident_b = I("ident_b", [128, 128], BF16)
    C.whb = [Sx(f"whb{j}", [D, 1536]) for j in range(3)]
    C.wqb = Sx("wqb", [D, 1408])
    C.wob0 = Sx("wob0", [D, D])
    C.kspec = [Sx("kspec_l", [LAT, 2, 512], F32), Sx("kspec_c", [CTX, 2, 512], F32)]
    C.filt = [Sx("filt_l", [LAT, 2, 512]), Sx("filt_c", [CTX, 2, 512])]
    C.u = Sx("u_s", [NB, T, 512])
    C.x0T = Sx("x0T_s", [NB, 512, T])
    C.qT = Sx("qT_s", [NB, 8, 64, T])
    C.kT = Sx("kT_s", [NB, 2, 64, T])
    C.v = Sx("v_s", [NB, T, 128])
    C.yaT = Sx("yaT_s", [NB, 512, T])
    C.ybT = Sx("ybT_s", [NB, 8, 64, T])


def host_l0(inputs, m):
    f = lambda a: np.ascontiguousarray(np.asarray(a, dtype=np.float32))
    import ml_dtypes
    bf = lambda a: np.ascontiguousarray(np.asarray(a, dtype=np.float32).astype(ml_dtypes.bfloat16))
    w = np.asarray(inputs["e_w_in"])[0]
    d = np.arange(64)
    partner = np.where((d % 32) < 16, d + 16, d - 16)
    qcols = 1536 + (np.arange(8)[:, None] * 64 + partner[None, :]).reshape(-1)
    kcols = 2048 + (np.arange(2)[:, None] * 64 + partner[None, :]).reshape(-1)
    m["e_w_hy"] = f(w[:, :1536])
    m["e_w_qkv"] = f(np.concatenate([w[:, 1536:2304], w[:, qcols], w[:, kcols]], 1))
    m["e_w_out"] = f(np.asarray(inputs["e_w_out"])[0])
    m["hy_conv"] = f(np.asarray(inputs["hy_conv"])[0])
    m["hy_w1"] = f(np.asarray(inputs["hy_ffn_w1"])[0])
    m["hy_w2"] = f(np.asarray(inputs["hy_ffn_w2"])[0])
    m["hy_w3"] = f(np.asarray(inputs["hy_ffn_w3"])[0])
    m["hy_vec"] = f(np.stack([np.asarray(inputs["hy_ffn_b1"])[0], np.asarray(inputs["hy_ffn_b2"])[0], np.asarray(inputs["hy_sin_freq"])[0]], 1))
    m["hy_decay"] = f(np.asarray(inputs["hy_decay"])[0])
    m["hy_bias"] = f(np.asarray(inputs["hy_bias"])[0])
    m["attn_sink"] = f(np.asarray(inputs["attn_sink"])[0])
    for tag, Lf in (("l", LAT), ("c", CTX)):
        t = np.arange(Lf, dtype=np.float32)
        t_norm = t / np.float32(max(Lf - 1, 1))
        bands = np.linspace(1e-4, 15, 16, dtype=np.float32)
        ang = (2.0 * np.pi * t[:, None] * bands[None, :] / Lf).astype(np.float32)
        pe = np.concatenate([t_norm[:, None], np.cos(ang), -np.sin(ang)], -1).astype(np.float32)
        m["peT_" + tag] = f(pe.T)
        m["negtn_" + tag] = f((-t_norm).reshape(Lf // 128, 128).T)
        N = 2 * Lf
        nt = Lf // 128
        tt = np.arange(Lf, dtype=np.float64)
        ff = np.arange(Lf, dtype=np.float64) + 0.5
        th = 2.0 * np.pi * np.outer(tt, ff) / N
        Cm, Sm = np.cos(th), np.sin(th)
        fw = np.stack([Cm, Sm], 0).reshape(2, nt, 128, nt, 128)
        m["fwd_" + tag] = bf(fw.transpose(3, 0, 2, 1, 4))
        tw = min(512, Lf)
        iv = np.stack([Cm.T, -Sm.T], 0).reshape(2, nt, 128, Lf // tw, tw)
        m["inv_" + tag] = bf(iv.transpose(3, 2, 1, 0, 4))
    pos = np.arange(LAT)
    inv_freq = (10000.0 ** (-np.arange(16, dtype=np.float32) / 16)).astype(np.float32)
    P = np.where(d[:, None] < 32, (pos // 64)[None, :], (pos % 64)[None, :]).astype(np.float32)
    ang = (P * inv_freq[d % 16][:, None]).astype(np.float32)
    sgn = np.where((d % 32) < 16, -1.0, 1.0)[:, None]
    m["rope"] = f(np.stack([np.cos(ang), sgn * np.sin(ang)], 1))
    qi = np.arange(128)[:, None]
    kj = np.arange(384)[None, :] - 128
    m["amask"] = f(np.where(np.abs(qi - kj) <= 128, 0.0, -30000.0))
    m["ident_b"] = bf(np.eye(128))
    return m


def l0_setup(S, C):
    for j in range(3):
        prep_weight(S, C.whb[j], C.e_w_hy, D, 1536, scale=C.hy_conv[j, :], tag=f"ph{j}")
    prep_weight(S, C.wqb, C.e_w_qkv, D, 1408, tag="pq")
    prep_weight(S, C.wob0, C.e_w_out, D, D, tag="po")
    for si, Lf in enumerate((LAT, CTX)):
        hyena_filter(S, C, si, Lf)
        hyena_fwd(S, C, si, Lf, C.filt[si], None, C.kspec[si], is_filter=True)


def hyena_filter(S, C, si, Lf):
    S.begin_phase()
    w1 = S.sb("hw1", [33, 64], F32)
    w2 = S.sb("hw2", [64, 64], F32)
    w3 = S.sb("hw3", [64, 1024], F32)
    vec = S.sb("hvec", [64, 3], F32)
    peT = S.sb("hpe", [33, Lf], F32)
    ntn = S.sb("hntn", [128, Lf // 128], F32)
    dec = S.sb("hdec", [128, 1024], F32)
    for dst, src in ((w1, C.hy_w1), (w2, C.hy_w2), (w3, C.hy_w3), (vec, C.hy_vec), (peT, C.peT[si]), (ntn, C.negtn[si])):
        S.dma("sp", dst.ap[:], src, writes=[dst])
    S.dma("sp", dec.ap[:], C.hy_decay[:].partition_broadcast(128), writes=[dec])
    S.op("dve", lambda e: e.scalar_tensor_tensor(out=dec.ap[:], in0=dec.ap[:], scalar=-1.0, in1=dec.ap[:], op0=ALU.mult, op1=ALU.max), reads=[dec], writes=[dec])
    h1 = S.sb("hh1", [64, Lf], F32)
    h2 = S.sb("hh2", [64, Lf], F32)
    tmp = S.sb("htmp", [64, 512], F32)
    S.sin_ki = S.sb("hki", [64, 512], mybir.dt.int32)
    S.sin_kf = S.sb("hkf", [64, 512], F32)
    pp = mk_ring(S, "hpp", [128, 512], F32, 2, psum=True)
    W = min(512, Lf)
    for c0 in range(0, Lf, W):
        p = pp.next()
        S.op("pe", lambda e: e.matmul(p.ap[:64, :W], w1.ap[:], peT.ap[:, c0:c0 + W], start=True, stop=True), reads=[w1, peT], writes=[p])
        S.op("dve", lambda e: e.tensor_scalar(out=tmp.ap[:, :W], in0=p.ap[:64, :W], scalar1=vec.ap[:, 0:1], scalar2=vec.ap[:, 2:3], op0=ALU.add, op1=ALU.mult),
             reads=[p, vec], writes=[tmp])
        sin_tail(S, h1, c0, W, tmp)
    for c0 in range(0, Lf, W):
        p = pp.next()
        S.op("pe", lambda e: e.matmul(p.ap[:64, :W], w2.ap[:], h1.ap[:, c0:c0 + W], start=True, stop=True), reads=[w2, h1], writes=[p])
        S.op("dve", lambda e: e.tensor_scalar(out=tmp.ap[:, :W], in0=p.ap[:64, :W], scalar1=vec.ap[:, 1:2], scalar2=vec.ap[:, 2:3], op0=ALU.add, op1=ALU.mult),
             reads=[p, vec], writes=[tmp])
        sin_tail(S, h2, c0, W, tmp)
    ex = mk_ring(S, "hex", [128, 1024], F32, 2)
    fo = mk_ring(S, "hfo", [128, 2, 512], BF16, 2)
    for tc in range(Lf // 128):
        e_ = ex.next()
        S.op("act", lambda e: e.activation(out=e_.ap[:], in_=dec.ap[:], func=AF.Exp, scale=ntn.ap[:, tc:tc + 1]), reads=[dec, ntn], writes=[e_])
        for hf in range(2):
            p = pp.next()
            S.op("pe", lambda e: e.matmul(p.ap[:], h2.ap[:, tc * 128:(tc + 1) * 128], w3.ap[:, hf * 512:(hf + 1) * 512], start=True, stop=True),
                 reads=[h2, w3], writes=[p])
            S.op("dve", lambda e: e.tensor_tensor(out=e_.ap[:, hf * 512:(hf + 1) * 512], in0=p.ap[:], in1=e_.ap[:, hf * 512:(hf + 1) * 512], op=ALU.mult),
                 reads=[p, e_], writes=[e_])
        if tc == 0:
            S.op("dve", lambda e: e.memset(e_.ap[0:1, 512:1024], 0.0), reads=[], writes=[e_])
        o = fo.next()
        S.op("dve", lambda e: e.tensor_tensor(out=o.ap[:, 0, :], in0=e_.ap[:, 512:1024], in1=e_.ap[:, 0:512], op=ALU.add), reads=[e_], writes=[o])
        S.op("pool", lambda e: e.tensor_tensor(out=o.ap[:, 1, :], in0=e_.ap[:, 512:1024], in1=e_.ap[:, 0:512], op=ALU.subtract), reads=[e_], writes=[o])
        S.dma("sp", C.filt[si][tc * 128:(tc + 1) * 128, :, :], o.ap[:], reads=[o])
    S.end_phase()


def sin_tail(S, dst, c0, W, tmp):
    ki, kf = S.sin_ki, S.sin_kf
    S.op("dve", lambda e: e.tensor_scalar(out=tmp.ap[:, :W], in0=tmp.ap[:, :W], scalar1=1.0 / (2.0 * PI), scalar2=16.5, op0=ALU.mult, op1=ALU.add),
         reads=[tmp], writes=[tmp])
    S.op("dve", lambda e: e.tensor_copy(out=ki.ap[:, :W], in_=tmp.ap[:, :W]), reads=[tmp], writes=[ki])
    S.op("dve", lambda e: e.tensor_copy(out=kf.ap[:, :W], in_=ki.ap[:, :W]), reads=[ki], writes=[kf])
    S.op("dve", lambda e: e.scalar_tensor_tensor(out=tmp.ap[:, :W], in0=tmp.ap[:, :W], scalar=-0.5, in1=kf.ap[:, :W], op0=ALU.add, op1=ALU.subtract),
         reads=[tmp, kf], writes=[tmp])
    S.op("dve", lambda e: e.scalar_tensor_tensor(out=tmp.ap[:, :W], in0=tmp.ap[:, :W], scalar=-0.5, in1=tmp.ap[:, :W], op0=ALU.is_lt, op1=ALU.add),
         reads=[tmp], writes=[tmp])
    S.op("act", lambda e: e.activation(out=dst.ap[:, c0:c0 + W], in_=tmp.ap[:, :W], func=AF.Sin, scale=2.0 * PI * 0.999999), reads=[tmp], writes=[dst])


def hyena_fwd(S, C, si, Lf, src, bi, dst, is_filter):
    nt = Lf // 128
    N = 2 * Lf
    if is_filter:
        S.begin_phase()
    a_in = S.sb("fa", [128, nt, 2 if is_filter else 1, 512], BF16)
    if is_filter:
        S.dma("sp", a_in.ap[:], src.rearrange("(tc p) s c -> p tc s c", p=128), writes=[a_in])
        bb = S.sb("fbias", [128, 512], F32)
        S.dma("sp", bb.ap[:], C.hy_bias[:].partition_broadcast(128), writes=[bb])
        S.op("dve", lambda e: e.tensor_scalar(out=bb.ap[:], in0=bb.ap[:], scalar1=2.0 / N, scalar2=None, op0=ALU.mult), reads=[bb], writes=[bb])
    else:
        S.dma("sp", a_in.ap[:, :, 0, :], src.rearrange("(tc p) c -> p tc c", p=128), writes=[a_in])
    fm = mk_ring(S, "ffm", [128, 2, nt, 128], BF16, 2)
    pr = mk_ring(S, "fpr", [128, 512], F32, 2, psum=True)
    pi_ = mk_ring(S, "fpi", [128, 512], F32, 2, psum=True)
    if is_filter:
        ko = mk_ring(S, "fko", [128, 2, 512], F32, 2)
    else:
        ks = mk_ring(S, "fks", [128, 2, 512], F32, 2)
        tt = mk_ring(S, "ftt", [128, 4, 512], F32, 2)
    for fcn in range(nt):
        m = fm.next()
        for cs in range(2):
            S.dma("sp" if cs == 0 else "pool", m.ap[:, cs, :, :], C.fwd[si][fcn, cs, :, :, :], writes=[m])
        a, b = pr.next(), pi_.next()
        for cs, p in ((0, a), (1, b)):
            for tc in range(nt):
                S.op("pe", lambda e: e.matmul(p.ap[:], m.ap[:, cs, tc, :], a_in.ap[:, tc, cs if is_filter else 0, :], start=(tc == 0), stop=(tc == nt - 1)),
                     reads=[m, a_in], writes=[p])
        if is_filter:
            o = ko.next()
            S.op("dve", lambda e: e.scalar_tensor_tensor(out=o.ap[:, 0, :], in0=a.ap[:], scalar=2.0 / N, in1=bb.ap[:], op0=ALU.mult, op1=ALU.add),
                 reads=[a, bb], writes=[o])
            S.op("act", lambda e: e.activation(out=o.ap[:, 1, :], in_=b.ap[:], func=AF.Identity, scale=2.0 / N), reads=[b], writes=[o])
            S.dma("pool", dst[fcn * 128:(fcn + 1) * 128, :, :], o.ap[:], reads=[o])
        else:
            k = ks.next()
            S.dma("sp", k.ap[:], C.kspec[si][fcn * 128:(fcn + 1) * 128, :, :], writes=[k])
            t = tt.next()
            S.op("dve", lambda e: e.tensor_tensor(out=t.ap[:, 0, :], in0=a.ap[:], in1=k.ap[:, 0, :], op=ALU.mult), reads=[a, k], writes=[t])
            S.op("dve", lambda e: e.tensor_tensor(out=t.ap[:, 1, :], in0=b.ap[:], in1=k.ap[:, 1, :], op=ALU.mult), reads=[b, k], writes=[t])
            S.op("dve", lambda e: e.tensor_tensor(out=t.ap[:, 2, :], in0=a.ap[:], in1=k.ap[:, 1, :], op=ALU.mult), reads=[a, k], writes=[t])
            S.op("dve", lambda e: e.tensor_tensor(out=t.ap[:, 3, :], in0=b.ap[:], in1=k.ap[:, 0, :], op=ALU.mult), reads=[b, k], writes=[t])
            S.op("pool", lambda e: e.tensor_tensor(out=dst.ap[:, fcn, 0, :], in0=t.ap[:, 0, :], in1=t.ap[:, 1, :], op=ALU.add), reads=[t], writes=[dst])
            S.op("pool", lambda e: e.tensor_tensor(out=dst.ap[:, fcn, 1, :], in0=t.ap[:, 2, :], in1=t.ap[:, 3, :], op=ALU.subtract), reads=[t], writes=[dst])
    if is_filter:
        S.end_phase()


def l0_inproj(S, C, bi, src_stage):
    S.begin_phase()
    ident = C.ident
    hT = S.sb("ihT", [128, 8, T + 4], BF16)
    S.op("pool", lambda e: e.memset(hT.ap[:], 0.0), writes=[hT])
    xin = mk_ring(S, "ixin", [128, 1024], F32, 2)
    ptr = mk_ring(S, "iptr", [128, 512], F32, 2, psum=True)
    for i in range(NTT):
        xi = xin.next()
        S.dma("sp", xi.ap[:], tile_src(C, src_stage, bi, i), writes=[xi])
        transpose_mod(S, C, xi, hT, colof(i), 0, 0, (C.R - 1 if i < 2 else bi), ptr, ident)
    S.begin_sub()
    wt = S.sb("iwt", [128, 3, 8, 1024], BF16)
    wv = S.sb("iwv", [128, 8, 128], BF16)
    for j in range(3):
        for kc in range(8):
            S.dma("sp" if kc % 2 else "pool", wt.ap[:, j, kc, :], C.whb[j][kc * 128:(kc + 1) * 128, 512:1536], writes=[wt])
    for kc in range(8):
        S.dma("sp", wv.ap[:, kc, :], C.wqb[kc * 128:(kc + 1) * 128, 640:768], writes=[wv])
    pa = mk_ring(S, "ipa", [128, 512], F32, 2, psum=True)
    pb = mk_ring(S, "ipb", [128, 512], F32, 2, psum=True)
    x1s = mk_ring(S, "ix1", [128, 512], F32, 2)
    ut = mk_ring(S, "iut", [128, 512], BF16, 2)
    vt = mk_ring(S, "ivt", [128, 128], BF16, 2)
    for i in range(NTT):
        c0 = colof(i)
        a, b = pa.next(), pb.next()
        for half, p in ((0, a), (1, b)):
            n = 0
            for j in range(3):
                for kc in range(8):
                    S.op("pe", lambda e: e.matmul(p.ap[:], hT.ap[:, kc, c0 + j - 1:c0 + j - 1 + 128], wt.ap[:, j, kc, half * 512:(half + 1) * 512],
                                                  start=(n == 0), stop=(n == 23)), reads=[hT, wt], writes=[p])
                    n += 1
        x1 = x1s.next()
        S.op("act", lambda e: e.activation(out=x1.ap[:], in_=a.ap[:], func=AF.Identity), reads=[a], writes=[x1])
        u = ut.next()
        S.op("dve", lambda e: e.tensor_tensor(out=u.ap[:], in0=b.ap[:], in1=x1.ap[:], op=ALU.mult), reads=[b, x1], writes=[u])
        S.dma("pool", C.u[bi, i * 128:(i + 1) * 128, :], u.ap[:], reads=[u])
        p = pa.next()
        for kc in range(8):
            S.op("pe", lambda e: e.matmul(p.ap[:, :128], hT.ap[:, kc, c0:c0 + 128], wv.ap[:, kc, :], start=(kc == 0), stop=(kc == 7)),
                 reads=[hT, wv], writes=[p])
        v = vt.next()
        S.op("act", lambda e: e.activation(out=v.ap[:], in_=p.ap[:, :128], func=AF.Identity), reads=[p], writes=[v])
        S.dma("pool", C.v[bi, i * 128:(i + 1) * 128, :], v.ap[:], reads=[v])
    S.end_sub()
    S.begin_sub()
    w0 = S.sb("iw0", [128, 3, 8, 512], BF16)
    wq = S.sb("iwq", [128, 8, 1280], BF16)
    rope = S.sb("irope", [64, 2, LAT], F32)
    S.dma("sp", rope.ap[:], C.rope[:, :, :], writes=[rope])
    for j in range(3):
        for kc in range(8):
            S.dma("sp" if kc % 2 else "pool", w0.ap[:, j, kc, :], C.whb[j][kc * 128:(kc + 1) * 128, 0:512], writes=[w0])
    for kc in range(8):
        S.dma("sp", wq.ap[:, kc, 0:640], C.wqb[kc * 128:(kc + 1) * 128, 0:640], writes=[wq])
        S.dma("pool", wq.ap[:, kc, 640:1280], C.wqb[kc * 128:(kc + 1) * 128, 768:1408], writes=[wq])
    pa = mk_ring(S, "jpa", [128, 512], F32, 3, psum=True)
    ot = mk_ring(S, "jot", [128, 512], BF16, 3)
    t1 = mk_ring(S, "jt1", [64, 512], F32, 2)
    t2 = mk_ring(S, "jt2", [64, 512], F32, 2)
    tiles = [(0, 256)] + [(256 + 512 * k, 512) for k in range(4)]
    for (tok0, w) in tiles:
        c0 = colof(tok0 // 128)
        for cc in range(4):
            p = pa.next()
            n = 0
            for j in range(3):
                for kc in range(8):
                    S.op("pe", lambda e: e.matmul(p.ap[:, :w], w0.ap[:, j, kc, cc * 128:(cc + 1) * 128], hT.ap[:, kc, c0 + j - 1:c0 + j - 1 + w],
                                                  start=(n == 0), stop=(n == 23)), reads=[w0, hT], writes=[p])
                    n += 1
            o = ot.next()
            S.op("act", lambda e: e.activation(out=o.ap[:, :w], in_=p.ap[:, :w], func=AF.Identity), reads=[p], writes=[o])
            S.dma("pool", C.x0T[bi, cc * 128:(cc + 1) * 128, tok0:tok0 + w], o.ap[:, :w], reads=[o])
        for hh in range(10):
            p = pa.next()
            for kc in range(8):
                S.op("pe", lambda e: e.matmul(p.ap[:64, :w], wq.ap[:, kc, hh * 64:(hh + 1) * 64], hT.ap[:, kc, c0:c0 + w], start=(kc == 0), stop=(kc == 7)),
                     reads=[wq, hT], writes=[p])
            o = ot.next()
            dst = C.qT[bi, hh, :, tok0:tok0 + w] if hh < 8 else C.kT[bi, hh - 8, :, tok0:tok0 + w]
            if tok0 == 0:
                S.op("act", lambda e: e.activation(out=o.ap[:64, :w], in_=p.ap[:64, :w], func=AF.Identity), reads=[p], writes=[o])
            else:
                p2 = pa.next()
                for kc in range(8):
                    S.op("pe", lambda e: e.matmul(p2.ap[:64, :w], wq.ap[:, kc, 640 + hh * 64:640 + (hh + 1) * 64], hT.ap[:, kc, c0:c0 + w],
                                                  start=(kc == 0), stop=(kc == 7)), reads=[wq, hT], writes=[p2])
                l0_ = tok0 - 256
                a, b = t1.next(), t2.next()
                S.op("dve", lambda e: e.tensor_tensor(out=a.ap[:, :w], in0=p.ap[:64, :w], in1=rope.ap[:, 0, l0_:l0_ + w], op=ALU.mult), reads=[p, rope], writes=[a])
                S.op("dve", lambda e: e.tensor_tensor(out=b.ap[:, :w], in0=p2.ap[:64, :w], in1=rope.ap[:, 1, l0_:l0_ + w], op=ALU.mult), reads=[p2, rope], writes=[b])
                S.op("pool", lambda e: e.tensor_tensor(out=o.ap[:64, :w], in0=a.ap[:, :w], in1=b.ap[:, :w], op=ALU.add), reads=[a, b], writes=[o])
            S.dma("sp", dst, o.ap[:64, :w], reads=[o])
    S.end_sub()
    S.end_phase()


def l0_hyena(S, C, bi):
    for si, (Lf, tok0) in enumerate(((LAT, 256), (CTX, 0))):
        S.begin_phase()
        nt = Lf // 128
        Y = S.sb("hyY", [128, nt, 2, 512], BF16)
        S.begin_sub()
        hyena_fwd(S, C, si, Lf, C.u[bi, tok0:tok0 + Lf, :], bi, Y, is_filter=False)
        S.end_sub()
        tw = min(512, Lf)
        iv = mk_ring(S, "hyiv", [128, nt, 2, tw], BF16, 2 if Lf == CTX else 1)
        x0 = mk_ring(S, "hyx0", [128, tw], BF16, 8)
        x0_pend = {}

        def x0_load(tt_):
            for cc_ in range(4):
                xz_ = x0.next()
                t0__ = tok0 + tt_ * tw
                S.dma("sp", xz_.ap[:], C.x0T[bi, cc_ * 128:(cc_ + 1) * 128, t0__:t0__ + tw], writes=[xz_])
                x0_pend[(tt_, cc_)] = xz_

        x0_load(0)
        ya = mk_ring(S, "hyya", [128, tw], BF16, 2)
        pp = mk_ring(S, "hypp", [128, 512], F32, 2, psum=True)
        for tt in range(Lf // tw):
            m = iv.next()
            for fc in range(nt):
                S.dma("sp" if fc % 2 else "pool", m.ap[:, fc, :, :], C.inv[si][tt, :, fc, :, :], writes=[m])
            if tt + 1 < Lf // tw:
                x0_load(tt + 1)
            for cc in range(4):
                p = pp.next()
                n = 0
                for fc in range(nt):
                    for cs in range(2):
                        S.op("pe", lambda e: e.matmul(p.ap[:, :tw], Y.ap[:, fc, cs, cc * 128:(cc + 1) * 128], m.ap[:, fc, cs, :],
                                                      start=(n == 0), stop=(n == 2 * nt - 1)), reads=[Y, m], writes=[p])
                        n += 1
                xz = x0_pend.pop((tt, cc))
                t0_ = tok0 + tt * tw
                o = ya.next()
                S.op("dve", lambda e: e.tensor_tensor(out=o.ap[:], in0=p.ap[:, :tw], in1=xz.ap[:], op=ALU.mult), reads=[p, xz], writes=[o])
                S.dma("pool", C.yaT[bi, cc * 128:(cc + 1) * 128, t0_:t0_ + tw], o.ap[:], reads=[o])
        S.end_phase()


def l0_attn(S, C, bi):
    S.begin_phase()
    qT = S.sb("aqT", [64, 8, T], BF16)
    kT = S.sb("akT", [64, 2, T], BF16)
    v = S.sb("av", [128, NTT, 128], BF16)
    yb = S.sb("ayb", [64, 8, T], BF16)
    mask = S.sb("amask", [128, 384], F32)
    sink = S.sb("asink", [128, 8], F32)
    idb = S.sb("aidb", [128, 128], BF16)
    for h in range(8):
        S.dma("sp" if h % 2 else "pool", qT.ap[:, h, :], C.qT[bi, h, :, :], writes=[qT])
    for h in range(2):
        S.dma("sp", kT.ap[:, h, :], C.kT[bi, h, :, :], writes=[kT])
    S.dma("sp", v.ap[:], C.v[bi].rearrange("(i p) c -> p i c", p=128), writes=[v])
    S.dma("sp", mask.ap[:], C.amask[:, :], writes=[mask])
    S.dma("sp", sink.ap[:], C.attn_sink[:].partition_broadcast(128), writes=[sink])
    S.dma("sp", idb.ap[:], C.ident_b[:, :], writes=[idb])
    psl = mk_ring(S, "apsl", [128, 512], F32, 2, psum=True)
    psc = mk_ring(S, "apsc", [128, 512], F32, 2, psum=True)
    ppt = mk_ring(S, "appt", [128, 5, 128], BF16, 2, psum=True)
    ppv = mk_ring(S, "appv", [128, 128], F32, 2, psum=True)
    sc = mk_ring(S, "asc", [128, 640], F32, 2)
    pe_ = mk_ring(S, "ape", [128, 640], F32, 2)
    pn = mk_ring(S, "apn", [128, 640], BF16, 2)
    pts = mk_ring(S, "apts", [128, 5, 128], BF16, 2)
    sm = mk_ring(S, "asm", [128, 8], F32, 4)
    def head_chain(qb, hh, n, lo, hi, m0, ktiles, nk, q0):
            h = hh // 4
            s_ = sc.next()
            if n:
                a = psl.next()
                S.op("pe", lambda e: e.matmul(a.ap[:, :n], qT.ap[:, hh, q0:q0 + 128], kT.ap[:, h, 256 + lo:256 + hi], start=True, stop=True),
                     reads=[qT, kT], writes=[a])
                S.op("dve", lambda e: e.tensor_tensor(out=s_.ap[:, :n], in0=a.ap[:, :n], in1=mask.ap[:, m0:m0 + n], op=ALU.add), reads=[a, mask], writes=[s_])
            b = psc.next()
            S.op("pe", lambda e: e.matmul(b.ap[:, :256], qT.ap[:, hh, q0:q0 + 128], kT.ap[:, h, 0:256], start=True, stop=True), reads=[qT, kT], writes=[b])
            S.op("act", lambda e: e.activation(out=s_.ap[:, n:nk], in_=b.ap[:, :256], func=AF.Identity), reads=[b], writes=[s_])
            yield
            w = sm.next()
            S.op("dve", lambda e: e.tensor_reduce(out=w.ap[:, 0:1], in_=s_.ap[:, :nk], axis=AX.X, op=ALU.max), reads=[s_], writes=[w])
            S.op("dve", lambda e: e.tensor_scalar(out=w.ap[:, 1:2], in0=w.ap[:, 0:1], scalar1=0.125, scalar2=sink.ap[:, hh:hh + 1], op0=ALU.mult, op1=ALU.max),
                 reads=[w, sink], writes=[w])
            S.op("dve", lambda e: e.tensor_scalar(out=w.ap[:, 2:3], in0=w.ap[:, 1:2], scalar1=-1.0, scalar2=None, op0=ALU.mult), reads=[w], writes=[w])
            p_ = pe_.next()
            S.op("act", lambda e: e.activation(out=p_.ap[:, :nk], in_=s_.ap[:, :nk], func=AF.Exp, scale=0.125, bias=w.ap[:, 2:3]),
                 reads=[s_, w], writes=[p_])
            yield
            S.op("dve", lambda e: e.tensor_reduce(out=w.ap[:, 3:4], in_=p_.ap[:, :nk], axis=AX.X, op=ALU.add), reads=[p_], writes=[w])
            S.op("act", lambda e: e.activation(out=w.ap[:, 4:5], in_=w.ap[:, 2:3], func=AF.Exp, bias=sink.ap[:, hh:hh + 1], scale=1.0), reads=[w, sink], writes=[w])
            S.op("dve", lambda e: e.tensor_tensor(out=w.ap[:, 5:6], in0=w.ap[:, 3:4], in1=w.ap[:, 4:5], op=ALU.add), reads=[w], writes=[w])
            S.op("dve", lambda e: e.reciprocal(out=w.ap[:, 6:7], in_=w.ap[:, 5:6]), reads=[w], writes=[w])
            pn_ = pn.next()
            S.op("act", lambda e: e.activation(out=pn_.ap[:, :nk], in_=p_.ap[:, :nk], func=AF.Identity, scale=w.ap[:, 6:7]), reads=[p_, w], writes=[pn_])
            yield
            pt = ppt.next()
            nch = nk // 128
            for j in range(nch):
                S.op("pe", lambda e: e.transpose(out=pt.ap[:, j, :], in_=pn_.ap[:, j * 128:(j + 1) * 128], identity=idb.ap[:]), reads=[pn_, idb], writes=[pt])
            ps_ = pts.next()
            S.op("act", lambda e: e.activation(out=ps_.ap[:, :nch, :], in_=pt.ap[:, :nch, :], func=AF.Identity), reads=[pt], writes=[ps_])
            yield
            o = ppv.next()
            for j in range(nch):
                S.op("pe", lambda e: e.matmul(o.ap[:64, :], v.ap[:, ktiles[j], h * 64:(h + 1) * 64], ps_.ap[:, j, :], start=(j == 0), stop=(j == nch - 1)),
                     reads=[v, ps_], writes=[o])
            S.op("dve", lambda e: e.tensor_copy(out=yb.ap[:, hh, q0:q0 + 128], in_=o.ap[:64, :]), reads=[o], writes=[yb])
            yield

    for qb in range(NTT):
        is_ctx = qb < 2
        q0 = qb * 128
        lo = hi = m0 = 0
        if is_ctx:
            n = 0
            ktiles = []
        else:
            lq = q0 - 256
            lo, hi = max(0, lq - 128), min(LAT, lq + 256)
            n = hi - lo
            m0 = lo - (lq - 128)
            ktiles = [2 + lo // 128 + j for j in range(n // 128)]
        ktiles = ktiles + [0, 1]
        nk = n + 256
        for h0 in range(0, 8, 2):
            interleave([head_chain(qb, hh, n, lo, hi, m0, ktiles, nk, q0) for hh in (h0, h0 + 1)])
    for h in range(8):
        S.dma("sp" if h % 2 else "pool", C.ybT[bi, h, :, :], yb.ap[:, h, :], reads=[yb])
    S.end_phase()


def l0_outproj(S, C, bi, src_stage, dst_stage):
    S.begin_phase()
    ya = S.sb("oya", [128, 4, T], BF16)
    yb = S.sb("oyb", [64, 8, T], BF16)
    wa = S.sb("owa", [128, 4, D], BF16)
    wb = S.sb("owb", [64, 8, D], BF16)
    for c in range(4):
        S.dma("sp", ya.ap[:, c, :], C.yaT[bi, c * 128:(c + 1) * 128, :], writes=[ya])
        S.dma("pool", wa.ap[:, c, :], C.wob0[c * 128:(c + 1) * 128, :], writes=[wa])
    for h in range(8):
        S.dma("sp", yb.ap[:, h, :], C.ybT[bi, h, :, :], writes=[yb])
        S.dma("pool", wb.ap[:, h, :], C.wob0[512 + h * 64:512 + (h + 1) * 64, :], writes=[wb])
    epi = Epi(S, C, "o")
    epi.load(0, 0, bi)
    xin = mk_ring(S, "oxin", [128, 1024], F32, 3)
    py = mk_ring(S, "opy", [128, 1024], F32, 2, psum=True)
    for i in range(NTT):
        xi = xin.next()
        S.dma("sp", xi.ap[:], tile_src(C, src_stage, bi, i), writes=[xi])
        p = py.next()
        for hf in range(2):
            for c in range(4):
                S.op("pe", lambda e: e.matmul(p.ap[:, hf * 512:(hf + 1) * 512], ya.ap[:, c, i * 128:(i + 1) * 128], wa.ap[:, c, hf * 512:(hf + 1) * 512],
                                              start=(c == 0), stop=False), reads=[ya, wa], writes=[p])
            for h in range(8):
                S.op("pe", lambda e: e.matmul(p.ap[:, hf * 512:(hf + 1) * 512], yb.ap[:, h, i * 128:(i + 1) * 128], wb.ap[:, h, hf * 512:(hf + 1) * 512],
                                              start=False, stop=(h == 7)), reads=[yb, wb], writes=[p])
        epi.run(p, xi, i < 2, C.xs[dst_stage - 1][bi, i * 128:(i + 1) * 128, :])
    S.end_phase()


def V(b, *idx):
    return (b, b.ap[idx] if idx else b.ap[:])


def TT(S, eng, o, a, b, op):
    return S.op(eng, lambda e: e.tensor_tensor(out=o[1], in0=a[1], in1=b[1], op=op), reads=[a[0], b[0]], writes=[o[0]])


def TS(S, eng, o, a, s1, s2, op0, op1=None, extra=()):
    if op1 is None:
        return S.op(eng, lambda e: e.tensor_scalar(out=o[1], in0=a[1], scalar1=s1, scalar2=None, op0=op0), reads=[a[0], *extra], writes=[o[0]])
    return S.op(eng, lambda e: e.tensor_scalar(out=o[1], in0=a[1], scalar1=s1, scalar2=s2, op0=op0, op1=op1), reads=[a[0], *extra], writes=[o[0]])


def STT(S, o, a, sc, b, op0, op1, extra=()):
    return S.op("dve", lambda e: e.scalar_tensor_tensor(out=o[1], in0=a[1], scalar=sc, in1=b[1], op0=op0, op1=op1), reads=[a[0], b[0], *extra], writes=[o[0]])


def ACT(S, o, a, func, scale=1.0, bias=None, extra=()):
    if bias is None:
        return S.op("act", lambda e: e.activation(out=o[1], in_=a[1], func=func, scale=scale), reads=[a[0], *extra], writes=[o[0]])
    return S.op("act", lambda e: e.activation(out=o[1], in_=a[1], func=func, scale=scale, bias=bias), reads=[a[0], *extra], writes=[o[0]])


def MM(S, o, l, r, start=True, stop=True):
    return S.op("pe", lambda e: e.matmul(o[1], l[1], r[1], start=start, stop=stop), reads=[l[0], r[0]], writes=[o[0]])


def TR(S, o, a, ident):
    return S.op("pe", lambda e: e.transpose(out=o[1], in_=a[1], identity=ident[1]), reads=[a[0], ident[0]], writes=[o[0]])


RW_E = float(np.exp(-0.5))
INV_DT = BF16


def declare_l1(nc, C, dbg=None):
    NB = C.NB
    I = lambda n, s, dt=F32: dram(nc, n, s, dt, "ExternalInput")
    Sx = lambda n, s, dt=BF16: dram(nc, n, s, dt, "ExternalOutput" if (dbg and n in dbg) else None)
    C.o_w_rw = I("o_w_rw", [D, 1920])
    C.o_w_dn = I("o_w_dn", [D, 1536])
    C.o_w_z = I("o_w_z", [D, 528])
    C.o_w_out = I("o_w_out", [D, D])
    C.rw_mu = I("rw_mu", [1920])
    C.dn_conv = I("dn_conv", [3, 1536])
    C.rw_rows = I("rw_rows", [10, 512])
    C.rw_w2 = I("rw_w2", [128, 512])
    C.rw_a2 = I("rw_a2", [128, 512])
    C.rw_g2 = I("rw_g2", [128, 512])
    C.dn_rows = I("dn_rows", [3, 8])
    C.dn_ng = I("dn_ng", [512])
    C.tri = I("tri", [2, 128, 128])
    C.tris = I("tris", [2, 128, 128])
    C.blkm = I("blkm", [4, 128, 128])
    C.tsw = Sx("tsw", [3, 1920], F32)
    C.wrb = [Sx(f"wrb{j}", [D, 1920]) for j in range(3)]
    C.wdb = [Sx(f"wdb{j}", [D, 1536]) for j in range(3)]
    C.wzb = Sx("wzb", [D, 528])
    C.wob1 = Sx("wob1", [D, D])
    C.rw_ops = Sx("rw_ops", [NB, 2, 6, T, 512])
    C.rw_v = Sx("rw_v", [NB, T, 512])
    C.rw_gc = Sx("rw_gc", [NB, 2, NTT, 64, 8], F32)
    C.rw_g = Sx("rw_g", [NB, T, 512], F32)
    C.rw_bonus = Sx("rw_bonus", [NB, T, 512], F32)
    C.y_rw = Sx("y_rw", [NB, 2, T, 512], F32)
    C.dn_qk = Sx("dn_qk", [NB, 2, T, 512])
    C.dn_ops = Sx("dn_ops", [NB, 2, 5, T, 512])
    C.dn_G = Sx("dn_G", [NB, 2, 3, T, 4], F32)
    C.dn_z = Sx("dn_z", [NB, T, 512], F32)
    C.y_dn = Sx("y_dn", [NB, 2, T, 512], F32)


def host_l1(inputs, m):
    f = lambda a: np.ascontiguousarray(np.asarray(a, dtype=np.float32))
    w = np.asarray(inputs["o_w_in"])[0]
    m["o_w_rw"] = f(w[:, :1920])
    m["o_w_dn"] = f(w[:, 1920:1920 + 1536])
    m["o_w_z"] = f(w[:, 1920 + 1536:])
    m["o_w_out"] = f(np.asarray(inputs["o_w_out"])[0])
    m["rw_mu"] = f(np.asarray(inputs["rw_mu"])[0])
    m["dn_conv"] = f(np.asarray(inputs["dn_conv"])[0])
    g = lambda k: np.asarray(inputs[k])[0]
    m["rw_rows"] = f(np.stack([g("rw_w0")[0], g("rw_w0")[1], g("rw_a0")[0], g("rw_a0")[1], g("rw_kk"), g("rw_ka"),
                               g("rw_rk").reshape(512), g("rw_lnx_g"), g("rw_lnx_b"), np.zeros(512, np.float32)], 0))
    m["rw_w2"] = f(g("rw_w2").reshape(128, 512))
    m["rw_a2"] = f(g("rw_a2").reshape(128, 512))
    m["rw_g2"] = f(g("rw_g2"))
    m["dn_rows"] = f(np.stack([g("dn_A_log").reshape(8), g("dn_dt_bias").reshape(8), np.zeros(8, np.float32)], 0))
    m["dn_ng"] = f(np.tile(g("dn_norm_g"), 4))
    j = np.arange(128)[:, None]
    t = np.arange(128)[None, :]
    m["tri"] = f(np.stack([(j <= t), (j >= t)], 0))
    m["tris"] = f(np.stack([(j < t), (j > t)], 0))
    bd = lambda n: (j // n == t // n)
    m["blkm"] = f(np.stack([bd(16), bd(32) & ~bd(16), bd(64) & ~bd(32), ~bd(64)], 0))
    return m


def l1_setup(S, C):
    S.begin_phase()
    mu = S.sb("smu", [1, 1920], F32)
    o = S.sb("smo", [1, 3, 1920], F32)
    S.dma("sp", mu.ap[:], C.rw_mu[:].partition_broadcast(1), writes=[mu])
    TS(S, "dve", V(o, slice(None), 0, slice(None)), V(mu), 0.5, None, ALU.mult)
    TS(S, "dve", V(o, slice(None), 1, slice(None)), V(mu), -1.0, 1.0, ALU.mult, ALU.add)
    TS(S, "dve", V(o, slice(None), 2, slice(None)), V(mu), 0.5, None, ALU.mult)
    S.dma("sp", C.tsw.rearrange("(o j) n -> o j n", o=1), o.ap[:], reads=[o])
    S.end_phase()
    for j in range(3):
        prep_weight(S, C.wrb[j], C.o_w_rw, D, 1920, scale=C.tsw[j, :], tag=f"qr{j}")
        prep_weight(S, C.wdb[j], C.o_w_dn, D, 1536, scale=C.dn_conv[j, :], tag=f"qd{j}")
    prep_weight(S, C.wzb, C.o_w_z, D, 528, tag="qz")
    prep_weight(S, C.wob1, C.o_w_out, D, D, tag="qo")


def build_hT(S, C, bi, src_stage, l):
    hT = S.sb("bhT", [128, 8, T + 4], BF16)
    S.op("pool", lambda e: e.memset(hT.ap[:], 0.0), writes=[hT])
    S.begin_sub()
    xin = mk_ring(S, "bxin", [128, 1024], F32, 2)
    ptr = mk_ring(S, "bptr", [128, 512], F32, 2, psum=True)
    for i in range(NTT):
        xi = xin.next()
        S.dma("sp", xi.ap[:], tile_src(C, src_stage, bi, i), writes=[xi])
        transpose_mod(S, C, xi, hT, colof(i), l, 0, (C.R - 1 if i < 2 else bi), ptr, C.ident)
    S.end_sub()
    return hT


def bc3(b, n, w):
    return (b, b.ap[:, 0:n].unsqueeze(2).to_broadcast([128, n, w]))


def r3(b, n, *pre):
    ap = b.ap[pre] if pre else b.ap[:]
    return (b, ap.rearrange("p (h d) -> p h d", h=n))


def proj3(S, C, p, hT, c0, w, wt, col0, ncol):
    n = 0
    for j in range(3):
        for kc in range(8):
            MM(S, (p, p.ap[:, :ncol]), (hT, hT.ap[:, kc, c0 + j - 1:c0 + j - 1 + 128]), (wt, wt.ap[:, j, kc, col0:col0 + ncol]), start=(n == 0), stop=(n == 23))
            n += 1


def l1_feat_rw(S, C, bi, hT):
    S.begin_sub()
    ones = S.sb("fones", [128, 128], F32)
    S.op("pool", lambda e: e.memset(ones.ap[:], 1.0), writes=[ones])
    tri = S.sb("ftri", [128, 2, 128], F32)
    for d in range(2):
        S.dma("sp", tri.ap[:, d, :], C.tri[d, :, :], writes=[tri])
    rows = S.sb("frows", [128, 7, 512], F32)
    for q in range(7):
        S.dma("sp", rows.ap[:, q, :], C.rw_rows[q, :].partition_broadcast(128), writes=[rows])
    lwb = S.sb("flwb", [128, 3, 512], BF16)
    loraT = S.sb("floraT", [128, 3, T], BF16)
    S.begin_sub()
    lw = S.sb("flw", [128, 3, 512], F32)
    for q, src in enumerate((C.rw_w2, C.rw_a2, C.rw_g2)):
        S.dma("sp", lw.ap[:, q, :], src[:, :], writes=[lw])
    S.op("dve", lambda e: e.tensor_copy(out=lwb.ap[:], in_=lw.ap[:]), reads=[lw], writes=[lwb])
    wl = S.sb("fwl", [128, 3, 8, 384], BF16)
    for j in range(3):
        for kc in range(8):
            S.dma("sp" if kc % 2 else "pool", wl.ap[:, j, kc, :], C.wrb[j][kc * 128:(kc + 1) * 128, 1536:1920], writes=[wl])
    pp = mk_ring(S, "fpp", [128, 512], F32, 2, psum=True)
    for (tok0, w) in [(0, 256)] + [(256 + 512 * k, 512) for k in range(4)]:
        c0 = colof(tok0 // 128)
        for q, fn in enumerate((AF.Tanh, AF.Identity, AF.Sigmoid)):
            p = pp.next()
            n = 0
            for j in range(3):
                for kc in range(8):
                    MM(S, (p, p.ap[:, :w]), (wl, wl.ap[:, j, kc, q * 128:(q + 1) * 128]), (hT, hT.ap[:, kc, c0 + j - 1:c0 + j - 1 + w]), start=(n == 0), stop=(n == 23))
                    n += 1
            ACT(S, (loraT, loraT.ap[:, q, tok0:tok0 + w]), (p, p.ap[:, :w]), fn)
    S.end_sub()
    wr = S.sb("fwr", [128, 3, 8, 1536], BF16)
    for j in range(3):
        for kc in range(8):
            S.dma("sp" if kc % 2 else "pool", wr.ap[:, j, kc, :], C.wrb[j][kc * 128:(kc + 1) * 128, 0:1536], writes=[wr])
    prkv = [S.ps(f"fp{n}", [128, 512], F32) for n in "rkv"]
    pqd = [mk_ring(S, f"fpq{d}", [128, 512], F32, 2, psum=True) for d in range(2)]
    F = lambda n: S.sb("f_" + n, [128, 512], F32)
    rs, ks, kkr, sq, kk, ksum = [F(n) for n in "rs ks kkr sq kk ksum".split()]
    Fd = []
    for d in range(2):
        X = Ctx()
        X.zt, X.logw, X.a_, X.Gs, X.eG, X.enG, X.eE, X.kd, X.bd = [F(f"{n}{d}") for n in "zt logw a Gs eG enG eE kd bd".split()]
        X.eP, X.t1 = X.zt, X.Gs
        Fd.append(X)
    tb, bon, gs = sq, Fd[0].zt, Fd[0].Gs
    vb = mk_ring(S, "fvb", [128, 512], BF16, 2)
    ot = mk_ring(S, "fot", [128, 6, 512], BF16, 2)
    sm = mk_ring(S, "fsm", [128, 16], F32, 2)
    gcs = mk_ring(S, "fgcs", [64, 8], F32, 2)
    KK, KA, RK = [(rows, rows.ap[:, q, :]) for q in (4, 5, 6)]

    def dir_chain(d, i, rws):
        X = Fd[d]
        zt, logw, a_, Gs, eG, enG, eP, eE, t1, kd, bd = X.zt, X.logw, X.a_, X.Gs, X.eG, X.enG, X.eP, X.eE, X.t1, X.kd, X.bd
        o = ot.next()
        pq = pqd[d]
        pgc = prkv[d]
        pz, pa_ = pq.next(), pq.next()
        MM(S, V(pz), (loraT, loraT.ap[d * 64:(d + 1) * 64, 0, rws]), (lwb, lwb.ap[d * 64:(d + 1) * 64, 0, :]))
        MM(S, V(pa_), (loraT, loraT.ap[d * 64:(d + 1) * 64, 1, rws]), (lwb, lwb.ap[d * 64:(d + 1) * 64, 1, :]))
        TT(S, "dve", V(zt), V(pz), (rows, rows.ap[:, d, :]), ALU.add)
        ACT(S, V(zt), V(zt), AF.Sigmoid)
        TS(S, "pool", V(logw), V(zt), -RW_E, None, ALU.mult)
        TT(S, "dve", V(a_), V(pa_), (rows, rows.ap[:, 2 + d, :]), ALU.add)
        ACT(S, V(a_), V(a_), AF.Sigmoid)
        yield
        pG, pT = pq.next(), pq.next()
        MM(S, V(pG), (tri, tri.ap[:, d, :]), V(logw))
        MM(S, V(pT), V(ones), V(logw))
        for h in range(8):
            MM(S, (pgc, pgc.ap[:64, h:h + 1]), (logw, logw.ap[:, h * 64:(h + 1) * 64]), (ones, ones.ap[:, 0:1]))
        gc = gcs.next()
        ACT(S, V(gc), (pgc, pgc.ap[:64, 0:8]), AF.Exp)
        S.dma("pool", C.rw_gc[bi, d, i, :, :], gc.ap[:], reads=[gc])
        ACT(S, V(Gs), V(pG), AF.Identity)
        yield
        ACT(S, V(eG), V(Gs), AF.Exp)
        ACT(S, V(enG), V(Gs), AF.Exp, scale=-1.0)
        TT(S, "dve", V(eP), V(Gs), V(logw), ALU.subtract)
        ACT(S, V(eP), V(eP), AF.Exp)
        TT(S, "dve", V(eE), V(pT), V(Gs), ALU.subtract)
        ACT(S, V(eE), V(eE), AF.Exp)
        yield
        STT(S, V(t1), V(a_), -1.0, KA, ALU.add, ALU.mult)
        STT(S, V(kd), V(t1), 1.0, V(ks), ALU.add, ALU.mult)
        TT(S, "pool", V(bd), V(kk), V(a_), ALU.mult)
        yield
        O = lambda q: (o, o.ap[:, q, :])
        TT(S, "dve", O(0), V(rs), V(eG), ALU.mult)
        TT(S, "pool", O(1), V(kd), V(enG), ALU.mult)
        TT(S, "dve", O(2), V(bd), V(enG), ALU.mult)
        TT(S, "pool", O(3), V(kk), V(eP), ALU.mult)
        TT(S, "dve", O(4), V(kd), V(eE), ALU.mult)
        TT(S, "pool", O(5), V(bd), V(eE), ALU.mult)
        S.dma("sp", C.rw_ops[bi, d, :, rws, :].rearrange("q t c -> t q c"), o.ap[:], reads=[o])
        yield

    for i in range(NTT):
        c0 = colof(i)
        rws = slice(i * 128, (i + 1) * 128)
        for n in range(3):
            proj3(S, C, prkv[n], hT, c0, 128, wr, n * 512, 512)
        ACT(S, V(rs), V(prkv[0]), AF.Identity)
        ACT(S, V(ks), V(prkv[1]), AF.Identity)
        v_ = vb.next()
        ACT(S, V(v_), V(prkv[2]), AF.Identity)
        S.dma("pool", C.rw_v[bi, rws, :], v_.ap[:], reads=[v_])
        w = sm.next()
        TT(S, "dve", V(kkr), V(ks), KK, ALU.mult)
        TT(S, "pool", V(sq), V(kkr), V(kkr), ALU.mult)
        S.op("dve", lambda e: e.tensor_reduce(out=w.ap[:, 0:8], in_=r3(sq, 8)[1], axis=AX.X, op=ALU.add), reads=[sq], writes=[w])
        TS(S, "dve", V(w, slice(None), slice(0, 8)), V(w, slice(None), slice(0, 8)), 1e-6, None, ALU.add)
        ACT(S, V(w, slice(None), slice(0, 8)), V(w, slice(None), slice(0, 8)), AF.Sqrt)
        S.op("dve", lambda e: e.reciprocal(out=w.ap[:, 0:8], in_=w.ap[:, 0:8]), reads=[w], writes=[w])
        TT(S, "dve", r3(kk, 8), r3(kkr, 8), bc3(w, 8, 64), ALU.mult)
        interleave([dir_chain(d, i, rws) for d in range(2)])
        TT(S, "pool", V(ksum), V(Fd[0].kd), V(Fd[1].kd), ALU.add)
        TT(S, "dve", V(tb), V(rs), V(ksum), ALU.mult)
        TT(S, "pool", V(tb), V(tb), RK, ALU.mult)
        S.op("dve", lambda e: e.tensor_reduce(out=w.ap[:, 8:16], in_=r3(tb, 8)[1], axis=AX.X, op=ALU.add), reads=[tb], writes=[w])
        TT(S, "dve", r3(bon, 8), r3(v_, 8), (w, w.ap[:, 8:16].unsqueeze(2).to_broadcast([128, 8, 64])), ALU.mult)
        S.dma("pool", C.rw_bonus[bi, rws, :], bon.ap[:], reads=[bon])
        pg = pqd[0].next()
        MM(S, V(pg), (loraT, loraT.ap[:, 2, rws]), (lwb, lwb.ap[:, 2, :]))
        ACT(S, V(gs), V(pg), AF.Identity)
        S.dma("pool", C.rw_g[bi, rws, :], gs.ap[:], reads=[gs])
    S.end_sub()


def l1_feat_dn(S, C, bi, hT):
    S.begin_sub()
    ones = S.sb("gones", [128, 128], F32)
    S.op("pool", lambda e: e.memset(ones.ap[:], 1.0), writes=[ones])
    tri = S.sb("gtri", [128, 2, 128], F32)
    for d in range(2):
        S.dma("sp", tri.ap[:, d, :], C.tri[d, :, :], writes=[tri])
    dr = S.sb("gdr", [128, 2, 8], F32)
    for q in range(2):
        S.dma("sp", dr.ap[:, q, :], C.dn_rows[q, :].partition_broadcast(128), writes=[dr])
    ACT(S, V(dr, slice(None), 0, slice(None)), V(dr, slice(None), 0, slice(None)), AF.Exp)
    TS(S, "dve", V(dr, slice(None), 0, slice(None)), V(dr, slice(None), 0, slice(None)), -1.0, None, ALU.mult)
    wd = S.sb("gwd", [128, 3, 8, 1536], BF16)
    wz = S.sb("gwz", [128, 8, 528], BF16)
    for j in range(3):
        for kc in range(8):
            S.dma("sp" if kc % 2 else "pool", wd.ap[:, j, kc, :], C.wdb[j][kc * 128:(kc + 1) * 128, :], writes=[wd])
    for kc in range(8):
        S.dma("sp", wz.ap[:, kc, :], C.wzb[kc * 128:(kc + 1) * 128, :], writes=[wz])
    pqkv = [S.ps(f"gp{n}", [128, 512], F32) for n in "qkv"]
    pz = S.ps("gpz", [128, 512], F32)
    pgt = S.ps("gpgt", [128, 16], F32)
    pG = mk_ring(S, "gpG", [128, 8], F32, 2, psum=True)
    F = lambda n: S.sb("g_" + n, [128, 512], F32)
    qs, ks, vs, sq, zs = [F(n) for n in "qs ks vs sq zs".split()]
    qk = mk_ring(S, "gqk", [128, 2, 512], BF16, 2)
    ot = mk_ring(S, "got", [128, 5, 512], BF16, 2)
    sm = mk_ring(S, "gsm", [128, 64], F32, 2)
    go = mk_ring(S, "ggo", [128, 3, 4], F32, 2)
    for i in range(NTT):
        c0 = colof(i)
        rws = slice(i * 128, (i + 1) * 128)
        for n in range(3):
            proj3(S, C, pqkv[n], hT, c0, 128, wd, n * 512, 512)
        for kc in range(8):
            MM(S, V(pz), (hT, hT.ap[:, kc, c0:c0 + 128]), (wz, wz.ap[:, kc, 0:512]), start=(kc == 0), stop=(kc == 7))
        for kc in range(8):
            MM(S, V(pgt), (hT, hT.ap[:, kc, c0:c0 + 128]), (wz, wz.ap[:, kc, 512:528]), start=(kc == 0), stop=(kc == 7))
        for src, dst in zip(pqkv + [pz], (qs, ks, vs, zs)):
            ACT(S, V(dst), V(src), AF.Silu)
        S.dma("pool", C.dn_z[bi, rws, :], zs.ap[:], reads=[zs])
        w = sm.next()
        Wc = lambda a, b: (w, w.ap[:, a:b])
        ACT(S, Wc(0, 16), V(pgt), AF.Identity)
        qk_ = qk.next()
        for n, (src, sc) in enumerate(((qs, 128.0 ** -0.5), (ks, 1.0))):
            TT(S, "pool", V(sq), V(src), V(src), ALU.mult)
            S.op("dve", lambda e: e.tensor_reduce(out=w.ap[:, 16 + 4 * n:20 + 4 * n], in_=r3(sq, 4)[1], axis=AX.X, op=ALU.add), reads=[sq], writes=[w])
            TS(S, "dve", Wc(16 + 4 * n, 20 + 4 * n), Wc(16 + 4 * n, 20 + 4 * n), 1e-6, None, ALU.add)
            ACT(S, Wc(16 + 4 * n, 20 + 4 * n), Wc(16 + 4 * n, 20 + 4 * n), AF.Sqrt)
            S.op("dve", lambda e: e.reciprocal(out=w.ap[:, 16 + 4 * n:20 + 4 * n], in_=w.ap[:, 16 + 4 * n:20 + 4 * n]), reads=[w], writes=[w])
            if sc != 1.0:
                TS(S, "dve", Wc(16, 20), Wc(16, 20), sc, None, ALU.mult)
            TT(S, "dve", r3(src, 4), r3(src, 4), (w, w.ap[:, 16 + 4 * n:20 + 4 * n].unsqueeze(2).to_broadcast([128, 4, 128])), ALU.mult)
            S.op("pool", lambda e: e.tensor_copy(out=qk_.ap[:, n, :], in_=src.ap[:]), reads=[src], writes=[qk_])
        S.dma("sp", C.dn_qk[bi, :, rws, :].rearrange("q t c -> t q c"), qk_.ap[:], reads=[qk_])
        for d in range(2):
            o = ot.next()
            g_ = go.next()
            TT(S, "dve", Wc(24, 28), Wc(d * 4, d * 4 + 4), (dr, dr.ap[:, 1, d * 4:d * 4 + 4]), ALU.add)
            ACT(S, Wc(24, 28), Wc(24, 28), AF.Exp)
            ACT(S, Wc(24, 28), Wc(24, 28), AF.Ln, bias=1.0)
            TT(S, "dve", (g_, g_.ap[:, 0, :]), Wc(24, 28), (dr, dr.ap[:, 0, d * 4:d * 4 + 4]), ALU.mult)
            ACT(S, Wc(28, 32), Wc(8 + d * 4, 12 + d * 4), AF.Sigmoid)
            p = pG.next()
            MM(S, (p, p.ap[:, 0:4]), (tri, tri.ap[:, d, :]), (g_, g_.ap[:, 0, :]))
            MM(S, (p, p.ap[:, 4:8]), V(ones), (g_, g_.ap[:, 0, :]))
            ACT(S, (g_, g_.ap[:, 1:3, :]), (p, p.ap[:, 0:8].rearrange("p (a b) -> p a b", a=2)), AF.Identity)
            S.dma("pool", C.dn_G[bi, d, :, rws, :].rearrange("q t c -> t q c"), g_.ap[:], reads=[g_])
            ACT(S, Wc(32, 36), (g_, g_.ap[:, 1, :]), AF.Exp)
            TT(S, "dve", Wc(36, 40), (g_, g_.ap[:, 2, :]), (g_, g_.ap[:, 1, :]), ALU.subtract)
            ACT(S, Wc(36, 40), Wc(36, 40), AF.Exp)
            TT(S, "dve", Wc(40, 44), Wc(28, 32), Wc(32, 36), ALU.mult)
            B4 = lambda a: (w, w.ap[:, a:a + 4].unsqueeze(2).to_broadcast([128, 4, 128]))
            O = lambda q: (o, o.ap[:, q, :].rearrange("p (h d) -> p h d", h=4))
            TT(S, "dve", O(0), r3(qs, 4), B4(32), ALU.mult)
            TT(S, "pool", O(1), r3(ks, 4), B4(28), ALU.mult)
            TT(S, "dve", O(2), r3(ks, 4), B4(40), ALU.mult)
            TT(S, "pool", O(3), r3(ks, 4), B4(36), ALU.mult)
            TT(S, "dve", O(4), r3(vs, 4), B4(28), ALU.mult)
            S.dma("sp", C.dn_ops[bi, d, :, rws, :].rearrange("q t c -> t q c"), o.ap[:], reads=[o])
    S.end_sub()


def inv_group(S, P, PT, K, ps, out, nh=4):
    mk = lambda: K.rW.next()
    pA, pB, pC = ps
    PD, PDT, Z, ZT = mk(), mk(), K.rZ.next(), K.rZ.next()
    TT(S, "dve", V(PD), V(P), V(K.mb[0]), ALU.mult)
    TT(S, "pool", V(PDT), V(PT), V(K.mb[0]), ALU.mult)
    TT(S, "dve", V(Z), V(K.ident4), V(PD), ALU.subtract)
    TT(S, "pool", V(ZT), V(K.ident4), V(PDT), ALU.subtract)
    yield
    cur, curT = PD, PDT
    for lv in range(3):
        for h in range(nh):
            MM(S, (pA, pA.ap[:, h, :]), (curT, curT.ap[:, h, :]), (cur, cur.ap[:, h, :]))
        for h in range(nh):
            MM(S, (pB, pB.ap[:, h, :]), (cur, cur.ap[:, h, :]), (curT, curT.ap[:, h, :]))
        Pn, PTn = mk(), mk()
        ACT(S, V(Pn), V(pA), AF.Identity)
        S.op("dve", lambda e: e.tensor_copy(out=PTn.ap[:], in_=pB.ap[:]), reads=[pB], writes=[PTn])
        yield
        for h in range(nh):
            MM(S, (pC, pC.ap[:, h, :]), (PTn, PTn.ap[:, h, :]), (Z, Z.ap[:, h, :]))
        for h in range(nh):
            MM(S, (pA, pA.ap[:, h, :]), (Pn, Pn.ap[:, h, :]), (ZT, ZT.ap[:, h, :]))
        TT(S, "dve", V(Z), V(Z), V(pC), ALU.add)
        TT(S, "dve", V(ZT), V(ZT), V(pA), ALU.add)
        yield
        cur, curT = Pn, PTn
    for m in range(1, 4):
        last = m == 3
        O, OT, Y = mk(), mk(), mk()
        TT(S, "dve", V(O), V(P), V(K.mb[m]), ALU.mult)
        TT(S, "pool", V(OT), V(PT), V(K.mb[m]), ALU.mult)
        for h in range(nh):
            MM(S, (pA, pA.ap[:, h, :]), (OT, OT.ap[:, h, :]), (Z, Z.ap[:, h, :]))
        ACT(S, V(Y), V(pA), AF.Identity)
        if not last:
            YT = mk()
            for h in range(nh):
                MM(S, (pB, pB.ap[:, h, :]), (Z, Z.ap[:, h, :]), (OT, OT.ap[:, h, :]))
            S.op("dve", lambda e: e.tensor_copy(out=YT.ap[:], in_=pB.ap[:]), reads=[pB], writes=[YT])
        yield
        for h in range(nh):
            MM(S, (pC, pC.ap[:, h, :]), (ZT, ZT.ap[:, h, :]), (Y, Y.ap[:, h, :]))
        if not last:
            for h in range(nh):
                MM(S, (pB, pB.ap[:, h, :]), (Y, Y.ap[:, h, :]), (ZT, ZT.ap[:, h, :]))
        TT(S, "dve", V(Z), V(Z), V(pC), ALU.subtract)
        if not last:
            TT(S, "dve", V(ZT), V(ZT), V(pB), ALU.subtract)
        yield
    out.append(Z)


def interleave(gens):
    gens = list(gens)
    while gens:
        for g in list(gens):
            try:
                next(g)
            except StopIteration:
                gens.remove(g)


def scan_consts(S, C, tag):
    K = Ctx()
    K.idb = S.sb(tag + "idb", [128, 128], BF16)
    S.dma("sp", K.idb.ap[:], C.ident_b[:, :], writes=[K.idb])
    K.ident4 = S.sb(tag + "id4", [128, 4, 128], F32)
    K.mSI = [S.sb(tag + f"mSI{d}", [128, 4, 2, 128], F32) for d in range(2)]
    K.mS = [S.sb(tag + f"mS{d}", [128, 4, 128], F32) for d in range(2)]
    K.mI = [S.sb(tag + f"mI{d}", [128, 4, 128], F32) for d in range(2)]
    for h in range(4):
        S.dma("sp", K.ident4.ap[:, h, :], C.ident_d[:, :], writes=[K.ident4])
        for d in range(2):
            S.dma("sp", K.mSI[d].ap[:, h, 0, :], C.tris[d, :, :], writes=[K.mSI[d]])
            S.dma("pool", K.mSI[d].ap[:, h, 1, :], C.tri[d, :, :], writes=[K.mSI[d]])
            S.dma("sp", K.mS[d].ap[:, h, :], C.tris[d, :, :], writes=[K.mS[d]])
            S.dma("pool", K.mI[d].ap[:, h, :], C.tri[d, :, :], writes=[K.mI[d]])
    K.mb = [S.sb(tag + f"mb{m}", [128, 4, 128], F32) for m in range(4)]
    for m in range(4):
        for h in range(4):
            S.dma("sp" if h % 2 else "pool", K.mb[m].ap[:, h, :], C.blkm[m, :, :], writes=[K.mb[m]])
    return K


def chain_res(S, K, tag):
    R = Ctx()
    R.__dict__.update(K.__dict__)
    R.rP = mk_ring(S, tag + "rP", [128, 4, 128], INV_DT, 1)
    R.rPT = mk_ring(S, tag + "rPT", [128, 4, 128], INV_DT, 1)
    R.rW = mk_ring(S, tag + "rW", [128, 4, 128], INV_DT, 8)
    R.rZ = mk_ring(S, tag + "rZ", [128, 4, 128], INV_DT, 4)
    return R


def tile_order(d):
    return list(range(NTT)) if d == 0 else [1, 0] + list(range(NTT - 1, 1, -1))


def l1_scan_rw(S, C, bi):
    S.begin_phase()
    S.keep_pool = True
    K0 = scan_consts(S, C, "r")
    interleave([rw_chain(S, C, chain_res(S, K0, f"r{d}"), bi, d) for d in range(2)])
    S.keep_pool = False
    S.end_phase()


def rw_chain(S, C, K, bi, d):
    t = f"r{d}"
    X0, X1, X2 = [S.ps(t + f"X{n}", [128, 4, 128], F32) for n in range(3)]
    ptr = S.ps(t + "ptr", [64, 8, 128], BF16)
    ot_r = mk_ring(S, t + "ot", [128, 6, 512], BF16, 2)
    vt_r = mk_ring(S, t + "vt", [128, 512], BF16, 2)
    gc_r = mk_ring(S, t + "gc", [64, 8], F32, 2)
    AR_r = mk_ring(S, t + "AR", [64, 8, 2, 128], BF16, 1)
    KT_r = mk_ring(S, t + "KT", [64, 8, 128], BF16, 1)
    BT_r = mk_ring(S, t + "BT", [64, 8, 128], BF16, 1)
    KN_r = mk_ring(S, t + "KN", [128, 8, 2, 128], BF16, 1)
    NB_r = mk_ring(S, t + "NB", [128, 8, 128], BF16, 1)
    Zb_r = mk_ring(S, t + "Zb", [128, 4, 128], BF16, 1)
    WT_r = mk_ring(S, t + "WT", [64, 8, 128], BF16, 1)
    Xs_r = mk_ring(S, t + "Xs", [128, 4, 64], BF16, 1)
    nU0_r = mk_ring(S, t + "nU0", [128, 8, 64], F32, 1)
    nU_r = mk_ring(S, t + "nU", [128, 8, 64], BF16, 1)
    ys_r = mk_ring(S, t + "ys", [128, 512], F32, 2)
    ST = S.sb(t + "ST", [64, 8, 64], F32)
    STb = S.sb(t + "STb", [64, 8, 64], BF16)
    S.op("dve", lambda e: e.memset(ST.ap[:], 0.0), writes=[ST])
    S.op("dve", lambda e: e.memset(STb.ap[:], 0.0), writes=[STb])
    v8 = lambda p: p.ap[:].rearrange("p a b -> p (a b)").rearrange("p (h v) -> p h v", v=64)
    order = tile_order(d)
    pend = {}

    def load(i):
        rws_ = slice(i * 128, (i + 1) * 128)
        ot, vt, gc = ot_r.next(), vt_r.next(), gc_r.next()
        S.dma("sp", ot.ap[:], C.rw_ops[bi, d, :, rws_, :].rearrange("q t c -> t q c"), writes=[ot])
        S.dma("pool", vt.ap[:], C.rw_v[bi, rws_, :], writes=[vt])
        S.dma("pool", gc.ap[:], C.rw_gc[bi, d, i, :, :], writes=[gc])
        pend[i] = (ot, vt, gc)

    load(order[0])
    for n_, i in enumerate(order):
        rws = slice(i * 128, (i + 1) * 128)
        if n_ + 1 < len(order):
            load(order[n_ + 1])
        ot, vt, gc = pend.pop(i)
        AR, KT, BT = AR_r.next(), KT_r.next(), BT_r.next()
        bview = lambda X: X.ap[:].bitcast(BF16).rearrange("p a (b c) -> p (a b) c", b=2)
        tgt = [(ptr, ptr.ap[:]), (X0, bview(X0)[:64]), (X1, bview(X1)[:64]), (X2, bview(X2)[:64])]
        for n_, (q, dst) in enumerate(((3, (AR, AR.ap[:, :, 0, :])), (0, (AR, AR.ap[:, :, 1, :])), (1, V(KT)), (2, V(BT)))):
            tb_, tv = tgt[n_]
            for h in range(8):
                TR(S, (tb_, tv[:, h, :]), (ot, ot.ap[:, q, h * 64:(h + 1) * 64]), V(K.idb))
        for n_, (q, dst) in enumerate(((3, (AR, AR.ap[:, :, 0, :])), (0, (AR, AR.ap[:, :, 1, :])), (1, V(KT)), (2, V(BT)))):
            tb_, tv = tgt[n_]
            if n_ % 2 == 0:
                ACT(S, dst, (tb_, tv), AF.Identity)
            else:
                S.op("dve", lambda e: e.tensor_copy(out=dst[1], in_=tv), reads=[tb_], writes=[dst[0]])
        yield
        KN, NBm, WT, nU0 = KN_r.next(), NB_r.next(), WT_r.next(), nU0_r.next()
        for g in range(2):
            hs = [(hl, g * 4 + hl) for hl in range(4)]
            P, PT = K.rP.next(), K.rPT.next()
            for hl, h in hs:
                MM(S, (X0, X0.ap[:, hl, :]), (BT, BT.ap[:, h, :]), (AR, AR.ap[:, h, 0, :]))
            for hl, h in hs:
                MM(S, (X1, X1.ap[:, hl, :]), (AR, AR.ap[:, h, 0, :]), (BT, BT.ap[:, h, :]))
            for hl, h in hs:
                MM(S, (X2, X2.ap[:, hl, :]), (BT, BT.ap[:, h, :]), (AR, AR.ap[:, h, 1, :]))
            TT(S, "dve", V(P), V(X0), V(K.mS[d]), ALU.mult)
            TT(S, "dve", V(PT), V(X1), V(K.mS[1 - d]), ALU.mult)
            TT(S, "dve", (NBm, NBm.ap[:, g * 4:(g + 1) * 4, :]), V(X2), V(K.mI[d]), ALU.mult)
            yield
            for hl, h in hs:
                MM(S, (X0, X0.ap[:, hl, :]), (KT, KT.ap[:, h, :]), (AR, AR.ap[:, h, 0, :]))
            for hl, h in hs:
                MM(S, (X1, X1.ap[:, hl, :]), (KT, KT.ap[:, h, :]), (AR, AR.ap[:, h, 1, :]))
            TT(S, "dve", (KN, KN.ap[:, g * 4:(g + 1) * 4, 0, :]), V(X0), V(K.mS[d]), ALU.mult)
            TT(S, "dve", (KN, KN.ap[:, g * 4:(g + 1) * 4, 1, :]), V(X1), V(K.mI[d]), ALU.mult)
            yield
            zo = []
            yield from inv_group(S, P, PT, K, (X0, X1, X2), zo)
            Zb = zo[0]
            for hl, h in hs:
                MM(S, (X0, X0.ap[:, hl, 0:64]), (KN, KN.ap[:, h, 0, :]), (vt, vt.ap[:, h * 64:(h + 1) * 64]))
            Xs = Xs_r.next()
            ACT(S, V(Xs), (X0, X0.ap[:, :, 0:64]), AF.Identity)
            yield
            for hl, h in hs:
                MM(S, (X1, X1.ap[:64, hl, :]), (ot, ot.ap[:, 3, h * 64:(h + 1) * 64]), (Zb, Zb.ap[:, hl, :]))
            ACT(S, (WT, WT.ap[:, g * 4:(g + 1) * 4, :]), (X1, X1.ap[:64, :, :]), AF.Identity)
            for hl, h in hs:
                MM(S, (X2, X2.ap[:, hl, 0:64]), (Zb, Zb.ap[:, hl, :]), (Xs, Xs.ap[:, hl, :]))
            TS(S, "dve", (nU0, nU0.ap[:, g * 4:(g + 1) * 4, :]), (X2, X2.ap[:, :, 0:64]), -1.0, None, ALU.mult)
            yield
        for h in range(8):
            MM(S, (X0, v8(X0)[:, h, :]), (WT, WT.ap[:, h, :]), (STb, STb.ap[:, h, :]))
        nU = nU_r.next()
        TT(S, "dve", V(nU), V(nU0), (X0, v8(X0)), ALU.subtract)
        yield
        for h in range(8):
            MM(S, (X1, v8(X1)[:, h, :]), (AR, AR.ap[:, h, 1, :]), (STb, STb.ap[:, h, :]), start=True, stop=False)
            MM(S, (X1, v8(X1)[:, h, :]), (KN, KN.ap[:, h, 1, :]), (vt, vt.ap[:, h * 64:(h + 1) * 64]), start=False, stop=False)
            MM(S, (X1, v8(X1)[:, h, :]), (NBm, NBm.ap[:, h, :]), (nU, nU.ap[:, h, :]), start=False, stop=True)
        ys = ys_r.next()
        ACT(S, r3(ys, 8), (X1, v8(X1)), AF.Identity)
        S.dma("sp", C.y_rw[bi, d, rws, :], ys.ap[:], reads=[ys])
        for h in range(8):
            MM(S, (X2, v8(X2)[:64, h, :]), (ot, ot.ap[:, 4, h * 64:(h + 1) * 64]), (vt, vt.ap[:, h * 64:(h + 1) * 64]), start=True, stop=False)
            MM(S, (X2, v8(X2)[:64, h, :]), (ot, ot.ap[:, 5, h * 64:(h + 1) * 64]), (nU, nU.ap[:, h, :]), start=False, stop=True)
        TT(S, "dve", V(ST), V(ST), (gc, gc.ap[:, 0:8].unsqueeze(2).to_broadcast([64, 8, 64])), ALU.mult)
        TT(S, "dve", V(ST), V(ST), (X2, v8(X2)[:64, :, :]), ALU.add)
        ACT(S, V(STb), V(ST), AF.Identity)
        yield


def l1_scan_dn(S, C, bi):
    S.begin_phase()
    S.keep_pool = True
    K0 = scan_consts(S, C, "d")
    K0.ones = S.sb("dones", [128, 128], F32)
    S.op("pool", lambda e: e.memset(K0.ones.ap[:], 1.0), writes=[K0.ones])
    K0.tri = S.sb("dtri", [128, 2, 128], F32)
    for d in range(2):
        S.dma("sp", K0.tri.ap[:, d, :], C.tri[d, :, :], writes=[K0.tri])
    interleave([dn_chain(S, C, chain_res(S, K0, f"d{d}"), bi, d) for d in range(2)])
    S.keep_pool = False
    S.end_phase()


def dn_chain(S, C, K, bi, d):
    t = f"d{d}"
    ones, tri = K.ones, K.tri
    X0, X1, X2 = [S.ps(t + f"X{n}", [128, 4, 128], F32) for n in range(3)]
    ptr = S.ps(t + "ptr", [128, 4, 128], BF16)
    ot_r = mk_ring(S, t + "ot", [128, 5, 512], BF16, 2)
    qk_r = mk_ring(S, t + "qk", [128, 2, 512], BF16, 2)
    G_r = mk_ring(S, t + "G", [128, 3, 4], F32, 2)
    FT_r = mk_ring(S, t + "FT", [128, 4, 4, 128], BF16, 2)
    gl_r = mk_ring(S, t + "gl", [128, 4, 128], F32, 1)
    ET_r = mk_ring(S, t + "ET", [128, 4, 128], F32, 1)
    qkT_r = mk_ring(S, t + "qkT", [128, 4, 128], BF16, 2)
    Zb_r = mk_ring(S, t + "Zb", [128, 4, 128], BF16, 2)
    wT_r = mk_ring(S, t + "wT", [128, 4, 128], BF16, 2)
    u0_r = mk_ring(S, t + "u0", [128, 4, 128], F32, 2)
    u_r = mk_ring(S, t + "u", [128, 4, 128], BF16, 2)
    ys_r = mk_ring(S, t + "ys", [128, 512], F32, 2)
    sm_r = mk_ring(S, t + "sm", [128, 8], F32, 2)
    ST = S.sb(t + "ST", [128, 4, 128], F32)
    STb = S.sb(t + "STb", [128, 4, 128], BF16)
    S.op("dve", lambda e: e.memset(ST.ap[:], 0.0), writes=[ST])
    S.op("dve", lambda e: e.memset(STb.ap[:], 0.0), writes=[STb])
    order = tile_order(d)
    pend = {}

    def load(i):
        rws_ = slice(i * 128, (i + 1) * 128)
        ot, qk, G = ot_r.next(), qk_r.next(), G_r.next()
        S.dma("sp", ot.ap[:], C.dn_ops[bi, d, :, rws_, :].rearrange("q t c -> t q c"), writes=[ot])
        S.dma("pool", qk.ap[:], C.dn_qk[bi, :, rws_, :].rearrange("q t c -> t q c"), writes=[qk])
        S.dma("pool", G.ap[:], C.dn_G[bi, d, :, rws_, :].rearrange("q t c -> t q c"), writes=[G])
        pend[i] = (ot, qk, G)

    load(order[0])
    for n_, i in enumerate(order):
        rws = slice(i * 128, (i + 1) * 128)
        if n_ + 1 < len(order):
            load(order[n_ + 1])
        ot, qk, G = pend.pop(i)
        FT = FT_r.next()
        bview = lambda X: X.ap[:].bitcast(BF16).rearrange("p a (b c) -> p (a b) c", b=2)[:, 0:4, :]
        tgt = [(ptr, ptr.ap[:]), (X0, bview(X0)), (X1, bview(X1)), (X2, bview(X2))]
        for q, src in enumerate(((qk, 1), (qk, 0), (ot, 1), (ot, 0))):
            tb_, tv = tgt[q]
            for h in range(4):
                TR(S, (tb_, tv[:, h, :]), (src[0], src[0].ap[:, src[1], h * 128:(h + 1) * 128]), V(K.idb))
        for q in range(4):
            tb_, tv = tgt[q]
            if q % 2:
                ACT(S, (FT, FT.ap[:, q, :, :]), (tb_, tv), AF.Identity)
            else:
                S.op("dve", lambda e: e.tensor_copy(out=FT.ap[:, q, :, :], in_=tv), reads=[tb_], writes=[FT])
        yield
        gl = gl_r.next()
        for h in range(4):
            TS(S, "pool", (gl, gl.ap[:, h, :]), (tri, tri.ap[:, d, :]), G.ap[:, 0, h:h + 1], None, ALU.mult, extra=[G])
        for h in range(4):
            MM(S, (X0, X0.ap[:, h, :]), V(ones), (gl, gl.ap[:, h, :]))
        ET = ET_r.next()
        for h in range(4):
            TS(S, "dve", (ET, ET.ap[:, h, :]), (X0, X0.ap[:, h, :]), G.ap[:, 1, h:h + 1], 0.0, ALU.subtract, ALU.min, extra=[G])
        ACT(S, V(ET), V(ET), AF.Exp)
        yield
        for h in range(4):
            MM(S, (X1, X1.ap[:, h, :]), (FT, FT.ap[:, 0, h, :]), (FT, FT.ap[:, 2, h, :]))
            MM(S, (X2, X2.ap[:, h, :]), (FT, FT.ap[:, 0, h, :]), (FT, FT.ap[:, 1, h, :]))
        P, PT = K.rP.next(), K.rPT.next()
        TT(S, "dve", V(P), V(X1), V(ET), ALU.mult)
        TT(S, "dve", V(P), V(P), V(K.mS[d]), ALU.mult)
        qkT = qkT_r.next()
        TT(S, "pool", V(ET), V(ET), V(K.mI[d]), ALU.mult)
        TT(S, "dve", V(qkT), V(X2), V(ET), ALU.mult)
        yield
        for h in range(4):
            TR(S, (ptr, ptr.ap[:, h, :]), (P, P.ap[:, h, :]), V(K.idb))
        ACT(S, V(PT), V(ptr), AF.Identity)
        yield
        zo = []
        yield from inv_group(S, P, PT, K, (X0, X1, X2), zo)
        Zb = zo[0]
        for h in range(4):
            MM(S, (X0, X0.ap[:, h, :]), (Zb, Zb.ap[:, h, :]), (ot, ot.ap[:, 4, h * 128:(h + 1) * 128]))
            MM(S, (X1, X1.ap[:, h, :]), (ot, ot.ap[:, 2, h * 128:(h + 1) * 128]), (Zb, Zb.ap[:, h, :]))
        u0, wT = u0_r.next(), wT_r.next()
        ACT(S, V(u0), V(X0), AF.Identity)
        S.op("dve", lambda e: e.tensor_copy(out=wT.ap[:], in_=X1.ap[:]), reads=[X1], writes=[wT])
        yield
        for h in range(4):
            MM(S, (X2, X2.ap[:, h, :]), (wT, wT.ap[:, h, :]), (STb, STb.ap[:, h, :]))
        u = u_r.next()
        TT(S, "dve", V(u), V(u0), V(X2), ALU.subtract)
        yield
        for h in range(4):
            MM(S, (X0, X0.ap[:, h, :]), (FT, FT.ap[:, 3, h, :]), (STb, STb.ap[:, h, :]), start=True, stop=False)
            MM(S, (X0, X0.ap[:, h, :]), (qkT, qkT.ap[:, h, :]), (u, u.ap[:, h, :]), start=False, stop=True)
        ys = ys_r.next()
        ACT(S, r3(ys, 4), V(X0), AF.Identity)
        S.dma("sp", C.y_dn[bi, d, rws, :], ys.ap[:], reads=[ys])
        for h in range(4):
            MM(S, (X1, X1.ap[:, h, :]), (ot, ot.ap[:, 3, h * 128:(h + 1) * 128]), (u, u.ap[:, h, :]))
        sm = sm_r.next()
        ACT(S, (sm, sm.ap[:, 0:4]), (G, G.ap[:, 2, :]), AF.Exp)
        TT(S, "dve", V(ST), V(ST), (sm, sm.ap[:, 0:4].unsqueeze(2).to_broadcast([128, 4, 128])), ALU.mult)
        TT(S, "dve", V(ST), V(ST), V(X1), ALU.add)
        ACT(S, V(STb), V(ST), AF.Identity)
        yield


def l1_out(S, C, bi, src_stage, dst_stage, last):
    S.begin_phase()
    wo = S.sb("xwo", [128, 8, D], BF16)
    for c in range(8):
        S.dma("sp" if c % 2 else "pool", wo.ap[:, c, :], C.wob1[c * 128:(c + 1) * 128, :], writes=[wo])
    rows = S.sb("xrows", [128, 3, 512], F32)
    S.dma("sp", rows.ap[:, 0, :], C.rw_rows[7, :].partition_broadcast(128), writes=[rows])
    S.dma("sp", rows.ap[:, 1, :], C.rw_rows[8, :].partition_broadcast(128), writes=[rows])
    S.dma("sp", rows.ap[:, 2, :], C.dn_ng[:].partition_broadcast(128), writes=[rows])
    epi = Epi(S, C, "x")
    epi.load(1, 0, bi)
    xin = mk_ring(S, "xxin", [128, 1024], F32, 2)
    ya = mk_ring(S, "xya", [128, 2, 512], F32, 2)
    yb = mk_ring(S, "xyb", [128, 2, 512], F32, 2)
    ex = mk_ring(S, "xex", [128, 3, 512], F32, 2)
    sq_r = mk_ring(S, "xsq", [128, 512], F32, 2)
    ycat = mk_ring(S, "xyc", [128, 1024], F32, 2)
    yT = mk_ring(S, "xyT", [128, 8, 128], BF16, 2)
    sm = mk_ring(S, "xsm", [128, 32], F32, 2)
    ptr = mk_ring(S, "xptr", [128, 4, 128], F32, 2, psum=True)
    py = mk_ring(S, "xpy", [128, 1024], F32, 2, psum=True)
    def out_chain(i):
        rws = slice(i * 128, (i + 1) * 128)
        xi, a, b, e_, yc, w, sq = xin.next(), ya.next(), yb.next(), ex.next(), ycat.next(), sm.next(), sq_r.next()
        S.dma("sp", xi.ap[:], tile_src(C, src_stage, bi, i), writes=[xi])
        S.dma("sp", a.ap[:], C.y_rw[bi, :, rws, :].rearrange("q t c -> t q c"), writes=[a])
        S.dma("pool", b.ap[:], C.y_dn[bi, :, rws, :].rearrange("q t c -> t q c"), writes=[b])
        S.dma("sp", e_.ap[:, 0, :], C.rw_g[bi, rws, :], writes=[e_])
        S.dma("pool", e_.ap[:, 1, :], C.rw_bonus[bi, rws, :], writes=[e_])
        S.dma("sp", e_.ap[:, 2, :], C.dn_z[bi, rws, :], writes=[e_])
        y = (a, a.ap[:, 0, :])
        TT(S, "dve", y, y, (a, a.ap[:, 1, :]), ALU.add)
        S.op("dve", lambda e: e.tensor_reduce(out=w.ap[:, 0:8], in_=r3(a, 8, slice(None), 0, slice(None))[1], axis=AX.X, op=ALU.add), reads=[a], writes=[w])
        TS(S, "dve", (w, w.ap[:, 0:8]), (w, w.ap[:, 0:8]), 1.0 / 64, None, ALU.mult)
        TT(S, "dve", r3(a, 8, slice(None), 0, slice(None)), r3(a, 8, slice(None), 0, slice(None)), bc3(w, 8, 64), ALU.subtract)
        TT(S, "pool", V(sq), y, y, ALU.mult)
        S.op("dve", lambda e: e.tensor_reduce(out=w.ap[:, 8:16], in_=r3(sq, 8)[1], axis=AX.X, op=ALU.add), reads=[sq], writes=[w])
        TS(S, "dve", (w, w.ap[:, 8:16]), (w, w.ap[:, 8:16]), 1.0 / 64, 64e-5, ALU.mult, ALU.add)
        ACT(S, (w, w.ap[:, 8:16]), (w, w.ap[:, 8:16]), AF.Sqrt)
        S.op("dve", lambda e: e.reciprocal(out=w.ap[:, 8:16], in_=w.ap[:, 8:16]), reads=[w], writes=[w])
        TT(S, "dve", r3(a, 8, slice(None), 0, slice(None)), r3(a, 8, slice(None), 0, slice(None)),
           (w, w.ap[:, 8:16].unsqueeze(2).to_broadcast([128, 8, 64])), ALU.mult)
        TT(S, "pool", y, y, (rows, rows.ap[:, 0, :]), ALU.mult)
        TT(S, "pool", y, y, (rows, rows.ap[:, 1, :]), ALU.add)
        TT(S, "dve", y, y, (e_, e_.ap[:, 1, :]), ALU.add)
        TT(S, "dve", (yc, yc.ap[:, 0:512]), y, (e_, e_.ap[:, 0, :]), ALU.mult)
        yield
        o = (b, b.ap[:, 0, :])
        TT(S, "dve", o, o, (b, b.ap[:, 1, :]), ALU.add)
        TT(S, "pool", V(sq), o, o, ALU.mult)
        S.op("dve", lambda e: e.tensor_reduce(out=w.ap[:, 16:20], in_=r3(sq, 4)[1], axis=AX.X, op=ALU.add), reads=[sq], writes=[w])
        TS(S, "dve", (w, w.ap[:, 16:20]), (w, w.ap[:, 16:20]), 1.0 / 128, 1e-6, ALU.mult, ALU.add)
        ACT(S, (w, w.ap[:, 16:20]), (w, w.ap[:, 16:20]), AF.Sqrt)
        S.op("dve", lambda e: e.reciprocal(out=w.ap[:, 16:20], in_=w.ap[:, 16:20]), reads=[w], writes=[w])
        TT(S, "dve", r3(b, 4, slice(None), 0, slice(None)), r3(b, 4, slice(None), 0, slice(None)),
           (w, w.ap[:, 16:20].unsqueeze(2).to_broadcast([128, 4, 128])), ALU.mult)
        TT(S, "pool", o, o, (rows, rows.ap[:, 2, :]), ALU.mult)
        TT(S, "dve", (yc, yc.ap[:, 512:1024]), o, (e_, e_.ap[:, 2, :]), ALU.mult)
        yield
        yt = yT.next()
        for g in range(2):
            p = ptr.next()
            for j in range(4):
                c = g * 4 + j
                S.op("pe", lambda e: e.transpose(out=p.ap[:, j, :], in_=yc.ap[:, c * 128:(c + 1) * 128], identity=C.ident.ap[:]), reads=[yc, C.ident], writes=[p])
            ACT(S, (yt, yt.ap[:, g * 4:(g + 1) * 4, :]), V(p), AF.Identity)
        yield
        p = py.next()
        for hf in range(2):
            for c in range(8):
                MM(S, (p, p.ap[:, hf * 512:(hf + 1) * 512]), (yt, yt.ap[:, c, :]), (wo, wo.ap[:, c, hf * 512:(hf + 1) * 512]), start=(c == 0), stop=(c == 7))
        if last:
            dst = C.xs[dst_stage - 1][bi, rws, :]
        else:
            dst = C.xs[dst_stage - 1][bi, rws, :]
        epi.run(p, xi, i < 2, dst)
        yield

    tl = list(range(2 if last else 0, NTT))
    for k in range(0, len(tl), 2):
        interleave([out_chain(i) for i in tl[k:k + 2]])
    S.end_phase()


NB_FULL = 4
N_CORES = 8


def build_full(NB, dbg=None):
    nc = bass.Bass("TRN2", target_bir_lowering=False)
    C = declare_common(nc, NB, dbg=dbg)
    declare_l0(nc, C, dbg=dbg)
    declare_l1(nc, C, dbg=dbg)
    S = Sched(nc)
    common_setup(S, C)
    phase_mod(S, C)
    for l in range(2):
        prep_weight(S, C.w1b[l], C.mlp_w1[l], D, DFF, tag=f"pm1{l}")
        prep_weight(S, C.w2b[l], C.mlp_w2[l], DFF, D, tag=f"pm2{l}")
    l0_setup(S, C)
    l1_setup(S, C)
    for bi in range(NB):
        l0_inproj(S, C, bi, 0)
        l0_hyena(S, C, bi)
        l0_attn(S, C, bi)
        l0_outproj(S, C, bi, 0, 1)
        phase_mlp(S, C, 0, bi, 1, 2, False)
        S.begin_phase()
        hT = build_hT(S, C, bi, 2, 1)
        l1_feat_rw(S, C, bi, hT)
        l1_feat_dn(S, C, bi, hT)
        S.end_phase()
        l1_scan_rw(S, C, bi)
        l1_scan_dn(S, C, bi)
        l1_out(S, C, bi, 2, 3, True)
        phase_mlp(S, C, 1, bi, 3, None, True)
    S.finish()
    return nc


def kernel(**inputs):
    NB = NB_FULL
    nc = build_full(NB)
    in_maps = [host_l1(inputs, host_l0(inputs, host_common(inputs, c, NB))) for c in range(N_CORES)]
    res = run_bass_kernel_spmd(nc, in_maps, core_ids=list(range(N_CORES)))
    out = np.concatenate([np.asarray(r["out"], dtype=np.float32) for r in res.results], axis=0)
    return out
```

```python
import numpy as np
from contextlib import ExitStack
import concourse.bass as bass
import concourse.mybir as mybir
from concourse.bass_utils import run_bass_kernel_spmd

F32 = mybir.dt.float32
BF16 = mybir.dt.bfloat16
AF = mybir.ActivationFunctionType
ALU = mybir.AluOpType
AX = mybir.AxisListType

SAME_ENGINE_SYNC = True
NO_SWDGE = True
POOL_TO_DVE = True
EPOCH = 30000


class Buf:
    __slots__ = ("ap", "name", "lw", "rd")

    def __init__(self, ap, name):
        self.ap = ap
        self.name = name
        self.lw = None
        self.rd = []


class Sched:
    def __init__(self, nc, ndma=24):
        self.nc = nc
        self.stack = ExitStack()
        self.engs = {"pe": nc.tensor, "act": nc.scalar, "dve": nc.vector, "pool": nc.gpsimd, "sp": nc.sync}
        self.csem = {}
        self.ccnt = {}
        self.nsem = 0
        for e in ("pe", "act", "dve", "pool"):
            self._new_csem(e)
        self.dq = {}
        for q in ("sp", "pool", "act"):
            self.dq[q] = [[self._sem(f"d_{q}{i}"), 0] for i in range(ndma)]
        self.dqi = {q: 0 for q in self.dq}
        self.waited = {}
        self.phase_stack = None
        self.ninst = 0

    def _sem(self, name):
        self.nsem += 1
        return self.stack.enter_context(self.nc.semaphore(name))

    def _new_csem(self, e):
        self.csem[e] = self._sem(f"c_{e}_{self.nsem}")
        self.ccnt[e] = 0

    def sb(self, name, shape, dtype, persist=False):
        st = self.stack if (persist or self.phase_stack is None) else self.phase_stack
        self.nsem += 0
        self.uid = getattr(self, "uid", 0) + 1
        t = st.enter_context(self.nc.sbuf_tensor(f"{name}_{self.uid}", list(shape), dtype))
        return Buf(t, name)

    def ps(self, name, shape, dtype, persist=False):
        st = self.stack if (persist or self.phase_stack is None) else self.phase_stack
        self.uid = getattr(self, "uid", 0) + 1
        t = st.enter_context(self.nc.psum_tensor(f"{name}_{self.uid}", list(shape), dtype))
        return Buf(t, name)

    def view(self, ap, name="v"):
        return Buf(ap, name)

    def begin_sub(self):
        if not hasattr(self, "sub_stk"):
            self.sub_stk = []
        self.sub_stk.append(self.phase_stack)
        self.phase_stack = ExitStack()

    def end_sub(self):
        self.barrier()
        self.phase_stack.close()
        self.phase_stack = self.sub_stk.pop()

    def begin_phase(self):
        assert self.phase_stack is None
        self.phase_stack = ExitStack()

    def end_phase(self):
        self.barrier()
        self.phase_stack.close()
        self.phase_stack = None

    def _wait(self, F, tok):
        sem, val, eng = tok
        if eng == F == "pe":
            return
        if eng == F and not SAME_ENGINE_SYNC:
            return
        key = (F, id(sem))
        if self.waited.get(key, 0) >= val:
            return
        self.engs[F].wait_ge(sem, val)
        self.waited[key] = val
        self.ninst += 1

    def _deps(self, F, reads, writes):
        for b in reads:
            if b.lw is not None:
                self._wait(F, b.lw)
        for b in writes:
            if b.lw is not None:
                self._wait(F, b.lw)
            for t in b.rd:
                self._wait(F, t)

    def _commit(self, tok, reads, writes):
        for b in reads:
            if tok[2] == "dma":
                b.rd.append(tok)
            else:
                b.rd = [t for t in b.rd if t[2] != tok[2]]
                b.rd.append(tok)
        for b in writes:
            b.lw = tok
            b.rd = []

    def op(self, F, fn, reads=(), writes=()):
        if F == "pool" and POOL_TO_DVE and not getattr(self, "keep_pool", False):
            F = "dve"
        self._deps(F, reads, writes)
        if self.ccnt[F] >= EPOCH:
            self._new_csem(F)
        inst = fn(self.engs[F])
        self.ccnt[F] += 1
        inst.then_inc(self.csem[F], 1)
        tok = (self.csem[F], self.ccnt[F], F)
        self._commit(tok, reads, writes)
        self.ninst += 1
        return tok

    def dma(self, Q, out, in_, reads=(), writes=(), **kw):
        if NO_SWDGE:
            Q = "sp"
        self._deps(Q, reads, writes)
        pool = self.dq[Q]
        i = self.dqi[Q]
        self.dqi[Q] = (i + 1) % len(pool)
        sem, val = pool[i]
        if val > 0:
            self._wait(Q, (sem, val, "dma"))
        inst = self.engs[Q].dma_start(out=out, in_=in_, **kw)
        inst.then_inc(sem, 16)
        pool[i][1] = val + 16
        tok = (sem, val + 16, "dma")
        self._commit(tok, reads, writes)
        self.ninst += 1
        return tok

    def barrier(self, engines=("pe", "act", "dve", "pool", "sp")):
        toks = []
        for e in ("pe", "act", "dve", "pool"):
            if self.ccnt[e] > 0:
                toks.append((self.csem[e], self.ccnt[e], "bar"))
        for q in self.dq:
            for sem, val in self.dq[q]:
                if val > 0:
                    toks.append((sem, val, "dma"))
        for F in engines:
            for t in toks:
                self._wait(F, t)

    def finish(self):
        self.barrier()
        self.stack.close()


D = 1024
LAT = 2048
CTX = 256
T = LAT + CTX
NTT = T // 128
DFF = 4096
ALPHA = (2.0 * 2) ** 0.25
LN_EPS = 1e-5


class Ring:
    def __init__(self, bufs):
        self.bufs = bufs
        self.i = 0

    def next(self):
        b = self.bufs[self.i]
        self.i = (self.i + 1) % len(self.bufs)
        return b


def mk_ring(S, name, shape, dtype, n=2, psum=False):
    f = S.ps if psum else S.sb
    return Ring([f(f"{name}{i}", shape, dtype) for i in range(n)])


class Ctx:
    pass


def prep_weight(S, dst, src, K, N, scale=None, tag="pw"):
    S.begin_phase()
    CB = min(N, 2048)
    rin = mk_ring(S, tag + "i", [128, CB], F32, 3)
    rout = mk_ring(S, tag + "o", [128, CB], BF16, 2)
    sc = S.sb(tag + "s", [128, CB], F32) if scale is not None else None
    for c0 in range(0, N, CB):
        cw = min(CB, N - c0)
        if scale is not None:
            S.dma("sp", sc.ap[:, :cw], scale[c0:c0 + cw].partition_broadcast(128), writes=[sc])
        k0s = list(range(0, K, 128))
        pend = {}

        def load(k0):
            kw = min(128, K - k0)
            a = rin.next()
            S.dma("sp", a.ap[:kw, :cw], src[k0:k0 + kw, c0:c0 + cw], writes=[a])
            pend[k0] = a

        load(k0s[0])
        for n_, k0 in enumerate(k0s):
            kw = min(128, K - k0)
            if n_ + 1 < len(k0s):
                load(k0s[n_ + 1])
            a = pend.pop(k0)
            o = rout.next()
            if scale is not None:
                S.op("dve", lambda e: e.tensor_tensor(out=o.ap[:kw, :cw], in0=a.ap[:kw, :cw], in1=sc.ap[:kw, :cw], op=ALU.mult),
                     reads=[a, sc], writes=[o])
            else:
                S.op("dve", lambda e: e.tensor_copy(out=o.ap[:kw, :cw], in_=a.ap[:kw, :cw]), reads=[a], writes=[o])
            S.dma("pool", dst[k0:k0 + kw, c0:c0 + cw], o.ap[:kw, :cw], reads=[o])
    S.end_phase()


def phase_mod(S, C):
    nc, R = C.nc, C.R
    S.begin_phase()
    cT = S.sb("cT", [128, 8, R], F32)
    S.dma("sp", cT.ap[:], C.cT[:, :, :], writes=[cT])
    sig = S.sb("sig", [128, 8, R], F32)
    scT = S.sb("scT", [128, 8, R], BF16)
    S.op("act", lambda e: e.activation(out=sig.ap[:], in_=cT.ap[:], func=AF.Sigmoid), reads=[cT], writes=[sig])
    S.op("dve", lambda e: e.tensor_tensor(out=scT.ap[:], in0=cT.ap[:], in1=sig.ap[:], op=ALU.mult), reads=[cT, sig], writes=[scT])
    scbc = S.sb("scbc", [128, 8, R, 128], BF16)
    for kc in range(8):
        for r in range(R):
            S.op("dve", lambda e: e.tensor_copy(out=scbc.ap[:, kc, r, :], in_=scT.ap[:, kc, r:r + 1].to_broadcast([128, 128])),
                 reads=[scT], writes=[scbc])
    mb = S.sb("mb", [128, 2, 48], F32)
    S.dma("sp", mb.ap[:], C.mod_bT[:, :, :], writes=[mb])
    wst = mk_ring(S, "mws", [128, 3072], F32, 2)
    wbf = S.sb("mwbf", [128, 8, 6144], BF16)
    pacc = mk_ring(S, "mps", [128, 512], F32, 2, psum=True)
    gt = mk_ring(S, "mgt", [128, 512], F32, 2)
    mbr = S.sb("mbr", [128, 2048], F32)
    for l in range(2):
        for kc in range(8):
            for hf in range(2):
                a = wst.next()
                S.dma("sp" if hf == 0 else "pool", a.ap[:], C.mod_w[l, kc * 128:(kc + 1) * 128, hf * 3072:(hf + 1) * 3072], writes=[a])
                S.op("dve" if hf == 0 else "act",
                     (lambda e: e.tensor_copy(out=wbf.ap[:, kc, hf * 3072:(hf + 1) * 3072], in_=a.ap[:])) if hf == 0 else
                     (lambda e: e.activation(out=wbf.ap[:, kc, hf * 3072:(hf + 1) * 3072], in_=a.ap[:], func=AF.Identity)),
                     reads=[a], writes=[wbf])
        for fc in range(48):
            p = pacc.next()
            for kc in range(8):
                S.op("pe", lambda e: e.matmul(p.ap[:, :R], wbf.ap[:, kc, fc * 128:(fc + 1) * 128], scT.ap[:, kc, :],
                                              start=(kc == 0), stop=(kc == 7)), reads=[wbf, scT], writes=[p])
            is_scale = (fc // 8) in (1, 4)
            S.op("dve", lambda e: e.tensor_scalar(out=C.modT.ap[:, l, fc, :], in0=p.ap[:, :R], scalar1=mb.ap[:, l, fc:fc + 1],
                                                  scalar2=(1.0 if is_scale else 0.0), op0=ALU.add, op1=ALU.add),
                 reads=[p, mb], writes=[C.modT])
        for gi, c0 in enumerate((2048, 5120)):
            S.dma("sp", mbr.ap[:, gi * 1024:(gi + 1) * 1024], C.mod_b[l, c0:c0 + 1024].partition_broadcast(128), writes=[mbr])
        for gi, c0 in enumerate((2048, 5120)):
            for r in range(R):
                for hf in range(2):
                    p = pacc.next()
                    for kc in range(8):
                        S.op("pe", lambda e: e.matmul(p.ap[:], scbc.ap[:, kc, r, :], wbf.ap[:, kc, c0 + hf * 512:c0 + (hf + 1) * 512],
                                                      start=(kc == 0), stop=(kc == 7)), reads=[scbc, wbf], writes=[p])
                    g = gt.next()
                    S.op("dve", lambda e: e.tensor_tensor(out=g.ap[:], in0=p.ap[:], in1=mbr.ap[:, gi * 1024 + hf * 512:gi * 1024 + (hf + 1) * 512], op=ALU.add),
                         reads=[p, mbr], writes=[g])
                    S.dma("pool", C.gbc[l, gi, r, :, hf * 512:(hf + 1) * 512], g.ap[:], reads=[g])
    S.end_phase()


def tile_src(C, stage, bi, i):
    if stage == 0:
        if i < 2:
            return C.ctx[bi, i * 128:(i + 1) * 128, :]
        return C.x[bi, (i - 2) * 128:(i - 1) * 128, :]
    return C.xs[stage - 1][bi, i * 128:(i + 1) * 128, :]


def rstd_op(S, mv, o, i, eps):
    S.op("dve", lambda e: e.tensor_scalar(out=mv.ap[:, o:o + 1], in0=mv.ap[:, i:i + 1], scalar1=eps, scalar2=None, op0=ALU.add), reads=[mv], writes=[mv])
    S.op("act", lambda e: e.activation(out=mv.ap[:, o:o + 1], in_=mv.ap[:, o:o + 1], func=AF.Sqrt), reads=[mv], writes=[mv])
    S.op("dve", lambda e: e.reciprocal(out=mv.ap[:, o:o + 1], in_=mv.ap[:, o:o + 1]), reads=[mv], writes=[mv])


class Epi:
    def __init__(self, S, C, tag):
        self.S, self.C = S, C
        self.gb = [S.sb(tag + "gb0", [128, 1024], F32), S.sb(tag + "gb1", [128, 1024], F32)]
        self.lg = S.sb(tag + "lg", [128, 1024], F32)
        self.lb = S.sb(tag + "lb", [128, 1024], F32)
        self.t1 = mk_ring(S, tag + "t1", [128, 1024], F32, 2)
        self.xo = mk_ring(S, tag + "xo", [128, 1024], F32, 2)
        self.st = mk_ring(S, tag + "st", [128, 2, 6], F32, 2)
        self.mv = mk_ring(S, tag + "mv", [128, 4], F32, 2)

    def load(self, l, sub, bi):
        S, C = self.S, self.C
        S.dma("sp", self.gb[0].ap[:], C.gbc[l, sub, bi, :, :], writes=[self.gb[0]])
        S.dma("sp", self.gb[1].ap[:], C.gbc[l, sub, C.R - 1, :, :], writes=[self.gb[1]])
        S.dma("sp", self.lg.ap[:], C.ln_g[l, sub, :].partition_broadcast(128), writes=[self.lg])
        S.dma("sp", self.lb.ap[:], C.ln_b[l, sub, :].partition_broadcast(128), writes=[self.lb])

    def run(self, y, xin, is_ctx, dst):
        S = self.S
        gb = self.gb[1 if is_ctx else 0]
        t1, xo, st, mv = self.t1.next(), self.xo.next(), self.st.next(), self.mv.next()
        S.op("dve", lambda e: e.tensor_tensor(out=t1.ap[:], in0=y.ap[:], in1=gb.ap[:], op=ALU.mult), reads=[y, gb], writes=[t1])
        S.op("dve", lambda e: e.scalar_tensor_tensor(out=t1.ap[:], in0=xin.ap[:], scalar=ALPHA, in1=t1.ap[:], op0=ALU.mult, op1=ALU.add),
             reads=[xin, t1], writes=[t1])
        for h in range(2):
            S.op("dve", lambda e: e.bn_stats(out=st.ap[:, h, :], in_=t1.ap[:, h * 512:(h + 1) * 512]), reads=[t1], writes=[st])
        S.op("dve", lambda e: e.bn_aggr(out=mv.ap[:, 0:2], in_=st.ap[:]), reads=[st], writes=[mv])
        rstd_op(S, mv, 2, 1, LN_EPS)
        S.op("dve", lambda e: e.tensor_scalar(out=mv.ap[:, 3:4], in0=mv.ap[:, 0:1], scalar1=mv.ap[:, 2:3], scalar2=-1.0, op0=ALU.mult, op1=ALU.mult),
             reads=[mv], writes=[mv])
        S.op("act", lambda e: e.activation(out=xo.ap[:], in_=t1.ap[:], func=AF.Identity, scale=mv.ap[:, 2:3], bias=mv.ap[:, 3:4]),
             reads=[t1, mv], writes=[xo])
        S.op("pool", lambda e: e.tensor_tensor(out=xo.ap[:], in0=xo.ap[:], in1=self.lg.ap[:], op=ALU.mult), reads=[xo, self.lg], writes=[xo])
        S.op("pool", lambda e: e.tensor_tensor(out=xo.ap[:], in0=xo.ap[:], in1=self.lb.ap[:], op=ALU.add), reads=[xo, self.lb], writes=[xo])
        S.dma("pool", dst, xo.ap[:], reads=[xo])


def transpose_mod(S, C, xin, hT, col0, l, fc0, r, ptr, ident):
    for g in range(2):
        p = ptr.next()
        for j in range(4):
            kc = g * 4 + j
            S.op("pe", lambda e: e.transpose(out=p.ap[:, j * 128:(j + 1) * 128], in_=xin.ap[:, kc * 128:(kc + 1) * 128], identity=ident.ap[:]),
                 reads=[xin, ident], writes=[p])
        for j in range(4):
            kc = g * 4 + j
            S.op("act", lambda e: e.activation(out=hT.ap[:, kc, col0:col0 + 128], in_=p.ap[:, j * 128:(j + 1) * 128], func=AF.Identity,
                                               scale=C.modT.ap[:, l, fc0 + 8 + kc, r:r + 1], bias=C.modT.ap[:, l, fc0 + kc, r:r + 1]),
                 reads=[p, C.modT], writes=[hT])


def phase_mlp(S, C, l, bi, src_stage, dst_stage, last):
    S.begin_phase()
    ident = C.ident
    w1 = S.sb("w1", [128, 8, DFF], BF16)
    w2 = S.sb("w2", [128, 32, D], BF16)
    for kc in range(8):
        S.dma("sp" if kc % 2 == 0 else "pool", w1.ap[:, kc, :], C.w1b[l][kc * 128:(kc + 1) * 128, :], writes=[w1])
    for ko in range(32):
        S.dma("sp" if ko % 2 == 0 else "pool", w2.ap[:, ko, :], C.w2b[l][ko * 128:(ko + 1) * 128, :], writes=[w2])
    epi = Epi(S, C, "m")
    epi.load(l, 1, bi)
    xin = mk_ring(S, "mxin", [128, 1024], F32, 4)
    hT = mk_ring(S, "mhT", [128, 8, 256], BF16, 2)
    aT = S.sb("maT", [128, 32, 256], BF16)
    rl = mk_ring(S, "mrl", [128, 256], BF16, 2)
    ptr = mk_ring(S, "mptr", [128, 512], F32, 2, psum=True)
    pup = mk_ring(S, "mpup", [128, 256], F32, 2, psum=True)
    pdn = mk_ring(S, "mpdn", [128, 1024], F32, 2, psum=True)
    t0 = 1 if last else 0
    mlp_pend = {}

    def mlp_load(tt_):
        mlp_pend[tt_] = []
        for s_ in range(2):
            xi_ = xin.next()
            S.dma("sp", xi_.ap[:], tile_src(C, src_stage, bi, tt_ * 2 + s_), writes=[xi_])
            mlp_pend[tt_].append(xi_)

    for tt in range(t0, 9):
        is_ctx = tt == 0
        r = C.R - 1 if is_ctx else bi
        if tt == t0:
            mlp_load(tt)
        if tt + 1 < 9:
            mlp_load(tt + 1)
        xs_ = mlp_pend.pop(tt)
        h = hT.next()
        for s in range(2):
            transpose_mod(S, C, xs_[s], h, s * 128, l, 24, r, ptr, ident)
        for fo in range(32):
            p = pup.next()
            for kc in range(8):
                S.op("pe", lambda e: e.matmul(p.ap[:], w1.ap[:, kc, fo * 128:(fo + 1) * 128], h.ap[:, kc, :], start=(kc == 0), stop=(kc == 7)),
                     reads=[w1, h], writes=[p])
            rr = rl.next()
            S.op("act", lambda e: e.activation(out=rr.ap[:], in_=p.ap[:], func=AF.Relu), reads=[p], writes=[rr])
            S.op("pool", lambda e: e.tensor_tensor(out=aT.ap[:, fo, :], in0=rr.ap[:], in1=rr.ap[:], op=ALU.mult), reads=[rr], writes=[aT])
        for s in range(2):
            p = pdn.next()
            for hf in range(2):
                for ko in range(32):
                    S.op("pe", lambda e: e.matmul(p.ap[:, hf * 512:(hf + 1) * 512], aT.ap[:, ko, s * 128:(s + 1) * 128], w2.ap[:, ko, hf * 512:(hf + 1) * 512],
                                                  start=(ko == 0), stop=(ko == 31)), reads=[aT, w2], writes=[p])
            i = tt * 2 + s
            if last:
                dst = C.out[bi, (i - 2) * 128:(i - 1) * 128, :]
            else:
                dst = C.xs[dst_stage - 1][bi, i * 128:(i + 1) * 128, :]
            epi.run(p, xs_[s], is_ctx, dst)
    S.end_phase()


def dram(nc, name, shape, dtype, kind=None):
    if kind is None:
        return nc.dram_tensor(name, list(shape), dtype).ap()
    return nc.dram_tensor(name, list(shape), dtype, kind=kind).ap()


def declare_common(nc, NB, dbg=None):
    C = Ctx()
    C.nc, C.NB, C.R = nc, NB, NB + 1
    R = C.R
    I = lambda n, s: dram(nc, n, s, F32, "ExternalInput")
    C.x = I("x", [NB, LAT, D])
    C.ctx = I("ctx", [NB, CTX, D])
    C.cT = I("cT", [128, 8, R])
    C.mod_w = I("mod_w", [2, D, 6 * D])
    C.mod_b = I("mod_b", [2, 6 * D])
    C.mod_bT = I("mod_bT", [128, 2, 48])
    C.ln_g = I("ln_g", [2, 2, D])
    C.ln_b = I("ln_b", [2, 2, D])
    C.mlp_w1 = I("mlp_w1", [2, D, DFF])
    C.mlp_w2 = I("mlp_w2", [2, DFF, D])
    C.ident_d = I("ident", [128, 128])
    C.out = dram(nc, "out", [NB, LAT, D], F32, "ExternalOutput")
    C.gbc = dram(nc, "gbc", [2, 2, R, 128, D], F32)
    C.w1b = [dram(nc, f"w1b{l}", [D, DFF], BF16) for l in range(2)]
    C.w2b = [dram(nc, f"w2b{l}", [DFF, D], BF16) for l in range(2)]
    nst = 3
    C.xs = [dram(nc, f"xs{i}", [NB, T, D], F32, "ExternalOutput" if (dbg and f"xs{i}" in dbg) else None) for i in range(nst)]
    return C


def common_setup(S, C):
    C.modT = S.sb("modT", [128, 2, 48, C.R], F32, persist=True)
    C.ident = S.sb("identsb", [128, 128], F32, persist=True)
    S.dma("sp", C.ident.ap[:], C.ident_d[:, :], writes=[C.ident])


def host_common(inputs, core, NB):
    b0 = core * NB
    f = lambda a: np.ascontiguousarray(np.asarray(a, dtype=np.float32))
    cs = np.concatenate([np.asarray(inputs["c"])[b0:b0 + NB], np.asarray(inputs["c_ctx"])[None, :]], 0)
    m = {
        "x": f(np.asarray(inputs["x"])[b0:b0 + NB]),
        "ctx": f(np.asarray(inputs["ctx"])[b0:b0 + NB]),
        "cT": f(cs.reshape(NB + 1, 8, 128).transpose(2, 1, 0)),
        "mod_w": f(inputs["mod_w"]),
        "mod_b": f(inputs["mod_b"]),
        "mod_bT": f(np.asarray(inputs["mod_b"]).reshape(2, 48, 128).transpose(2, 0, 1)),
        "ln_g": f(inputs["ln_g"]),
        "ln_b": f(inputs["ln_b"]),
        "mlp_w1": f(inputs["mlp_w1"]),
        "mlp_w2": f(inputs["mlp_w2"]),
        "ident": np.eye(128, dtype=np.float32),
    }
    return m


HYW = 512
PI = float(np.pi)


def colof(i):
    return 1 + 128 * i if i < 2 else 259 + 128 * (i - 2)


def declare_l0(nc, C, dbg=None):
    NB = C.NB
    I = lambda n, s, dt=F32: dram(nc, n, s, dt, "ExternalInput")
    Sx = lambda n, s, dt=BF16: dram(nc, n, s, dt, "ExternalOutput" if (dbg and n in dbg) else None)
    C.e_w_hy = I("e_w_hy", [D, 1536])
    C.e_w_qkv = I("e_w_qkv", [D, 768 + 640])
    C.e_w_out = I("e_w_out", [D, D])
    C.hy_conv = I("hy_conv", [3, 1536])
    C.hy_w1 = I("hy_w1", [33, 64])
    C.hy_w2 = I("hy_w2", [64, 64])
    C.hy_w3 = I("hy_w3", [64, 1024])
    C.hy_vec = I("hy_vec", [64, 3])
    C.hy_decay = I("hy_decay", [1024])
    C.hy_bias = I("hy_bias", [512])
    C.attn_sink = I("attn_sink", [8])
    C.peT = [I("peT_l", [33, LAT]), I("peT_c", [33, CTX])]
    C.negtn = [I("negtn_l", [128, LAT // 128]), I("negtn_c", [128, CTX // 128])]
    C.fwd = [I("fwd_l", [16, 2, 128, 16, 128], BF16), I("fwd_c", [2, 2, 128, 2, 128], BF16)]
    C.inv = [I("inv_l", [4, 128, 16, 2, 512], BF16), I("inv_c", [1, 128, 2, 2, 256], BF16)]
    C.rope = I("rope", [64, 2, LAT])
    C.amask = I("amask", [128, 384])
    C.ident_b = I("ident_b", [128, 128], BF16)
    C.whb = [Sx(f"whb{j}", [D, 1536]) for j in range(3)]
    C.wqb = Sx("wqb", [D, 1408])
    C.wob0 = Sx("wob0", [D, D])
    C.kspec = [Sx("kspec_l", [LAT, 2, 512], F32), Sx("kspec_c", [CTX, 2, 512], F32)]
    C.filt = [Sx("filt_l", [LAT, 2, 512]), Sx("filt_c", [CTX, 2, 512])]
    C.u = Sx("u_s", [NB, T, 512])
    C.x0T = Sx("x0T_s", [NB, 512, T])
    C.qT = Sx("qT_s", [NB, 8, 64, T])
    C.kT = Sx("kT_s", [NB, 2, 64, T])
    C.v = Sx("v_s", [NB, T, 128])
    C.yaT = Sx("yaT_s", [NB, 512, T])
    C.ybT = Sx("ybT_s", [NB, 8, 64, T])


def host_l0(inputs, m):
    f = lambda a: np.ascontiguousarray(np.asarray(a, dtype=np.float32))
    import ml_dtypes
    bf = lambda a: np.ascontiguousarray(np.asarray(a, dtype=np.float32).astype(ml_dtypes.bfloat16))
    w = np.asarray(inputs["e_w_in"])[0]
    d = np.arange(64)
    partner = np.where((d % 32) < 16, d + 16, d - 16)
    qcols = 1536 + (np.arange(8)[:, None] * 64 + partner[None, :]).reshape(-1)
    kcols = 2048 + (np.arange(2)[:, None] * 64 + partner[None, :]).reshape(-1)
    m["e_w_hy"] = f(w[:, :1536])
    m["e_w_qkv"] = f(np.concatenate([w[:, 1536:2304], w[:, qcols], w[:, kcols]], 1))
    m["e_w_out"] = f(np.asarray(inputs["e_w_out"])[0])
    m["hy_conv"] = f(np.asarray(inputs["hy_conv"])[0])
    m["hy_w1"] = f(np.asarray(inputs["hy_ffn_w1"])[0])
    m["hy_w2"] = f(np.asarray(inputs["hy_ffn_w2"])[0])
    m["hy_w3"] = f(np.asarray(inputs["hy_ffn_w3"])[0])
    m["hy_vec"] = f(np.stack([np.asarray(inputs["hy_ffn_b1"])[0], np.asarray(inputs["hy_ffn_b2"])[0], np.asarray(inputs["hy_sin_freq"])[0]], 1))
    m["hy_decay"] = f(np.asarray(inputs["hy_decay"])[0])
    m["hy_bias"] = f(np.asarray(inputs["hy_bias"])[0])
    m["attn_sink"] = f(np.asarray(inputs["attn_sink"])[0])
    for tag, Lf in (("l", LAT), ("c", CTX)):
        t = np.arange(Lf, dtype=np.float32)
        t_norm = t / np.float32(max(Lf - 1, 1))
        bands = np.linspace(1e-4, 15, 16, dtype=np.float32)
        ang = (2.0 * np.pi * t[:, None] * bands[None, :] / Lf).astype(np.float32)
        pe = np.concatenate([t_norm[:, None], np.cos(ang), -np.sin(ang)], -1).astype(np.float32)
        m["peT_" + tag] = f(pe.T)
        m["negtn_" + tag] = f((-t_norm).reshape(Lf // 128, 128).T)
        N = 2 * Lf
        nt = Lf // 128
        tt = np.arange(Lf, dtype=np.float64)
        ff = np.arange(Lf, dtype=np.float64) + 0.5
        th = 2.0 * np.pi * np.outer(tt, ff) / N
        Cm, Sm = np.cos(th), np.sin(th)
        fw = np.stack([Cm, Sm], 0).reshape(2, nt, 128, nt, 128)
        m["fwd_" + tag] = bf(fw.transpose(3, 0, 2, 1, 4))
        tw = min(512, Lf)
        iv = np.stack([Cm.T, -Sm.T], 0).reshape(2, nt, 128, Lf // tw, tw)
        m["inv_" + tag] = bf(iv.transpose(3, 2, 1, 0, 4))
    pos = np.arange(LAT)
    inv_freq = (10000.0 ** (-np.arange(16, dtype=np.float32) / 16)).astype(np.float32)
    P = np.where(d[:, None] < 32, (pos // 64)[None, :], (pos % 64)[None, :]).astype(np.float32)
    ang = (P * inv_freq[d % 16][:, None]).astype(np.float32)
    sgn = np.where((d % 32) < 16, -1.0, 1.0)[:, None]
    m["rope"] = f(np.stack([np.cos(ang), sgn * np.sin(ang)], 1))
    qi = np.arange(128)[:, None]
    kj = np.arange(384)[None, :] - 128
    m["amask"] = f(np.where(np.abs(qi - kj) <= 128, 0.0, -30000.0))
    m["ident_b"] = bf(np.eye(128))
    return m


def l0_setup(S, C):
    for j in range(3):
        prep_weight(S, C.whb[j], C.e_w_hy, D, 1536, scale=C.hy_conv[j, :], tag=f"ph{j}")
    prep_weight(S, C.wqb, C.e_w_qkv, D, 1408, tag="pq")
    prep_weight(S, C.wob0, C.e_w_out, D, D, tag="po")
    for si, Lf in enumerate((LAT, CTX)):
        hyena_filter(S, C, si, Lf)
        hyena_fwd(S, C, si, Lf, C.filt[si], None, C.kspec[si], is_filter=True)


def hyena_filter(S, C, si, Lf):
    S.begin_phase()
    w1 = S.sb("hw1", [33, 64], F32)
    w2 = S.sb("hw2", [64, 64], F32)
    w3 = S.sb("hw3", [64, 1024], F32)
    vec = S.sb("hvec", [64, 3], F32)
    peT = S.sb("hpe", [33, Lf], F32)
    ntn = S.sb("hntn", [128, Lf // 128], F32)
    dec = S.sb("hdec", [128, 1024], F32)
    for dst, src in ((w1, C.hy_w1), (w2, C.hy_w2), (w3, C.hy_w3), (vec, C.hy_vec), (peT, C.peT[si]), (ntn, C.negtn[si])):
        S.dma("sp", dst.ap[:], src, writes=[dst])
    S.dma("sp", dec.ap[:], C.hy_decay[:].partition_broadcast(128), writes=[dec])
    S.op("dve", lambda e: e.scalar_tensor_tensor(out=dec.ap[:], in0=dec.ap[:], scalar=-1.0, in1=dec.ap[:], op0=ALU.mult, op1=ALU.max), reads=[dec], writes=[dec])
    h1 = S.sb("hh1", [64, Lf], F32)
    h2 = S.sb("hh2", [64, Lf], F32)
    tmp = S.sb("htmp", [64, 512], F32)
    S.sin_ki = S.sb("hki", [64, 512], mybir.dt.int32)
    S.sin_kf = S.sb("hkf", [64, 512], F32)
    pp = mk_ring(S, "hpp", [128, 512], F32, 2, psum=True)
    W = min(512, Lf)
    for c0 in range(0, Lf, W):
        p = pp.next()
        S.op("pe", lambda e: e.matmul(p.ap[:64, :W], w1.ap[:], peT.ap[:, c0:c0 + W], start=True, stop=True), reads=[w1, peT], writes=[p])
        S.op("dve", lambda e: e.tensor_scalar(out=tmp.ap[:, :W], in0=p.ap[:64, :W], scalar1=vec.ap[:, 0:1], scalar2=vec.ap[:, 2:3], op0=ALU.add, op1=ALU.mult),
             reads=[p, vec], writes=[tmp])
        sin_tail(S, h1, c0, W, tmp)
    for c0 in range(0, Lf, W):
        p = pp.next()
        S.op("pe", lambda e: e.matmul(p.ap[:64, :W], w2.ap[:], h1.ap[:, c0:c0 + W], start=True, stop=True), reads=[w2, h1], writes=[p])
        S.op("dve", lambda e: e.tensor_scalar(out=tmp.ap[:, :W], in0=p.ap[:64, :W], scalar1=vec.ap[:, 1:2], scalar2=vec.ap[:, 2:3], op0=ALU.add, op1=ALU.mult),
             reads=[p, vec], writes=[tmp])
        sin_tail(S, h2, c0, W, tmp)
    ex = mk_ring(S, "hex", [128, 1024], F32, 2)
    fo = mk_ring(S, "hfo", [128, 2, 512], BF16, 2)
    for tc in range(Lf // 128):
        e_ = ex.next()
        S.op("act", lambda e: e.activation(out=e_.ap[:], in_=dec.ap[:], func=AF.Exp, scale=ntn.ap[:, tc:tc + 1]), reads=[dec, ntn], writes=[e_])
        for hf in range(2):
            p = pp.next()
            S.op("pe", lambda e: e.matmul(p.ap[:], h2.ap[:, tc * 128:(tc + 1) * 128], w3.ap[:, hf * 512:(hf + 1) * 512], start=True, stop=True),
                 reads=[h2, w3], writes=[p])
            S.op("dve", lambda e: e.tensor_tensor(out=e_.ap[:, hf * 512:(hf + 1) * 512], in0=p.ap[:], in1=e_.ap[:, hf * 512:(hf + 1) * 512], op=ALU.mult),
                 reads=[p, e_], writes=[e_])
        if tc == 0:
            S.op("dve", lambda e: e.memset(e_.ap[0:1, 512:1024], 0.0), reads=[], writes=[e_])
        o = fo.next()
        S.op("dve", lambda e: e.tensor_tensor(out=o.ap[:, 0, :], in0=e_.ap[:, 512:1024], in1=e_.ap[:, 0:512], op=ALU.add), reads=[e_], writes=[o])
        S.op("pool", lambda e: e.tensor_tensor(out=o.ap[:, 1, :], in0=e_.ap[:, 512:1024], in1=e_.ap[:, 0:512], op=ALU.subtract), reads=[e_], writes=[o])
        S.dma("sp", C.filt[si][tc * 128:(tc + 1) * 128, :, :], o.ap[:], reads=[o])
    S.end_phase()


def sin_tail(S, dst, c0, W, tmp):
    ki, kf = S.sin_ki, S.sin_kf
    S.op("dve", lambda e: e.tensor_scalar(out=tmp.ap[:, :W], in0=tmp.ap[:, :W], scalar1=1.0 / (2.0 * PI), scalar2=16.5, op0=ALU.mult, op1=ALU.add),
         reads=[tmp], writes=[tmp])
    S.op("dve", lambda e: e.tensor_copy(out=ki.ap[:, :W], in_=tmp.ap[:, :W]), reads=[tmp], writes=[ki])
    S.op("dve", lambda e: e.tensor_copy(out=kf.ap[:, :W], in_=ki.ap[:, :W]), reads=[ki], writes=[kf])
    S.op("dve", lambda e: e.scalar_tensor_tensor(out=tmp.ap[:, :W], in0=tmp.ap[:, :W], scalar=-0.5, in1=kf.ap[:, :W], op0=ALU.add, op1=ALU.subtract),
         reads=[tmp, kf], writes=[tmp])
    S.op("dve", lambda e: e.scalar_tensor_tensor(out=tmp.ap[:, :W], in0=tmp.ap[:, :W], scalar=-0.5, in1=tmp.ap[:, :W], op0=ALU.is_lt, op1=ALU.add),
         reads=[tmp], writes=[tmp])
    S.op("act", lambda e: e.activation(out=dst.ap[:, c0:c0 + W], in_=tmp.ap[:, :W], func=AF.Sin, scale=2.0 * PI * 0.999999), reads=[tmp], writes=[dst])


def hyena_fwd(S, C, si, Lf, src, bi, dst, is_filter):
    nt = Lf // 128
    N = 2 * Lf
    if is_filter:
        S.begin_phase()
    a_in = S.sb("fa", [128, nt, 2 if is_filter else 1, 512], BF16)
    if is_filter:
        S.dma("sp", a_in.ap[:], src.rearrange("(tc p) s c -> p tc s c", p=128), writes=[a_in])
        bb = S.sb("fbias", [128, 512], F32)
        S.dma("sp", bb.ap[:], C.hy_bias[:].partition_broadcast(128), writes=[bb])
        S.op("dve", lambda e: e.tensor_scalar(out=bb.ap[:], in0=bb.ap[:], scalar1=2.0 / N, scalar2=None, op0=ALU.mult), reads=[bb], writes=[bb])
    else:
        S.dma("sp", a_in.ap[:, :, 0, :], src.rearrange("(tc p) c -> p tc c", p=128), writes=[a_in])
    fm = mk_ring(S, "ffm", [128, 2, nt, 128], BF16, 2)
    pr = mk_ring(S, "fpr", [128, 512], F32, 2, psum=True)
    pi_ = mk_ring(S, "fpi", [128, 512], F32, 2, psum=True)
    if is_filter:
        ko = mk_ring(S, "fko", [128, 2, 512], F32, 2)
    else:
        ks = mk_ring(S, "fks", [128, 2, 512], F32, 2)
        tt = mk_ring(S, "ftt", [128, 4, 512], F32, 2)
    for fcn in range(nt):
        m = fm.next()
        for cs in range(2):
            S.dma("sp" if cs == 0 else "pool", m.ap[:, cs, :, :], C.fwd[si][fcn, cs, :, :, :], writes=[m])
        a, b = pr.next(), pi_.next()
        for cs, p in ((0, a), (1, b)):
            for tc in range(nt):
                S.op("pe", lambda e: e.matmul(p.ap[:], m.ap[:, cs, tc, :], a_in.ap[:, tc, cs if is_filter else 0, :], start=(tc == 0), stop=(tc == nt - 1)),
                     reads=[m, a_in], writes=[p])
        if is_filter:
            o = ko.next()
            S.op("dve", lambda e: e.scalar_tensor_tensor(out=o.ap[:, 0, :], in0=a.ap[:], scalar=2.0 / N, in1=bb.ap[:], op0=ALU.mult, op1=ALU.add),
                 reads=[a, bb], writes=[o])
            S.op("act", lambda e: e.activation(out=o.ap[:, 1, :], in_=b.ap[:], func=AF.Identity, scale=2.0 / N), reads=[b], writes=[o])
            S.dma("pool", dst[fcn * 128:(fcn + 1) * 128, :, :], o.ap[:], reads=[o])
        else:
            k = ks.next()
            S.dma("sp", k.ap[:], C.kspec[si][fcn * 128:(fcn + 1) * 128, :, :], writes=[k])
            t = tt.next()
            S.op("dve", lambda e: e.tensor_tensor(out=t.ap[:, 0, :], in0=a.ap[:], in1=k.ap[:, 0, :], op=ALU.mult), reads=[a, k], writes=[t])
            S.op("dve", lambda e: e.tensor_tensor(out=t.ap[:, 1, :], in0=b.ap[:], in1=k.ap[:, 1, :], op=ALU.mult), reads=[b, k], writes=[t])
            S.op("dve", lambda e: e.tensor_tensor(out=t.ap[:, 2, :], in0=a.ap[:], in1=k.ap[:, 1, :], op=ALU.mult), reads=[a, k], writes=[t])
            S.op("dve", lambda e: e.tensor_tensor(out=t.ap[:, 3, :], in0=b.ap[:], in1=k.ap[:, 0, :], op=ALU.mult), reads=[b, k], writes=[t])
            S.op("pool", lambda e: e.tensor_tensor(out=dst.ap[:, fcn, 0, :], in0=t.ap[:, 0, :], in1=t.ap[:, 1, :], op=ALU.add), reads=[t], writes=[dst])
            S.op("pool", lambda e: e.tensor_tensor(out=dst.ap[:, fcn, 1, :], in0=t.ap[:, 2, :], in1=t.ap[:, 3, :], op=ALU.subtract), reads=[t], writes=[dst])
    if is_filter:
        S.end_phase()


def l0_inproj(S, C, bi, src_stage):
    S.begin_phase()
    ident = C.ident
    hT = S.sb("ihT", [128, 8, T + 4], BF16)
    S.op("pool", lambda e: e.memset(hT.ap[:], 0.0), writes=[hT])
    xin = mk_ring(S, "ixin", [128, 1024], F32, 2)
    ptr = mk_ring(S, "iptr", [128, 512], F32, 2, psum=True)
    for i in range(NTT):
        xi = xin.next()
        S.dma("sp", xi.ap[:], tile_src(C, src_stage, bi, i), writes=[xi])
        transpose_mod(S, C, xi, hT, colof(i), 0, 0, (C.R - 1 if i < 2 else bi), ptr, ident)
    S.begin_sub()
    wt = S.sb("iwt", [128, 3, 8, 1024], BF16)
    wv = S.sb("iwv", [128, 8, 128], BF16)
    for j in range(3):
        for kc in range(8):
            S.dma("sp" if kc % 2 else "pool", wt.ap[:, j, kc, :], C.whb[j][kc * 128:(kc + 1) * 128, 512:1536], writes=[wt])
    for kc in range(8):
        S.dma("sp", wv.ap[:, kc, :], C.wqb[kc * 128:(kc + 1) * 128, 640:768], writes=[wv])
    pa = mk_ring(S, "ipa", [128, 512], F32, 2, psum=True)
    pb = mk_ring(S, "ipb", [128, 512], F32, 2, psum=True)
    x1s = mk_ring(S, "ix1", [128, 512], F32, 2)
    ut = mk_ring(S, "iut", [128, 512], BF16, 2)
    vt = mk_ring(S, "ivt", [128, 128], BF16, 2)
    for i in range(NTT):
        c0 = colof(i)
        a, b = pa.next(), pb.next()
        for half, p in ((0, a), (1, b)):
            n = 0
            for j in range(3):
                for kc in range(8):
                    S.op("pe", lambda e: e.matmul(p.ap[:], hT.ap[:, kc, c0 + j - 1:c0 + j - 1 + 128], wt.ap[:, j, kc, half * 512:(half + 1) * 512],
                                                  start=(n == 0), stop=(n == 23)), reads=[hT, wt], writes=[p])
                    n += 1
        x1 = x1s.next()
        S.op("act", lambda e: e.activation(out=x1.ap[:], in_=a.ap[:], func=AF.Identity), reads=[a], writes=[x1])
        u = ut.next()
        S.op("dve", lambda e: e.tensor_tensor(out=u.ap[:], in0=b.ap[:], in1=x1.ap[:], op=ALU.mult), reads=[b, x1], writes=[u])
        S.dma("pool", C.u[bi, i * 128:(i + 1) * 128, :], u.ap[:], reads=[u])
        p = pa.next()
        for kc in range(8):
            S.op("pe", lambda e: e.matmul(p.ap[:, :128], hT.ap[:, kc, c0:c0 + 128], wv.ap[:, kc, :], start=(kc == 0), stop=(kc == 7)),
                 reads=[hT, wv], writes=[p])
        v = vt.next()
        S.op("act", lambda e: e.activation(out=v.ap[:], in_=p.ap[:, :128], func=AF.Identity), reads=[p], writes=[v])
        S.dma("pool", C.v[bi, i * 128:(i + 1) * 128, :], v.ap[:], reads=[v])
    S.end_sub()
    S.begin_sub()
    w0 = S.sb("iw0", [128, 3, 8, 512], BF16)
    wq = S.sb("iwq", [128, 8, 1280], BF16)
    rope = S.sb("irope", [64, 2, LAT], F32)
    S.dma("sp", rope.ap[:], C.rope[:, :, :], writes=[rope])
    for j in range(3):
        for kc in range(8):
            S.dma("sp" if kc % 2 else "pool", w0.ap[:, j, kc, :], C.whb[j][kc * 128:(kc + 1) * 128, 0:512], writes=[w0])
    for kc in range(8):
        S.dma("sp", wq.ap[:, kc, 0:640], C.wqb[kc * 128:(kc + 1) * 128, 0:640], writes=[wq])
        S.dma("pool", wq.ap[:, kc, 640:1280], C.wqb[kc * 128:(kc + 1) * 128, 768:1408], writes=[wq])
    pa = mk_ring(S, "jpa", [128, 512], F32, 3, psum=True)
    ot = mk_ring(S, "jot", [128, 512], BF16, 3)
    t1 = mk_ring(S, "jt1", [64, 512], F32, 2)
    t2 = mk_ring(S, "jt2", [64, 512], F32, 2)
    tiles = [(0, 256)] + [(256 + 512 * k, 512) for k in range(4)]
    for (tok0, w) in tiles:
        c0 = colof(tok0 // 128)
        for cc in range(4):
            p = pa.next()
            n = 0
            for j in range(3):
                for kc in range(8):
                    S.op("pe", lambda e: e.matmul(p.ap[:, :w], w0.ap[:, j, kc, cc * 128:(cc + 1) * 128], hT.ap[:, kc, c0 + j - 1:c0 + j - 1 + w],
                                                  start=(n == 0), stop=(n == 23)), reads=[w0, hT], writes=[p])
                    n += 1
            o = ot.next()
            S.op("act", lambda e: e.activation(out=o.ap[:, :w], in_=p.ap[:, :w], func=AF.Identity), reads=[p], writes=[o])
            S.dma("pool", C.x0T[bi, cc * 128:(cc + 1) * 128, tok0:tok0 + w], o.ap[:, :w], reads=[o])
        for hh in range(10):
            p = pa.next()
            for kc in range(8):
                S.op("pe", lambda e: e.matmul(p.ap[:64, :w], wq.ap[:, kc, hh * 64:(hh + 1) * 64], hT.ap[:, kc, c0:c0 + w], start=(kc == 0), stop=(kc == 7)),
                     reads=[wq, hT], writes=[p])
            o = ot.next()
            dst = C.qT[bi, hh, :, tok0:tok0 + w] if hh < 8 else C.kT[bi, hh - 8, :, tok0:tok0 + w]
            if tok0 == 0:
                S.op("act", lambda e: e.activation(out=o.ap[:64, :w], in_=p.ap[:64, :w], func=AF.Identity), reads=[p], writes=[o])
            else:
                p2 = pa.next()
                for kc in range(8):
                    S.op("pe", lambda e: e.matmul(p2.ap[:64, :w], wq.ap[:, kc, 640 + hh * 64:640 + (hh + 1) * 64], hT.ap[:, kc, c0:c0 + w],
                                                  start=(kc == 0), stop=(kc == 7)), reads=[wq, hT], writes=[p2])
                l0_ = tok0 - 256
                a, b = t1.next(), t2.next()
                S.op("dve", lambda e: e.tensor_tensor(out=a.ap[:, :w], in0=p.ap[:64, :w], in1=rope.ap[:, 0, l0_:l0_ + w], op=ALU.mult), reads=[p, rope], writes=[a])
                S.op("dve", lambda e: e.tensor_tensor(out=b.ap[:, :w], in0=p2.ap[:64, :w], in1=rope.ap[:, 1, l0_:l0_ + w], op=ALU.mult), reads=[p2, rope], writes=[b])
                S.op("pool", lambda e: e.tensor_tensor(out=o.ap[:64, :w], in0=a.ap[:, :w], in1=b.ap[:, :w], op=ALU.add), reads=[a, b], writes=[o])
            S.dma("sp", dst, o.ap[:64, :w], reads=[o])
    S.end_sub()
    S.end_phase()


def l0_hyena(S, C, bi):
    for si, (Lf, tok0) in enumerate(((LAT, 256), (CTX, 0))):
        S.begin_phase()
        nt = Lf // 128
        Y = S.sb("hyY", [128, nt, 2, 512], BF16)
        S.begin_sub()
        hyena_fwd(S, C, si, Lf, C.u[bi, tok0:tok0 + Lf, :], bi, Y, is_filter=False)
        S.end_sub()
        tw = min(512, Lf)
        iv = mk_ring(S, "hyiv", [128, nt, 2, tw], BF16, 2 if Lf == CTX else 1)
        x0 = mk_ring(S, "hyx0", [128, tw], BF16, 8)
        x0_pend = {}

        def x0_load(tt_):
            for cc_ in range(4):
                xz_ = x0.next()
                t0__ = tok0 + tt_ * tw
                S.dma("sp", xz_.ap[:], C.x0T[bi, cc_ * 128:(cc_ + 1) * 128, t0__:t0__ + tw], writes=[xz_])
                x0_pend[(tt_, cc_)] = xz_

        x0_load(0)
        ya = mk_ring(S, "hyya", [128, tw], BF16, 2)
        pp = mk_ring(S, "hypp", [128, 512], F32, 2, psum=True)
        for tt in range(Lf // tw):
            m = iv.next()
            for fc in range(nt):
                S.dma("sp" if fc % 2 else "pool", m.ap[:, fc, :, :], C.inv[si][tt, :, fc, :, :], writes=[m])
            if tt + 1 < Lf // tw:
                x0_load(tt + 1)
            for cc in range(4):
                p = pp.next()
                n = 0
                for fc in range(nt):
                    for cs in range(2):
                        S.op("pe", lambda e: e.matmul(p.ap[:, :tw], Y.ap[:, fc, cs, cc * 128:(cc + 1) * 128], m.ap[:, fc, cs, :],
                                                      start=(n == 0), stop=(n == 2 * nt - 1)), reads=[Y, m], writes=[p])
                        n += 1
                xz = x0_pend.pop((tt, cc))
                t0_ = tok0 + tt * tw
                o = ya.next()
                S.op("dve", lambda e: e.tensor_tensor(out=o.ap[:], in0=p.ap[:, :tw], in1=xz.ap[:], op=ALU.mult), reads=[p, xz], writes=[o])
                S.dma("pool", C.yaT[bi, cc * 128:(cc + 1) * 128, t0_:t0_ + tw], o.ap[:], reads=[o])
        S.end_phase()


def l0_attn(S, C, bi):
    S.begin_phase()
    qT = S.sb("aqT", [64, 8, T], BF16)
    kT = S.sb("akT", [64, 2, T], BF16)
    v = S.sb("av", [128, NTT, 128], BF16)
    yb = S.sb("ayb", [64, 8, T], BF16)
    mask = S.sb("amask", [128, 384], F32)
    sink = S.sb("asink", [128, 8], F32)
    idb = S.sb("aidb", [128, 128], BF16)
    for h in range(8):
        S.dma("sp" if h % 2 else "pool", qT.ap[:, h, :], C.qT[bi, h, :, :], writes=[qT])
    for h in range(2):
        S.dma("sp", kT.ap[:, h, :], C.kT[bi, h, :, :], writes=[kT])
    S.dma("sp", v.ap[:], C.v[bi].rearrange("(i p) c -> p i c", p=128), writes=[v])
    S.dma("sp", mask.ap[:], C.amask[:, :], writes=[mask])
    S.dma("sp", sink.ap[:], C.attn_sink[:].partition_broadcast(128), writes=[sink])
    S.dma("sp", idb.ap[:], C.ident_b[:, :], writes=[idb])
    psl = mk_ring(S, "apsl", [128, 512], F32, 2, psum=True)
    psc = mk_ring(S, "apsc", [128, 512], F32, 2, psum=True)
    ppt = mk_ring(S, "appt", [128, 5, 128], BF16, 2, psum=True)
    ppv = mk_ring(S, "appv", [128, 128], F32, 2, psum=True)
    sc = mk_ring(S, "asc", [128, 640], F32, 2)
    pe_ = mk_ring(S, "ape", [128, 640], F32, 2)
    pn = mk_ring(S, "apn", [128, 640], BF16, 2)
    pts = mk_ring(S, "apts", [128, 5, 128], BF16, 2)
    sm = mk_ring(S, "asm", [128, 8], F32, 4)
    def head_chain(qb, hh, n, lo, hi, m0, ktiles, nk, q0):
            h = hh // 4
            s_ = sc.next()
            if n:
                a = psl.next()
                S.op("pe", lambda e: e.matmul(a.ap[:, :n], qT.ap[:, hh, q0:q0 + 128], kT.ap[:, h, 256 + lo:256 + hi], start=True, stop=True),
                     reads=[qT, kT], writes=[a])
                S.op("dve", lambda e: e.tensor_tensor(out=s_.ap[:, :n], in0=a.ap[:, :n], in1=mask.ap[:, m0:m0 + n], op=ALU.add), reads=[a, mask], writes=[s_])
            b = psc.next()
            S.op("pe", lambda e: e.matmul(b.ap[:, :256], qT.ap[:, hh, q0:q0 + 128], kT.ap[:, h, 0:256], start=True, stop=True), reads=[qT, kT], writes=[b])
            S.op("act", lambda e: e.activation(out=s_.ap[:, n:nk], in_=b.ap[:, :256], func=AF.Identity), reads=[b], writes=[s_])
            yield
            w = sm.next()
            S.op("dve", lambda e: e.tensor_reduce(out=w.ap[:, 0:1], in_=s_.ap[:, :nk], axis=AX.X, op=ALU.max), reads=[s_], writes=[w])
            S.op("dve", lambda e: e.tensor_scalar(out=w.ap[:, 1:2], in0=w.ap[:, 0:1], scalar1=0.125, scalar2=sink.ap[:, hh:hh + 1], op0=ALU.mult, op1=ALU.max),
                 reads=[w, sink], writes=[w])
            S.op("dve", lambda e: e.tensor_scalar(out=w.ap[:, 2:3], in0=w.ap[:, 1:2], scalar1=-1.0, scalar2=None, op0=ALU.mult), reads=[w], writes=[w])
            p_ = pe_.next()
            S.op("act", lambda e: e.activation(out=p_.ap[:, :nk], in_=s_.ap[:, :nk], func=AF.Exp, scale=0.125, bias=w.ap[:, 2:3]),
                 reads=[s_, w], writes=[p_])
            yield
            S.op("dve", lambda e: e.tensor_reduce(out=w.ap[:, 3:4], in_=p_.ap[:, :nk], axis=AX.X, op=ALU.add), reads=[p_], writes=[w])
            S.op("act", lambda e: e.activation(out=w.ap[:, 4:5], in_=w.ap[:, 2:3], func=AF.Exp, bias=sink.ap[:, hh:hh + 1], scale=1.0), reads=[w, sink], writes=[w])
            S.op("dve", lambda e: e.tensor_tensor(out=w.ap[:, 5:6], in0=w.ap[:, 3:4], in1=w.ap[:, 4:5], op=ALU.add), reads=[w], writes=[w])
            S.op("dve", lambda e: e.reciprocal(out=w.ap[:, 6:7], in_=w.ap[:, 5:6]), reads=[w], writes=[w])
            pn_ = pn.next()
            S.op("act", lambda e: e.activation(out=pn_.ap[:, :nk], in_=p_.ap[:, :nk], func=AF.Identity, scale=w.ap[:, 6:7]), reads=[p_, w], writes=[pn_])
            yield
            pt = ppt.next()
            nch = nk // 128
            for j in range(nch):
                S.op("pe", lambda e: e.transpose(out=pt.ap[:, j, :], in_=pn_.ap[:, j * 128:(j + 1) * 128], identity=idb.ap[:]), reads=[pn_, idb], writes=[pt])
            ps_ = pts.next()
            S.op("act", lambda e: e.activation(out=ps_.ap[:, :nch, :], in_=pt.ap[:, :nch, :], func=AF.Identity), reads=[pt], writes=[ps_])
            yield
            o = ppv.next()
            for j in range(nch):
                S.op("pe", lambda e: e.matmul(o.ap[:64, :], v.ap[:, ktiles[j], h * 64:(h + 1) * 64], ps_.ap[:, j, :], start=(j == 0), stop=(j == nch - 1)),
                     reads=[v, ps_], writes=[o])
            S.op("dve", lambda e: e.tensor_copy(out=yb.ap[:, hh, q0:q0 + 128], in_=o.ap[:64, :]), reads=[o], writes=[yb])
            yield

    for qb in range(NTT):
        is_ctx = qb < 2
        q0 = qb * 128
        lo = hi = m0 = 0
        if is_ctx:
            n = 0
            ktiles = []
        else:
            lq = q0 - 256
            lo, hi = max(0, lq - 128), min(LAT, lq + 256)
            n = hi - lo
            m0 = lo - (lq - 128)
            ktiles = [2 + lo // 128 + j for j in range(n // 128)]
        ktiles = ktiles + [0, 1]
        nk = n + 256
        for h0 in range(0, 8, 2):
            interleave([head_chain(qb, hh, n, lo, hi, m0, ktiles, nk, q0) for hh in (h0, h0 + 1)])
    for h in range(8):
        S.dma("sp" if h % 2 else "pool", C.ybT[bi, h, :, :], yb.ap[:, h, :], reads=[yb])
    S.end_phase()


def l0_outproj(S, C, bi, src_stage, dst_stage):
    S.begin_phase()
    ya = S.sb("oya", [128, 4, T], BF16)
    yb = S.sb("oyb", [64, 8, T], BF16)
    wa = S.sb("owa", [128, 4, D], BF16)
    wb = S.sb("owb", [64, 8, D], BF16)
    for c in range(4):
        S.dma("sp", ya.ap[:, c, :], C.yaT[bi, c * 128:(c + 1) * 128, :], writes=[ya])
        S.dma("pool", wa.ap[:, c, :], C.wob0[c * 128:(c + 1) * 128, :], writes=[wa])
    for h in range(8):
        S.dma("sp", yb.ap[:, h, :], C.ybT[bi, h, :, :], writes=[yb])
        S.dma("pool", wb.ap[:, h, :], C.wob0[512 + h * 64:512 + (h + 1) * 64, :], writes=[wb])
    epi = Epi(S, C, "o")
    epi.load(0, 0, bi)
    xin = mk_ring(S, "oxin", [128, 1024], F32, 3)
    py = mk_ring(S, "opy", [128, 1024], F32, 2, psum=True)
    op_pend = {}

    def op_load(i_):
        xi_ = xin.next()
        S.dma("sp", xi_.ap[:], tile_src(C, src_stage, bi, i_), writes=[xi_])
        op_pend[i_] = xi_

    op_load(0)
    for i in range(NTT):
        if i + 1 < NTT:
            op_load(i + 1)
        xi = op_pend.pop(i)
        p = py.next()
        for hf in range(2):
            for c in range(4):
                S.op("pe", lambda e: e.matmul(p.ap[:, hf * 512:(hf + 1) * 512], ya.ap[:, c, i * 128:(i + 1) * 128], wa.ap[:, c, hf * 512:(hf + 1) * 512],
                                              start=(c == 0), stop=False), reads=[ya, wa], writes=[p])
            for h in range(8):
                S.op("pe", lambda e: e.matmul(p.ap[:, hf * 512:(hf + 1) * 512], yb.ap[:, h, i * 128:(i + 1) * 128], wb.ap[:, h, hf * 512:(hf + 1) * 512],
                                              start=False, stop=(h == 7)), reads=[yb, wb], writes=[p])
        epi.run(p, xi, i < 2, C.xs[dst_stage - 1][bi, i * 128:(i + 1) * 128, :])
    S.end_phase()


def V(b, *idx):
    return (b, b.ap[idx] if idx else b.ap[:])


def TT(S, eng, o, a, b, op):
    return S.op(eng, lambda e: e.tensor_tensor(out=o[1], in0=a[1], in1=b[1], op=op), reads=[a[0], b[0]], writes=[o[0]])


def TS(S, eng, o, a, s1, s2, op0, op1=None, extra=()):
    if op1 is None:
        return S.op(eng, lambda e: e.tensor_scalar(out=o[1], in0=a[1], scalar1=s1, scalar2=None, op0=op0), reads=[a[0], *extra], writes=[o[0]])
    return S.op(eng, lambda e: e.tensor_scalar(out=o[1], in0=a[1], scalar1=s1, scalar2=s2, op0=op0, op1=op1), reads=[a[0], *extra], writes=[o[0]])


def STT(S, o, a, sc, b, op0, op1, extra=()):
    return S.op("dve", lambda e: e.scalar_tensor_tensor(out=o[1], in0=a[1], scalar=sc, in1=b[1], op0=op0, op1=op1), reads=[a[0], b[0], *extra], writes=[o[0]])


def ACT(S, o, a, func, scale=1.0, bias=None, extra=()):
    if bias is None:
        return S.op("act", lambda e: e.activation(out=o[1], in_=a[1], func=func, scale=scale), reads=[a[0], *extra], writes=[o[0]])
    return S.op("act", lambda e: e.activation(out=o[1], in_=a[1], func=func, scale=scale, bias=bias), reads=[a[0], *extra], writes=[o[0]])


def MM(S, o, l, r, start=True, stop=True):
    return S.op("pe", lambda e: e.matmul(o[1], l[1], r[1], start=start, stop=stop), reads=[l[0], r[0]], writes=[o[0]])


def TR(S, o, a, ident):
    return S.op("pe", lambda e: e.transpose(out=o[1], in_=a[1], identity=ident[1]), reads=[a[0], ident[0]], writes=[o[0]])


RW_E = float(np.exp(-0.5))
INV_DT = BF16


def declare_l1(nc, C, dbg=None):
    NB = C.NB
    I = lambda n, s, dt=F32: dram(nc, n, s, dt, "ExternalInput")
    Sx = lambda n, s, dt=BF16: dram(nc, n, s, dt, "ExternalOutput" if (dbg and n in dbg) else None)
    C.o_w_rw = I("o_w_rw", [D, 1920])
    C.o_w_dn = I("o_w_dn", [D, 1536])
    C.o_w_z = I("o_w_z", [D, 528])
    C.o_w_out = I("o_w_out", [D, D])
    C.rw_mu = I("rw_mu", [1920])
    C.dn_conv = I("dn_conv", [3, 1536])
    C.rw_rows = I("rw_rows", [10, 512])
    C.rw_w2 = I("rw_w2", [128, 512])
    C.rw_a2 = I("rw_a2", [128, 512])
    C.rw_g2 = I("rw_g2", [128, 512])
    C.dn_rows = I("dn_rows", [3, 8])
    C.dn_ng = I("dn_ng", [512])
    C.tri = I("tri", [2, 128, 128])
    C.tris = I("tris", [2, 128, 128])
    C.blkm = I("blkm", [4, 128, 128])
    C.tsw = Sx("tsw", [3, 1920], F32)
    C.wrb = [Sx(f"wrb{j}", [D, 1920]) for j in range(3)]
    C.wdb = [Sx(f"wdb{j}", [D, 1536]) for j in range(3)]
    C.wzb = Sx("wzb", [D, 528])
    C.wob1 = Sx("wob1", [D, D])
    C.rw_ops = Sx("rw_ops", [NB, 2, 6, T, 512])
    C.rw_v = Sx("rw_v", [NB, T, 512])
    C.rw_gc = Sx("rw_gc", [NB, 2, NTT, 64, 8], F32)
    C.rw_g = Sx("rw_g", [NB, T, 512], F32)
    C.rw_bonus = Sx("rw_bonus", [NB, T, 512], F32)
    C.y_rw = Sx("y_rw", [NB, 2, T, 512], F32)
    C.dn_qk = Sx("dn_qk", [NB, 2, T, 512])
    C.dn_ops = Sx("dn_ops", [NB, 2, 5, T, 512])
    C.dn_G = Sx("dn_G", [NB, 2, 3, T, 4], F32)
    C.dn_z = Sx("dn_z", [NB, T, 512], F32)
    C.y_dn = Sx("y_dn", [NB, 2, T, 512], F32)


def host_l1(inputs, m):
    f = lambda a: np.ascontiguousarray(np.asarray(a, dtype=np.float32))
    w = np.asarray(inputs["o_w_in"])[0]
    m["o_w_rw"] = f(w[:, :1920])
    m["o_w_dn"] = f(w[:, 1920:1920 + 1536])
    m["o_w_z"] = f(w[:, 1920 + 1536:])
    m["o_w_out"] = f(np.asarray(inputs["o_w_out"])[0])
    m["rw_mu"] = f(np.asarray(inputs["rw_mu"])[0])
    m["dn_conv"] = f(np.asarray(inputs["dn_conv"])[0])
    g = lambda k: np.asarray(inputs[k])[0]
    m["rw_rows"] = f(np.stack([g("rw_w0")[0], g("rw_w0")[1], g("rw_a0")[0], g("rw_a0")[1], g("rw_kk"), g("rw_ka"),
                               g("rw_rk").reshape(512), g("rw_lnx_g"), g("rw_lnx_b"), np.zeros(512, np.float32)], 0))
    m["rw_w2"] = f(g("rw_w2").reshape(128, 512))
    m["rw_a2"] = f(g("rw_a2").reshape(128, 512))
    m["rw_g2"] = f(g("rw_g2"))
    m["dn_rows"] = f(np.stack([g("dn_A_log").reshape(8), g("dn_dt_bias").reshape(8), np.zeros(8, np.float32)], 0))
    m["dn_ng"] = f(np.tile(g("dn_norm_g"), 4))
    j = np.arange(128)[:, None]
    t = np.arange(128)[None, :]
    m["tri"] = f(np.stack([(j <= t), (j >= t)], 0))
    m["tris"] = f(np.stack([(j < t), (j > t)], 0))
    bd = lambda n: (j // n == t // n)
    m["blkm"] = f(np.stack([bd(16), bd(32) & ~bd(16), bd(64) & ~bd(32), ~bd(64)], 0))
    return m


def l1_setup(S, C):
    S.begin_phase()
    mu = S.sb("smu", [1, 1920], F32)
    o = S.sb("smo", [1, 3, 1920], F32)
    S.dma("sp", mu.ap[:], C.rw_mu[:].partition_broadcast(1), writes=[mu])
    TS(S, "dve", V(o, slice(None), 0, slice(None)), V(mu), 0.5, None, ALU.mult)
    TS(S, "dve", V(o, slice(None), 1, slice(None)), V(mu), -1.0, 1.0, ALU.mult, ALU.add)
    TS(S, "dve", V(o, slice(None), 2, slice(None)), V(mu), 0.5, None, ALU.mult)
    S.dma("sp", C.tsw.rearrange("(o j) n -> o j n", o=1), o.ap[:], reads=[o])
    S.end_phase()
    for j in range(3):
        prep_weight(S, C.wrb[j], C.o_w_rw, D, 1920, scale=C.tsw[j, :], tag=f"qr{j}")
        prep_weight(S, C.wdb[j], C.o_w_dn, D, 1536, scale=C.dn_conv[j, :], tag=f"qd{j}")
    prep_weight(S, C.wzb, C.o_w_z, D, 528, tag="qz")
    prep_weight(S, C.wob1, C.o_w_out, D, D, tag="qo")


def build_hT(S, C, bi, src_stage, l):
    hT = S.sb("bhT", [128, 8, T + 4], BF16)
    S.op("pool", lambda e: e.memset(hT.ap[:], 0.0), writes=[hT])
    S.begin_sub()
    xin = mk_ring(S, "bxin", [128, 1024], F32, 2)
    ptr = mk_ring(S, "bptr", [128, 512], F32, 2, psum=True)
    for i in range(NTT):
        xi = xin.next()
        S.dma("sp", xi.ap[:], tile_src(C, src_stage, bi, i), writes=[xi])
        transpose_mod(S, C, xi, hT, colof(i), l, 0, (C.R - 1 if i < 2 else bi), ptr, C.ident)
    S.end_sub()
    return hT


def bc3(b, n, w):
    return (b, b.ap[:, 0:n].unsqueeze(2).to_broadcast([128, n, w]))


def r3(b, n, *pre):
    ap = b.ap[pre] if pre else b.ap[:]
    return (b, ap.rearrange("p (h d) -> p h d", h=n))


def proj3(S, C, p, hT, c0, w, wt, col0, ncol):
    n = 0
    for j in range(3):
        for kc in range(8):
            MM(S, (p, p.ap[:, :ncol]), (hT, hT.ap[:, kc, c0 + j - 1:c0 + j - 1 + 128]), (wt, wt.ap[:, j, kc, col0:col0 + ncol]), start=(n == 0), stop=(n == 23))
            n += 1


def l1_feat_rw(S, C, bi, hT):
    S.begin_sub()
    ones = S.sb("fones", [128, 128], F32)
    S.op("pool", lambda e: e.memset(ones.ap[:], 1.0), writes=[ones])
    tri = S.sb("ftri", [128, 2, 128], F32)
    for d in range(2):
        S.dma("sp", tri.ap[:, d, :], C.tri[d, :, :], writes=[tri])
    rows = S.sb("frows", [128, 7, 512], F32)
    for q in range(7):
        S.dma("sp", rows.ap[:, q, :], C.rw_rows[q, :].partition_broadcast(128), writes=[rows])
    lwb = S.sb("flwb", [128, 3, 512], BF16)
    loraT = S.sb("floraT", [128, 3, T], BF16)
    S.begin_sub()
    lw = S.sb("flw", [128, 3, 512], F32)
    for q, src in enumerate((C.rw_w2, C.rw_a2, C.rw_g2)):
        S.dma("sp", lw.ap[:, q, :], src[:, :], writes=[lw])
    S.op("dve", lambda e: e.tensor_copy(out=lwb.ap[:], in_=lw.ap[:]), reads=[lw], writes=[lwb])
    wl = S.sb("fwl", [128, 3, 8, 384], BF16)
    for j in range(3):
        for kc in range(8):
            S.dma("sp" if kc % 2 else "pool", wl.ap[:, j, kc, :], C.wrb[j][kc * 128:(kc + 1) * 128, 1536:1920], writes=[wl])
    pp = mk_ring(S, "fpp", [128, 512], F32, 2, psum=True)
    for (tok0, w) in [(0, 256)] + [(256 + 512 * k, 512) for k in range(4)]:
        c0 = colof(tok0 // 128)
        for q, fn in enumerate((AF.Tanh, AF.Identity, AF.Sigmoid)):
            p = pp.next()
            n = 0
            for j in range(3):
                for kc in range(8):
                    MM(S, (p, p.ap[:, :w]), (wl, wl.ap[:, j, kc, q * 128:(q + 1) * 128]), (hT, hT.ap[:, kc, c0 + j - 1:c0 + j - 1 + w]), start=(n == 0), stop=(n == 23))
                    n += 1
            ACT(S, (loraT, loraT.ap[:, q, tok0:tok0 + w]), (p, p.ap[:, :w]), fn)
    S.end_sub()
    wr = S.sb("fwr", [128, 3, 8, 1536], BF16)
    for j in range(3):
        for kc in range(8):
            S.dma("sp" if kc % 2 else "pool", wr.ap[:, j, kc, :], C.wrb[j][kc * 128:(kc + 1) * 128, 0:1536], writes=[wr])
    prkv = [S.ps(f"fp{n}", [128, 512], F32) for n in "rkv"]
    pqd = [mk_ring(S, f"fpq{d}", [128, 512], F32, 2, psum=True) for d in range(2)]
    F = lambda n: S.sb("f_" + n, [128, 512], F32)
    rs, ks, kkr, sq, kk, ksum = [F(n) for n in "rs ks kkr sq kk ksum".split()]
    Fd = []
    for d in range(2):
        X = Ctx()
        X.zt, X.logw, X.a_, X.Gs, X.eG, X.enG, X.eE, X.kd, X.bd = [F(f"{n}{d}") for n in "zt logw a Gs eG enG eE kd bd".split()]
        X.eP, X.t1 = X.zt, X.Gs
        Fd.append(X)
    tb, bon, gs = sq, Fd[0].zt, Fd[0].Gs
    vb = mk_ring(S, "fvb", [128, 512], BF16, 2)
    ot = mk_ring(S, "fot", [128, 6, 512], BF16, 2)
    sm = mk_ring(S, "fsm", [128, 16], F32, 2)
    gcs = mk_ring(S, "fgcs", [64, 8], F32, 2)
    KK, KA, RK = [(rows, rows.ap[:, q, :]) for q in (4, 5, 6)]

    def dir_chain(d, i, rws):
        X = Fd[d]
        zt, logw, a_, Gs, eG, enG, eP, eE, t1, kd, bd = X.zt, X.logw, X.a_, X.Gs, X.eG, X.enG, X.eP, X.eE, X.t1, X.kd, X.bd
        o = ot.next()
        pq = pqd[d]
        pgc = prkv[d]
        pz, pa_ = pq.next(), pq.next()
        MM(S, V(pz), (loraT, loraT.ap[d * 64:(d + 1) * 64, 0, rws]), (lwb, lwb.ap[d * 64:(d + 1) * 64, 0, :]))
        MM(S, V(pa_), (loraT, loraT.ap[d * 64:(d + 1) * 64, 1, rws]), (lwb, lwb.ap[d * 64:(d + 1) * 64, 1, :]))
        TT(S, "dve", V(zt), V(pz), (rows, rows.ap[:, d, :]), ALU.add)
        ACT(S, V(zt), V(zt), AF.Sigmoid)
        TS(S, "pool", V(logw), V(zt), -RW_E, None, ALU.mult)
        TT(S, "dve", V(a_), V(pa_), (rows, rows.ap[:, 2 + d, :]), ALU.add)
        ACT(S, V(a_), V(a_), AF.Sigmoid)
        yield
        pG, pT = pq.next(), pq.next()
        MM(S, V(pG), (tri, tri.ap[:, d, :]), V(logw))
        MM(S, V(pT), V(ones), V(logw))
        for h in range(8):
            MM(S, (pgc, pgc.ap[:64, h:h + 1]), (logw, logw.ap[:, h * 64:(h + 1) * 64]), (ones, ones.ap[:, 0:1]))
        gc = gcs.next()
        ACT(S, V(gc), (pgc, pgc.ap[:64, 0:8]), AF.Exp)
        S.dma("pool", C.rw_gc[bi, d, i, :, :], gc.ap[:], reads=[gc])
        ACT(S, V(Gs), V(pG), AF.Identity)
        yield
        ACT(S, V(eG), V(Gs), AF.Exp)
        ACT(S, V(enG), V(Gs), AF.Exp, scale=-1.0)
        TT(S, "dve", V(eP), V(Gs), V(logw), ALU.subtract)
        ACT(S, V(eP), V(eP), AF.Exp)
        TT(S, "dve", V(eE), V(pT), V(Gs), ALU.subtract)
        ACT(S, V(eE), V(eE), AF.Exp)
        yield
        STT(S, V(t1), V(a_), -1.0, KA, ALU.add, ALU.mult)
        STT(S, V(kd), V(t1), 1.0, V(ks), ALU.add, ALU.mult)
        TT(S, "pool", V(bd), V(kk), V(a_), ALU.mult)
        yield
        O = lambda q: (o, o.ap[:, q, :])
        TT(S, "dve", O(0), V(rs), V(eG), ALU.mult)
        TT(S, "pool", O(1), V(kd), V(enG), ALU.mult)
        TT(S, "dve", O(2), V(bd), V(enG), ALU.mult)
        TT(S, "pool", O(3), V(kk), V(eP), ALU.mult)
        TT(S, "dve", O(4), V(kd), V(eE), ALU.mult)
        TT(S, "pool", O(5), V(bd), V(eE), ALU.mult)
        S.dma("sp", C.rw_ops[bi, d, :, rws, :].rearrange("q t c -> t q c"), o.ap[:], reads=[o])
        yield

    for i in range(NTT):
        c0 = colof(i)
        rws = slice(i * 128, (i + 1) * 128)
        for n in range(3):
            proj3(S, C, prkv[n], hT, c0, 128, wr, n * 512, 512)
        ACT(S, V(rs), V(prkv[0]), AF.Identity)
        ACT(S, V(ks), V(prkv[1]), AF.Identity)
        v_ = vb.next()
        ACT(S, V(v_), V(prkv[2]), AF.Identity)
        S.dma("pool", C.rw_v[bi, rws, :], v_.ap[:], reads=[v_])
        w = sm.next()
        TT(S, "dve", V(kkr), V(ks), KK, ALU.mult)
        TT(S, "pool", V(sq), V(kkr), V(kkr), ALU.mult)
        S.op("dve", lambda e: e.tensor_reduce(out=w.ap[:, 0:8], in_=r3(sq, 8)[1], axis=AX.X, op=ALU.add), reads=[sq], writes=[w])
        TS(S, "dve", V(w, slice(None), slice(0, 8)), V(w, slice(None), slice(0, 8)), 1e-6, None, ALU.add)
        ACT(S, V(w, slice(None), slice(0, 8)), V(w, slice(None), slice(0, 8)), AF.Sqrt)
        S.op("dve", lambda e: e.reciprocal(out=w.ap[:, 0:8], in_=w.ap[:, 0:8]), reads=[w], writes=[w])
        TT(S, "dve", r3(kk, 8), r3(kkr, 8), bc3(w, 8, 64), ALU.mult)
        interleave([dir_chain(d, i, rws) for d in range(2)])
        TT(S, "pool", V(ksum), V(Fd[0].kd), V(Fd[1].kd), ALU.add)
        TT(S, "dve", V(tb), V(rs), V(ksum), ALU.mult)
        TT(S, "pool", V(tb), V(tb), RK, ALU.mult)
        S.op("dve", lambda e: e.tensor_reduce(out=w.ap[:, 8:16], in_=r3(tb, 8)[1], axis=AX.X, op=ALU.add), reads=[tb], writes=[w])
        TT(S, "dve", r3(bon, 8), r3(v_, 8), (w, w.ap[:, 8:16].unsqueeze(2).to_broadcast([128, 8, 64])), ALU.mult)
        S.dma("pool", C.rw_bonus[bi, rws, :], bon.ap[:], reads=[bon])
        pg = pqd[0].next()
        MM(S, V(pg), (loraT, loraT.ap[:, 2, rws]), (lwb, lwb.ap[:, 2, :]))
        ACT(S, V(gs), V(pg), AF.Identity)
        S.dma("pool", C.rw_g[bi, rws, :], gs.ap[:], reads=[gs])
    S.end_sub()


def l1_feat_dn(S, C, bi, hT):
    S.begin_sub()
    ones = S.sb("gones", [128, 128], F32)
    S.op("pool", lambda e: e.memset(ones.ap[:], 1.0), writes=[ones])
    tri = S.sb("gtri", [128, 2, 128], F32)
    for d in range(2):
        S.dma("sp", tri.ap[:, d, :], C.tri[d, :, :], writes=[tri])
    dr = S.sb("gdr", [128, 2, 8], F32)
    for q in range(2):
        S.dma("sp", dr.ap[:, q, :], C.dn_rows[q, :].partition_broadcast(128), writes=[dr])
    ACT(S, V(dr, slice(None), 0, slice(None)), V(dr, slice(None), 0, slice(None)), AF.Exp)
    TS(S, "dve", V(dr, slice(None), 0, slice(None)), V(dr, slice(None), 0, slice(None)), -1.0, None, ALU.mult)
    wd = S.sb("gwd", [128, 3, 8, 1536], BF16)
    wz = S.sb("gwz", [128, 8, 528], BF16)
    for j in range(3):
        for kc in range(8):
            S.dma("sp" if kc % 2 else "pool", wd.ap[:, j, kc, :], C.wdb[j][kc * 128:(kc + 1) * 128, :], writes=[wd])
    for kc in range(8):
        S.dma("sp", wz.ap[:, kc, :], C.wzb[kc * 128:(kc + 1) * 128, :], writes=[wz])
    pqkv = [S.ps(f"gp{n}", [128, 512], F32) for n in "qkv"]
    pz = S.ps("gpz", [128, 512], F32)
    pgt = S.ps("gpgt", [128, 16], F32)
    pG = mk_ring(S, "gpG", [128, 8], F32, 2, psum=True)
    F = lambda n: S.sb("g_" + n, [128, 512], F32)
    qs, ks, vs, sq, zs = [F(n) for n in "qs ks vs sq zs".split()]
    qk = mk_ring(S, "gqk", [128, 2, 512], BF16, 2)
    ot = mk_ring(S, "got", [128, 5, 512], BF16, 2)
    sm = mk_ring(S, "gsm", [128, 64], F32, 2)
    go = mk_ring(S, "ggo", [128, 3, 4], F32, 2)
    for i in range(NTT):
        c0 = colof(i)
        rws = slice(i * 128, (i + 1) * 128)
        for n in range(3):
            proj3(S, C, pqkv[n], hT, c0, 128, wd, n * 512, 512)
        for kc in range(8):
            MM(S, V(pz), (hT, hT.ap[:, kc, c0:c0 + 128]), (wz, wz.ap[:, kc, 0:512]), start=(kc == 0), stop=(kc == 7))
        for kc in range(8):
            MM(S, V(pgt), (hT, hT.ap[:, kc, c0:c0 + 128]), (wz, wz.ap[:, kc, 512:528]), start=(kc == 0), stop=(kc == 7))
        for src, dst in zip(pqkv + [pz], (qs, ks, vs, zs)):
            ACT(S, V(dst), V(src), AF.Silu)
        S.dma("pool", C.dn_z[bi, rws, :], zs.ap[:], reads=[zs])
        w = sm.next()
        Wc = lambda a, b: (w, w.ap[:, a:b])
        ACT(S, Wc(0, 16), V(pgt), AF.Identity)
        qk_ = qk.next()
        for n, (src, sc) in enumerate(((qs, 128.0 ** -0.5), (ks, 1.0))):
            TT(S, "pool", V(sq), V(src), V(src), ALU.mult)
            S.op("dve", lambda e: e.tensor_reduce(out=w.ap[:, 16 + 4 * n:20 + 4 * n], in_=r3(sq, 4)[1], axis=AX.X, op=ALU.add), reads=[sq], writes=[w])
            TS(S, "dve", Wc(16 + 4 * n, 20 + 4 * n), Wc(16 + 4 * n, 20 + 4 * n), 1e-6, None, ALU.add)
            ACT(S, Wc(16 + 4 * n, 20 + 4 * n), Wc(16 + 4 * n, 20 + 4 * n), AF.Sqrt)
            S.op("dve", lambda e: e.reciprocal(out=w.ap[:, 16 + 4 * n:20 + 4 * n], in_=w.ap[:, 16 + 4 * n:20 + 4 * n]), reads=[w], writes=[w])
            if sc != 1.0:
                TS(S, "dve", Wc(16, 20), Wc(16, 20), sc, None, ALU.mult)
            TT(S, "dve", r3(src, 4), r3(src, 4), (w, w.ap[:, 16 + 4 * n:20 + 4 * n].unsqueeze(2).to_broadcast([128, 4, 128])), ALU.mult)
            S.op("pool", lambda e: e.tensor_copy(out=qk_.ap[:, n, :], in_=src.ap[:]), reads=[src], writes=[qk_])
        S.dma("sp", C.dn_qk[bi, :, rws, :].rearrange("q t c -> t q c"), qk_.ap[:], reads=[qk_])
        for d in range(2):
            o = ot.next()
            g_ = go.next()
            TT(S, "dve", Wc(24, 28), Wc(d * 4, d * 4 + 4), (dr, dr.ap[:, 1, d * 4:d * 4 + 4]), ALU.add)
            ACT(S, Wc(24, 28), Wc(24, 28), AF.Exp)
            ACT(S, Wc(24, 28), Wc(24, 28), AF.Ln, bias=1.0)
            TT(S, "dve", (g_, g_.ap[:, 0, :]), Wc(24, 28), (dr, dr.ap[:, 0, d * 4:d * 4 + 4]), ALU.mult)
            ACT(S, Wc(28, 32), Wc(8 + d * 4, 12 + d * 4), AF.Sigmoid)
            p = pG.next()
            MM(S, (p, p.ap[:, 0:4]), (tri, tri.ap[:, d, :]), (g_, g_.ap[:, 0, :]))
            MM(S, (p, p.ap[:, 4:8]), V(ones), (g_, g_.ap[:, 0, :]))
            ACT(S, (g_, g_.ap[:, 1:3, :]), (p, p.ap[:, 0:8].rearrange("p (a b) -> p a b", a=2)), AF.Identity)
            S.dma("pool", C.dn_G[bi, d, :, rws, :].rearrange("q t c -> t q c"), g_.ap[:], reads=[g_])
            ACT(S, Wc(32, 36), (g_, g_.ap[:, 1, :]), AF.Exp)
            TT(S, "dve", Wc(36, 40), (g_, g_.ap[:, 2, :]), (g_, g_.ap[:, 1, :]), ALU.subtract)
            ACT(S, Wc(36, 40), Wc(36, 40), AF.Exp)
            TT(S, "dve", Wc(40, 44), Wc(28, 32), Wc(32, 36), ALU.mult)
            B4 = lambda a: (w, w.ap[:, a:a + 4].unsqueeze(2).to_broadcast([128, 4, 128]))
            O = lambda q: (o, o.ap[:, q, :].rearrange("p (h d) -> p h d", h=4))
            TT(S, "dve", O(0), r3(qs, 4), B4(32), ALU.mult)
            TT(S, "pool", O(1), r3(ks, 4), B4(28), ALU.mult)
            TT(S, "dve", O(2), r3(ks, 4), B4(40), ALU.mult)
            TT(S, "pool", O(3), r3(ks, 4), B4(36), ALU.mult)
            TT(S, "dve", O(4), r3(vs, 4), B4(28), ALU.mult)
            S.dma("sp", C.dn_ops[bi, d, :, rws, :].rearrange("q t c -> t q c"), o.ap[:], reads=[o])
    S.end_sub()


def inv_group(S, P, PT, K, ps, out, nh=4):
    mk = lambda: K.rW.next()
    pA, pB, pC = ps
    PD, PDT, Z, ZT = mk(), mk(), K.rZ.next(), K.rZ.next()
    TT(S, "dve", V(PD), V(P), V(K.mb[0]), ALU.mult)
    TT(S, "pool", V(PDT), V(PT), V(K.mb[0]), ALU.mult)
    TT(S, "dve", V(Z), V(K.ident4), V(PD), ALU.subtract)
    TT(S, "pool", V(ZT), V(K.ident4), V(PDT), ALU.subtract)
    yield
    cur, curT = PD, PDT
    for lv in range(3):
        for h in range(nh):
            MM(S, (pA, pA.ap[:, h, :]), (curT, curT.ap[:, h, :]), (cur, cur.ap[:, h, :]))
        for h in range(nh):
            MM(S, (pB, pB.ap[:, h, :]), (cur, cur.ap[:, h, :]), (curT, curT.ap[:, h, :]))
        Pn, PTn = mk(), mk()
        ACT(S, V(Pn), V(pA), AF.Identity)
        S.op("dve", lambda e: e.tensor_copy(out=PTn.ap[:], in_=pB.ap[:]), reads=[pB], writes=[PTn])
        yield
        for h in range(nh):
            MM(S, (pC, pC.ap[:, h, :]), (PTn, PTn.ap[:, h, :]), (Z, Z.ap[:, h, :]))
        for h in range(nh):
            MM(S, (pA, pA.ap[:, h, :]), (Pn, Pn.ap[:, h, :]), (ZT, ZT.ap[:, h, :]))
        TT(S, "dve", V(Z), V(Z), V(pC), ALU.add)
        TT(S, "dve", V(ZT), V(ZT), V(pA), ALU.add)
        yield
        cur, curT = Pn, PTn
    for m in range(1, 4):
        last = m == 3
        O, OT, Y = mk(), mk(), mk()
        TT(S, "dve", V(O), V(P), V(K.mb[m]), ALU.mult)
        TT(S, "pool", V(OT), V(PT), V(K.mb[m]), ALU.mult)
        for h in range(nh):
            MM(S, (pA, pA.ap[:, h, :]), (OT, OT.ap[:, h, :]), (Z, Z.ap[:, h, :]))
        ACT(S, V(Y), V(pA), AF.Identity)
        if not last:
            YT = mk()
            for h in range(nh):
                MM(S, (pB, pB.ap[:, h, :]), (Z, Z.ap[:, h, :]), (OT, OT.ap[:, h, :]))
            S.op("dve", lambda e: e.tensor_copy(out=YT.ap[:], in_=pB.ap[:]), reads=[pB], writes=[YT])
        yield
        for h in range(nh):
            MM(S, (pC, pC.ap[:, h, :]), (ZT, ZT.ap[:, h, :]), (Y, Y.ap[:, h, :]))
        if not last:
            for h in range(nh):
                MM(S, (pB, pB.ap[:, h, :]), (Y, Y.ap[:, h, :]), (ZT, ZT.ap[:, h, :]))
        TT(S, "dve", V(Z), V(Z), V(pC), ALU.subtract)
        if not last:
            TT(S, "dve", V(ZT), V(ZT), V(pB), ALU.subtract)
        yield
    out.append(Z)


def interleave(gens):
    gens = list(gens)
    while gens:
        for g in list(gens):
            try:
                next(g)
            except StopIteration:
                gens.remove(g)


def scan_consts(S, C, tag):
    K = Ctx()
    K.idb = S.sb(tag + "idb", [128, 128], BF16)
    S.dma("sp", K.idb.ap[:], C.ident_b[:, :], writes=[K.idb])
    K.ident4 = S.sb(tag + "id4", [128, 4, 128], F32)
    K.mSI = [S.sb(tag + f"mSI{d}", [128, 4, 2, 128], F32) for d in range(2)]
    K.mS = [S.sb(tag + f"mS{d}", [128, 4, 128], F32) for d in range(2)]
    K.mI = [S.sb(tag + f"mI{d}", [128, 4, 128], F32) for d in range(2)]
    for h in range(4):
        S.dma("sp", K.ident4.ap[:, h, :], C.ident_d[:, :], writes=[K.ident4])
        for d in range(2):
            S.dma("sp", K.mSI[d].ap[:, h, 0, :], C.tris[d, :, :], writes=[K.mSI[d]])
            S.dma("pool", K.mSI[d].ap[:, h, 1, :], C.tri[d, :, :], writes=[K.mSI[d]])
            S.dma("sp", K.mS[d].ap[:, h, :], C.tris[d, :, :], writes=[K.mS[d]])
            S.dma("pool", K.mI[d].ap[:, h, :], C.tri[d, :, :], writes=[K.mI[d]])
    K.mb = [S.sb(tag + f"mb{m}", [128, 4, 128], F32) for m in range(4)]
    for m in range(4):
        for h in range(4):
            S.dma("sp" if h % 2 else "pool", K.mb[m].ap[:, h, :], C.blkm[m, :, :], writes=[K.mb[m]])
    return K


def chain_res(S, K, tag):
    R = Ctx()
    R.__dict__.update(K.__dict__)
    R.rP = mk_ring(S, tag + "rP", [128, 4, 128], INV_DT, 1)
    R.rPT = mk_ring(S, tag + "rPT", [128, 4, 128], INV_DT, 1)
    R.rW = mk_ring(S, tag + "rW", [128, 4, 128], INV_DT, 8)
    R.rZ = mk_ring(S, tag + "rZ", [128, 4, 128], INV_DT, 4)
    return R


def tile_order(d):
    return list(range(NTT)) if d == 0 else [1, 0] + list(range(NTT - 1, 1, -1))


def l1_scan_rw(S, C, bi):
    S.begin_phase()
    S.keep_pool = True
    K0 = scan_consts(S, C, "r")
    interleave([rw_chain(S, C, chain_res(S, K0, f"r{d}"), bi, d) for d in range(2)])
    S.keep_pool = False
    S.end_phase()


def rw_chain(S, C, K, bi, d):
    t = f"r{d}"
    X0, X1, X2 = [S.ps(t + f"X{n}", [128, 4, 128], F32) for n in range(3)]
    ptr = S.ps(t + "ptr", [64, 8, 128], BF16)
    ot_r = mk_ring(S, t + "ot", [128, 6, 512], BF16, 2)
    vt_r = mk_ring(S, t + "vt", [128, 512], BF16, 2)
    gc_r = mk_ring(S, t + "gc", [64, 8], F32, 2)
    AR_r = mk_ring(S, t + "AR", [64, 8, 2, 128], BF16, 1)
    KT_r = mk_ring(S, t + "KT", [64, 8, 128], BF16, 1)
    BT_r = mk_ring(S, t + "BT", [64, 8, 128], BF16, 1)
    KN_r = mk_ring(S, t + "KN", [128, 8, 2, 128], BF16, 1)
    NB_r = mk_ring(S, t + "NB", [128, 8, 128], BF16, 1)
    Zb_r = mk_ring(S, t + "Zb", [128, 4, 128], BF16, 1)
    WT_r = mk_ring(S, t + "WT", [64, 8, 128], BF16, 1)
    Xs_r = mk_ring(S, t + "Xs", [128, 4, 64], BF16, 1)
    nU0_r = mk_ring(S, t + "nU0", [128, 8, 64], F32, 1)
    nU_r = mk_ring(S, t + "nU", [128, 8, 64], BF16, 1)
    ys_r = mk_ring(S, t + "ys", [128, 512], F32, 2)
    ST = S.sb(t + "ST", [64, 8, 64], F32)
    STb = S.sb(t + "STb", [64, 8, 64], BF16)
    S.op("dve", lambda e: e.memset(ST.ap[:], 0.0), writes=[ST])
    S.op("dve", lambda e: e.memset(STb.ap[:], 0.0), writes=[STb])
    v8 = lambda p: p.ap[:].rearrange("p a b -> p (a b)").rearrange("p (h v) -> p h v", v=64)
    order = tile_order(d)
    pend = {}

    def load(i):
        rws_ = slice(i * 128, (i + 1) * 128)
        ot, vt, gc = ot_r.next(), vt_r.next(), gc_r.next()
        S.dma("sp", ot.ap[:], C.rw_ops[bi, d, :, rws_, :].rearrange("q t c -> t q c"), writes=[ot])
        S.dma("pool", vt.ap[:], C.rw_v[bi, rws_, :], writes=[vt])
        S.dma("pool", gc.ap[:], C.rw_gc[bi, d, i, :, :], writes=[gc])
        pend[i] = (ot, vt, gc)

    load(order[0])
    for n_, i in enumerate(order):
        rws = slice(i * 128, (i + 1) * 128)
        if n_ + 1 < len(order):
            load(order[n_ + 1])
        ot, vt, gc = pend.pop(i)
        AR, KT, BT = AR_r.next(), KT_r.next(), BT_r.next()
        bview = lambda X: X.ap[:].bitcast(BF16).rearrange("p a (b c) -> p (a b) c", b=2)
        tgt = [(ptr, ptr.ap[:]), (X0, bview(X0)[:64]), (X1, bview(X1)[:64]), (X2, bview(X2)[:64])]
        for n_, (q, dst) in enumerate(((3, (AR, AR.ap[:, :, 0, :])), (0, (AR, AR.ap[:, :, 1, :])), (1, V(KT)), (2, V(BT)))):
            tb_, tv = tgt[n_]
            for h in range(8):
                TR(S, (tb_, tv[:, h, :]), (ot, ot.ap[:, q, h * 64:(h + 1) * 64]), V(K.idb))
        for n_, (q, dst) in enumerate(((3, (AR, AR.ap[:, :, 0, :])), (0, (AR, AR.ap[:, :, 1, :])), (1, V(KT)), (2, V(BT)))):
            tb_, tv = tgt[n_]
            if n_ % 2 == 0:
                ACT(S, dst, (tb_, tv), AF.Identity)
            else:
                S.op("dve", lambda e: e.tensor_copy(out=dst[1], in_=tv), reads=[tb_], writes=[dst[0]])
        yield
        KN, NBm, WT, nU0 = KN_r.next(), NB_r.next(), WT_r.next(), nU0_r.next()
        for g in range(2):
            hs = [(hl, g * 4 + hl) for hl in range(4)]
            P, PT = K.rP.next(), K.rPT.next()
            for hl, h in hs:
                MM(S, (X0, X0.ap[:, hl, :]), (BT, BT.ap[:, h, :]), (AR, AR.ap[:, h, 0, :]))
            for hl, h in hs:
                MM(S, (X1, X1.ap[:, hl, :]), (AR, AR.ap[:, h, 0, :]), (BT, BT.ap[:, h, :]))
            for hl, h in hs:
                MM(S, (X2, X2.ap[:, hl, :]), (BT, BT.ap[:, h, :]), (AR, AR.ap[:, h, 1, :]))
            TT(S, "dve", V(P), V(X0), V(K.mS[d]), ALU.mult)
            TT(S, "dve", V(PT), V(X1), V(K.mS[1 - d]), ALU.mult)
            TT(S, "dve", (NBm, NBm.ap[:, g * 4:(g + 1) * 4, :]), V(X2), V(K.mI[d]), ALU.mult)
            yield
            for hl, h in hs:
                MM(S, (X0, X0.ap[:, hl, :]), (KT, KT.ap[:, h, :]), (AR, AR.ap[:, h, 0, :]))
            for hl, h in hs:
                MM(S, (X1, X1.ap[:, hl, :]), (KT, KT.ap[:, h, :]), (AR, AR.ap[:, h, 1, :]))
            TT(S, "dve", (KN, KN.ap[:, g * 4:(g + 1) * 4, 0, :]), V(X0), V(K.mS[d]), ALU.mult)
            TT(S, "dve", (KN, KN.ap[:, g * 4:(g + 1) * 4, 1, :]), V(X1), V(K.mI[d]), ALU.mult)
            yield
            zo = []
            yield from inv_group(S, P, PT, K, (X0, X1, X2), zo)
            Zb = zo[0]
            for hl, h in hs:
                MM(S, (X0, X0.ap[:, hl, 0:64]), (KN, KN.ap[:, h, 0, :]), (vt, vt.ap[:, h * 64:(h + 1) * 64]))
            Xs = Xs_r.next()
            ACT(S, V(Xs), (X0, X0.ap[:, :, 0:64]), AF.Identity)
            yield
            for hl, h in hs:
                MM(S, (X1, X1.ap[:64, hl, :]), (ot, ot.ap[:, 3, h * 64:(h + 1) * 64]), (Zb, Zb.ap[:, hl, :]))
            ACT(S, (WT, WT.ap[:, g * 4:(g + 1) * 4, :]), (X1, X1.ap[:64, :, :]), AF.Identity)
            for hl, h in hs:
                MM(S, (X2, X2.ap[:, hl, 0:64]), (Zb, Zb.ap[:, hl, :]), (Xs, Xs.ap[:, hl, :]))
            TS(S, "dve", (nU0, nU0.ap[:, g * 4:(g + 1) * 4, :]), (X2, X2.ap[:, :, 0:64]), -1.0, None, ALU.mult)
            yield
        for h in range(8):
            MM(S, (X0, v8(X0)[:, h, :]), (WT, WT.ap[:, h, :]), (STb, STb.ap[:, h, :]))
        nU = nU_r.next()
        TT(S, "dve", V(nU), V(nU0), (X0, v8(X0)), ALU.subtract)
        yield
        for h in range(8):
            MM(S, (X1, v8(X1)[:, h, :]), (AR, AR.ap[:, h, 1, :]), (STb, STb.ap[:, h, :]), start=True, stop=False)
            MM(S, (X1, v8(X1)[:, h, :]), (KN, KN.ap[:, h, 1, :]), (vt, vt.ap[:, h * 64:(h + 1) * 64]), start=False, stop=False)
            MM(S, (X1, v8(X1)[:, h, :]), (NBm, NBm.ap[:, h, :]), (nU, nU.ap[:, h, :]), start=False, stop=True)
        ys = ys_r.next()
        ACT(S, r3(ys, 8), (X1, v8(X1)), AF.Identity)
        S.dma("sp", C.y_rw[bi, d, rws, :], ys.ap[:], reads=[ys])
        for h in range(8):
            MM(S, (X2, v8(X2)[:64, h, :]), (ot, ot.ap[:, 4, h * 64:(h + 1) * 64]), (vt, vt.ap[:, h * 64:(h + 1) * 64]), start=True, stop=False)
            MM(S, (X2, v8(X2)[:64, h, :]), (ot, ot.ap[:, 5, h * 64:(h + 1) * 64]), (nU, nU.ap[:, h, :]), start=False, stop=True)
        TT(S, "dve", V(ST), V(ST), (gc, gc.ap[:, 0:8].unsqueeze(2).to_broadcast([64, 8, 64])), ALU.mult)
        TT(S, "dve", V(ST), V(ST), (X2, v8(X2)[:64, :, :]), ALU.add)
        ACT(S, V(STb), V(ST), AF.Identity)
        yield


def l1_scan_dn(S, C, bi):
    S.begin_phase()
    S.keep_pool = True
    K0 = scan_consts(S, C, "d")
    K0.ones = S.sb("dones", [128, 128], F32)
    S.op("pool", lambda e: e.memset(K0.ones.ap[:], 1.0), writes=[K0.ones])
    K0.tri = S.sb("dtri", [128, 2, 128], F32)
    for d in range(2):
        S.dma("sp", K0.tri.ap[:, d, :], C.tri[d, :, :], writes=[K0.tri])
    interleave([dn_chain(S, C, chain_res(S, K0, f"d{d}"), bi, d) for d in range(2)])
    S.keep_pool = False
    S.end_phase()


def dn_chain(S, C, K, bi, d):
    t = f"d{d}"
    ones, tri = K.ones, K.tri
    X0, X1, X2 = [S.ps(t + f"X{n}", [128, 4, 128], F32) for n in range(3)]
    ptr = S.ps(t + "ptr", [128, 4, 128], BF16)
    ot_r = mk_ring(S, t + "ot", [128, 5, 512], BF16, 2)
    qk_r = mk_ring(S, t + "qk", [128, 2, 512], BF16, 2)
    G_r = mk_ring(S, t + "G", [128, 3, 4], F32, 2)
    FT_r = mk_ring(S, t + "FT", [128, 4, 4, 128], BF16, 2)
    gl_r = mk_ring(S, t + "gl", [128, 4, 128], F32, 1)
    ET_r = mk_ring(S, t + "ET", [128, 4, 128], F32, 1)
    qkT_r = mk_ring(S, t + "qkT", [128, 4, 128], BF16, 2)
    Zb_r = mk_ring(S, t + "Zb", [128, 4, 128], BF16, 2)
    wT_r = mk_ring(S, t + "wT", [128, 4, 128], BF16, 2)
    u0_r = mk_ring(S, t + "u0", [128, 4, 128], F32, 2)
    u_r = mk_ring(S, t + "u", [128, 4, 128], BF16, 2)
    ys_r = mk_ring(S, t + "ys", [128, 512], F32, 2)
    sm_r = mk_ring(S, t + "sm", [128, 8], F32, 2)
    ST = S.sb(t + "ST", [128, 4, 128], F32)
    STb = S.sb(t + "STb", [128, 4, 128], BF16)
    S.op("dve", lambda e: e.memset(ST.ap[:], 0.0), writes=[ST])
    S.op("dve", lambda e: e.memset(STb.ap[:], 0.0), writes=[STb])
    order = tile_order(d)
    pend = {}

    def load(i):
        rws_ = slice(i * 128, (i + 1) * 128)
        ot, qk, G = ot_r.next(), qk_r.next(), G_r.next()
        S.dma("sp", ot.ap[:], C.dn_ops[bi, d, :, rws_, :].rearrange("q t c -> t q c"), writes=[ot])
        S.dma("pool", qk.ap[:], C.dn_qk[bi, :, rws_, :].rearrange("q t c -> t q c"), writes=[qk])
        S.dma("pool", G.ap[:], C.dn_G[bi, d, :, rws_, :].rearrange("q t c -> t q c"), writes=[G])
        pend[i] = (ot, qk, G)

    load(order[0])
    for n_, i in enumerate(order):
        rws = slice(i * 128, (i + 1) * 128)
        if n_ + 1 < len(order):
            load(order[n_ + 1])
        ot, qk, G = pend.pop(i)
        FT = FT_r.next()
        bview = lambda X: X.ap[:].bitcast(BF16).rearrange("p a (b c) -> p (a b) c", b=2)[:, 0:4, :]
        tgt = [(ptr, ptr.ap[:]), (X0, bview(X0)), (X1, bview(X1)), (X2, bview(X2))]
        for q, src in enumerate(((qk, 1), (qk, 0), (ot, 1), (ot, 0))):
            tb_, tv = tgt[q]
            for h in range(4):
                TR(S, (tb_, tv[:, h, :]), (src[0], src[0].ap[:, src[1], h * 128:(h + 1) * 128]), V(K.idb))
        for q in range(4):
            tb_, tv = tgt[q]
            if q % 2:
                ACT(S, (FT, FT.ap[:, q, :, :]), (tb_, tv), AF.Identity)
            else:
                S.op("dve", lambda e: e.tensor_copy(out=FT.ap[:, q, :, :], in_=tv), reads=[tb_], writes=[FT])
        yield
        gl = gl_r.next()
        for h in range(4):
            TS(S, "pool", (gl, gl.ap[:, h, :]), (tri, tri.ap[:, d, :]), G.ap[:, 0, h:h + 1], None, ALU.mult, extra=[G])
        for h in range(4):
            MM(S, (X0, X0.ap[:, h, :]), V(ones), (gl, gl.ap[:, h, :]))
        ET = ET_r.next()
        for h in range(4):
            TS(S, "dve", (ET, ET.ap[:, h, :]), (X0, X0.ap[:, h, :]), G.ap[:, 1, h:h + 1], 0.0, ALU.subtract, ALU.min, extra=[G])
        ACT(S, V(ET), V(ET), AF.Exp)
        yield
        for h in range(4):
            MM(S, (X1, X1.ap[:, h, :]), (FT, FT.ap[:, 0, h, :]), (FT, FT.ap[:, 2, h, :]))
            MM(S, (X2, X2.ap[:, h, :]), (FT, FT.ap[:, 0, h, :]), (FT, FT.ap[:, 1, h, :]))
        P, PT = K.rP.next(), K.rPT.next()
        TT(S, "dve", V(P), V(X1), V(ET), ALU.mult)
        TT(S, "dve", V(P), V(P), V(K.mS[d]), ALU.mult)
        qkT = qkT_r.next()
        TT(S, "pool", V(ET), V(ET), V(K.mI[d]), ALU.mult)
        TT(S, "dve", V(qkT), V(X2), V(ET), ALU.mult)
        yield
        for h in range(4):
            TR(S, (ptr, ptr.ap[:, h, :]), (P, P.ap[:, h, :]), V(K.idb))
        ACT(S, V(PT), V(ptr), AF.Identity)
        yield
        zo = []
        yield from inv_group(S, P, PT, K, (X0, X1, X2), zo)
        Zb = zo[0]
        for h in range(4):
            MM(S, (X0, X0.ap[:, h, :]), (Zb, Zb.ap[:, h, :]), (ot, ot.ap[:, 4, h * 128:(h + 1) * 128]))
            MM(S, (X1, X1.ap[:, h, :]), (ot, ot.ap[:, 2, h * 128:(h + 1) * 128]), (Zb, Zb.ap[:, h, :]))
        u0, wT = u0_r.next(), wT_r.next()
        ACT(S, V(u0), V(X0), AF.Identity)
        S.op("dve", lambda e: e.tensor_copy(out=wT.ap[:], in_=X1.ap[:]), reads=[X1], writes=[wT])
        yield
        for h in range(4):
            MM(S, (X2, X2.ap[:, h, :]), (wT, wT.ap[:, h, :]), (STb, STb.ap[:, h, :]))
        u = u_r.next()
        TT(S, "dve", V(u), V(u0), V(X2), ALU.subtract)
        yield
        for h in range(4):
            MM(S, (X0, X0.ap[:, h, :]), (FT, FT.ap[:, 3, h, :]), (STb, STb.ap[:, h, :]), start=True, stop=False)
            MM(S, (X0, X0.ap[:, h, :]), (qkT, qkT.ap[:, h, :]), (u, u.ap[:, h, :]), start=False, stop=True)
        ys = ys_r.next()
        ACT(S, r3(ys, 4), V(X0), AF.Identity)
        S.dma("sp", C.y_dn[bi, d, rws, :], ys.ap[:], reads=[ys])
        for h in range(4):
            MM(S, (X1, X1.ap[:, h, :]), (ot, ot.ap[:, 3, h * 128:(h + 1) * 128]), (u, u.ap[:, h, :]))
        sm = sm_r.next()
        ACT(S, (sm, sm.ap[:, 0:4]), (G, G.ap[:, 2, :]), AF.Exp)
        TT(S, "dve", V(ST), V(ST), (sm, sm.ap[:, 0:4].unsqueeze(2).to_broadcast([128, 4, 128])), ALU.mult)
        TT(S, "dve", V(ST), V(ST), V(X1), ALU.add)
        ACT(S, V(STb), V(ST), AF.Identity)
        yield


def l1_out(S, C, bi, src_stage, dst_stage, last):
    S.begin_phase()
    wo = S.sb("xwo", [128, 8, D], BF16)
    for c in range(8):
        S.dma("sp" if c % 2 else "pool", wo.ap[:, c, :], C.wob1[c * 128:(c + 1) * 128, :], writes=[wo])
    rows = S.sb("xrows", [128, 3, 512], F32)
    S.dma("sp", rows.ap[:, 0, :], C.rw_rows[7, :].partition_broadcast(128), writes=[rows])
    S.dma("sp", rows.ap[:, 1, :], C.rw_rows[8, :].partition_broadcast(128), writes=[rows])
    S.dma("sp", rows.ap[:, 2, :], C.dn_ng[:].partition_broadcast(128), writes=[rows])
    epi = Epi(S, C, "x")
    epi.load(1, 0, bi)
    xin = mk_ring(S, "xxin", [128, 1024], F32, 2)
    ya = mk_ring(S, "xya", [128, 2, 512], F32, 2)
    yb = mk_ring(S, "xyb", [128, 2, 512], F32, 2)
    ex = mk_ring(S, "xex", [128, 3, 512], F32, 2)
    sq_r = mk_ring(S, "xsq", [128, 512], F32, 2)
    ycat = mk_ring(S, "xyc", [128, 1024], F32, 2)
    yT = mk_ring(S, "xyT", [128, 8, 128], BF16, 2)
    sm = mk_ring(S, "xsm", [128, 32], F32, 2)
    ptr = mk_ring(S, "xptr", [128, 4, 128], F32, 2, psum=True)
    py = mk_ring(S, "xpy", [128, 1024], F32, 2, psum=True)
    def out_chain(i):
        rws = slice(i * 128, (i + 1) * 128)
        xi, a, b, e_, yc, w, sq = xin.next(), ya.next(), yb.next(), ex.next(), ycat.next(), sm.next(), sq_r.next()
        S.dma("sp", xi.ap[:], tile_src(C, src_stage, bi, i), writes=[xi])
        S.dma("sp", a.ap[:], C.y_rw[bi, :, rws, :].rearrange("q t c -> t q c"), writes=[a])
        S.dma("pool", b.ap[:], C.y_dn[bi, :, rws, :].rearrange("q t c -> t q c"), writes=[b])
        S.dma("sp", e_.ap[:, 0, :], C.rw_g[bi, rws, :], writes=[e_])
        S.dma("pool", e_.ap[:, 1, :], C.rw_bonus[bi, rws, :], writes=[e_])
        S.dma("sp", e_.ap[:, 2, :], C.dn_z[bi, rws, :], writes=[e_])
        y = (a, a.ap[:, 0, :])
        TT(S, "dve", y, y, (a, a.ap[:, 1, :]), ALU.add)
        S.op("dve", lambda e: e.tensor_reduce(out=w.ap[:, 0:8], in_=r3(a, 8, slice(None), 0, slice(None))[1], axis=AX.X, op=ALU.add), reads=[a], writes=[w])
        TS(S, "dve", (w, w.ap[:, 0:8]), (w, w.ap[:, 0:8]), 1.0 / 64, None, ALU.mult)
        TT(S, "dve", r3(a, 8, slice(None), 0, slice(None)), r3(a, 8, slice(None), 0, slice(None)), bc3(w, 8, 64), ALU.subtract)
        TT(S, "pool", V(sq), y, y, ALU.mult)
        S.op("dve", lambda e: e.tensor_reduce(out=w.ap[:, 8:16], in_=r3(sq, 8)[1], axis=AX.X, op=ALU.add), reads=[sq], writes=[w])
        TS(S, "dve", (w, w.ap[:, 8:16]), (w, w.ap[:, 8:16]), 1.0 / 64, 64e-5, ALU.mult, ALU.add)
        ACT(S, (w, w.ap[:, 8:16]), (w, w.ap[:, 8:16]), AF.Sqrt)
        S.op("dve", lambda e: e.reciprocal(out=w.ap[:, 8:16], in_=w.ap[:, 8:16]), reads=[w], writes=[w])
        TT(S, "dve", r3(a, 8, slice(None), 0, slice(None)), r3(a, 8, slice(None), 0, slice(None)),
           (w, w.ap[:, 8:16].unsqueeze(2).to_broadcast([128, 8, 64])), ALU.mult)
        TT(S, "pool", y, y, (rows, rows.ap[:, 0, :]), ALU.mult)
        TT(S, "pool", y, y, (rows, rows.ap[:, 1, :]), ALU.add)
        TT(S, "dve", y, y, (e_, e_.ap[:, 1, :]), ALU.add)
        TT(S, "dve", (yc, yc.ap[:, 0:512]), y, (e_, e_.ap[:, 0, :]), ALU.mult)
        yield
        o = (b, b.ap[:, 0, :])
        TT(S, "dve", o, o, (b, b.ap[:, 1, :]), ALU.add)
        TT(S, "pool", V(sq), o, o, ALU.mult)
        S.op("dve", lambda e: e.tensor_reduce(out=w.ap[:, 16:20], in_=r3(sq, 4)[1], axis=AX.X, op=ALU.add), reads=[sq], writes=[w])
        TS(S, "dve", (w, w.ap[:, 16:20]), (w, w.ap[:, 16:20]), 1.0 / 128, 1e-6, ALU.mult, ALU.add)
        ACT(S, (w, w.ap[:, 16:20]), (w, w.ap[:, 16:20]), AF.Sqrt)
        S.op("dve", lambda e: e.reciprocal(out=w.ap[:, 16:20], in_=w.ap[:, 16:20]), reads=[w], writes=[w])
        TT(S, "dve", r3(b, 4, slice(None), 0, slice(None)), r3(b, 4, slice(None), 0, slice(None)),
           (w, w.ap[:, 16:20].unsqueeze(2).to_broadcast([128, 4, 128])), ALU.mult)
        TT(S, "pool", o, o, (rows, rows.ap[:, 2, :]), ALU.mult)
        TT(S, "dve", (yc, yc.ap[:, 512:1024]), o, (e_, e_.ap[:, 2, :]), ALU.mult)
        yield
        yt = yT.next()
        for g in range(2):
            p = ptr.next()
            for j in range(4):
                c = g * 4 + j
                S.op("pe", lambda e: e.transpose(out=p.ap[:, j, :], in_=yc.ap[:, c * 128:(c + 1) * 128], identity=C.ident.ap[:]), reads=[yc, C.ident], writes=[p])
            ACT(S, (yt, yt.ap[:, g * 4:(g + 1) * 4, :]), V(p), AF.Identity)
        yield
        p = py.next()
        for hf in range(2):
            for c in range(8):
                MM(S, (p, p.ap[:, hf * 512:(hf + 1) * 512]), (yt, yt.ap[:, c, :]), (wo, wo.ap[:, c, hf * 512:(hf + 1) * 512]), start=(c == 0), stop=(c == 7))
        if last:
            dst = C.xs[dst_stage - 1][bi, rws, :]
        else:
            dst = C.xs[dst_stage - 1][bi, rws, :]
        epi.run(p, xi, i < 2, dst)
        yield

    tl = list(range(2 if last else 0, NTT))
    for k in range(0, len(tl), 2):
        interleave([out_chain(i) for i in tl[k:k + 2]])
    S.end_phase()


NB_FULL = 4
N_CORES = 8


def build_full(NB, dbg=None):
    nc = bass.Bass("TRN2", target_bir_lowering=False)
    C = declare_common(nc, NB, dbg=dbg)
    declare_l0(nc, C, dbg=dbg)
    declare_l1(nc, C, dbg=dbg)
    S = Sched(nc)
    common_setup(S, C)
    phase_mod(S, C)
    for l in range(2):
        prep_weight(S, C.w1b[l], C.mlp_w1[l], D, DFF, tag=f"pm1{l}")
        prep_weight(S, C.w2b[l], C.mlp_w2[l], DFF, D, tag=f"pm2{l}")
    l0_setup(S, C)
    l1_setup(S, C)
    for bi in range(NB):
        l0_inproj(S, C, bi, 0)
        l0_hyena(S, C, bi)
        l0_attn(S, C, bi)
        l0_outproj(S, C, bi, 0, 1)
        phase_mlp(S, C, 0, bi, 1, 2, False)
        S.begin_phase()
        hT = build_hT(S, C, bi, 2, 1)
        l1_feat_rw(S, C, bi, hT)
        l1_feat_dn(S, C, bi, hT)
        S.end_phase()
        l1_scan_rw(S, C, bi)
        l1_scan_dn(S, C, bi)
        l1_out(S, C, bi, 2, 3, True)
        phase_mlp(S, C, 1, bi, 3, None, True)
    S.finish()
    return nc


def kernel(**inputs):
    NB = NB_FULL
    nc = build_full(NB)
    in_maps = [host_l1(inputs, host_l0(inputs, host_common(inputs, c, NB))) for c in range(N_CORES)]
    res = run_bass_kernel_spmd(nc, in_maps, core_ids=list(range(N_CORES)))
    out = np.concatenate([np.asarray(r["out"], dtype=np.float32) for r in res.results], axis=0)
    return out
```
